# Optimizing a Trainium2 kernel written in Bass

```python
import jax, jax.numpy as jnp
from jax import lax
import numpy as np

D_MODEL = 1024
BATCH = 2
SEQ = 8192
DEPTH = 1
DEC_BATCH = 32
DEC_SEQ = 1
PAST_LEN = 8192
PAGE_SIZE = 128

HEAD_DIM = 64
NSA_HEADS = 8
NSA_KV_HEADS = 2
NSA_REP = NSA_HEADS // NSA_KV_HEADS
CMP_LEN = 32
CMP_STRIDE = 16
SEL_LEN = 64
SEL_TOPK = 16
WINDOW = 512
Q_BLOCK = 128
ROPE_THETA = 10000.0
GDN_HEADS = 4
GDN_DK = 128
GDN_DV = 128
CONV_W = 4
GDN_CHUNK = 64
NSA_Q_W = NSA_HEADS * HEAD_DIM
NSA_KV_W = NSA_KV_HEADS * HEAD_DIM
GDN_QK_W = GDN_HEADS * GDN_DK
GDN_V_W = GDN_HEADS * GDN_DV
GDN_CONV_CH = 2 * GDN_QK_W + GDN_V_W
MIX_W = NSA_Q_W + GDN_V_W
IN_W = NSA_Q_W + 6 * NSA_KV_W + 3 * NSA_HEADS + GDN_CONV_CH + GDN_V_W + 2 * GDN_HEADS
_FF_RAW = -(-8 * D_MODEL // 3)
D_FF = -(-_FF_RAW // 256) * 256
NEG = -1e30
SEL_BONUS = 1e4
EPS = 1e-6

kernel_name = 'hybrid_nsa_gdn_decode_step'

F32 = jnp.float32


def rms_norm(x, g):
    xf = x.astype(F32)
    y = xf * lax.rsqrt(jnp.mean(xf * xf, axis=-1, keepdims=True) + EPS)
    return (y * g.astype(F32)).astype(x.dtype)


def rope(x, pos):
    half = HEAD_DIM // 2
    inv = jnp.power(ROPE_THETA, -jnp.arange(half, dtype=F32) * 2.0 / HEAD_DIM)
    ang = pos.astype(F32)[:, None] * inv[None, :]
    cos = jnp.cos(ang)[None, :, None, :]
    sin = jnp.sin(ang)[None, :, None, :]
    xf = x.astype(F32)
    x1, x2 = xf[..., :half], xf[..., half:]
    return jnp.concatenate([x1 * cos - x2 * sin, x2 * cos + x1 * sin], axis=-1).astype(x.dtype)


def masked_softmax(s, mask):
    p = jax.nn.softmax(jnp.where(mask, s, NEG), axis=-1)
    return jnp.where(mask, p, 0.0)


def l2norm(x):
    return x * lax.rsqrt(jnp.sum(x * x, axis=-1, keepdims=True) + EPS)


def split_proj(proj, pos):
    B, T, _ = proj.shape
    G, R, dh = NSA_KV_HEADS, NSA_REP, HEAD_DIM
    cuts = np.cumsum([NSA_Q_W, 6 * NSA_KV_W, 3 * NSA_HEADS, GDN_CONV_CH, GDN_V_W, GDN_HEADS]).tolist()
    q, kv, gt, conv_in, z, a, b = jnp.split(proj, cuts, axis=-1)
    q = rope(q.reshape(B, T, NSA_HEADS, dh), pos).reshape(B, T, G, R, dh)
    kv = kv.reshape(B, T, 3, 2, G, dh)
    k = rope(kv[:, :, :, 0].reshape(B, T, 3 * G, dh), pos).reshape(B, T, 3, G, dh)
    v = kv[:, :, :, 1]
    kv4 = jnp.stack([k[:, :, 0], v[:, :, 0], k[:, :, 1], v[:, :, 1]], axis=2)
    kvw = jnp.stack([k[:, :, 2], v[:, :, 2]], axis=2)
    gates = jax.nn.sigmoid(gt.astype(F32)).reshape(B, T, G, R, 3)
    return q, gates, kv4, kvw, conv_in, z, a, b


def compress(k, pe, w):
    L = k.shape[1]
    n_cmp = (L - CMP_LEN) // CMP_STRIDE + 1
    idx = jnp.arange(n_cmp)[:, None] * CMP_STRIDE + jnp.arange(CMP_LEN)[None, :]
    blk = k[:, idx] + pe[None, None, :, None, :]
    return jnp.einsum('bnlgd,lde->bnge', blk, w)


def sel_blocks(k):
    B, L, G, dh = k.shape
    n_sel = -(-L // SEL_LEN)
    kp = jnp.pad(k, ((0, 0), (0, n_sel * SEL_LEN - L), (0, 0), (0, 0)))
    return kp.reshape(B, n_sel, SEL_LEN, G, dh).transpose(0, 3, 1, 2, 4)


def cmp_to_sel(n_cmp, n_sel):
    cs = jnp.arange(n_cmp)[:, None] * CMP_STRIDE
    ss = jnp.arange(n_sel)[None, :] * SEL_LEN
    return ((cs < ss + SEL_LEN) & (cs + CMP_LEN > ss)).astype(F32)


def nsa_core(q, q_pos, gates, ck, cv, ksb, vsb, kw, vw, kw_pos):
    B, Tq, G, R, dh = q.shape
    n_cmp = ck.shape[1]
    n_sel = ksb.shape[2]
    qf = q.astype(F32) * (HEAD_DIM ** -0.5)
    cmp_end = jnp.arange(n_cmp) * CMP_STRIDE + CMP_LEN - 1
    m_cmp = (cmp_end[None, :] <= q_pos[:, None])[None, :, None, None, :]
    p_cmp = masked_softmax(jnp.einsum('bqgrd,bngd->bqgrn', qf, ck.astype(F32)), m_cmp)
    o_cmp = jnp.einsum('bqgrn,bngd->bqgrd', p_cmp, cv.astype(F32))
    imp = jnp.einsum('bqgrn,ns->bqgs', p_cmp, cmp_to_sel(n_cmp, n_sel))
    blk = jnp.arange(n_sel)[None, :]
    cur = (q_pos // SEL_LEN)[:, None]
    valid = blk * SEL_LEN <= q_pos[:, None]
    forced = (blk == 0) | (blk == cur) | (blk == cur - 1)
    score = jnp.where(valid[None, :, None, :], imp + jnp.where(forced, SEL_BONUS, 0.0)[None, :, None, :], NEG)
    top_val, top_idx = lax.top_k(score, min(SEL_TOPK, n_sel))
    kk = top_idx.shape[-1]
    idx = top_idx.transpose(0, 2, 1, 3)
    sel_ok = (top_val > NEG / 2).transpose(0, 2, 1, 3)
    bi = jnp.arange(B)[:, None, None, None]
    gi = jnp.arange(G)[None, :, None, None]
    kg = ksb[bi, gi, idx]
    vg = vsb[bi, gi, idx]
    tok = idx[..., None] * SEL_LEN + jnp.arange(SEL_LEN)
    m_sel = ((tok <= q_pos[None, None, :, None, None]) & sel_ok[..., None]).reshape(B, G, Tq, 1, kk * SEL_LEN)
    s_sel = jnp.einsum('bqgrd,bgqkld->bgqrkl', qf, kg.astype(F32)).reshape(B, G, Tq, R, kk * SEL_LEN)
    p_sel = masked_softmax(s_sel, m_sel).reshape(B, G, Tq, R, kk, SEL_LEN)
    o_sel = jnp.einsum('bgqrkl,bgqkld->bqgrd', p_sel, vg.astype(F32))
    m_win = ((kw_pos[None, :] <= q_pos[:, None]) & (kw_pos[None, :] >= q_pos[:, None] - WINDOW)
             & (kw_pos[None, :] >= 0))[None, :, None, None, :]
    p_win = masked_softmax(jnp.einsum('bqgrd,bsgd->bqgrs', qf, kw.astype(F32)), m_win)
    o_win = jnp.einsum('bqgrs,bsgd->bqgrd', p_win, vw.astype(F32))
    g = gates.astype(F32)
    return g[..., 0:1] * o_cmp + g[..., 1:2] * o_sel + g[..., 2:3] * o_win


def nsa_prompt(q, gates, kv4, kvw, pe, wc):
    B, T, G, R, dh = q.shape
    ck = compress(kv4[:, :, 0], pe[0], wc[0])
    cv = compress(kv4[:, :, 1], pe[1], wc[1])
    ksb = sel_blocks(kv4[:, :, 2])
    vsb = sel_blocks(kv4[:, :, 3])
    padw = ((0, 0), (WINDOW, 0), (0, 0), (0, 0))
    kwp = jnp.pad(kvw[:, :, 0], padw)
    vwp = jnp.pad(kvw[:, :, 1], padw)
    nb = T // Q_BLOCK
    qb = q.reshape(B, nb, Q_BLOCK, G, R, dh).swapaxes(0, 1)
    gb = gates.reshape(B, nb, Q_BLOCK, G, R, 3).swapaxes(0, 1)

    def block(args):
        qi, gi, i = args
        start = i * Q_BLOCK
        q_pos = start + jnp.arange(Q_BLOCK)
        kw = lax.dynamic_slice_in_dim(kwp, start, WINDOW + Q_BLOCK, axis=1)
        vw = lax.dynamic_slice_in_dim(vwp, start, WINDOW + Q_BLOCK, axis=1)
        kw_pos = start - WINDOW + jnp.arange(WINDOW + Q_BLOCK)
        return nsa_core(qi, q_pos, gi, ck, cv, ksb, vsb, kw, vw, kw_pos)

    ob = lax.map(block, (qb, gb, jnp.arange(nb, dtype=jnp.int32)))
    return ob.swapaxes(0, 1).reshape(B, T, NSA_Q_W)


def nsa_sample(q, gates, kv4, kvw, kv_past, win_buf, pos, pe, wc):
    B, T = q.shape[0], q.shape[1]
    P = kv_past.shape[1]
    W = win_buf.shape[1]
    kv_all = jnp.concatenate([kv_past.astype(kv4.dtype), kv4], axis=1)
    ck = compress(kv_all[:, :, 0], pe[0], wc[0])
    cv = compress(kv_all[:, :, 1], pe[1], wc[1])
    ksb = sel_blocks(kv_all[:, :, 2])
    vsb = sel_blocks(kv_all[:, :, 3])
    kw_all = jnp.concatenate([win_buf.astype(kvw.dtype), kvw], axis=1)
    kw_pos = P - W + jnp.arange(W + T)
    o = nsa_core(q, pos, gates, ck, cv, ksb, vsb, kw_all[:, :, 0], kw_all[:, :, 1], kw_pos)
    new_win = kw_all[:, -min(WINDOW, P + T):]
    return o.reshape(B, T, NSA_Q_W), new_win


def gated_delta(q, k, v, g, beta, S0, C):
    B, T, H, dk = q.shape
    dv = v.shape[-1]
    N = T // C
    qc = q.reshape(B, N, C, H, dk).transpose(1, 0, 3, 2, 4)
    kc = k.reshape(B, N, C, H, dk).transpose(1, 0, 3, 2, 4)
    vc = v.reshape(B, N, C, H, dv).transpose(1, 0, 3, 2, 4)
    gc = g.reshape(B, N, C, H).transpose(1, 0, 3, 2)
    bc = beta.reshape(B, N, C, H).transpose(1, 0, 3, 2)
    gcum = jnp.cumsum(gc, axis=-1)
    ii = jnp.arange(C)[:, None]
    jj = jnp.arange(C)[None, :]
    incl = ii >= jj
    strict = ii > jj
    diff = gcum[..., :, None] - gcum[..., None, :]
    dec = jnp.where(incl, jnp.exp(jnp.where(incl, diff, 0.0)), 0.0)
    kb = kc * bc[..., None]
    Lm = jnp.where(strict, jnp.einsum('nbhid,nbhjd->nbhij', kb, kc) * dec, 0.0)
    A = Lm + jnp.eye(C, dtype=F32)
    rhs = jnp.concatenate([vc * bc[..., None], kb * jnp.exp(gcum)[..., None]], axis=-1)
    sol = lax.linalg.triangular_solve(A, rhs, left_side=True, lower=True)
    u, w = sol[..., :dv], sol[..., dv:]
    aqk = jnp.einsum('nbhid,nbhjd->nbhij', qc, kc) * dec
    qg = qc * jnp.exp(gcum)[..., None]
    kg = kc * jnp.exp(gcum[..., -1:] - gcum)[..., None]
    glast = jnp.exp(gcum[..., -1])

    def step(S, xs):
        u_i, w_i, aqk_i, qg_i, kg_i, gl_i = xs
        vn = u_i - jnp.einsum('bhck,bhkv->bhcv', w_i, S)
        o = jnp.einsum('bhck,bhkv->bhcv', qg_i, S) + jnp.einsum('bhij,bhjv->bhiv', aqk_i, vn)
        S = S * gl_i[..., None, None] + jnp.einsum('bhck,bhcv->bhkv', kg_i, vn)
        return S, o

    S, o = lax.scan(step, S0, (u, w, aqk, qg, kg, glast))
    return o.transpose(1, 0, 3, 2, 4).reshape(B, T, H, dv), S


def gdn_mixer(conv_in, z, a, b, conv_buf, S0, conv_w, a_log, dt_bias, norm_w, chunk):
    B, T, _ = conv_in.shape
    xp = jnp.concatenate([conv_buf.astype(conv_in.dtype), conv_in], axis=1)
    conv = xp[:, 0:T] * conv_w[0]
    for j in range(1, CONV_W):
        conv = conv + xp[:, j:j + T] * conv_w[j]
    c = jax.nn.silu(conv.astype(F32))
    qg, kg, vg = jnp.split(c, [GDN_QK_W, 2 * GDN_QK_W], axis=-1)
    qg = l2norm(qg.reshape(B, T, GDN_HEADS, GDN_DK)) * (GDN_DK ** -0.5)
    kg = l2norm(kg.reshape(B, T, GDN_HEADS, GDN_DK))
    vg = vg.reshape(B, T, GDN_HEADS, GDN_DV)
    beta = jax.nn.sigmoid(b.astype(F32))
    g = -jnp.exp(a_log.astype(F32)) * jax.nn.softplus(a.astype(F32) + dt_bias.astype(F32))
    o, S = gated_delta(qg, kg, vg, g, beta, S0.astype(F32), chunk)
    o = o * lax.rsqrt(jnp.mean(o * o, axis=-1, keepdims=True) + EPS) * norm_w.astype(F32)
    o = o * jax.nn.silu(z.astype(F32).reshape(B, T, GDN_HEADS, GDN_DV))
    return o.reshape(B, T, GDN_V_W), S, xp[:, -(CONV_W - 1):]


def channel_mix(h, g, w_gate_up, w_down):
    gate, up = jnp.split(rms_norm(h, g) @ w_gate_up, 2, axis=-1)
    return h + (jax.nn.silu(gate) * up) @ w_down


def setup_inputs(seed: int = 0) -> dict:
    key = jax.random.key(seed)
    ks = jax.random.split(key, 20)
    n_pages = PAST_LEN // PAGE_SIZE
    n_pool = DEC_BATCH * n_pages * 5 // 4
    w_buf = min(WINDOW, PAST_LEN)
    G, dh = NSA_KV_HEADS, HEAD_DIM
    nrm = jax.random.normal
    page_table = jax.random.permutation(ks[0], n_pool)[:DEC_BATCH * n_pages].reshape(DEC_BATCH, n_pages).astype(jnp.int32)
    return {
        'x_prompt': nrm(ks[1], (BATCH, SEQ, D_MODEL), F32),
        'x_sample': nrm(ks[2], (DEC_BATCH, DEC_SEQ, D_MODEL), F32),
        'cache_nsa_kv': nrm(ks[3], (DEPTH, n_pool, PAGE_SIZE, 4, G, dh), F32),
        'cache_nsa_win': nrm(ks[4], (DEPTH, DEC_BATCH, w_buf, 2, G, dh), F32),
        'state_gdn_S': 0.1 * nrm(ks[5], (DEPTH, DEC_BATCH, GDN_HEADS, GDN_DK, GDN_DV), F32),
        'state_gdn_conv': nrm(ks[6], (DEPTH, DEC_BATCH, CONV_W - 1, GDN_CONV_CH), F32),
        'page_table': page_table,
        'norm_mix': 1.0 + 0.02 * nrm(ks[7], (DEPTH, D_MODEL), F32),
        'w_in': nrm(ks[8], (DEPTH, D_MODEL, IN_W), F32) * D_MODEL ** -0.5,
        'nsa_cmp_pe': 0.1 * nrm(ks[9], (DEPTH, 2, CMP_LEN, dh), F32),
        'nsa_cmp_w': nrm(ks[10], (DEPTH, 2, CMP_LEN, dh, dh), F32) * (CMP_LEN * dh) ** -0.5,
        'gdn_conv_w': 0.5 * nrm(ks[11], (DEPTH, CONV_W, GDN_CONV_CH), F32),
        'gdn_a_log': jnp.log(jax.random.uniform(ks[12], (DEPTH, GDN_HEADS), F32, 1.0, 16.0)),
        'gdn_dt_bias': jnp.log(jnp.expm1(jax.random.uniform(ks[13], (DEPTH, GDN_HEADS), F32, 0.001, 0.1))),
        'gdn_norm': 1.0 + 0.02 * nrm(ks[14], (DEPTH, GDN_DV), F32),
        'w_out': nrm(ks[15], (DEPTH, MIX_W, D_MODEL), F32) * MIX_W ** -0.5,
        'norm_ffn': 1.0 + 0.02 * nrm(ks[16], (DEPTH, D_MODEL), F32),
        'w_gate_up': nrm(ks[17], (DEPTH, D_MODEL, 2 * D_FF), F32) * D_MODEL ** -0.5,
        'w_down': nrm(ks[18], (DEPTH, D_FF, D_MODEL), F32) * D_FF ** -0.5,
        'norm_final': 1.0 + 0.02 * nrm(ks[19], (D_MODEL,), F32),
    }


def reference(x_prompt, x_sample, cache_nsa_kv, cache_nsa_win, state_gdn_S, state_gdn_conv, page_table,
              norm_mix, w_in, nsa_cmp_pe, nsa_cmp_w, gdn_conv_w, gdn_a_log, gdn_dt_bias, gdn_norm,
              w_out, norm_ffn, w_gate_up, w_down, norm_final):
    Bp, Tp, _ = x_prompt.shape
    Bs, Ts, _ = x_sample.shape
    n_pages = page_table.shape[1]
    past_len = n_pages * PAGE_SIZE
    pos_p = jnp.arange(Tp, dtype=jnp.int32)
    pos_s = past_len + jnp.arange(Ts, dtype=jnp.int32)
    hp, hs = x_prompt, x_sample
    kv_p, win_p, S_p, conv_p = [], [], [], []
    kv_s, win_s, S_s, conv_s = [], [], [], []
    for l in range(DEPTH):
        proj = rms_norm(hp, norm_mix[l]) @ w_in[l]
        q, gates, kv4, kvw, conv_in, z, a, b = split_proj(proj, pos_p)
        o_nsa = nsa_prompt(q, gates, kv4, kvw, nsa_cmp_pe[l], nsa_cmp_w[l])
        buf0 = jnp.zeros((Bp, CONV_W - 1, GDN_CONV_CH), proj.dtype)
        S0 = jnp.zeros((Bp, GDN_HEADS, GDN_DK, GDN_DV), F32)
        o_gdn, S_new, conv_new = gdn_mixer(conv_in, z, a, b, buf0, S0, gdn_conv_w[l], gdn_a_log[l],
                                           gdn_dt_bias[l], gdn_norm[l], min(GDN_CHUNK, Tp))
        hp = hp + jnp.concatenate([o_nsa, o_gdn], axis=-1).astype(hp.dtype) @ w_out[l]
        hp = channel_mix(hp, norm_ffn[l], w_gate_up[l], w_down[l])
        kv_p.append(kv4)
        win_p.append(kvw[:, -min(WINDOW, Tp):])
        S_p.append(S_new)
        conv_p.append(conv_new)
        proj = rms_norm(hs, norm_mix[l]) @ w_in[l]
        q, gates, kv4, kvw, conv_in, z, a, b = split_proj(proj, pos_s)
        kv_past = cache_nsa_kv[l][page_table].reshape(Bs, past_len, 4, NSA_KV_HEADS, HEAD_DIM)
        o_nsa, win_new = nsa_sample(q, gates, kv4, kvw, kv_past, cache_nsa_win[l], pos_s,
                                    nsa_cmp_pe[l], nsa_cmp_w[l])
        o_gdn, S_new, conv_new = gdn_mixer(conv_in, z, a, b, state_gdn_conv[l], state_gdn_S[l],
                                           gdn_conv_w[l], gdn_a_log[l], gdn_dt_bias[l], gdn_norm[l], Ts)
        hs = hs + jnp.concatenate([o_nsa, o_gdn], axis=-1).astype(hs.dtype) @ w_out[l]
        hs = channel_mix(hs, norm_ffn[l], w_gate_up[l], w_down[l])
        kv_s.append(kv4)
        win_s.append(win_new)
        S_s.append(S_new)
        conv_s.append(conv_new)
    y_prompt = rms_norm(hp, norm_final)
    y_sample = rms_norm(hs, norm_final)
    return (y_prompt, y_sample,
            jnp.stack(kv_p), jnp.stack(win_p), jnp.stack(S_p), jnp.stack(conv_p),
            jnp.stack(kv_s), jnp.stack(win_s), jnp.stack(S_s), jnp.stack(conv_s))
```

```python
import numpy as np
from contextlib import ExitStack
import concourse.bass as bass
import concourse.mybir as mybir
from concourse.bass_utils import run_bass_kernel_spmd

F32 = mybir.dt.float32
BF16 = mybir.dt.bfloat16
I32 = mybir.dt.int32
AF = mybir.ActivationFunctionType
ALU = mybir.AluOpType
AX = mybir.AxisListType

ENGS = ("pe", "act", "dve", "pool", "sp")

D_MODEL = 1024
SEQ = 8192
HEAD_DIM = 64
D_FF = 2816
EPS = 1e-6
NEGB = -30000.0


class Buf:
    __slots__ = ("name", "w", "rs")

    def __init__(self, name=""):
        self.name = name
        self.w = None
        self.rs = []


class FW:
    def __init__(self, nc, stack, ndma_sems=16):
        self.nc = nc
        self.eng = {"pe": nc.tensor, "act": nc.scalar, "dve": nc.vector, "pool": nc.gpsimd, "sp": nc.sync}
        self.sem = {e: stack.enter_context(nc.semaphore("s_" + e)) for e in ENGS}
        self.cnt = {e: 0 for e in ENGS}
        self.waited = {e: {} for e in ENGS}
        self.dsems = {}
        self.dstate = {}
        for q in ("sp", "pool"):
            self.dsems[q] = [stack.enter_context(nc.semaphore("d_%s%d" % (q, i))) for i in range(ndma_sems)]
            self.dstate[q] = {"i": 0, "val": [0] * ndma_sems}
        self.n_inst = 0
        self.dead = False

    def _wait(self, e, ev):
        if ev is None:
            return
        if ev[0] == "c":
            _, src, n = ev
            if src == "pe" and e == "pe":
                return
            key = ("c", src)
            if self.waited[e].get(key, 0) >= n:
                return
            self.eng[e].wait_ge(self.sem[src], n)
            self.waited[e][key] = n
        else:
            _, q, idx, val = ev
            key = ("d", q, idx)
            if self.waited[e].get(key, 0) >= val:
                return
            self.eng[e].wait_ge(self.dsems[q][idx], val)
            self.waited[e][key] = val

    def _deps(self, e, reads, writes):
        for b in reads:
            self._wait(e, b.w)
        for b in writes:
            self._wait(e, b.w)
            for r in b.rs:
                self._wait(e, r)

    def _commit(self, ev, reads, writes):
        for b in reads:
            b.rs.append(ev)
            if len(b.rs) > 96:
                b.rs = b.rs[-96:]
        for b in writes:
            b.w = ev
            b.rs = []

    def op(self, e, fn, reads=(), writes=()):
        if self.dead:
            return None
        self._deps(e, reads, writes)
        ins = fn(self.eng[e])
        self.cnt[e] += 1
        ins.then_inc(self.sem[e], 1)
        ev = ("c", e, self.cnt[e])
        self._commit(ev, reads, writes)
        self.n_inst += 1
        return ev

    def dma(self, q, out, in_, reads=(), writes=(), fn=None):
        if self.dead:
            return None
        st = self.dstate[q]
        idx = st["i"] % len(self.dsems[q])
        st["i"] += 1
        if st["val"][idx] > 0:
            self._wait(q, ("d", q, idx, st["val"][idx]))
        self._deps(q, reads, writes)
        if fn is None:
            ins = self.eng[q].dma_start(out=out, in_=in_)
        else:
            ins = fn(self.eng[q])
        st["val"][idx] += 16
        ins.then_inc(self.dsems[q][idx], 16)
        ev = ("d", q, idx, st["val"][idx])
        self._commit(ev, reads, writes)
        self.n_inst += 1
        return ev

    def barrier(self):
        for e in ENGS:
            for src in ENGS:
                if src != e and self.cnt[src] > 0:
                    self._wait(e, ("c", src, self.cnt[src]))
            for q in ("sp", "pool"):
                stq = self.dstate[q]
                for idx, v in enumerate(stq["val"]):
                    if v:
                        self._wait(e, ("d", q, idx, v))

    def drain(self):
        for q in ("sp", "pool"):
            st = self.dstate[q]
            for idx, v in enumerate(st["val"]):
                if v:
                    self._wait("sp", ("d", q, idx, v))


class _Stop(Exception):
    pass


STOP = None
SKIP_CC = False


def build_nc(NT=64, dbg=False, phaseB=False):
    TT = NT * 128
    NG = NT // 4
    nc = bass.Bass("TRN2", target_bir_lowering=False)

    def din(name, shape, dt=F32):
        return nc.dram_tensor(name, list(shape), dt, kind="ExternalInput").ap()

    def dout(name, shape, dt=F32):
        return nc.dram_tensor(name, list(shape), dt, kind="ExternalOutput").ap()

    x_d = din("x", [TT, 1024])
    wtok_d = din("w_tok", [1024, 782])
    wgdn_d = din("w_gdn", [1024, 384])
    gmix_d = din("g_mix", [128, 8])
    cw_d = din("conv_w", [128, 12])
    hsc_d = din("head_sc", [128, 2])
    gnorm_d = din("gdn_norm_b", [128, 128])
    cmpw_d = din("cmp_w", [128, 2 * 16 * 64])
    cmppe_d = din("cmp_pe", [128, 32])
    ident_d = din("c_ident", [128, 128])
    cos_d = din("c_cos", [128, NT * 32])
    sin_d = din("c_sin", [128, NT * 32])
    tri_d = din("c_tri", [128, 256])
    cmpmask_d = din("c_cmpmask", [128, 17 * 128])
    prel_d = din("c_prel", [128, 256])
    eind_d = din("c_eind", [64, TT])
    c2s_d = din("c_c2s", [128, 4 * 128])
    gmask_d = din("c_gmask", [128, 5 * 128])

    TB = TT // 4
    NB = TB // 512
    if phaseB:
        xown_d = din("x_own", [TB, 1024])
        sel4_d = din("sel4", [128, 4])
        wout_d = din("w_out_p", [1024, 1024])
        gffn_d = din("g_ffn", [128, 8])
        wgu_d = din("w_gu", [1024, 5632])
        wdn_d = din("w_dn", [2816, 1024])
        nfin_d = din("nfin_b", [128, 1024])
        y_o = dout("y_out", [TB, 1024])
    if phaseB:
        xs_d = din("xs", [4, 1024])
        win_full_d = din("w_in_full", [1024, 3360])
        cache_d = din("cache_kv", [2560 * 128, 512])
        ptab_d = din("ptab_b", [128, 256], I32)
        iota_d = din("c_iota", [128, 1])
        wincache_d = din("win_cache", [4, 512, 256])
        gS_d = din("gdn_S", [16, 128, 128])
        gconv_d = din("gdn_conv", [4, 3, 1536])
        convwb_d = din("conv_w_b", [4, 4, 1536])
        alogb_d = din("alog_b", [4, 8])
        gnrow_d = din("gnorm_row", [1, 128])
        cmpw64_d = din("cmp_w64", [128, 2 * 32 * 64])
        woutn_d = din("w_out_n", [64, 8 * 1024])
        woutg_d = din("w_out_g", [128, 4 * 1024])
        ropes_d = din("c_rope_s", [4, 64])
        ones511_d = din("c_ones511", [128, 4])
        pe64_d = din("cmp_pe64", [128, 64])
        eind_s_d = din("c_eind_s", [64, 8192])
        oh4_d = din("c_oh4", [4, 80])
        bonus_d = din("c_bonus_s", [1, 128])
        ys_o = dout("ys_out", [4, 1024])
        kvs_o = dout("kvs_out", [4, 512])
        wins_o = dout("wins_out", [4, 512, 256])
        Ss_o = dout("Ss_out", [16, 128, 128])
        convs_o = dout("convs_out", [4, 3, 1536])
    kv_o = dout("kv_out", [TT, 256])
    win_o = dout("win_out", [512, 128])
    S_o = dout("S_out", [128, 128])
    conv_o = dout("conv_out", [128, 3, 3])
    CH = min(2048, TT)
    NCH = TT // CH
    omT_o = dout("omT_out", [256, TT], BF16) if not phaseB else None
    xin = [nc.dram_tensor("xin%d" % k, [256, CH], BF16).ap() for k in range(NCH)] if phaseB else None
    b_xin = [Buf() for _ in range(NCH)]
    dbg_o = dout("dbg_out", [128, 1300]) if dbg else None

    st = ExitStack()
    with st:
        fw = FW(nc, st)

        cur = [st]

        def sb(name, shape, dt=F32):
            return cur[0].enter_context(nc.sbuf_tensor(name, list(shape), dt))

        def ps(name, shape, dt=F32):
            return st.enter_context(nc.psum_tensor(name, list(shape), dt))

        def pe(fn, r=(), w=()):
            return fw.op("pe", fn, r, w)

        def act(fn, r=(), w=()):
            return fw.op("act", fn, r, w)

        def dve(fn, r=(), w=()):
            return fw.op("dve", fn, r, w)

        def pool(fn, r=(), w=()):
            return fw.op("pool", fn, r, w)

        def mm(out, lhsT, rhs, start=True, stop=True, r=(), w=()):
            return pe(lambda e: e.matmul(out, lhsT=lhsT, rhs=rhs, start=start, stop=stop), r, w)

        def tr(out, in_, ident, r=(), w=()):
            return pe(lambda e: e.transpose(out, in_, ident), r, w)

        def bcast(ap, shape, axis):
            return ap.unsqueeze(axis).to_broadcast(list(shape))

        ident_f = sb("ident_f", [128, 128]); b_identf = Buf()
        ident_b = sb("ident_b", [128, 128], BF16); b_identb = Buf()
        ones_f2 = sb("ones_f2", [128, 128]); b_onesf2 = Buf()
        hs_res = sb("hs_res", [4, 1024]); b_hsres = Buf()
        stA = st.enter_context(ExitStack())
        cur[0] = stA
        pool(lambda e: e.memset(ones_f2[:], 1.0), w=[b_onesf2])
        ones_b = sb("ones_b", [128, 128], BF16); b_onesb = Buf()
        ones_f = sb("ones_f", [128, 128]); b_onesf = Buf()
        csT = [sb("csT%d" % i_, [128, 2, 4, 32]) for i_ in range(2)]; b_csT = [Buf() for _ in range(2)]
        tri = sb("tri", [128, 2, 128], BF16); b_tri = Buf()
        cmpmask = sb("cmpmask", [128, 17, 128], BF16); b_cmpmask = Buf()
        prel = sb("prel", [128, 256]); b_prel = Buf()
        gmask = sb("gmask", [128, 5, 128]); b_gmask = Buf()
        wtok = sb("wtok", [128, 8, 782], BF16); b_wtok = Buf()
        wgdn = sb("wgdn", [128, 8, 384], BF16); b_wgdn = Buf()
        gmix = sb("gmix", [128, 8]); b_gmix = Buf()
        cw = sb("cw", [128, 12]); b_cw = Buf()
        hsc = sb("hsc", [128, 2]); b_hsc = Buf()
        negA = sb("negA", [128, 1]); b_negA = Buf()
        gnb = sb("gnb", [128, 128]); b_gnb = Buf()
        cmpw = sb("cmpw", [128, 2, 16, 64], BF16); b_cmpw = Buf()
        cmppe = sb("cmppe", [128, 2, 16], BF16); b_cmppe = Buf()

        fw.dma("sp", ident_f[:], ident_d[:, :], writes=[b_identf])
        fw.dma("pool", ident_b[:], ident_d[:, :], writes=[b_identb])
        fw.dma("pool", tri[:].rearrange("p a b -> p (a b)"), tri_d[:, :], writes=[b_tri])
        fw.dma("pool", cmpmask[:].rearrange("p a b -> p (a b)"), cmpmask_d[:, :], writes=[b_cmpmask])
        fw.dma("sp", prel[:], prel_d[:, :], writes=[b_prel])
        fw.dma("sp", gmask[:].rearrange("p a b -> p (a b)"), gmask_d[:, :], writes=[b_gmask])
        fw.dma("sp", gmix[:], gmix_d[:, :], writes=[b_gmix])
        fw.dma("sp", cw[:], cw_d[:, :], writes=[b_cw])
        fw.dma("sp", hsc[:], hsc_d[:, :], writes=[b_hsc])
        fw.dma("sp", gnb[:], gnorm_d[:, :], writes=[b_gnb])
        fw.dma("pool", cmpw[:].rearrange("p a b c -> p (a b c)"), cmpw_d[:, :], writes=[b_cmpw])
        fw.dma("pool", cmppe[:].rearrange("p a b -> p (a b)"), cmppe_d[:, :], writes=[b_cmppe])
        for kt in range(8):
            fw.dma("pool", wtok[:, kt, :], wtok_d[kt * 128:(kt + 1) * 128, :], writes=[b_wtok])
            fw.dma("pool", wgdn[:, kt, :], wgdn_d[kt * 128:(kt + 1) * 128, :], writes=[b_wgdn])
        pool(lambda e: e.memset(ones_b[:], 1.0), w=[b_onesb])
        pool(lambda e: e.memset(ones_f[:], 1.0), w=[b_onesf])
        for kt in range(8):
            dve(lambda e, kt=kt: e.tensor_scalar(out=wtok[:, kt, :], in0=wtok[:, kt, :], scalar1=gmix[:, kt:kt + 1],
                                                 scalar2=None, op0=ALU.mult), r=[b_gmix], w=[b_wtok])
            dve(lambda e, kt=kt: e.tensor_scalar(out=wgdn[:, kt, :], in0=wgdn[:, kt, :], scalar1=gmix[:, kt:kt + 1],
                                                 scalar2=None, op0=ALU.mult), r=[b_gmix], w=[b_wgdn])
        act(lambda e: e.activation(out=negA[:], in_=hsc[:, 0:1], func=AF.Exp), r=[b_hsc], w=[b_negA])
        dve(lambda e: e.tensor_scalar(out=negA[:], in0=negA[:], scalar1=-1.0, scalar2=None, op0=ALU.mult), w=[b_negA])

        if phaseB:
            wgu_s = nc.dram_tensor("wgu_s", [22, 128, 8 * 256], BF16).ap()
            wdn_s = nc.dram_tensor("wdn_s", [22, 128, 1024], BF16).ap()
            b_wgus = [Buf() for _ in range(22)]
            b_wdns = Buf()
            for f in range(22):
                dstv = wgu_s[f].rearrange("p (k c) -> p k c", k=8)
                fw.dma("pool", dstv[:, :, 0:128], wgu_d[:, f * 128:(f + 1) * 128].rearrange("(k p) c -> p k c", p=128), writes=[b_wgus[f]])
                fw.dma("pool", dstv[:, :, 128:256], wgu_d[:, 2816 + f * 128:2816 + (f + 1) * 128].rearrange("(k p) c -> p k c", p=128),
                       writes=[b_wgus[f]])
            fw.dma("pool", wdn_s[:, :, :], wdn_d[:, :].rearrange("(f p) c -> f p c", p=128), writes=[b_wdns])
        KselT = sb("KselT", [128, TT], BF16); b_ksel = [Buf() for _ in range(NT)]; b_eind = Buf()
        Vsel = sb("Vsel", [128, NT, 65], BF16); b_vsel = [Buf() for _ in range(NT)]
        KwinT = sb("KwinT", [64, 8 * 128], BF16); b_kwin = [Buf() for _ in range(8)]
        Vwin = sb("Vwin", [128, 8, 65], BF16); b_vwin = [Buf() for _ in range(8)]
        Rk = sb("Rk", [128, 2, 160], BF16); b_Rk = Buf()
        ckT = sb("ckT", [64, 512], BF16); b_ckT = Buf()
        cvx = sb("cvx", [128, 4, 193], BF16); b_cvx = Buf()
        c2s_f = sb("c2s_f", [128, 4, 128]); b_c2sf = Buf()
        ckb = sb("ckb", [64, 1]); b_ckb = Buf()
        cvb = sb("cvb", [8, 64]); b_cvb = Buf()
        cvrow = sb("cvrow", [1, 64], BF16); b_cvrow = Buf()

        fw.dma("pool", KselT[64:128, :], eind_d[:, :], writes=[b_eind])
        pool(lambda e: e.memset(Vsel[:, :, 64:65], 1.0), w=b_vsel)
        pool(lambda e: e.memset(Vwin[:, :, 64:65], 1.0), w=b_vwin)
        pool(lambda e: e.memset(Rk[:], 0.0), w=[b_Rk])
        pool(lambda e: e.memset(ckT[:], 0.0), w=[b_ckT])
        pool(lambda e: e.memset(cvx[:], 0.0), w=[b_cvx])
        fw.dma("sp", c2s_f[:].rearrange("p a b -> p (a b)"), c2s_d[:, :], writes=[b_c2sf])
        pool(lambda e: e.tensor_copy(out=cvx[:, :, 64:192], in_=c2s_f[:]), r=[b_c2sf], w=[b_cvx])
        pool(lambda e: e.memset(cvx[:, :, 192:193], 1.0), w=[b_cvx])

        PA = ps("PA", [128, 1024]); b_PA = Buf()
        PB = ps("PB", [128, 1024]); b_PB = Buf()
        PC = ps("PC", [128, 1024]); b_PC = Buf()
        PD = ps("PD", [128, 512]); b_PD = Buf()
        PT = ps("PT", [128, 1024], BF16); b_PT = Buf()

        for lp in range(16):
            mm(PD[0:64, 0:1], lhsT=cmpw[:, 0, lp, :], rhs=cmppe[:, 0, lp:lp + 1], start=(lp == 0), stop=(lp == 15),
               r=[b_cmpw, b_cmppe], w=[b_PD])
        act(lambda e: e.copy(out=ckb[:], in_=PD[0:64, 0:1]), r=[b_PD], w=[b_ckb])
        for lp in range(16):
            mm(PD[0:1, 64:128], lhsT=cmppe[:, 1, lp:lp + 1], rhs=cmpw[:, 1, lp, :], start=(lp == 0), stop=(lp == 15),
               r=[b_cmpw, b_cmppe], w=[b_PD])
        act(lambda e: e.copy(out=cvrow[:], in_=PD[0:1, 64:128]), r=[b_PD], w=[b_cvrow])
        mm(PD[0:8, 128:192], lhsT=ones_b[0:1, 0:8], rhs=cvrow[:], r=[b_onesb, b_cvrow], w=[b_PD])
        act(lambda e: e.copy(out=cvb[:], in_=PD[0:8, 128:192]), r=[b_PD], w=[b_cvb])

        NXB = 2
        xt = [sb("xt%d" % i, [128, 1024]) for i in range(NXB)]; b_xt = [Buf() for _ in range(NXB)]
        ssq = [sb("ssq%d" % i, [128, 1]) for i in range(NXB)]; b_ssq = [Buf() for _ in range(NXB)]
        xs = [sb("xs%d" % i, [128, 1024], BF16) for i in range(NXB)]; b_xs = [Buf() for _ in range(NXB)]
        xnT = sb("xnT", [128, 8, 512], BF16); b_xnT = [Buf() for _ in range(4)]
        pj = [sb("pj%d" % i, [128, 782]) for i in range(2)]; b_pj = [Buf() for _ in range(2)]
        rq = [sb("rq%d" % i, [128, 7, 64]) for i in range(2)]; b_rq = [Buf() for _ in range(2)]
        rt = sb("rt", [128, 4, 7, 32]); b_rt = Buf()
        ko = [sb("ko%d" % i, [128, 6, 64]) for i in range(2)]; b_ko = [Buf() for _ in range(2)]
        qkb = sb("qkb", [128, 7, 64], BF16); b_qkb = Buf()
        kvc2 = sb("kvc2", [128, 2, 128], BF16); b_kvc2 = Buf()
        QT = [sb("QT%d" % i_, [64, 512], BF16) for i_ in range(2)]; b_QT = [Buf() for _ in range(2)]
        Qaug = [sb("Qaug%d" % i_, [128, 2, 256], BF16) for i_ in range(2)]; b_Qaug = [Buf() for _ in range(2)]
        gates = [sb("gates%d" % i_, [128, 12]) for i_ in range(2)]; b_gates = [Buf() for _ in range(2)]
        zsil = [sb("zsil%d" % i_, [128, 4, 128]) for i_ in range(2)]; b_zsil = [[Buf() for _ in range(4)] for _ in range(2)]
        abg = [sb("abg%d" % i_, [128, 4, 2]) for i_ in range(2)]; b_abg = [Buf() for _ in range(2)]
        cvnew = sb("cvnew", [8, 64], BF16); b_cvnew = Buf()
        PTc = [sb("PTc%d" % i, [128, 512], BF16) for i in range(4)]; b_PTc = [Buf() for _ in range(4)]
        PTs = [sb("PTs%d" % i, [128, 1024], BF16) for i in range(2)]; b_PTs = [Buf() for _ in range(2)]
        acc_c = sb("acc_c", [128, 4, 193]); b_accc = Buf()
        acc_sw = sb("acc_sw", [128, 4, 65]); b_accsw = Buf()
        rcp = sb("rcp", [128, 12]); b_rcp = Buf()
        imp = sb("imp", [128, 128]); b_imp = Buf()
        score = sb("score", [128, 128]); b_score = Buf()
        mx8 = sb("mx8", [128, 16]); b_mx8 = Buf()
        thr = sb("thr", [128, 1]); b_thr = Buf()
        sc2 = sb("sc2", [128, 128]); b_sc2 = Buf()
        mbt = sb("mbt", [128, 2, 128]); b_mbt = Buf()
        coef = sb("coef", [128, 6]); b_coef = Buf()
        onsa = sb("onsa", [128, 128]); b_onsa = Buf()
        om = sb("om", [128, 256], BF16); b_om = Buf()
        omT = [sb("omT%d" % i_, [128, 2, 512], BF16) for i_ in range(2)]; b_omT = [Buf() for _ in range(2)]
        raw = [sb("raw%d" % i_, [128, 3, 515]) for i_ in range(2)]; b_raw = [Buf() for _ in range(2)]
        cacc = sb("cacc", [128, 3, 512]); b_cacc = Buf()
        csil = cacc; b_csil = b_cacc
        sqb = sb("sqb", [128, 2, 512], BF16); b_sqb = Buf()
        rnorm = sb("rnorm", [128, 2, 512]); b_rnorm = Buf()
        gT = sb("gT", [128, 3, 512], BF16); b_gT = Buf()
        gtok = sb("gtok", [128, 4, 3, 128], BF16); b_gtok = Buf()
        gsc = sb("gsc", [128, 16, 4]); b_gsc = Buf()
        glc = sb("glc", [128, 2, 4]); b_glc = Buf()
        dg1 = sb("dg1", [128, 4, 128]); b_dg1 = Buf()
        dg2 = sb("dg2", [128, 4, 128]); b_dg2 = Buf()
        dgn = sb("dgn", [128, 4, 128]); b_dgn = Buf()
        gmask4 = sb("gmask4", [128, 2, 4, 128]); b_gmask4 = Buf()
        decT = sb("decT", [128, 4, 128], BF16); b_decT = Buf()
        decbT = sb("decbT", [128, 4, 128], BF16); b_decbT = Buf()
        Um = [sb("Um%d" % i, [128, 4, 128], BF16) for i in range(2)]; b_Um = [Buf() for _ in range(2)]
        Lm = [sb("Lm%d" % i, [128, 4, 128], BF16) for i in range(2)]; b_Lm = [Buf() for _ in range(2)]
        Pm = [sb("Pm%d" % i, [128, 4, 128], BF16) for i in range(2)]; b_Pm = [Buf() for _ in range(2)]
        Xm = sb("Xm", [128, 4, 256], BF16); b_Xm = Buf()
        uw = sb("uw", [128, 4, 256], BF16); b_uw = Buf()
        kgm = sb("kgm", [128, 4, 2, 128], BF16); b_kgm = Buf()
        aqkT = sb("aqkT", [128, 4, 128], BF16); b_aqkT = Buf()
        Dg = sb("Dg", [128, 4, 128], BF16); b_Dg = Buf()
        QpA = sb("QpA", [128, 4, 128], BF16); QpB = sb("QpB", [128, 4, 128], BF16); b_Qp = Buf()
        glc8 = sb("glc8", [128, 8]); b_glc8 = Buf()
        MTf = sb("MTf", [128, 8, 128], BF16); b_MTf = Buf()
        MT8 = sb("MT8", [128, 8, 128], BF16); b_MT8 = Buf()
        Sb9 = sb("Sb9", [128, 9, 128], BF16); b_Sb9 = [Buf() for _ in range(9)]
        Sf = sb("Sf", [128, 128]); b_Sf = Buf()
        og4 = sb("og4", [128, 4, 128]); b_og4 = Buf()
        og4q = dgn; b_og4q = b_dgn
        og4s = sb("og4s", [128, 4]); b_og4s = Buf()
        omg = sb("omg", [128, 4, 128], BF16); b_omg = Buf()

        pool(lambda e: e.memset(kgm[:], 0.0), w=[b_kgm])
        for mk_ in range(2):
            pool(lambda e, mk_=mk_: e.tensor_copy(out=gmask4[:, mk_], in_=bcast(gmask[:, mk_, :], [128, 4, 128], 1)), r=[b_gmask], w=[b_gmask4])
        pool(lambda e: e.memset(raw[0][:], 0.0), w=[b_raw[0]])
        pool(lambda e: e.memset(raw[1][:], 0.0), w=[b_raw[1]])
        pool(lambda e: e.memset(QpA[:], 0.0), w=[b_Qp])
        pool(lambda e: e.memset(QpB[:], 0.0), w=[b_Qp])
        pool(lambda e: e.memset(Sb9[:, 0, :], 0.0), w=[b_Sb9[0]])

        G_G, G_BETA, G_GCUM, G_GL, G_EG, G_EKG, G_LNB, G_NEGG, G_SKBG, G_GB = range(10)


        hits = {}

        def chk2(name):
            if STOP == name:
                fw.dead = True

        def chk(name):
            hits[name] = hits.get(name, 0) + 1
            if STOP == name or STOP == "%s@%d" % (name, hits[name]):
                raise _Stop()

        def gdn_gen(grp):
            gp = grp % 2
            for c3 in range(3):
                dve(lambda e, c3=c3: e.tensor_scalar(out=cacc[:, c3, :], in0=raw[gp][:, c3, 0:512], scalar1=cw[:, c3 * 4:c3 * 4 + 1],
                                                      scalar2=None, op0=ALU.mult), r=[b_raw[gp], b_cw], w=[b_cacc])
                for jj in range(1, 4):
                    dve(lambda e, c3=c3, jj=jj: e.scalar_tensor_tensor(out=cacc[:, c3, :], in0=raw[gp][:, c3, jj:jj + 512],
                                                                        scalar=cw[:, c3 * 4 + jj:c3 * 4 + jj + 1], in1=cacc[:, c3, :],
                                                                        op0=ALU.mult, op1=ALU.add), r=[b_raw[gp], b_cw], w=[b_cacc])
            act(lambda e: e.activation(out=csil[:].rearrange("p a b -> p (a b)"), in_=cacc[:].rearrange("p a b -> p (a b)"),
                                       func=AF.Silu), r=[b_cacc], w=[b_csil])
            yield
            act(lambda e: e.activation(out=sqb[:].rearrange("p a b -> p (a b)"), in_=csil[:, 0:2, :].rearrange("p a b -> p (a b)"),
                                       func=AF.Square), r=[b_csil], w=[b_sqb])
            for c3 in range(2):
                mm(PB[:, c3 * 512:(c3 + 1) * 512], lhsT=ones_b[:], rhs=sqb[:, c3, :], r=[b_onesb, b_sqb], w=[b_PB])
            act(lambda e: e.activation(out=rnorm[:].rearrange("p a b -> p (a b)"), in_=PB[:, 0:1024], func=AF.Sqrt, bias=EPS),
                r=[b_PB], w=[b_rnorm])
            dve(lambda e: e.reciprocal(out=rnorm[:].rearrange("p a b -> p (a b)"), in_=rnorm[:].rearrange("p a b -> p (a b)")),
                w=[b_rnorm])
            dve(lambda e: e.scalar_tensor_tensor(out=gT[:, 0, :], in0=csil[:, 0, :], scalar=128.0 ** -0.5, in1=rnorm[:, 0, :],
                                                 op0=ALU.mult, op1=ALU.mult), r=[b_csil, b_rnorm], w=[b_gT])
            dve(lambda e: e.tensor_tensor(out=gT[:, 1, :], in0=csil[:, 1, :], in1=rnorm[:, 1, :], op=ALU.mult),
                r=[b_csil, b_rnorm], w=[b_gT])
            act(lambda e: e.copy(out=gT[:, 2, :], in_=csil[:, 2, :]), r=[b_csil], w=[b_gT])
            for tt in range(4):
                for c3 in range(3):
                    tr(PT[:, c3 * 128:(c3 + 1) * 128], gT[:, c3, tt * 128:(tt + 1) * 128], ident_b[:], r=[b_gT, b_identb], w=[b_PT])
                act(lambda e, tt=tt: e.copy(out=gtok[:, tt, :, :], in_=PT[:, 0:384].rearrange("p (c d) -> p c d", c=3)),
                    r=[b_PT], w=[b_gtok])
            yield
            a_ap = abg[gp][:, :, 0]
            b_ap = abg[gp][:, :, 1]
            act(lambda e: e.activation(out=gsc[:, G_G, :], in_=a_ap, func=AF.Exp, bias=hsc[:, 1:2]), r=[b_abg[gp], b_hsc], w=[b_gsc])
            act(lambda e: e.activation(out=gsc[:, G_G, :], in_=gsc[:, G_G, :], func=AF.Ln, bias=1.0), w=[b_gsc])
            dve(lambda e: e.tensor_scalar(out=gsc[:, G_G, :], in0=gsc[:, G_G, :], scalar1=negA[:, 0:1], scalar2=None, op0=ALU.mult),
                r=[b_negA], w=[b_gsc])
            act(lambda e: e.activation(out=gsc[:, G_BETA, :], in_=b_ap, func=AF.Sigmoid), r=[b_abg[gp]], w=[b_gsc])
            act(lambda e: e.activation(out=gsc[:, G_LNB, :], in_=gsc[:, G_BETA, :], func=AF.Ln), w=[b_gsc])
            mm(PD[:, 0:4], lhsT=gmask[:, 2, :], rhs=gsc[:, G_G, :], r=[b_gmask, b_gsc], w=[b_PD])
            mm(PD[:, 4:8], lhsT=gmask[:, 3, :], rhs=gsc[:, G_G, :], r=[b_gmask, b_gsc], w=[b_PD])
            mm(PD[:, 8:12], lhsT=gmask[:, 4, :], rhs=gsc[:, G_G, :], r=[b_gmask, b_gsc], w=[b_PD])
            act(lambda e: e.copy(out=gsc[:, G_GCUM, :], in_=PD[:, 0:4]), r=[b_PD], w=[b_gsc])
            act(lambda e: e.copy(out=glc[:].rearrange("p a b -> p (a b)"), in_=PD[:, 4:12]), r=[b_PD], w=[b_glc])
            dve(lambda e: e.tensor_copy(out=gsc[0:64, G_GL, :], in_=glc[0:64, 0, :]), r=[b_glc], w=[b_gsc])
            dve(lambda e: e.tensor_copy(out=gsc[64:128, G_GL, :], in_=glc[64:128, 1, :]), r=[b_glc], w=[b_gsc])
            act(lambda e: e.activation(out=gsc[:, G_EG, :], in_=gsc[:, G_GCUM, :], func=AF.Exp), w=[b_gsc])
            dve(lambda e: e.tensor_tensor(out=gsc[:, G_EKG, :], in0=gsc[:, G_GL, :], in1=gsc[:, G_GCUM, :], op=ALU.subtract), w=[b_gsc])
            act(lambda e: e.activation(out=gsc[:, G_EKG, :], in_=gsc[:, G_EKG, :], func=AF.Exp), w=[b_gsc])
            act(lambda e: e.activation(out=glc[:].rearrange("p a b -> p (a b)"), in_=glc[:].rearrange("p a b -> p (a b)"), func=AF.Exp),
                w=[b_glc])
            dve(lambda e: e.tensor_scalar(out=gsc[:, G_NEGG, :], in0=gsc[:, G_GCUM, :], scalar1=-1.0, scalar2=None, op0=ALU.mult), w=[b_gsc])
            dve(lambda e: e.tensor_tensor(out=gsc[:, G_SKBG, :], in0=gsc[:, G_BETA, :], in1=gsc[:, G_EG, :], op=ALU.mult), w=[b_gsc])
            dve(lambda e: e.tensor_scalar(out=gsc[:, G_SKBG, :], in0=gsc[:, G_SKBG, :], scalar1=-1.0, scalar2=None, op0=ALU.mult), w=[b_gsc])
            dve(lambda e: e.tensor_tensor(out=gsc[:, G_GB, :], in0=gsc[:, G_GCUM, :], in1=gsc[:, G_LNB, :], op=ALU.add), w=[b_gsc])

            yield
            dve(lambda e: e.tensor_copy(out=glc8[:, 0:8:2], in_=glc[:, 0, :]), r=[b_glc], w=[b_glc8])
            dve(lambda e: e.tensor_copy(out=glc8[:, 1:8:2], in_=glc[:, 1, :]), r=[b_glc], w=[b_glc8])
            identf4 = bcast(ident_f[:], [128, 4, 128], 1)
            identb4 = bcast(ident_b[:], [128, 4, 128], 1)

            def colb(col):
                return bcast(gsc[:, col, :], [128, 4, 128], 2)
            dve(lambda e: e.tensor_tensor(out=dg1[:], in0=identf4, in1=colb(G_GCUM), op=ALU.mult), r=[b_identf, b_gsc], w=[b_dg1])
            dve(lambda e: e.tensor_tensor(out=dg2[:], in0=identf4, in1=colb(G_GB), op=ALU.mult), r=[b_identf, b_gsc], w=[b_dg2])
            dve(lambda e: e.tensor_tensor(out=dgn[:], in0=identf4, in1=colb(G_NEGG), op=ALU.mult), r=[b_identf, b_gsc], w=[b_dgn])
            for (PSx, bPSx, dgx, mk_) in ((PC[:, 0:512], b_PC, dg1, 0), (PA[:, 0:512], b_PA, dg2, 1)):
                mm(PSx, lhsT=ones_f[:], rhs=dgx[:].rearrange("p a b -> p (a b)"), start=True, stop=False, r=[b_onesf, b_dg1, b_dg2], w=[bPSx])
                mm(PSx, lhsT=ident_f[:], rhs=gmask4[:, mk_].rearrange("p a b -> p (a b)"), start=False, stop=False, r=[b_identf, b_gmask4], w=[bPSx])
                for tt in range(4):
                    mm(PSx[:, tt * 128:(tt + 1) * 128], lhsT=dgn[:, tt, :], rhs=ones_f[:], start=False, stop=True,
                       r=[b_dgn, b_onesf], w=[bPSx])
            act(lambda e: e.activation(out=decT[:].rearrange("p a b -> p (a b)"), in_=PC[:, 0:512], func=AF.Exp), r=[b_PC], w=[b_decT])
            act(lambda e: e.activation(out=decbT[:].rearrange("p a b -> p (a b)"), in_=PA[:, 0:512], func=AF.Exp), r=[b_PA], w=[b_decbT])
            yield
            for tt in range(4):
                kT_t = gT[:, 1, tt * 128:(tt + 1) * 128]
                qT_t = gT[:, 0, tt * 128:(tt + 1) * 128]
                mm(PC[:, 512 + tt * 128:512 + (tt + 1) * 128], lhsT=kT_t, rhs=kT_t, r=[b_gT], w=[b_PC])
                mm(PB[:, tt * 128:(tt + 1) * 128], lhsT=kT_t, rhs=qT_t, r=[b_gT], w=[b_PB])
            dve(lambda e: e.tensor_tensor(out=Um[0][:].rearrange("p a b -> p (a b)"), in0=PC[:, 512:1024], in1=decbT[:].rearrange("p a b -> p (a b)"),
                                          op=ALU.mult), r=[b_PC, b_decbT], w=[b_Um[0]])
            dve(lambda e: e.tensor_tensor(out=aqkT[:].rearrange("p a b -> p (a b)"), in0=PB[:, 0:512], in1=decT[:].rearrange("p a b -> p (a b)"),
                                          op=ALU.mult), r=[b_PB, b_decT], w=[b_aqkT])
            for tt in range(4):
                tr(PT[:, tt * 128:(tt + 1) * 128], Um[0][:, tt, :], ident_b[:], r=[b_Um[0], b_identb], w=[b_PT])
            act(lambda e: e.copy(out=Lm[0][:].rearrange("p a b -> p (a b)"), in_=PT[:, 0:512]), r=[b_PT], w=[b_Lm[0]])
            dve(lambda e: e.tensor_tensor(out=Pm[0][:], in0=identb4, in1=Um[0][:], op=ALU.subtract), r=[b_identb, b_Um[0]], w=[b_Pm[0]])
            yield
            cu, cp = 0, 0
            for lvl in range(5):
                nu = 1 - cu
                for tt in range(4):
                    mm(PC[:, tt * 128:(tt + 1) * 128], lhsT=Um[cu][:, tt, :], rhs=Lm[cu][:, tt, :], r=[b_Um[cu], b_Lm[cu]], w=[b_PC])
                if lvl < 4:
                    for tt in range(4):
                        mm(PA[:, tt * 128:(tt + 1) * 128], lhsT=Lm[cu][:, tt, :], rhs=Um[cu][:, tt, :], r=[b_Um[cu], b_Lm[cu]], w=[b_PA])
                act(lambda e, nu=nu: e.copy(out=Lm[nu][:].rearrange("p a b -> p (a b)"), in_=PC[:, 0:512]), r=[b_PC], w=[b_Lm[nu]])
                if lvl < 4:
                    act(lambda e, nu=nu: e.copy(out=Um[nu][:].rearrange("p a b -> p (a b)"), in_=PA[:, 0:512]), r=[b_PA], w=[b_Um[nu]])
                for tt in range(4):
                    mm(PB[:, tt * 128:(tt + 1) * 128], lhsT=Lm[nu][:, tt, :], rhs=Pm[cp][:, tt, :], r=[b_Lm[nu], b_Pm[cp]], w=[b_PB])
                dve(lambda e, cp=cp: e.tensor_tensor(out=Pm[1 - cp][:].rearrange("p a b -> p (a b)"), in0=PB[:, 0:512],
                                                     in1=Pm[cp][:].rearrange("p a b -> p (a b)"), op=ALU.add), r=[b_PB, b_Pm[cp]], w=[b_Pm[1 - cp]])
                cu = nu
                cp = 1 - cp
            yield
            Tt = Pm[cp]
            bTt = b_Pm[cp]
            dve(lambda e: e.tensor_tensor(out=Xm[:, :, 0:128], in0=gtok[:, :, 2, :], in1=colb(G_BETA), op=ALU.mult), r=[b_gtok, b_gsc], w=[b_Xm])
            dve(lambda e: e.tensor_tensor(out=Xm[:, :, 128:256], in0=gtok[:, :, 1, :], in1=colb(G_SKBG), op=ALU.mult), r=[b_gtok, b_gsc], w=[b_Xm])
            for tt in range(4):
                mm(PC[:, tt * 256:(tt + 1) * 256], lhsT=Tt[:, tt, :], rhs=Xm[:, tt, :], r=[bTt, b_Xm], w=[b_PC])
            act(lambda e: e.copy(out=uw[:].rearrange("p a b -> p (a b)"), in_=PC[:, 0:1024]), r=[b_PC], w=[b_uw])
            dve(lambda e: e.tensor_tensor(out=kgm[0:64, :, 0, :], in0=gtok[0:64, :, 1, :], in1=bcast(gsc[0:64, G_EKG, :], [64, 4, 128], 2), op=ALU.mult),
                r=[b_gtok, b_gsc], w=[b_kgm])
            dve(lambda e: e.tensor_tensor(out=kgm[64:128, :, 1, :], in0=gtok[64:128, :, 1, :], in1=bcast(gsc[64:128, G_EKG, :], [64, 4, 128], 2), op=ALU.mult),
                r=[b_gtok, b_gsc], w=[b_kgm])
            dve(lambda e: e.tensor_tensor(out=Dg[:], in0=identb4, in1=colb(G_EG), op=ALU.mult), r=[b_identb, b_gsc], w=[b_Dg])
            for tt in range(4):
                mm(PA[:, tt * 128:(tt + 1) * 128], lhsT=gtok[:, tt, 0, :], rhs=Dg[:, tt, :], start=True, stop=False, r=[b_gtok, b_Dg], w=[b_PA])
                mm(PA[:, tt * 128:(tt + 1) * 128], lhsT=uw[:, tt, 128:256], rhs=aqkT[:, tt, :], start=False, stop=True, r=[b_uw, b_aqkT], w=[b_PA])
            act(lambda e: e.copy(out=QpA[:, :, 0:64], in_=PA[:, 0:512].rearrange("p (a b) -> p a b", a=4)[:, :, 0:64]), r=[b_PA], w=[b_Qp])
            act(lambda e: e.copy(out=QpB[:, :, 64:128], in_=PA[:, 0:512].rearrange("p (a b) -> p a b", a=4)[:, :, 64:128]), r=[b_PA], w=[b_Qp])
            yield
            for tt in range(4):
                for c in range(2):
                    r0 = 64 * c
                    ch = tt * 2 + c
                    mm(PB[:, ch * 128:(ch + 1) * 128], lhsT=uw[:, tt, 128:256], rhs=kgm[:, tt, c, :], r=[b_uw, b_kgm], w=[b_PB])
            dve(lambda e: e.tensor_tensor(out=MTf[:], in0=bcast(ident_f[:], [128, 8, 128], 1), in1=bcast(glc8[:], [128, 8, 128], 2), op=ALU.mult),
                r=[b_identf, b_glc8], w=[b_MTf])
            dve(lambda e: e.tensor_tensor(out=MT8[:].rearrange("p a b -> p (a b)"), in0=PB[:, 0:1024], in1=MTf[:].rearrange("p a b -> p (a b)"),
                                          op=ALU.add), r=[b_PB, b_MTf], w=[b_MT8])
            for ch in range(8):
                tt, c = ch // 2, ch % 2
                r0 = 64 * c
                i = grp * 4 + tt
                PSc = PC[:, (ch % 2) * 512:(ch % 2) * 512 + 128]
                mm(PSc, lhsT=kgm[:, tt, c, :], rhs=uw[:, tt, 0:128], start=True, stop=False, r=[b_kgm, b_uw], w=[b_PC])
                mm(PSc, lhsT=MT8[:, ch, :], rhs=Sb9[:, ch, :], start=False, stop=True, r=[b_MT8, b_Sb9[ch]], w=[b_PC])
                if ch < 7:
                    act(lambda e, ch=ch, PSc=PSc: e.copy(out=Sb9[:, ch + 1, :], in_=PSc), r=[b_PC], w=[b_Sb9[ch + 1]])
                else:
                    act(lambda e, PSc=PSc: e.copy(out=Sb9[:, 8, :], in_=PSc), r=[b_PC], w=[b_Sb9[8]])
                    if i == NT - 1:
                        act(lambda e, PSc=PSc: e.copy(out=Sf[:], in_=PSc), r=[b_PC], w=[b_Sf])
            yield
            for tt in range(4):
                mm(PA[:, tt * 128:(tt + 1) * 128], lhsT=QpA[:, tt, :], rhs=Sb9[:, 2 * tt, :], start=True, stop=False, r=[b_Qp, b_Sb9[2 * tt]], w=[b_PA])
                mm(PA[:, tt * 128:(tt + 1) * 128], lhsT=QpB[:, tt, :], rhs=Sb9[:, 2 * tt + 1, :], start=False, stop=False,
                   r=[b_Qp, b_Sb9[2 * tt + 1]], w=[b_PA])
                mm(PA[:, tt * 128:(tt + 1) * 128], lhsT=aqkT[:, tt, :], rhs=uw[:, tt, 0:128], start=False, stop=True, r=[b_aqkT, b_uw], w=[b_PA])
            act(lambda e: e.copy(out=Sb9[:, 0, :], in_=Sb9[:, 8, :]), r=[b_Sb9[8]], w=[b_Sb9[0]])
            act(lambda e: e.copy(out=og4[:].rearrange("p a b -> p (a b)"), in_=PA[:, 0:512]), r=[b_PA], w=[b_og4])
            dve(lambda e: e.tensor_tensor(out=og4q[:], in0=og4[:], in1=og4[:], op=ALU.mult), r=[b_og4], w=[b_og4q])
            dve(lambda e: e.tensor_reduce(out=og4s[:], in_=og4q[:], axis=AX.X, op=ALU.add), r=[b_og4q], w=[b_og4s])
            act(lambda e: e.activation(out=og4s[:], in_=og4s[:], func=AF.Sqrt, scale=1.0 / 128, bias=EPS), w=[b_og4s])
            dve(lambda e: e.reciprocal(out=og4s[:], in_=og4s[:]), w=[b_og4s])
            dve(lambda e: e.tensor_tensor(out=og4[:], in0=og4[:], in1=bcast(og4s[:], [128, 4, 128], 2), op=ALU.mult), r=[b_og4s], w=[b_og4])
            dve(lambda e: e.tensor_tensor(out=og4[:], in0=og4[:], in1=bcast(gnb[:], [128, 4, 128], 1), op=ALU.mult), r=[b_gnb], w=[b_og4])
            dve(lambda e: e.tensor_tensor(out=omg[:], in0=og4[:], in1=zsil[gp][:], op=ALU.mult), r=[b_og4] + b_zsil[gp], w=[b_omg])
            for tt in range(4):
                tr(PT[:, tt * 128:(tt + 1) * 128], omg[:, tt, :], ident_b[:], r=[b_omg, b_identb], w=[b_PT])
            act(lambda e: e.copy(out=omT[gp][:, 1, :], in_=PT[:, 0:512]), r=[b_PT], w=[b_omT[gp]])
            for hh in range(2):
                if phaseB:
                    kch = (grp * 512) // CH
                    oc = (grp * 512) % CH
                    fw.dma("sp", xin[kch][hh * 128:(hh + 1) * 128, oc:oc + 512], omT[gp][:, hh, :], reads=[b_omT[gp]], writes=[b_xin[kch]])
                else:
                    fw.dma("sp", omT_o[hh * 128:(hh + 1) * 128, grp * 512:(grp + 1) * 512], omT[gp][:, hh, :], reads=[b_omT[gp]])


        def gen_G(grp):
            gp2 = grp % 2
            fw.dma("sp", csT[gp2][:, 0].rearrange("p a b -> p (a b)"), cos_d[:, grp * 128:(grp + 1) * 128], writes=[b_csT[gp2]])
            fw.dma("sp", csT[gp2][:, 1].rearrange("p a b -> p (a b)"), sin_d[:, grp * 128:(grp + 1) * 128], writes=[b_csT[gp2]])
            for tt in range(4):
                i = grp * 4 + tt
                s = i % NXB
                fw.dma("sp", xt[s][:], x_d[i * 128:(i + 1) * 128, :], writes=[b_xt[s]])
                act(lambda e, s=s: e.activation(out=xs[s][:], in_=xt[s][:], func=AF.Square, accum_out=ssq[s][:]),
                    r=[b_xt[s]], w=[b_xs[s], b_ssq[s]])
                act(lambda e, s=s: e.activation(out=ssq[s][:], in_=ssq[s][:], func=AF.Sqrt, scale=1.0 / 1024, bias=EPS), w=[b_ssq[s]])
                dve(lambda e, s=s: e.reciprocal(out=ssq[s][:], in_=ssq[s][:]), w=[b_ssq[s]])
                dve(lambda e, s=s: e.tensor_scalar(out=xs[s][:], in0=xt[s][:], scalar1=ssq[s][:, 0:1], scalar2=None,
                                                   op0=ALU.mult), r=[b_xt[s], b_ssq[s]], w=[b_xs[s]])
                for kt in range(8):
                    tr(PT[:, kt * 128:(kt + 1) * 128], xs[s][:, kt * 128:(kt + 1) * 128], ident_b[:],
                       r=[b_xs[s], b_identb], w=[b_PT])
                act(lambda e, tt=tt: e.copy(out=xnT[:, :, tt * 128:(tt + 1) * 128],
                                            in_=PT[:].rearrange("p (k t) -> p k t", k=8)),
                    r=[b_PT], w=[b_xnT[tt]])

            yield
            pool(lambda e: e.tensor_copy(out=raw[gp2][:, :, 0:3], in_=raw[1 - gp2][:, :, 512:515]), r=[b_raw[1 - gp2]], w=[b_raw[gp2]])
            for c3 in range(3):
                for kt in range(8):
                    mm(PB[:, 0:512], lhsT=wgdn[:, kt, c3 * 128:(c3 + 1) * 128], rhs=xnT[:, kt, :],
                       start=(kt == 0), stop=(kt == 7), r=[b_wgdn] + b_xnT, w=[b_PB])
                act(lambda e, c3=c3: e.copy(out=raw[gp2][:, c3, 3:515], in_=PB[:, 0:512]), r=[b_PB], w=[b_raw[gp2]])

            yield

        def gen_F(i):
            grp = i // 4
            tt = i % 4
            gp2 = grp % 2
            p2 = i % 2
            for kt in range(8):
                mm(PA[:, 0:512], lhsT=xnT[:, kt, tt * 128:(tt + 1) * 128], rhs=wtok[:, kt, 0:512],
                   start=(kt == 0), stop=(kt == 7), r=[b_xnT[tt], b_wtok], w=[b_PA])
            for kt in range(8):
                mm(PA[:, 512:782], lhsT=xnT[:, kt, tt * 128:(tt + 1) * 128], rhs=wtok[:, kt, 512:782],
                   start=(kt == 0), stop=(kt == 7), r=[b_xnT[tt], b_wtok], w=[b_PA])
            act(lambda e: e.copy(out=pj[p2][:], in_=PA[:, 0:782]), r=[b_PA], w=[b_pj[p2]])
            yield
            x1 = pj[p2][:, 0:448].rearrange("p (h d) -> p h d", h=7)[:, :, 0:32]
            x2 = pj[p2][:, 0:448].rearrange("p (h d) -> p h d", h=7)[:, :, 32:64]
            cosb = bcast(csT[gp2][:, 0, tt, :], [128, 7, 32], 1)
            sinb = bcast(csT[gp2][:, 1, tt, :], [128, 7, 32], 1)
            dve(lambda e: e.tensor_tensor(out=rt[:, 0], in0=x1, in1=cosb, op=ALU.mult), r=[b_pj[p2], b_csT[gp2]], w=[b_rt])
            dve(lambda e: e.tensor_tensor(out=rt[:, 1], in0=x2, in1=sinb, op=ALU.mult), r=[b_pj[p2], b_csT[gp2]], w=[b_rt])
            dve(lambda e: e.tensor_tensor(out=rt[:, 2], in0=x2, in1=cosb, op=ALU.mult), r=[b_pj[p2], b_csT[gp2]], w=[b_rt])
            dve(lambda e: e.tensor_tensor(out=rt[:, 3], in0=x1, in1=sinb, op=ALU.mult), r=[b_pj[p2], b_csT[gp2]], w=[b_rt])
            dve(lambda e: e.tensor_tensor(out=rq[p2][:, :, 0:32], in0=rt[:, 0], in1=rt[:, 1], op=ALU.subtract),
                 r=[b_rt], w=[b_rq[p2]])
            dve(lambda e: e.tensor_tensor(out=rq[p2][:, :, 32:64], in0=rt[:, 2], in1=rt[:, 3], op=ALU.add),
                 r=[b_rt], w=[b_rq[p2]])
            yield
            pool(lambda e: e.tensor_copy(out=ko[p2][:, 0:6:2, :], in_=rq[p2][:, 4:7, :]), r=[b_rq[p2]], w=[b_ko[p2]])
            pool(lambda e: e.tensor_copy(out=ko[p2][:, 1:6:2, :],
                                         in_=pj[p2][:, 448:640].rearrange("p (h d) -> p h d", h=3)),
                 r=[b_pj[p2]], w=[b_ko[p2]])
            fw.dma("sp", kv_o[i * 128:(i + 1) * 128, :], ko[p2][:, 0:4, :].rearrange("p a b -> p (a b)"), reads=[b_ko[p2]])
            if i >= NT - 4:
                wi = i - (NT - 4)
                fw.dma("sp", win_o[wi * 128:(wi + 1) * 128, :], ko[p2][:, 4:6, :].rearrange("p a b -> p (a b)"),
                       reads=[b_ko[p2]])
            act(lambda e: e.copy(out=qkb[:], in_=rq[p2][:]), r=[b_rq[p2]], w=[b_qkb])
            act(lambda e: e.copy(out=Vsel[:, i, 0:64], in_=pj[p2][:, 512:576]), r=[b_pj[p2]], w=[b_vsel[i]])
            act(lambda e: e.copy(out=Vwin[:, i % 8, 0:64], in_=pj[p2][:, 576:640]), r=[b_pj[p2]], w=[b_vwin[i % 8]])
            dve(lambda e: e.tensor_copy(out=kvc2[:, 0, :].rearrange("p (a d) -> p a d", a=2),
                                         in_=bcast(rq[p2][:, 4, :], [128, 2, 64], 1)), r=[b_rq[p2]], w=[b_kvc2])
            dve(lambda e: e.tensor_copy(out=kvc2[:, 1, :].rearrange("p (a d) -> p a d", a=2),
                                         in_=bcast(pj[p2][:, 448:512], [128, 2, 64], 1)), r=[b_pj[p2]], w=[b_kvc2])
            act(lambda e: e.activation(out=gates[i % 2][:], in_=pj[p2][:, 640:652], func=AF.Sigmoid), r=[b_pj[p2]], w=[b_gates[i % 2]])
            act(lambda e, tt=tt: e.activation(out=zsil[gp2][:, tt, :], in_=pj[p2][:, 652:780], func=AF.Silu),
                r=[b_pj[p2]], w=[b_zsil[gp2][tt]])
            pool(lambda e, tt=tt: e.tensor_copy(out=abg[gp2][:, tt, :], in_=pj[p2][:, 780:782]), r=[b_pj[p2]], w=[b_abg[gp2]])
            yield
            for h in range(4):
                tr(PT[0:64, h * 128:(h + 1) * 128], qkb[:, h, :], ident_b[:], r=[b_qkb, b_identb], w=[b_PT])
            tr(PT[0:64, 512:640], qkb[:, 5, :], ident_b[:], r=[b_qkb, b_identb], w=[b_PT])
            tr(PT[0:64, 640:768], qkb[:, 6, :], ident_b[:], r=[b_qkb, b_identb], w=[b_PT])
            tr(PT[:, 768:896], kvc2[:, 0, :], ident_b[:], r=[b_kvc2, b_identb], w=[b_PT])
            tr(PT[:, 896:1024], kvc2[:, 1, :], ident_b[:], r=[b_kvc2, b_identb], w=[b_PT])
            act(lambda e: e.copy(out=QT[i % 2][:], in_=PT[0:64, 0:512]), r=[b_PT], w=[b_QT[i % 2]])
            act(lambda e: e.copy(out=Qaug[i % 2][0:64, 0, :], in_=PT[0:64, 0:256]), r=[b_PT], w=[b_Qaug[i % 2]])
            act(lambda e: e.copy(out=Qaug[i % 2][0:64, 1, :], in_=PT[0:64, 0:256]), r=[b_PT], w=[b_Qaug[i % 2]])
            act(lambda e: e.copy(out=KselT[0:64, i * 128:(i + 1) * 128], in_=PT[0:64, 512:640]),
                r=[b_PT], w=[b_ksel[i]])
            act(lambda e: e.copy(out=KwinT[0:64, (i % 8) * 128:(i % 8 + 1) * 128], in_=PT[0:64, 640:768]),
                r=[b_PT], w=[b_kwin[i % 8]])
            yield
            pool(lambda e: e.tensor_copy(out=Rk[:, :, 0:32], in_=Rk[:, :, 128:160]), w=[b_Rk])
            act(lambda e: e.copy(out=Rk[0:64, :, 32:160], in_=PT[0:64, 768:1024].rearrange("p (a t) -> p a t", a=2)),
                r=[b_PT], w=[b_Rk])
            act(lambda e: e.copy(out=Rk[64:128, :, 31:159], in_=PT[64:128, 768:1024].rearrange("p (a t) -> p a t", a=2)),
                r=[b_PT], w=[b_Rk])
            yield
            m0 = 1 if i == 0 else 0
            nb = 8 - m0
            n0 = 8 * i - 1 + m0
            for lp in range(16):
                c0 = 16 + 2 * lp + 16 * m0
                mm(PD[0:64, 0:nb], lhsT=cmpw[:, 0, lp, :], rhs=Rk[:, 0, c0:c0 + 16 * (nb - 1) + 1:16],
                   start=(lp == 0), stop=(lp == 15), r=[b_cmpw, b_Rk], w=[b_PD])
            act(lambda e: e.activation(out=ckT[:, n0:n0 + nb], in_=PD[0:64, 0:nb], func=AF.Identity, bias=ckb[:, 0:1]),
                r=[b_PD, b_ckb], w=[b_ckT])
            for lp in range(16):
                c0 = 16 + 2 * lp + 16 * m0
                mm(PD[0:nb, 64:128], lhsT=Rk[:, 1, c0:c0 + 16 * (nb - 1) + 1:16], rhs=cmpw[:, 1, lp, :],
                   start=(lp == 0), stop=(lp == 15), r=[b_cmpw, b_Rk], w=[b_PD])
            dve(lambda e: e.tensor_tensor(out=cvnew[0:nb, :], in0=PD[0:nb, 64:128], in1=cvb[0:nb, :], op=ALU.add),
                r=[b_PD, b_cvb], w=[b_cvnew])
            segs = []
            n = n0
            while n < n0 + nb:
                jt = n // 128
                cnt = min(n0 + nb - n, (jt + 1) * 128 - n)
                segs.append((n, cnt))
                n += cnt
            for (ns, cnt) in segs:
                fw.dma("sp", cvx[ns % 128:ns % 128 + cnt, ns // 128, 0:64], cvnew[ns - n0:ns - n0 + cnt, :],
                       reads=[b_cvnew], writes=[b_cvx])

            yield

        def gen_B(i):
            grp = i // 4
            tt = i % 4
            gp2 = grp % 2
            p2 = i % 2
            njt = (8 * i + 6) // 128 + 1
            for jt in range(njt):
                pb = jt
                mm(PA[:, 0:512], lhsT=ckT[:, jt * 128:(jt + 1) * 128], rhs=QT[i % 2][:], r=[b_ckT, b_QT[i % 2]], w=[b_PA])
                act(lambda e, pb=pb: e.activation(out=PTc[pb][:], in_=PA[:, 0:512], func=AF.Exp, scale=0.125),
                    r=[b_PA], w=[b_PTc[pb]])
                mk = None
                if jt == njt - 1:
                    mk = i % 16
                elif jt == njt - 2 and i % 16 == 0:
                    mk = 16
                if mk is not None:
                    dve(lambda e, mk=mk, pb=pb: e.tensor_tensor(out=PTc[pb][:].rearrange("p (h q) -> p h q", h=4),
                                                                in0=PTc[pb][:].rearrange("p (h q) -> p h q", h=4),
                                                                in1=bcast(cmpmask[:, mk, :], [128, 4, 128], 1), op=ALU.mult),
                        r=[b_cmpmask], w=[b_PTc[pb]])
            for h in range(4):
                for jt in range(njt):
                    mm(PC[:, (h // 2) * 512 + (h % 2) * 193:(h // 2) * 512 + (h % 2) * 193 + 193],
                       lhsT=PTc[jt][:, h * 128:(h + 1) * 128], rhs=cvx[:, jt, :],
                       start=(jt == 0), stop=(jt == njt - 1), r=[b_PTc[jt], b_cvx], w=[b_PC])
            act(lambda e: e.copy(out=acc_c[:, 0:2, :], in_=PC[:, 0:386].rearrange("p (h c) -> p h c", h=2)),
                r=[b_PC], w=[b_accc])
            act(lambda e: e.copy(out=acc_c[:, 2:4, :], in_=PC[:, 512:898].rearrange("p (h c) -> p h c", h=2)),
                r=[b_PC], w=[b_accc])
            yield
            dve(lambda e: e.tensor_scalar(out=rcp[:, 0:4], in0=acc_c[:, :, 192], scalar1=1e-30, scalar2=None, op0=ALU.max),
                r=[b_accc], w=[b_rcp])
            dve(lambda e: e.reciprocal(out=rcp[:, 0:4], in_=rcp[:, 0:4]), w=[b_rcp])
            dve(lambda e: e.tensor_scalar(out=imp[:], in0=acc_c[:, 0, 64:192], scalar1=rcp[:, 0:1], scalar2=None, op0=ALU.mult),
                r=[b_accc, b_rcp], w=[b_imp])
            for h in range(1, 4):
                dve(lambda e, h=h: e.scalar_tensor_tensor(out=imp[:], in0=acc_c[:, h, 64:192], scalar=rcp[:, h:h + 1],
                                                          in1=imp[:], op0=ALU.mult, op1=ALU.add),
                    r=[b_accc, b_rcp], w=[b_imp])
            yield
            dve(lambda e: e.tensor_tensor(out=score[:], in0=imp[:], in1=prel[:, 128 - 2 * i:256 - 2 * i], op=ALU.add),
                r=[b_imp, b_prel], w=[b_score])
            dve(lambda e: e.tensor_scalar(out=score[:, 0:1], in0=score[:, 0:1], scalar1=1e4, scalar2=None, op0=ALU.add),
                w=[b_score])
            yield
            dve(lambda e: e.max(out=mx8[:, 0:8], in_=score[:]), r=[b_score], w=[b_mx8])
            dve(lambda e: e.match_replace(out=sc2[:], in_to_replace=mx8[:, 0:8], in_values=score[:], imm_value=-3e38),
                r=[b_score, b_mx8], w=[b_sc2])
            dve(lambda e: e.max(out=mx8[:, 8:16], in_=sc2[:]), r=[b_sc2], w=[b_mx8])
            dve(lambda e: e.tensor_reduce(out=thr[:], in_=mx8[:, 8:16], axis=AX.X, op=ALU.min), r=[b_mx8], w=[b_thr])
            dve(lambda e: e.tensor_scalar(out=sc2[:], in0=score[:], scalar1=thr[:, 0:1], scalar2=None, op0=ALU.is_ge),
                r=[b_score, b_thr], w=[b_sc2])
            dve(lambda e: e.scalar_tensor_tensor(out=sc2[:], in0=score[:], scalar=-1e29, in1=sc2[:],
                                                 op0=ALU.is_gt, op1=ALU.mult), r=[b_score], w=[b_sc2])
            dve(lambda e: e.tensor_scalar(out=mbt[:, 0, :], in0=sc2[:], scalar1=-NEGB, scalar2=NEGB,
                                          op0=ALU.mult, op1=ALU.add), r=[b_sc2], w=[b_mbt])
            dve(lambda e: e.tensor_copy(out=mbt[:, 1, 0:64], in_=mbt[:, 0, 64:128]), w=[b_mbt])
            dve(lambda e: e.tensor_copy(out=mbt[:, 1, 64:128], in_=mbt[:, 0, 0:64]), w=[b_mbt])
            yield
            mm(PD[:, 0:128], lhsT=mbt[:, 1, :], rhs=ident_f[:], r=[b_mbt, b_identf], w=[b_PD])
            mm(PD[:, 128:256], lhsT=mbt[:, 0, :], rhs=ident_f[:], r=[b_mbt, b_identf], w=[b_PD])
            for hh_ in range(2):
                dve(lambda e, hh_=hh_: e.tensor_copy(out=Qaug[i % 2][64:128, 0, hh_ * 128:(hh_ + 1) * 128], in_=PD[64:128, 0:128]),
                    r=[b_PD], w=[b_Qaug[i % 2]])
                dve(lambda e, hh_=hh_: e.tensor_copy(out=Qaug[i % 2][64:128, 1, hh_ * 128:(hh_ + 1) * 128], in_=PD[64:128, 128:256]),
                    r=[b_PD], w=[b_Qaug[i % 2]])

            yield
            sgroups = []
            t = 0
            while t <= i:
                gt_ = min(4, i + 1 - t)
                sgroups.append((t, gt_))
                t += gt_

            def emit_S(gi_):
                t_, gt_ = sgroups[gi_]
                PSs_ = PA if gi_ % 2 == 0 else PB
                bPSs_ = b_PA if gi_ % 2 == 0 else b_PB
                for u in range(gt_):
                    tk = t_ + u
                    ab = 0 if tk < 32 else 1
                    mm(PSs_[:, u * 256:(u + 1) * 256], lhsT=KselT[:, tk * 128:(tk + 1) * 128], rhs=Qaug[i % 2][:, ab, :],
                       r=[b_ksel[tk], b_eind, b_Qaug[i % 2]], w=[bPSs_])
            emit_S(0)
            for gi in range(len(sgroups)):
                t, gt_ = sgroups[gi]
                pb = gi % 2
                PSs = PA if pb == 0 else PB
                bPSs = b_PA if pb == 0 else b_PB
                if gi + 1 < len(sgroups):
                    emit_S(gi + 1)
                yield
                act(lambda e, PSs=PSs, gt_=gt_, pb=pb: e.activation(out=PTs[pb][:, 0:gt_ * 256], in_=PSs[:, 0:gt_ * 256],
                                                                  func=AF.Exp, scale=0.125), r=[bPSs], w=[b_PTs[pb]])
                if t + gt_ - 1 == i:
                    u = gt_ - 1
                    dve(lambda e, u=u, pb=pb: e.tensor_tensor(
                        out=PTs[pb][:, u * 256:(u + 1) * 256].rearrange("p (h q) -> p h q", h=2),
                        in0=PTs[pb][:, u * 256:(u + 1) * 256].rearrange("p (h q) -> p h q", h=2),
                        in1=bcast(tri[:, 0, :], [128, 2, 128], 1), op=ALU.mult), r=[b_tri], w=[b_PTs[pb]])
                for u in range(gt_):
                    tk = t + u
                    for h in range(2):
                        mm(PC[:, h * 512:h * 512 + 65], lhsT=PTs[pb][:, u * 256 + h * 128:u * 256 + (h + 1) * 128],
                           rhs=Vsel[:, tk, :], start=(tk == 0), stop=(tk == i), r=[b_PTs[pb], b_vsel[tk]], w=[b_PC])
            gi = len(sgroups)
            yield
            act(lambda e: e.copy(out=acc_sw[:, 0, :], in_=PC[:, 0:65]), r=[b_PC], w=[b_accsw])
            act(lambda e: e.copy(out=acc_sw[:, 1, :], in_=PC[:, 512:577]), r=[b_PC], w=[b_accsw])

            yield
            t0w = max(0, i - 4)
            wt = list(range(t0w, i + 1))
            pb = gi % 2
            PSs = PA if pb == 0 else PB
            bPSs = b_PA if pb == 0 else b_PB
            for gsub in range(0, len(wt), 4):
                sub = wt[gsub:gsub + 4]
                for u, tk in enumerate(sub):
                    mm(PSs[:, u * 256:(u + 1) * 256], lhsT=KwinT[:, (tk % 8) * 128:(tk % 8 + 1) * 128], rhs=QT[i % 2][:, 0:256],
                       r=[b_kwin[tk % 8], b_QT[i % 2]], w=[bPSs])
                act(lambda e, PSs=PSs, n_=len(sub), pb=pb: e.activation(out=PTs[pb][:, 0:n_ * 256], in_=PSs[:, 0:n_ * 256],
                                                                       func=AF.Exp, scale=0.125), r=[bPSs], w=[b_PTs[pb]])
                for u, tk in enumerate(sub):
                    mkk = None
                    if tk == i:
                        mkk = 0
                    elif tk == i - 4:
                        mkk = 1
                    if mkk is not None:
                        dve(lambda e, u=u, pb=pb, mkk=mkk: e.tensor_tensor(
                            out=PTs[pb][:, u * 256:(u + 1) * 256].rearrange("p (h q) -> p h q", h=2),
                            in0=PTs[pb][:, u * 256:(u + 1) * 256].rearrange("p (h q) -> p h q", h=2),
                            in1=bcast(tri[:, mkk, :], [128, 2, 128], 1), op=ALU.mult), r=[b_tri], w=[b_PTs[pb]])
                for u, tk in enumerate(sub):
                    for h in range(2):
                        mm(PC[:, 386 + h * 512:451 + h * 512], lhsT=PTs[pb][:, u * 256 + h * 128:u * 256 + (h + 1) * 128],
                           rhs=Vwin[:, tk % 8, :], start=(tk == wt[0]), stop=(tk == i), r=[b_PTs[pb], b_vwin[tk % 8]], w=[b_PC])
                pb = 1 - pb
                PSs = PA if pb == 0 else PB
                bPSs = b_PA if pb == 0 else b_PB
            act(lambda e: e.copy(out=acc_sw[:, 2, :], in_=PC[:, 386:451]), r=[b_PC], w=[b_accsw])
            act(lambda e: e.copy(out=acc_sw[:, 3, :], in_=PC[:, 898:963]), r=[b_PC], w=[b_accsw])

            yield
            dve(lambda e: e.tensor_scalar(out=rcp[:, 4:8], in0=acc_sw[:, :, 64], scalar1=1e-30, scalar2=None, op0=ALU.max),
                r=[b_accsw], w=[b_rcp])
            dve(lambda e: e.reciprocal(out=rcp[:, 4:8], in_=rcp[:, 4:8]), w=[b_rcp])
            g3 = gates[i % 2][:, 0:6].rearrange("p (h j) -> p h j", h=2)
            cf = coef[:].rearrange("p (h j) -> p h j", h=2)
            dve(lambda e: e.tensor_tensor(out=cf[:, :, 0], in0=g3[:, :, 0], in1=rcp[:, 0:2], op=ALU.mult),
                r=[b_gates[i % 2], b_rcp], w=[b_coef])
            dve(lambda e: e.tensor_tensor(out=cf[:, :, 1], in0=g3[:, :, 1], in1=rcp[:, 4:6], op=ALU.mult),
                r=[b_gates[i % 2], b_rcp], w=[b_coef])
            dve(lambda e: e.tensor_tensor(out=cf[:, :, 2], in0=g3[:, :, 2], in1=rcp[:, 6:8], op=ALU.mult),
                r=[b_gates[i % 2], b_rcp], w=[b_coef])
            for h in range(2):
                dve(lambda e, h=h: e.tensor_scalar(out=onsa[:, h * 64:(h + 1) * 64], in0=acc_c[:, h, 0:64],
                                                   scalar1=coef[:, 3 * h:3 * h + 1], scalar2=None, op0=ALU.mult),
                    r=[b_accc, b_coef], w=[b_onsa])
                dve(lambda e, h=h: e.scalar_tensor_tensor(out=onsa[:, h * 64:(h + 1) * 64], in0=acc_sw[:, h, 0:64],
                                                          scalar=coef[:, 3 * h + 1:3 * h + 2], in1=onsa[:, h * 64:(h + 1) * 64],
                                                          op0=ALU.mult, op1=ALU.add), r=[b_accsw, b_coef], w=[b_onsa])
                dve(lambda e, h=h: e.scalar_tensor_tensor(out=om[:, h * 64:(h + 1) * 64], in0=acc_sw[:, 2 + h, 0:64],
                                                          scalar=coef[:, 3 * h + 2:3 * h + 3], in1=onsa[:, h * 64:(h + 1) * 64],
                                                          op0=ALU.mult, op1=ALU.add), r=[b_accsw, b_coef, b_onsa], w=[b_om])
            b_om_tiles = None

            if dbg and i == 1:
                fw.dma("sp", dbg_o[:, 0:772], acc_c[:].rearrange("p a b -> p (a b)"), reads=[b_accc])
                fw.dma("sp", dbg_o[:, 772:1032], acc_sw[:].rearrange("p a b -> p (a b)"), reads=[b_accsw])
                fw.dma("sp", dbg_o[:, 1032:1160], score[:], reads=[b_score])
                fw.dma("sp", dbg_o[:, 1160:1172], gates[i % 2][:], reads=[b_gates[i % 2]])
                fw.dma("sp", dbg_o[:, 1172:1300], mbt[:, 0, :], reads=[b_mbt])
            yield
            tr(PT[:, 0:128], om[:, 0:128], ident_b[:], r=[b_om, b_identb], w=[b_PT])
            act(lambda e, tt=tt: e.copy(out=omT[gp2][:, 0, tt * 128:(tt + 1) * 128], in_=PT[:, 0:128]), r=[b_PT], w=[b_omT[gp2]])
            yield

        gdn_prev = None
        try:
          chk("setup")
          def drain_gen(g_):
              if g_ is not None:
                  for _ in g_:
                      pass

          def interleave(gens):
              alive = [g_ for g_ in gens if g_ is not None]
              while alive:
                  for g_ in list(alive):
                      try:
                          next(g_)
                      except StopIteration:
                          alive.remove(g_)

          drain_gen(gen_G(0))
          drain_gen(gen_F(0))
          for i in range(NT):
              grp = i // 4
              gF = None
              if i + 1 < NT:
                  def chain_next(i=i):
                      if (i + 1) % 4 == 0:
                          yield from gen_G((i + 1) // 4)
                      yield from gen_F(i + 1)
                  gF = chain_next()
              interleave([gen_B(i), gF])
              if gdn_prev is not None:
                  for _ in range(4):
                      next(gdn_prev, None)
              if i % 4 == 3:
                  drain_gen(gdn_prev)
                  gdn_prev = gdn_gen(grp)

        except _Stop:
            pass
        if gdn_prev is not None:
            for _ in gdn_prev:
                pass
        fw.dma("sp", S_o[:, :], Sf[:], reads=[b_Sf])
        fw.dma("sp", conv_o[:, :, :], raw[(NG - 1) % 2][:, :, 512:515], reads=[b_raw[(NG - 1) % 2]])

        if phaseB:
            xout = [nc.dram_tensor("xout%d" % k, [1024, CH], BF16).ap() for k in range(NCH)]
            b_xout = Buf()
            RG = [[0, 1, 2, 3], [4, 5, 6, 7]]
            ccs = st.enter_context(nc.semaphore("ccs"))
            for k in range(NCH if not SKIP_CC else 0):
                fw._wait("pool", b_xin[k].w)
                nc.gpsimd.collective_compute("AllGather", ALU.bypass, replica_groups=RG, ins=[xin[k][:, :].opt()],
                                             outs=[xout[k][:, :].opt()]).then_inc(ccs)
                nc.gpsimd.wait_ge(ccs, k + 1)
            pool(lambda e: e.memset(cvnew[0:1, 0:1], 0.0), w=[b_xout, b_cvnew])
            fw.barrier()
            stA.close()
            stS = st.enter_context(ExitStack())
            cur[0] = stS
            nb_ = [0]

            def T(shape, dt=F32):
                nb_[0] += 1
                return sb("sm%d" % nb_[0], shape, dt), Buf()

            xs_t, b_xs = T([4, 1024])
            gmix2, b_gmix2 = T([128, 8])
            ropes, b_ropes = T([4, 64])
            ptab, b_ptab = T([128, 256], I32)
            iota_c, b_iota = T([128, 1])
            idxs, b_idxs = T([128, 256], I32)
            cmpw64, b_cmpw64 = T([128, 2, 32, 64], BF16)
            pe64, b_pe64 = T([128, 2, 32], BF16)
            c2s2, b_c2s2 = T([128, 4, 128])
            on511, b_on511 = T([128, 4])
            oh4, b_oh4 = T([4, 80])
            bonus, b_bonus = T([1, 128])
            alogb, b_alogb = T([4, 8])
            gnrow, b_gnrow = T([1, 128])
            pjs, b_pjs = T([4, 3360])
            Xs, b_Xs = T([4, 2056])
            QKT, b_QKT = T([128, 14, 4], BF16)
            OGT, b_OGT = T([128, 4, 4], BF16)
            one11, b_one11 = T([1, 1])
            cb_s, b_cbs = T([64, 2])
            stSg = st.enter_context(ExitStack())
            cur[0] = stSg
            cwb, b_cwb = T([4, 4, 1536])
            stc, b_stc = T([4, 3, 1536])
            fw.dma("sp", xs_t[:], xs_d[:, :], writes=[b_xs])
            fw.dma("sp", gmix2[:], gmix_d[:, :], writes=[b_gmix2])
            fw.dma("sp", ropes[:], ropes_d[:, :], writes=[b_ropes])
            fw.dma("sp", ptab[:], ptab_d[:, :], writes=[b_ptab])
            fw.dma("sp", iota_c[:], iota_d[:, :], writes=[b_iota])
            fw.dma("pool", cmpw64[:].rearrange("p a b c -> p (a b c)"), cmpw64_d[:, :], writes=[b_cmpw64])
            fw.dma("pool", pe64[:].rearrange("p a b -> p (a b)"), pe64_d[:, :], writes=[b_pe64])
            fw.dma("sp", c2s2[:].rearrange("p a b -> p (a b)"), c2s_d[:, :], writes=[b_c2s2])
            fw.dma("sp", on511[:], ones511_d[:, :], writes=[b_on511])
            fw.dma("sp", oh4[:], oh4_d[:, :], writes=[b_oh4])
            fw.dma("sp", bonus[:], bonus_d[:, :], writes=[b_bonus])
            fw.dma("sp", alogb[:], alogb_d[:, :], writes=[b_alogb])
            fw.dma("sp", gnrow[:], gnrow_d[:, :], writes=[b_gnrow])
            fw.dma("sp", cwb[:].rearrange("p a b -> p (a b)"), convwb_d[:, :, :].rearrange("p a b -> p (a b)"), writes=[b_cwb])
            fw.dma("sp", stc[:].rearrange("p a b -> p (a b)"), gconv_d[:, :, :].rearrange("p a b -> p (a b)"), writes=[b_stc])
            dve(lambda e: e.tensor_scalar(out=idxs[:], in0=ptab[:], scalar1=128.0, scalar2=iota_c[:, 0:1], op0=ALU.mult, op1=ALU.add),
                r=[b_ptab, b_iota], w=[b_idxs])

            s_sq, b_ssq_ = T([4, 1024], BF16)
            s_ss, b_sss = T([4, 1])
            s_xn, b_sxn = T([4, 1024], BF16)
            xsT, b_xsT = T([128, 8, 4], BF16)
            wch0, b_wch0 = T([128, 8, 480], BF16)
            wch1, b_wch1 = T([128, 8, 480], BF16)
            wch = [wch0, wch1]; b_wch = [b_wch0, b_wch1]
            act(lambda e: e.activation(out=s_sq[:], in_=xs_t[:], func=AF.Square, accum_out=s_ss[:]), r=[b_xs], w=[b_ssq_, b_sss])
            act(lambda e: e.activation(out=s_ss[:], in_=s_ss[:], func=AF.Sqrt, scale=1.0 / 1024, bias=EPS), w=[b_sss])
            dve(lambda e: e.reciprocal(out=s_ss[:], in_=s_ss[:]), w=[b_sss])
            dve(lambda e: e.tensor_scalar(out=s_xn[:], in0=xs_t[:], scalar1=s_ss[:, 0:1], scalar2=None, op0=ALU.mult), r=[b_xs, b_sss], w=[b_sxn])
            for kt in range(8):
                tr(PT[:, kt * 4:(kt + 1) * 4], s_xn[0:4, kt * 128:(kt + 1) * 128], ident_b[0:4, 0:4], r=[b_sxn, b_identb], w=[b_PT])
            for kt in range(8):
                act(lambda e, kt=kt: e.activation(out=xsT[:, kt, :], in_=PT[:, kt * 4:(kt + 1) * 4], func=AF.Copy, scale=gmix2[:, kt:kt + 1]),
                    r=[b_PT, b_gmix2], w=[b_xsT])
            for ch in range(7):
                wb = ch % 2
                fw.dma("pool", wch[wb][:], win_full_d[:, ch * 480:(ch + 1) * 480].rearrange("(k p) c -> p k c", p=128), writes=[b_wch[wb]])
                for kt in range(8):
                    mm(PA[0:4, 0:480], lhsT=xsT[:, kt, :], rhs=wch[wb][:, kt, :], start=(kt == 0), stop=(kt == 7), r=[b_xsT, b_wch[wb]], w=[b_PA])
                act(lambda e, ch=ch: e.copy(out=pjs[:, ch * 480:(ch + 1) * 480], in_=PA[0:4, 0:480]), r=[b_PA], w=[b_pjs])

            chk2('s1')
            qkr, b_qkr = T([4, 14, 64])
            rqk, b_rqk = T([4, 14, 64])
            rts, b_rts = T([4, 4, 14, 32])
            kvv = pjs[:, 512:1280].rearrange("p (b k g d) -> p b k g d", b=3, k=2, g=2)
            pool(lambda e: e.tensor_copy(out=qkr[:, 0:8, :], in_=pjs[:, 0:512].rearrange("p (h d) -> p h d", h=8)), r=[b_pjs], w=[b_qkr])
            for br in range(3):
                pool(lambda e, br=br: e.tensor_copy(out=qkr[:, 8 + 2 * br:10 + 2 * br, :], in_=kvv[:, br, 0, :, :]), r=[b_pjs], w=[b_qkr])
            cs_b = bcast(ropes[:, 0:32], [4, 14, 32], 1)
            sn_b = bcast(ropes[:, 32:64], [4, 14, 32], 1)
            pool(lambda e: e.tensor_tensor(out=rts[:, 0], in0=qkr[:, :, 0:32], in1=cs_b, op=ALU.mult), r=[b_qkr, b_ropes], w=[b_rts])
            pool(lambda e: e.tensor_tensor(out=rts[:, 1], in0=qkr[:, :, 32:64], in1=sn_b, op=ALU.mult), r=[b_qkr, b_ropes], w=[b_rts])
            pool(lambda e: e.tensor_tensor(out=rts[:, 2], in0=qkr[:, :, 32:64], in1=cs_b, op=ALU.mult), r=[b_qkr, b_ropes], w=[b_rts])
            pool(lambda e: e.tensor_tensor(out=rts[:, 3], in0=qkr[:, :, 0:32], in1=sn_b, op=ALU.mult), r=[b_qkr, b_ropes], w=[b_rts])
            pool(lambda e: e.tensor_tensor(out=rqk[:, :, 0:32], in0=rts[:, 0], in1=rts[:, 1], op=ALU.subtract), r=[b_rts], w=[b_rqk])
            pool(lambda e: e.tensor_tensor(out=rqk[:, :, 32:64], in0=rts[:, 2], in1=rts[:, 3], op=ALU.add), r=[b_rts], w=[b_rqk])
            kvs_t, b_kvs = T([4, 4, 2, 64])
            wnew, b_wnew = T([4, 2, 2, 64])
            pool(lambda e: e.tensor_copy(out=kvs_t[:, 0], in_=rqk[:, 8:10, :]), r=[b_rqk], w=[b_kvs])
            pool(lambda e: e.tensor_copy(out=kvs_t[:, 1], in_=kvv[:, 0, 1, :, :]), r=[b_pjs], w=[b_kvs])
            pool(lambda e: e.tensor_copy(out=kvs_t[:, 2], in_=rqk[:, 10:12, :]), r=[b_rqk], w=[b_kvs])
            pool(lambda e: e.tensor_copy(out=kvs_t[:, 3], in_=kvv[:, 1, 1, :, :]), r=[b_pjs], w=[b_kvs])
            pool(lambda e: e.tensor_copy(out=wnew[:, 0], in_=rqk[:, 12:14, :]), r=[b_rqk], w=[b_wnew])
            pool(lambda e: e.tensor_copy(out=wnew[:, 1], in_=kvv[:, 2, 1, :, :]), r=[b_pjs], w=[b_wnew])
            fw.dma("sp", kvs_o[:, :], kvs_t[:].rearrange("p a b c -> p (a b c)"), reads=[b_kvs])
            fw.dma("sp", wins_o[:, 511, :], wnew[:].rearrange("p a b c -> p (a b c)"), reads=[b_wnew])
            for s_ in range(4):
                fw.dma("sp", wins_o[s_, 0:511, :], wincache_d[s_, 1:512, :])
            fw.dma("sp", convs_o[:, 0:2, :], gconv_d[:, 1:3, :])
            fw.dma("sp", convs_o[:, 2, :], pjs[:, 1304:2840], reads=[b_pjs])
            chk2('s2')
            qkb_s, b_qkbs = T([4, 14, 64], BF16)
            act(lambda e: e.copy(out=qkb_s[:], in_=rqk[:]), r=[b_rqk], w=[b_qkbs])
            for hd in range(14):
                tr(PT[0:64, hd * 4:(hd + 1) * 4], qkb_s[0:4, hd, :], ident_b[0:4, 0:4], r=[b_qkbs, b_identb], w=[b_PT])
            act(lambda e: e.copy(out=QKT[0:64].rearrange("p a b -> p (a b)"), in_=PT[0:64, 0:56]), r=[b_PT], w=[b_QKT])
            fw.dma("sp", QKT[64:128].rearrange("p a b -> p (a b)"), QKT[0:64].rearrange("p a b -> p (a b)"), reads=[b_QKT], writes=[b_QKT])

            chk2('s3')
            for kv in range(2):
                for l in range(32):
                    mm(PD[0:64, kv:kv + 1], lhsT=cmpw64[0:64, kv, l, :], rhs=pe64[0:64, kv, l:l + 1], start=(l == 0), stop=(l == 31),
                       r=[b_cmpw64, b_pe64], w=[b_PD])
            act(lambda e: e.copy(out=cb_s[:], in_=PD[0:64, 0:2]), r=[b_PD], w=[b_cbs])

            chk2('s3b')
            tmpc, b_tmpc = T([4, 1536])
            caccs, b_caccs = T([4, 1536])
            sqs, b_sqs = T([4, 1024])
            rn8, b_rn8 = T([4, 8])
            gsm, b_gsm = T([4, 8])
            dve(lambda e: e.tensor_tensor(out=caccs[:], in0=stc[:, 0, :], in1=cwb[:, 0, :], op=ALU.mult), r=[b_stc, b_cwb], w=[b_caccs])
            for jj in range(1, 4):
                src = stc[:, jj, :] if jj < 3 else pjs[:, 1304:2840]
                dve(lambda e, jj=jj, src=src: e.tensor_tensor(out=tmpc[:], in0=src, in1=cwb[:, jj, :], op=ALU.mult),
                    r=[b_stc, b_cwb, b_pjs], w=[b_tmpc])
                dve(lambda e: e.tensor_tensor(out=caccs[:], in0=caccs[:], in1=tmpc[:], op=ALU.add), r=[b_tmpc], w=[b_caccs])
            act(lambda e: e.activation(out=Xs[:, 0:1536], in_=caccs[:], func=AF.Silu), r=[b_caccs], w=[b_Xs])
            dve(lambda e: e.tensor_tensor(out=sqs[:], in0=Xs[:, 0:1024], in1=Xs[:, 0:1024], op=ALU.mult), r=[b_Xs], w=[b_sqs])
            dve(lambda e: e.tensor_reduce(out=rn8[:], in_=sqs[:].rearrange("p (h d) -> p h d", h=8), axis=AX.X, op=ALU.add), r=[b_sqs], w=[b_rn8])
            act(lambda e: e.activation(out=rn8[:], in_=rn8[:], func=AF.Sqrt, bias=EPS), w=[b_rn8])
            dve(lambda e: e.reciprocal(out=rn8[:], in_=rn8[:]), w=[b_rn8])
            dve(lambda e: e.tensor_scalar(out=rn8[:, 0:4], in0=rn8[:, 0:4], scalar1=128.0 ** -0.5, scalar2=None, op0=ALU.mult), w=[b_rn8])
            dve(lambda e: e.tensor_tensor(out=Xs[:, 0:1024].rearrange("p (h d) -> p h d", h=8), in0=Xs[:, 0:1024].rearrange("p (h d) -> p h d", h=8),
                                          in1=bcast(rn8[:], [4, 8, 128], 2), op=ALU.mult), r=[b_rn8], w=[b_Xs])
            act(lambda e: e.activation(out=Xs[:, 1536:2048], in_=pjs[:, 2840:3352], func=AF.Silu), r=[b_pjs], w=[b_Xs])
            act(lambda e: e.activation(out=Xs[:, 2048:2052], in_=pjs[:, 3356:3360], func=AF.Sigmoid), r=[b_pjs], w=[b_Xs])
            dve(lambda e: e.tensor_tensor(out=gsm[:, 0:4], in0=pjs[:, 3352:3356], in1=alogb[:, 4:8], op=ALU.add), r=[b_pjs, b_alogb], w=[b_gsm])
            act(lambda e: e.activation(out=gsm[:, 0:4], in_=gsm[:, 0:4], func=AF.Exp), w=[b_gsm])
            act(lambda e: e.activation(out=gsm[:, 0:4], in_=gsm[:, 0:4], func=AF.Ln, bias=1.0), w=[b_gsm])
            act(lambda e: e.activation(out=gsm[:, 4:8], in_=alogb[:, 0:4], func=AF.Exp), r=[b_alogb], w=[b_gsm])
            dve(lambda e: e.tensor_tensor(out=gsm[:, 0:4], in0=gsm[:, 0:4], in1=gsm[:, 4:8], op=ALU.mult), w=[b_gsm])
            act(lambda e: e.activation(out=Xs[:, 2052:2056], in_=gsm[:, 0:4], func=AF.Exp, scale=-1.0), r=[b_gsm], w=[b_Xs])

            chk2('s4')
            Rrow, b_Rrow = T([1, 2056])
            cols_s, b_cols = T([128, 8])
            S_t = [T([128, 128]) for _ in range(2)]
            r1, b_r1 = T([1, 257])
            vn, b_vn = T([1, 128])
            og, b_og = T([1, 128])
            ogs, b_ogs = T([1, 4])
            ogq, b_ogq = T([1, 128])
            egb, b_egb = T([128, 1])
            Snew = [T([128, 128]) for _ in range(2)]
            pool(lambda e: e.memset(one11[:], 1.0), w=[b_one11])
            for s_ in range(4):
                for chn, (c0, c1) in enumerate([(0, 512), (512, 1024), (1024, 1536), (1536, 2048), (2048, 2056)]):
                    mm(PD[0:1, 0:c1 - c0], lhsT=ident_f[0:4, s_:s_ + 1], rhs=Xs[0:4, c0:c1], r=[b_identf, b_Xs], w=[b_PD])
                    act(lambda e, c0=c0, c1=c1: e.copy(out=Rrow[0:1, c0:c1], in_=PD[0:1, 0:c1 - c0]), r=[b_PD], w=[b_Rrow])
                for hq in range(8):
                    mm(PD[:, hq:hq + 1], lhsT=Rrow[0:1, hq * 128:(hq + 1) * 128], rhs=one11[:], r=[b_Rrow, b_one11], w=[b_PD])
                act(lambda e: e.copy(out=cols_s[:], in_=PD[:, 0:8]), r=[b_PD], w=[b_cols])
                for h in range(4):
                    sbi = (s_ * 4 + h) % 2
                    St, bSt = S_t[sbi]
                    Sn, bSn = Snew[sbi]
                    fw.dma("sp", St[:], gS_d[s_ * 4 + h, :, :], writes=[bSt])
                    mm(PD[0:1, 0:128], lhsT=cols_s[:, 4 + h:5 + h], rhs=St[:], r=[b_cols, bSt], w=[b_PD])
                    mm(PD[0:1, 128:256], lhsT=cols_s[:, h:h + 1], rhs=St[:], r=[b_cols, bSt], w=[b_PD])
                    mm(PD[0:1, 256:257], lhsT=cols_s[:, h:h + 1], rhs=cols_s[:, 4 + h:5 + h], r=[b_cols], w=[b_PD])
                    act(lambda e: e.copy(out=r1[:], in_=PD[0:1, 0:257]), r=[b_PD], w=[b_r1])
                    egs = Rrow[0:1, 2052 + h:2053 + h]
                    bts = Rrow[0:1, 2048 + h:2049 + h]
                    dve(lambda e, egs=egs: e.tensor_scalar(out=vn[:], in0=r1[0:1, 0:128], scalar1=egs, scalar2=None, op0=ALU.mult),
                        r=[b_r1, b_Rrow], w=[b_vn])
                    dve(lambda e, h=h: e.tensor_tensor(out=vn[:], in0=Rrow[0:1, 1024 + h * 128:1024 + (h + 1) * 128], in1=vn[:], op=ALU.subtract),
                        r=[b_Rrow], w=[b_vn])
                    dve(lambda e, bts=bts: e.tensor_scalar(out=vn[:], in0=vn[:], scalar1=bts, scalar2=None, op0=ALU.mult), r=[b_Rrow], w=[b_vn])
                    dve(lambda e, egs=egs: e.tensor_scalar(out=og[:], in0=r1[0:1, 128:256], scalar1=egs, scalar2=None, op0=ALU.mult),
                        r=[b_r1, b_Rrow], w=[b_og])
                    dve(lambda e: e.scalar_tensor_tensor(out=og[:], in0=vn[:], scalar=r1[0:1, 256:257], in1=og[:], op0=ALU.mult, op1=ALU.add),
                        r=[b_vn, b_r1], w=[b_og])
                    act(lambda e: e.activation(out=ogq[:], in_=og[:], func=AF.Square, accum_out=ogs[0:1, 0:1]), r=[b_og], w=[b_ogq, b_ogs])
                    act(lambda e: e.activation(out=ogs[0:1, 0:1], in_=ogs[0:1, 0:1], func=AF.Sqrt, scale=1.0 / 128, bias=EPS), w=[b_ogs])
                    dve(lambda e: e.reciprocal(out=ogs[0:1, 0:1], in_=ogs[0:1, 0:1]), w=[b_ogs])
                    dve(lambda e: e.scalar_tensor_tensor(out=og[:], in0=og[:], scalar=ogs[0:1, 0:1], in1=gnrow[:], op0=ALU.mult, op1=ALU.mult),
                        r=[b_ogs, b_gnrow], w=[b_og])
                    dve(lambda e, h=h: e.tensor_tensor(out=og[:], in0=og[:], in1=Rrow[0:1, 1536 + h * 128:1536 + (h + 1) * 128], op=ALU.mult),
                        r=[b_Rrow], w=[b_og])
                    mm(PD[:, 300:301], lhsT=og[:], rhs=one11[:], r=[b_og, b_one11], w=[b_PD])
                    act(lambda e, h=h, s_=s_: e.copy(out=OGT[:, h, s_:s_ + 1], in_=PD[:, 300:301]), r=[b_PD], w=[b_OGT])
                    mm(PB[:, 0:128], lhsT=Rrow[0:1, 512 + h * 128:512 + (h + 1) * 128], rhs=vn[:], r=[b_Rrow, b_vn], w=[b_PB])
                    mm(PD[:, 310:311], lhsT=ones_f2[0:1, :], rhs=egs, r=[b_onesf2, b_Rrow], w=[b_PD])
                    act(lambda e: e.copy(out=egb[:], in_=PD[:, 310:311]), r=[b_PD], w=[b_egb])
                    dve(lambda e, St=St, Sn=Sn: e.scalar_tensor_tensor(out=Sn[:], in0=St[:], scalar=egb[:, 0:1], in1=PB[:, 0:128],
                                                                    op0=ALU.mult, op1=ALU.add), r=[bSt, b_egb, b_PB], w=[bSn])
                    fw.dma("sp", Ss_o[s_ * 4 + h, :, :], Sn[:], reads=[bSn])

            chk2('s5')
            fw.barrier()
            stSg.close()
            stSn = st.enter_context(ExitStack())
            cur[0] = stSn
            woutn, b_woutn = T([64, 8, 1024], BF16)
            woutg, b_woutg = T([128, 4, 1024], BF16)
            eind64, b_eind64 = T([128, 8192], BF16)
            fw.dma("pool", woutn[:].rearrange("p a b -> p (a b)"), woutn_d[:, :], writes=[b_woutn])
            fw.dma("pool", woutg[:].rearrange("p a b -> p (a b)"), woutg_d[:, :], writes=[b_woutg])
            fw.dma("pool", eind64[0:64, :], eind_s_d[:, :], writes=[b_eind64])
            fw.dma("pool", eind64[64:128, :], eind_s_d[:, :], writes=[b_eind64])
            KTs, b_KTs = T([128, 3, 8192], BF16)
            Vs, b_Vs = T([128, 64, 2, 65], BF16)
            pg = [T([128, 512]) for _ in range(3)]
            pgb = [T([128, 384], BF16) for _ in range(2)]
            ckTs, b_ckTs = T([64, 512], BF16)
            cvTs, b_cvTs = T([64, 512], BF16)
            cvxs, b_cvxs = T([128, 4, 193], BF16)
            Pc, b_Pc = T([128, 16], BF16)
            accs, b_accs = T([4, 193])
            rcs, b_rcs = T([4, 4])
            impn, b_impn = T([4, 128])
            scs, b_scs = T([1, 136])
            sc2s, b_sc2s = T([1, 136])
            mx8s, b_mx8s = T([1, 16])
            thrs, b_thrs = T([1, 1])
            mbrow, b_mbrow = T([1, 128])
            mbrow2, b_mbrow2 = T([1, 2, 128])
            mbc, b_mbc = T([128, 2])
            MBq, b_MBq = T([128, 2, 4], BF16)
            Psel, b_Psel = T([128, 256], BF16)
            pnew, b_pnew = T([4, 2])
            vrow, b_vrow = T([4, 128])
            Abr, b_Abr = T([4, 3, 8, 64])
            asel, b_asel = T([4, 2, 65])
            wc, b_wc = T([128, 4, 256])
            wcb, b_wcb = T([128, 4, 128], BF16)
            Vws, b_Vws = T([128, 4, 2, 65], BF16)
            KwTs, b_KwTs = T([128, 4, 128], BF16)
            Pw, b_Pw = T([128, 16], BF16)
            pool(lambda e: e.memset(Vs[:, :, :, 64:65], 1.0), w=[b_Vs])
            pool(lambda e: e.memset(Vws[:, :, :, 64:65], 1.0), w=[b_Vws])
            pool(lambda e: e.memset(ckTs[:], 0.0), w=[b_ckTs])
            pool(lambda e: e.memset(cvTs[:], 0.0), w=[b_cvTs])
            pool(lambda e: e.memset(cvxs[:], 0.0), w=[b_cvxs])
            pool(lambda e: e.tensor_copy(out=cvxs[:, :, 64:192], in_=c2s2[:]), r=[b_c2s2], w=[b_cvxs])
            pool(lambda e: e.tensor_copy(out=cvxs[:, :, 192], in_=on511[:]), r=[b_on511], w=[b_cvxs])
            pool(lambda e: e.memset(scs[:], 1e4), w=[b_scs])
            for s_ in range(4):
                for p_ in range(64):
                    pgt, bpg = pg[p_ % 3]
                    pgbt, bpgb = pgb[p_ % 2]
                    col = s_ * 64 + p_
                    fw.dma("pool", None, None, reads=[b_idxs], writes=[bpg],
                           fn=lambda e, pgt=pgt, col=col: e.indirect_dma_start(
                               out=pgt[:], out_offset=None, in_=cache_d[:, :],
                               in_offset=bass.IndirectOffsetOnAxis(ap=idxs[:, col:col + 1], axis=0)))
                    dve(lambda e, pgt=pgt, pgbt=pgbt: e.tensor_copy(out=pgbt[:], in_=pgt[:, 0:384]), r=[bpg], w=[bpgb])
                    act(lambda e, pgt=pgt, p_=p_: e.copy(out=Vs[:, p_, :, 0:64], in_=pgt[:, 384:512].rearrange("p (g d) -> p g d", g=2)),
                        r=[bpg], w=[b_Vs])
                    for kg in range(3):
                        tr(PT[:, kg * 128:(kg + 1) * 128], pgbt[:, kg * 128:(kg + 1) * 128], ident_b[:], r=[bpgb, b_identb], w=[b_PT])
                    act(lambda e, p_=p_: e.copy(out=KTs[:, :, p_ * 128:(p_ + 1) * 128], in_=PT[:, 0:384].rearrange("p (a t) -> p a t", a=3)),
                        r=[b_PT], w=[b_KTs])
                chk2('s6')
                fw.dma("sp", wc[:], wincache_d[s_, :, :].rearrange("(t p) c -> p t c", p=128), writes=[b_wc])
                pool(lambda e: e.tensor_copy(out=wcb[:], in_=wc[:, :, 0:128]), r=[b_wc], w=[b_wcb])
                pool(lambda e: e.tensor_copy(out=Vws[:, :, :, 0:64], in_=wc[:, :, 128:256].rearrange("p t (g d) -> p t g d", g=2)),
                     r=[b_wc], w=[b_Vws])
                for t_ in range(4):
                    tr(PT[:, t_ * 128:(t_ + 1) * 128], wcb[:, t_, :], ident_b[:], r=[b_wcb, b_identb], w=[b_PT])
                act(lambda e: e.copy(out=KwTs[:].rearrange("p a b -> p (a b)"), in_=PT[:, 0:512]), r=[b_PT], w=[b_KwTs])
                chk2('s7')
                for g_ in range(2):
                    sg = s_ * 2 + g_
                    Qg = QKT[0:64, 4 * g_:4 * g_ + 4, s_]
                    g0_, g1_ = g_ * 64, (g_ + 1) * 64
                    Qgg = QKT[g0_:g1_, 4 * g_:4 * g_ + 4, s_]
                    for kv in range(2):
                        for l in range(32):
                            mm(PA[0:64, 0:511], lhsT=cmpw64[g0_:g1_, kv, l, :], rhs=KTs[g0_:g1_, kv, l:l + 16 * 510 + 1:16],
                               start=(l == 0), stop=(l == 31), r=[b_cmpw64, b_KTs], w=[b_PA])
                        dst = ckTs if kv == 0 else cvTs
                        bd = b_ckTs if kv == 0 else b_cvTs
                        act(lambda e, dst=dst, kv=kv: e.activation(out=dst[:, 0:511], in_=PA[0:64, 0:511], func=AF.Identity, bias=cb_s[:, kv:kv + 1]),
                            r=[b_PA, b_cbs], w=[bd])
                    for jt in range(4):
                        tr(PT[:, jt * 64:(jt + 1) * 64], cvTs[:, jt * 128:(jt + 1) * 128], ident_b[0:64, 0:64], r=[b_cvTs, b_identb], w=[b_PT])
                    act(lambda e: e.copy(out=cvxs[:, :, 0:64], in_=PT[:, 0:256].rearrange("p (a d) -> p a d", a=4)), r=[b_PT], w=[b_cvxs])
                    chk2('s8')
                    for jt in range(4):
                        mm(PD[:, jt * 4:(jt + 1) * 4], lhsT=ckTs[:, jt * 128:(jt + 1) * 128], rhs=Qg, r=[b_ckTs, b_QKT], w=[b_PD])
                    act(lambda e: e.activation(out=Pc[:], in_=PD[:, 0:16], func=AF.Exp, scale=0.125), r=[b_PD], w=[b_Pc])
                    for jt in range(4):
                        mm(PB[0:4, 0:193], lhsT=Pc[:, jt * 4:(jt + 1) * 4], rhs=cvxs[:, jt, :], start=(jt == 0), stop=(jt == 3),
                           r=[b_Pc, b_cvxs], w=[b_PB])
                    act(lambda e: e.copy(out=accs[:], in_=PB[0:4, 0:193]), r=[b_PB], w=[b_accs])
                    dve(lambda e: e.tensor_scalar(out=rcs[:, 0:1], in0=accs[:, 192:193], scalar1=1e-30, scalar2=None, op0=ALU.max), r=[b_accs], w=[b_rcs])
                    dve(lambda e: e.reciprocal(out=rcs[:, 0:1], in_=rcs[:, 0:1]), w=[b_rcs])
                    dve(lambda e, sg=sg: e.tensor_scalar(out=Abr[:, 0, sg, :], in0=accs[:, 0:64], scalar1=rcs[:, 0:1], scalar2=None, op0=ALU.mult),
                        r=[b_accs, b_rcs], w=[b_Abr])
                    dve(lambda e: e.tensor_scalar(out=impn[:], in0=accs[:, 64:192], scalar1=rcs[:, 0:1], scalar2=None, op0=ALU.mult),
                        r=[b_accs, b_rcs], w=[b_impn])
                    mm(PD[0:1, 64:192], lhsT=ones_f2[0:4, 0:1], rhs=impn[:], r=[b_onesf2, b_impn], w=[b_PD])
                    chk2('s8b')
                    dve(lambda e: e.tensor_tensor(out=scs[0:1, 0:128], in0=PD[0:1, 64:192], in1=bonus[:], op=ALU.add), r=[b_PD, b_bonus], w=[b_scs])
                    dve(lambda e: e.max(out=mx8s[:, 0:8], in_=scs[0:1, 0:129]), r=[b_scs], w=[b_mx8s])
                    dve(lambda e: e.match_replace(out=sc2s[0:1, 0:129], in_to_replace=mx8s[:, 0:8], in_values=scs[0:1, 0:129], imm_value=-3e38),
                        r=[b_scs, b_mx8s], w=[b_sc2s])
                    dve(lambda e: e.max(out=mx8s[:, 8:16], in_=sc2s[0:1, 0:129]), r=[b_sc2s], w=[b_mx8s])
                    dve(lambda e: e.tensor_reduce(out=thrs[:], in_=mx8s[:, 8:16], axis=AX.X, op=ALU.min), r=[b_mx8s], w=[b_thrs])
                    dve(lambda e: e.tensor_scalar(out=mbrow[:], in0=scs[0:1, 0:128], scalar1=thrs[0:1, 0:1], scalar2=None, op0=ALU.is_ge),
                        r=[b_scs, b_thrs], w=[b_mbrow])
                    dve(lambda e: e.tensor_scalar(out=mbrow[:], in0=mbrow[:], scalar1=-NEGB, scalar2=NEGB, op0=ALU.mult, op1=ALU.add), w=[b_mbrow])
                    for a_ in range(2):
                        for dp_ in range(2):
                            dve(lambda e, a_=a_, dp_=dp_: e.tensor_copy(out=mbrow2[0:1, a_, dp_ * 64:(dp_ + 1) * 64], in_=mbrow[0:1, a_ * 64:(a_ + 1) * 64]),
                                r=[b_mbrow], w=[b_mbrow2])
                    for a_ in range(2):
                        mm(PD[:, 200 + a_:201 + a_], lhsT=mbrow2[0:1, a_, :], rhs=one11[:], r=[b_mbrow2, b_one11], w=[b_PD])
                    act(lambda e: e.copy(out=mbc[:], in_=PD[:, 200:202]), r=[b_PD], w=[b_mbc])
                    for a_ in range(2):
                        dve(lambda e, a_=a_: e.tensor_scalar(out=MBq[:, a_, :], in0=ones_f2[:, 0:4], scalar1=mbc[:, a_:a_ + 1], scalar2=None, op0=ALU.mult),
                            r=[b_onesf2, b_mbc], w=[b_MBq])
                    chk2('s9')
                    for t_ in range(64):
                        a_ = 0 if t_ < 32 else 1
                        mm(PA[:, 512 + t_ * 4:512 + (t_ + 1) * 4], lhsT=KTs[g0_:g1_, 2, t_ * 128:(t_ + 1) * 128], rhs=Qgg, start=True, stop=False,
                           r=[b_KTs, b_QKT], w=[b_PA])
                        mm(PA[:, 512 + t_ * 4:512 + (t_ + 1) * 4], lhsT=eind64[g0_:g1_, t_ * 128:(t_ + 1) * 128], rhs=MBq[g0_:g1_, a_, :], start=False, stop=True,
                           r=[b_eind64, b_MBq], w=[b_PA])
                    act(lambda e: e.activation(out=Psel[:], in_=PA[:, 512:768], func=AF.Exp, scale=0.125), r=[b_PA], w=[b_Psel])
                    for t_ in range(64):
                        mm(PB[0:4, 256:321], lhsT=Psel[:, t_ * 4:(t_ + 1) * 4], rhs=Vs[:, t_, g_, :], start=(t_ == 0), stop=(t_ == 63),
                           r=[b_Psel, b_Vs], w=[b_PB])
                    chk2('s10')
                    mm(PD[0:4, 210:211], lhsT=Qg, rhs=QKT[0:64, 10 + g_, s_:s_ + 1], r=[b_QKT], w=[b_PD])
                    mm(PD[0:4, 211:212], lhsT=Qg, rhs=QKT[0:64, 12 + g_, s_:s_ + 1], r=[b_QKT], w=[b_PD])
                    act(lambda e: e.activation(out=pnew[:], in_=PD[0:4, 210:212], func=AF.Exp, scale=0.125), r=[b_PD], w=[b_pnew])
                    mm(PD[0:4, 220:284], lhsT=oh4[0:4, s_ * 4:(s_ + 1) * 4], rhs=kvv[:, 1, 1, g_, :], r=[b_oh4, b_pjs], w=[b_PD])
                    mm(PD[0:4, 284:348], lhsT=oh4[0:4, s_ * 4:(s_ + 1) * 4], rhs=kvv[:, 2, 1, g_, :], r=[b_oh4, b_pjs], w=[b_PD])
                    act(lambda e: e.copy(out=vrow[:], in_=PD[0:4, 220:348]), r=[b_PD], w=[b_vrow])
                    dve(lambda e: e.scalar_tensor_tensor(out=asel[:, 0, 0:64], in0=vrow[:, 0:64], scalar=pnew[:, 0:1], in1=PB[0:4, 256:320],
                                                         op0=ALU.mult, op1=ALU.add), r=[b_vrow, b_pnew, b_PB], w=[b_asel])
                    dve(lambda e: e.tensor_tensor(out=asel[:, 0, 64:65], in0=PB[0:4, 320:321], in1=pnew[:, 0:1], op=ALU.add),
                        r=[b_PB, b_pnew], w=[b_asel])
                    chk2('s10b')
                    for t_ in range(4):
                        mm(PD[:, 352 + t_ * 4:356 + t_ * 4], lhsT=KwTs[g0_:g1_, t_, :], rhs=Qgg, r=[b_KwTs, b_QKT], w=[b_PD])
                    act(lambda e: e.activation(out=Pw[:], in_=PD[:, 352:368], func=AF.Exp, scale=0.125), r=[b_PD], w=[b_Pw])
                    for t_ in range(4):
                        mm(PB[0:4, 384:449], lhsT=Pw[:, t_ * 4:(t_ + 1) * 4], rhs=Vws[:, t_, g_, :], start=(t_ == 0), stop=(t_ == 3),
                           r=[b_Pw, b_Vws], w=[b_PB])
                    dve(lambda e: e.scalar_tensor_tensor(out=asel[:, 1, 0:64], in0=vrow[:, 64:128], scalar=pnew[:, 1:2], in1=PB[0:4, 384:448],
                                                         op0=ALU.mult, op1=ALU.add), r=[b_vrow, b_pnew, b_PB], w=[b_asel])
                    dve(lambda e: e.tensor_tensor(out=asel[:, 1, 64:65], in0=PB[0:4, 448:449], in1=pnew[:, 1:2], op=ALU.add),
                        r=[b_PB, b_pnew], w=[b_asel])
                    dve(lambda e: e.reciprocal(out=rcs[:, 1:3], in_=asel[:, :, 64]), r=[b_asel], w=[b_rcs])
                    for br in range(2):
                        dve(lambda e, br=br, sg=sg: e.tensor_scalar(out=Abr[:, 1 + br, sg, :], in0=asel[:, br, 0:64], scalar1=rcs[:, 1 + br:2 + br],
                                                                    scalar2=None, op0=ALU.mult), r=[b_asel, b_rcs], w=[b_Abr])

            chk2('s11')
            gts, b_gts = T([4, 8, 3])
            osum, b_osum = T([4, 8, 64])
            otmp, b_otmp = T([4, 8, 64])
            onb, b_onb = T([4, 8, 64], BF16)
            OT, b_OT = T([64, 8, 4], BF16)
            act(lambda e: e.activation(out=gts[:].rearrange("p a b -> p (a b)"), in_=pjs[:, 1280:1304], func=AF.Sigmoid), r=[b_pjs], w=[b_gts])
            for br in range(3):
                for g_ in range(2):
                    for r_ in range(4):
                        h_ = 4 * g_ + r_
                        for s_ in range(4):
                            mm(PC[0:4, h_ * 64:(h_ + 1) * 64], lhsT=oh4[0:4, 16 + (r_ * 4 + s_) * 4:16 + (r_ * 4 + s_ + 1) * 4],
                               rhs=Abr[:, br, s_ * 2 + g_, :], start=(s_ == 0), stop=(s_ == 3), r=[b_oh4, b_Abr], w=[b_PC])
                gb = bcast(gts[:, :, br], [4, 8, 64], 2)
                if br == 0:
                    dve(lambda e, gb=gb: e.tensor_tensor(out=osum[:], in0=PC[0:4, 0:512].rearrange("p (h d) -> p h d", h=8), in1=gb, op=ALU.mult),
                        r=[b_PC, b_gts], w=[b_osum])
                else:
                    dve(lambda e, gb=gb: e.tensor_tensor(out=otmp[:], in0=PC[0:4, 0:512].rearrange("p (h d) -> p h d", h=8), in1=gb, op=ALU.mult),
                        r=[b_PC, b_gts], w=[b_otmp])
                    dve(lambda e: e.tensor_tensor(out=osum[:], in0=osum[:], in1=otmp[:], op=ALU.add), r=[b_otmp], w=[b_osum])
            act(lambda e: e.copy(out=onb[:], in_=osum[:]), r=[b_osum], w=[b_onb])
            for h_ in range(8):
                tr(PT[0:64, h_ * 4:(h_ + 1) * 4], onb[0:4, h_, :], ident_b[0:4, 0:4], r=[b_onb, b_identb], w=[b_PT])
            act(lambda e: e.copy(out=OT[:].rearrange("p a b -> p (a b)"), in_=PT[0:64, 0:32]), r=[b_PT], w=[b_OT])
            chk2('s12')
            for half in range(2):
                for h_ in range(8):
                    mm(PA[0:4, half * 512:(half + 1) * 512], lhsT=OT[:, h_, :], rhs=woutn[:, h_, half * 512:(half + 1) * 512],
                       start=(h_ == 0), stop=False, r=[b_OT, b_woutn], w=[b_PA])
                for h_ in range(4):
                    mm(PA[0:4, half * 512:(half + 1) * 512], lhsT=OGT[:, h_, :], rhs=woutg[:, h_, half * 512:(half + 1) * 512],
                       start=False, stop=(h_ == 3), r=[b_OGT, b_woutg], w=[b_PA])
            dve(lambda e: e.tensor_tensor(out=hs_res[:], in0=PA[0:4, :], in1=xs_t[:], op=ALU.add), r=[b_PA, b_xs], w=[b_hsres])
            fw.barrier()
            stSn.close()
            stS.close()
            stB = st.enter_context(ExitStack())
            cur[0] = stB
            wout = sb("wout", [128, 8, 1024], BF16); b_wout = Buf()
            wdn = sb("wdn", [128, 22, 1024], BF16); b_wdn = Buf()
            nfb = sb("nfb", [128, 1024]); b_nfb = Buf()
            gffn = sb("gffn", [128, 8]); b_gffn = Buf()
            sel4 = sb("sel4_sb", [128, 4]); b_sel4 = Buf()
            cand = [sb("cand%d" % i, [128, 8, 512], BF16) for i in range(2)]; b_cand = [Buf() for _ in range(2)]
            mixT = sb("mixT", [128, 8, 512], BF16); b_mixT = Buf()
            xt2 = [sb("xt2_%d" % i, [128, 1024]) for i in range(2)]; b_xt2 = [Buf() for _ in range(2)]
            hres = sb("hres", [128, 4, 1024]); b_hres = [Buf() for _ in range(4)]
            hsq = sb("hsq", [128, 1024], BF16); b_hsq = Buf()
            hss = sb("hss", [128, 1]); b_hss = Buf()
            hs = sb("hs", [128, 1024], BF16); b_hs = Buf()
            hnT = sb("hnT", [128, 8, 512], BF16); b_hnT = Buf()
            wg = [sb("wg%d" % i, [128, 8, 256], BF16) for i in range(3)]; b_wg = [Buf() for _ in range(3)]
            sg = [sb("sg%d" % i, [128, 512]) for i in range(2)]; b_sg = [Buf() for _ in range(2)]
            actT = sb("actT", [128, 22, 512], BF16); b_actT = Buf()
            yb = [sb("yb%d" % i, [128, 1024]) for i in range(2)]; b_yb = [Buf() for _ in range(2)]
            ysq = sb("ysq", [128, 1024], BF16); b_ysq = Buf()
            yss = sb("yss", [128, 1]); b_yss = Buf()
            hsn_ss = sb("hsn_ss", [4, 1]); b_hsnss = Buf()
            hsn_sq = sb("hsn_sq", [4, 1024], BF16); b_hsnsq = Buf()
            hsn = sb("hsn", [4, 1024], BF16); b_hsn = Buf()
            hnTs = sb("hnTs", [128, 8, 4], BF16); b_hnTs = Buf()
            sgs = sb("sgs", [128, 4]); b_sgs = Buf()
            actTs = sb("actTs", [128, 22, 4], BF16); b_actTs = Buf()
            ysb = sb("ysb", [4, 1024]); b_ysb = Buf()
            fw.dma("sp", nfb[:], nfin_d[:, :], writes=[b_nfb])
            fw.dma("sp", gffn[:], gffn_d[:, :], writes=[b_gffn])
            fw.dma("sp", sel4[:], sel4_d[:, :], writes=[b_sel4])
            fw.dma("pool", wout[:], wout_d[:, :].rearrange("(k p) c -> p k c", p=128), writes=[b_wout])
            fw.dma("sp", wdn[:], wdn_s[:, :, :].rearrange("f p c -> p f c"), reads=[b_wdns], writes=[b_wdn])
            act(lambda e: e.activation(out=hsn_sq[:], in_=hs_res[:], func=AF.Square, accum_out=hsn_ss[:]), r=[b_hsres], w=[b_hsnsq, b_hsnss])
            act(lambda e: e.activation(out=hsn_ss[:], in_=hsn_ss[:], func=AF.Sqrt, scale=1.0 / 1024, bias=EPS), w=[b_hsnss])
            dve(lambda e: e.reciprocal(out=hsn_ss[:], in_=hsn_ss[:]), w=[b_hsnss])
            dve(lambda e: e.tensor_scalar(out=hsn[:], in0=hs_res[:], scalar1=hsn_ss[:, 0:1], scalar2=None, op0=ALU.mult), r=[b_hsres, b_hsnss], w=[b_hsn])
            for kt in range(8):
                tr(PT[:, kt * 4:(kt + 1) * 4], hsn[0:4, kt * 128:(kt + 1) * 128], ident_b[0:4, 0:4], r=[b_hsn, b_identb], w=[b_PT])
            for kt in range(8):
                act(lambda e, kt=kt: e.activation(out=hnTs[:, kt, :], in_=PT[:, kt * 4:(kt + 1) * 4], func=AF.Copy, scale=gffn[:, kt:kt + 1]),
                    r=[b_PT, b_gffn], w=[b_hnTs])
            wgi = 0
            for bi in range(NB):
                for j4 in range(4):
                    cb = j4 % 2
                    c0 = j4 * TB + bi * 512
                    fw.dma("sp", cand[cb][:], xout[c0 // CH][:, c0 % CH:c0 % CH + 512].rearrange("(k p) t -> p k t", p=128),
                           reads=[b_xout], writes=[b_cand[cb]])
                    if j4 == 0:
                        dve(lambda e, cb=cb: e.tensor_scalar(out=mixT[:], in0=cand[cb][:], scalar1=sel4[:, 0:1], scalar2=None, op0=ALU.mult),
                            r=[b_cand[cb], b_sel4], w=[b_mixT])
                    else:
                        dve(lambda e, cb=cb, j4=j4: e.scalar_tensor_tensor(out=mixT[:], in0=cand[cb][:], scalar=sel4[:, j4:j4 + 1], in1=mixT[:],
                                                                           op0=ALU.mult, op1=ALU.add), r=[b_cand[cb], b_sel4], w=[b_mixT])
                for tt in range(4):
                    r0 = bi * 512 + tt * 128
                    xs_ = tt % 2
                    fw.dma("sp", xt2[xs_][:], xown_d[r0:r0 + 128, :], writes=[b_xt2[xs_]])
                    for half in range(2):
                        for kt in range(8):
                            mm(PA[:, half * 512:(half + 1) * 512], lhsT=mixT[:, kt, tt * 128:(tt + 1) * 128],
                               rhs=wout[:, kt, half * 512:(half + 1) * 512], start=(kt == 0), stop=(kt == 7),
                               r=[b_mixT, b_wout], w=[b_PA])
                    dve(lambda e, tt=tt, xs_=xs_: e.tensor_tensor(out=hres[:, tt, :], in0=PA[:, :], in1=xt2[xs_][:], op=ALU.add),
                        r=[b_PA, b_xt2[xs_]], w=[b_hres[tt]])
                    act(lambda e, tt=tt: e.activation(out=hsq[:], in_=hres[:, tt, :], func=AF.Square, accum_out=hss[:]),
                        r=[b_hres[tt]], w=[b_hsq, b_hss])
                    act(lambda e: e.activation(out=hss[:], in_=hss[:], func=AF.Sqrt, scale=1.0 / 1024, bias=EPS), w=[b_hss])
                    dve(lambda e: e.reciprocal(out=hss[:], in_=hss[:]), w=[b_hss])
                    dve(lambda e, tt=tt: e.tensor_scalar(out=hs[:], in0=hres[:, tt, :], scalar1=hss[:, 0:1], scalar2=None, op0=ALU.mult),
                        r=[b_hres[tt], b_hss], w=[b_hs])
                    for kt in range(8):
                        tr(PT[:, kt * 128:(kt + 1) * 128], hs[:, kt * 128:(kt + 1) * 128], ident_b[:], r=[b_hs, b_identb], w=[b_PT])
                    for kt in range(8):
                        act(lambda e, kt=kt, tt=tt: e.activation(out=hnT[:, kt, tt * 128:(tt + 1) * 128], in_=PT[:, kt * 128:(kt + 1) * 128],
                                                                 func=AF.Copy, scale=gffn[:, kt:kt + 1]), r=[b_PT, b_gffn], w=[b_hnT])
                for f in range(22):
                    wb = wgi % 3
                    wgi += 1
                    fw.dma("sp", wg[wb][:].rearrange("p k c -> p (k c)"), wgu_s[f], reads=[b_wgus[f]], writes=[b_wg[wb]])
                    for kt in range(8):
                        mm(PA[:, 0:512], lhsT=wg[wb][:, kt, 0:128], rhs=hnT[:, kt, :], start=(kt == 0), stop=(kt == 7),
                           r=[b_wg[wb], b_hnT], w=[b_PA])
                    for kt in range(8):
                        mm(PB[:, 0:512], lhsT=wg[wb][:, kt, 128:256], rhs=hnT[:, kt, :], start=(kt == 0), stop=(kt == 7),
                           r=[b_wg[wb], b_hnT], w=[b_PB])
                    sb_ = f % 2
                    act(lambda e, sb_=sb_: e.activation(out=sg[sb_][:], in_=PA[:, 0:512], func=AF.Silu), r=[b_PA], w=[b_sg[sb_]])
                    dve(lambda e, sb_=sb_, f=f: e.tensor_tensor(out=actT[:, f, :], in0=PB[:, 0:512], in1=sg[sb_][:], op=ALU.mult),
                        r=[b_PB, b_sg[sb_]], w=[b_actT])
                    if bi == 0:
                        for kt in range(8):
                            mm(PD[:, 0:4], lhsT=wg[wb][:, kt, 0:128], rhs=hnTs[:, kt, :], start=(kt == 0), stop=(kt == 7),
                               r=[b_wg[wb], b_hnTs], w=[b_PD])
                        for kt in range(8):
                            mm(PD[:, 4:8], lhsT=wg[wb][:, kt, 128:256], rhs=hnTs[:, kt, :], start=(kt == 0), stop=(kt == 7),
                               r=[b_wg[wb], b_hnTs], w=[b_PD])
                        act(lambda e: e.activation(out=sgs[:], in_=PD[:, 0:4], func=AF.Silu), r=[b_PD], w=[b_sgs])
                        dve(lambda e, f=f: e.tensor_tensor(out=actTs[:, f, :], in0=PD[:, 4:8], in1=sgs[:], op=ALU.mult),
                            r=[b_PD, b_sgs], w=[b_actTs])
                for tt in range(4):
                    r0 = bi * 512 + tt * 128
                    for half in range(2):
                        for f in range(22):
                            mm(PC[:, half * 512:(half + 1) * 512], lhsT=actT[:, f, tt * 128:(tt + 1) * 128],
                               rhs=wdn[:, f, half * 512:(half + 1) * 512], start=(f == 0), stop=(f == 21),
                               r=[b_actT, b_wdn], w=[b_PC])
                    ys_ = tt % 2
                    dve(lambda e, tt=tt, ys_=ys_: e.tensor_tensor(out=yb[ys_][:], in0=PC[:, :], in1=hres[:, tt, :], op=ALU.add),
                        r=[b_PC, b_hres[tt]], w=[b_yb[ys_]])
                    act(lambda e, ys_=ys_: e.activation(out=ysq[:], in_=yb[ys_][:], func=AF.Square, accum_out=yss[:]),
                        r=[b_yb[ys_]], w=[b_ysq, b_yss])
                    act(lambda e: e.activation(out=yss[:], in_=yss[:], func=AF.Sqrt, scale=1.0 / 1024, bias=EPS), w=[b_yss])
                    dve(lambda e: e.reciprocal(out=yss[:], in_=yss[:]), w=[b_yss])
                    dve(lambda e, ys_=ys_: e.scalar_tensor_tensor(out=yb[ys_][:], in0=yb[ys_][:], scalar=yss[:, 0:1], in1=nfb[:],
                                                                  op0=ALU.mult, op1=ALU.mult), r=[b_yss, b_nfb], w=[b_yb[ys_]])
                    fw.dma("sp", y_o[r0:r0 + 128, :], yb[ys_][:], reads=[b_yb[ys_]])
            for half in range(2):
                for f in range(22):
                    mm(PC[0:4, half * 512:(half + 1) * 512], lhsT=actTs[:, f, :], rhs=wdn[:, f, half * 512:(half + 1) * 512],
                       start=(f == 0), stop=(f == 21), r=[b_actTs, b_wdn], w=[b_PC])
            dve(lambda e: e.tensor_tensor(out=ysb[:], in0=PC[0:4, :], in1=hs_res[:], op=ALU.add), r=[b_PC, b_hsres], w=[b_ysb])
            act(lambda e: e.activation(out=hsn_sq[:], in_=ysb[:], func=AF.Square, accum_out=hsn_ss[:]), r=[b_ysb], w=[b_hsnsq, b_hsnss])
            act(lambda e: e.activation(out=hsn_ss[:], in_=hsn_ss[:], func=AF.Sqrt, scale=1.0 / 1024, bias=EPS), w=[b_hsnss])
            dve(lambda e: e.reciprocal(out=hsn_ss[:], in_=hsn_ss[:]), w=[b_hsnss])
            dve(lambda e: e.scalar_tensor_tensor(out=ysb[:], in0=ysb[:], scalar=hsn_ss[:, 0:1], in1=nfb[0:4, :], op0=ALU.mult, op1=ALU.mult),
                r=[b_hsnss, b_nfb], w=[b_ysb])
            fw.dma("sp", ys_o[:, :], ysb[:], reads=[b_ysb])
        fw.drain()
    return nc


def _consts(NT):
    TT = NT * 128
    c = {}
    c["c_ident"] = np.eye(128, dtype=np.float32)
    half = 32
    inv = np.power(np.float32(10000.0), -np.arange(half, dtype=np.float32) * np.float32(2.0) / np.float32(64)).astype(np.float32)
    pos = (np.arange(NT)[None, :] * 128 + np.arange(128)[:, None]).astype(np.float32)
    ang = (pos[:, :, None] * inv[None, None, :]).astype(np.float32)
    c["c_cos"] = np.cos(ang).astype(np.float32).reshape(128, NT * 32)
    c["c_sin"] = np.sin(ang).astype(np.float32).reshape(128, NT * 32)
    k = np.arange(128)[:, None]
    q = np.arange(128)[None, :]
    tri = np.stack([(k <= q), (k >= q)], axis=1).astype(np.float32)
    c["c_tri"] = tri.reshape(128, 256)
    cm = np.zeros((128, 17, 128), np.float32)
    for m in range(17):
        cm[:, m, :] = (16 * k - q <= 128 * m - 31)
    c["c_cmpmask"] = cm.reshape(128, 17 * 128)
    qq = np.arange(128)[:, None]
    r = np.arange(256)[None, :] - 128
    hi = (qq >= 64).astype(np.int64)
    prel = np.zeros((128, 256), np.float32)
    prel[(r == hi) | (r == hi - 1)] = 1e4
    prel[r > hi] = -1e30
    c["c_prel"] = prel
    kk = np.arange(TT)[None, :]
    e = np.arange(64)[:, None]
    c["c_eind"] = (e == (kk // 64) % 64).astype(np.float32)
    n = np.arange(512)[:, None]
    s_ = np.arange(128)[None, :]
    c2s = ((n * 16 < s_ * 64 + 64) & (n * 16 + 32 > s_ * 64) & (n < 511)).astype(np.float32)
    c["c_c2s"] = c2s.reshape(4, 128, 128).transpose(1, 0, 2).reshape(128, 512)
    j = np.arange(128)[:, None]
    i = np.arange(128)[None, :]
    same = (j // 64) == (i // 64)
    gm = np.zeros((128, 5, 128), np.float32)
    gm[:, 0, :] = np.where(same & (i >= j), 0.0, NEGB)
    gm[:, 1, :] = np.where(same & (i > j), 0.0, NEGB)
    gm[:, 2, :] = (same & (j <= i))
    gm[:, 3, :] = (j < 64) * np.ones((1, 128))
    gm[:, 4, :] = (j >= 64) * np.ones((1, 128))
    c["c_gmask"] = gm.reshape(128, 5 * 128)
    angs = (np.float32(8192.0) * inv).astype(np.float32)
    c["c_rope_s"] = np.tile(np.concatenate([np.cos(angs), np.sin(angs)]).astype(np.float32)[None, :], (4, 1))
    nn_ = np.arange(512).reshape(4, 128).T
    c["c_ones511"] = (nn_ < 511).astype(np.float32)
    oh = np.zeros((4, 80), np.float32)
    for s_ in range(4):
        oh[s_, s_ * 4:(s_ + 1) * 4] = 1.0
    for r_ in range(4):
        for s_ in range(4):
            oh[r_, 16 + (r_ * 4 + s_) * 4 + s_] = 1.0
    c["c_oh4"] = oh
    bon = np.zeros((1, 128), np.float32)
    bon[0, 0] = 1e4
    bon[0, 127] = 1e4
    c["c_bonus_s"] = bon
    kk8 = np.arange(8192)[None, :]
    c["c_eind_s"] = (e == (kk8 // 64) % 64).astype(np.float32)
    return c


def _core_weights(inp, g, hp):
    jh = 2 * g + hp
    w_in = inp["w_in"][0]
    own = [4 * g + 2 * hp, 4 * g + 2 * hp + 1]
    oth = [4 * g + 2 * (1 - hp), 4 * g + 2 * (1 - hp) + 1]
    heads = own + oth
    cols = []
    for h in heads:
        cols += list(range(h * 64, (h + 1) * 64))

    def kvcol(branch, kv):
        base = 512 + ((branch * 2 + kv) * 2 + g) * 64
        return list(range(base, base + 64))
    for branch in range(3):
        cols += kvcol(branch, 0)
    for branch in range(3):
        cols += kvcol(branch, 1)
    for h in heads:
        cols += [1280 + h * 3 + t for t in range(3)]
    cols += list(range(2840 + jh * 128, 2840 + (jh + 1) * 128))
    cols += [3352 + jh, 3356 + jh]
    w_tok = np.ascontiguousarray(w_in[:, cols])
    gcols = []
    for part in range(3):
        gcols += list(range(1304 + part * 512 + jh * 128, 1304 + part * 512 + (jh + 1) * 128))
    w_gdn = np.ascontiguousarray(w_in[:, gcols])
    d = {"w_tok": w_tok, "w_gdn": w_gdn}
    d["g_mix"] = np.ascontiguousarray(inp["norm_mix"][0].reshape(8, 128).T)
    cwf = inp["gdn_conv_w"][0]
    gch = [jh * 128 + part * 512 + np.arange(128) for part in range(3)]
    cw = np.stack([cwf[:, ch].T for ch in gch], axis=1)
    d["conv_w"] = np.ascontiguousarray(cw.reshape(128, 12))
    d["head_sc"] = np.ascontiguousarray(np.stack([np.full(128, inp["gdn_a_log"][0, jh]),
                                                  np.full(128, inp["gdn_dt_bias"][0, jh])], axis=1).astype(np.float32))
    d["gdn_norm_b"] = np.ascontiguousarray(np.tile(inp["gdn_norm"][0][None, :], (128, 1)))
    cwt = inp["nsa_cmp_w"][0]
    d["cmp_w"] = np.ascontiguousarray(cwt.reshape(2, 16, 2, 64, 64).transpose(2, 3, 0, 1, 4).reshape(128, 2 * 16 * 64))
    pe = inp["nsa_cmp_pe"][0]
    d["cmp_pe"] = np.ascontiguousarray(pe.reshape(2, 16, 2, 64).transpose(2, 3, 0, 1).reshape(128, 32))
    return d


def _core_inputs(inp, c, NT, consts):
    b, j = c // 4, c % 4
    g, hp = j // 2, j % 2
    TT = NT * 128
    TB = TT // 4
    d = dict(consts)
    d.update(_core_weights(inp, g, hp))
    d["x"] = np.ascontiguousarray(inp["x_prompt"][b, :TT])
    d["x_own"] = np.ascontiguousarray(inp["x_prompt"][b, j * TB:(j + 1) * TB])
    sel = np.zeros((128, 4), np.float32)
    sel[:, j] = 1.0
    d["sel4"] = sel
    perm = []
    for jj in range(4):
        perm += list(range(128 * jj, 128 * jj + 128)) + list(range(512 + 128 * jj, 512 + 128 * jj + 128))
    d["w_out_p"] = np.ascontiguousarray(inp["w_out"][0][perm, :])
    d["g_ffn"] = np.ascontiguousarray(inp["norm_ffn"][0].reshape(8, 128).T)
    d["w_gu"] = np.ascontiguousarray(inp["w_gate_up"][0])
    d["w_dn"] = np.ascontiguousarray(inp["w_down"][0])
    d["nfin_b"] = np.ascontiguousarray(np.tile(inp["norm_final"][None, :], (128, 1)))
    s0 = 4 * c
    d["xs"] = np.ascontiguousarray(inp["x_sample"][s0:s0 + 4, 0, :])
    d["w_in_full"] = np.ascontiguousarray(inp["w_in"][0])
    d["cache_kv"] = inp["cache_nsa_kv"][0].reshape(2560 * 128, 512)
    d["ptab_b"] = np.ascontiguousarray(np.tile(inp["page_table"][s0:s0 + 4].reshape(1, 256), (128, 1)).astype(np.int32))
    d["c_iota"] = np.arange(128, dtype=np.float32).reshape(128, 1)
    d["win_cache"] = np.ascontiguousarray(inp["cache_nsa_win"][0, s0:s0 + 4].reshape(4, 512, 256))
    d["gdn_S"] = np.ascontiguousarray(inp["state_gdn_S"][0, s0:s0 + 4].reshape(16, 128, 128))
    d["gdn_conv"] = np.ascontiguousarray(inp["state_gdn_conv"][0, s0:s0 + 4])
    d["conv_w_b"] = np.ascontiguousarray(np.tile(inp["gdn_conv_w"][0][None], (4, 1, 1)))
    d["alog_b"] = np.ascontiguousarray(np.tile(np.concatenate([inp["gdn_a_log"][0], inp["gdn_dt_bias"][0]])[None, :], (4, 1)))
    d["gnorm_row"] = np.ascontiguousarray(inp["gdn_norm"][0][None, :])
    cwt = inp["nsa_cmp_w"][0]
    w64 = cwt.transpose(2, 0, 1, 3).reshape(64, 2 * 32 * 64)
    d["cmp_w64"] = np.ascontiguousarray(np.concatenate([w64, w64], axis=0))
    pe = inp["nsa_cmp_pe"][0]
    p64 = pe.transpose(2, 0, 1).reshape(64, 64)
    d["cmp_pe64"] = np.ascontiguousarray(np.concatenate([p64, p64], axis=0))
    wo = inp["w_out"][0]
    d["w_out_n"] = np.ascontiguousarray(wo[:512].reshape(8, 64, 1024).transpose(1, 0, 2).reshape(64, 8192))
    d["w_out_g"] = np.ascontiguousarray(wo[512:].reshape(4, 128, 1024).transpose(1, 0, 2).reshape(128, 4096))
    return d


def _run(inp, NT):
    nc = build_nc(NT, phaseB=True)
    consts = _consts(NT)
    maps = [_core_inputs(inp, c, NT, consts) for c in range(8)]
    res = run_bass_kernel_spmd(nc, maps, core_ids=list(range(8)))
    return res.results


def kernel(**inputs):
    inp = {k: np.asarray(v) for k, v in inputs.items()}
    NT = 64
    TT = NT * 128
    TB = TT // 4
    R = _run(inp, NT)
    y_prompt = np.zeros((2, TT, 1024), np.float32)
    kv_prompt = np.zeros((1, 2, TT, 4, 2, 64), np.float32)
    win_prompt = np.zeros((1, 2, 512, 2, 2, 64), np.float32)
    S_prompt = np.zeros((1, 2, 4, 128, 128), np.float32)
    conv_prompt = np.zeros((1, 2, 3, 1536), np.float32)
    for c in range(8):
        b, j = c // 4, c % 4
        g, hp = j // 2, j % 2
        r = R[c]
        y_prompt[b, j * TB:(j + 1) * TB] = r["y_out"]
        if hp == 0:
            kv_prompt[0, b, :, :, g, :] = r["kv_out"].reshape(TT, 4, 64)
            win_prompt[0, b, :, :, g, :] = r["win_out"].reshape(512, 2, 64)
        S_prompt[0, b, j] = r["S_out"]
        cv = r["conv_out"]
        for part in range(3):
            conv_prompt[0, b, :, part * 512 + j * 128:part * 512 + (j + 1) * 128] = cv[:, part, :].T
    y_sample = np.zeros((32, 1, 1024), np.float32)
    kv_sample = np.zeros((1, 32, 1, 4, 2, 64), np.float32)
    win_sample = np.zeros((1, 32, 512, 2, 2, 64), np.float32)
    S_sample = np.zeros((1, 32, 4, 128, 128), np.float32)
    conv_sample = np.zeros((1, 32, 3, 1536), np.float32)
    for c in range(8):
        r = R[c]
        s0 = 4 * c
        y_sample[s0:s0 + 4, 0] = r["ys_out"]
        kv_sample[0, s0:s0 + 4, 0] = r["kvs_out"].reshape(4, 4, 2, 64)
        win_sample[0, s0:s0 + 4] = r["wins_out"].reshape(4, 512, 2, 2, 64)
        S_sample[0, s0:s0 + 4] = r["Ss_out"].reshape(4, 4, 128, 128)
        conv_sample[0, s0:s0 + 4] = r["convs_out"]
    return (y_prompt, y_sample, kv_prompt, win_prompt, S_prompt, conv_prompt, kv_sample, win_sample, S_sample, conv_sample)
```

```python
import numpy as np
from contextlib import ExitStack
import concourse.bass as bass
import concourse.mybir as mybir
from concourse.bass_utils import run_bass_kernel_spmd

F32 = mybir.dt.float32
BF16 = mybir.dt.bfloat16
I32 = mybir.dt.int32
AF = mybir.ActivationFunctionType
ALU = mybir.AluOpType
AX = mybir.AxisListType

ENGS = ("pe", "act", "dve", "pool", "sp")

D_MODEL = 1024
SEQ = 8192
HEAD_DIM = 64
D_FF = 2816
EPS = 1e-6
NEGB = -30000.0


class Buf:
    __slots__ = ("name", "w", "rs")

    def __init__(self, name=""):
        self.name = name
        self.w = None
        self.rs = []


class FW:
    def __init__(self, nc, stack, ndma_sems=16):
        self.nc = nc
        self.eng = {"pe": nc.tensor, "act": nc.scalar, "dve": nc.vector, "pool": nc.gpsimd, "sp": nc.sync}
        self.sem = {e: stack.enter_context(nc.semaphore("s_" + e)) for e in ENGS}
        self.cnt = {e: 0 for e in ENGS}
        self.waited = {e: {} for e in ENGS}
        self.dsems = {}
        self.dstate = {}
        for q in ("sp", "pool"):
            self.dsems[q] = [stack.enter_context(nc.semaphore("d_%s%d" % (q, i))) for i in range(ndma_sems)]
            self.dstate[q] = {"i": 0, "val": [0] * ndma_sems}
        self.n_inst = 0
        self.dead = False

    def _wait(self, e, ev):
        if ev is None:
            return
        if ev[0] == "c":
            _, src, n = ev
            if src == "pe" and e == "pe":
                return
            key = ("c", src)
            if self.waited[e].get(key, 0) >= n:
                return
            self.eng[e].wait_ge(self.sem[src], n)
            self.waited[e][key] = n
        else:
            _, q, idx, val = ev
            key = ("d", q, idx)
            if self.waited[e].get(key, 0) >= val:
                return
            self.eng[e].wait_ge(self.dsems[q][idx], val)
            self.waited[e][key] = val

    def _deps(self, e, reads, writes):
        for b in reads:
            self._wait(e, b.w)
        for b in writes:
            self._wait(e, b.w)
            for r in b.rs:
                self._wait(e, r)

    def _commit(self, ev, reads, writes):
        for b in reads:
            b.rs.append(ev)
            if len(b.rs) > 96:
                b.rs = b.rs[-96:]
        for b in writes:
            b.w = ev
            b.rs = []

    def op(self, e, fn, reads=(), writes=()):
        if self.dead:
            return None
        self._deps(e, reads, writes)
        ins = fn(self.eng[e])
        self.cnt[e] += 1
        ins.then_inc(self.sem[e], 1)
        ev = ("c", e, self.cnt[e])
        self._commit(ev, reads, writes)
        self.n_inst += 1
        return ev

    def dma(self, q, out, in_, reads=(), writes=(), fn=None):
        if self.dead:
            return None
        st = self.dstate[q]
        idx = st["i"] % len(self.dsems[q])
        st["i"] += 1
        if st["val"][idx] > 0:
            self._wait(q, ("d", q, idx, st["val"][idx]))
        self._deps(q, reads, writes)
        if fn is None:
            ins = self.eng[q].dma_start(out=out, in_=in_)
        else:
            ins = fn(self.eng[q])
        st["val"][idx] += 16
        ins.then_inc(self.dsems[q][idx], 16)
        ev = ("d", q, idx, st["val"][idx])
        self._commit(ev, reads, writes)
        self.n_inst += 1
        return ev

    def barrier(self):
        for e in ENGS:
            for src in ENGS:
                if src != e and self.cnt[src] > 0:
                    self._wait(e, ("c", src, self.cnt[src]))
            for q in ("sp", "pool"):
                stq = self.dstate[q]
                for idx, v in enumerate(stq["val"]):
                    if v:
                        self._wait(e, ("d", q, idx, v))

    def drain(self):
        for q in ("sp", "pool"):
            st = self.dstate[q]
            for idx, v in enumerate(st["val"]):
                if v:
                    self._wait("sp", ("d", q, idx, v))


class _Stop(Exception):
    pass


STOP = None
SKIP_CC = False


def build_nc(NT=64, dbg=False, phaseB=False):
    TT = NT * 128
    NG = NT // 4
    nc = bass.Bass("TRN2", target_bir_lowering=False)

    def din(name, shape, dt=F32):
        return nc.dram_tensor(name, list(shape), dt, kind="ExternalInput").ap()

    def dout(name, shape, dt=F32):
        return nc.dram_tensor(name, list(shape), dt, kind="ExternalOutput").ap()

    x_d = din("x", [TT, 1024])
    wtok_d = din("w_tok", [1024, 782])
    wgdn_d = din("w_gdn", [1024, 384])
    gmix_d = din("g_mix", [128, 8])
    cw_d = din("conv_w", [128, 12])
    hsc_d = din("head_sc", [128, 2])
    gnorm_d = din("gdn_norm_b", [128, 128])
    cmpw_d = din("cmp_w", [128, 2 * 16 * 64])
    cmppe_d = din("cmp_pe", [128, 32])
    ident_d = din("c_ident", [128, 128])
    cos_d = din("c_cos", [128, NT * 32])
    sin_d = din("c_sin", [128, NT * 32])
    tri_d = din("c_tri", [128, 256])
    cmpmask_d = din("c_cmpmask", [128, 17 * 128])
    prel_d = din("c_prel", [128, 256])
    eind_d = din("c_eind", [64, TT])
    c2s_d = din("c_c2s", [128, 4 * 128])
    gmask_d = din("c_gmask", [128, 5 * 128])

    TB = TT // 4
    NB = TB // 512
    if phaseB:
        xown_d = din("x_own", [TB, 1024])
        sel4_d = din("sel4", [128, 4])
        wout_d = din("w_out_p", [1024, 1024])
        gffn_d = din("g_ffn", [128, 8])
        wgu_d = din("w_gu", [1024, 5632])
        wdn_d = din("w_dn", [2816, 1024])
        nfin_d = din("nfin_b", [128, 1024])
        y_o = dout("y_out", [TB, 1024])
    if phaseB:
        xs_d = din("xs", [4, 1024])
        win_full_d = din("w_in_full", [1024, 3360])
        cache_d = din("cache_kv", [2560 * 128, 512])
        ptab_d = din("ptab_b", [128, 256], I32)
        iota_d = din("c_iota", [128, 1])
        wincache_d = din("win_cache", [4, 512, 256])
        gS_d = din("gdn_S", [16, 128, 128])
        gconv_d = din("gdn_conv", [4, 3, 1536])
        convwb_d = din("conv_w_b", [4, 4, 1536])
        alogb_d = din("alog_b", [4, 8])
        gnrow_d = din("gnorm_row", [1, 128])
        cmpw64_d = din("cmp_w64", [128, 2 * 32 * 64])
        woutn_d = din("w_out_n", [64, 8 * 1024])
        woutg_d = din("w_out_g", [128, 4 * 1024])
        ropes_d = din("c_rope_s", [4, 64])
        ones511_d = din("c_ones511", [128, 4])
        pe64_d = din("cmp_pe64", [128, 64])
        eind_s_d = din("c_eind_s", [64, 8192])
        oh4_d = din("c_oh4", [4, 80])
        bonus_d = din("c_bonus_s", [1, 128])
        ys_o = dout("ys_out", [4, 1024])
        kvs_o = dout("kvs_out", [4, 512])
        wins_o = dout("wins_out", [4, 512, 256])
        Ss_o = dout("Ss_out", [16, 128, 128])
        convs_o = dout("convs_out", [4, 3, 1536])
    kv_o = dout("kv_out", [TT, 256])
    win_o = dout("win_out", [512, 128])
    S_o = dout("S_out", [128, 128])
    conv_o = dout("conv_out", [128, 3, 3])
    CH = min(2048, TT)
    NCH = TT // CH
    omT_o = dout("omT_out", [256, TT], BF16) if not phaseB else None
    xin = [nc.dram_tensor("xin%d" % k, [256, CH], BF16).ap() for k in range(NCH)] if phaseB else None
    b_xin = [Buf() for _ in range(NCH)]
    dbg_o = dout("dbg_out", [128, 1300]) if dbg else None

    st = ExitStack()
    with st:
        fw = FW(nc, st)

        cur = [st]

        def sb(name, shape, dt=F32):
            return cur[0].enter_context(nc.sbuf_tensor(name, list(shape), dt))

        def ps(name, shape, dt=F32):
            return st.enter_context(nc.psum_tensor(name, list(shape), dt))

        def pe(fn, r=(), w=()):
            return fw.op("pe", fn, r, w)

        def act(fn, r=(), w=()):
            return fw.op("act", fn, r, w)

        def dve(fn, r=(), w=()):
            return fw.op("dve", fn, r, w)

        def pool(fn, r=(), w=()):
            return fw.op("pool", fn, r, w)

        def mm(out, lhsT, rhs, start=True, stop=True, r=(), w=()):
            return pe(lambda e: e.matmul(out, lhsT=lhsT, rhs=rhs, start=start, stop=stop), r, w)

        def tr(out, in_, ident, r=(), w=()):
            return pe(lambda e: e.transpose(out, in_, ident), r, w)

        def bcast(ap, shape, axis):
            return ap.unsqueeze(axis).to_broadcast(list(shape))

        ident_f = sb("ident_f", [128, 128]); b_identf = Buf()
        ident_b = sb("ident_b", [128, 128], BF16); b_identb = Buf()
        ones_f2 = sb("ones_f2", [128, 128]); b_onesf2 = Buf()
        hs_res = sb("hs_res", [4, 1024]); b_hsres = Buf()
        stA = st.enter_context(ExitStack())
        cur[0] = stA
        pool(lambda e: e.memset(ones_f2[:], 1.0), w=[b_onesf2])
        ones_b = sb("ones_b", [128, 128], BF16); b_onesb = Buf()
        ones_f = sb("ones_f", [128, 128]); b_onesf = Buf()
        csT = [sb("csT%d" % i_, [128, 2, 4, 32]) for i_ in range(2)]; b_csT = [Buf() for _ in range(2)]
        tri = sb("tri", [128, 2, 128], BF16); b_tri = Buf()
        cmpmask = sb("cmpmask", [128, 17, 128], BF16); b_cmpmask = Buf()
        prel = sb("prel", [128, 256]); b_prel = Buf()
        gmask = sb("gmask", [128, 5, 128]); b_gmask = Buf()
        wtok = sb("wtok", [128, 8, 782], BF16); b_wtok = Buf()
        wgdn = sb("wgdn", [128, 8, 384], BF16); b_wgdn = Buf()
        gmix = sb("gmix", [128, 8]); b_gmix = Buf()
        cw = sb("cw", [128, 12]); b_cw = Buf()
        hsc = sb("hsc", [128, 2]); b_hsc = Buf()
        negA = sb("negA", [128, 1]); b_negA = Buf()
        gnb = sb("gnb", [128, 128]); b_gnb = Buf()
        cmpw = sb("cmpw", [128, 2, 16, 64], BF16); b_cmpw = Buf()
        cmppe = sb("cmppe", [128, 2, 16], BF16); b_cmppe = Buf()

        fw.dma("sp", ident_f[:], ident_d[:, :], writes=[b_identf])
        fw.dma("pool", ident_b[:], ident_d[:, :], writes=[b_identb])
        fw.dma("pool", tri[:].rearrange("p a b -> p (a b)"), tri_d[:, :], writes=[b_tri])
        fw.dma("pool", cmpmask[:].rearrange("p a b -> p (a b)"), cmpmask_d[:, :], writes=[b_cmpmask])
        fw.dma("sp", prel[:], prel_d[:, :], writes=[b_prel])
        fw.dma("sp", gmask[:].rearrange("p a b -> p (a b)"), gmask_d[:, :], writes=[b_gmask])
        fw.dma("sp", gmix[:], gmix_d[:, :], writes=[b_gmix])
        fw.dma("sp", cw[:], cw_d[:, :], writes=[b_cw])
        fw.dma("sp", hsc[:], hsc_d[:, :], writes=[b_hsc])
        fw.dma("sp", gnb[:], gnorm_d[:, :], writes=[b_gnb])
        fw.dma("pool", cmpw[:].rearrange("p a b c -> p (a b c)"), cmpw_d[:, :], writes=[b_cmpw])
        fw.dma("pool", cmppe[:].rearrange("p a b -> p (a b)"), cmppe_d[:, :], writes=[b_cmppe])
        for kt in range(8):
            fw.dma("pool", wtok[:, kt, :], wtok_d[kt * 128:(kt + 1) * 128, :], writes=[b_wtok])
            fw.dma("pool", wgdn[:, kt, :], wgdn_d[kt * 128:(kt + 1) * 128, :], writes=[b_wgdn])
        pool(lambda e: e.memset(ones_b[:], 1.0), w=[b_onesb])
        pool(lambda e: e.memset(ones_f[:], 1.0), w=[b_onesf])
        for kt in range(8):
            dve(lambda e, kt=kt: e.tensor_scalar(out=wtok[:, kt, :], in0=wtok[:, kt, :], scalar1=gmix[:, kt:kt + 1],
                                                 scalar2=None, op0=ALU.mult), r=[b_gmix], w=[b_wtok])
            dve(lambda e, kt=kt: e.tensor_scalar(out=wgdn[:, kt, :], in0=wgdn[:, kt, :], scalar1=gmix[:, kt:kt + 1],
                                                 scalar2=None, op0=ALU.mult), r=[b_gmix], w=[b_wgdn])
        act(lambda e: e.activation(out=negA[:], in_=hsc[:, 0:1], func=AF.Exp), r=[b_hsc], w=[b_negA])
        dve(lambda e: e.tensor_scalar(out=negA[:], in0=negA[:], scalar1=-1.0, scalar2=None, op0=ALU.mult), w=[b_negA])

        KselT = sb("KselT", [128, TT], BF16); b_ksel = [Buf() for _ in range(NT)]; b_eind = Buf()
        Vsel = sb("Vsel", [128, NT, 65], BF16); b_vsel = [Buf() for _ in range(NT)]
        KwinT = sb("KwinT", [64, 8 * 128], BF16); b_kwin = [Buf() for _ in range(8)]
        Vwin = sb("Vwin", [128, 8, 65], BF16); b_vwin = [Buf() for _ in range(8)]
        Rk = sb("Rk", [128, 2, 160], BF16); b_Rk = Buf()
        ckT = sb("ckT", [64, 512], BF16); b_ckT = Buf()
        cvx = sb("cvx", [128, 4, 193], BF16); b_cvx = Buf()
        c2s_f = sb("c2s_f", [128, 4, 128]); b_c2sf = Buf()
        ckb = sb("ckb", [64, 1]); b_ckb = Buf()
        cvb = sb("cvb", [8, 64]); b_cvb = Buf()
        cvrow = sb("cvrow", [1, 64], BF16); b_cvrow = Buf()

        fw.dma("pool", KselT[64:128, :], eind_d[:, :], writes=[b_eind])
        pool(lambda e: e.memset(Vsel[:, :, 64:65], 1.0), w=b_vsel)
        pool(lambda e: e.memset(Vwin[:, :, 64:65], 1.0), w=b_vwin)
        pool(lambda e: e.memset(Rk[:], 0.0), w=[b_Rk])
        pool(lambda e: e.memset(ckT[:], 0.0), w=[b_ckT])
        pool(lambda e: e.memset(cvx[:], 0.0), w=[b_cvx])
        fw.dma("sp", c2s_f[:].rearrange("p a b -> p (a b)"), c2s_d[:, :], writes=[b_c2sf])
        pool(lambda e: e.tensor_copy(out=cvx[:, :, 64:192], in_=c2s_f[:]), r=[b_c2sf], w=[b_cvx])
        pool(lambda e: e.memset(cvx[:, :, 192:193], 1.0), w=[b_cvx])

        PA = ps("PA", [128, 1024]); b_PA = Buf()
        PB = ps("PB", [128, 1024]); b_PB = Buf()
        PC = ps("PC", [128, 1024]); b_PC = Buf()
        PD = ps("PD", [128, 512]); b_PD = Buf()
        PT = ps("PT", [128, 1024], BF16); b_PT = Buf()

        for lp in range(16):
            mm(PD[0:64, 0:1], lhsT=cmpw[:, 0, lp, :], rhs=cmppe[:, 0, lp:lp + 1], start=(lp == 0), stop=(lp == 15),
               r=[b_cmpw, b_cmppe], w=[b_PD])
        act(lambda e: e.copy(out=ckb[:], in_=PD[0:64, 0:1]), r=[b_PD], w=[b_ckb])
        for lp in range(16):
            mm(PD[0:1, 64:128], lhsT=cmppe[:, 1, lp:lp + 1], rhs=cmpw[:, 1, lp, :], start=(lp == 0), stop=(lp == 15),
               r=[b_cmpw, b_cmppe], w=[b_PD])
        act(lambda e: e.copy(out=cvrow[:], in_=PD[0:1, 64:128]), r=[b_PD], w=[b_cvrow])
        mm(PD[0:8, 128:192], lhsT=ones_b[0:1, 0:8], rhs=cvrow[:], r=[b_onesb, b_cvrow], w=[b_PD])
        act(lambda e: e.copy(out=cvb[:], in_=PD[0:8, 128:192]), r=[b_PD], w=[b_cvb])

        NXB = 2
        xt = [sb("xt%d" % i, [128, 1024]) for i in range(NXB)]; b_xt = [Buf() for _ in range(NXB)]
        ssq = [sb("ssq%d" % i, [128, 1]) for i in range(NXB)]; b_ssq = [Buf() for _ in range(NXB)]
        xs = [sb("xs%d" % i, [128, 1024], BF16) for i in range(NXB)]; b_xs = [Buf() for _ in range(NXB)]
        xnT = sb("xnT", [128, 8, 512], BF16); b_xnT = [Buf() for _ in range(4)]
        pj = [sb("pj%d" % i, [128, 782]) for i in range(2)]; b_pj = [Buf() for _ in range(2)]
        rq = [sb("rq%d" % i, [128, 7, 64]) for i in range(2)]; b_rq = [Buf() for _ in range(2)]
        rt = sb("rt", [128, 4, 7, 32]); b_rt = Buf()
        ko = [sb("ko%d" % i, [128, 6, 64]) for i in range(2)]; b_ko = [Buf() for _ in range(2)]
        qkb = sb("qkb", [128, 7, 64], BF16); b_qkb = Buf()
        kvc2 = sb("kvc2", [128, 2, 128], BF16); b_kvc2 = Buf()
        QT = [sb("QT%d" % i_, [64, 512], BF16) for i_ in range(2)]; b_QT = [Buf() for _ in range(2)]
        Qaug = [sb("Qaug%d" % i_, [128, 2, 256], BF16) for i_ in range(2)]; b_Qaug = [Buf() for _ in range(2)]
        gates = [sb("gates%d" % i_, [128, 12]) for i_ in range(2)]; b_gates = [Buf() for _ in range(2)]
        gz = sb("gz", [128, 140]); b_gz = Buf()
        zsil = [sb("zsil%d" % i_, [128, 4, 128]) for i_ in range(2)]; b_zsil = [[Buf() for _ in range(4)] for _ in range(2)]
        abg = [sb("abg%d" % i_, [128, 4, 2]) for i_ in range(2)]; b_abg = [Buf() for _ in range(2)]
        cvnew = sb("cvnew", [8, 64], BF16); b_cvnew = Buf()
        PTc = [sb("PTc%d" % i, [128, 512], BF16) for i in range(4)]; b_PTc = [Buf() for _ in range(4)]
        PTs = [sb("PTs%d" % i, [128, 1024], BF16) for i in range(2)]; b_PTs = [Buf() for _ in range(2)]
        acc_c = sb("acc_c", [128, 4, 193]); b_accc = Buf()
        acc_sw = sb("acc_sw", [128, 4, 65]); b_accsw = Buf()
        rcp = sb("rcp", [128, 12]); b_rcp = Buf()
        imp = sb("imp", [128, 128]); b_imp = Buf()
        score = sb("score", [128, 128]); b_score = Buf()
        mx8 = sb("mx8", [128, 16]); b_mx8 = Buf()
        thr = sb("thr", [128, 1]); b_thr = Buf()
        sc2 = sb("sc2", [128, 128]); b_sc2 = Buf()
        mbt = sb("mbt", [128, 2, 128]); b_mbt = Buf()
        coef = sb("coef", [128, 6]); b_coef = Buf()
        onsa = sb("onsa", [128, 128]); b_onsa = Buf()
        om = sb("om", [128, 256], BF16); b_om = Buf()
        omT = [sb("omT%d" % i_, [128, 2, 512], BF16) for i_ in range(2)]; b_omT = [Buf() for _ in range(2)]
        raw = [sb("raw%d" % i_, [128, 3, 515]) for i_ in range(2)]; b_raw = [Buf() for _ in range(2)]
        cacc = sb("cacc", [128, 3, 512]); b_cacc = Buf()
        csil = cacc; b_csil = b_cacc
        sqb = sb("sqb", [128, 2, 512], BF16); b_sqb = Buf()
        rnorm = sb("rnorm", [128, 2, 512]); b_rnorm = Buf()
        gT = sb("gT", [128, 3, 512], BF16); b_gT = Buf()
        gtok = sb("gtok", [128, 4, 3, 128], BF16); b_gtok = Buf()
        gsc = sb("gsc", [128, 16, 4]); b_gsc = Buf()
        glc = sb("glc", [128, 2, 4]); b_glc = Buf()
        dg1 = sb("dg1", [128, 4, 128]); b_dg1 = Buf()
        dg2 = sb("dg2", [128, 4, 128]); b_dg2 = Buf()
        dgn = sb("dgn", [128, 4, 128]); b_dgn = Buf()
        gmask4 = sb("gmask4", [128, 2, 4, 128]); b_gmask4 = Buf()
        decT = sb("decT", [128, 4, 128], BF16); b_decT = Buf()
        decbT = sb("decbT", [128, 4, 128], BF16); b_decbT = Buf()
        Um = [sb("Um%d" % i, [128, 4, 128], BF16) for i in range(2)]; b_Um = [Buf() for _ in range(2)]
        Lm = [sb("Lm%d" % i, [128, 4, 128], BF16) for i in range(2)]; b_Lm = [Buf() for _ in range(2)]
        Pm = [sb("Pm%d" % i, [128, 4, 128], BF16) for i in range(2)]; b_Pm = [Buf() for _ in range(2)]
        Xm = sb("Xm", [128, 4, 256], BF16); b_Xm = Buf()
        uw = sb("uw", [128, 4, 256], BF16); b_uw = Buf()
        kgm = sb("kgm", [128, 4, 2, 128], BF16); b_kgm = Buf()
        aqkT = sb("aqkT", [128, 4, 128], BF16); b_aqkT = Buf()
        Dg = sb("Dg", [128, 4, 128], BF16); b_Dg = Buf()
        QpA = sb("QpA", [128, 4, 128], BF16); QpB = sb("QpB", [128, 4, 128], BF16); b_Qp = Buf()
        glc8 = sb("glc8", [128, 8]); b_glc8 = Buf()
        MTf = sb("MTf", [128, 8, 128], BF16); b_MTf = Buf()
        MT8 = sb("MT8", [128, 8, 128], BF16); b_MT8 = Buf()
        Sb9 = sb("Sb9", [128, 9, 128], BF16); b_Sb9 = [Buf() for _ in range(9)]
        Sf = sb("Sf", [128, 128]); b_Sf = Buf()
        og4 = sb("og4", [128, 4, 128]); b_og4 = Buf()
        og4q = dgn; b_og4q = b_dgn
        og4s = sb("og4s", [128, 4]); b_og4s = Buf()
        omg = sb("omg", [128, 4, 128], BF16); b_omg = Buf()

        pool(lambda e: e.memset(kgm[:], 0.0), w=[b_kgm])
        for mk_ in range(2):
            pool(lambda e, mk_=mk_: e.tensor_copy(out=gmask4[:, mk_], in_=bcast(gmask[:, mk_, :], [128, 4, 128], 1)), r=[b_gmask], w=[b_gmask4])
        pool(lambda e: e.memset(raw[0][:], 0.0), w=[b_raw[0]])
        pool(lambda e: e.memset(raw[1][:], 0.0), w=[b_raw[1]])
        pool(lambda e: e.memset(QpA[:], 0.0), w=[b_Qp])
        pool(lambda e: e.memset(QpB[:], 0.0), w=[b_Qp])
        pool(lambda e: e.memset(Sb9[:, 0, :], 0.0), w=[b_Sb9[0]])

        G_G, G_BETA, G_GCUM, G_GL, G_EG, G_EKG, G_LNB, G_NEGG, G_SKBG, G_GB = range(10)


        hits = {}

        def chk2(name):
            if STOP == name:
                fw.dead = True

        def chk(name):
            hits[name] = hits.get(name, 0) + 1
            if STOP == name or STOP == "%s@%d" % (name, hits[name]):
                raise _Stop()

        def gdn_gen(grp):
            gp = grp % 2
            for c3 in range(3):
                dve(lambda e, c3=c3: e.tensor_scalar(out=cacc[:, c3, :], in0=raw[gp][:, c3, 0:512], scalar1=cw[:, c3 * 4:c3 * 4 + 1],
                                                      scalar2=None, op0=ALU.mult), r=[b_raw[gp], b_cw], w=[b_cacc])
                for jj in range(1, 4):
                    dve(lambda e, c3=c3, jj=jj: e.scalar_tensor_tensor(out=cacc[:, c3, :], in0=raw[gp][:, c3, jj:jj + 512],
                                                                        scalar=cw[:, c3 * 4 + jj:c3 * 4 + jj + 1], in1=cacc[:, c3, :],
                                                                        op0=ALU.mult, op1=ALU.add), r=[b_raw[gp], b_cw], w=[b_cacc])
            act(lambda e: e.activation(out=csil[:].rearrange("p a b -> p (a b)"), in_=cacc[:].rearrange("p a b -> p (a b)"),
                                       func=AF.Silu), r=[b_cacc], w=[b_csil])
            yield
            act(lambda e: e.activation(out=sqb[:].rearrange("p a b -> p (a b)"), in_=csil[:, 0:2, :].rearrange("p a b -> p (a b)"),
                                       func=AF.Square), r=[b_csil], w=[b_sqb])
            for c3 in range(2):
                mm(PB[:, c3 * 512:(c3 + 1) * 512], lhsT=ones_b[:], rhs=sqb[:, c3, :], r=[b_onesb, b_sqb], w=[b_PB])
            act(lambda e: e.activation(out=rnorm[:].rearrange("p a b -> p (a b)"), in_=PB[:, 0:1024], func=AF.Ln, bias=EPS),
                r=[b_PB], w=[b_rnorm])
            act(lambda e: e.activation(out=rnorm[:].rearrange("p a b -> p (a b)"), in_=rnorm[:].rearrange("p a b -> p (a b)"),
                                       func=AF.Exp, scale=-0.5), w=[b_rnorm])
            dve(lambda e: e.scalar_tensor_tensor(out=gT[:, 0, :], in0=csil[:, 0, :], scalar=128.0 ** -0.5, in1=rnorm[:, 0, :],
                                                 op0=ALU.mult, op1=ALU.mult), r=[b_csil, b_rnorm], w=[b_gT])
            dve(lambda e: e.tensor_tensor(out=gT[:, 1, :], in0=csil[:, 1, :], in1=rnorm[:, 1, :], op=ALU.mult),
                r=[b_csil, b_rnorm], w=[b_gT])
            act(lambda e: e.copy(out=gT[:, 2, :], in_=csil[:, 2, :]), r=[b_csil], w=[b_gT])
            for tt in range(4):
                for c3 in range(3):
                    tr(PT[:, c3 * 128:(c3 + 1) * 128], gT[:, c3, tt * 128:(tt + 1) * 128], ident_b[:], r=[b_gT, b_identb], w=[b_PT])
                act(lambda e, tt=tt: e.copy(out=gtok[:, tt, :, :], in_=PT[:, 0:384].rearrange("p (c d) -> p c d", c=3)),
                    r=[b_PT], w=[b_gtok])
            yield
            a_ap = abg[gp][:, :, 0]
            b_ap = abg[gp][:, :, 1]
            act(lambda e: e.activation(out=gsc[:, G_G, :], in_=a_ap, func=AF.Exp, bias=hsc[:, 1:2]), r=[b_abg[gp], b_hsc], w=[b_gsc])
            act(lambda e: e.activation(out=gsc[:, G_G, :], in_=gsc[:, G_G, :], func=AF.Ln, bias=1.0), w=[b_gsc])
            dve(lambda e: e.tensor_scalar(out=gsc[:, G_G, :], in0=gsc[:, G_G, :], scalar1=negA[:, 0:1], scalar2=None, op0=ALU.mult),
                r=[b_negA], w=[b_gsc])
            act(lambda e: e.activation(out=gsc[:, G_BETA, :], in_=b_ap, func=AF.Exp, scale=-1.0), r=[b_abg[gp]], w=[b_gsc])
            dve(lambda e: e.tensor_scalar(out=gsc[:, G_BETA, :], in0=gsc[:, G_BETA, :], scalar1=1.0, scalar2=None, op0=ALU.add), w=[b_gsc])
            dve(lambda e: e.reciprocal(out=gsc[:, G_BETA, :], in_=gsc[:, G_BETA, :]), w=[b_gsc])
            act(lambda e: e.activation(out=gsc[:, G_LNB, :], in_=gsc[:, G_BETA, :], func=AF.Ln), w=[b_gsc])
            mm(PD[:, 0:4], lhsT=gmask[:, 2, :], rhs=gsc[:, G_G, :], r=[b_gmask, b_gsc], w=[b_PD])
            mm(PD[:, 4:8], lhsT=gmask[:, 3, :], rhs=gsc[:, G_G, :], r=[b_gmask, b_gsc], w=[b_PD])
            mm(PD[:, 8:12], lhsT=gmask[:, 4, :], rhs=gsc[:, G_G, :], r=[b_gmask, b_gsc], w=[b_PD])
            act(lambda e: e.copy(out=gsc[:, G_GCUM, :], in_=PD[:, 0:4]), r=[b_PD], w=[b_gsc])
            act(lambda e: e.copy(out=glc[:].rearrange("p a b -> p (a b)"), in_=PD[:, 4:12]), r=[b_PD], w=[b_glc])
            dve(lambda e: e.tensor_copy(out=gsc[0:64, G_GL, :], in_=glc[0:64, 0, :]), r=[b_glc], w=[b_gsc])
            dve(lambda e: e.tensor_copy(out=gsc[64:128, G_GL, :], in_=glc[64:128, 1, :]), r=[b_glc], w=[b_gsc])
            act(lambda e: e.activation(out=gsc[:, G_EG, :], in_=gsc[:, G_GCUM, :], func=AF.Exp), w=[b_gsc])
            dve(lambda e: e.tensor_tensor(out=gsc[:, G_EKG, :], in0=gsc[:, G_GL, :], in1=gsc[:, G_GCUM, :], op=ALU.subtract), w=[b_gsc])
            act(lambda e: e.activation(out=gsc[:, G_EKG, :], in_=gsc[:, G_EKG, :], func=AF.Exp), w=[b_gsc])
            act(lambda e: e.activation(out=glc[:].rearrange("p a b -> p (a b)"), in_=glc[:].rearrange("p a b -> p (a b)"), func=AF.Exp),
                w=[b_glc])
            dve(lambda e: e.tensor_scalar(out=gsc[:, G_NEGG, :], in0=gsc[:, G_GCUM, :], scalar1=-1.0, scalar2=None, op0=ALU.mult), w=[b_gsc])
            dve(lambda e: e.tensor_tensor(out=gsc[:, G_SKBG, :], in0=gsc[:, G_BETA, :], in1=gsc[:, G_EG, :], op=ALU.mult), w=[b_gsc])
            dve(lambda e: e.tensor_scalar(out=gsc[:, G_SKBG, :], in0=gsc[:, G_SKBG, :], scalar1=-1.0, scalar2=None, op0=ALU.mult), w=[b_gsc])
            dve(lambda e: e.tensor_tensor(out=gsc[:, G_GB, :], in0=gsc[:, G_GCUM, :], in1=gsc[:, G_LNB, :], op=ALU.add), w=[b_gsc])

            yield
            dve(lambda e: e.tensor_copy(out=glc8[:, 0:8:2], in_=glc[:, 0, :]), r=[b_glc], w=[b_glc8])
            dve(lambda e: e.tensor_copy(out=glc8[:, 1:8:2], in_=glc[:, 1, :]), r=[b_glc], w=[b_glc8])
            identf4 = bcast(ident_f[:], [128, 4, 128], 1)
            identb4 = bcast(ident_b[:], [128, 4, 128], 1)

            def colb(col):
                return bcast(gsc[:, col, :], [128, 4, 128], 2)
            dve(lambda e: e.tensor_tensor(out=dg1[:], in0=identf4, in1=colb(G_GCUM), op=ALU.mult), r=[b_identf, b_gsc], w=[b_dg1])
            dve(lambda e: e.tensor_tensor(out=dg2[:], in0=identf4, in1=colb(G_GB), op=ALU.mult), r=[b_identf, b_gsc], w=[b_dg2])
            dve(lambda e: e.tensor_tensor(out=dgn[:], in0=identf4, in1=colb(G_NEGG), op=ALU.mult), r=[b_identf, b_gsc], w=[b_dgn])
            for (PSx, bPSx, dgx, mk_) in ((PC[:, 0:512], b_PC, dg1, 0), (PA[:, 0:512], b_PA, dg2, 1)):
                mm(PSx, lhsT=ones_f[:], rhs=dgx[:].rearrange("p a b -> p (a b)"), start=True, stop=False, r=[b_onesf, b_dg1, b_dg2], w=[bPSx])
                mm(PSx, lhsT=ident_f[:], rhs=gmask4[:, mk_].rearrange("p a b -> p (a b)"), start=False, stop=False, r=[b_identf, b_gmask4], w=[bPSx])
                for tt in range(4):
                    mm(PSx[:, tt * 128:(tt + 1) * 128], lhsT=dgn[:, tt, :], rhs=ones_f[:], start=False, stop=True,
                       r=[b_dgn, b_onesf], w=[bPSx])
            act(lambda e: e.activation(out=decT[:].rearrange("p a b -> p (a b)"), in_=PC[:, 0:512], func=AF.Exp), r=[b_PC], w=[b_decT])
            act(lambda e: e.activation(out=decbT[:].rearrange("p a b -> p (a b)"), in_=PA[:, 0:512], func=AF.Exp), r=[b_PA], w=[b_decbT])
            yield
            for tt in range(4):
                kT_t = gT[:, 1, tt * 128:(tt + 1) * 128]
                qT_t = gT[:, 0, tt * 128:(tt + 1) * 128]
                mm(PC[:, 512 + tt * 128:512 + (tt + 1) * 128], lhsT=kT_t, rhs=kT_t, r=[b_gT], w=[b_PC])
                mm(PB[:, tt * 128:(tt + 1) * 128], lhsT=kT_t, rhs=qT_t, r=[b_gT], w=[b_PB])
            dve(lambda e: e.tensor_tensor(out=Um[0][:].rearrange("p a b -> p (a b)"), in0=PC[:, 512:1024], in1=decbT[:].rearrange("p a b -> p (a b)"),
                                          op=ALU.mult), r=[b_PC, b_decbT], w=[b_Um[0]])
            dve(lambda e: e.tensor_tensor(out=aqkT[:].rearrange("p a b -> p (a b)"), in0=PB[:, 0:512], in1=decT[:].rearrange("p a b -> p (a b)"),
                                          op=ALU.mult), r=[b_PB, b_decT], w=[b_aqkT])
            for tt in range(4):
                tr(PT[:, tt * 128:(tt + 1) * 128], Um[0][:, tt, :], ident_b[:], r=[b_Um[0], b_identb], w=[b_PT])
            act(lambda e: e.copy(out=Lm[0][:].rearrange("p a b -> p (a b)"), in_=PT[:, 0:512]), r=[b_PT], w=[b_Lm[0]])
            dve(lambda e: e.tensor_tensor(out=Pm[0][:], in0=identb4, in1=Um[0][:], op=ALU.subtract), r=[b_identb, b_Um[0]], w=[b_Pm[0]])
            yield
            cu, cp = 0, 0
            for lvl in range(5):
                nu = 1 - cu
                for tt in range(4):
                    mm(PC[:, tt * 128:(tt + 1) * 128], lhsT=Um[cu][:, tt, :], rhs=Lm[cu][:, tt, :], r=[b_Um[cu], b_Lm[cu]], w=[b_PC])
                if lvl < 4:
                    for tt in range(4):
                        mm(PA[:, tt * 128:(tt + 1) * 128], lhsT=Lm[cu][:, tt, :], rhs=Um[cu][:, tt, :], r=[b_Um[cu], b_Lm[cu]], w=[b_PA])
                act(lambda e, nu=nu: e.copy(out=Lm[nu][:].rearrange("p a b -> p (a b)"), in_=PC[:, 0:512]), r=[b_PC], w=[b_Lm[nu]])
                if lvl < 4:
                    act(lambda e, nu=nu: e.copy(out=Um[nu][:].rearrange("p a b -> p (a b)"), in_=PA[:, 0:512]), r=[b_PA], w=[b_Um[nu]])
                for tt in range(4):
                    mm(PB[:, tt * 128:(tt + 1) * 128], lhsT=Lm[nu][:, tt, :], rhs=Pm[cp][:, tt, :], r=[b_Lm[nu], b_Pm[cp]], w=[b_PB])
                dve(lambda e, cp=cp: e.tensor_tensor(out=Pm[1 - cp][:].rearrange("p a b -> p (a b)"), in0=PB[:, 0:512],
                                                     in1=Pm[cp][:].rearrange("p a b -> p (a b)"), op=ALU.add), r=[b_PB, b_Pm[cp]], w=[b_Pm[1 - cp]])
                cu = nu
                cp = 1 - cp
            yield
            Tt = Pm[cp]
            bTt = b_Pm[cp]
            dve(lambda e: e.tensor_tensor(out=Xm[:, :, 0:128], in0=gtok[:, :, 2, :], in1=colb(G_BETA), op=ALU.mult), r=[b_gtok, b_gsc], w=[b_Xm])
            dve(lambda e: e.tensor_tensor(out=Xm[:, :, 128:256], in0=gtok[:, :, 1, :], in1=colb(G_SKBG), op=ALU.mult), r=[b_gtok, b_gsc], w=[b_Xm])
            for tt in range(4):
                mm(PC[:, tt * 256:(tt + 1) * 256], lhsT=Tt[:, tt, :], rhs=Xm[:, tt, :], r=[bTt, b_Xm], w=[b_PC])
            act(lambda e: e.copy(out=uw[:].rearrange("p a b -> p (a b)"), in_=PC[:, 0:1024]), r=[b_PC], w=[b_uw])
            dve(lambda e: e.tensor_tensor(out=kgm[0:64, :, 0, :], in0=gtok[0:64, :, 1, :], in1=bcast(gsc[0:64, G_EKG, :], [64, 4, 128], 2), op=ALU.mult),
                r=[b_gtok, b_gsc], w=[b_kgm])
            dve(lambda e: e.tensor_tensor(out=kgm[64:128, :, 1, :], in0=gtok[64:128, :, 1, :], in1=bcast(gsc[64:128, G_EKG, :], [64, 4, 128], 2), op=ALU.mult),
                r=[b_gtok, b_gsc], w=[b_kgm])
            dve(lambda e: e.tensor_tensor(out=Dg[:], in0=identb4, in1=colb(G_EG), op=ALU.mult), r=[b_identb, b_gsc], w=[b_Dg])
            for tt in range(4):
                mm(PA[:, tt * 128:(tt + 1) * 128], lhsT=gtok[:, tt, 0, :], rhs=Dg[:, tt, :], start=True, stop=False, r=[b_gtok, b_Dg], w=[b_PA])
                mm(PA[:, tt * 128:(tt + 1) * 128], lhsT=uw[:, tt, 128:256], rhs=aqkT[:, tt, :], start=False, stop=True, r=[b_uw, b_aqkT], w=[b_PA])
            act(lambda e: e.copy(out=QpA[:, :, 0:64], in_=PA[:, 0:512].rearrange("p (a b) -> p a b", a=4)[:, :, 0:64]), r=[b_PA], w=[b_Qp])
            act(lambda e: e.copy(out=QpB[:, :, 64:128], in_=PA[:, 0:512].rearrange("p (a b) -> p a b", a=4)[:, :, 64:128]), r=[b_PA], w=[b_Qp])
            yield
            for tt in range(4):
                for c in range(2):
                    r0 = 64 * c
                    ch = tt * 2 + c
                    mm(PB[:, ch * 128:(ch + 1) * 128], lhsT=uw[:, tt, 128:256], rhs=kgm[:, tt, c, :], r=[b_uw, b_kgm], w=[b_PB])
            dve(lambda e: e.tensor_tensor(out=MTf[:], in0=bcast(ident_f[:], [128, 8, 128], 1), in1=bcast(glc8[:], [128, 8, 128], 2), op=ALU.mult),
                r=[b_identf, b_glc8], w=[b_MTf])
            dve(lambda e: e.tensor_tensor(out=MT8[:].rearrange("p a b -> p (a b)"), in0=PB[:, 0:1024], in1=MTf[:].rearrange("p a b -> p (a b)"),
                                          op=ALU.add), r=[b_PB, b_MTf], w=[b_MT8])
            for ch in range(8):
                tt, c = ch // 2, ch % 2
                r0 = 64 * c
                i = grp * 4 + tt
                PSc = PC[:, (ch % 2) * 512:(ch % 2) * 512 + 128]
                mm(PSc, lhsT=kgm[:, tt, c, :], rhs=uw[:, tt, 0:128], start=True, stop=False, r=[b_kgm, b_uw], w=[b_PC])
                mm(PSc, lhsT=MT8[:, ch, :], rhs=Sb9[:, ch, :], start=False, stop=True, r=[b_MT8, b_Sb9[ch]], w=[b_PC])
                if ch < 7:
                    act(lambda e, ch=ch, PSc=PSc: e.copy(out=Sb9[:, ch + 1, :], in_=PSc), r=[b_PC], w=[b_Sb9[ch + 1]])
                else:
                    act(lambda e, PSc=PSc: e.copy(out=Sb9[:, 8, :], in_=PSc), r=[b_PC], w=[b_Sb9[8]])
                    if i == NT - 1:
                        act(lambda e, PSc=PSc: e.copy(out=Sf[:], in_=PSc), r=[b_PC], w=[b_Sf])
            yield
            for tt in range(4):
                mm(PA[:, tt * 128:(tt + 1) * 128], lhsT=QpA[:, tt, :], rhs=Sb9[:, 2 * tt, :], start=True, stop=False, r=[b_Qp, b_Sb9[2 * tt]], w=[b_PA])
                mm(PA[:, tt * 128:(tt + 1) * 128], lhsT=QpB[:, tt, :], rhs=Sb9[:, 2 * tt + 1, :], start=False, stop=False,
                   r=[b_Qp, b_Sb9[2 * tt + 1]], w=[b_PA])
                mm(PA[:, tt * 128:(tt + 1) * 128], lhsT=aqkT[:, tt, :], rhs=uw[:, tt, 0:128], start=False, stop=True, r=[b_aqkT, b_uw], w=[b_PA])
            act(lambda e: e.copy(out=Sb9[:, 0, :], in_=Sb9[:, 8, :]), r=[b_Sb9[8]], w=[b_Sb9[0]])
            act(lambda e: e.copy(out=og4[:].rearrange("p a b -> p (a b)"), in_=PA[:, 0:512]), r=[b_PA], w=[b_og4])
            dve(lambda e: e.tensor_tensor(out=og4q[:], in0=og4[:], in1=og4[:], op=ALU.mult), r=[b_og4], w=[b_og4q])
            dve(lambda e: e.tensor_reduce(out=og4s[:], in_=og4q[:], axis=AX.X, op=ALU.add), r=[b_og4q], w=[b_og4s])
            act(lambda e: e.activation(out=og4s[:], in_=og4s[:], func=AF.Ln, scale=1.0 / 128, bias=EPS), w=[b_og4s])
            act(lambda e: e.activation(out=og4s[:], in_=og4s[:], func=AF.Exp, scale=-0.5), w=[b_og4s])
            dve(lambda e: e.tensor_tensor(out=og4[:], in0=og4[:], in1=bcast(og4s[:], [128, 4, 128], 2), op=ALU.mult), r=[b_og4s], w=[b_og4])
            dve(lambda e: e.tensor_tensor(out=og4[:], in0=og4[:], in1=bcast(gnb[:], [128, 4, 128], 1), op=ALU.mult), r=[b_gnb], w=[b_og4])
            dve(lambda e: e.tensor_tensor(out=omg[:], in0=og4[:], in1=zsil[gp][:], op=ALU.mult), r=[b_og4] + b_zsil[gp], w=[b_omg])
            for tt in range(4):
                tr(PT[:, tt * 128:(tt + 1) * 128], omg[:, tt, :], ident_b[:], r=[b_omg, b_identb], w=[b_PT])
            act(lambda e: e.copy(out=omT[gp][:, 1, :], in_=PT[:, 0:512]), r=[b_PT], w=[b_omT[gp]])
            for hh in range(2):
                if phaseB:
                    kch = (grp * 512) // CH
                    oc = (grp * 512) % CH
                    fw.dma("sp", xin[kch][hh * 128:(hh + 1) * 128, oc:oc + 512], omT[gp][:, hh, :], reads=[b_omT[gp]], writes=[b_xin[kch]])
                else:
                    fw.dma("sp", omT_o[hh * 128:(hh + 1) * 128, grp * 512:(grp + 1) * 512], omT[gp][:, hh, :], reads=[b_omT[gp]])


        def gen_G(grp):
            gp2 = grp % 2
            fw.dma("sp", csT[gp2][:, 0].rearrange("p a b -> p (a b)"), cos_d[:, grp * 128:(grp + 1) * 128], writes=[b_csT[gp2]])
            fw.dma("sp", csT[gp2][:, 1].rearrange("p a b -> p (a b)"), sin_d[:, grp * 128:(grp + 1) * 128], writes=[b_csT[gp2]])
            for tt in range(4):
                i = grp * 4 + tt
                s = i % NXB
                fw.dma("sp", xt[s][:], x_d[i * 128:(i + 1) * 128, :], writes=[b_xt[s]])
                act(lambda e, s=s: e.activation(out=xs[s][:], in_=xt[s][:], func=AF.Square, accum_out=ssq[s][:]),
                    r=[b_xt[s]], w=[b_xs[s], b_ssq[s]])
                act(lambda e, s=s: e.activation(out=ssq[s][:], in_=ssq[s][:], func=AF.Ln, scale=1.0 / 1024, bias=EPS), w=[b_ssq[s]])
                act(lambda e, s=s: e.activation(out=ssq[s][:], in_=ssq[s][:], func=AF.Exp, scale=-0.5), w=[b_ssq[s]])
                dve(lambda e, s=s: e.tensor_scalar(out=xs[s][:], in0=xt[s][:], scalar1=ssq[s][:, 0:1], scalar2=None,
                                                   op0=ALU.mult), r=[b_xt[s], b_ssq[s]], w=[b_xs[s]])
                for kt in range(8):
                    tr(PT[:, kt * 128:(kt + 1) * 128], xs[s][:, kt * 128:(kt + 1) * 128], ident_b[:],
                       r=[b_xs[s], b_identb], w=[b_PT])
                act(lambda e, tt=tt: e.copy(out=xnT[:, :, tt * 128:(tt + 1) * 128],
                                            in_=PT[:].rearrange("p (k t) -> p k t", k=8)),
                    r=[b_PT], w=[b_xnT[tt]])

            yield
            pool(lambda e: e.tensor_copy(out=raw[gp2][:, :, 0:3], in_=raw[1 - gp2][:, :, 512:515]), r=[b_raw[1 - gp2]], w=[b_raw[gp2]])
            for c3 in range(3):
                for kt in range(8):
                    mm(PB[:, 0:512], lhsT=wgdn[:, kt, c3 * 128:(c3 + 1) * 128], rhs=xnT[:, kt, :],
                       start=(kt == 0), stop=(kt == 7), r=[b_wgdn] + b_xnT, w=[b_PB])
                act(lambda e, c3=c3: e.copy(out=raw[gp2][:, c3, 3:515], in_=PB[:, 0:512]), r=[b_PB], w=[b_raw[gp2]])

            yield

        def gen_F(i):
            grp = i // 4
            tt = i % 4
            gp2 = grp % 2
            p2 = i % 2
            for kt in range(8):
                mm(PA[:, 0:512], lhsT=xnT[:, kt, tt * 128:(tt + 1) * 128], rhs=wtok[:, kt, 0:512],
                   start=(kt == 0), stop=(kt == 7), r=[b_xnT[tt], b_wtok], w=[b_PA])
            for kt in range(8):
                mm(PA[:, 512:782], lhsT=xnT[:, kt, tt * 128:(tt + 1) * 128], rhs=wtok[:, kt, 512:782],
                   start=(kt == 0), stop=(kt == 7), r=[b_xnT[tt], b_wtok], w=[b_PA])
            act(lambda e: e.copy(out=pj[p2][:], in_=PA[:, 0:782]), r=[b_PA], w=[b_pj[p2]])
            yield
            x1 = pj[p2][:, 0:448].rearrange("p (h d) -> p h d", h=7)[:, :, 0:32]
            x2 = pj[p2][:, 0:448].rearrange("p (h d) -> p h d", h=7)[:, :, 32:64]
            cosb = bcast(csT[gp2][:, 0, tt, :], [128, 7, 32], 1)
            sinb = bcast(csT[gp2][:, 1, tt, :], [128, 7, 32], 1)
            dve(lambda e: e.tensor_tensor(out=rt[:, 0], in0=x1, in1=cosb, op=ALU.mult), r=[b_pj[p2], b_csT[gp2]], w=[b_rt])
            dve(lambda e: e.tensor_tensor(out=rt[:, 1], in0=x2, in1=sinb, op=ALU.mult), r=[b_pj[p2], b_csT[gp2]], w=[b_rt])
            dve(lambda e: e.tensor_tensor(out=rt[:, 2], in0=x2, in1=cosb, op=ALU.mult), r=[b_pj[p2], b_csT[gp2]], w=[b_rt])
            dve(lambda e: e.tensor_tensor(out=rt[:, 3], in0=x1, in1=sinb, op=ALU.mult), r=[b_pj[p2], b_csT[gp2]], w=[b_rt])
            dve(lambda e: e.tensor_tensor(out=rq[p2][:, :, 0:32], in0=rt[:, 0], in1=rt[:, 1], op=ALU.subtract),
                 r=[b_rt], w=[b_rq[p2]])
            dve(lambda e: e.tensor_tensor(out=rq[p2][:, :, 32:64], in0=rt[:, 2], in1=rt[:, 3], op=ALU.add),
                 r=[b_rt], w=[b_rq[p2]])
            yield
            pool(lambda e: e.tensor_copy(out=ko[p2][:, 0:6:2, :], in_=rq[p2][:, 4:7, :]), r=[b_rq[p2]], w=[b_ko[p2]])
            pool(lambda e: e.tensor_copy(out=ko[p2][:, 1:6:2, :],
                                         in_=pj[p2][:, 448:640].rearrange("p (h d) -> p h d", h=3)),
                 r=[b_pj[p2]], w=[b_ko[p2]])
            fw.dma("sp", kv_o[i * 128:(i + 1) * 128, :], ko[p2][:, 0:4, :].rearrange("p a b -> p (a b)"), reads=[b_ko[p2]])
            if i >= NT - 4:
                wi = i - (NT - 4)
                fw.dma("sp", win_o[wi * 128:(wi + 1) * 128, :], ko[p2][:, 4:6, :].rearrange("p a b -> p (a b)"),
                       reads=[b_ko[p2]])
            act(lambda e: e.copy(out=qkb[:], in_=rq[p2][:]), r=[b_rq[p2]], w=[b_qkb])
            act(lambda e: e.copy(out=Vsel[:, i, 0:64], in_=pj[p2][:, 512:576]), r=[b_pj[p2]], w=[b_vsel[i]])
            act(lambda e: e.copy(out=Vwin[:, i % 8, 0:64], in_=pj[p2][:, 576:640]), r=[b_pj[p2]], w=[b_vwin[i % 8]])
            dve(lambda e: e.tensor_copy(out=kvc2[:, 0, :].rearrange("p (a d) -> p a d", a=2),
                                         in_=bcast(rq[p2][:, 4, :], [128, 2, 64], 1)), r=[b_rq[p2]], w=[b_kvc2])
            dve(lambda e: e.tensor_copy(out=kvc2[:, 1, :].rearrange("p (a d) -> p a d", a=2),
                                         in_=bcast(pj[p2][:, 448:512], [128, 2, 64], 1)), r=[b_pj[p2]], w=[b_kvc2])
            act(lambda e: e.activation(out=gz[:], in_=pj[p2][:, 640:780], func=AF.Exp, scale=-1.0), r=[b_pj[p2]], w=[b_gz])
            dve(lambda e: e.tensor_scalar(out=gz[:], in0=gz[:], scalar1=1.0, scalar2=None, op0=ALU.add), w=[b_gz])
            dve(lambda e: e.reciprocal(out=gz[:], in_=gz[:]), w=[b_gz])
            dve(lambda e: e.tensor_copy(out=gates[i % 2][:], in_=gz[:, 0:12]), r=[b_gz], w=[b_gates[i % 2]])
            dve(lambda e, tt=tt: e.tensor_tensor(out=zsil[gp2][:, tt, :], in0=gz[:, 12:140], in1=pj[p2][:, 652:780], op=ALU.mult),
                r=[b_gz, b_pj[p2]], w=[b_zsil[gp2][tt]])
            pool(lambda e, tt=tt: e.tensor_copy(out=abg[gp2][:, tt, :], in_=pj[p2][:, 780:782]), r=[b_pj[p2]], w=[b_abg[gp2]])
            yield
            for h in range(4):
                tr(PT[0:64, h * 128:(h + 1) * 128], qkb[:, h, :], ident_b[:], r=[b_qkb, b_identb], w=[b_PT])
            tr(PT[0:64, 512:640], qkb[:, 5, :], ident_b[:], r=[b_qkb, b_identb], w=[b_PT])
            tr(PT[0:64, 640:768], qkb[:, 6, :], ident_b[:], r=[b_qkb, b_identb], w=[b_PT])
            tr(PT[:, 768:896], kvc2[:, 0, :], ident_b[:], r=[b_kvc2, b_identb], w=[b_PT])
            tr(PT[:, 896:1024], kvc2[:, 1, :], ident_b[:], r=[b_kvc2, b_identb], w=[b_PT])
            act(lambda e: e.copy(out=QT[i % 2][:], in_=PT[0:64, 0:512]), r=[b_PT], w=[b_QT[i % 2]])
            act(lambda e: e.copy(out=Qaug[i % 2][0:64, 0, :], in_=PT[0:64, 0:256]), r=[b_PT], w=[b_Qaug[i % 2]])
            act(lambda e: e.copy(out=Qaug[i % 2][0:64, 1, :], in_=PT[0:64, 0:256]), r=[b_PT], w=[b_Qaug[i % 2]])
            act(lambda e: e.copy(out=KselT[0:64, i * 128:(i + 1) * 128], in_=PT[0:64, 512:640]),
                r=[b_PT], w=[b_ksel[i]])
            act(lambda e: e.copy(out=KwinT[0:64, (i % 8) * 128:(i % 8 + 1) * 128], in_=PT[0:64, 640:768]),
                r=[b_PT], w=[b_kwin[i % 8]])
            yield
            pool(lambda e: e.tensor_copy(out=Rk[:, :, 0:32], in_=Rk[:, :, 128:160]), w=[b_Rk])
            act(lambda e: e.copy(out=Rk[0:64, :, 32:160], in_=PT[0:64, 768:1024].rearrange("p (a t) -> p a t", a=2)),
                r=[b_PT], w=[b_Rk])
            act(lambda e: e.copy(out=Rk[64:128, :, 31:159], in_=PT[64:128, 768:1024].rearrange("p (a t) -> p a t", a=2)),
                r=[b_PT], w=[b_Rk])
            yield
            m0 = 1 if i == 0 else 0
            nb = 8 - m0
            n0 = 8 * i - 1 + m0
            for lp in range(16):
                c0 = 16 + 2 * lp + 16 * m0
                mm(PD[0:64, 0:nb], lhsT=cmpw[:, 0, lp, :], rhs=Rk[:, 0, c0:c0 + 16 * (nb - 1) + 1:16],
                   start=(lp == 0), stop=(lp == 15), r=[b_cmpw, b_Rk], w=[b_PD])
            act(lambda e: e.activation(out=ckT[:, n0:n0 + nb], in_=PD[0:64, 0:nb], func=AF.Identity, bias=ckb[:, 0:1]),
                r=[b_PD, b_ckb], w=[b_ckT])
            for lp in range(16):
                c0 = 16 + 2 * lp + 16 * m0
                mm(PD[0:nb, 64:128], lhsT=Rk[:, 1, c0:c0 + 16 * (nb - 1) + 1:16], rhs=cmpw[:, 1, lp, :],
                   start=(lp == 0), stop=(lp == 15), r=[b_cmpw, b_Rk], w=[b_PD])
            dve(lambda e: e.tensor_tensor(out=cvnew[0:nb, :], in0=PD[0:nb, 64:128], in1=cvb[0:nb, :], op=ALU.add),
                r=[b_PD, b_cvb], w=[b_cvnew])
            segs = []
            n = n0
            while n < n0 + nb:
                jt = n // 128
                cnt = min(n0 + nb - n, (jt + 1) * 128 - n)
                segs.append((n, cnt))
                n += cnt
            for (ns, cnt) in segs:
                fw.dma("sp", cvx[ns % 128:ns % 128 + cnt, ns // 128, 0:64], cvnew[ns - n0:ns - n0 + cnt, :],
                       reads=[b_cvnew], writes=[b_cvx])

            yield

        def gen_B(i):
            grp = i // 4
            tt = i % 4
            gp2 = grp % 2
            p2 = i % 2
            njt = (8 * i + 6) // 128 + 1
            for jt in range(njt):
                pb = jt
                mm(PA[:, 0:512], lhsT=ckT[:, jt * 128:(jt + 1) * 128], rhs=QT[i % 2][:], r=[b_ckT, b_QT[i % 2]], w=[b_PA])
                act(lambda e, pb=pb: e.activation(out=PTc[pb][:], in_=PA[:, 0:512], func=AF.Exp, scale=0.125),
                    r=[b_PA], w=[b_PTc[pb]])
                mk = None
                if jt == njt - 1:
                    mk = i % 16
                elif jt == njt - 2 and i % 16 == 0:
                    mk = 16
                if mk is not None:
                    dve(lambda e, mk=mk, pb=pb: e.tensor_tensor(out=PTc[pb][:].rearrange("p (h q) -> p h q", h=4),
                                                                in0=PTc[pb][:].rearrange("p (h q) -> p h q", h=4),
                                                                in1=bcast(cmpmask[:, mk, :], [128, 4, 128], 1), op=ALU.mult),
                        r=[b_cmpmask], w=[b_PTc[pb]])
            for h in range(4):
                for jt in range(njt):
                    mm(PC[:, (h // 2) * 512 + (h % 2) * 193:(h // 2) * 512 + (h % 2) * 193 + 193],
                       lhsT=PTc[jt][:, h * 128:(h + 1) * 128], rhs=cvx[:, jt, :],
                       start=(jt == 0), stop=(jt == njt - 1), r=[b_PTc[jt], b_cvx], w=[b_PC])
            act(lambda e: e.copy(out=acc_c[:, 0:2, :], in_=PC[:, 0:386].rearrange("p (h c) -> p h c", h=2)),
                r=[b_PC], w=[b_accc])
            act(lambda e: e.copy(out=acc_c[:, 2:4, :], in_=PC[:, 512:898].rearrange("p (h c) -> p h c", h=2)),
                r=[b_PC], w=[b_accc])
            yield
            dve(lambda e: e.tensor_scalar(out=rcp[:, 0:4], in0=acc_c[:, :, 192], scalar1=1e-30, scalar2=None, op0=ALU.max),
                r=[b_accc], w=[b_rcp])
            dve(lambda e: e.reciprocal(out=rcp[:, 0:4], in_=rcp[:, 0:4]), w=[b_rcp])
            dve(lambda e: e.tensor_scalar(out=imp[:], in0=acc_c[:, 0, 64:192], scalar1=rcp[:, 0:1], scalar2=None, op0=ALU.mult),
                r=[b_accc, b_rcp], w=[b_imp])
            for h in range(1, 4):
                dve(lambda e, h=h: e.scalar_tensor_tensor(out=imp[:], in0=acc_c[:, h, 64:192], scalar=rcp[:, h:h + 1],
                                                          in1=imp[:], op0=ALU.mult, op1=ALU.add),
                    r=[b_accc, b_rcp], w=[b_imp])
            yield
            dve(lambda e: e.tensor_tensor(out=score[:], in0=imp[:], in1=prel[:, 128 - 2 * i:256 - 2 * i], op=ALU.add),
                r=[b_imp, b_prel], w=[b_score])
            dve(lambda e: e.tensor_scalar(out=score[:, 0:1], in0=score[:, 0:1], scalar1=1e4, scalar2=None, op0=ALU.add),
                w=[b_score])
            yield
            dve(lambda e: e.max(out=mx8[:, 0:8], in_=score[:]), r=[b_score], w=[b_mx8])
            dve(lambda e: e.match_replace(out=sc2[:], in_to_replace=mx8[:, 0:8], in_values=score[:], imm_value=-3e38),
                r=[b_score, b_mx8], w=[b_sc2])
            dve(lambda e: e.max(out=mx8[:, 8:16], in_=sc2[:]), r=[b_sc2], w=[b_mx8])
            dve(lambda e: e.tensor_reduce(out=thr[:], in_=mx8[:, 8:16], axis=AX.X, op=ALU.min), r=[b_mx8], w=[b_thr])
            dve(lambda e: e.tensor_scalar(out=sc2[:], in0=score[:], scalar1=thr[:, 0:1], scalar2=None, op0=ALU.is_ge),
                r=[b_score, b_thr], w=[b_sc2])
            dve(lambda e: e.scalar_tensor_tensor(out=sc2[:], in0=score[:], scalar=-1e29, in1=sc2[:],
                                                 op0=ALU.is_gt, op1=ALU.mult), r=[b_score], w=[b_sc2])
            dve(lambda e: e.tensor_scalar(out=mbt[:, 0, :], in0=sc2[:], scalar1=-NEGB, scalar2=NEGB,
                                          op0=ALU.mult, op1=ALU.add), r=[b_sc2], w=[b_mbt])
            dve(lambda e: e.tensor_copy(out=mbt[:, 1, 0:64], in_=mbt[:, 0, 64:128]), w=[b_mbt])
            dve(lambda e: e.tensor_copy(out=mbt[:, 1, 64:128], in_=mbt[:, 0, 0:64]), w=[b_mbt])
            yield
            mm(PD[:, 0:128], lhsT=mbt[:, 1, :], rhs=ident_f[:], r=[b_mbt, b_identf], w=[b_PD])
            mm(PD[:, 128:256], lhsT=mbt[:, 0, :], rhs=ident_f[:], r=[b_mbt, b_identf], w=[b_PD])
            for hh_ in range(2):
                dve(lambda e, hh_=hh_: e.tensor_copy(out=Qaug[i % 2][64:128, 0, hh_ * 128:(hh_ + 1) * 128], in_=PD[64:128, 0:128]),
                    r=[b_PD], w=[b_Qaug[i % 2]])
                dve(lambda e, hh_=hh_: e.tensor_copy(out=Qaug[i % 2][64:128, 1, hh_ * 128:(hh_ + 1) * 128], in_=PD[64:128, 128:256]),
                    r=[b_PD], w=[b_Qaug[i % 2]])

            yield
            sgroups = []
            t = 0
            while t <= i:
                gt_ = min(4, i + 1 - t)
                sgroups.append((t, gt_))
                t += gt_

            def emit_S(gi_):
                t_, gt_ = sgroups[gi_]
                PSs_ = PA if gi_ % 2 == 0 else PB
                bPSs_ = b_PA if gi_ % 2 == 0 else b_PB
                for u in range(gt_):
                    tk = t_ + u
                    ab = 0 if tk < 32 else 1
                    mm(PSs_[:, u * 256:(u + 1) * 256], lhsT=KselT[:, tk * 128:(tk + 1) * 128], rhs=Qaug[i % 2][:, ab, :],
                       r=[b_ksel[tk], b_eind, b_Qaug[i % 2]], w=[bPSs_])
            emit_S(0)
            for gi in range(len(sgroups)):
                t, gt_ = sgroups[gi]
                pb = gi % 2
                PSs = PA if pb == 0 else PB
                bPSs = b_PA if pb == 0 else b_PB
                if gi + 1 < len(sgroups):
                    emit_S(gi + 1)
                yield
                act(lambda e, PSs=PSs, gt_=gt_, pb=pb: e.activation(out=PTs[pb][:, 0:gt_ * 256], in_=PSs[:, 0:gt_ * 256],
                                                                  func=AF.Exp, scale=0.125), r=[bPSs], w=[b_PTs[pb]])
                if t + gt_ - 1 == i:
                    u = gt_ - 1
                    dve(lambda e, u=u, pb=pb: e.tensor_tensor(
                        out=PTs[pb][:, u * 256:(u + 1) * 256].rearrange("p (h q) -> p h q", h=2),
                        in0=PTs[pb][:, u * 256:(u + 1) * 256].rearrange("p (h q) -> p h q", h=2),
                        in1=bcast(tri[:, 0, :], [128, 2, 128], 1), op=ALU.mult), r=[b_tri], w=[b_PTs[pb]])
                for u in range(gt_):
                    tk = t + u
                    for h in range(2):
                        mm(PC[:, h * 512:h * 512 + 65], lhsT=PTs[pb][:, u * 256 + h * 128:u * 256 + (h + 1) * 128],
                           rhs=Vsel[:, tk, :], start=(tk == 0), stop=(tk == i), r=[b_PTs[pb], b_vsel[tk]], w=[b_PC])
            gi = len(sgroups)
            yield
            act(lambda e: e.copy(out=acc_sw[:, 0, :], in_=PC[:, 0:65]), r=[b_PC], w=[b_accsw])
            act(lambda e: e.copy(out=acc_sw[:, 1, :], in_=PC[:, 512:577]), r=[b_PC], w=[b_accsw])

            yield
            t0w = max(0, i - 4)
            wt = list(range(t0w, i + 1))
            pb = gi % 2
            PSs = PA if pb == 0 else PB
            bPSs = b_PA if pb == 0 else b_PB
            for gsub in range(0, len(wt), 4):
                sub = wt[gsub:gsub + 4]
                for u, tk in enumerate(sub):
                    mm(PSs[:, u * 256:(u + 1) * 256], lhsT=KwinT[:, (tk % 8) * 128:(tk % 8 + 1) * 128], rhs=QT[i % 2][:, 0:256],
                       r=[b_kwin[tk % 8], b_QT[i % 2]], w=[bPSs])
                act(lambda e, PSs=PSs, n_=len(sub), pb=pb: e.activation(out=PTs[pb][:, 0:n_ * 256], in_=PSs[:, 0:n_ * 256],
                                                                       func=AF.Exp, scale=0.125), r=[bPSs], w=[b_PTs[pb]])
                for u, tk in enumerate(sub):
                    mkk = None
                    if tk == i:
                        mkk = 0
                    elif tk == i - 4:
                        mkk = 1
                    if mkk is not None:
                        dve(lambda e, u=u, pb=pb, mkk=mkk: e.tensor_tensor(
                            out=PTs[pb][:, u * 256:(u + 1) * 256].rearrange("p (h q) -> p h q", h=2),
                            in0=PTs[pb][:, u * 256:(u + 1) * 256].rearrange("p (h q) -> p h q", h=2),
                            in1=bcast(tri[:, mkk, :], [128, 2, 128], 1), op=ALU.mult), r=[b_tri], w=[b_PTs[pb]])
                for u, tk in enumerate(sub):
                    for h in range(2):
                        mm(PC[:, 386 + h * 512:451 + h * 512], lhsT=PTs[pb][:, u * 256 + h * 128:u * 256 + (h + 1) * 128],
                           rhs=Vwin[:, tk % 8, :], start=(tk == wt[0]), stop=(tk == i), r=[b_PTs[pb], b_vwin[tk % 8]], w=[b_PC])
                pb = 1 - pb
                PSs = PA if pb == 0 else PB
                bPSs = b_PA if pb == 0 else b_PB
            act(lambda e: e.copy(out=acc_sw[:, 2, :], in_=PC[:, 386:451]), r=[b_PC], w=[b_accsw])
            act(lambda e: e.copy(out=acc_sw[:, 3, :], in_=PC[:, 898:963]), r=[b_PC], w=[b_accsw])

            yield
            dve(lambda e: e.tensor_scalar(out=rcp[:, 4:8], in0=acc_sw[:, :, 64], scalar1=1e-30, scalar2=None, op0=ALU.max),
                r=[b_accsw], w=[b_rcp])
            dve(lambda e: e.reciprocal(out=rcp[:, 4:8], in_=rcp[:, 4:8]), w=[b_rcp])
            g3 = gates[i % 2][:, 0:6].rearrange("p (h j) -> p h j", h=2)
            cf = coef[:].rearrange("p (h j) -> p h j", h=2)
            dve(lambda e: e.tensor_tensor(out=cf[:, :, 0], in0=g3[:, :, 0], in1=rcp[:, 0:2], op=ALU.mult),
                r=[b_gates[i % 2], b_rcp], w=[b_coef])
            dve(lambda e: e.tensor_tensor(out=cf[:, :, 1], in0=g3[:, :, 1], in1=rcp[:, 4:6], op=ALU.mult),
                r=[b_gates[i % 2], b_rcp], w=[b_coef])
            dve(lambda e: e.tensor_tensor(out=cf[:, :, 2], in0=g3[:, :, 2], in1=rcp[:, 6:8], op=ALU.mult),
                r=[b_gates[i % 2], b_rcp], w=[b_coef])
            for h in range(2):
                dve(lambda e, h=h: e.tensor_scalar(out=onsa[:, h * 64:(h + 1) * 64], in0=acc_c[:, h, 0:64],
                                                   scalar1=coef[:, 3 * h:3 * h + 1], scalar2=None, op0=ALU.mult),
                    r=[b_accc, b_coef], w=[b_onsa])
                dve(lambda e, h=h: e.scalar_tensor_tensor(out=onsa[:, h * 64:(h + 1) * 64], in0=acc_sw[:, h, 0:64],
                                                          scalar=coef[:, 3 * h + 1:3 * h + 2], in1=onsa[:, h * 64:(h + 1) * 64],
                                                          op0=ALU.mult, op1=ALU.add), r=[b_accsw, b_coef], w=[b_onsa])
                dve(lambda e, h=h: e.scalar_tensor_tensor(out=om[:, h * 64:(h + 1) * 64], in0=acc_sw[:, 2 + h, 0:64],
                                                          scalar=coef[:, 3 * h + 2:3 * h + 3], in1=onsa[:, h * 64:(h + 1) * 64],
                                                          op0=ALU.mult, op1=ALU.add), r=[b_accsw, b_coef, b_onsa], w=[b_om])
            b_om_tiles = None

            if dbg and i == 1:
                fw.dma("sp", dbg_o[:, 0:772], acc_c[:].rearrange("p a b -> p (a b)"), reads=[b_accc])
                fw.dma("sp", dbg_o[:, 772:1032], acc_sw[:].rearrange("p a b -> p (a b)"), reads=[b_accsw])
                fw.dma("sp", dbg_o[:, 1032:1160], score[:], reads=[b_score])
                fw.dma("sp", dbg_o[:, 1160:1172], gates[i % 2][:], reads=[b_gates[i % 2]])
                fw.dma("sp", dbg_o[:, 1172:1300], mbt[:, 0, :], reads=[b_mbt])
            yield
            tr(PT[:, 0:128], om[:, 0:128], ident_b[:], r=[b_om, b_identb], w=[b_PT])
            act(lambda e, tt=tt: e.copy(out=omT[gp2][:, 0, tt * 128:(tt + 1) * 128], in_=PT[:, 0:128]), r=[b_PT], w=[b_omT[gp2]])
            yield

        if phaseB:
            wgu_s = nc.dram_tensor("wgu_s", [22, 128, 8 * 256], BF16).ap()
            wdn_s = nc.dram_tensor("wdn_s", [22, 128, 1024], BF16).ap()
            b_wgus = [Buf() for _ in range(22)]
            b_wdns = Buf()
            for f in range(22):
                dstv = wgu_s[f].rearrange("p (k c) -> p k c", k=8)
                fw.dma("pool", dstv[:, :, 0:128], wgu_d[:, f * 128:(f + 1) * 128].rearrange("(k p) c -> p k c", p=128), writes=[b_wgus[f]])
                fw.dma("pool", dstv[:, :, 128:256], wgu_d[:, 2816 + f * 128:2816 + (f + 1) * 128].rearrange("(k p) c -> p k c", p=128),
                       writes=[b_wgus[f]])
            fw.dma("pool", wdn_s[:, :, :], wdn_d[:, :].rearrange("(f p) c -> f p c", p=128), writes=[b_wdns])
        gdn_prev = None
        try:
          chk("setup")
          def drain_gen(g_):
              if g_ is not None:
                  for _ in g_:
                      pass

          def interleave(gens):
              alive = [g_ for g_ in gens if g_ is not None]
              while alive:
                  for g_ in list(alive):
                      try:
                          next(g_)
                      except StopIteration:
                          alive.remove(g_)

          drain_gen(gen_G(0))
          drain_gen(gen_F(0))
          for i in range(NT):
              grp = i // 4
              gF = None
              if i + 1 < NT:
                  def chain_next(i=i):
                      if (i + 1) % 4 == 0:
                          yield from gen_G((i + 1) // 4)
                      yield from gen_F(i + 1)
                  gF = chain_next()
              interleave([gen_B(i), gF])
              if gdn_prev is not None:
                  for _ in range(4):
                      next(gdn_prev, None)
              if i % 4 == 3:
                  drain_gen(gdn_prev)
                  gdn_prev = gdn_gen(grp)

        except _Stop:
            pass
        if gdn_prev is not None:
            for _ in gdn_prev:
                pass
        fw.dma("sp", S_o[:, :], Sf[:], reads=[b_Sf])
        fw.dma("sp", conv_o[:, :, :], raw[(NG - 1) % 2][:, :, 512:515], reads=[b_raw[(NG - 1) % 2]])

        if phaseB:
            xout = [nc.dram_tensor("xout%d" % k, [1024, CH], BF16).ap() for k in range(NCH)]
            b_xout = Buf()
            RG = [[0, 1, 2, 3], [4, 5, 6, 7]]
            ccs = st.enter_context(nc.semaphore("ccs"))
            for k in range(NCH if not SKIP_CC else 0):
                fw._wait("pool", b_xin[k].w)
                nc.gpsimd.collective_compute("AllGather", ALU.bypass, replica_groups=RG, ins=[xin[k][:, :].opt()],
                                             outs=[xout[k][:, :].opt()]).then_inc(ccs)
                nc.gpsimd.wait_ge(ccs, k + 1)
            pool(lambda e: e.memset(cvnew[0:1, 0:1], 0.0), w=[b_xout, b_cvnew])
            fw.barrier()
            stA.close()
            stS = st.enter_context(ExitStack())
            cur[0] = stS
            nb_ = [0]

            def T(shape, dt=F32):
                nb_[0] += 1
                return sb("sm%d" % nb_[0], shape, dt), Buf()

            xs_t, b_xs = T([4, 1024])
            gmix2, b_gmix2 = T([128, 8])
            ropes, b_ropes = T([4, 64])
            ptab, b_ptab = T([128, 256], I32)
            iota_c, b_iota = T([128, 1])
            idxs, b_idxs = T([128, 256], I32)
            cmpw64, b_cmpw64 = T([128, 2, 32, 64], BF16)
            pe64, b_pe64 = T([128, 2, 32], BF16)
            c2s2, b_c2s2 = T([128, 4, 128])
            on511, b_on511 = T([128, 4])
            oh4, b_oh4 = T([4, 80])
            bonus, b_bonus = T([1, 128])
            alogb, b_alogb = T([4, 8])
            gnrow, b_gnrow = T([1, 128])
            pjs, b_pjs = T([4, 3360])
            Xs, b_Xs = T([4, 2056])
            QKT, b_QKT = T([128, 14, 4], BF16)
            OGT, b_OGT = T([128, 4, 4], BF16)
            one11, b_one11 = T([1, 1])
            cb_s, b_cbs = T([64, 2])
            stSg = st.enter_context(ExitStack())
            cur[0] = stSg
            cwb, b_cwb = T([4, 4, 1536])
            stc, b_stc = T([4, 3, 1536])
            fw.dma("sp", xs_t[:], xs_d[:, :], writes=[b_xs])
            fw.dma("sp", gmix2[:], gmix_d[:, :], writes=[b_gmix2])
            fw.dma("sp", ropes[:], ropes_d[:, :], writes=[b_ropes])
            fw.dma("sp", ptab[:], ptab_d[:, :], writes=[b_ptab])
            fw.dma("sp", iota_c[:], iota_d[:, :], writes=[b_iota])
            fw.dma("pool", cmpw64[:].rearrange("p a b c -> p (a b c)"), cmpw64_d[:, :], writes=[b_cmpw64])
            fw.dma("pool", pe64[:].rearrange("p a b -> p (a b)"), pe64_d[:, :], writes=[b_pe64])
            fw.dma("sp", c2s2[:].rearrange("p a b -> p (a b)"), c2s_d[:, :], writes=[b_c2s2])
            fw.dma("sp", on511[:], ones511_d[:, :], writes=[b_on511])
            fw.dma("sp", oh4[:], oh4_d[:, :], writes=[b_oh4])
            fw.dma("sp", bonus[:], bonus_d[:, :], writes=[b_bonus])
            fw.dma("sp", alogb[:], alogb_d[:, :], writes=[b_alogb])
            fw.dma("sp", gnrow[:], gnrow_d[:, :], writes=[b_gnrow])
            fw.dma("sp", cwb[:].rearrange("p a b -> p (a b)"), convwb_d[:, :, :].rearrange("p a b -> p (a b)"), writes=[b_cwb])
            fw.dma("sp", stc[:].rearrange("p a b -> p (a b)"), gconv_d[:, :, :].rearrange("p a b -> p (a b)"), writes=[b_stc])
            dve(lambda e: e.tensor_scalar(out=idxs[:], in0=ptab[:], scalar1=128.0, scalar2=iota_c[:, 0:1], op0=ALU.mult, op1=ALU.add),
                r=[b_ptab, b_iota], w=[b_idxs])

            s_sq, b_ssq_ = T([4, 1024], BF16)
            s_ss, b_sss = T([4, 1])
            s_xn, b_sxn = T([4, 1024], BF16)
            xsT, b_xsT = T([128, 8, 4], BF16)
            wch0, b_wch0 = T([128, 8, 480], BF16)
            wch1, b_wch1 = T([128, 8, 480], BF16)
            wch = [wch0, wch1]; b_wch = [b_wch0, b_wch1]
            act(lambda e: e.activation(out=s_sq[:], in_=xs_t[:], func=AF.Square, accum_out=s_ss[:]), r=[b_xs], w=[b_ssq_, b_sss])
            act(lambda e: e.activation(out=s_ss[:], in_=s_ss[:], func=AF.Sqrt, scale=1.0 / 1024, bias=EPS), w=[b_sss])
            dve(lambda e: e.reciprocal(out=s_ss[:], in_=s_ss[:]), w=[b_sss])
            dve(lambda e: e.tensor_scalar(out=s_xn[:], in0=xs_t[:], scalar1=s_ss[:, 0:1], scalar2=None, op0=ALU.mult), r=[b_xs, b_sss], w=[b_sxn])
            for kt in range(8):
                tr(PT[:, kt * 4:(kt + 1) * 4], s_xn[0:4, kt * 128:(kt + 1) * 128], ident_b[0:4, 0:4], r=[b_sxn, b_identb], w=[b_PT])
            for kt in range(8):
                act(lambda e, kt=kt: e.activation(out=xsT[:, kt, :], in_=PT[:, kt * 4:(kt + 1) * 4], func=AF.Copy, scale=gmix2[:, kt:kt + 1]),
                    r=[b_PT, b_gmix2], w=[b_xsT])
            for ch in range(7):
                wb = ch % 2
                fw.dma("pool", wch[wb][:], win_full_d[:, ch * 480:(ch + 1) * 480].rearrange("(k p) c -> p k c", p=128), writes=[b_wch[wb]])
                for kt in range(8):
                    mm(PA[0:4, 0:480], lhsT=xsT[:, kt, :], rhs=wch[wb][:, kt, :], start=(kt == 0), stop=(kt == 7), r=[b_xsT, b_wch[wb]], w=[b_PA])
                act(lambda e, ch=ch: e.copy(out=pjs[:, ch * 480:(ch + 1) * 480], in_=PA[0:4, 0:480]), r=[b_PA], w=[b_pjs])

            chk2('s1')
            qkr, b_qkr = T([4, 14, 64])
            rqk, b_rqk = T([4, 14, 64])
            rts, b_rts = T([4, 4, 14, 32])
            kvv = pjs[:, 512:1280].rearrange("p (b k g d) -> p b k g d", b=3, k=2, g=2)
            pool(lambda e: e.tensor_copy(out=qkr[:, 0:8, :], in_=pjs[:, 0:512].rearrange("p (h d) -> p h d", h=8)), r=[b_pjs], w=[b_qkr])
            for br in range(3):
                pool(lambda e, br=br: e.tensor_copy(out=qkr[:, 8 + 2 * br:10 + 2 * br, :], in_=kvv[:, br, 0, :, :]), r=[b_pjs], w=[b_qkr])
            cs_b = bcast(ropes[:, 0:32], [4, 14, 32], 1)
            sn_b = bcast(ropes[:, 32:64], [4, 14, 32], 1)
            pool(lambda e: e.tensor_tensor(out=rts[:, 0], in0=qkr[:, :, 0:32], in1=cs_b, op=ALU.mult), r=[b_qkr, b_ropes], w=[b_rts])
            pool(lambda e: e.tensor_tensor(out=rts[:, 1], in0=qkr[:, :, 32:64], in1=sn_b, op=ALU.mult), r=[b_qkr, b_ropes], w=[b_rts])
            pool(lambda e: e.tensor_tensor(out=rts[:, 2], in0=qkr[:, :, 32:64], in1=cs_b, op=ALU.mult), r=[b_qkr, b_ropes], w=[b_rts])
            pool(lambda e: e.tensor_tensor(out=rts[:, 3], in0=qkr[:, :, 0:32], in1=sn_b, op=ALU.mult), r=[b_qkr, b_ropes], w=[b_rts])
            pool(lambda e: e.tensor_tensor(out=rqk[:, :, 0:32], in0=rts[:, 0], in1=rts[:, 1], op=ALU.subtract), r=[b_rts], w=[b_rqk])
            pool(lambda e: e.tensor_tensor(out=rqk[:, :, 32:64], in0=rts[:, 2], in1=rts[:, 3], op=ALU.add), r=[b_rts], w=[b_rqk])
            kvs_t, b_kvs = T([4, 4, 2, 64])
            wnew, b_wnew = T([4, 2, 2, 64])
            pool(lambda e: e.tensor_copy(out=kvs_t[:, 0], in_=rqk[:, 8:10, :]), r=[b_rqk], w=[b_kvs])
            pool(lambda e: e.tensor_copy(out=kvs_t[:, 1], in_=kvv[:, 0, 1, :, :]), r=[b_pjs], w=[b_kvs])
            pool(lambda e: e.tensor_copy(out=kvs_t[:, 2], in_=rqk[:, 10:12, :]), r=[b_rqk], w=[b_kvs])
            pool(lambda e: e.tensor_copy(out=kvs_t[:, 3], in_=kvv[:, 1, 1, :, :]), r=[b_pjs], w=[b_kvs])
            pool(lambda e: e.tensor_copy(out=wnew[:, 0], in_=rqk[:, 12:14, :]), r=[b_rqk], w=[b_wnew])
            pool(lambda e: e.tensor_copy(out=wnew[:, 1], in_=kvv[:, 2, 1, :, :]), r=[b_pjs], w=[b_wnew])
            fw.dma("sp", kvs_o[:, :], kvs_t[:].rearrange("p a b c -> p (a b c)"), reads=[b_kvs])
            fw.dma("sp", wins_o[:, 511, :], wnew[:].rearrange("p a b c -> p (a b c)"), reads=[b_wnew])
            for s_ in range(4):
                fw.dma("sp", wins_o[s_, 0:511, :], wincache_d[s_, 1:512, :])
            fw.dma("sp", convs_o[:, 0:2, :], gconv_d[:, 1:3, :])
            fw.dma("sp", convs_o[:, 2, :], pjs[:, 1304:2840], reads=[b_pjs])
            chk2('s2')
            qkb_s, b_qkbs = T([4, 14, 64], BF16)
            act(lambda e: e.copy(out=qkb_s[:], in_=rqk[:]), r=[b_rqk], w=[b_qkbs])
            for hd in range(14):
                tr(PT[0:64, hd * 4:(hd + 1) * 4], qkb_s[0:4, hd, :], ident_b[0:4, 0:4], r=[b_qkbs, b_identb], w=[b_PT])
            act(lambda e: e.copy(out=QKT[0:64].rearrange("p a b -> p (a b)"), in_=PT[0:64, 0:56]), r=[b_PT], w=[b_QKT])
            fw.dma("sp", QKT[64:128].rearrange("p a b -> p (a b)"), QKT[0:64].rearrange("p a b -> p (a b)"), reads=[b_QKT], writes=[b_QKT])

            chk2('s3')
            for kv in range(2):
                for l in range(32):
                    mm(PD[0:64, kv:kv + 1], lhsT=cmpw64[0:64, kv, l, :], rhs=pe64[0:64, kv, l:l + 1], start=(l == 0), stop=(l == 31),
                       r=[b_cmpw64, b_pe64], w=[b_PD])
            act(lambda e: e.copy(out=cb_s[:], in_=PD[0:64, 0:2]), r=[b_PD], w=[b_cbs])

            chk2('s3b')
            tmpc, b_tmpc = T([4, 1536])
            caccs, b_caccs = T([4, 1536])
            sqs, b_sqs = T([4, 1024])
            rn8, b_rn8 = T([4, 8])
            gsm, b_gsm = T([4, 8])
            dve(lambda e: e.tensor_tensor(out=caccs[:], in0=stc[:, 0, :], in1=cwb[:, 0, :], op=ALU.mult), r=[b_stc, b_cwb], w=[b_caccs])
            for jj in range(1, 4):
                src = stc[:, jj, :] if jj < 3 else pjs[:, 1304:2840]
                dve(lambda e, jj=jj, src=src: e.tensor_tensor(out=tmpc[:], in0=src, in1=cwb[:, jj, :], op=ALU.mult),
                    r=[b_stc, b_cwb, b_pjs], w=[b_tmpc])
                dve(lambda e: e.tensor_tensor(out=caccs[:], in0=caccs[:], in1=tmpc[:], op=ALU.add), r=[b_tmpc], w=[b_caccs])
            act(lambda e: e.activation(out=Xs[:, 0:1536], in_=caccs[:], func=AF.Silu), r=[b_caccs], w=[b_Xs])
            dve(lambda e: e.tensor_tensor(out=sqs[:], in0=Xs[:, 0:1024], in1=Xs[:, 0:1024], op=ALU.mult), r=[b_Xs], w=[b_sqs])
            dve(lambda e: e.tensor_reduce(out=rn8[:], in_=sqs[:].rearrange("p (h d) -> p h d", h=8), axis=AX.X, op=ALU.add), r=[b_sqs], w=[b_rn8])
            act(lambda e: e.activation(out=rn8[:], in_=rn8[:], func=AF.Sqrt, bias=EPS), w=[b_rn8])
            dve(lambda e: e.reciprocal(out=rn8[:], in_=rn8[:]), w=[b_rn8])
            dve(lambda e: e.tensor_scalar(out=rn8[:, 0:4], in0=rn8[:, 0:4], scalar1=128.0 ** -0.5, scalar2=None, op0=ALU.mult), w=[b_rn8])
            dve(lambda e: e.tensor_tensor(out=Xs[:, 0:1024].rearrange("p (h d) -> p h d", h=8), in0=Xs[:, 0:1024].rearrange("p (h d) -> p h d", h=8),
                                          in1=bcast(rn8[:], [4, 8, 128], 2), op=ALU.mult), r=[b_rn8], w=[b_Xs])
            act(lambda e: e.activation(out=Xs[:, 1536:2048], in_=pjs[:, 2840:3352], func=AF.Silu), r=[b_pjs], w=[b_Xs])
            act(lambda e: e.activation(out=Xs[:, 2048:2052], in_=pjs[:, 3356:3360], func=AF.Sigmoid), r=[b_pjs], w=[b_Xs])
            dve(lambda e: e.tensor_tensor(out=gsm[:, 0:4], in0=pjs[:, 3352:3356], in1=alogb[:, 4:8], op=ALU.add), r=[b_pjs, b_alogb], w=[b_gsm])
            act(lambda e: e.activation(out=gsm[:, 0:4], in_=gsm[:, 0:4], func=AF.Exp), w=[b_gsm])
            act(lambda e: e.activation(out=gsm[:, 0:4], in_=gsm[:, 0:4], func=AF.Ln, bias=1.0), w=[b_gsm])
            act(lambda e: e.activation(out=gsm[:, 4:8], in_=alogb[:, 0:4], func=AF.Exp), r=[b_alogb], w=[b_gsm])
            dve(lambda e: e.tensor_tensor(out=gsm[:, 0:4], in0=gsm[:, 0:4], in1=gsm[:, 4:8], op=ALU.mult), w=[b_gsm])
            act(lambda e: e.activation(out=Xs[:, 2052:2056], in_=gsm[:, 0:4], func=AF.Exp, scale=-1.0), r=[b_gsm], w=[b_Xs])

            chk2('s4')
            Rrow, b_Rrow = T([1, 2056])
            cols_s, b_cols = T([128, 8])
            S_t = [T([128, 128]) for _ in range(2)]
            r1, b_r1 = T([1, 257])
            vn, b_vn = T([1, 128])
            og, b_og = T([1, 128])
            ogs, b_ogs = T([1, 4])
            ogq, b_ogq = T([1, 128])
            egb, b_egb = T([128, 1])
            Snew = [T([128, 128]) for _ in range(2)]
            pool(lambda e: e.memset(one11[:], 1.0), w=[b_one11])
            for s_ in range(4):
                for chn, (c0, c1) in enumerate([(0, 512), (512, 1024), (1024, 1536), (1536, 2048), (2048, 2056)]):
                    mm(PD[0:1, 0:c1 - c0], lhsT=ident_f[0:4, s_:s_ + 1], rhs=Xs[0:4, c0:c1], r=[b_identf, b_Xs], w=[b_PD])
                    act(lambda e, c0=c0, c1=c1: e.copy(out=Rrow[0:1, c0:c1], in_=PD[0:1, 0:c1 - c0]), r=[b_PD], w=[b_Rrow])
                for hq in range(8):
                    mm(PD[:, hq:hq + 1], lhsT=Rrow[0:1, hq * 128:(hq + 1) * 128], rhs=one11[:], r=[b_Rrow, b_one11], w=[b_PD])
                act(lambda e: e.copy(out=cols_s[:], in_=PD[:, 0:8]), r=[b_PD], w=[b_cols])
                for h in range(4):
                    sbi = (s_ * 4 + h) % 2
                    St, bSt = S_t[sbi]
                    Sn, bSn = Snew[sbi]
                    fw.dma("sp", St[:], gS_d[s_ * 4 + h, :, :], writes=[bSt])
                    mm(PD[0:1, 0:128], lhsT=cols_s[:, 4 + h:5 + h], rhs=St[:], r=[b_cols, bSt], w=[b_PD])
                    mm(PD[0:1, 128:256], lhsT=cols_s[:, h:h + 1], rhs=St[:], r=[b_cols, bSt], w=[b_PD])
                    mm(PD[0:1, 256:257], lhsT=cols_s[:, h:h + 1], rhs=cols_s[:, 4 + h:5 + h], r=[b_cols], w=[b_PD])
                    act(lambda e: e.copy(out=r1[:], in_=PD[0:1, 0:257]), r=[b_PD], w=[b_r1])
                    egs = Rrow[0:1, 2052 + h:2053 + h]
                    bts = Rrow[0:1, 2048 + h:2049 + h]
                    dve(lambda e, egs=egs: e.tensor_scalar(out=vn[:], in0=r1[0:1, 0:128], scalar1=egs, scalar2=None, op0=ALU.mult),
                        r=[b_r1, b_Rrow], w=[b_vn])
                    dve(lambda e, h=h: e.tensor_tensor(out=vn[:], in0=Rrow[0:1, 1024 + h * 128:1024 + (h + 1) * 128], in1=vn[:], op=ALU.subtract),
                        r=[b_Rrow], w=[b_vn])
                    dve(lambda e, bts=bts: e.tensor_scalar(out=vn[:], in0=vn[:], scalar1=bts, scalar2=None, op0=ALU.mult), r=[b_Rrow], w=[b_vn])
                    dve(lambda e, egs=egs: e.tensor_scalar(out=og[:], in0=r1[0:1, 128:256], scalar1=egs, scalar2=None, op0=ALU.mult),
                        r=[b_r1, b_Rrow], w=[b_og])
                    dve(lambda e: e.scalar_tensor_tensor(out=og[:], in0=vn[:], scalar=r1[0:1, 256:257], in1=og[:], op0=ALU.mult, op1=ALU.add),
                        r=[b_vn, b_r1], w=[b_og])
                    act(lambda e: e.activation(out=ogq[:], in_=og[:], func=AF.Square, accum_out=ogs[0:1, 0:1]), r=[b_og], w=[b_ogq, b_ogs])
                    act(lambda e: e.activation(out=ogs[0:1, 0:1], in_=ogs[0:1, 0:1], func=AF.Sqrt, scale=1.0 / 128, bias=EPS), w=[b_ogs])
                    dve(lambda e: e.reciprocal(out=ogs[0:1, 0:1], in_=ogs[0:1, 0:1]), w=[b_ogs])
                    dve(lambda e: e.scalar_tensor_tensor(out=og[:], in0=og[:], scalar=ogs[0:1, 0:1], in1=gnrow[:], op0=ALU.mult, op1=ALU.mult),
                        r=[b_ogs, b_gnrow], w=[b_og])
                    dve(lambda e, h=h: e.tensor_tensor(out=og[:], in0=og[:], in1=Rrow[0:1, 1536 + h * 128:1536 + (h + 1) * 128], op=ALU.mult),
                        r=[b_Rrow], w=[b_og])
                    mm(PD[:, 300:301], lhsT=og[:], rhs=one11[:], r=[b_og, b_one11], w=[b_PD])
                    act(lambda e, h=h, s_=s_: e.copy(out=OGT[:, h, s_:s_ + 1], in_=PD[:, 300:301]), r=[b_PD], w=[b_OGT])
                    mm(PB[:, 0:128], lhsT=Rrow[0:1, 512 + h * 128:512 + (h + 1) * 128], rhs=vn[:], r=[b_Rrow, b_vn], w=[b_PB])
                    mm(PD[:, 310:311], lhsT=ones_f2[0:1, :], rhs=egs, r=[b_onesf2, b_Rrow], w=[b_PD])
                    act(lambda e: e.copy(out=egb[:], in_=PD[:, 310:311]), r=[b_PD], w=[b_egb])
                    dve(lambda e, St=St, Sn=Sn: e.scalar_tensor_tensor(out=Sn[:], in0=St[:], scalar=egb[:, 0:1], in1=PB[:, 0:128],
                                                                    op0=ALU.mult, op1=ALU.add), r=[bSt, b_egb, b_PB], w=[bSn])
                    fw.dma("sp", Ss_o[s_ * 4 + h, :, :], Sn[:], reads=[bSn])

            chk2('s5')
            fw.barrier()
            stSg.close()
            stSn = st.enter_context(ExitStack())
            cur[0] = stSn
            woutn, b_woutn = T([64, 8, 1024], BF16)
            woutg, b_woutg = T([128, 4, 1024], BF16)
            eind64, b_eind64 = T([128, 8192], BF16)
            fw.dma("pool", woutn[:].rearrange("p a b -> p (a b)"), woutn_d[:, :], writes=[b_woutn])
            fw.dma("pool", woutg[:].rearrange("p a b -> p (a b)"), woutg_d[:, :], writes=[b_woutg])
            fw.dma("pool", eind64[0:64, :], eind_s_d[:, :], writes=[b_eind64])
            fw.dma("pool", eind64[64:128, :], eind_s_d[:, :], writes=[b_eind64])
            KTs, b_KTs = T([128, 3, 8192], BF16)
            Vs, b_Vs = T([128, 64, 2, 65], BF16)
            pg = [T([128, 512]) for _ in range(3)]
            pgb = [T([128, 384], BF16) for _ in range(2)]
            ckTs, b_ckTs = T([64, 512], BF16)
            cvTs, b_cvTs = T([64, 512], BF16)
            cvxs, b_cvxs = T([128, 4, 193], BF16)
            Pc, b_Pc = T([128, 16], BF16)
            accs, b_accs = T([4, 193])
            rcs, b_rcs = T([4, 4])
            impn, b_impn = T([4, 128])
            scs, b_scs = T([1, 136])
            sc2s, b_sc2s = T([1, 136])
            mx8s, b_mx8s = T([1, 16])
            thrs, b_thrs = T([1, 1])
            mbrow, b_mbrow = T([1, 128])
            mbrow2, b_mbrow2 = T([1, 2, 128])
            mbc, b_mbc = T([128, 2])
            MBq, b_MBq = T([128, 2, 4], BF16)
            Psel, b_Psel = T([128, 256], BF16)
            pnew, b_pnew = T([4, 2])
            vrow, b_vrow = T([4, 128])
            Abr, b_Abr = T([4, 3, 8, 64])
            asel, b_asel = T([4, 2, 65])
            wc, b_wc = T([128, 4, 256])
            wcb, b_wcb = T([128, 4, 128], BF16)
            Vws, b_Vws = T([128, 4, 2, 65], BF16)
            KwTs, b_KwTs = T([128, 4, 128], BF16)
            Pw, b_Pw = T([128, 16], BF16)
            pool(lambda e: e.memset(Vs[:, :, :, 64:65], 1.0), w=[b_Vs])
            pool(lambda e: e.memset(Vws[:, :, :, 64:65], 1.0), w=[b_Vws])
            pool(lambda e: e.memset(ckTs[:], 0.0), w=[b_ckTs])
            pool(lambda e: e.memset(cvTs[:], 0.0), w=[b_cvTs])
            pool(lambda e: e.memset(cvxs[:], 0.0), w=[b_cvxs])
            pool(lambda e: e.tensor_copy(out=cvxs[:, :, 64:192], in_=c2s2[:]), r=[b_c2s2], w=[b_cvxs])
            pool(lambda e: e.tensor_copy(out=cvxs[:, :, 192], in_=on511[:]), r=[b_on511], w=[b_cvxs])
            pool(lambda e: e.memset(scs[:], 1e4), w=[b_scs])
            for s_ in range(4):
                for p_ in range(64):
                    pgt, bpg = pg[p_ % 3]
                    pgbt, bpgb = pgb[p_ % 2]
                    col = s_ * 64 + p_
                    fw.dma("pool", None, None, reads=[b_idxs], writes=[bpg],
                           fn=lambda e, pgt=pgt, col=col: e.indirect_dma_start(
                               out=pgt[:], out_offset=None, in_=cache_d[:, :],
                               in_offset=bass.IndirectOffsetOnAxis(ap=idxs[:, col:col + 1], axis=0)))
                    dve(lambda e, pgt=pgt, pgbt=pgbt: e.tensor_copy(out=pgbt[:], in_=pgt[:, 0:384]), r=[bpg], w=[bpgb])
                    act(lambda e, pgt=pgt, p_=p_: e.copy(out=Vs[:, p_, :, 0:64], in_=pgt[:, 384:512].rearrange("p (g d) -> p g d", g=2)),
                        r=[bpg], w=[b_Vs])
                    for kg in range(3):
                        tr(PT[:, kg * 128:(kg + 1) * 128], pgbt[:, kg * 128:(kg + 1) * 128], ident_b[:], r=[bpgb, b_identb], w=[b_PT])
                    act(lambda e, p_=p_: e.copy(out=KTs[:, :, p_ * 128:(p_ + 1) * 128], in_=PT[:, 0:384].rearrange("p (a t) -> p a t", a=3)),
                        r=[b_PT], w=[b_KTs])
                chk2('s6')
                fw.dma("sp", wc[:], wincache_d[s_, :, :].rearrange("(t p) c -> p t c", p=128), writes=[b_wc])
                pool(lambda e: e.tensor_copy(out=wcb[:], in_=wc[:, :, 0:128]), r=[b_wc], w=[b_wcb])
                pool(lambda e: e.tensor_copy(out=Vws[:, :, :, 0:64], in_=wc[:, :, 128:256].rearrange("p t (g d) -> p t g d", g=2)),
                     r=[b_wc], w=[b_Vws])
                for t_ in range(4):
                    tr(PT[:, t_ * 128:(t_ + 1) * 128], wcb[:, t_, :], ident_b[:], r=[b_wcb, b_identb], w=[b_PT])
                act(lambda e: e.copy(out=KwTs[:].rearrange("p a b -> p (a b)"), in_=PT[:, 0:512]), r=[b_PT], w=[b_KwTs])
                chk2('s7')
                for g_ in range(2):
                    sg = s_ * 2 + g_
                    Qg = QKT[0:64, 4 * g_:4 * g_ + 4, s_]
                    g0_, g1_ = g_ * 64, (g_ + 1) * 64
                    Qgg = QKT[g0_:g1_, 4 * g_:4 * g_ + 4, s_]
                    for kv in range(2):
                        for l in range(32):
                            mm(PA[0:64, 0:511], lhsT=cmpw64[g0_:g1_, kv, l, :], rhs=KTs[g0_:g1_, kv, l:l + 16 * 510 + 1:16],
                               start=(l == 0), stop=(l == 31), r=[b_cmpw64, b_KTs], w=[b_PA])
                        dst = ckTs if kv == 0 else cvTs
                        bd = b_ckTs if kv == 0 else b_cvTs
                        act(lambda e, dst=dst, kv=kv: e.activation(out=dst[:, 0:511], in_=PA[0:64, 0:511], func=AF.Identity, bias=cb_s[:, kv:kv + 1]),
                            r=[b_PA, b_cbs], w=[bd])
                    for jt in range(4):
                        tr(PT[:, jt * 64:(jt + 1) * 64], cvTs[:, jt * 128:(jt + 1) * 128], ident_b[0:64, 0:64], r=[b_cvTs, b_identb], w=[b_PT])
                    act(lambda e: e.copy(out=cvxs[:, :, 0:64], in_=PT[:, 0:256].rearrange("p (a d) -> p a d", a=4)), r=[b_PT], w=[b_cvxs])
                    chk2('s8')
                    for jt in range(4):
                        mm(PD[:, jt * 4:(jt + 1) * 4], lhsT=ckTs[:, jt * 128:(jt + 1) * 128], rhs=Qg, r=[b_ckTs, b_QKT], w=[b_PD])
                    act(lambda e: e.activation(out=Pc[:], in_=PD[:, 0:16], func=AF.Exp, scale=0.125), r=[b_PD], w=[b_Pc])
                    for jt in range(4):
                        mm(PB[0:4, 0:193], lhsT=Pc[:, jt * 4:(jt + 1) * 4], rhs=cvxs[:, jt, :], start=(jt == 0), stop=(jt == 3),
                           r=[b_Pc, b_cvxs], w=[b_PB])
                    act(lambda e: e.copy(out=accs[:], in_=PB[0:4, 0:193]), r=[b_PB], w=[b_accs])
                    dve(lambda e: e.tensor_scalar(out=rcs[:, 0:1], in0=accs[:, 192:193], scalar1=1e-30, scalar2=None, op0=ALU.max), r=[b_accs], w=[b_rcs])
                    dve(lambda e: e.reciprocal(out=rcs[:, 0:1], in_=rcs[:, 0:1]), w=[b_rcs])
                    dve(lambda e, sg=sg: e.tensor_scalar(out=Abr[:, 0, sg, :], in0=accs[:, 0:64], scalar1=rcs[:, 0:1], scalar2=None, op0=ALU.mult),
                        r=[b_accs, b_rcs], w=[b_Abr])
                    dve(lambda e: e.tensor_scalar(out=impn[:], in0=accs[:, 64:192], scalar1=rcs[:, 0:1], scalar2=None, op0=ALU.mult),
                        r=[b_accs, b_rcs], w=[b_impn])
                    mm(PD[0:1, 64:192], lhsT=ones_f2[0:4, 0:1], rhs=impn[:], r=[b_onesf2, b_impn], w=[b_PD])
                    chk2('s8b')
                    dve(lambda e: e.tensor_tensor(out=scs[0:1, 0:128], in0=PD[0:1, 64:192], in1=bonus[:], op=ALU.add), r=[b_PD, b_bonus], w=[b_scs])
                    dve(lambda e: e.max(out=mx8s[:, 0:8], in_=scs[0:1, 0:129]), r=[b_scs], w=[b_mx8s])
                    dve(lambda e: e.match_replace(out=sc2s[0:1, 0:129], in_to_replace=mx8s[:, 0:8], in_values=scs[0:1, 0:129], imm_value=-3e38),
                        r=[b_scs, b_mx8s], w=[b_sc2s])
                    dve(lambda e: e.max(out=mx8s[:, 8:16], in_=sc2s[0:1, 0:129]), r=[b_sc2s], w=[b_mx8s])
                    dve(lambda e: e.tensor_reduce(out=thrs[:], in_=mx8s[:, 8:16], axis=AX.X, op=ALU.min), r=[b_mx8s], w=[b_thrs])
                    dve(lambda e: e.tensor_scalar(out=mbrow[:], in0=scs[0:1, 0:128], scalar1=thrs[0:1, 0:1], scalar2=None, op0=ALU.is_ge),
                        r=[b_scs, b_thrs], w=[b_mbrow])
                    dve(lambda e: e.tensor_scalar(out=mbrow[:], in0=mbrow[:], scalar1=-NEGB, scalar2=NEGB, op0=ALU.mult, op1=ALU.add), w=[b_mbrow])
                    for a_ in range(2):
                        for dp_ in range(2):
                            dve(lambda e, a_=a_, dp_=dp_: e.tensor_copy(out=mbrow2[0:1, a_, dp_ * 64:(dp_ + 1) * 64], in_=mbrow[0:1, a_ * 64:(a_ + 1) * 64]),
                                r=[b_mbrow], w=[b_mbrow2])
                    for a_ in range(2):
                        mm(PD[:, 200 + a_:201 + a_], lhsT=mbrow2[0:1, a_, :], rhs=one11[:], r=[b_mbrow2, b_one11], w=[b_PD])
                    act(lambda e: e.copy(out=mbc[:], in_=PD[:, 200:202]), r=[b_PD], w=[b_mbc])
                    for a_ in range(2):
                        dve(lambda e, a_=a_: e.tensor_scalar(out=MBq[:, a_, :], in0=ones_f2[:, 0:4], scalar1=mbc[:, a_:a_ + 1], scalar2=None, op0=ALU.mult),
                            r=[b_onesf2, b_mbc], w=[b_MBq])
                    chk2('s9')
                    for t_ in range(64):
                        a_ = 0 if t_ < 32 else 1
                        mm(PA[:, 512 + t_ * 4:512 + (t_ + 1) * 4], lhsT=KTs[g0_:g1_, 2, t_ * 128:(t_ + 1) * 128], rhs=Qgg, start=True, stop=False,
                           r=[b_KTs, b_QKT], w=[b_PA])
                        mm(PA[:, 512 + t_ * 4:512 + (t_ + 1) * 4], lhsT=eind64[g0_:g1_, t_ * 128:(t_ + 1) * 128], rhs=MBq[g0_:g1_, a_, :], start=False, stop=True,
                           r=[b_eind64, b_MBq], w=[b_PA])
                    act(lambda e: e.activation(out=Psel[:], in_=PA[:, 512:768], func=AF.Exp, scale=0.125), r=[b_PA], w=[b_Psel])
                    for t_ in range(64):
                        mm(PB[0:4, 256:321], lhsT=Psel[:, t_ * 4:(t_ + 1) * 4], rhs=Vs[:, t_, g_, :], start=(t_ == 0), stop=(t_ == 63),
                           r=[b_Psel, b_Vs], w=[b_PB])
                    chk2('s10')
                    mm(PD[0:4, 210:211], lhsT=Qg, rhs=QKT[0:64, 10 + g_, s_:s_ + 1], r=[b_QKT], w=[b_PD])
                    mm(PD[0:4, 211:212], lhsT=Qg, rhs=QKT[0:64, 12 + g_, s_:s_ + 1], r=[b_QKT], w=[b_PD])
                    act(lambda e: e.activation(out=pnew[:], in_=PD[0:4, 210:212], func=AF.Exp, scale=0.125), r=[b_PD], w=[b_pnew])
                    mm(PD[0:4, 220:284], lhsT=oh4[0:4, s_ * 4:(s_ + 1) * 4], rhs=kvv[:, 1, 1, g_, :], r=[b_oh4, b_pjs], w=[b_PD])
                    mm(PD[0:4, 284:348], lhsT=oh4[0:4, s_ * 4:(s_ + 1) * 4], rhs=kvv[:, 2, 1, g_, :], r=[b_oh4, b_pjs], w=[b_PD])
                    act(lambda e: e.copy(out=vrow[:], in_=PD[0:4, 220:348]), r=[b_PD], w=[b_vrow])
                    dve(lambda e: e.scalar_tensor_tensor(out=asel[:, 0, 0:64], in0=vrow[:, 0:64], scalar=pnew[:, 0:1], in1=PB[0:4, 256:320],
                                                         op0=ALU.mult, op1=ALU.add), r=[b_vrow, b_pnew, b_PB], w=[b_asel])
                    dve(lambda e: e.tensor_tensor(out=asel[:, 0, 64:65], in0=PB[0:4, 320:321], in1=pnew[:, 0:1], op=ALU.add),
                        r=[b_PB, b_pnew], w=[b_asel])
                    chk2('s10b')
                    for t_ in range(4):
                        mm(PD[:, 352 + t_ * 4:356 + t_ * 4], lhsT=KwTs[g0_:g1_, t_, :], rhs=Qgg, r=[b_KwTs, b_QKT], w=[b_PD])
                    act(lambda e: e.activation(out=Pw[:], in_=PD[:, 352:368], func=AF.Exp, scale=0.125), r=[b_PD], w=[b_Pw])
                    for t_ in range(4):
                        mm(PB[0:4, 384:449], lhsT=Pw[:, t_ * 4:(t_ + 1) * 4], rhs=Vws[:, t_, g_, :], start=(t_ == 0), stop=(t_ == 3),
                           r=[b_Pw, b_Vws], w=[b_PB])
                    dve(lambda e: e.scalar_tensor_tensor(out=asel[:, 1, 0:64], in0=vrow[:, 64:128], scalar=pnew[:, 1:2], in1=PB[0:4, 384:448],
                                                         op0=ALU.mult, op1=ALU.add), r=[b_vrow, b_pnew, b_PB], w=[b_asel])
                    dve(lambda e: e.tensor_tensor(out=asel[:, 1, 64:65], in0=PB[0:4, 448:449], in1=pnew[:, 1:2], op=ALU.add),
                        r=[b_PB, b_pnew], w=[b_asel])
                    dve(lambda e: e.reciprocal(out=rcs[:, 1:3], in_=asel[:, :, 64]), r=[b_asel], w=[b_rcs])
                    for br in range(2):
                        dve(lambda e, br=br, sg=sg: e.tensor_scalar(out=Abr[:, 1 + br, sg, :], in0=asel[:, br, 0:64], scalar1=rcs[:, 1 + br:2 + br],
                                                                    scalar2=None, op0=ALU.mult), r=[b_asel, b_rcs], w=[b_Abr])

            chk2('s11')
            gts, b_gts = T([4, 8, 3])
            osum, b_osum = T([4, 8, 64])
            otmp, b_otmp = T([4, 8, 64])
            onb, b_onb = T([4, 8, 64], BF16)
            OT, b_OT = T([64, 8, 4], BF16)
            act(lambda e: e.activation(out=gts[:].rearrange("p a b -> p (a b)"), in_=pjs[:, 1280:1304], func=AF.Sigmoid), r=[b_pjs], w=[b_gts])
            for br in range(3):
                for g_ in range(2):
                    for r_ in range(4):
                        h_ = 4 * g_ + r_
                        for s_ in range(4):
                            mm(PC[0:4, h_ * 64:(h_ + 1) * 64], lhsT=oh4[0:4, 16 + (r_ * 4 + s_) * 4:16 + (r_ * 4 + s_ + 1) * 4],
                               rhs=Abr[:, br, s_ * 2 + g_, :], start=(s_ == 0), stop=(s_ == 3), r=[b_oh4, b_Abr], w=[b_PC])
                gb = bcast(gts[:, :, br], [4, 8, 64], 2)
                if br == 0:
                    dve(lambda e, gb=gb: e.tensor_tensor(out=osum[:], in0=PC[0:4, 0:512].rearrange("p (h d) -> p h d", h=8), in1=gb, op=ALU.mult),
                        r=[b_PC, b_gts], w=[b_osum])
                else:
                    dve(lambda e, gb=gb: e.tensor_tensor(out=otmp[:], in0=PC[0:4, 0:512].rearrange("p (h d) -> p h d", h=8), in1=gb, op=ALU.mult),
                        r=[b_PC, b_gts], w=[b_otmp])
                    dve(lambda e: e.tensor_tensor(out=osum[:], in0=osum[:], in1=otmp[:], op=ALU.add), r=[b_otmp], w=[b_osum])
            act(lambda e: e.copy(out=onb[:], in_=osum[:]), r=[b_osum], w=[b_onb])
            for h_ in range(8):
                tr(PT[0:64, h_ * 4:(h_ + 1) * 4], onb[0:4, h_, :], ident_b[0:4, 0:4], r=[b_onb, b_identb], w=[b_PT])
            act(lambda e: e.copy(out=OT[:].rearrange("p a b -> p (a b)"), in_=PT[0:64, 0:32]), r=[b_PT], w=[b_OT])
            chk2('s12')
            for half in range(2):
                for h_ in range(8):
                    mm(PA[0:4, half * 512:(half + 1) * 512], lhsT=OT[:, h_, :], rhs=woutn[:, h_, half * 512:(half + 1) * 512],
                       start=(h_ == 0), stop=False, r=[b_OT, b_woutn], w=[b_PA])
                for h_ in range(4):
                    mm(PA[0:4, half * 512:(half + 1) * 512], lhsT=OGT[:, h_, :], rhs=woutg[:, h_, half * 512:(half + 1) * 512],
                       start=False, stop=(h_ == 3), r=[b_OGT, b_woutg], w=[b_PA])
            dve(lambda e: e.tensor_tensor(out=hs_res[:], in0=PA[0:4, :], in1=xs_t[:], op=ALU.add), r=[b_PA, b_xs], w=[b_hsres])
            fw.barrier()
            stSn.close()
            stS.close()
            stB = st.enter_context(ExitStack())
            cur[0] = stB
            wout = sb("wout", [128, 8, 1024], BF16); b_wout = Buf()
            wdn = sb("wdn", [128, 22, 1024], BF16); b_wdn = Buf()
            nfb = sb("nfb", [128, 1024]); b_nfb = Buf()
            gffn = sb("gffn", [128, 8]); b_gffn = Buf()
            sel4 = sb("sel4_sb", [128, 4]); b_sel4 = Buf()
            cand = [sb("cand%d" % i, [128, 8, 512], BF16) for i in range(2)]; b_cand = [Buf() for _ in range(2)]
            mixT = sb("mixT", [128, 8, 512], BF16); b_mixT = Buf()
            xt2 = [sb("xt2_%d" % i, [128, 1024]) for i in range(2)]; b_xt2 = [Buf() for _ in range(2)]
            hres = sb("hres", [128, 4, 1024]); b_hres = [Buf() for _ in range(4)]
            hsq = sb("hsq", [128, 1024], BF16); b_hsq = Buf()
            hss = sb("hss", [128, 1]); b_hss = Buf()
            hs = sb("hs", [128, 1024], BF16); b_hs = Buf()
            hnT = sb("hnT", [128, 8, 512], BF16); b_hnT = Buf()
            wg = [sb("wg%d" % i, [128, 8, 256], BF16) for i in range(3)]; b_wg = [Buf() for _ in range(3)]
            sg = [sb("sg%d" % i, [128, 512]) for i in range(2)]; b_sg = [Buf() for _ in range(2)]
            actT = sb("actT", [128, 22, 512], BF16); b_actT = Buf()
            yb = [sb("yb%d" % i, [128, 1024]) for i in range(2)]; b_yb = [Buf() for _ in range(2)]
            ysq = sb("ysq", [128, 1024], BF16); b_ysq = Buf()
            yss = sb("yss", [128, 1]); b_yss = Buf()
            hsn_ss = sb("hsn_ss", [4, 1]); b_hsnss = Buf()
            hsn_sq = sb("hsn_sq", [4, 1024], BF16); b_hsnsq = Buf()
            hsn = sb("hsn", [4, 1024], BF16); b_hsn = Buf()
            hnTs = sb("hnTs", [128, 8, 4], BF16); b_hnTs = Buf()
            sgs = sb("sgs", [128, 4]); b_sgs = Buf()
            actTs = sb("actTs", [128, 22, 4], BF16); b_actTs = Buf()
            ysb = sb("ysb", [4, 1024]); b_ysb = Buf()
            fw.dma("sp", nfb[:], nfin_d[:, :], writes=[b_nfb])
            fw.dma("sp", gffn[:], gffn_d[:, :], writes=[b_gffn])
            fw.dma("sp", sel4[:], sel4_d[:, :], writes=[b_sel4])
            fw.dma("pool", wout[:], wout_d[:, :].rearrange("(k p) c -> p k c", p=128), writes=[b_wout])
            fw.dma("sp", wdn[:], wdn_s[:, :, :].rearrange("f p c -> p f c"), reads=[b_wdns], writes=[b_wdn])
            act(lambda e: e.activation(out=hsn_sq[:], in_=hs_res[:], func=AF.Square, accum_out=hsn_ss[:]), r=[b_hsres], w=[b_hsnsq, b_hsnss])
            act(lambda e: e.activation(out=hsn_ss[:], in_=hsn_ss[:], func=AF.Sqrt, scale=1.0 / 1024, bias=EPS), w=[b_hsnss])
            dve(lambda e: e.reciprocal(out=hsn_ss[:], in_=hsn_ss[:]), w=[b_hsnss])
            dve(lambda e: e.tensor_scalar(out=hsn[:], in0=hs_res[:], scalar1=hsn_ss[:, 0:1], scalar2=None, op0=ALU.mult), r=[b_hsres, b_hsnss], w=[b_hsn])
            for kt in range(8):
                tr(PT[:, kt * 4:(kt + 1) * 4], hsn[0:4, kt * 128:(kt + 1) * 128], ident_b[0:4, 0:4], r=[b_hsn, b_identb], w=[b_PT])
            for kt in range(8):
                act(lambda e, kt=kt: e.activation(out=hnTs[:, kt, :], in_=PT[:, kt * 4:(kt + 1) * 4], func=AF.Copy, scale=gffn[:, kt:kt + 1]),
                    r=[b_PT, b_gffn], w=[b_hnTs])
            wgi = 0
            for bi in range(NB):
                for j4 in range(4):
                    cb = j4 % 2
                    c0 = j4 * TB + bi * 512
                    fw.dma("sp", cand[cb][:], xout[c0 // CH][:, c0 % CH:c0 % CH + 512].rearrange("(k p) t -> p k t", p=128),
                           reads=[b_xout], writes=[b_cand[cb]])
                    if j4 == 0:
                        dve(lambda e, cb=cb: e.tensor_scalar(out=mixT[:], in0=cand[cb][:], scalar1=sel4[:, 0:1], scalar2=None, op0=ALU.mult),
                            r=[b_cand[cb], b_sel4], w=[b_mixT])
                    else:
                        dve(lambda e, cb=cb, j4=j4: e.scalar_tensor_tensor(out=mixT[:], in0=cand[cb][:], scalar=sel4[:, j4:j4 + 1], in1=mixT[:],
                                                                           op0=ALU.mult, op1=ALU.add), r=[b_cand[cb], b_sel4], w=[b_mixT])
                for tt in range(4):
                    r0 = bi * 512 + tt * 128
                    xs_ = tt % 2
                    fw.dma("sp", xt2[xs_][:], xown_d[r0:r0 + 128, :], writes=[b_xt2[xs_]])
                    for half in range(2):
                        for kt in range(8):
                            mm(PA[:, half * 512:(half + 1) * 512], lhsT=mixT[:, kt, tt * 128:(tt + 1) * 128],
                               rhs=wout[:, kt, half * 512:(half + 1) * 512], start=(kt == 0), stop=(kt == 7),
                               r=[b_mixT, b_wout], w=[b_PA])
                    dve(lambda e, tt=tt, xs_=xs_: e.tensor_tensor(out=hres[:, tt, :], in0=PA[:, :], in1=xt2[xs_][:], op=ALU.add),
                        r=[b_PA, b_xt2[xs_]], w=[b_hres[tt]])
                    act(lambda e, tt=tt: e.activation(out=hsq[:], in_=hres[:, tt, :], func=AF.Square, accum_out=hss[:]),
                        r=[b_hres[tt]], w=[b_hsq, b_hss])
                    act(lambda e: e.activation(out=hss[:], in_=hss[:], func=AF.Sqrt, scale=1.0 / 1024, bias=EPS), w=[b_hss])
                    dve(lambda e: e.reciprocal(out=hss[:], in_=hss[:]), w=[b_hss])
                    dve(lambda e, tt=tt: e.tensor_scalar(out=hs[:], in0=hres[:, tt, :], scalar1=hss[:, 0:1], scalar2=None, op0=ALU.mult),
                        r=[b_hres[tt], b_hss], w=[b_hs])
                    for kt in range(8):
                        tr(PT[:, kt * 128:(kt + 1) * 128], hs[:, kt * 128:(kt + 1) * 128], ident_b[:], r=[b_hs, b_identb], w=[b_PT])
                    for kt in range(8):
                        act(lambda e, kt=kt, tt=tt: e.activation(out=hnT[:, kt, tt * 128:(tt + 1) * 128], in_=PT[:, kt * 128:(kt + 1) * 128],
                                                                 func=AF.Copy, scale=gffn[:, kt:kt + 1]), r=[b_PT, b_gffn], w=[b_hnT])
                for f in range(22):
                    wb = wgi % 3
                    wgi += 1
                    fw.dma("sp", wg[wb][:].rearrange("p k c -> p (k c)"), wgu_s[f], reads=[b_wgus[f]], writes=[b_wg[wb]])
                    for kt in range(8):
                        mm(PA[:, 0:512], lhsT=wg[wb][:, kt, 0:128], rhs=hnT[:, kt, :], start=(kt == 0), stop=(kt == 7),
                           r=[b_wg[wb], b_hnT], w=[b_PA])
                    for kt in range(8):
                        mm(PB[:, 0:512], lhsT=wg[wb][:, kt, 128:256], rhs=hnT[:, kt, :], start=(kt == 0), stop=(kt == 7),
                           r=[b_wg[wb], b_hnT], w=[b_PB])
                    sb_ = f % 2
                    act(lambda e, sb_=sb_: e.activation(out=sg[sb_][:], in_=PA[:, 0:512], func=AF.Silu), r=[b_PA], w=[b_sg[sb_]])
                    dve(lambda e, sb_=sb_, f=f: e.tensor_tensor(out=actT[:, f, :], in0=PB[:, 0:512], in1=sg[sb_][:], op=ALU.mult),
                        r=[b_PB, b_sg[sb_]], w=[b_actT])
                    if bi == 0:
                        for kt in range(8):
                            mm(PD[:, 0:4], lhsT=wg[wb][:, kt, 0:128], rhs=hnTs[:, kt, :], start=(kt == 0), stop=(kt == 7),
                               r=[b_wg[wb], b_hnTs], w=[b_PD])
                        for kt in range(8):
                            mm(PD[:, 4:8], lhsT=wg[wb][:, kt, 128:256], rhs=hnTs[:, kt, :], start=(kt == 0), stop=(kt == 7),
                               r=[b_wg[wb], b_hnTs], w=[b_PD])
                        act(lambda e: e.activation(out=sgs[:], in_=PD[:, 0:4], func=AF.Silu), r=[b_PD], w=[b_sgs])
                        dve(lambda e, f=f: e.tensor_tensor(out=actTs[:, f, :], in0=PD[:, 4:8], in1=sgs[:], op=ALU.mult),
                            r=[b_PD, b_sgs], w=[b_actTs])
                for tt in range(4):
                    r0 = bi * 512 + tt * 128
                    for half in range(2):
                        for f in range(22):
                            mm(PC[:, half * 512:(half + 1) * 512], lhsT=actT[:, f, tt * 128:(tt + 1) * 128],
                               rhs=wdn[:, f, half * 512:(half + 1) * 512], start=(f == 0), stop=(f == 21),
                               r=[b_actT, b_wdn], w=[b_PC])
                    ys_ = tt % 2
                    dve(lambda e, tt=tt, ys_=ys_: e.tensor_tensor(out=yb[ys_][:], in0=PC[:, :], in1=hres[:, tt, :], op=ALU.add),
                        r=[b_PC, b_hres[tt]], w=[b_yb[ys_]])
                    act(lambda e, ys_=ys_: e.activation(out=ysq[:], in_=yb[ys_][:], func=AF.Square, accum_out=yss[:]),
                        r=[b_yb[ys_]], w=[b_ysq, b_yss])
                    act(lambda e: e.activation(out=yss[:], in_=yss[:], func=AF.Sqrt, scale=1.0 / 1024, bias=EPS), w=[b_yss])
                    dve(lambda e: e.reciprocal(out=yss[:], in_=yss[:]), w=[b_yss])
                    dve(lambda e, ys_=ys_: e.scalar_tensor_tensor(out=yb[ys_][:], in0=yb[ys_][:], scalar=yss[:, 0:1], in1=nfb[:],
                                                                  op0=ALU.mult, op1=ALU.mult), r=[b_yss, b_nfb], w=[b_yb[ys_]])
                    fw.dma("sp", y_o[r0:r0 + 128, :], yb[ys_][:], reads=[b_yb[ys_]])
            for half in range(2):
                for f in range(22):
                    mm(PC[0:4, half * 512:(half + 1) * 512], lhsT=actTs[:, f, :], rhs=wdn[:, f, half * 512:(half + 1) * 512],
                       start=(f == 0), stop=(f == 21), r=[b_actTs, b_wdn], w=[b_PC])
            dve(lambda e: e.tensor_tensor(out=ysb[:], in0=PC[0:4, :], in1=hs_res[:], op=ALU.add), r=[b_PC, b_hsres], w=[b_ysb])
            act(lambda e: e.activation(out=hsn_sq[:], in_=ysb[:], func=AF.Square, accum_out=hsn_ss[:]), r=[b_ysb], w=[b_hsnsq, b_hsnss])
            act(lambda e: e.activation(out=hsn_ss[:], in_=hsn_ss[:], func=AF.Sqrt, scale=1.0 / 1024, bias=EPS), w=[b_hsnss])
            dve(lambda e: e.reciprocal(out=hsn_ss[:], in_=hsn_ss[:]), w=[b_hsnss])
            dve(lambda e: e.scalar_tensor_tensor(out=ysb[:], in0=ysb[:], scalar=hsn_ss[:, 0:1], in1=nfb[0:4, :], op0=ALU.mult, op1=ALU.mult),
                r=[b_hsnss, b_nfb], w=[b_ysb])
            fw.dma("sp", ys_o[:, :], ysb[:], reads=[b_ysb])
        fw.drain()
    return nc


def _consts(NT):
    TT = NT * 128
    c = {}
    c["c_ident"] = np.eye(128, dtype=np.float32)
    half = 32
    inv = np.power(np.float32(10000.0), -np.arange(half, dtype=np.float32) * np.float32(2.0) / np.float32(64)).astype(np.float32)
    pos = (np.arange(NT)[None, :] * 128 + np.arange(128)[:, None]).astype(np.float32)
    ang = (pos[:, :, None] * inv[None, None, :]).astype(np.float32)
    c["c_cos"] = np.cos(ang).astype(np.float32).reshape(128, NT * 32)
    c["c_sin"] = np.sin(ang).astype(np.float32).reshape(128, NT * 32)
    k = np.arange(128)[:, None]
    q = np.arange(128)[None, :]
    tri = np.stack([(k <= q), (k >= q)], axis=1).astype(np.float32)
    c["c_tri"] = tri.reshape(128, 256)
    cm = np.zeros((128, 17, 128), np.float32)
    for m in range(17):
        cm[:, m, :] = (16 * k - q <= 128 * m - 31)
    c["c_cmpmask"] = cm.reshape(128, 17 * 128)
    qq = np.arange(128)[:, None]
    r = np.arange(256)[None, :] - 128
    hi = (qq >= 64).astype(np.int64)
    prel = np.zeros((128, 256), np.float32)
    prel[(r == hi) | (r == hi - 1)] = 1e4
    prel[r > hi] = -1e30
    c["c_prel"] = prel
    kk = np.arange(TT)[None, :]
    e = np.arange(64)[:, None]
    c["c_eind"] = (e == (kk // 64) % 64).astype(np.float32)
    n = np.arange(512)[:, None]
    s_ = np.arange(128)[None, :]
    c2s = ((n * 16 < s_ * 64 + 64) & (n * 16 + 32 > s_ * 64) & (n < 511)).astype(np.float32)
    c["c_c2s"] = c2s.reshape(4, 128, 128).transpose(1, 0, 2).reshape(128, 512)
    j = np.arange(128)[:, None]
    i = np.arange(128)[None, :]
    same = (j // 64) == (i // 64)
    gm = np.zeros((128, 5, 128), np.float32)
    gm[:, 0, :] = np.where(same & (i >= j), 0.0, NEGB)
    gm[:, 1, :] = np.where(same & (i > j), 0.0, NEGB)
    gm[:, 2, :] = (same & (j <= i))
    gm[:, 3, :] = (j < 64) * np.ones((1, 128))
    gm[:, 4, :] = (j >= 64) * np.ones((1, 128))
    c["c_gmask"] = gm.reshape(128, 5 * 128)
    angs = (np.float32(8192.0) * inv).astype(np.float32)
    c["c_rope_s"] = np.tile(np.concatenate([np.cos(angs), np.sin(angs)]).astype(np.float32)[None, :], (4, 1))
    nn_ = np.arange(512).reshape(4, 128).T
    c["c_ones511"] = (nn_ < 511).astype(np.float32)
    oh = np.zeros((4, 80), np.float32)
    for s_ in range(4):
        oh[s_, s_ * 4:(s_ + 1) * 4] = 1.0
    for r_ in range(4):
        for s_ in range(4):
            oh[r_, 16 + (r_ * 4 + s_) * 4 + s_] = 1.0
    c["c_oh4"] = oh
    bon = np.zeros((1, 128), np.float32)
    bon[0, 0] = 1e4
    bon[0, 127] = 1e4
    c["c_bonus_s"] = bon
    kk8 = np.arange(8192)[None, :]
    c["c_eind_s"] = (e == (kk8 // 64) % 64).astype(np.float32)
    return c


def _core_weights(inp, g, hp):
    jh = 2 * g + hp
    w_in = inp["w_in"][0]
    own = [4 * g + 2 * hp, 4 * g + 2 * hp + 1]
    oth = [4 * g + 2 * (1 - hp), 4 * g + 2 * (1 - hp) + 1]
    heads = own + oth
    cols = []
    for h in heads:
        cols += list(range(h * 64, (h + 1) * 64))

    def kvcol(branch, kv):
        base = 512 + ((branch * 2 + kv) * 2 + g) * 64
        return list(range(base, base + 64))
    for branch in range(3):
        cols += kvcol(branch, 0)
    for branch in range(3):
        cols += kvcol(branch, 1)
    for h in heads:
        cols += [1280 + h * 3 + t for t in range(3)]
    cols += list(range(2840 + jh * 128, 2840 + (jh + 1) * 128))
    cols += [3352 + jh, 3356 + jh]
    w_tok = np.ascontiguousarray(w_in[:, cols])
    gcols = []
    for part in range(3):
        gcols += list(range(1304 + part * 512 + jh * 128, 1304 + part * 512 + (jh + 1) * 128))
    w_gdn = np.ascontiguousarray(w_in[:, gcols])
    d = {"w_tok": w_tok, "w_gdn": w_gdn}
    d["g_mix"] = np.ascontiguousarray(inp["norm_mix"][0].reshape(8, 128).T)
    cwf = inp["gdn_conv_w"][0]
    gch = [jh * 128 + part * 512 + np.arange(128) for part in range(3)]
    cw = np.stack([cwf[:, ch].T for ch in gch], axis=1)
    d["conv_w"] = np.ascontiguousarray(cw.reshape(128, 12))
    d["head_sc"] = np.ascontiguousarray(np.stack([np.full(128, inp["gdn_a_log"][0, jh]),
                                                  np.full(128, inp["gdn_dt_bias"][0, jh])], axis=1).astype(np.float32))
    d["gdn_norm_b"] = np.ascontiguousarray(np.tile(inp["gdn_norm"][0][None, :], (128, 1)))
    cwt = inp["nsa_cmp_w"][0]
    d["cmp_w"] = np.ascontiguousarray(cwt.reshape(2, 16, 2, 64, 64).transpose(2, 3, 0, 1, 4).reshape(128, 2 * 16 * 64))
    pe = inp["nsa_cmp_pe"][0]
    d["cmp_pe"] = np.ascontiguousarray(pe.reshape(2, 16, 2, 64).transpose(2, 3, 0, 1).reshape(128, 32))
    return d


def _core_inputs(inp, c, NT, consts):
    b, j = c // 4, c % 4
    g, hp = j // 2, j % 2
    TT = NT * 128
    TB = TT // 4
    d = dict(consts)
    d.update(_core_weights(inp, g, hp))
    d["x"] = np.ascontiguousarray(inp["x_prompt"][b, :TT])
    d["x_own"] = np.ascontiguousarray(inp["x_prompt"][b, j * TB:(j + 1) * TB])
    sel = np.zeros((128, 4), np.float32)
    sel[:, j] = 1.0
    d["sel4"] = sel
    perm = []
    for jj in range(4):
        perm += list(range(128 * jj, 128 * jj + 128)) + list(range(512 + 128 * jj, 512 + 128 * jj + 128))
    d["w_out_p"] = np.ascontiguousarray(inp["w_out"][0][perm, :])
    d["g_ffn"] = np.ascontiguousarray(inp["norm_ffn"][0].reshape(8, 128).T)
    d["w_gu"] = np.ascontiguousarray(inp["w_gate_up"][0])
    d["w_dn"] = np.ascontiguousarray(inp["w_down"][0])
    d["nfin_b"] = np.ascontiguousarray(np.tile(inp["norm_final"][None, :], (128, 1)))
    s0 = 4 * c
    d["xs"] = np.ascontiguousarray(inp["x_sample"][s0:s0 + 4, 0, :])
    d["w_in_full"] = np.ascontiguousarray(inp["w_in"][0])
    d["cache_kv"] = inp["cache_nsa_kv"][0].reshape(2560 * 128, 512)
    d["ptab_b"] = np.ascontiguousarray(np.tile(inp["page_table"][s0:s0 + 4].reshape(1, 256), (128, 1)).astype(np.int32))
    d["c_iota"] = np.arange(128, dtype=np.float32).reshape(128, 1)
    d["win_cache"] = np.ascontiguousarray(inp["cache_nsa_win"][0, s0:s0 + 4].reshape(4, 512, 256))
    d["gdn_S"] = np.ascontiguousarray(inp["state_gdn_S"][0, s0:s0 + 4].reshape(16, 128, 128))
    d["gdn_conv"] = np.ascontiguousarray(inp["state_gdn_conv"][0, s0:s0 + 4])
    d["conv_w_b"] = np.ascontiguousarray(np.tile(inp["gdn_conv_w"][0][None], (4, 1, 1)))
    d["alog_b"] = np.ascontiguousarray(np.tile(np.concatenate([inp["gdn_a_log"][0], inp["gdn_dt_bias"][0]])[None, :], (4, 1)))
    d["gnorm_row"] = np.ascontiguousarray(inp["gdn_norm"][0][None, :])
    cwt = inp["nsa_cmp_w"][0]
    w64 = cwt.transpose(2, 0, 1, 3).reshape(64, 2 * 32 * 64)
    d["cmp_w64"] = np.ascontiguousarray(np.concatenate([w64, w64], axis=0))
    pe = inp["nsa_cmp_pe"][0]
    p64 = pe.transpose(2, 0, 1).reshape(64, 64)
    d["cmp_pe64"] = np.ascontiguousarray(np.concatenate([p64, p64], axis=0))
    wo = inp["w_out"][0]
    d["w_out_n"] = np.ascontiguousarray(wo[:512].reshape(8, 64, 1024).transpose(1, 0, 2).reshape(64, 8192))
    d["w_out_g"] = np.ascontiguousarray(wo[512:].reshape(4, 128, 1024).transpose(1, 0, 2).reshape(128, 4096))
    return d


def _run(inp, NT):
    nc = build_nc(NT, phaseB=True)
    consts = _consts(NT)
    maps = [_core_inputs(inp, c, NT, consts) for c in range(8)]
    res = run_bass_kernel_spmd(nc, maps, core_ids=list(range(8)))
    return res.results


def kernel(**inputs):
    inp = {k: np.asarray(v) for k, v in inputs.items()}
    NT = 64
    TT = NT * 128
    TB = TT // 4
    R = _run(inp, NT)
    y_prompt = np.zeros((2, TT, 1024), np.float32)
    kv_prompt = np.zeros((1, 2, TT, 4, 2, 64), np.float32)
    win_prompt = np.zeros((1, 2, 512, 2, 2, 64), np.float32)
    S_prompt = np.zeros((1, 2, 4, 128, 128), np.float32)
    conv_prompt = np.zeros((1, 2, 3, 1536), np.float32)
    for c in range(8):
        b, j = c // 4, c % 4
        g, hp = j // 2, j % 2
        r = R[c]
        y_prompt[b, j * TB:(j + 1) * TB] = r["y_out"]
        if hp == 0:
            kv_prompt[0, b, :, :, g, :] = r["kv_out"].reshape(TT, 4, 64)
            win_prompt[0, b, :, :, g, :] = r["win_out"].reshape(512, 2, 64)
        S_prompt[0, b, j] = r["S_out"]
        cv = r["conv_out"]
        for part in range(3):
            conv_prompt[0, b, :, part * 512 + j * 128:part * 512 + (j + 1) * 128] = cv[:, part, :].T
    y_sample = np.zeros((32, 1, 1024), np.float32)
    kv_sample = np.zeros((1, 32, 1, 4, 2, 64), np.float32)
    win_sample = np.zeros((1, 32, 512, 2, 2, 64), np.float32)
    S_sample = np.zeros((1, 32, 4, 128, 128), np.float32)
    conv_sample = np.zeros((1, 32, 3, 1536), np.float32)
    for c in range(8):
        r = R[c]
        s0 = 4 * c
        y_sample[s0:s0 + 4, 0] = r["ys_out"]
        kv_sample[0, s0:s0 + 4, 0] = r["kvs_out"].reshape(4, 4, 2, 64)
        win_sample[0, s0:s0 + 4] = r["wins_out"].reshape(4, 512, 2, 2, 64)
        S_sample[0, s0:s0 + 4] = r["Ss_out"].reshape(4, 4, 128, 128)
        conv_sample[0, s0:s0 + 4] = r["convs_out"]
    return (y_prompt, y_sample, kv_prompt, win_prompt, S_prompt, conv_prompt, kv_sample, win_sample, S_sample, conv_sample)
```

```python
import numpy as np
from contextlib import ExitStack
import concourse.bass as bass
import concourse.mybir as mybir
from concourse.bass_utils import run_bass_kernel_spmd

F32 = mybir.dt.float32
BF16 = mybir.dt.bfloat16
I32 = mybir.dt.int32
AF = mybir.ActivationFunctionType
ALU = mybir.AluOpType
AX = mybir.AxisListType

ENGS = ("pe", "act", "dve", "pool", "sp")

D_MODEL = 1024
SEQ = 8192
HEAD_DIM = 64
D_FF = 2816
EPS = 1e-6
NEGB = -30000.0


class Buf:
    __slots__ = ("name", "w", "rs")

    def __init__(self, name=""):
        self.name = name
        self.w = None
        self.rs = []


class FW:
    def __init__(self, nc, stack, ndma_sems=16):
        self.nc = nc
        self.eng = {"pe": nc.tensor, "act": nc.scalar, "dve": nc.vector, "pool": nc.gpsimd, "sp": nc.sync}
        self.sem = {e: stack.enter_context(nc.semaphore("s_" + e)) for e in ENGS}
        self.cnt = {e: 0 for e in ENGS}
        self.waited = {e: {} for e in ENGS}
        self.dsems = {}
        self.dstate = {}
        for q in ("sp", "pool"):
            self.dsems[q] = [stack.enter_context(nc.semaphore("d_%s%d" % (q, i))) for i in range(ndma_sems)]
            self.dstate[q] = {"i": 0, "val": [0] * ndma_sems}
        self.n_inst = 0
        self.dead = False

    def _wait(self, e, ev):
        if ev is None:
            return
        if ev[0] == "c":
            _, src, n = ev
            if src == "pe" and e == "pe":
                return
            key = ("c", src)
            if self.waited[e].get(key, 0) >= n:
                return
            self.eng[e].wait_ge(self.sem[src], n)
            self.waited[e][key] = n
        else:
            _, q, idx, val = ev
            key = ("d", q, idx)
            if self.waited[e].get(key, 0) >= val:
                return
            self.eng[e].wait_ge(self.dsems[q][idx], val)
            self.waited[e][key] = val

    def _deps(self, e, reads, writes):
        for b in reads:
            self._wait(e, b.w)
        for b in writes:
            self._wait(e, b.w)
            for r in b.rs:
                self._wait(e, r)

    def _commit(self, ev, reads, writes):
        for b in reads:
            b.rs.append(ev)
            if len(b.rs) > 96:
                b.rs = b.rs[-96:]
        for b in writes:
            b.w = ev
            b.rs = []

    def op(self, e, fn, reads=(), writes=()):
        if self.dead:
            return None
        self._deps(e, reads, writes)
        ins = fn(self.eng[e])
        self.cnt[e] += 1
        ins.then_inc(self.sem[e], 1)
        ev = ("c", e, self.cnt[e])
        self._commit(ev, reads, writes)
        self.n_inst += 1
        return ev

    def dma(self, q, out, in_, reads=(), writes=(), fn=None):
        if self.dead:
            return None
        st = self.dstate[q]
        idx = st["i"] % len(self.dsems[q])
        st["i"] += 1
        if st["val"][idx] > 0:
            self._wait(q, ("d", q, idx, st["val"][idx]))
        self._deps(q, reads, writes)
        if fn is None:
            ins = self.eng[q].dma_start(out=out, in_=in_)
        else:
            ins = fn(self.eng[q])
        st["val"][idx] += 16
        ins.then_inc(self.dsems[q][idx], 16)
        ev = ("d", q, idx, st["val"][idx])
        self._commit(ev, reads, writes)
        self.n_inst += 1
        return ev

    def barrier(self):
        for e in ENGS:
            for src in ENGS:
                if src != e and self.cnt[src] > 0:
                    self._wait(e, ("c", src, self.cnt[src]))
            for q in ("sp", "pool"):
                stq = self.dstate[q]
                for idx, v in enumerate(stq["val"]):
                    if v:
                        self._wait(e, ("d", q, idx, v))

    def drain(self):
        for q in ("sp", "pool"):
            st = self.dstate[q]
            for idx, v in enumerate(st["val"]):
                if v:
                    self._wait("sp", ("d", q, idx, v))


class _Stop(Exception):
    pass


STOP = None
SKIP_CC = False


def build_nc(NT=64, dbg=False, phaseB=False):
    TT = NT * 128
    NG = NT // 4
    nc = bass.Bass("TRN2", target_bir_lowering=False)

    def din(name, shape, dt=F32):
        return nc.dram_tensor(name, list(shape), dt, kind="ExternalInput").ap()

    def dout(name, shape, dt=F32):
        return nc.dram_tensor(name, list(shape), dt, kind="ExternalOutput").ap()

    x_d = din("x", [TT, 1024])
    wtok_d = din("w_tok", [1024, 782])
    wgdn_d = din("w_gdn", [1024, 384])
    gmix_d = din("g_mix", [128, 8])
    cw_d = din("conv_w", [128, 12])
    hsc_d = din("head_sc", [128, 2])
    gnorm_d = din("gdn_norm_b", [128, 128])
    cmpw_d = din("cmp_w", [128, 2 * 16 * 64])
    cmppe_d = din("cmp_pe", [128, 32])
    ident_d = din("c_ident", [128, 128])
    cos_d = din("c_cos", [128, NT * 32])
    sin_d = din("c_sin", [128, NT * 32])
    tri_d = din("c_tri", [128, 256])
    cmpmask_d = din("c_cmpmask", [128, 17 * 128])
    prel_d = din("c_prel", [128, 256])
    eind_d = din("c_eind", [64, TT])
    c2s_d = din("c_c2s", [128, 4 * 128])
    gmask_d = din("c_gmask", [128, 5 * 128])

    TB = TT // 4
    NB = TB // 512
    if phaseB:
        xown_d = din("x_own", [TB, 1024])
        sel4_d = din("sel4", [128, 4])
        wout_d = din("w_out_p", [1024, 1024])
        gffn_d = din("g_ffn", [128, 8])
        wgu_d = din("w_gu", [1024, 5632])
        wdn_d = din("w_dn", [2816, 1024])
        nfin_d = din("nfin_b", [128, 1024])
        y_o = dout("y_out", [TB, 1024])
    if phaseB:
        xs_d = din("xs", [4, 1024])
        win_full_d = din("w_in_full", [1024, 3360])
        cache_d = din("cache_kv", [2560 * 128, 512])
        ptab_d = din("ptab_b", [128, 256], I32)
        iota_d = din("c_iota", [128, 1])
        wincache_d = din("win_cache", [4, 512, 256])
        gS_d = din("gdn_S", [16, 128, 128])
        gconv_d = din("gdn_conv", [4, 3, 1536])
        convwb_d = din("conv_w_b", [4, 4, 1536])
        alogb_d = din("alog_b", [4, 8])
        gnrow_d = din("gnorm_row", [1, 128])
        cmpw64_d = din("cmp_w64", [128, 2 * 32 * 64])
        woutn_d = din("w_out_n", [64, 8 * 1024])
        woutg_d = din("w_out_g", [128, 4 * 1024])
        ropes_d = din("c_rope_s", [4, 64])
        ones511_d = din("c_ones511", [128, 4])
        pe64_d = din("cmp_pe64", [128, 64])
        eind_s_d = din("c_eind_s", [64, 8192])
        oh4_d = din("c_oh4", [4, 80])
        bonus_d = din("c_bonus_s", [1, 128])
        ys_o = dout("ys_out", [4, 1024])
        kvs_o = dout("kvs_out", [4, 512])
        wins_o = dout("wins_out", [4, 512, 256])
        Ss_o = dout("Ss_out", [16, 128, 128])
        convs_o = dout("convs_out", [4, 3, 1536])
    kv_o = dout("kv_out", [TT, 256])
    win_o = dout("win_out", [512, 128])
    S_o = dout("S_out", [128, 128])
    conv_o = dout("conv_out", [128, 3, 3])
    CH = min(2048, TT)
    NCH = TT // CH
    omT_o = dout("omT_out", [256, TT], BF16) if not phaseB else None
    xin = [nc.dram_tensor("xin%d" % k, [256, CH], BF16).ap() for k in range(NCH)] if phaseB else None
    b_xin = [Buf() for _ in range(NCH)]
    dbg_o = dout("dbg_out", [128, 1300]) if dbg else None

    st = ExitStack()
    with st:
        fw = FW(nc, st)

        cur = [st]

        def sb(name, shape, dt=F32):
            return cur[0].enter_context(nc.sbuf_tensor(name, list(shape), dt))

        def ps(name, shape, dt=F32):
            return st.enter_context(nc.psum_tensor(name, list(shape), dt))

        def pe(fn, r=(), w=()):
            return fw.op("pe", fn, r, w)

        def act(fn, r=(), w=()):
            return fw.op("act", fn, r, w)

        def dve(fn, r=(), w=()):
            return fw.op("dve", fn, r, w)

        def pool(fn, r=(), w=()):
            return fw.op("pool", fn, r, w)

        def mm(out, lhsT, rhs, start=True, stop=True, r=(), w=()):
            return pe(lambda e: e.matmul(out, lhsT=lhsT, rhs=rhs, start=start, stop=stop), r, w)

        def tr(out, in_, ident, r=(), w=()):
            return pe(lambda e: e.transpose(out, in_, ident), r, w)

        def bcast(ap, shape, axis):
            return ap.unsqueeze(axis).to_broadcast(list(shape))

        ident_f = sb("ident_f", [128, 128]); b_identf = Buf()
        ident_b = sb("ident_b", [128, 128], BF16); b_identb = Buf()
        ones_f2 = sb("ones_f2", [128, 128]); b_onesf2 = Buf()
        hs_res = sb("hs_res", [4, 1024]); b_hsres = Buf()
        stA = st.enter_context(ExitStack())
        cur[0] = stA
        pool(lambda e: e.memset(ones_f2[:], 1.0), w=[b_onesf2])
        ones_b = sb("ones_b", [128, 128], BF16); b_onesb = Buf()
        ones_f = sb("ones_f", [128, 128]); b_onesf = Buf()
        csT = [sb("csT%d" % i_, [128, 2, 4, 32]) for i_ in range(2)]; b_csT = [Buf() for _ in range(2)]
        tri = sb("tri", [128, 2, 128], BF16); b_tri = Buf()
        cmpmask = sb("cmpmask", [128, 17, 128], BF16); b_cmpmask = Buf()
        prel = sb("prel", [128, 256]); b_prel = Buf()
        gmask = sb("gmask", [128, 5, 128]); b_gmask = Buf()
        wtok = sb("wtok", [128, 8, 782], BF16); b_wtok = Buf()
        wgdn = sb("wgdn", [128, 8, 384], BF16); b_wgdn = Buf()
        gmix = sb("gmix", [128, 8]); b_gmix = Buf()
        cw = sb("cw", [128, 12]); b_cw = Buf()
        hsc = sb("hsc", [128, 2]); b_hsc = Buf()
        negA = sb("negA", [128, 1]); b_negA = Buf()
        gnb = sb("gnb", [128, 128]); b_gnb = Buf()
        cmpw = sb("cmpw", [128, 2, 16, 64], BF16); b_cmpw = Buf()
        cmppe = sb("cmppe", [128, 2, 16], BF16); b_cmppe = Buf()

        fw.dma("sp", ident_f[:], ident_d[:, :], writes=[b_identf])
        fw.dma("pool", ident_b[:], ident_d[:, :], writes=[b_identb])
        fw.dma("pool", tri[:].rearrange("p a b -> p (a b)"), tri_d[:, :], writes=[b_tri])
        fw.dma("pool", cmpmask[:].rearrange("p a b -> p (a b)"), cmpmask_d[:, :], writes=[b_cmpmask])
        fw.dma("sp", prel[:], prel_d[:, :], writes=[b_prel])
        fw.dma("sp", gmask[:].rearrange("p a b -> p (a b)"), gmask_d[:, :], writes=[b_gmask])
        fw.dma("sp", gmix[:], gmix_d[:, :], writes=[b_gmix])
        fw.dma("sp", cw[:], cw_d[:, :], writes=[b_cw])
        fw.dma("sp", hsc[:], hsc_d[:, :], writes=[b_hsc])
        fw.dma("sp", gnb[:], gnorm_d[:, :], writes=[b_gnb])
        fw.dma("pool", cmpw[:].rearrange("p a b c -> p (a b c)"), cmpw_d[:, :], writes=[b_cmpw])
        fw.dma("pool", cmppe[:].rearrange("p a b -> p (a b)"), cmppe_d[:, :], writes=[b_cmppe])
        for kt in range(8):
            fw.dma("pool", wtok[:, kt, :], wtok_d[kt * 128:(kt + 1) * 128, :], writes=[b_wtok])
            fw.dma("pool", wgdn[:, kt, :], wgdn_d[kt * 128:(kt + 1) * 128, :], writes=[b_wgdn])
        pool(lambda e: e.memset(ones_b[:], 1.0), w=[b_onesb])
        pool(lambda e: e.memset(ones_f[:], 1.0), w=[b_onesf])
        for kt in range(8):
            dve(lambda e, kt=kt: e.tensor_scalar(out=wtok[:, kt, :], in0=wtok[:, kt, :], scalar1=gmix[:, kt:kt + 1],
                                                 scalar2=None, op0=ALU.mult), r=[b_gmix], w=[b_wtok])
            dve(lambda e, kt=kt: e.tensor_scalar(out=wgdn[:, kt, :], in0=wgdn[:, kt, :], scalar1=gmix[:, kt:kt + 1],
                                                 scalar2=None, op0=ALU.mult), r=[b_gmix], w=[b_wgdn])
        act(lambda e: e.activation(out=negA[:], in_=hsc[:, 0:1], func=AF.Exp), r=[b_hsc], w=[b_negA])
        dve(lambda e: e.tensor_scalar(out=negA[:], in0=negA[:], scalar1=-1.0, scalar2=None, op0=ALU.mult), w=[b_negA])

        KselT = sb("KselT", [128, TT], BF16); b_ksel = [Buf() for _ in range(NT)]; b_eind = Buf()
        Vsel = sb("Vsel", [128, NT, 65], BF16); b_vsel = [Buf() for _ in range(NT)]
        KwinT = sb("KwinT", [64, 8 * 128], BF16); b_kwin = [Buf() for _ in range(8)]
        Vwin = sb("Vwin", [128, 8, 65], BF16); b_vwin = [Buf() for _ in range(8)]
        Rk = sb("Rk", [128, 2, 160], BF16); b_Rk = Buf()
        ckT = sb("ckT", [64, 512], BF16); b_ckT = Buf()
        cvx = sb("cvx", [128, 4, 193], BF16); b_cvx = Buf()
        c2s_f = sb("c2s_f", [128, 4, 128]); b_c2sf = Buf()
        ckb = sb("ckb", [64, 1]); b_ckb = Buf()
        cvb = sb("cvb", [8, 64]); b_cvb = Buf()
        cvrow = sb("cvrow", [1, 64], BF16); b_cvrow = Buf()

        fw.dma("pool", KselT[64:128, :], eind_d[:, :], writes=[b_eind])
        pool(lambda e: e.memset(Vsel[:, :, 64:65], 1.0), w=b_vsel)
        pool(lambda e: e.memset(Vwin[:, :, 64:65], 1.0), w=b_vwin)
        pool(lambda e: e.memset(Rk[:], 0.0), w=[b_Rk])
        pool(lambda e: e.memset(ckT[:], 0.0), w=[b_ckT])
        pool(lambda e: e.memset(cvx[:], 0.0), w=[b_cvx])
        fw.dma("sp", c2s_f[:].rearrange("p a b -> p (a b)"), c2s_d[:, :], writes=[b_c2sf])
        pool(lambda e: e.tensor_copy(out=cvx[:, :, 64:192], in_=c2s_f[:]), r=[b_c2sf], w=[b_cvx])
        pool(lambda e: e.memset(cvx[:, :, 192:193], 1.0), w=[b_cvx])

        PA = ps("PA", [128, 1024]); b_PA = Buf()
        PB = ps("PB", [128, 1024]); b_PB = Buf()
        PC = ps("PC", [128, 1024]); b_PC = Buf()
        PD = ps("PD", [128, 512]); b_PD = Buf()
        PT = ps("PT", [128, 1024], BF16); b_PT = Buf()

        for lp in range(16):
            mm(PD[0:64, 0:1], lhsT=cmpw[:, 0, lp, :], rhs=cmppe[:, 0, lp:lp + 1], start=(lp == 0), stop=(lp == 15),
               r=[b_cmpw, b_cmppe], w=[b_PD])
        act(lambda e: e.copy(out=ckb[:], in_=PD[0:64, 0:1]), r=[b_PD], w=[b_ckb])
        for lp in range(16):
            mm(PD[0:1, 64:128], lhsT=cmppe[:, 1, lp:lp + 1], rhs=cmpw[:, 1, lp, :], start=(lp == 0), stop=(lp == 15),
               r=[b_cmpw, b_cmppe], w=[b_PD])
        act(lambda e: e.copy(out=cvrow[:], in_=PD[0:1, 64:128]), r=[b_PD], w=[b_cvrow])
        mm(PD[0:8, 128:192], lhsT=ones_b[0:1, 0:8], rhs=cvrow[:], r=[b_onesb, b_cvrow], w=[b_PD])
        act(lambda e: e.copy(out=cvb[:], in_=PD[0:8, 128:192]), r=[b_PD], w=[b_cvb])

        NXB = 2
        xt = [sb("xt%d" % i, [128, 1024]) for i in range(NXB)]; b_xt = [Buf() for _ in range(NXB)]
        ssq = [sb("ssq%d" % i, [128, 1]) for i in range(NXB)]; b_ssq = [Buf() for _ in range(NXB)]
        xs = [sb("xs%d" % i, [128, 1024], BF16) for i in range(NXB)]; b_xs = [Buf() for _ in range(NXB)]
        xnT = sb("xnT", [128, 8, 512], BF16); b_xnT = [Buf() for _ in range(4)]
        pj = [sb("pj%d" % i, [128, 782]) for i in range(2)]; b_pj = [Buf() for _ in range(2)]
        rq = [sb("rq%d" % i, [128, 7, 64]) for i in range(2)]; b_rq = [Buf() for _ in range(2)]
        rt = sb("rt", [128, 4, 7, 32]); b_rt = Buf()
        ko = [sb("ko%d" % i, [128, 6, 64]) for i in range(2)]; b_ko = [Buf() for _ in range(2)]
        qkb = sb("qkb", [128, 7, 64], BF16); b_qkb = Buf()
        kvc2 = sb("kvc2", [128, 2, 128], BF16); b_kvc2 = Buf()
        QT = [sb("QT%d" % i_, [64, 512], BF16) for i_ in range(2)]; b_QT = [Buf() for _ in range(2)]
        Qaug = [sb("Qaug%d" % i_, [128, 2, 256], BF16) for i_ in range(2)]; b_Qaug = [Buf() for _ in range(2)]
        gates = [sb("gates%d" % i_, [128, 12]) for i_ in range(2)]; b_gates = [Buf() for _ in range(2)]
        gz = sb("gz", [128, 140]); b_gz = Buf()
        zsil = [sb("zsil%d" % i_, [128, 4, 128]) for i_ in range(2)]; b_zsil = [[Buf() for _ in range(4)] for _ in range(2)]
        abg = [sb("abg%d" % i_, [128, 4, 2]) for i_ in range(2)]; b_abg = [Buf() for _ in range(2)]
        cvnew = sb("cvnew", [8, 64], BF16); b_cvnew = Buf()
        PTc = [sb("PTc%d" % i, [128, 512], BF16) for i in range(4)]; b_PTc = [Buf() for _ in range(4)]
        PTs = [sb("PTs%d" % i, [128, 1024], BF16) for i in range(2)]; b_PTs = [Buf() for _ in range(2)]
        acc_c = sb("acc_c", [128, 4, 193]); b_accc = Buf()
        acc_sw = sb("acc_sw", [128, 4, 65]); b_accsw = Buf()
        rcp = sb("rcp", [128, 12]); b_rcp = Buf()
        imp = sb("imp", [128, 128]); b_imp = Buf()
        score = sb("score", [128, 128]); b_score = Buf()
        mx8 = sb("mx8", [128, 16]); b_mx8 = Buf()
        thr = sb("thr", [128, 1]); b_thr = Buf()
        sc2 = sb("sc2", [128, 128]); b_sc2 = Buf()
        mbt = sb("mbt", [128, 2, 128]); b_mbt = Buf()
        coef = sb("coef", [128, 6]); b_coef = Buf()
        onsa = sb("onsa", [128, 128]); b_onsa = Buf()
        om = sb("om", [128, 256], BF16); b_om = Buf()
        omT = [sb("omT%d" % i_, [128, 2, 512], BF16) for i_ in range(2)]; b_omT = [Buf() for _ in range(2)]
        raw = [sb("raw%d" % i_, [128, 3, 515]) for i_ in range(2)]; b_raw = [Buf() for _ in range(2)]
        cacc = sb("cacc", [128, 3, 512]); b_cacc = Buf()
        csil = cacc; b_csil = b_cacc
        sqb = sb("sqb", [128, 2, 512], BF16); b_sqb = Buf()
        rnorm = sb("rnorm", [128, 2, 512]); b_rnorm = Buf()
        gT = sb("gT", [128, 3, 512], BF16); b_gT = Buf()
        gtok = sb("gtok", [128, 4, 3, 128], BF16); b_gtok = Buf()
        gsc = sb("gsc", [128, 16, 4]); b_gsc = Buf()
        glc = sb("glc", [128, 2, 4]); b_glc = Buf()
        dg1 = sb("dg1", [128, 4, 128]); b_dg1 = Buf()
        dg2 = sb("dg2", [128, 4, 128]); b_dg2 = Buf()
        dgn = sb("dgn", [128, 4, 128]); b_dgn = Buf()
        gmask4 = sb("gmask4", [128, 2, 4, 128]); b_gmask4 = Buf()
        decT = sb("decT", [128, 4, 128], BF16); b_decT = Buf()
        decbT = sb("decbT", [128, 4, 128], BF16); b_decbT = Buf()
        Um = [sb("Um%d" % i, [128, 4, 128], BF16) for i in range(2)]; b_Um = [Buf() for _ in range(2)]
        Lm = [sb("Lm%d" % i, [128, 4, 128], BF16) for i in range(2)]; b_Lm = [Buf() for _ in range(2)]
        Pm = [sb("Pm%d" % i, [128, 4, 128], BF16) for i in range(2)]; b_Pm = [Buf() for _ in range(2)]
        Xm = sb("Xm", [128, 4, 256], BF16); b_Xm = Buf()
        uw = sb("uw", [128, 4, 256], BF16); b_uw = Buf()
        kgm = sb("kgm", [128, 4, 2, 128], BF16); b_kgm = Buf()
        aqkT = sb("aqkT", [128, 4, 128], BF16); b_aqkT = Buf()
        Dg = sb("Dg", [128, 4, 128], BF16); b_Dg = Buf()
        QpA = sb("QpA", [128, 4, 128], BF16); QpB = sb("QpB", [128, 4, 128], BF16); b_Qp = Buf()
        glc8 = sb("glc8", [128, 8]); b_glc8 = Buf()
        MTf = sb("MTf", [128, 8, 128], BF16); b_MTf = Buf()
        MT8 = sb("MT8", [128, 8, 128], BF16); b_MT8 = Buf()
        Sb9 = sb("Sb9", [128, 9, 128], BF16); b_Sb9 = [Buf() for _ in range(9)]
        Sf = sb("Sf", [128, 128]); b_Sf = Buf()
        og4 = sb("og4", [128, 4, 128]); b_og4 = Buf()
        og4q = dgn; b_og4q = b_dgn
        og4s = sb("og4s", [128, 4]); b_og4s = Buf()
        omg = sb("omg", [128, 4, 128], BF16); b_omg = Buf()

        pool(lambda e: e.memset(kgm[:], 0.0), w=[b_kgm])
        for mk_ in range(2):
            pool(lambda e, mk_=mk_: e.tensor_copy(out=gmask4[:, mk_], in_=bcast(gmask[:, mk_, :], [128, 4, 128], 1)), r=[b_gmask], w=[b_gmask4])
        pool(lambda e: e.memset(raw[0][:], 0.0), w=[b_raw[0]])
        pool(lambda e: e.memset(raw[1][:], 0.0), w=[b_raw[1]])
        pool(lambda e: e.memset(QpA[:], 0.0), w=[b_Qp])
        pool(lambda e: e.memset(QpB[:], 0.0), w=[b_Qp])
        pool(lambda e: e.memset(Sb9[:, 0, :], 0.0), w=[b_Sb9[0]])

        G_G, G_BETA, G_GCUM, G_GL, G_EG, G_EKG, G_LNB, G_NEGG, G_SKBG, G_GB = range(10)


        hits = {}

        def chk2(name):
            if STOP == name:
                fw.dead = True

        def chk(name):
            hits[name] = hits.get(name, 0) + 1
            if STOP == name or STOP == "%s@%d" % (name, hits[name]):
                raise _Stop()

        def gdn_gen(grp):
            gp = grp % 2
            for c3 in range(3):
                dve(lambda e, c3=c3: e.tensor_scalar(out=cacc[:, c3, :], in0=raw[gp][:, c3, 0:512], scalar1=cw[:, c3 * 4:c3 * 4 + 1],
                                                      scalar2=None, op0=ALU.mult), r=[b_raw[gp], b_cw], w=[b_cacc])
                for jj in range(1, 4):
                    dve(lambda e, c3=c3, jj=jj: e.scalar_tensor_tensor(out=cacc[:, c3, :], in0=raw[gp][:, c3, jj:jj + 512],
                                                                        scalar=cw[:, c3 * 4 + jj:c3 * 4 + jj + 1], in1=cacc[:, c3, :],
                                                                        op0=ALU.mult, op1=ALU.add), r=[b_raw[gp], b_cw], w=[b_cacc])
            act(lambda e: e.activation(out=csil[:].rearrange("p a b -> p (a b)"), in_=cacc[:].rearrange("p a b -> p (a b)"),
                                       func=AF.Silu), r=[b_cacc], w=[b_csil])
            yield
            act(lambda e: e.activation(out=sqb[:].rearrange("p a b -> p (a b)"), in_=csil[:, 0:2, :].rearrange("p a b -> p (a b)"),
                                       func=AF.Square), r=[b_csil], w=[b_sqb])
            for c3 in range(2):
                mm(PB[:, c3 * 512:(c3 + 1) * 512], lhsT=ones_b[:], rhs=sqb[:, c3, :], r=[b_onesb, b_sqb], w=[b_PB])
            act(lambda e: e.activation(out=rnorm[:].rearrange("p a b -> p (a b)"), in_=PB[:, 0:1024], func=AF.Ln, bias=EPS),
                r=[b_PB], w=[b_rnorm])
            act(lambda e: e.activation(out=rnorm[:].rearrange("p a b -> p (a b)"), in_=rnorm[:].rearrange("p a b -> p (a b)"),
                                       func=AF.Exp, scale=-0.5), w=[b_rnorm])
            dve(lambda e: e.scalar_tensor_tensor(out=gT[:, 0, :], in0=csil[:, 0, :], scalar=128.0 ** -0.5, in1=rnorm[:, 0, :],
                                                 op0=ALU.mult, op1=ALU.mult), r=[b_csil, b_rnorm], w=[b_gT])
            dve(lambda e: e.tensor_tensor(out=gT[:, 1, :], in0=csil[:, 1, :], in1=rnorm[:, 1, :], op=ALU.mult),
                r=[b_csil, b_rnorm], w=[b_gT])
            act(lambda e: e.copy(out=gT[:, 2, :], in_=csil[:, 2, :]), r=[b_csil], w=[b_gT])
            for tt in range(4):
                for c3 in range(3):
                    tr(PT[:, c3 * 128:(c3 + 1) * 128], gT[:, c3, tt * 128:(tt + 1) * 128], ident_b[:], r=[b_gT, b_identb], w=[b_PT])
                act(lambda e, tt=tt: e.copy(out=gtok[:, tt, :, :], in_=PT[:, 0:384].rearrange("p (c d) -> p c d", c=3)),
                    r=[b_PT], w=[b_gtok])
            yield
            a_ap = abg[gp][:, :, 0]
            b_ap = abg[gp][:, :, 1]
            act(lambda e: e.activation(out=gsc[:, G_G, :], in_=a_ap, func=AF.Exp, bias=hsc[:, 1:2]), r=[b_abg[gp], b_hsc], w=[b_gsc])
            act(lambda e: e.activation(out=gsc[:, G_G, :], in_=gsc[:, G_G, :], func=AF.Ln, bias=1.0), w=[b_gsc])
            dve(lambda e: e.tensor_scalar(out=gsc[:, G_G, :], in0=gsc[:, G_G, :], scalar1=negA[:, 0:1], scalar2=None, op0=ALU.mult),
                r=[b_negA], w=[b_gsc])
            act(lambda e: e.activation(out=gsc[:, G_BETA, :], in_=b_ap, func=AF.Exp, scale=-1.0), r=[b_abg[gp]], w=[b_gsc])
            dve(lambda e: e.tensor_scalar(out=gsc[:, G_BETA, :], in0=gsc[:, G_BETA, :], scalar1=1.0, scalar2=None, op0=ALU.add), w=[b_gsc])
            dve(lambda e: e.reciprocal(out=gsc[:, G_BETA, :], in_=gsc[:, G_BETA, :]), w=[b_gsc])
            act(lambda e: e.activation(out=gsc[:, G_LNB, :], in_=gsc[:, G_BETA, :], func=AF.Ln), w=[b_gsc])
            mm(PD[:, 0:4], lhsT=gmask[:, 2, :], rhs=gsc[:, G_G, :], r=[b_gmask, b_gsc], w=[b_PD])
            mm(PD[:, 4:8], lhsT=gmask[:, 3, :], rhs=gsc[:, G_G, :], r=[b_gmask, b_gsc], w=[b_PD])
            mm(PD[:, 8:12], lhsT=gmask[:, 4, :], rhs=gsc[:, G_G, :], r=[b_gmask, b_gsc], w=[b_PD])
            act(lambda e: e.copy(out=gsc[:, G_GCUM, :], in_=PD[:, 0:4]), r=[b_PD], w=[b_gsc])
            act(lambda e: e.copy(out=glc[:].rearrange("p a b -> p (a b)"), in_=PD[:, 4:12]), r=[b_PD], w=[b_glc])
            dve(lambda e: e.tensor_copy(out=gsc[0:64, G_GL, :], in_=glc[0:64, 0, :]), r=[b_glc], w=[b_gsc])
            dve(lambda e: e.tensor_copy(out=gsc[64:128, G_GL, :], in_=glc[64:128, 1, :]), r=[b_glc], w=[b_gsc])
            act(lambda e: e.activation(out=gsc[:, G_EG, :], in_=gsc[:, G_GCUM, :], func=AF.Exp), w=[b_gsc])
            dve(lambda e: e.tensor_tensor(out=gsc[:, G_EKG, :], in0=gsc[:, G_GL, :], in1=gsc[:, G_GCUM, :], op=ALU.subtract), w=[b_gsc])
            act(lambda e: e.activation(out=gsc[:, G_EKG, :], in_=gsc[:, G_EKG, :], func=AF.Exp), w=[b_gsc])
            act(lambda e: e.activation(out=glc[:].rearrange("p a b -> p (a b)"), in_=glc[:].rearrange("p a b -> p (a b)"), func=AF.Exp),
                w=[b_glc])
            dve(lambda e: e.tensor_scalar(out=gsc[:, G_NEGG, :], in0=gsc[:, G_GCUM, :], scalar1=-1.0, scalar2=None, op0=ALU.mult), w=[b_gsc])
            dve(lambda e: e.tensor_tensor(out=gsc[:, G_SKBG, :], in0=gsc[:, G_BETA, :], in1=gsc[:, G_EG, :], op=ALU.mult), w=[b_gsc])
            dve(lambda e: e.tensor_scalar(out=gsc[:, G_SKBG, :], in0=gsc[:, G_SKBG, :], scalar1=-1.0, scalar2=None, op0=ALU.mult), w=[b_gsc])
            dve(lambda e: e.tensor_tensor(out=gsc[:, G_GB, :], in0=gsc[:, G_GCUM, :], in1=gsc[:, G_LNB, :], op=ALU.add), w=[b_gsc])

            yield
            dve(lambda e: e.tensor_copy(out=glc8[:, 0:8:2], in_=glc[:, 0, :]), r=[b_glc], w=[b_glc8])
            dve(lambda e: e.tensor_copy(out=glc8[:, 1:8:2], in_=glc[:, 1, :]), r=[b_glc], w=[b_glc8])
            identf4 = bcast(ident_f[:], [128, 4, 128], 1)
            identb4 = bcast(ident_b[:], [128, 4, 128], 1)

            def colb(col):
                return bcast(gsc[:, col, :], [128, 4, 128], 2)
            dve(lambda e: e.tensor_tensor(out=dg1[:], in0=identf4, in1=colb(G_GCUM), op=ALU.mult), r=[b_identf, b_gsc], w=[b_dg1])
            dve(lambda e: e.tensor_tensor(out=dg2[:], in0=identf4, in1=colb(G_GB), op=ALU.mult), r=[b_identf, b_gsc], w=[b_dg2])
            dve(lambda e: e.tensor_tensor(out=dgn[:], in0=identf4, in1=colb(G_NEGG), op=ALU.mult), r=[b_identf, b_gsc], w=[b_dgn])
            for (PSx, bPSx, dgx, mk_) in ((PC[:, 0:512], b_PC, dg1, 0), (PA[:, 0:512], b_PA, dg2, 1)):
                mm(PSx, lhsT=ones_f[:], rhs=dgx[:].rearrange("p a b -> p (a b)"), start=True, stop=False, r=[b_onesf, b_dg1, b_dg2], w=[bPSx])
                mm(PSx, lhsT=ident_f[:], rhs=gmask4[:, mk_].rearrange("p a b -> p (a b)"), start=False, stop=False, r=[b_identf, b_gmask4], w=[bPSx])
                for tt in range(4):
                    mm(PSx[:, tt * 128:(tt + 1) * 128], lhsT=dgn[:, tt, :], rhs=ones_f[:], start=False, stop=True,
                       r=[b_dgn, b_onesf], w=[bPSx])
            act(lambda e: e.activation(out=decT[:].rearrange("p a b -> p (a b)"), in_=PC[:, 0:512], func=AF.Exp), r=[b_PC], w=[b_decT])
            act(lambda e: e.activation(out=decbT[:].rearrange("p a b -> p (a b)"), in_=PA[:, 0:512], func=AF.Exp), r=[b_PA], w=[b_decbT])
            yield
            for tt in range(4):
                kT_t = gT[:, 1, tt * 128:(tt + 1) * 128]
                qT_t = gT[:, 0, tt * 128:(tt + 1) * 128]
                mm(PC[:, 512 + tt * 128:512 + (tt + 1) * 128], lhsT=kT_t, rhs=kT_t, r=[b_gT], w=[b_PC])
                mm(PB[:, tt * 128:(tt + 1) * 128], lhsT=kT_t, rhs=qT_t, r=[b_gT], w=[b_PB])
            dve(lambda e: e.tensor_tensor(out=Um[0][:].rearrange("p a b -> p (a b)"), in0=PC[:, 512:1024], in1=decbT[:].rearrange("p a b -> p (a b)"),
                                          op=ALU.mult), r=[b_PC, b_decbT], w=[b_Um[0]])
            dve(lambda e: e.tensor_tensor(out=aqkT[:].rearrange("p a b -> p (a b)"), in0=PB[:, 0:512], in1=decT[:].rearrange("p a b -> p (a b)"),
                                          op=ALU.mult), r=[b_PB, b_decT], w=[b_aqkT])
            for tt in range(4):
                tr(PT[:, tt * 128:(tt + 1) * 128], Um[0][:, tt, :], ident_b[:], r=[b_Um[0], b_identb], w=[b_PT])
            act(lambda e: e.copy(out=Lm[0][:].rearrange("p a b -> p (a b)"), in_=PT[:, 0:512]), r=[b_PT], w=[b_Lm[0]])
            dve(lambda e: e.tensor_tensor(out=Pm[0][:], in0=identb4, in1=Um[0][:], op=ALU.subtract), r=[b_identb, b_Um[0]], w=[b_Pm[0]])
            yield
            cu, cp = 0, 0
            for lvl in range(5):
                nu = 1 - cu
                for tt in range(4):
                    mm(PC[:, tt * 128:(tt + 1) * 128], lhsT=Um[cu][:, tt, :], rhs=Lm[cu][:, tt, :], r=[b_Um[cu], b_Lm[cu]], w=[b_PC])
                if lvl < 4:
                    for tt in range(4):
                        mm(PA[:, tt * 128:(tt + 1) * 128], lhsT=Lm[cu][:, tt, :], rhs=Um[cu][:, tt, :], r=[b_Um[cu], b_Lm[cu]], w=[b_PA])
                act(lambda e, nu=nu: e.copy(out=Lm[nu][:].rearrange("p a b -> p (a b)"), in_=PC[:, 0:512]), r=[b_PC], w=[b_Lm[nu]])
                if lvl < 4:
                    act(lambda e, nu=nu: e.copy(out=Um[nu][:].rearrange("p a b -> p (a b)"), in_=PA[:, 0:512]), r=[b_PA], w=[b_Um[nu]])
                for tt in range(4):
                    mm(PB[:, tt * 128:(tt + 1) * 128], lhsT=Lm[nu][:, tt, :], rhs=Pm[cp][:, tt, :], r=[b_Lm[nu], b_Pm[cp]], w=[b_PB])
                dve(lambda e, cp=cp: e.tensor_tensor(out=Pm[1 - cp][:].rearrange("p a b -> p (a b)"), in0=PB[:, 0:512],
                                                     in1=Pm[cp][:].rearrange("p a b -> p (a b)"), op=ALU.add), r=[b_PB, b_Pm[cp]], w=[b_Pm[1 - cp]])
                cu = nu
                cp = 1 - cp
            yield
            Tt = Pm[cp]
            bTt = b_Pm[cp]
            dve(lambda e: e.tensor_tensor(out=Xm[:, :, 0:128], in0=gtok[:, :, 2, :], in1=colb(G_BETA), op=ALU.mult), r=[b_gtok, b_gsc], w=[b_Xm])
            dve(lambda e: e.tensor_tensor(out=Xm[:, :, 128:256], in0=gtok[:, :, 1, :], in1=colb(G_SKBG), op=ALU.mult), r=[b_gtok, b_gsc], w=[b_Xm])
            for tt in range(4):
                mm(PC[:, tt * 256:(tt + 1) * 256], lhsT=Tt[:, tt, :], rhs=Xm[:, tt, :], r=[bTt, b_Xm], w=[b_PC])
            act(lambda e: e.copy(out=uw[:].rearrange("p a b -> p (a b)"), in_=PC[:, 0:1024]), r=[b_PC], w=[b_uw])
            dve(lambda e: e.tensor_tensor(out=kgm[0:64, :, 0, :], in0=gtok[0:64, :, 1, :], in1=bcast(gsc[0:64, G_EKG, :], [64, 4, 128], 2), op=ALU.mult),
                r=[b_gtok, b_gsc], w=[b_kgm])
            dve(lambda e: e.tensor_tensor(out=kgm[64:128, :, 1, :], in0=gtok[64:128, :, 1, :], in1=bcast(gsc[64:128, G_EKG, :], [64, 4, 128], 2), op=ALU.mult),
                r=[b_gtok, b_gsc], w=[b_kgm])
            dve(lambda e: e.tensor_tensor(out=Dg[:], in0=identb4, in1=colb(G_EG), op=ALU.mult), r=[b_identb, b_gsc], w=[b_Dg])
            for tt in range(4):
                mm(PA[:, tt * 128:(tt + 1) * 128], lhsT=gtok[:, tt, 0, :], rhs=Dg[:, tt, :], start=True, stop=False, r=[b_gtok, b_Dg], w=[b_PA])
                mm(PA[:, tt * 128:(tt + 1) * 128], lhsT=uw[:, tt, 128:256], rhs=aqkT[:, tt, :], start=False, stop=True, r=[b_uw, b_aqkT], w=[b_PA])
            act(lambda e: e.copy(out=QpA[:, :, 0:64], in_=PA[:, 0:512].rearrange("p (a b) -> p a b", a=4)[:, :, 0:64]), r=[b_PA], w=[b_Qp])
            act(lambda e: e.copy(out=QpB[:, :, 64:128], in_=PA[:, 0:512].rearrange("p (a b) -> p a b", a=4)[:, :, 64:128]), r=[b_PA], w=[b_Qp])
            yield
            for tt in range(4):
                for c in range(2):
                    r0 = 64 * c
                    ch = tt * 2 + c
                    mm(PB[:, ch * 128:(ch + 1) * 128], lhsT=uw[:, tt, 128:256], rhs=kgm[:, tt, c, :], r=[b_uw, b_kgm], w=[b_PB])
            dve(lambda e: e.tensor_tensor(out=MTf[:], in0=bcast(ident_f[:], [128, 8, 128], 1), in1=bcast(glc8[:], [128, 8, 128], 2), op=ALU.mult),
                r=[b_identf, b_glc8], w=[b_MTf])
            dve(lambda e: e.tensor_tensor(out=MT8[:].rearrange("p a b -> p (a b)"), in0=PB[:, 0:1024], in1=MTf[:].rearrange("p a b -> p (a b)"),
                                          op=ALU.add), r=[b_PB, b_MTf], w=[b_MT8])
            for ch in range(8):
                tt, c = ch // 2, ch % 2
                r0 = 64 * c
                i = grp * 4 + tt
                PSc = PC[:, (ch % 2) * 512:(ch % 2) * 512 + 128]
                mm(PSc, lhsT=kgm[:, tt, c, :], rhs=uw[:, tt, 0:128], start=True, stop=False, r=[b_kgm, b_uw], w=[b_PC])
                mm(PSc, lhsT=MT8[:, ch, :], rhs=Sb9[:, ch, :], start=False, stop=True, r=[b_MT8, b_Sb9[ch]], w=[b_PC])
                if ch < 7:
                    act(lambda e, ch=ch, PSc=PSc: e.copy(out=Sb9[:, ch + 1, :], in_=PSc), r=[b_PC], w=[b_Sb9[ch + 1]])
                else:
                    act(lambda e, PSc=PSc: e.copy(out=Sb9[:, 8, :], in_=PSc), r=[b_PC], w=[b_Sb9[8]])
                    if i == NT - 1:
                        act(lambda e, PSc=PSc: e.copy(out=Sf[:], in_=PSc), r=[b_PC], w=[b_Sf])
            yield
            for tt in range(4):
                mm(PA[:, tt * 128:(tt + 1) * 128], lhsT=QpA[:, tt, :], rhs=Sb9[:, 2 * tt, :], start=True, stop=False, r=[b_Qp, b_Sb9[2 * tt]], w=[b_PA])
                mm(PA[:, tt * 128:(tt + 1) * 128], lhsT=QpB[:, tt, :], rhs=Sb9[:, 2 * tt + 1, :], start=False, stop=False,
                   r=[b_Qp, b_Sb9[2 * tt + 1]], w=[b_PA])
                mm(PA[:, tt * 128:(tt + 1) * 128], lhsT=aqkT[:, tt, :], rhs=uw[:, tt, 0:128], start=False, stop=True, r=[b_aqkT, b_uw], w=[b_PA])
            act(lambda e: e.copy(out=Sb9[:, 0, :], in_=Sb9[:, 8, :]), r=[b_Sb9[8]], w=[b_Sb9[0]])
            act(lambda e: e.copy(out=og4[:].rearrange("p a b -> p (a b)"), in_=PA[:, 0:512]), r=[b_PA], w=[b_og4])
            dve(lambda e: e.tensor_tensor(out=og4q[:], in0=og4[:], in1=og4[:], op=ALU.mult), r=[b_og4], w=[b_og4q])
            dve(lambda e: e.tensor_reduce(out=og4s[:], in_=og4q[:], axis=AX.X, op=ALU.add), r=[b_og4q], w=[b_og4s])
            act(lambda e: e.activation(out=og4s[:], in_=og4s[:], func=AF.Ln, scale=1.0 / 128, bias=EPS), w=[b_og4s])
            act(lambda e: e.activation(out=og4s[:], in_=og4s[:], func=AF.Exp, scale=-0.5), w=[b_og4s])
            dve(lambda e: e.tensor_tensor(out=og4[:], in0=og4[:], in1=bcast(og4s[:], [128, 4, 128], 2), op=ALU.mult), r=[b_og4s], w=[b_og4])
            dve(lambda e: e.tensor_tensor(out=og4[:], in0=og4[:], in1=bcast(gnb[:], [128, 4, 128], 1), op=ALU.mult), r=[b_gnb], w=[b_og4])
            dve(lambda e: e.tensor_tensor(out=omg[:], in0=og4[:], in1=zsil[gp][:], op=ALU.mult), r=[b_og4] + b_zsil[gp], w=[b_omg])
            for tt in range(4):
                tr(PT[:, tt * 128:(tt + 1) * 128], omg[:, tt, :], ident_b[:], r=[b_omg, b_identb], w=[b_PT])
            act(lambda e: e.copy(out=omT[gp][:, 1, :], in_=PT[:, 0:512]), r=[b_PT], w=[b_omT[gp]])
            for hh in range(2):
                if phaseB:
                    kch = (grp * 512) // CH
                    oc = (grp * 512) % CH
                    fw.dma("sp", xin[kch][hh * 128:(hh + 1) * 128, oc:oc + 512], omT[gp][:, hh, :], reads=[b_omT[gp]], writes=[b_xin[kch]])
                else:
                    fw.dma("sp", omT_o[hh * 128:(hh + 1) * 128, grp * 512:(grp + 1) * 512], omT[gp][:, hh, :], reads=[b_omT[gp]])


        def gen_G(grp):
            gp2 = grp % 2
            fw.dma("sp", csT[gp2][:, 0].rearrange("p a b -> p (a b)"), cos_d[:, grp * 128:(grp + 1) * 128], writes=[b_csT[gp2]])
            fw.dma("sp", csT[gp2][:, 1].rearrange("p a b -> p (a b)"), sin_d[:, grp * 128:(grp + 1) * 128], writes=[b_csT[gp2]])
            for tt in range(4):
                i = grp * 4 + tt
                s = i % NXB
                fw.dma("sp", xt[s][:], x_d[i * 128:(i + 1) * 128, :], writes=[b_xt[s]])
                act(lambda e, s=s: e.activation(out=xs[s][:], in_=xt[s][:], func=AF.Square, accum_out=ssq[s][:]),
                    r=[b_xt[s]], w=[b_xs[s], b_ssq[s]])
                act(lambda e, s=s: e.activation(out=ssq[s][:], in_=ssq[s][:], func=AF.Ln, scale=1.0 / 1024, bias=EPS), w=[b_ssq[s]])
                act(lambda e, s=s: e.activation(out=ssq[s][:], in_=ssq[s][:], func=AF.Exp, scale=-0.5), w=[b_ssq[s]])
                dve(lambda e, s=s: e.tensor_scalar(out=xs[s][:], in0=xt[s][:], scalar1=ssq[s][:, 0:1], scalar2=None,
                                                   op0=ALU.mult), r=[b_xt[s], b_ssq[s]], w=[b_xs[s]])
                for kt in range(8):
                    tr(PT[:, kt * 128:(kt + 1) * 128], xs[s][:, kt * 128:(kt + 1) * 128], ident_b[:],
                       r=[b_xs[s], b_identb], w=[b_PT])
                act(lambda e, tt=tt: e.copy(out=xnT[:, :, tt * 128:(tt + 1) * 128],
                                            in_=PT[:].rearrange("p (k t) -> p k t", k=8)),
                    r=[b_PT], w=[b_xnT[tt]])

            yield
            pool(lambda e: e.tensor_copy(out=raw[gp2][:, :, 0:3], in_=raw[1 - gp2][:, :, 512:515]), r=[b_raw[1 - gp2]], w=[b_raw[gp2]])
            for c3 in range(3):
                for kt in range(8):
                    mm(PB[:, 0:512], lhsT=wgdn[:, kt, c3 * 128:(c3 + 1) * 128], rhs=xnT[:, kt, :],
                       start=(kt == 0), stop=(kt == 7), r=[b_wgdn] + b_xnT, w=[b_PB])
                act(lambda e, c3=c3: e.copy(out=raw[gp2][:, c3, 3:515], in_=PB[:, 0:512]), r=[b_PB], w=[b_raw[gp2]])

            yield

        def gen_F(i):
            grp = i // 4
            tt = i % 4
            gp2 = grp % 2
            p2 = i % 2
            for kt in range(8):
                mm(PA[:, 0:512], lhsT=xnT[:, kt, tt * 128:(tt + 1) * 128], rhs=wtok[:, kt, 0:512],
                   start=(kt == 0), stop=(kt == 7), r=[b_xnT[tt], b_wtok], w=[b_PA])
            yield
            for kt in range(8):
                mm(PA[:, 512:782], lhsT=xnT[:, kt, tt * 128:(tt + 1) * 128], rhs=wtok[:, kt, 512:782],
                   start=(kt == 0), stop=(kt == 7), r=[b_xnT[tt], b_wtok], w=[b_PA])
            yield
            act(lambda e: e.copy(out=pj[p2][:], in_=PA[:, 0:782]), r=[b_PA], w=[b_pj[p2]])
            yield
            x1 = pj[p2][:, 0:448].rearrange("p (h d) -> p h d", h=7)[:, :, 0:32]
            x2 = pj[p2][:, 0:448].rearrange("p (h d) -> p h d", h=7)[:, :, 32:64]
            cosb = bcast(csT[gp2][:, 0, tt, :], [128, 7, 32], 1)
            sinb = bcast(csT[gp2][:, 1, tt, :], [128, 7, 32], 1)
            dve(lambda e: e.tensor_tensor(out=rt[:, 0], in0=x1, in1=cosb, op=ALU.mult), r=[b_pj[p2], b_csT[gp2]], w=[b_rt])
            yield
            dve(lambda e: e.tensor_tensor(out=rt[:, 1], in0=x2, in1=sinb, op=ALU.mult), r=[b_pj[p2], b_csT[gp2]], w=[b_rt])
            yield
            dve(lambda e: e.tensor_tensor(out=rt[:, 2], in0=x2, in1=cosb, op=ALU.mult), r=[b_pj[p2], b_csT[gp2]], w=[b_rt])
            yield
            dve(lambda e: e.tensor_tensor(out=rt[:, 3], in0=x1, in1=sinb, op=ALU.mult), r=[b_pj[p2], b_csT[gp2]], w=[b_rt])
            yield
            dve(lambda e: e.tensor_tensor(out=rq[p2][:, :, 0:32], in0=rt[:, 0], in1=rt[:, 1], op=ALU.subtract),
                 r=[b_rt], w=[b_rq[p2]])
            yield
            dve(lambda e: e.tensor_tensor(out=rq[p2][:, :, 32:64], in0=rt[:, 2], in1=rt[:, 3], op=ALU.add),
                 r=[b_rt], w=[b_rq[p2]])
            yield
            pool(lambda e: e.tensor_copy(out=ko[p2][:, 0:6:2, :], in_=rq[p2][:, 4:7, :]), r=[b_rq[p2]], w=[b_ko[p2]])
            yield
            pool(lambda e: e.tensor_copy(out=ko[p2][:, 1:6:2, :],
                                         in_=pj[p2][:, 448:640].rearrange("p (h d) -> p h d", h=3)),
                 r=[b_pj[p2]], w=[b_ko[p2]])
            yield
            fw.dma("sp", kv_o[i * 128:(i + 1) * 128, :], ko[p2][:, 0:4, :].rearrange("p a b -> p (a b)"), reads=[b_ko[p2]])
            yield
            if i >= NT - 4:
                wi = i - (NT - 4)
                fw.dma("sp", win_o[wi * 128:(wi + 1) * 128, :], ko[p2][:, 4:6, :].rearrange("p a b -> p (a b)"),
                       reads=[b_ko[p2]])
            yield
            act(lambda e: e.copy(out=qkb[:], in_=rq[p2][:]), r=[b_rq[p2]], w=[b_qkb])
            yield
            act(lambda e: e.copy(out=Vsel[:, i, 0:64], in_=pj[p2][:, 512:576]), r=[b_pj[p2]], w=[b_vsel[i]])
            yield
            act(lambda e: e.copy(out=Vwin[:, i % 8, 0:64], in_=pj[p2][:, 576:640]), r=[b_pj[p2]], w=[b_vwin[i % 8]])
            yield
            dve(lambda e: e.tensor_copy(out=kvc2[:, 0, :].rearrange("p (a d) -> p a d", a=2),
                                         in_=bcast(rq[p2][:, 4, :], [128, 2, 64], 1)), r=[b_rq[p2]], w=[b_kvc2])
            yield
            dve(lambda e: e.tensor_copy(out=kvc2[:, 1, :].rearrange("p (a d) -> p a d", a=2),
                                         in_=bcast(pj[p2][:, 448:512], [128, 2, 64], 1)), r=[b_pj[p2]], w=[b_kvc2])
            yield
            act(lambda e: e.activation(out=gz[:], in_=pj[p2][:, 640:780], func=AF.Exp, scale=-1.0), r=[b_pj[p2]], w=[b_gz])
            yield
            dve(lambda e: e.tensor_scalar(out=gz[:], in0=gz[:], scalar1=1.0, scalar2=None, op0=ALU.add), w=[b_gz])
            yield
            dve(lambda e: e.reciprocal(out=gz[:], in_=gz[:]), w=[b_gz])
            yield
            dve(lambda e: e.tensor_copy(out=gates[i % 2][:], in_=gz[:, 0:12]), r=[b_gz], w=[b_gates[i % 2]])
            yield
            dve(lambda e, tt=tt: e.tensor_tensor(out=zsil[gp2][:, tt, :], in0=gz[:, 12:140], in1=pj[p2][:, 652:780], op=ALU.mult),
                r=[b_gz, b_pj[p2]], w=[b_zsil[gp2][tt]])
            yield
            pool(lambda e, tt=tt: e.tensor_copy(out=abg[gp2][:, tt, :], in_=pj[p2][:, 780:782]), r=[b_pj[p2]], w=[b_abg[gp2]])
            yield
            for h in range(4):
                tr(PT[0:64, h * 128:(h + 1) * 128], qkb[:, h, :], ident_b[:], r=[b_qkb, b_identb], w=[b_PT])
            yield
            tr(PT[0:64, 512:640], qkb[:, 5, :], ident_b[:], r=[b_qkb, b_identb], w=[b_PT])
            yield
            tr(PT[0:64, 640:768], qkb[:, 6, :], ident_b[:], r=[b_qkb, b_identb], w=[b_PT])
            yield
            tr(PT[:, 768:896], kvc2[:, 0, :], ident_b[:], r=[b_kvc2, b_identb], w=[b_PT])
            yield
            tr(PT[:, 896:1024], kvc2[:, 1, :], ident_b[:], r=[b_kvc2, b_identb], w=[b_PT])
            yield
            act(lambda e: e.copy(out=QT[i % 2][:], in_=PT[0:64, 0:512]), r=[b_PT], w=[b_QT[i % 2]])
            yield
            act(lambda e: e.copy(out=Qaug[i % 2][0:64, 0, :], in_=PT[0:64, 0:256]), r=[b_PT], w=[b_Qaug[i % 2]])
            yield
            act(lambda e: e.copy(out=Qaug[i % 2][0:64, 1, :], in_=PT[0:64, 0:256]), r=[b_PT], w=[b_Qaug[i % 2]])
            yield
            act(lambda e: e.copy(out=KselT[0:64, i * 128:(i + 1) * 128], in_=PT[0:64, 512:640]),
                r=[b_PT], w=[b_ksel[i]])
            yield
            act(lambda e: e.copy(out=KwinT[0:64, (i % 8) * 128:(i % 8 + 1) * 128], in_=PT[0:64, 640:768]),
                r=[b_PT], w=[b_kwin[i % 8]])
            yield
            pool(lambda e: e.tensor_copy(out=Rk[:, :, 0:32], in_=Rk[:, :, 128:160]), w=[b_Rk])
            yield
            act(lambda e: e.copy(out=Rk[0:64, :, 32:160], in_=PT[0:64, 768:1024].rearrange("p (a t) -> p a t", a=2)),
                r=[b_PT], w=[b_Rk])
            yield
            act(lambda e: e.copy(out=Rk[64:128, :, 31:159], in_=PT[64:128, 768:1024].rearrange("p (a t) -> p a t", a=2)),
                r=[b_PT], w=[b_Rk])
            yield
            m0 = 1 if i == 0 else 0
            nb = 8 - m0
            n0 = 8 * i - 1 + m0
            for lp in range(16):
                c0 = 16 + 2 * lp + 16 * m0
                mm(PD[0:64, 0:nb], lhsT=cmpw[:, 0, lp, :], rhs=Rk[:, 0, c0:c0 + 16 * (nb - 1) + 1:16],
                   start=(lp == 0), stop=(lp == 15), r=[b_cmpw, b_Rk], w=[b_PD])
            yield
            act(lambda e: e.activation(out=ckT[:, n0:n0 + nb], in_=PD[0:64, 0:nb], func=AF.Identity, bias=ckb[:, 0:1]),
                r=[b_PD, b_ckb], w=[b_ckT])
            yield
            for lp in range(16):
                c0 = 16 + 2 * lp + 16 * m0
                mm(PD[0:nb, 64:128], lhsT=Rk[:, 1, c0:c0 + 16 * (nb - 1) + 1:16], rhs=cmpw[:, 1, lp, :],
                   start=(lp == 0), stop=(lp == 15), r=[b_cmpw, b_Rk], w=[b_PD])
            yield
            dve(lambda e: e.tensor_tensor(out=cvnew[0:nb, :], in0=PD[0:nb, 64:128], in1=cvb[0:nb, :], op=ALU.add),
                r=[b_PD, b_cvb], w=[b_cvnew])
            yield
            segs = []
            n = n0
            while n < n0 + nb:
                jt = n // 128
                cnt = min(n0 + nb - n, (jt + 1) * 128 - n)
                segs.append((n, cnt))
                n += cnt
            for (ns, cnt) in segs:
                fw.dma("sp", cvx[ns % 128:ns % 128 + cnt, ns // 128, 0:64], cvnew[ns - n0:ns - n0 + cnt, :],
                       reads=[b_cvnew], writes=[b_cvx])
            yield

            yield

        def gen_B(i):
            grp = i // 4
            tt = i % 4
            gp2 = grp % 2
            p2 = i % 2
            njt = (8 * i + 6) // 128 + 1
            for jt in range(njt):
                pb = jt
                mm(PA[:, 0:512], lhsT=ckT[:, jt * 128:(jt + 1) * 128], rhs=QT[i % 2][:], r=[b_ckT, b_QT[i % 2]], w=[b_PA])
                act(lambda e, pb=pb: e.activation(out=PTc[pb][:], in_=PA[:, 0:512], func=AF.Exp, scale=0.125),
                    r=[b_PA], w=[b_PTc[pb]])
                mk = None
                if jt == njt - 1:
                    mk = i % 16
                elif jt == njt - 2 and i % 16 == 0:
                    mk = 16
                if mk is not None:
                    dve(lambda e, mk=mk, pb=pb: e.tensor_tensor(out=PTc[pb][:].rearrange("p (h q) -> p h q", h=4),
                                                                in0=PTc[pb][:].rearrange("p (h q) -> p h q", h=4),
                                                                in1=bcast(cmpmask[:, mk, :], [128, 4, 128], 1), op=ALU.mult),
                        r=[b_cmpmask], w=[b_PTc[pb]])
            yield
            for h in range(4):
                for jt in range(njt):
                    mm(PC[:, (h // 2) * 512 + (h % 2) * 193:(h // 2) * 512 + (h % 2) * 193 + 193],
                       lhsT=PTc[jt][:, h * 128:(h + 1) * 128], rhs=cvx[:, jt, :],
                       start=(jt == 0), stop=(jt == njt - 1), r=[b_PTc[jt], b_cvx], w=[b_PC])
            yield
            act(lambda e: e.copy(out=acc_c[:, 0:2, :], in_=PC[:, 0:386].rearrange("p (h c) -> p h c", h=2)),
                r=[b_PC], w=[b_accc])
            yield
            act(lambda e: e.copy(out=acc_c[:, 2:4, :], in_=PC[:, 512:898].rearrange("p (h c) -> p h c", h=2)),
                r=[b_PC], w=[b_accc])
            yield
            dve(lambda e: e.tensor_scalar(out=rcp[:, 0:4], in0=acc_c[:, :, 192], scalar1=1e-30, scalar2=None, op0=ALU.max),
                r=[b_accc], w=[b_rcp])
            yield
            dve(lambda e: e.reciprocal(out=rcp[:, 0:4], in_=rcp[:, 0:4]), w=[b_rcp])
            yield
            dve(lambda e: e.tensor_scalar(out=imp[:], in0=acc_c[:, 0, 64:192], scalar1=rcp[:, 0:1], scalar2=None, op0=ALU.mult),
                r=[b_accc, b_rcp], w=[b_imp])
            yield
            for h in range(1, 4):
                dve(lambda e, h=h: e.scalar_tensor_tensor(out=imp[:], in0=acc_c[:, h, 64:192], scalar=rcp[:, h:h + 1],
                                                          in1=imp[:], op0=ALU.mult, op1=ALU.add),
                    r=[b_accc, b_rcp], w=[b_imp])
            yield
            dve(lambda e: e.tensor_tensor(out=score[:], in0=imp[:], in1=prel[:, 128 - 2 * i:256 - 2 * i], op=ALU.add),
                r=[b_imp, b_prel], w=[b_score])
            yield
            dve(lambda e: e.tensor_scalar(out=score[:, 0:1], in0=score[:, 0:1], scalar1=1e4, scalar2=None, op0=ALU.add),
                w=[b_score])
            yield
            dve(lambda e: e.max(out=mx8[:, 0:8], in_=score[:]), r=[b_score], w=[b_mx8])
            yield
            dve(lambda e: e.match_replace(out=sc2[:], in_to_replace=mx8[:, 0:8], in_values=score[:], imm_value=-3e38),
                r=[b_score, b_mx8], w=[b_sc2])
            yield
            dve(lambda e: e.max(out=mx8[:, 8:16], in_=sc2[:]), r=[b_sc2], w=[b_mx8])
            yield
            dve(lambda e: e.tensor_reduce(out=thr[:], in_=mx8[:, 8:16], axis=AX.X, op=ALU.min), r=[b_mx8], w=[b_thr])
            yield
            dve(lambda e: e.tensor_scalar(out=sc2[:], in0=score[:], scalar1=thr[:, 0:1], scalar2=None, op0=ALU.is_ge),
                r=[b_score, b_thr], w=[b_sc2])
            yield
            dve(lambda e: e.scalar_tensor_tensor(out=sc2[:], in0=score[:], scalar=-1e29, in1=sc2[:],
                                                 op0=ALU.is_gt, op1=ALU.mult), r=[b_score], w=[b_sc2])
            yield
            dve(lambda e: e.tensor_scalar(out=mbt[:, 0, :], in0=sc2[:], scalar1=-NEGB, scalar2=NEGB,
                                          op0=ALU.mult, op1=ALU.add), r=[b_sc2], w=[b_mbt])
            yield
            dve(lambda e: e.tensor_copy(out=mbt[:, 1, 0:64], in_=mbt[:, 0, 64:128]), w=[b_mbt])
            yield
            dve(lambda e: e.tensor_copy(out=mbt[:, 1, 64:128], in_=mbt[:, 0, 0:64]), w=[b_mbt])
            yield
            mm(PD[:, 0:128], lhsT=mbt[:, 1, :], rhs=ident_f[:], r=[b_mbt, b_identf], w=[b_PD])
            yield
            mm(PD[:, 128:256], lhsT=mbt[:, 0, :], rhs=ident_f[:], r=[b_mbt, b_identf], w=[b_PD])
            yield
            for hh_ in range(2):
                dve(lambda e, hh_=hh_: e.tensor_copy(out=Qaug[i % 2][64:128, 0, hh_ * 128:(hh_ + 1) * 128], in_=PD[64:128, 0:128]),
                    r=[b_PD], w=[b_Qaug[i % 2]])
                dve(lambda e, hh_=hh_: e.tensor_copy(out=Qaug[i % 2][64:128, 1, hh_ * 128:(hh_ + 1) * 128], in_=PD[64:128, 128:256]),
                    r=[b_PD], w=[b_Qaug[i % 2]])
            yield

            yield
            sgroups = []
            t = 0
            while t <= i:
                gt_ = min(4, i + 1 - t)
                sgroups.append((t, gt_))
                t += gt_

            def emit_S(gi_):
                t_, gt_ = sgroups[gi_]
                PSs_ = PA if gi_ % 2 == 0 else PB
                bPSs_ = b_PA if gi_ % 2 == 0 else b_PB
                for u in range(gt_):
                    tk = t_ + u
                    ab = 0 if tk < 32 else 1
                    mm(PSs_[:, u * 256:(u + 1) * 256], lhsT=KselT[:, tk * 128:(tk + 1) * 128], rhs=Qaug[i % 2][:, ab, :],
                       r=[b_ksel[tk], b_eind, b_Qaug[i % 2]], w=[bPSs_])
            emit_S(0)
            yield
            for gi in range(len(sgroups)):
                t, gt_ = sgroups[gi]
                pb = gi % 2
                PSs = PA if pb == 0 else PB
                bPSs = b_PA if pb == 0 else b_PB
                if gi + 1 < len(sgroups):
                    emit_S(gi + 1)
                yield
                act(lambda e, PSs=PSs, gt_=gt_, pb=pb: e.activation(out=PTs[pb][:, 0:gt_ * 256], in_=PSs[:, 0:gt_ * 256],
                                                                  func=AF.Exp, scale=0.125), r=[bPSs], w=[b_PTs[pb]])
                if t + gt_ - 1 == i:
                    u = gt_ - 1
                    dve(lambda e, u=u, pb=pb: e.tensor_tensor(
                        out=PTs[pb][:, u * 256:(u + 1) * 256].rearrange("p (h q) -> p h q", h=2),
                        in0=PTs[pb][:, u * 256:(u + 1) * 256].rearrange("p (h q) -> p h q", h=2),
                        in1=bcast(tri[:, 0, :], [128, 2, 128], 1), op=ALU.mult), r=[b_tri], w=[b_PTs[pb]])
                for u in range(gt_):
                    tk = t + u
                    for h in range(2):
                        mm(PC[:, h * 512:h * 512 + 65], lhsT=PTs[pb][:, u * 256 + h * 128:u * 256 + (h + 1) * 128],
                           rhs=Vsel[:, tk, :], start=(tk == 0), stop=(tk == i), r=[b_PTs[pb], b_vsel[tk]], w=[b_PC])
            yield
            gi = len(sgroups)
            yield
            act(lambda e: e.copy(out=acc_sw[:, 0, :], in_=PC[:, 0:65]), r=[b_PC], w=[b_accsw])
            yield
            act(lambda e: e.copy(out=acc_sw[:, 1, :], in_=PC[:, 512:577]), r=[b_PC], w=[b_accsw])
            yield

            yield
            t0w = max(0, i - 4)
            wt = list(range(t0w, i + 1))
            pb = gi % 2
            PSs = PA if pb == 0 else PB
            bPSs = b_PA if pb == 0 else b_PB
            for gsub in range(0, len(wt), 4):
                sub = wt[gsub:gsub + 4]
                for u, tk in enumerate(sub):
                    mm(PSs[:, u * 256:(u + 1) * 256], lhsT=KwinT[:, (tk % 8) * 128:(tk % 8 + 1) * 128], rhs=QT[i % 2][:, 0:256],
                       r=[b_kwin[tk % 8], b_QT[i % 2]], w=[bPSs])
                act(lambda e, PSs=PSs, n_=len(sub), pb=pb: e.activation(out=PTs[pb][:, 0:n_ * 256], in_=PSs[:, 0:n_ * 256],
                                                                       func=AF.Exp, scale=0.125), r=[bPSs], w=[b_PTs[pb]])
                for u, tk in enumerate(sub):
                    mkk = None
                    if tk == i:
                        mkk = 0
                    elif tk == i - 4:
                        mkk = 1
                    if mkk is not None:
                        dve(lambda e, u=u, pb=pb, mkk=mkk: e.tensor_tensor(
                            out=PTs[pb][:, u * 256:(u + 1) * 256].rearrange("p (h q) -> p h q", h=2),
                            in0=PTs[pb][:, u * 256:(u + 1) * 256].rearrange("p (h q) -> p h q", h=2),
                            in1=bcast(tri[:, mkk, :], [128, 2, 128], 1), op=ALU.mult), r=[b_tri], w=[b_PTs[pb]])
                for u, tk in enumerate(sub):
                    for h in range(2):
                        mm(PC[:, 386 + h * 512:451 + h * 512], lhsT=PTs[pb][:, u * 256 + h * 128:u * 256 + (h + 1) * 128],
                           rhs=Vwin[:, tk % 8, :], start=(tk == wt[0]), stop=(tk == i), r=[b_PTs[pb], b_vwin[tk % 8]], w=[b_PC])
                pb = 1 - pb
                PSs = PA if pb == 0 else PB
                bPSs = b_PA if pb == 0 else b_PB
            yield
            act(lambda e: e.copy(out=acc_sw[:, 2, :], in_=PC[:, 386:451]), r=[b_PC], w=[b_accsw])
            yield
            act(lambda e: e.copy(out=acc_sw[:, 3, :], in_=PC[:, 898:963]), r=[b_PC], w=[b_accsw])
            yield

            yield
            dve(lambda e: e.tensor_scalar(out=rcp[:, 4:8], in0=acc_sw[:, :, 64], scalar1=1e-30, scalar2=None, op0=ALU.max),
                r=[b_accsw], w=[b_rcp])
            yield
            dve(lambda e: e.reciprocal(out=rcp[:, 4:8], in_=rcp[:, 4:8]), w=[b_rcp])
            yield
            g3 = gates[i % 2][:, 0:6].rearrange("p (h j) -> p h j", h=2)
            cf = coef[:].rearrange("p (h j) -> p h j", h=2)
            dve(lambda e: e.tensor_tensor(out=cf[:, :, 0], in0=g3[:, :, 0], in1=rcp[:, 0:2], op=ALU.mult),
                r=[b_gates[i % 2], b_rcp], w=[b_coef])
            yield
            dve(lambda e: e.tensor_tensor(out=cf[:, :, 1], in0=g3[:, :, 1], in1=rcp[:, 4:6], op=ALU.mult),
                r=[b_gates[i % 2], b_rcp], w=[b_coef])
            yield
            dve(lambda e: e.tensor_tensor(out=cf[:, :, 2], in0=g3[:, :, 2], in1=rcp[:, 6:8], op=ALU.mult),
                r=[b_gates[i % 2], b_rcp], w=[b_coef])
            yield
            for h in range(2):
                dve(lambda e, h=h: e.tensor_scalar(out=onsa[:, h * 64:(h + 1) * 64], in0=acc_c[:, h, 0:64],
                                                   scalar1=coef[:, 3 * h:3 * h + 1], scalar2=None, op0=ALU.mult),
                    r=[b_accc, b_coef], w=[b_onsa])
                dve(lambda e, h=h: e.scalar_tensor_tensor(out=onsa[:, h * 64:(h + 1) * 64], in0=acc_sw[:, h, 0:64],
                                                          scalar=coef[:, 3 * h + 1:3 * h + 2], in1=onsa[:, h * 64:(h + 1) * 64],
                                                          op0=ALU.mult, op1=ALU.add), r=[b_accsw, b_coef], w=[b_onsa])
                dve(lambda e, h=h: e.scalar_tensor_tensor(out=om[:, h * 64:(h + 1) * 64], in0=acc_sw[:, 2 + h, 0:64],
                                                          scalar=coef[:, 3 * h + 2:3 * h + 3], in1=onsa[:, h * 64:(h + 1) * 64],
                                                          op0=ALU.mult, op1=ALU.add), r=[b_accsw, b_coef, b_onsa], w=[b_om])
            yield
            b_om_tiles = None

            if dbg and i == 1:
                fw.dma("sp", dbg_o[:, 0:772], acc_c[:].rearrange("p a b -> p (a b)"), reads=[b_accc])
                fw.dma("sp", dbg_o[:, 772:1032], acc_sw[:].rearrange("p a b -> p (a b)"), reads=[b_accsw])
                fw.dma("sp", dbg_o[:, 1032:1160], score[:], reads=[b_score])
                fw.dma("sp", dbg_o[:, 1160:1172], gates[i % 2][:], reads=[b_gates[i % 2]])
                fw.dma("sp", dbg_o[:, 1172:1300], mbt[:, 0, :], reads=[b_mbt])
            yield
            tr(PT[:, 0:128], om[:, 0:128], ident_b[:], r=[b_om, b_identb], w=[b_PT])
            yield
            act(lambda e, tt=tt: e.copy(out=omT[gp2][:, 0, tt * 128:(tt + 1) * 128], in_=PT[:, 0:128]), r=[b_PT], w=[b_omT[gp2]])
            yield

        if phaseB:
            wgu_s = nc.dram_tensor("wgu_s", [22, 128, 8 * 256], BF16).ap()
            wdn_s = nc.dram_tensor("wdn_s", [22, 128, 1024], BF16).ap()
            b_wgus = [Buf() for _ in range(22)]
            b_wdns = Buf()
            for f in range(22):
                dstv = wgu_s[f].rearrange("p (k c) -> p k c", k=8)
                fw.dma("pool", dstv[:, :, 0:128], wgu_d[:, f * 128:(f + 1) * 128].rearrange("(k p) c -> p k c", p=128), writes=[b_wgus[f]])
                fw.dma("pool", dstv[:, :, 128:256], wgu_d[:, 2816 + f * 128:2816 + (f + 1) * 128].rearrange("(k p) c -> p k c", p=128),
                       writes=[b_wgus[f]])
            fw.dma("pool", wdn_s[:, :, :], wdn_d[:, :].rearrange("(f p) c -> f p c", p=128), writes=[b_wdns])
        gdn_prev = None
        try:
          chk("setup")
          def drain_gen(g_):
              if g_ is not None:
                  for _ in g_:
                      pass

          def interleave(gens):
              alive = [g_ for g_ in gens if g_ is not None]
              while alive:
                  for g_ in list(alive):
                      try:
                          next(g_)
                      except StopIteration:
                          alive.remove(g_)

          drain_gen(gen_G(0))
          drain_gen(gen_F(0))
          for i in range(NT):
              grp = i // 4
              gF = None
              if i + 1 < NT:
                  def chain_next(i=i):
                      if (i + 1) % 4 == 0:
                          yield from gen_G((i + 1) // 4)
                      yield from gen_F(i + 1)
                  gF = chain_next()
              interleave([gen_B(i), gF])
              if gdn_prev is not None:
                  for _ in range(4):
                      next(gdn_prev, None)
              if i % 4 == 3:
                  drain_gen(gdn_prev)
                  gdn_prev = gdn_gen(grp)

        except _Stop:
            pass
        if gdn_prev is not None:
            for _ in gdn_prev:
                pass
        fw.dma("sp", S_o[:, :], Sf[:], reads=[b_Sf])
        fw.dma("sp", conv_o[:, :, :], raw[(NG - 1) % 2][:, :, 512:515], reads=[b_raw[(NG - 1) % 2]])

        if phaseB:
            xout = [nc.dram_tensor("xout%d" % k, [1024, CH], BF16).ap() for k in range(NCH)]
            b_xout = Buf()
            RG = [[0, 1, 2, 3], [4, 5, 6, 7]]
            ccs = st.enter_context(nc.semaphore("ccs"))
            for k in range(NCH if not SKIP_CC else 0):
                fw._wait("pool", b_xin[k].w)
                nc.gpsimd.collective_compute("AllGather", ALU.bypass, replica_groups=RG, ins=[xin[k][:, :].opt()],
                                             outs=[xout[k][:, :].opt()]).then_inc(ccs)
                nc.gpsimd.wait_ge(ccs, k + 1)
            pool(lambda e: e.memset(cvnew[0:1, 0:1], 0.0), w=[b_xout, b_cvnew])
            fw.barrier()
            stA.close()
            stS = st.enter_context(ExitStack())
            cur[0] = stS
            nb_ = [0]

            def T(shape, dt=F32):
                nb_[0] += 1
                return sb("sm%d" % nb_[0], shape, dt), Buf()

            xs_t, b_xs = T([4, 1024])
            gmix2, b_gmix2 = T([128, 8])
            ropes, b_ropes = T([4, 64])
            ptab, b_ptab = T([128, 256], I32)
            iota_c, b_iota = T([128, 1])
            idxs, b_idxs = T([128, 256], I32)
            cmpw64, b_cmpw64 = T([128, 2, 32, 64], BF16)
            pe64, b_pe64 = T([128, 2, 32], BF16)
            c2s2, b_c2s2 = T([128, 4, 128])
            on511, b_on511 = T([128, 4])
            oh4, b_oh4 = T([4, 80])
            bonus, b_bonus = T([1, 128])
            alogb, b_alogb = T([4, 8])
            gnrow, b_gnrow = T([1, 128])
            pjs, b_pjs = T([4, 3360])
            Xs, b_Xs = T([4, 2056])
            QKT, b_QKT = T([128, 14, 4], BF16)
            OGT, b_OGT = T([128, 4, 4], BF16)
            one11, b_one11 = T([1, 1])
            cb_s, b_cbs = T([64, 2])
            stSg = st.enter_context(ExitStack())
            cur[0] = stSg
            cwb, b_cwb = T([4, 4, 1536])
            stc, b_stc = T([4, 3, 1536])
            fw.dma("sp", xs_t[:], xs_d[:, :], writes=[b_xs])
            fw.dma("sp", gmix2[:], gmix_d[:, :], writes=[b_gmix2])
            fw.dma("sp", ropes[:], ropes_d[:, :], writes=[b_ropes])
            fw.dma("sp", ptab[:], ptab_d[:, :], writes=[b_ptab])
            fw.dma("sp", iota_c[:], iota_d[:, :], writes=[b_iota])
            fw.dma("pool", cmpw64[:].rearrange("p a b c -> p (a b c)"), cmpw64_d[:, :], writes=[b_cmpw64])
            fw.dma("pool", pe64[:].rearrange("p a b -> p (a b)"), pe64_d[:, :], writes=[b_pe64])
            fw.dma("sp", c2s2[:].rearrange("p a b -> p (a b)"), c2s_d[:, :], writes=[b_c2s2])
            fw.dma("sp", on511[:], ones511_d[:, :], writes=[b_on511])
            fw.dma("sp", oh4[:], oh4_d[:, :], writes=[b_oh4])
            fw.dma("sp", bonus[:], bonus_d[:, :], writes=[b_bonus])
            fw.dma("sp", alogb[:], alogb_d[:, :], writes=[b_alogb])
            fw.dma("sp", gnrow[:], gnrow_d[:, :], writes=[b_gnrow])
            fw.dma("sp", cwb[:].rearrange("p a b -> p (a b)"), convwb_d[:, :, :].rearrange("p a b -> p (a b)"), writes=[b_cwb])
            fw.dma("sp", stc[:].rearrange("p a b -> p (a b)"), gconv_d[:, :, :].rearrange("p a b -> p (a b)"), writes=[b_stc])
            dve(lambda e: e.tensor_scalar(out=idxs[:], in0=ptab[:], scalar1=128.0, scalar2=iota_c[:, 0:1], op0=ALU.mult, op1=ALU.add),
                r=[b_ptab, b_iota], w=[b_idxs])

            s_sq, b_ssq_ = T([4, 1024], BF16)
            s_ss, b_sss = T([4, 1])
            s_xn, b_sxn = T([4, 1024], BF16)
            xsT, b_xsT = T([128, 8, 4], BF16)
            wch0, b_wch0 = T([128, 8, 480], BF16)
            wch1, b_wch1 = T([128, 8, 480], BF16)
            wch = [wch0, wch1]; b_wch = [b_wch0, b_wch1]
            act(lambda e: e.activation(out=s_sq[:], in_=xs_t[:], func=AF.Square, accum_out=s_ss[:]), r=[b_xs], w=[b_ssq_, b_sss])
            act(lambda e: e.activation(out=s_ss[:], in_=s_ss[:], func=AF.Sqrt, scale=1.0 / 1024, bias=EPS), w=[b_sss])
            dve(lambda e: e.reciprocal(out=s_ss[:], in_=s_ss[:]), w=[b_sss])
            dve(lambda e: e.tensor_scalar(out=s_xn[:], in0=xs_t[:], scalar1=s_ss[:, 0:1], scalar2=None, op0=ALU.mult), r=[b_xs, b_sss], w=[b_sxn])
            for kt in range(8):
                tr(PT[:, kt * 4:(kt + 1) * 4], s_xn[0:4, kt * 128:(kt + 1) * 128], ident_b[0:4, 0:4], r=[b_sxn, b_identb], w=[b_PT])
            for kt in range(8):
                act(lambda e, kt=kt: e.activation(out=xsT[:, kt, :], in_=PT[:, kt * 4:(kt + 1) * 4], func=AF.Copy, scale=gmix2[:, kt:kt + 1]),
                    r=[b_PT, b_gmix2], w=[b_xsT])
            for ch in range(7):
                wb = ch % 2
                fw.dma("pool", wch[wb][:], win_full_d[:, ch * 480:(ch + 1) * 480].rearrange("(k p) c -> p k c", p=128), writes=[b_wch[wb]])
                for kt in range(8):
                    mm(PA[0:4, 0:480], lhsT=xsT[:, kt, :], rhs=wch[wb][:, kt, :], start=(kt == 0), stop=(kt == 7), r=[b_xsT, b_wch[wb]], w=[b_PA])
                act(lambda e, ch=ch: e.copy(out=pjs[:, ch * 480:(ch + 1) * 480], in_=PA[0:4, 0:480]), r=[b_PA], w=[b_pjs])

            chk2('s1')
            qkr, b_qkr = T([4, 14, 64])
            rqk, b_rqk = T([4, 14, 64])
            rts, b_rts = T([4, 4, 14, 32])
            kvv = pjs[:, 512:1280].rearrange("p (b k g d) -> p b k g d", b=3, k=2, g=2)
            pool(lambda e: e.tensor_copy(out=qkr[:, 0:8, :], in_=pjs[:, 0:512].rearrange("p (h d) -> p h d", h=8)), r=[b_pjs], w=[b_qkr])
            for br in range(3):
                pool(lambda e, br=br: e.tensor_copy(out=qkr[:, 8 + 2 * br:10 + 2 * br, :], in_=kvv[:, br, 0, :, :]), r=[b_pjs], w=[b_qkr])
            cs_b = bcast(ropes[:, 0:32], [4, 14, 32], 1)
            sn_b = bcast(ropes[:, 32:64], [4, 14, 32], 1)
            pool(lambda e: e.tensor_tensor(out=rts[:, 0], in0=qkr[:, :, 0:32], in1=cs_b, op=ALU.mult), r=[b_qkr, b_ropes], w=[b_rts])
            pool(lambda e: e.tensor_tensor(out=rts[:, 1], in0=qkr[:, :, 32:64], in1=sn_b, op=ALU.mult), r=[b_qkr, b_ropes], w=[b_rts])
            pool(lambda e: e.tensor_tensor(out=rts[:, 2], in0=qkr[:, :, 32:64], in1=cs_b, op=ALU.mult), r=[b_qkr, b_ropes], w=[b_rts])
            pool(lambda e: e.tensor_tensor(out=rts[:, 3], in0=qkr[:, :, 0:32], in1=sn_b, op=ALU.mult), r=[b_qkr, b_ropes], w=[b_rts])
            pool(lambda e: e.tensor_tensor(out=rqk[:, :, 0:32], in0=rts[:, 0], in1=rts[:, 1], op=ALU.subtract), r=[b_rts], w=[b_rqk])
            pool(lambda e: e.tensor_tensor(out=rqk[:, :, 32:64], in0=rts[:, 2], in1=rts[:, 3], op=ALU.add), r=[b_rts], w=[b_rqk])
            kvs_t, b_kvs = T([4, 4, 2, 64])
            wnew, b_wnew = T([4, 2, 2, 64])
            pool(lambda e: e.tensor_copy(out=kvs_t[:, 0], in_=rqk[:, 8:10, :]), r=[b_rqk], w=[b_kvs])
            pool(lambda e: e.tensor_copy(out=kvs_t[:, 1], in_=kvv[:, 0, 1, :, :]), r=[b_pjs], w=[b_kvs])
            pool(lambda e: e.tensor_copy(out=kvs_t[:, 2], in_=rqk[:, 10:12, :]), r=[b_rqk], w=[b_kvs])
            pool(lambda e: e.tensor_copy(out=kvs_t[:, 3], in_=kvv[:, 1, 1, :, :]), r=[b_pjs], w=[b_kvs])
            pool(lambda e: e.tensor_copy(out=wnew[:, 0], in_=rqk[:, 12:14, :]), r=[b_rqk], w=[b_wnew])
            pool(lambda e: e.tensor_copy(out=wnew[:, 1], in_=kvv[:, 2, 1, :, :]), r=[b_pjs], w=[b_wnew])
            fw.dma("sp", kvs_o[:, :], kvs_t[:].rearrange("p a b c -> p (a b c)"), reads=[b_kvs])
            fw.dma("sp", wins_o[:, 511, :], wnew[:].rearrange("p a b c -> p (a b c)"), reads=[b_wnew])
            for s_ in range(4):
                fw.dma("sp", wins_o[s_, 0:511, :], wincache_d[s_, 1:512, :])
            fw.dma("sp", convs_o[:, 0:2, :], gconv_d[:, 1:3, :])
            fw.dma("sp", convs_o[:, 2, :], pjs[:, 1304:2840], reads=[b_pjs])
            chk2('s2')
            qkb_s, b_qkbs = T([4, 14, 64], BF16)
            act(lambda e: e.copy(out=qkb_s[:], in_=rqk[:]), r=[b_rqk], w=[b_qkbs])
            for hd in range(14):
                tr(PT[0:64, hd * 4:(hd + 1) * 4], qkb_s[0:4, hd, :], ident_b[0:4, 0:4], r=[b_qkbs, b_identb], w=[b_PT])
            act(lambda e: e.copy(out=QKT[0:64].rearrange("p a b -> p (a b)"), in_=PT[0:64, 0:56]), r=[b_PT], w=[b_QKT])
            fw.dma("sp", QKT[64:128].rearrange("p a b -> p (a b)"), QKT[0:64].rearrange("p a b -> p (a b)"), reads=[b_QKT], writes=[b_QKT])

            chk2('s3')
            for kv in range(2):
                for l in range(32):
                    mm(PD[0:64, kv:kv + 1], lhsT=cmpw64[0:64, kv, l, :], rhs=pe64[0:64, kv, l:l + 1], start=(l == 0), stop=(l == 31),
                       r=[b_cmpw64, b_pe64], w=[b_PD])
            act(lambda e: e.copy(out=cb_s[:], in_=PD[0:64, 0:2]), r=[b_PD], w=[b_cbs])

            chk2('s3b')
            tmpc, b_tmpc = T([4, 1536])
            caccs, b_caccs = T([4, 1536])
            sqs, b_sqs = T([4, 1024])
            rn8, b_rn8 = T([4, 8])
            gsm, b_gsm = T([4, 8])
            dve(lambda e: e.tensor_tensor(out=caccs[:], in0=stc[:, 0, :], in1=cwb[:, 0, :], op=ALU.mult), r=[b_stc, b_cwb], w=[b_caccs])
            for jj in range(1, 4):
                src = stc[:, jj, :] if jj < 3 else pjs[:, 1304:2840]
                dve(lambda e, jj=jj, src=src: e.tensor_tensor(out=tmpc[:], in0=src, in1=cwb[:, jj, :], op=ALU.mult),
                    r=[b_stc, b_cwb, b_pjs], w=[b_tmpc])
                dve(lambda e: e.tensor_tensor(out=caccs[:], in0=caccs[:], in1=tmpc[:], op=ALU.add), r=[b_tmpc], w=[b_caccs])
            act(lambda e: e.activation(out=Xs[:, 0:1536], in_=caccs[:], func=AF.Silu), r=[b_caccs], w=[b_Xs])
            dve(lambda e: e.tensor_tensor(out=sqs[:], in0=Xs[:, 0:1024], in1=Xs[:, 0:1024], op=ALU.mult), r=[b_Xs], w=[b_sqs])
            dve(lambda e: e.tensor_reduce(out=rn8[:], in_=sqs[:].rearrange("p (h d) -> p h d", h=8), axis=AX.X, op=ALU.add), r=[b_sqs], w=[b_rn8])
            act(lambda e: e.activation(out=rn8[:], in_=rn8[:], func=AF.Sqrt, bias=EPS), w=[b_rn8])
            dve(lambda e: e.reciprocal(out=rn8[:], in_=rn8[:]), w=[b_rn8])
            dve(lambda e: e.tensor_scalar(out=rn8[:, 0:4], in0=rn8[:, 0:4], scalar1=128.0 ** -0.5, scalar2=None, op0=ALU.mult), w=[b_rn8])
            dve(lambda e: e.tensor_tensor(out=Xs[:, 0:1024].rearrange("p (h d) -> p h d", h=8), in0=Xs[:, 0:1024].rearrange("p (h d) -> p h d", h=8),
                                          in1=bcast(rn8[:], [4, 8, 128], 2), op=ALU.mult), r=[b_rn8], w=[b_Xs])
            act(lambda e: e.activation(out=Xs[:, 1536:2048], in_=pjs[:, 2840:3352], func=AF.Silu), r=[b_pjs], w=[b_Xs])
            act(lambda e: e.activation(out=Xs[:, 2048:2052], in_=pjs[:, 3356:3360], func=AF.Sigmoid), r=[b_pjs], w=[b_Xs])
            dve(lambda e: e.tensor_tensor(out=gsm[:, 0:4], in0=pjs[:, 3352:3356], in1=alogb[:, 4:8], op=ALU.add), r=[b_pjs, b_alogb], w=[b_gsm])
            act(lambda e: e.activation(out=gsm[:, 0:4], in_=gsm[:, 0:4], func=AF.Exp), w=[b_gsm])
            act(lambda e: e.activation(out=gsm[:, 0:4], in_=gsm[:, 0:4], func=AF.Ln, bias=1.0), w=[b_gsm])
            act(lambda e: e.activation(out=gsm[:, 4:8], in_=alogb[:, 0:4], func=AF.Exp), r=[b_alogb], w=[b_gsm])
            dve(lambda e: e.tensor_tensor(out=gsm[:, 0:4], in0=gsm[:, 0:4], in1=gsm[:, 4:8], op=ALU.mult), w=[b_gsm])
            act(lambda e: e.activation(out=Xs[:, 2052:2056], in_=gsm[:, 0:4], func=AF.Exp, scale=-1.0), r=[b_gsm], w=[b_Xs])

            chk2('s4')
            Rrow, b_Rrow = T([1, 2056])
            cols_s, b_cols = T([128, 8])
            S_t = [T([128, 128]) for _ in range(2)]
            r1, b_r1 = T([1, 257])
            vn, b_vn = T([1, 128])
            og, b_og = T([1, 128])
            ogs, b_ogs = T([1, 4])
            ogq, b_ogq = T([1, 128])
            egb, b_egb = T([128, 1])
            Snew = [T([128, 128]) for _ in range(2)]
            pool(lambda e: e.memset(one11[:], 1.0), w=[b_one11])
            for s_ in range(4):
                for chn, (c0, c1) in enumerate([(0, 512), (512, 1024), (1024, 1536), (1536, 2048), (2048, 2056)]):
                    mm(PD[0:1, 0:c1 - c0], lhsT=ident_f[0:4, s_:s_ + 1], rhs=Xs[0:4, c0:c1], r=[b_identf, b_Xs], w=[b_PD])
                    act(lambda e, c0=c0, c1=c1: e.copy(out=Rrow[0:1, c0:c1], in_=PD[0:1, 0:c1 - c0]), r=[b_PD], w=[b_Rrow])
                for hq in range(8):
                    mm(PD[:, hq:hq + 1], lhsT=Rrow[0:1, hq * 128:(hq + 1) * 128], rhs=one11[:], r=[b_Rrow, b_one11], w=[b_PD])
                act(lambda e: e.copy(out=cols_s[:], in_=PD[:, 0:8]), r=[b_PD], w=[b_cols])
                for h in range(4):
                    sbi = (s_ * 4 + h) % 2
                    St, bSt = S_t[sbi]
                    Sn, bSn = Snew[sbi]
                    fw.dma("sp", St[:], gS_d[s_ * 4 + h, :, :], writes=[bSt])
                    mm(PD[0:1, 0:128], lhsT=cols_s[:, 4 + h:5 + h], rhs=St[:], r=[b_cols, bSt], w=[b_PD])
                    mm(PD[0:1, 128:256], lhsT=cols_s[:, h:h + 1], rhs=St[:], r=[b_cols, bSt], w=[b_PD])
                    mm(PD[0:1, 256:257], lhsT=cols_s[:, h:h + 1], rhs=cols_s[:, 4 + h:5 + h], r=[b_cols], w=[b_PD])
                    act(lambda e: e.copy(out=r1[:], in_=PD[0:1, 0:257]), r=[b_PD], w=[b_r1])
                    egs = Rrow[0:1, 2052 + h:2053 + h]
                    bts = Rrow[0:1, 2048 + h:2049 + h]
                    dve(lambda e, egs=egs: e.tensor_scalar(out=vn[:], in0=r1[0:1, 0:128], scalar1=egs, scalar2=None, op0=ALU.mult),
                        r=[b_r1, b_Rrow], w=[b_vn])
                    dve(lambda e, h=h: e.tensor_tensor(out=vn[:], in0=Rrow[0:1, 1024 + h * 128:1024 + (h + 1) * 128], in1=vn[:], op=ALU.subtract),
                        r=[b_Rrow], w=[b_vn])
                    dve(lambda e, bts=bts: e.tensor_scalar(out=vn[:], in0=vn[:], scalar1=bts, scalar2=None, op0=ALU.mult), r=[b_Rrow], w=[b_vn])
                    dve(lambda e, egs=egs: e.tensor_scalar(out=og[:], in0=r1[0:1, 128:256], scalar1=egs, scalar2=None, op0=ALU.mult),
                        r=[b_r1, b_Rrow], w=[b_og])
                    dve(lambda e: e.scalar_tensor_tensor(out=og[:], in0=vn[:], scalar=r1[0:1, 256:257], in1=og[:], op0=ALU.mult, op1=ALU.add),
                        r=[b_vn, b_r1], w=[b_og])
                    act(lambda e: e.activation(out=ogq[:], in_=og[:], func=AF.Square, accum_out=ogs[0:1, 0:1]), r=[b_og], w=[b_ogq, b_ogs])
                    act(lambda e: e.activation(out=ogs[0:1, 0:1], in_=ogs[0:1, 0:1], func=AF.Sqrt, scale=1.0 / 128, bias=EPS), w=[b_ogs])
                    dve(lambda e: e.reciprocal(out=ogs[0:1, 0:1], in_=ogs[0:1, 0:1]), w=[b_ogs])
                    dve(lambda e: e.scalar_tensor_tensor(out=og[:], in0=og[:], scalar=ogs[0:1, 0:1], in1=gnrow[:], op0=ALU.mult, op1=ALU.mult),
                        r=[b_ogs, b_gnrow], w=[b_og])
                    dve(lambda e, h=h: e.tensor_tensor(out=og[:], in0=og[:], in1=Rrow[0:1, 1536 + h * 128:1536 + (h + 1) * 128], op=ALU.mult),
                        r=[b_Rrow], w=[b_og])
                    mm(PD[:, 300:301], lhsT=og[:], rhs=one11[:], r=[b_og, b_one11], w=[b_PD])
                    act(lambda e, h=h, s_=s_: e.copy(out=OGT[:, h, s_:s_ + 1], in_=PD[:, 300:301]), r=[b_PD], w=[b_OGT])
                    mm(PB[:, 0:128], lhsT=Rrow[0:1, 512 + h * 128:512 + (h + 1) * 128], rhs=vn[:], r=[b_Rrow, b_vn], w=[b_PB])
                    mm(PD[:, 310:311], lhsT=ones_f2[0:1, :], rhs=egs, r=[b_onesf2, b_Rrow], w=[b_PD])
                    act(lambda e: e.copy(out=egb[:], in_=PD[:, 310:311]), r=[b_PD], w=[b_egb])
                    dve(lambda e, St=St, Sn=Sn: e.scalar_tensor_tensor(out=Sn[:], in0=St[:], scalar=egb[:, 0:1], in1=PB[:, 0:128],
                                                                    op0=ALU.mult, op1=ALU.add), r=[bSt, b_egb, b_PB], w=[bSn])
                    fw.dma("sp", Ss_o[s_ * 4 + h, :, :], Sn[:], reads=[bSn])

            chk2('s5')
            fw.barrier()
            stSg.close()
            stSn = st.enter_context(ExitStack())
            cur[0] = stSn
            woutn, b_woutn = T([64, 8, 1024], BF16)
            woutg, b_woutg = T([128, 4, 1024], BF16)
            eind64, b_eind64 = T([128, 8192], BF16)
            fw.dma("pool", woutn[:].rearrange("p a b -> p (a b)"), woutn_d[:, :], writes=[b_woutn])
            fw.dma("pool", woutg[:].rearrange("p a b -> p (a b)"), woutg_d[:, :], writes=[b_woutg])
            fw.dma("pool", eind64[0:64, :], eind_s_d[:, :], writes=[b_eind64])
            fw.dma("pool", eind64[64:128, :], eind_s_d[:, :], writes=[b_eind64])
            KTs, b_KTs = T([128, 3, 8192], BF16)
            Vs, b_Vs = T([128, 64, 2, 65], BF16)
            pg = [T([128, 512]) for _ in range(3)]
            pgb = [T([128, 384], BF16) for _ in range(2)]
            ckTs, b_ckTs = T([64, 512], BF16)
            cvTs, b_cvTs = T([64, 512], BF16)
            cvxs, b_cvxs = T([128, 4, 193], BF16)
            Pc, b_Pc = T([128, 16], BF16)
            accs, b_accs = T([4, 193])
            rcs, b_rcs = T([4, 4])
            impn, b_impn = T([4, 128])
            scs, b_scs = T([1, 136])
            sc2s, b_sc2s = T([1, 136])
            mx8s, b_mx8s = T([1, 16])
            thrs, b_thrs = T([1, 1])
            mbrow, b_mbrow = T([1, 128])
            mbrow2, b_mbrow2 = T([1, 2, 128])
            mbc, b_mbc = T([128, 2])
            MBq, b_MBq = T([128, 2, 4], BF16)
            Psel, b_Psel = T([128, 256], BF16)
            pnew, b_pnew = T([4, 2])
            vrow, b_vrow = T([4, 128])
            Abr, b_Abr = T([4, 3, 8, 64])
            asel, b_asel = T([4, 2, 65])
            wc, b_wc = T([128, 4, 256])
            wcb, b_wcb = T([128, 4, 128], BF16)
            Vws, b_Vws = T([128, 4, 2, 65], BF16)
            KwTs, b_KwTs = T([128, 4, 128], BF16)
            Pw, b_Pw = T([128, 16], BF16)
            pool(lambda e: e.memset(Vs[:, :, :, 64:65], 1.0), w=[b_Vs])
            pool(lambda e: e.memset(Vws[:, :, :, 64:65], 1.0), w=[b_Vws])
            pool(lambda e: e.memset(ckTs[:], 0.0), w=[b_ckTs])
            pool(lambda e: e.memset(cvTs[:], 0.0), w=[b_cvTs])
            pool(lambda e: e.memset(cvxs[:], 0.0), w=[b_cvxs])
            pool(lambda e: e.tensor_copy(out=cvxs[:, :, 64:192], in_=c2s2[:]), r=[b_c2s2], w=[b_cvxs])
            pool(lambda e: e.tensor_copy(out=cvxs[:, :, 192], in_=on511[:]), r=[b_on511], w=[b_cvxs])
            pool(lambda e: e.memset(scs[:], 1e4), w=[b_scs])
            for s_ in range(4):
                for p_ in range(64):
                    pgt, bpg = pg[p_ % 3]
                    pgbt, bpgb = pgb[p_ % 2]
                    col = s_ * 64 + p_
                    fw.dma("pool", None, None, reads=[b_idxs], writes=[bpg],
                           fn=lambda e, pgt=pgt, col=col: e.indirect_dma_start(
                               out=pgt[:], out_offset=None, in_=cache_d[:, :],
                               in_offset=bass.IndirectOffsetOnAxis(ap=idxs[:, col:col + 1], axis=0)))
                    dve(lambda e, pgt=pgt, pgbt=pgbt: e.tensor_copy(out=pgbt[:], in_=pgt[:, 0:384]), r=[bpg], w=[bpgb])
                    act(lambda e, pgt=pgt, p_=p_: e.copy(out=Vs[:, p_, :, 0:64], in_=pgt[:, 384:512].rearrange("p (g d) -> p g d", g=2)),
                        r=[bpg], w=[b_Vs])
                    for kg in range(3):
                        tr(PT[:, kg * 128:(kg + 1) * 128], pgbt[:, kg * 128:(kg + 1) * 128], ident_b[:], r=[bpgb, b_identb], w=[b_PT])
                    act(lambda e, p_=p_: e.copy(out=KTs[:, :, p_ * 128:(p_ + 1) * 128], in_=PT[:, 0:384].rearrange("p (a t) -> p a t", a=3)),
                        r=[b_PT], w=[b_KTs])
                chk2('s6')
                fw.dma("sp", wc[:], wincache_d[s_, :, :].rearrange("(t p) c -> p t c", p=128), writes=[b_wc])
                pool(lambda e: e.tensor_copy(out=wcb[:], in_=wc[:, :, 0:128]), r=[b_wc], w=[b_wcb])
                pool(lambda e: e.tensor_copy(out=Vws[:, :, :, 0:64], in_=wc[:, :, 128:256].rearrange("p t (g d) -> p t g d", g=2)),
                     r=[b_wc], w=[b_Vws])
                for t_ in range(4):
                    tr(PT[:, t_ * 128:(t_ + 1) * 128], wcb[:, t_, :], ident_b[:], r=[b_wcb, b_identb], w=[b_PT])
                act(lambda e: e.copy(out=KwTs[:].rearrange("p a b -> p (a b)"), in_=PT[:, 0:512]), r=[b_PT], w=[b_KwTs])
                chk2('s7')
                for g_ in range(2):
                    sg = s_ * 2 + g_
                    Qg = QKT[0:64, 4 * g_:4 * g_ + 4, s_]
                    g0_, g1_ = g_ * 64, (g_ + 1) * 64
                    Qgg = QKT[g0_:g1_, 4 * g_:4 * g_ + 4, s_]
                    for kv in range(2):
                        for l in range(32):
                            mm(PA[0:64, 0:511], lhsT=cmpw64[g0_:g1_, kv, l, :], rhs=KTs[g0_:g1_, kv, l:l + 16 * 510 + 1:16],
                               start=(l == 0), stop=(l == 31), r=[b_cmpw64, b_KTs], w=[b_PA])
                        dst = ckTs if kv == 0 else cvTs
                        bd = b_ckTs if kv == 0 else b_cvTs
                        act(lambda e, dst=dst, kv=kv: e.activation(out=dst[:, 0:511], in_=PA[0:64, 0:511], func=AF.Identity, bias=cb_s[:, kv:kv + 1]),
                            r=[b_PA, b_cbs], w=[bd])
                    for jt in range(4):
                        tr(PT[:, jt * 64:(jt + 1) * 64], cvTs[:, jt * 128:(jt + 1) * 128], ident_b[0:64, 0:64], r=[b_cvTs, b_identb], w=[b_PT])
                    act(lambda e: e.copy(out=cvxs[:, :, 0:64], in_=PT[:, 0:256].rearrange("p (a d) -> p a d", a=4)), r=[b_PT], w=[b_cvxs])
                    chk2('s8')
                    for jt in range(4):
                        mm(PD[:, jt * 4:(jt + 1) * 4], lhsT=ckTs[:, jt * 128:(jt + 1) * 128], rhs=Qg, r=[b_ckTs, b_QKT], w=[b_PD])
                    act(lambda e: e.activation(out=Pc[:], in_=PD[:, 0:16], func=AF.Exp, scale=0.125), r=[b_PD], w=[b_Pc])
                    for jt in range(4):
                        mm(PB[0:4, 0:193], lhsT=Pc[:, jt * 4:(jt + 1) * 4], rhs=cvxs[:, jt, :], start=(jt == 0), stop=(jt == 3),
                           r=[b_Pc, b_cvxs], w=[b_PB])
                    act(lambda e: e.copy(out=accs[:], in_=PB[0:4, 0:193]), r=[b_PB], w=[b_accs])
                    dve(lambda e: e.tensor_scalar(out=rcs[:, 0:1], in0=accs[:, 192:193], scalar1=1e-30, scalar2=None, op0=ALU.max), r=[b_accs], w=[b_rcs])
                    dve(lambda e: e.reciprocal(out=rcs[:, 0:1], in_=rcs[:, 0:1]), w=[b_rcs])
                    dve(lambda e, sg=sg: e.tensor_scalar(out=Abr[:, 0, sg, :], in0=accs[:, 0:64], scalar1=rcs[:, 0:1], scalar2=None, op0=ALU.mult),
                        r=[b_accs, b_rcs], w=[b_Abr])
                    dve(lambda e: e.tensor_scalar(out=impn[:], in0=accs[:, 64:192], scalar1=rcs[:, 0:1], scalar2=None, op0=ALU.mult),
                        r=[b_accs, b_rcs], w=[b_impn])
                    mm(PD[0:1, 64:192], lhsT=ones_f2[0:4, 0:1], rhs=impn[:], r=[b_onesf2, b_impn], w=[b_PD])
                    chk2('s8b')
                    dve(lambda e: e.tensor_tensor(out=scs[0:1, 0:128], in0=PD[0:1, 64:192], in1=bonus[:], op=ALU.add), r=[b_PD, b_bonus], w=[b_scs])
                    dve(lambda e: e.max(out=mx8s[:, 0:8], in_=scs[0:1, 0:129]), r=[b_scs], w=[b_mx8s])
                    dve(lambda e: e.match_replace(out=sc2s[0:1, 0:129], in_to_replace=mx8s[:, 0:8], in_values=scs[0:1, 0:129], imm_value=-3e38),
                        r=[b_scs, b_mx8s], w=[b_sc2s])
                    dve(lambda e: e.max(out=mx8s[:, 8:16], in_=sc2s[0:1, 0:129]), r=[b_sc2s], w=[b_mx8s])
                    dve(lambda e: e.tensor_reduce(out=thrs[:], in_=mx8s[:, 8:16], axis=AX.X, op=ALU.min), r=[b_mx8s], w=[b_thrs])
                    dve(lambda e: e.tensor_scalar(out=mbrow[:], in0=scs[0:1, 0:128], scalar1=thrs[0:1, 0:1], scalar2=None, op0=ALU.is_ge),
                        r=[b_scs, b_thrs], w=[b_mbrow])
                    dve(lambda e: e.tensor_scalar(out=mbrow[:], in0=mbrow[:], scalar1=-NEGB, scalar2=NEGB, op0=ALU.mult, op1=ALU.add), w=[b_mbrow])
                    for a_ in range(2):
                        for dp_ in range(2):
                            dve(lambda e, a_=a_, dp_=dp_: e.tensor_copy(out=mbrow2[0:1, a_, dp_ * 64:(dp_ + 1) * 64], in_=mbrow[0:1, a_ * 64:(a_ + 1) * 64]),
                                r=[b_mbrow], w=[b_mbrow2])
                    for a_ in range(2):
                        mm(PD[:, 200 + a_:201 + a_], lhsT=mbrow2[0:1, a_, :], rhs=one11[:], r=[b_mbrow2, b_one11], w=[b_PD])
                    act(lambda e: e.copy(out=mbc[:], in_=PD[:, 200:202]), r=[b_PD], w=[b_mbc])
                    for a_ in range(2):
                        dve(lambda e, a_=a_: e.tensor_scalar(out=MBq[:, a_, :], in0=ones_f2[:, 0:4], scalar1=mbc[:, a_:a_ + 1], scalar2=None, op0=ALU.mult),
                            r=[b_onesf2, b_mbc], w=[b_MBq])
                    chk2('s9')
                    for t_ in range(64):
                        a_ = 0 if t_ < 32 else 1
                        mm(PA[:, 512 + t_ * 4:512 + (t_ + 1) * 4], lhsT=KTs[g0_:g1_, 2, t_ * 128:(t_ + 1) * 128], rhs=Qgg, start=True, stop=False,
                           r=[b_KTs, b_QKT], w=[b_PA])
                        mm(PA[:, 512 + t_ * 4:512 + (t_ + 1) * 4], lhsT=eind64[g0_:g1_, t_ * 128:(t_ + 1) * 128], rhs=MBq[g0_:g1_, a_, :], start=False, stop=True,
                           r=[b_eind64, b_MBq], w=[b_PA])
                    act(lambda e: e.activation(out=Psel[:], in_=PA[:, 512:768], func=AF.Exp, scale=0.125), r=[b_PA], w=[b_Psel])
                    for t_ in range(64):
                        mm(PB[0:4, 256:321], lhsT=Psel[:, t_ * 4:(t_ + 1) * 4], rhs=Vs[:, t_, g_, :], start=(t_ == 0), stop=(t_ == 63),
                           r=[b_Psel, b_Vs], w=[b_PB])
                    chk2('s10')
                    mm(PD[0:4, 210:211], lhsT=Qg, rhs=QKT[0:64, 10 + g_, s_:s_ + 1], r=[b_QKT], w=[b_PD])
                    mm(PD[0:4, 211:212], lhsT=Qg, rhs=QKT[0:64, 12 + g_, s_:s_ + 1], r=[b_QKT], w=[b_PD])
                    act(lambda e: e.activation(out=pnew[:], in_=PD[0:4, 210:212], func=AF.Exp, scale=0.125), r=[b_PD], w=[b_pnew])
                    mm(PD[0:4, 220:284], lhsT=oh4[0:4, s_ * 4:(s_ + 1) * 4], rhs=kvv[:, 1, 1, g_, :], r=[b_oh4, b_pjs], w=[b_PD])
                    mm(PD[0:4, 284:348], lhsT=oh4[0:4, s_ * 4:(s_ + 1) * 4], rhs=kvv[:, 2, 1, g_, :], r=[b_oh4, b_pjs], w=[b_PD])
                    act(lambda e: e.copy(out=vrow[:], in_=PD[0:4, 220:348]), r=[b_PD], w=[b_vrow])
                    dve(lambda e: e.scalar_tensor_tensor(out=asel[:, 0, 0:64], in0=vrow[:, 0:64], scalar=pnew[:, 0:1], in1=PB[0:4, 256:320],
                                                         op0=ALU.mult, op1=ALU.add), r=[b_vrow, b_pnew, b_PB], w=[b_asel])
                    dve(lambda e: e.tensor_tensor(out=asel[:, 0, 64:65], in0=PB[0:4, 320:321], in1=pnew[:, 0:1], op=ALU.add),
                        r=[b_PB, b_pnew], w=[b_asel])
                    chk2('s10b')
                    for t_ in range(4):
                        mm(PD[:, 352 + t_ * 4:356 + t_ * 4], lhsT=KwTs[g0_:g1_, t_, :], rhs=Qgg, r=[b_KwTs, b_QKT], w=[b_PD])
                    act(lambda e: e.activation(out=Pw[:], in_=PD[:, 352:368], func=AF.Exp, scale=0.125), r=[b_PD], w=[b_Pw])
                    for t_ in range(4):
                        mm(PB[0:4, 384:449], lhsT=Pw[:, t_ * 4:(t_ + 1) * 4], rhs=Vws[:, t_, g_, :], start=(t_ == 0), stop=(t_ == 3),
                           r=[b_Pw, b_Vws], w=[b_PB])
                    dve(lambda e: e.scalar_tensor_tensor(out=asel[:, 1, 0:64], in0=vrow[:, 64:128], scalar=pnew[:, 1:2], in1=PB[0:4, 384:448],
                                                         op0=ALU.mult, op1=ALU.add), r=[b_vrow, b_pnew, b_PB], w=[b_asel])
                    dve(lambda e: e.tensor_tensor(out=asel[:, 1, 64:65], in0=PB[0:4, 448:449], in1=pnew[:, 1:2], op=ALU.add),
                        r=[b_PB, b_pnew], w=[b_asel])
                    dve(lambda e: e.reciprocal(out=rcs[:, 1:3], in_=asel[:, :, 64]), r=[b_asel], w=[b_rcs])
                    for br in range(2):
                        dve(lambda e, br=br, sg=sg: e.tensor_scalar(out=Abr[:, 1 + br, sg, :], in0=asel[:, br, 0:64], scalar1=rcs[:, 1 + br:2 + br],
                                                                    scalar2=None, op0=ALU.mult), r=[b_asel, b_rcs], w=[b_Abr])

            chk2('s11')
            gts, b_gts = T([4, 8, 3])
            osum, b_osum = T([4, 8, 64])
            otmp, b_otmp = T([4, 8, 64])
            onb, b_onb = T([4, 8, 64], BF16)
            OT, b_OT = T([64, 8, 4], BF16)
            act(lambda e: e.activation(out=gts[:].rearrange("p a b -> p (a b)"), in_=pjs[:, 1280:1304], func=AF.Sigmoid), r=[b_pjs], w=[b_gts])
            for br in range(3):
                for g_ in range(2):
                    for r_ in range(4):
                        h_ = 4 * g_ + r_
                        for s_ in range(4):
                            mm(PC[0:4, h_ * 64:(h_ + 1) * 64], lhsT=oh4[0:4, 16 + (r_ * 4 + s_) * 4:16 + (r_ * 4 + s_ + 1) * 4],
                               rhs=Abr[:, br, s_ * 2 + g_, :], start=(s_ == 0), stop=(s_ == 3), r=[b_oh4, b_Abr], w=[b_PC])
                gb = bcast(gts[:, :, br], [4, 8, 64], 2)
                if br == 0:
                    dve(lambda e, gb=gb: e.tensor_tensor(out=osum[:], in0=PC[0:4, 0:512].rearrange("p (h d) -> p h d", h=8), in1=gb, op=ALU.mult),
                        r=[b_PC, b_gts], w=[b_osum])
                else:
                    dve(lambda e, gb=gb: e.tensor_tensor(out=otmp[:], in0=PC[0:4, 0:512].rearrange("p (h d) -> p h d", h=8), in1=gb, op=ALU.mult),
                        r=[b_PC, b_gts], w=[b_otmp])
                    dve(lambda e: e.tensor_tensor(out=osum[:], in0=osum[:], in1=otmp[:], op=ALU.add), r=[b_otmp], w=[b_osum])
            act(lambda e: e.copy(out=onb[:], in_=osum[:]), r=[b_osum], w=[b_onb])
            for h_ in range(8):
                tr(PT[0:64, h_ * 4:(h_ + 1) * 4], onb[0:4, h_, :], ident_b[0:4, 0:4], r=[b_onb, b_identb], w=[b_PT])
            act(lambda e: e.copy(out=OT[:].rearrange("p a b -> p (a b)"), in_=PT[0:64, 0:32]), r=[b_PT], w=[b_OT])
            chk2('s12')
            for half in range(2):
                for h_ in range(8):
                    mm(PA[0:4, half * 512:(half + 1) * 512], lhsT=OT[:, h_, :], rhs=woutn[:, h_, half * 512:(half + 1) * 512],
                       start=(h_ == 0), stop=False, r=[b_OT, b_woutn], w=[b_PA])
                for h_ in range(4):
                    mm(PA[0:4, half * 512:(half + 1) * 512], lhsT=OGT[:, h_, :], rhs=woutg[:, h_, half * 512:(half + 1) * 512],
                       start=False, stop=(h_ == 3), r=[b_OGT, b_woutg], w=[b_PA])
            dve(lambda e: e.tensor_tensor(out=hs_res[:], in0=PA[0:4, :], in1=xs_t[:], op=ALU.add), r=[b_PA, b_xs], w=[b_hsres])
            fw.barrier()
            stSn.close()
            stS.close()
            stB = st.enter_context(ExitStack())
            cur[0] = stB
            wout = sb("wout", [128, 8, 1024], BF16); b_wout = Buf()
            wdn = sb("wdn", [128, 22, 1024], BF16); b_wdn = Buf()
            nfb = sb("nfb", [128, 1024]); b_nfb = Buf()
            gffn = sb("gffn", [128, 8]); b_gffn = Buf()
            sel4 = sb("sel4_sb", [128, 4]); b_sel4 = Buf()
            cand = [sb("cand%d" % i, [128, 8, 512], BF16) for i in range(2)]; b_cand = [Buf() for _ in range(2)]
            mixT = sb("mixT", [128, 8, 512], BF16); b_mixT = Buf()
            xt2 = [sb("xt2_%d" % i, [128, 1024]) for i in range(2)]; b_xt2 = [Buf() for _ in range(2)]
            hres = sb("hres", [128, 4, 1024]); b_hres = [Buf() for _ in range(4)]
            hsq = sb("hsq", [128, 1024], BF16); b_hsq = Buf()
            hss = sb("hss", [128, 1]); b_hss = Buf()
            hs = sb("hs", [128, 1024], BF16); b_hs = Buf()
            hnT = sb("hnT", [128, 8, 512], BF16); b_hnT = Buf()
            wg = [sb("wg%d" % i, [128, 8, 256], BF16) for i in range(3)]; b_wg = [Buf() for _ in range(3)]
            sg = [sb("sg%d" % i, [128, 512]) for i in range(2)]; b_sg = [Buf() for _ in range(2)]
            actT = sb("actT", [128, 22, 512], BF16); b_actT = Buf()
            yb = [sb("yb%d" % i, [128, 1024]) for i in range(2)]; b_yb = [Buf() for _ in range(2)]
            ysq = sb("ysq", [128, 1024], BF16); b_ysq = Buf()
            yss = sb("yss", [128, 1]); b_yss = Buf()
            hsn_ss = sb("hsn_ss", [4, 1]); b_hsnss = Buf()
            hsn_sq = sb("hsn_sq", [4, 1024], BF16); b_hsnsq = Buf()
            hsn = sb("hsn", [4, 1024], BF16); b_hsn = Buf()
            hnTs = sb("hnTs", [128, 8, 4], BF16); b_hnTs = Buf()
            sgs = sb("sgs", [128, 4]); b_sgs = Buf()
            actTs = sb("actTs", [128, 22, 4], BF16); b_actTs = Buf()
            ysb = sb("ysb", [4, 1024]); b_ysb = Buf()
            fw.dma("sp", nfb[:], nfin_d[:, :], writes=[b_nfb])
            fw.dma("sp", gffn[:], gffn_d[:, :], writes=[b_gffn])
            fw.dma("sp", sel4[:], sel4_d[:, :], writes=[b_sel4])
            fw.dma("pool", wout[:], wout_d[:, :].rearrange("(k p) c -> p k c", p=128), writes=[b_wout])
            fw.dma("sp", wdn[:], wdn_s[:, :, :].rearrange("f p c -> p f c"), reads=[b_wdns], writes=[b_wdn])
            act(lambda e: e.activation(out=hsn_sq[:], in_=hs_res[:], func=AF.Square, accum_out=hsn_ss[:]), r=[b_hsres], w=[b_hsnsq, b_hsnss])
            act(lambda e: e.activation(out=hsn_ss[:], in_=hsn_ss[:], func=AF.Sqrt, scale=1.0 / 1024, bias=EPS), w=[b_hsnss])
            dve(lambda e: e.reciprocal(out=hsn_ss[:], in_=hsn_ss[:]), w=[b_hsnss])
            dve(lambda e: e.tensor_scalar(out=hsn[:], in0=hs_res[:], scalar1=hsn_ss[:, 0:1], scalar2=None, op0=ALU.mult), r=[b_hsres, b_hsnss], w=[b_hsn])
            for kt in range(8):
                tr(PT[:, kt * 4:(kt + 1) * 4], hsn[0:4, kt * 128:(kt + 1) * 128], ident_b[0:4, 0:4], r=[b_hsn, b_identb], w=[b_PT])
            for kt in range(8):
                act(lambda e, kt=kt: e.activation(out=hnTs[:, kt, :], in_=PT[:, kt * 4:(kt + 1) * 4], func=AF.Copy, scale=gffn[:, kt:kt + 1]),
                    r=[b_PT, b_gffn], w=[b_hnTs])
            wgi = 0
            for bi in range(NB):
                for j4 in range(4):
                    cb = j4 % 2
                    c0 = j4 * TB + bi * 512
                    fw.dma("sp", cand[cb][:], xout[c0 // CH][:, c0 % CH:c0 % CH + 512].rearrange("(k p) t -> p k t", p=128),
                           reads=[b_xout], writes=[b_cand[cb]])
                    if j4 == 0:
                        dve(lambda e, cb=cb: e.tensor_scalar(out=mixT[:], in0=cand[cb][:], scalar1=sel4[:, 0:1], scalar2=None, op0=ALU.mult),
                            r=[b_cand[cb], b_sel4], w=[b_mixT])
                    else:
                        dve(lambda e, cb=cb, j4=j4: e.scalar_tensor_tensor(out=mixT[:], in0=cand[cb][:], scalar=sel4[:, j4:j4 + 1], in1=mixT[:],
                                                                           op0=ALU.mult, op1=ALU.add), r=[b_cand[cb], b_sel4], w=[b_mixT])
                for tt in range(4):
                    r0 = bi * 512 + tt * 128
                    xs_ = tt % 2
                    fw.dma("sp", xt2[xs_][:], xown_d[r0:r0 + 128, :], writes=[b_xt2[xs_]])
                    for half in range(2):
                        for kt in range(8):
                            mm(PA[:, half * 512:(half + 1) * 512], lhsT=mixT[:, kt, tt * 128:(tt + 1) * 128],
                               rhs=wout[:, kt, half * 512:(half + 1) * 512], start=(kt == 0), stop=(kt == 7),
                               r=[b_mixT, b_wout], w=[b_PA])
                    dve(lambda e, tt=tt, xs_=xs_: e.tensor_tensor(out=hres[:, tt, :], in0=PA[:, :], in1=xt2[xs_][:], op=ALU.add),
                        r=[b_PA, b_xt2[xs_]], w=[b_hres[tt]])
                    act(lambda e, tt=tt: e.activation(out=hsq[:], in_=hres[:, tt, :], func=AF.Square, accum_out=hss[:]),
                        r=[b_hres[tt]], w=[b_hsq, b_hss])
                    act(lambda e: e.activation(out=hss[:], in_=hss[:], func=AF.Sqrt, scale=1.0 / 1024, bias=EPS), w=[b_hss])
                    dve(lambda e: e.reciprocal(out=hss[:], in_=hss[:]), w=[b_hss])
                    dve(lambda e, tt=tt: e.tensor_scalar(out=hs[:], in0=hres[:, tt, :], scalar1=hss[:, 0:1], scalar2=None, op0=ALU.mult),
                        r=[b_hres[tt], b_hss], w=[b_hs])
                    for kt in range(8):
                        tr(PT[:, kt * 128:(kt + 1) * 128], hs[:, kt * 128:(kt + 1) * 128], ident_b[:], r=[b_hs, b_identb], w=[b_PT])
                    for kt in range(8):
                        act(lambda e, kt=kt, tt=tt: e.activation(out=hnT[:, kt, tt * 128:(tt + 1) * 128], in_=PT[:, kt * 128:(kt + 1) * 128],
                                                                 func=AF.Copy, scale=gffn[:, kt:kt + 1]), r=[b_PT, b_gffn], w=[b_hnT])
                for f in range(22):
                    wb = wgi % 3
                    wgi += 1
                    fw.dma("sp", wg[wb][:].rearrange("p k c -> p (k c)"), wgu_s[f], reads=[b_wgus[f]], writes=[b_wg[wb]])
                    for kt in range(8):
                        mm(PA[:, 0:512], lhsT=wg[wb][:, kt, 0:128], rhs=hnT[:, kt, :], start=(kt == 0), stop=(kt == 7),
                           r=[b_wg[wb], b_hnT], w=[b_PA])
                    for kt in range(8):
                        mm(PB[:, 0:512], lhsT=wg[wb][:, kt, 128:256], rhs=hnT[:, kt, :], start=(kt == 0), stop=(kt == 7),
                           r=[b_wg[wb], b_hnT], w=[b_PB])
                    sb_ = f % 2
                    act(lambda e, sb_=sb_: e.activation(out=sg[sb_][:], in_=PA[:, 0:512], func=AF.Silu), r=[b_PA], w=[b_sg[sb_]])
                    dve(lambda e, sb_=sb_, f=f: e.tensor_tensor(out=actT[:, f, :], in0=PB[:, 0:512], in1=sg[sb_][:], op=ALU.mult),
                        r=[b_PB, b_sg[sb_]], w=[b_actT])
                    if bi == 0:
                        for kt in range(8):
                            mm(PD[:, 0:4], lhsT=wg[wb][:, kt, 0:128], rhs=hnTs[:, kt, :], start=(kt == 0), stop=(kt == 7),
                               r=[b_wg[wb], b_hnTs], w=[b_PD])
                        for kt in range(8):
                            mm(PD[:, 4:8], lhsT=wg[wb][:, kt, 128:256], rhs=hnTs[:, kt, :], start=(kt == 0), stop=(kt == 7),
                               r=[b_wg[wb], b_hnTs], w=[b_PD])
                        act(lambda e: e.activation(out=sgs[:], in_=PD[:, 0:4], func=AF.Silu), r=[b_PD], w=[b_sgs])
                        dve(lambda e, f=f: e.tensor_tensor(out=actTs[:, f, :], in0=PD[:, 4:8], in1=sgs[:], op=ALU.mult),
                            r=[b_PD, b_sgs], w=[b_actTs])
                for tt in range(4):
                    r0 = bi * 512 + tt * 128
                    for half in range(2):
                        for f in range(22):
                            mm(PC[:, half * 512:(half + 1) * 512], lhsT=actT[:, f, tt * 128:(tt + 1) * 128],
                               rhs=wdn[:, f, half * 512:(half + 1) * 512], start=(f == 0), stop=(f == 21),
                               r=[b_actT, b_wdn], w=[b_PC])
                    ys_ = tt % 2
                    dve(lambda e, tt=tt, ys_=ys_: e.tensor_tensor(out=yb[ys_][:], in0=PC[:, :], in1=hres[:, tt, :], op=ALU.add),
                        r=[b_PC, b_hres[tt]], w=[b_yb[ys_]])
                    act(lambda e, ys_=ys_: e.activation(out=ysq[:], in_=yb[ys_][:], func=AF.Square, accum_out=yss[:]),
                        r=[b_yb[ys_]], w=[b_ysq, b_yss])
                    act(lambda e: e.activation(out=yss[:], in_=yss[:], func=AF.Sqrt, scale=1.0 / 1024, bias=EPS), w=[b_yss])
                    dve(lambda e: e.reciprocal(out=yss[:], in_=yss[:]), w=[b_yss])
                    dve(lambda e, ys_=ys_: e.scalar_tensor_tensor(out=yb[ys_][:], in0=yb[ys_][:], scalar=yss[:, 0:1], in1=nfb[:],
                                                                  op0=ALU.mult, op1=ALU.mult), r=[b_yss, b_nfb], w=[b_yb[ys_]])
                    fw.dma("sp", y_o[r0:r0 + 128, :], yb[ys_][:], reads=[b_yb[ys_]])
            for half in range(2):
                for f in range(22):
                    mm(PC[0:4, half * 512:(half + 1) * 512], lhsT=actTs[:, f, :], rhs=wdn[:, f, half * 512:(half + 1) * 512],
                       start=(f == 0), stop=(f == 21), r=[b_actTs, b_wdn], w=[b_PC])
            dve(lambda e: e.tensor_tensor(out=ysb[:], in0=PC[0:4, :], in1=hs_res[:], op=ALU.add), r=[b_PC, b_hsres], w=[b_ysb])
            act(lambda e: e.activation(out=hsn_sq[:], in_=ysb[:], func=AF.Square, accum_out=hsn_ss[:]), r=[b_ysb], w=[b_hsnsq, b_hsnss])
            act(lambda e: e.activation(out=hsn_ss[:], in_=hsn_ss[:], func=AF.Sqrt, scale=1.0 / 1024, bias=EPS), w=[b_hsnss])
            dve(lambda e: e.reciprocal(out=hsn_ss[:], in_=hsn_ss[:]), w=[b_hsnss])
            dve(lambda e: e.scalar_tensor_tensor(out=ysb[:], in0=ysb[:], scalar=hsn_ss[:, 0:1], in1=nfb[0:4, :], op0=ALU.mult, op1=ALU.mult),
                r=[b_hsnss, b_nfb], w=[b_ysb])
            fw.dma("sp", ys_o[:, :], ysb[:], reads=[b_ysb])
        fw.drain()
    return nc


def _consts(NT):
    TT = NT * 128
    c = {}
    c["c_ident"] = np.eye(128, dtype=np.float32)
    half = 32
    inv = np.power(np.float32(10000.0), -np.arange(half, dtype=np.float32) * np.float32(2.0) / np.float32(64)).astype(np.float32)
    pos = (np.arange(NT)[None, :] * 128 + np.arange(128)[:, None]).astype(np.float32)
    ang = (pos[:, :, None] * inv[None, None, :]).astype(np.float32)
    c["c_cos"] = np.cos(ang).astype(np.float32).reshape(128, NT * 32)
    c["c_sin"] = np.sin(ang).astype(np.float32).reshape(128, NT * 32)
    k = np.arange(128)[:, None]
    q = np.arange(128)[None, :]
    tri = np.stack([(k <= q), (k >= q)], axis=1).astype(np.float32)
    c["c_tri"] = tri.reshape(128, 256)
    cm = np.zeros((128, 17, 128), np.float32)
    for m in range(17):
        cm[:, m, :] = (16 * k - q <= 128 * m - 31)
    c["c_cmpmask"] = cm.reshape(128, 17 * 128)
    qq = np.arange(128)[:, None]
    r = np.arange(256)[None, :] - 128
    hi = (qq >= 64).astype(np.int64)
    prel = np.zeros((128, 256), np.float32)
    prel[(r == hi) | (r == hi - 1)] = 1e4
    prel[r > hi] = -1e30
    c["c_prel"] = prel
    kk = np.arange(TT)[None, :]
    e = np.arange(64)[:, None]
    c["c_eind"] = (e == (kk // 64) % 64).astype(np.float32)
    n = np.arange(512)[:, None]
    s_ = np.arange(128)[None, :]
    c2s = ((n * 16 < s_ * 64 + 64) & (n * 16 + 32 > s_ * 64) & (n < 511)).astype(np.float32)
    c["c_c2s"] = c2s.reshape(4, 128, 128).transpose(1, 0, 2).reshape(128, 512)
    j = np.arange(128)[:, None]
    i = np.arange(128)[None, :]
    same = (j // 64) == (i // 64)
    gm = np.zeros((128, 5, 128), np.float32)
    gm[:, 0, :] = np.where(same & (i >= j), 0.0, NEGB)
    gm[:, 1, :] = np.where(same & (i > j), 0.0, NEGB)
    gm[:, 2, :] = (same & (j <= i))
    gm[:, 3, :] = (j < 64) * np.ones((1, 128))
    gm[:, 4, :] = (j >= 64) * np.ones((1, 128))
    c["c_gmask"] = gm.reshape(128, 5 * 128)
    angs = (np.float32(8192.0) * inv).astype(np.float32)
    c["c_rope_s"] = np.tile(np.concatenate([np.cos(angs), np.sin(angs)]).astype(np.float32)[None, :], (4, 1))
    nn_ = np.arange(512).reshape(4, 128).T
    c["c_ones511"] = (nn_ < 511).astype(np.float32)
    oh = np.zeros((4, 80), np.float32)
    for s_ in range(4):
        oh[s_, s_ * 4:(s_ + 1) * 4] = 1.0
    for r_ in range(4):
        for s_ in range(4):
            oh[r_, 16 + (r_ * 4 + s_) * 4 + s_] = 1.0
    c["c_oh4"] = oh
    bon = np.zeros((1, 128), np.float32)
    bon[0, 0] = 1e4
    bon[0, 127] = 1e4
    c["c_bonus_s"] = bon
    kk8 = np.arange(8192)[None, :]
    c["c_eind_s"] = (e == (kk8 // 64) % 64).astype(np.float32)
    return c


def _core_weights(inp, g, hp):
    jh = 2 * g + hp
    w_in = inp["w_in"][0]
    own = [4 * g + 2 * hp, 4 * g + 2 * hp + 1]
    oth = [4 * g + 2 * (1 - hp), 4 * g + 2 * (1 - hp) + 1]
    heads = own + oth
    cols = []
    for h in heads:
        cols += list(range(h * 64, (h + 1) * 64))

    def kvcol(branch, kv):
        base = 512 + ((branch * 2 + kv) * 2 + g) * 64
        return list(range(base, base + 64))
    for branch in range(3):
        cols += kvcol(branch, 0)
    for branch in range(3):
        cols += kvcol(branch, 1)
    for h in heads:
        cols += [1280 + h * 3 + t for t in range(3)]
    cols += list(range(2840 + jh * 128, 2840 + (jh + 1) * 128))
    cols += [3352 + jh, 3356 + jh]
    w_tok = np.ascontiguousarray(w_in[:, cols])
    gcols = []
    for part in range(3):
        gcols += list(range(1304 + part * 512 + jh * 128, 1304 + part * 512 + (jh + 1) * 128))
    w_gdn = np.ascontiguousarray(w_in[:, gcols])
    d = {"w_tok": w_tok, "w_gdn": w_gdn}
    d["g_mix"] = np.ascontiguousarray(inp["norm_mix"][0].reshape(8, 128).T)
    cwf = inp["gdn_conv_w"][0]
    gch = [jh * 128 + part * 512 + np.arange(128) for part in range(3)]
    cw = np.stack([cwf[:, ch].T for ch in gch], axis=1)
    d["conv_w"] = np.ascontiguousarray(cw.reshape(128, 12))
    d["head_sc"] = np.ascontiguousarray(np.stack([np.full(128, inp["gdn_a_log"][0, jh]),
                                                  np.full(128, inp["gdn_dt_bias"][0, jh])], axis=1).astype(np.float32))
    d["gdn_norm_b"] = np.ascontiguousarray(np.tile(inp["gdn_norm"][0][None, :], (128, 1)))
    cwt = inp["nsa_cmp_w"][0]
    d["cmp_w"] = np.ascontiguousarray(cwt.reshape(2, 16, 2, 64, 64).transpose(2, 3, 0, 1, 4).reshape(128, 2 * 16 * 64))
    pe = inp["nsa_cmp_pe"][0]
    d["cmp_pe"] = np.ascontiguousarray(pe.reshape(2, 16, 2, 64).transpose(2, 3, 0, 1).reshape(128, 32))
    return d


def _core_inputs(inp, c, NT, consts):
    b, j = c // 4, c % 4
    g, hp = j // 2, j % 2
    TT = NT * 128
    TB = TT // 4
    d = dict(consts)
    d.update(_core_weights(inp, g, hp))
    d["x"] = np.ascontiguousarray(inp["x_prompt"][b, :TT])
    d["x_own"] = np.ascontiguousarray(inp["x_prompt"][b, j * TB:(j + 1) * TB])
    sel = np.zeros((128, 4), np.float32)
    sel[:, j] = 1.0
    d["sel4"] = sel
    perm = []
    for jj in range(4):
        perm += list(range(128 * jj, 128 * jj + 128)) + list(range(512 + 128 * jj, 512 + 128 * jj + 128))
    d["w_out_p"] = np.ascontiguousarray(inp["w_out"][0][perm, :])
    d["g_ffn"] = np.ascontiguousarray(inp["norm_ffn"][0].reshape(8, 128).T)
    d["w_gu"] = np.ascontiguousarray(inp["w_gate_up"][0])
    d["w_dn"] = np.ascontiguousarray(inp["w_down"][0])
    d["nfin_b"] = np.ascontiguousarray(np.tile(inp["norm_final"][None, :], (128, 1)))
    s0 = 4 * c
    d["xs"] = np.ascontiguousarray(inp["x_sample"][s0:s0 + 4, 0, :])
    d["w_in_full"] = np.ascontiguousarray(inp["w_in"][0])
    d["cache_kv"] = inp["cache_nsa_kv"][0].reshape(2560 * 128, 512)
    d["ptab_b"] = np.ascontiguousarray(np.tile(inp["page_table"][s0:s0 + 4].reshape(1, 256), (128, 1)).astype(np.int32))
    d["c_iota"] = np.arange(128, dtype=np.float32).reshape(128, 1)
    d["win_cache"] = np.ascontiguousarray(inp["cache_nsa_win"][0, s0:s0 + 4].reshape(4, 512, 256))
    d["gdn_S"] = np.ascontiguousarray(inp["state_gdn_S"][0, s0:s0 + 4].reshape(16, 128, 128))
    d["gdn_conv"] = np.ascontiguousarray(inp["state_gdn_conv"][0, s0:s0 + 4])
    d["conv_w_b"] = np.ascontiguousarray(np.tile(inp["gdn_conv_w"][0][None], (4, 1, 1)))
    d["alog_b"] = np.ascontiguousarray(np.tile(np.concatenate([inp["gdn_a_log"][0], inp["gdn_dt_bias"][0]])[None, :], (4, 1)))
    d["gnorm_row"] = np.ascontiguousarray(inp["gdn_norm"][0][None, :])
    cwt = inp["nsa_cmp_w"][0]
    w64 = cwt.transpose(2, 0, 1, 3).reshape(64, 2 * 32 * 64)
    d["cmp_w64"] = np.ascontiguousarray(np.concatenate([w64, w64], axis=0))
    pe = inp["nsa_cmp_pe"][0]
    p64 = pe.transpose(2, 0, 1).reshape(64, 64)
    d["cmp_pe64"] = np.ascontiguousarray(np.concatenate([p64, p64], axis=0))
    wo = inp["w_out"][0]
    d["w_out_n"] = np.ascontiguousarray(wo[:512].reshape(8, 64, 1024).transpose(1, 0, 2).reshape(64, 8192))
    d["w_out_g"] = np.ascontiguousarray(wo[512:].reshape(4, 128, 1024).transpose(1, 0, 2).reshape(128, 4096))
    return d


def _run(inp, NT):
    nc = build_nc(NT, phaseB=True)
    consts = _consts(NT)
    maps = [_core_inputs(inp, c, NT, consts) for c in range(8)]
    res = run_bass_kernel_spmd(nc, maps, core_ids=list(range(8)))
    return res.results


def kernel(**inputs):
    inp = {k: np.asarray(v) for k, v in inputs.items()}
    NT = 64
    TT = NT * 128
    TB = TT // 4
    R = _run(inp, NT)
    y_prompt = np.zeros((2, TT, 1024), np.float32)
    kv_prompt = np.zeros((1, 2, TT, 4, 2, 64), np.float32)
    win_prompt = np.zeros((1, 2, 512, 2, 2, 64), np.float32)
    S_prompt = np.zeros((1, 2, 4, 128, 128), np.float32)
    conv_prompt = np.zeros((1, 2, 3, 1536), np.float32)
    for c in range(8):
        b, j = c // 4, c % 4
        g, hp = j // 2, j % 2
        r = R[c]
        y_prompt[b, j * TB:(j + 1) * TB] = r["y_out"]
        if hp == 0:
            kv_prompt[0, b, :, :, g, :] = r["kv_out"].reshape(TT, 4, 64)
            win_prompt[0, b, :, :, g, :] = r["win_out"].reshape(512, 2, 64)
        S_prompt[0, b, j] = r["S_out"]
        cv = r["conv_out"]
        for part in range(3):
            conv_prompt[0, b, :, part * 512 + j * 128:part * 512 + (j + 1) * 128] = cv[:, part, :].T
    y_sample = np.zeros((32, 1, 1024), np.float32)
    kv_sample = np.zeros((1, 32, 1, 4, 2, 64), np.float32)
    win_sample = np.zeros((1, 32, 512, 2, 2, 64), np.float32)
    S_sample = np.zeros((1, 32, 4, 128, 128), np.float32)
    conv_sample = np.zeros((1, 32, 3, 1536), np.float32)
    for c in range(8):
        r = R[c]
        s0 = 4 * c
        y_sample[s0:s0 + 4, 0] = r["ys_out"]
        kv_sample[0, s0:s0 + 4, 0] = r["kvs_out"].reshape(4, 4, 2, 64)
        win_sample[0, s0:s0 + 4] = r["wins_out"].reshape(4, 512, 2, 2, 64)
        S_sample[0, s0:s0 + 4] = r["Ss_out"].reshape(4, 4, 128, 128)
        conv_sample[0, s0:s0 + 4] = r["convs_out"]
    return (y_prompt, y_sample, kv_prompt, win_prompt, S_prompt, conv_prompt, kv_sample, win_sample, S_sample, conv_sample)
```

```python
import numpy as np
from contextlib import ExitStack
import concourse.bass as bass
import concourse.mybir as mybir
from concourse.bass_utils import run_bass_kernel_spmd

F32 = mybir.dt.float32
BF16 = mybir.dt.bfloat16
I32 = mybir.dt.int32
AF = mybir.ActivationFunctionType
ALU = mybir.AluOpType
AX = mybir.AxisListType

ENGS = ("pe", "act", "dve", "pool", "sp")

D_MODEL = 1024
SEQ = 8192
HEAD_DIM = 64
D_FF = 2816
EPS = 1e-6
NEGB = -30000.0


class Buf:
    __slots__ = ("name", "w", "rs")

    def __init__(self, name=""):
        self.name = name
        self.w = None
        self.rs = []


class FW:
    def __init__(self, nc, stack, ndma_sems=16):
        self.nc = nc
        self.eng = {"pe": nc.tensor, "act": nc.scalar, "dve": nc.vector, "pool": nc.gpsimd, "sp": nc.sync}
        self.sem = {e: stack.enter_context(nc.semaphore("s_" + e)) for e in ENGS}
        self.cnt = {e: 0 for e in ENGS}
        self.waited = {e: {} for e in ENGS}
        self.dsems = {}
        self.dstate = {}
        for q in ("sp", "pool"):
            self.dsems[q] = [stack.enter_context(nc.semaphore("d_%s%d" % (q, i))) for i in range(ndma_sems)]
            self.dstate[q] = {"i": 0, "val": [0] * ndma_sems}
        self.n_inst = 0
        self.dead = False

    def _wait(self, e, ev):
        if ev is None:
            return
        if ev[0] == "c":
            _, src, n = ev
            if src == "pe" and e == "pe":
                return
            key = ("c", src)
            if self.waited[e].get(key, 0) >= n:
                return
            self.eng[e].wait_ge(self.sem[src], n)
            self.waited[e][key] = n
        else:
            _, q, idx, val = ev
            key = ("d", q, idx)
            if self.waited[e].get(key, 0) >= val:
                return
            self.eng[e].wait_ge(self.dsems[q][idx], val)
            self.waited[e][key] = val

    def _deps(self, e, reads, writes):
        for b in reads:
            self._wait(e, b.w)
        for b in writes:
            self._wait(e, b.w)
            for r in b.rs:
                self._wait(e, r)

    def _commit(self, ev, reads, writes):
        for b in reads:
            b.rs.append(ev)
            if len(b.rs) > 96:
                b.rs = b.rs[-96:]
        for b in writes:
            b.w = ev
            b.rs = []

    def op(self, e, fn, reads=(), writes=()):
        if self.dead:
            return None
        self._deps(e, reads, writes)
        ins = fn(self.eng[e])
        self.cnt[e] += 1
        ins.then_inc(self.sem[e], 1)
        ev = ("c", e, self.cnt[e])
        self._commit(ev, reads, writes)
        self.n_inst += 1
        return ev

    def dma(self, q, out, in_, reads=(), writes=(), fn=None):
        if self.dead:
            return None
        st = self.dstate[q]
        idx = st["i"] % len(self.dsems[q])
        st["i"] += 1
        if st["val"][idx] > 0:
            self._wait(q, ("d", q, idx, st["val"][idx]))
        self._deps(q, reads, writes)
        if fn is None:
            ins = self.eng[q].dma_start(out=out, in_=in_)
        else:
            ins = fn(self.eng[q])
        st["val"][idx] += 16
        ins.then_inc(self.dsems[q][idx], 16)
        ev = ("d", q, idx, st["val"][idx])
        self._commit(ev, reads, writes)
        self.n_inst += 1
        return ev

    def barrier(self):
        for e in ENGS:
            for src in ENGS:
                if src != e and self.cnt[src] > 0:
                    self._wait(e, ("c", src, self.cnt[src]))
            for q in ("sp", "pool"):
                stq = self.dstate[q]
                for idx, v in enumerate(stq["val"]):
                    if v:
                        self._wait(e, ("d", q, idx, v))

    def drain(self):
        for q in ("sp", "pool"):
            st = self.dstate[q]
            for idx, v in enumerate(st["val"]):
                if v:
                    self._wait("sp", ("d", q, idx, v))


class _Stop(Exception):
    pass


STOP = None
SKIP_CC = False


def build_nc(NT=64, dbg=False, phaseB=False):
    TT = NT * 128
    NG = NT // 4
    nc = bass.Bass("TRN2", target_bir_lowering=False)

    def din(name, shape, dt=F32):
        return nc.dram_tensor(name, list(shape), dt, kind="ExternalInput").ap()

    def dout(name, shape, dt=F32):
        return nc.dram_tensor(name, list(shape), dt, kind="ExternalOutput").ap()

    x_d = din("x", [TT, 1024])
    wtok_d = din("w_tok", [1024, 782])
    wgdn_d = din("w_gdn", [1024, 384])
    gmix_d = din("g_mix", [128, 8])
    cw_d = din("conv_w", [128, 12])
    hsc_d = din("head_sc", [128, 2])
    gnorm_d = din("gdn_norm_b", [128, 128])
    cmpw_d = din("cmp_w", [128, 2 * 16 * 64])
    cmppe_d = din("cmp_pe", [128, 32])
    ident_d = din("c_ident", [128, 128])
    cos_d = din("c_cos", [128, NT * 32])
    sin_d = din("c_sin", [128, NT * 32])
    tri_d = din("c_tri", [128, 256])
    cmpmask_d = din("c_cmpmask", [128, 17 * 128])
    prel_d = din("c_prel", [128, 256])
    eind_d = din("c_eind", [64, TT])
    c2s_d = din("c_c2s", [128, 4 * 128])
    gmask_d = din("c_gmask", [128, 5 * 128])

    TB = TT // 4
    NB = TB // 512
    if phaseB:
        xown_d = din("x_own", [TB, 1024])
        sel4_d = din("sel4", [128, 4])
        wout_d = din("w_out_p", [1024, 1024])
        gffn_d = din("g_ffn", [128, 8])
        wgu_d = din("w_gu", [1024, 5632])
        wdn_d = din("w_dn", [2816, 1024])
        nfin_d = din("nfin_b", [128, 1024])
        y_o = dout("y_out", [TB, 1024])
    if phaseB:
        xs_d = din("xs", [4, 1024])
        win_full_d = din("w_in_full", [1024, 3360])
        cache_d = din("cache_kv", [2560 * 128, 512])
        ptab_d = din("ptab_b", [128, 256], I32)
        iota_d = din("c_iota", [128, 1])
        wincache_d = din("win_cache", [4, 512, 256])
        gS_d = din("gdn_S", [16, 128, 128])
        gconv_d = din("gdn_conv", [4, 3, 1536])
        convwb_d = din("conv_w_b", [4, 4, 1536])
        alogb_d = din("alog_b", [4, 8])
        gnrow_d = din("gnorm_row", [1, 128])
        cmpw64_d = din("cmp_w64", [128, 2 * 32 * 64])
        woutn_d = din("w_out_n", [64, 8 * 1024])
        woutg_d = din("w_out_g", [128, 4 * 1024])
        ropes_d = din("c_rope_s", [4, 64])
        ones511_d = din("c_ones511", [128, 4])
        pe64_d = din("cmp_pe64", [128, 64])
        efix_d = din("c_efix", [2, 128])
        oh2_d = din("c_oh2", [1, 4])
        oh4_d = din("c_oh4", [4, 80])
        bonus_d = din("c_bonus_s", [1, 128])
        ys_o = dout("ys_out", [4, 1024])
        kvs_o = dout("kvs_out", [4, 512])
        wins_o = dout("wins_out", [4, 512, 256])
        Ss_o = dout("Ss_out", [16, 128, 128])
        convs_o = dout("convs_out", [4, 3, 1536])
    kv_o = dout("kv_out", [TT, 256])
    win_o = dout("win_out", [512, 128])
    S_o = dout("S_out", [128, 128])
    conv_o = dout("conv_out", [128, 3, 3])
    CH = min(2048, TT)
    NCH = TT // CH
    omT_o = dout("omT_out", [256, TT], BF16) if not phaseB else None
    xin = [nc.dram_tensor("xin%d" % k, [256, CH], BF16).ap() for k in range(NCH)] if phaseB else None
    b_xin = [Buf() for _ in range(NCH)]
    dbg_o = dout("dbg_out", [128, 1300]) if dbg else None

    st = ExitStack()
    with st:
        fw = FW(nc, st)

        cur = [st]

        def sb(name, shape, dt=F32):
            return cur[0].enter_context(nc.sbuf_tensor(name, list(shape), dt))

        def ps(name, shape, dt=F32):
            return st.enter_context(nc.psum_tensor(name, list(shape), dt))

        def pe(fn, r=(), w=()):
            return fw.op("pe", fn, r, w)

        def act(fn, r=(), w=()):
            return fw.op("act", fn, r, w)

        def dve(fn, r=(), w=()):
            return fw.op("dve", fn, r, w)

        def pool(fn, r=(), w=()):
            return fw.op("pool", fn, r, w)

        def mm(out, lhsT, rhs, start=True, stop=True, r=(), w=()):
            return pe(lambda e: e.matmul(out, lhsT=lhsT, rhs=rhs, start=start, stop=stop), r, w)

        def tr(out, in_, ident, r=(), w=()):
            return pe(lambda e: e.transpose(out, in_, ident), r, w)

        def bcast(ap, shape, axis):
            return ap.unsqueeze(axis).to_broadcast(list(shape))

        ident_f = sb("ident_f", [128, 128]); b_identf = Buf()
        ident_b = sb("ident_b", [128, 128], BF16); b_identb = Buf()
        ones_f2 = sb("ones_f2", [128, 128]); b_onesf2 = Buf()
        hs_res = sb("hs_res", [4, 1024]); b_hsres = Buf()
        stA = st.enter_context(ExitStack())
        cur[0] = stA
        pool(lambda e: e.memset(ones_f2[:], 1.0), w=[b_onesf2])
        ones_b = sb("ones_b", [128, 128], BF16); b_onesb = Buf()
        ones_f = sb("ones_f", [128, 128]); b_onesf = Buf()
        csT = [sb("csT%d" % i_, [128, 2, 4, 32]) for i_ in range(2)]; b_csT = [Buf() for _ in range(2)]
        tri = sb("tri", [128, 2, 128], BF16); b_tri = Buf()
        cmpmask = sb("cmpmask", [128, 17, 128], BF16); b_cmpmask = Buf()
        prel = sb("prel", [128, 256]); b_prel = Buf()
        gmask = sb("gmask", [128, 5, 128]); b_gmask = Buf()
        wtok = sb("wtok", [128, 8, 782], BF16); b_wtok = Buf()
        wgdn = sb("wgdn", [128, 8, 384], BF16); b_wgdn = Buf()
        gmix = sb("gmix", [128, 8]); b_gmix = Buf()
        cw = sb("cw", [128, 12]); b_cw = Buf()
        hsc = sb("hsc", [128, 2]); b_hsc = Buf()
        negA = sb("negA", [128, 1]); b_negA = Buf()
        gnb = sb("gnb", [128, 128]); b_gnb = Buf()
        cmpw = sb("cmpw", [128, 2, 16, 64], BF16); b_cmpw = Buf()
        cmppe = sb("cmppe", [128, 2, 16], BF16); b_cmppe = Buf()

        fw.dma("sp", ident_f[:], ident_d[:, :], writes=[b_identf])
        fw.dma("pool", ident_b[:], ident_d[:, :], writes=[b_identb])
        fw.dma("pool", tri[:].rearrange("p a b -> p (a b)"), tri_d[:, :], writes=[b_tri])
        fw.dma("pool", cmpmask[:].rearrange("p a b -> p (a b)"), cmpmask_d[:, :], writes=[b_cmpmask])
        fw.dma("sp", prel[:], prel_d[:, :], writes=[b_prel])
        fw.dma("sp", gmask[:].rearrange("p a b -> p (a b)"), gmask_d[:, :], writes=[b_gmask])
        fw.dma("sp", gmix[:], gmix_d[:, :], writes=[b_gmix])
        fw.dma("sp", cw[:], cw_d[:, :], writes=[b_cw])
        fw.dma("sp", hsc[:], hsc_d[:, :], writes=[b_hsc])
        fw.dma("sp", gnb[:], gnorm_d[:, :], writes=[b_gnb])
        fw.dma("pool", cmpw[:].rearrange("p a b c -> p (a b c)"), cmpw_d[:, :], writes=[b_cmpw])
        fw.dma("pool", cmppe[:].rearrange("p a b -> p (a b)"), cmppe_d[:, :], writes=[b_cmppe])
        for kt in range(8):
            fw.dma("pool", wtok[:, kt, :], wtok_d[kt * 128:(kt + 1) * 128, :], writes=[b_wtok])
            fw.dma("pool", wgdn[:, kt, :], wgdn_d[kt * 128:(kt + 1) * 128, :], writes=[b_wgdn])
        pool(lambda e: e.memset(ones_b[:], 1.0), w=[b_onesb])
        pool(lambda e: e.memset(ones_f[:], 1.0), w=[b_onesf])
        for kt in range(8):
            dve(lambda e, kt=kt: e.tensor_scalar(out=wtok[:, kt, :], in0=wtok[:, kt, :], scalar1=gmix[:, kt:kt + 1],
                                                 scalar2=None, op0=ALU.mult), r=[b_gmix], w=[b_wtok])
            dve(lambda e, kt=kt: e.tensor_scalar(out=wgdn[:, kt, :], in0=wgdn[:, kt, :], scalar1=gmix[:, kt:kt + 1],
                                                 scalar2=None, op0=ALU.mult), r=[b_gmix], w=[b_wgdn])
        act(lambda e: e.activation(out=negA[:], in_=hsc[:, 0:1], func=AF.Exp), r=[b_hsc], w=[b_negA])
        dve(lambda e: e.tensor_scalar(out=negA[:], in0=negA[:], scalar1=-1.0, scalar2=None, op0=ALU.mult), w=[b_negA])

        KselT = sb("KselT", [128, TT], BF16); b_ksel = [Buf() for _ in range(NT)]; b_eind = Buf()
        Vsel = sb("Vsel", [128, NT, 65], BF16); b_vsel = [Buf() for _ in range(NT)]
        KwinT = sb("KwinT", [64, 8 * 128], BF16); b_kwin = [Buf() for _ in range(8)]
        Vwin = sb("Vwin", [128, 8, 65], BF16); b_vwin = [Buf() for _ in range(8)]
        Rk = sb("Rk", [128, 2, 160], BF16); b_Rk = Buf()
        ckT = sb("ckT", [64, 512], BF16); b_ckT = Buf()
        cvx = sb("cvx", [128, 4, 193], BF16); b_cvx = Buf()
        c2s_f = sb("c2s_f", [128, 4, 128]); b_c2sf = Buf()
        ckb = sb("ckb", [64, 1]); b_ckb = Buf()
        cvb = sb("cvb", [8, 64]); b_cvb = Buf()
        cvrow = sb("cvrow", [1, 64], BF16); b_cvrow = Buf()

        fw.dma("pool", KselT[64:128, :], eind_d[:, :], writes=[b_eind])
        pool(lambda e: e.memset(Vsel[:, :, 64:65], 1.0), w=b_vsel)
        pool(lambda e: e.memset(Vwin[:, :, 64:65], 1.0), w=b_vwin)
        pool(lambda e: e.memset(Rk[:], 0.0), w=[b_Rk])
        pool(lambda e: e.memset(ckT[:], 0.0), w=[b_ckT])
        pool(lambda e: e.memset(cvx[:], 0.0), w=[b_cvx])
        fw.dma("sp", c2s_f[:].rearrange("p a b -> p (a b)"), c2s_d[:, :], writes=[b_c2sf])
        pool(lambda e: e.tensor_copy(out=cvx[:, :, 64:192], in_=c2s_f[:]), r=[b_c2sf], w=[b_cvx])
        pool(lambda e: e.memset(cvx[:, :, 192:193], 1.0), w=[b_cvx])

        PA = ps("PA", [128, 1024]); b_PA = Buf()
        PB = ps("PB", [128, 1024]); b_PB = Buf()
        PC = ps("PC", [128, 1024]); b_PC = Buf()
        PD = ps("PD", [128, 512]); b_PD = Buf()
        PT = ps("PT", [128, 1024], BF16); b_PT = Buf()

        for lp in range(16):
            mm(PD[0:64, 0:1], lhsT=cmpw[:, 0, lp, :], rhs=cmppe[:, 0, lp:lp + 1], start=(lp == 0), stop=(lp == 15),
               r=[b_cmpw, b_cmppe], w=[b_PD])
        act(lambda e: e.copy(out=ckb[:], in_=PD[0:64, 0:1]), r=[b_PD], w=[b_ckb])
        for lp in range(16):
            mm(PD[0:1, 64:128], lhsT=cmppe[:, 1, lp:lp + 1], rhs=cmpw[:, 1, lp, :], start=(lp == 0), stop=(lp == 15),
               r=[b_cmpw, b_cmppe], w=[b_PD])
        act(lambda e: e.copy(out=cvrow[:], in_=PD[0:1, 64:128]), r=[b_PD], w=[b_cvrow])
        mm(PD[0:8, 128:192], lhsT=ones_b[0:1, 0:8], rhs=cvrow[:], r=[b_onesb, b_cvrow], w=[b_PD])
        act(lambda e: e.copy(out=cvb[:], in_=PD[0:8, 128:192]), r=[b_PD], w=[b_cvb])

        NXB = 2
        xt = [sb("xt%d" % i, [128, 1024]) for i in range(NXB)]; b_xt = [Buf() for _ in range(NXB)]
        ssq = [sb("ssq%d" % i, [128, 1]) for i in range(NXB)]; b_ssq = [Buf() for _ in range(NXB)]
        xs = [sb("xs%d" % i, [128, 1024], BF16) for i in range(NXB)]; b_xs = [Buf() for _ in range(NXB)]
        xnT = sb("xnT", [128, 8, 512], BF16); b_xnT = [Buf() for _ in range(4)]
        pj = [sb("pj%d" % i, [128, 782]) for i in range(2)]; b_pj = [Buf() for _ in range(2)]
        rq = [sb("rq%d" % i, [128, 7, 64]) for i in range(2)]; b_rq = [Buf() for _ in range(2)]
        rt = sb("rt", [128, 4, 7, 32]); b_rt = Buf()
        ko = [sb("ko%d" % i, [128, 6, 64]) for i in range(2)]; b_ko = [Buf() for _ in range(2)]
        qkb = sb("qkb", [128, 7, 64], BF16); b_qkb = Buf()
        kvc2 = sb("kvc2", [128, 2, 128], BF16); b_kvc2 = Buf()
        QT = [sb("QT%d" % i_, [64, 512], BF16) for i_ in range(2)]; b_QT = [Buf() for _ in range(2)]
        Qaug = [sb("Qaug%d" % i_, [128, 2, 256], BF16) for i_ in range(2)]; b_Qaug = [Buf() for _ in range(2)]
        gates = [sb("gates%d" % i_, [128, 12]) for i_ in range(2)]; b_gates = [Buf() for _ in range(2)]
        gz = sb("gz", [128, 140]); b_gz = Buf()
        zsil = [sb("zsil%d" % i_, [128, 4, 128]) for i_ in range(2)]; b_zsil = [[Buf() for _ in range(4)] for _ in range(2)]
        abg = [sb("abg%d" % i_, [128, 4, 2]) for i_ in range(2)]; b_abg = [Buf() for _ in range(2)]
        cvnew = sb("cvnew", [8, 64], BF16); b_cvnew = Buf()
        PTc = [sb("PTc%d" % i, [128, 512], BF16) for i in range(4)]; b_PTc = [Buf() for _ in range(4)]
        PTs = [sb("PTs%d" % i, [128, 1024], BF16) for i in range(2)]; b_PTs = [Buf() for _ in range(2)]
        acc_c = sb("acc_c", [128, 4, 193]); b_accc = Buf()
        acc_sw = sb("acc_sw", [128, 4, 65]); b_accsw = Buf()
        rcp = sb("rcp", [128, 12]); b_rcp = Buf()
        imp = sb("imp", [128, 128]); b_imp = Buf()
        score = sb("score", [128, 128]); b_score = Buf()
        mx8 = sb("mx8", [128, 16]); b_mx8 = Buf()
        thr = sb("thr", [128, 1]); b_thr = Buf()
        sc2 = sb("sc2", [128, 128]); b_sc2 = Buf()
        mbt = sb("mbt", [128, 2, 128]); b_mbt = Buf()
        coef = sb("coef", [128, 6]); b_coef = Buf()
        onsa = sb("onsa", [128, 128]); b_onsa = Buf()
        om = sb("om", [128, 256], BF16); b_om = Buf()
        omT = [sb("omT%d" % i_, [128, 2, 512], BF16) for i_ in range(2)]; b_omT = [Buf() for _ in range(2)]
        raw = [sb("raw%d" % i_, [128, 3, 515]) for i_ in range(2)]; b_raw = [Buf() for _ in range(2)]
        cacc = sb("cacc", [128, 3, 512]); b_cacc = Buf()
        csil = cacc; b_csil = b_cacc
        sqb = sb("sqb", [128, 2, 512], BF16); b_sqb = Buf()
        rnorm = sb("rnorm", [128, 2, 512]); b_rnorm = Buf()
        gT = sb("gT", [128, 3, 512], BF16); b_gT = Buf()
        gtok = sb("gtok", [128, 4, 3, 128], BF16); b_gtok = Buf()
        gsc = sb("gsc", [128, 16, 4]); b_gsc = Buf()
        glc = sb("glc", [128, 2, 4]); b_glc = Buf()
        dg1 = sb("dg1", [128, 4, 128]); b_dg1 = Buf()
        dg2 = sb("dg2", [128, 4, 128]); b_dg2 = Buf()
        dgn = sb("dgn", [128, 4, 128]); b_dgn = Buf()
        gmask4 = sb("gmask4", [128, 2, 4, 128]); b_gmask4 = Buf()
        decT = sb("decT", [128, 4, 128], BF16); b_decT = Buf()
        decbT = sb("decbT", [128, 4, 128], BF16); b_decbT = Buf()
        Um = [sb("Um%d" % i, [128, 4, 128], BF16) for i in range(2)]; b_Um = [Buf() for _ in range(2)]
        Lm = [sb("Lm%d" % i, [128, 4, 128], BF16) for i in range(2)]; b_Lm = [Buf() for _ in range(2)]
        Pm = [sb("Pm%d" % i, [128, 4, 128], BF16) for i in range(2)]; b_Pm = [Buf() for _ in range(2)]
        Xm = sb("Xm", [128, 4, 256], BF16); b_Xm = Buf()
        uw = sb("uw", [128, 4, 256], BF16); b_uw = Buf()
        kgm = sb("kgm", [128, 4, 2, 128], BF16); b_kgm = Buf()
        aqkT = sb("aqkT", [128, 4, 128], BF16); b_aqkT = Buf()
        Dg = sb("Dg", [128, 4, 128], BF16); b_Dg = Buf()
        QpA = sb("QpA", [128, 4, 128], BF16); QpB = sb("QpB", [128, 4, 128], BF16); b_Qp = Buf()
        glc8 = sb("glc8", [128, 8]); b_glc8 = Buf()
        MTf = sb("MTf", [128, 8, 128], BF16); b_MTf = Buf()
        MT8 = sb("MT8", [128, 8, 128], BF16); b_MT8 = Buf()
        Sb9 = sb("Sb9", [128, 9, 128], BF16); b_Sb9 = [Buf() for _ in range(9)]
        Sf = sb("Sf", [128, 128]); b_Sf = Buf()
        og4 = sb("og4", [128, 4, 128]); b_og4 = Buf()
        og4q = dgn; b_og4q = b_dgn
        og4s = sb("og4s", [128, 4]); b_og4s = Buf()
        omg = sb("omg", [128, 4, 128], BF16); b_omg = Buf()

        pool(lambda e: e.memset(kgm[:], 0.0), w=[b_kgm])
        for mk_ in range(2):
            pool(lambda e, mk_=mk_: e.tensor_copy(out=gmask4[:, mk_], in_=bcast(gmask[:, mk_, :], [128, 4, 128], 1)), r=[b_gmask], w=[b_gmask4])
        pool(lambda e: e.memset(raw[0][:], 0.0), w=[b_raw[0]])
        pool(lambda e: e.memset(raw[1][:], 0.0), w=[b_raw[1]])
        pool(lambda e: e.memset(QpA[:], 0.0), w=[b_Qp])
        pool(lambda e: e.memset(QpB[:], 0.0), w=[b_Qp])
        pool(lambda e: e.memset(Sb9[:, 0, :], 0.0), w=[b_Sb9[0]])

        G_G, G_BETA, G_GCUM, G_GL, G_EG, G_EKG, G_LNB, G_NEGG, G_SKBG, G_GB = range(10)


        hits = {}

        def chk2(name):
            if STOP == name:
                fw.dead = True

        def chk(name):
            hits[name] = hits.get(name, 0) + 1
            if STOP == name or STOP == "%s@%d" % (name, hits[name]):
                raise _Stop()

        def gdn_gen(grp):
            gp = grp % 2
            for c3 in range(3):
                dve(lambda e, c3=c3: e.tensor_scalar(out=cacc[:, c3, :], in0=raw[gp][:, c3, 0:512], scalar1=cw[:, c3 * 4:c3 * 4 + 1],
                                                      scalar2=None, op0=ALU.mult), r=[b_raw[gp], b_cw], w=[b_cacc])
                for jj in range(1, 4):
                    dve(lambda e, c3=c3, jj=jj: e.scalar_tensor_tensor(out=cacc[:, c3, :], in0=raw[gp][:, c3, jj:jj + 512],
                                                                        scalar=cw[:, c3 * 4 + jj:c3 * 4 + jj + 1], in1=cacc[:, c3, :],
                                                                        op0=ALU.mult, op1=ALU.add), r=[b_raw[gp], b_cw], w=[b_cacc])
            act(lambda e: e.activation(out=csil[:].rearrange("p a b -> p (a b)"), in_=cacc[:].rearrange("p a b -> p (a b)"),
                                       func=AF.Silu), r=[b_cacc], w=[b_csil])
            yield
            act(lambda e: e.activation(out=sqb[:].rearrange("p a b -> p (a b)"), in_=csil[:, 0:2, :].rearrange("p a b -> p (a b)"),
                                       func=AF.Square), r=[b_csil], w=[b_sqb])
            for c3 in range(2):
                mm(PB[:, c3 * 512:(c3 + 1) * 512], lhsT=ones_b[:], rhs=sqb[:, c3, :], r=[b_onesb, b_sqb], w=[b_PB])
            act(lambda e: e.activation(out=rnorm[:].rearrange("p a b -> p (a b)"), in_=PB[:, 0:1024], func=AF.Ln, bias=EPS),
                r=[b_PB], w=[b_rnorm])
            act(lambda e: e.activation(out=rnorm[:].rearrange("p a b -> p (a b)"), in_=rnorm[:].rearrange("p a b -> p (a b)"),
                                       func=AF.Exp, scale=-0.5), w=[b_rnorm])
            dve(lambda e: e.scalar_tensor_tensor(out=gT[:, 0, :], in0=csil[:, 0, :], scalar=128.0 ** -0.5, in1=rnorm[:, 0, :],
                                                 op0=ALU.mult, op1=ALU.mult), r=[b_csil, b_rnorm], w=[b_gT])
            dve(lambda e: e.tensor_tensor(out=gT[:, 1, :], in0=csil[:, 1, :], in1=rnorm[:, 1, :], op=ALU.mult),
                r=[b_csil, b_rnorm], w=[b_gT])
            act(lambda e: e.copy(out=gT[:, 2, :], in_=csil[:, 2, :]), r=[b_csil], w=[b_gT])
            for tt in range(4):
                for c3 in range(3):
                    tr(PT[:, c3 * 128:(c3 + 1) * 128], gT[:, c3, tt * 128:(tt + 1) * 128], ident_b[:], r=[b_gT, b_identb], w=[b_PT])
                act(lambda e, tt=tt: e.copy(out=gtok[:, tt, :, :], in_=PT[:, 0:384].rearrange("p (c d) -> p c d", c=3)),
                    r=[b_PT], w=[b_gtok])
            yield
            a_ap = abg[gp][:, :, 0]
            b_ap = abg[gp][:, :, 1]
            act(lambda e: e.activation(out=gsc[:, G_G, :], in_=a_ap, func=AF.Exp, bias=hsc[:, 1:2]), r=[b_abg[gp], b_hsc], w=[b_gsc])
            act(lambda e: e.activation(out=gsc[:, G_G, :], in_=gsc[:, G_G, :], func=AF.Ln, bias=1.0), w=[b_gsc])
            dve(lambda e: e.tensor_scalar(out=gsc[:, G_G, :], in0=gsc[:, G_G, :], scalar1=negA[:, 0:1], scalar2=None, op0=ALU.mult),
                r=[b_negA], w=[b_gsc])
            act(lambda e: e.activation(out=gsc[:, G_BETA, :], in_=b_ap, func=AF.Exp, scale=-1.0), r=[b_abg[gp]], w=[b_gsc])
            dve(lambda e: e.tensor_scalar(out=gsc[:, G_BETA, :], in0=gsc[:, G_BETA, :], scalar1=1.0, scalar2=None, op0=ALU.add), w=[b_gsc])
            dve(lambda e: e.reciprocal(out=gsc[:, G_BETA, :], in_=gsc[:, G_BETA, :]), w=[b_gsc])
            act(lambda e: e.activation(out=gsc[:, G_LNB, :], in_=gsc[:, G_BETA, :], func=AF.Ln), w=[b_gsc])
            mm(PD[:, 0:4], lhsT=gmask[:, 2, :], rhs=gsc[:, G_G, :], r=[b_gmask, b_gsc], w=[b_PD])
            mm(PD[:, 4:8], lhsT=gmask[:, 3, :], rhs=gsc[:, G_G, :], r=[b_gmask, b_gsc], w=[b_PD])
            mm(PD[:, 8:12], lhsT=gmask[:, 4, :], rhs=gsc[:, G_G, :], r=[b_gmask, b_gsc], w=[b_PD])
            act(lambda e: e.copy(out=gsc[:, G_GCUM, :], in_=PD[:, 0:4]), r=[b_PD], w=[b_gsc])
            act(lambda e: e.copy(out=glc[:].rearrange("p a b -> p (a b)"), in_=PD[:, 4:12]), r=[b_PD], w=[b_glc])
            dve(lambda e: e.tensor_copy(out=gsc[0:64, G_GL, :], in_=glc[0:64, 0, :]), r=[b_glc], w=[b_gsc])
            dve(lambda e: e.tensor_copy(out=gsc[64:128, G_GL, :], in_=glc[64:128, 1, :]), r=[b_glc], w=[b_gsc])
            act(lambda e: e.activation(out=gsc[:, G_EG, :], in_=gsc[:, G_GCUM, :], func=AF.Exp), w=[b_gsc])
            dve(lambda e: e.tensor_tensor(out=gsc[:, G_EKG, :], in0=gsc[:, G_GL, :], in1=gsc[:, G_GCUM, :], op=ALU.subtract), w=[b_gsc])
            act(lambda e: e.activation(out=gsc[:, G_EKG, :], in_=gsc[:, G_EKG, :], func=AF.Exp), w=[b_gsc])
            act(lambda e: e.activation(out=glc[:].rearrange("p a b -> p (a b)"), in_=glc[:].rearrange("p a b -> p (a b)"), func=AF.Exp),
                w=[b_glc])
            dve(lambda e: e.tensor_scalar(out=gsc[:, G_NEGG, :], in0=gsc[:, G_GCUM, :], scalar1=-1.0, scalar2=None, op0=ALU.mult), w=[b_gsc])
            dve(lambda e: e.tensor_tensor(out=gsc[:, G_SKBG, :], in0=gsc[:, G_BETA, :], in1=gsc[:, G_EG, :], op=ALU.mult), w=[b_gsc])
            dve(lambda e: e.tensor_scalar(out=gsc[:, G_SKBG, :], in0=gsc[:, G_SKBG, :], scalar1=-1.0, scalar2=None, op0=ALU.mult), w=[b_gsc])
            dve(lambda e: e.tensor_tensor(out=gsc[:, G_GB, :], in0=gsc[:, G_GCUM, :], in1=gsc[:, G_LNB, :], op=ALU.add), w=[b_gsc])

            yield
            dve(lambda e: e.tensor_copy(out=glc8[:, 0:8:2], in_=glc[:, 0, :]), r=[b_glc], w=[b_glc8])
            dve(lambda e: e.tensor_copy(out=glc8[:, 1:8:2], in_=glc[:, 1, :]), r=[b_glc], w=[b_glc8])
            identf4 = bcast(ident_f[:], [128, 4, 128], 1)
            identb4 = bcast(ident_b[:], [128, 4, 128], 1)

            def colb(col):
                return bcast(gsc[:, col, :], [128, 4, 128], 2)
            dve(lambda e: e.tensor_tensor(out=dg1[:], in0=identf4, in1=colb(G_GCUM), op=ALU.mult), r=[b_identf, b_gsc], w=[b_dg1])
            dve(lambda e: e.tensor_tensor(out=dg2[:], in0=identf4, in1=colb(G_GB), op=ALU.mult), r=[b_identf, b_gsc], w=[b_dg2])
            dve(lambda e: e.tensor_tensor(out=dgn[:], in0=identf4, in1=colb(G_NEGG), op=ALU.mult), r=[b_identf, b_gsc], w=[b_dgn])
            for (PSx, bPSx, dgx, mk_) in ((PC[:, 0:512], b_PC, dg1, 0), (PA[:, 0:512], b_PA, dg2, 1)):
                mm(PSx, lhsT=ones_f[:], rhs=dgx[:].rearrange("p a b -> p (a b)"), start=True, stop=False, r=[b_onesf, b_dg1, b_dg2], w=[bPSx])
                mm(PSx, lhsT=ident_f[:], rhs=gmask4[:, mk_].rearrange("p a b -> p (a b)"), start=False, stop=False, r=[b_identf, b_gmask4], w=[bPSx])
                for tt in range(4):
                    mm(PSx[:, tt * 128:(tt + 1) * 128], lhsT=dgn[:, tt, :], rhs=ones_f[:], start=False, stop=True,
                       r=[b_dgn, b_onesf], w=[bPSx])
            act(lambda e: e.activation(out=decT[:].rearrange("p a b -> p (a b)"), in_=PC[:, 0:512], func=AF.Exp), r=[b_PC], w=[b_decT])
            act(lambda e: e.activation(out=decbT[:].rearrange("p a b -> p (a b)"), in_=PA[:, 0:512], func=AF.Exp), r=[b_PA], w=[b_decbT])
            yield
            for tt in range(4):
                kT_t = gT[:, 1, tt * 128:(tt + 1) * 128]
                qT_t = gT[:, 0, tt * 128:(tt + 1) * 128]
                mm(PC[:, 512 + tt * 128:512 + (tt + 1) * 128], lhsT=kT_t, rhs=kT_t, r=[b_gT], w=[b_PC])
                mm(PB[:, tt * 128:(tt + 1) * 128], lhsT=kT_t, rhs=qT_t, r=[b_gT], w=[b_PB])
            dve(lambda e: e.tensor_tensor(out=Um[0][:].rearrange("p a b -> p (a b)"), in0=PC[:, 512:1024], in1=decbT[:].rearrange("p a b -> p (a b)"),
                                          op=ALU.mult), r=[b_PC, b_decbT], w=[b_Um[0]])
            dve(lambda e: e.tensor_tensor(out=aqkT[:].rearrange("p a b -> p (a b)"), in0=PB[:, 0:512], in1=decT[:].rearrange("p a b -> p (a b)"),
                                          op=ALU.mult), r=[b_PB, b_decT], w=[b_aqkT])
            for tt in range(4):
                tr(PT[:, tt * 128:(tt + 1) * 128], Um[0][:, tt, :], ident_b[:], r=[b_Um[0], b_identb], w=[b_PT])
            act(lambda e: e.copy(out=Lm[0][:].rearrange("p a b -> p (a b)"), in_=PT[:, 0:512]), r=[b_PT], w=[b_Lm[0]])
            dve(lambda e: e.tensor_tensor(out=Pm[0][:], in0=identb4, in1=Um[0][:], op=ALU.subtract), r=[b_identb, b_Um[0]], w=[b_Pm[0]])
            yield
            cu, cp = 0, 0
            for lvl in range(5):
                nu = 1 - cu
                for tt in range(4):
                    mm(PC[:, tt * 128:(tt + 1) * 128], lhsT=Um[cu][:, tt, :], rhs=Lm[cu][:, tt, :], r=[b_Um[cu], b_Lm[cu]], w=[b_PC])
                if lvl < 4:
                    for tt in range(4):
                        mm(PA[:, tt * 128:(tt + 1) * 128], lhsT=Lm[cu][:, tt, :], rhs=Um[cu][:, tt, :], r=[b_Um[cu], b_Lm[cu]], w=[b_PA])
                act(lambda e, nu=nu: e.copy(out=Lm[nu][:].rearrange("p a b -> p (a b)"), in_=PC[:, 0:512]), r=[b_PC], w=[b_Lm[nu]])
                if lvl < 4:
                    act(lambda e, nu=nu: e.copy(out=Um[nu][:].rearrange("p a b -> p (a b)"), in_=PA[:, 0:512]), r=[b_PA], w=[b_Um[nu]])
                for tt in range(4):
                    mm(PB[:, tt * 128:(tt + 1) * 128], lhsT=Lm[nu][:, tt, :], rhs=Pm[cp][:, tt, :], r=[b_Lm[nu], b_Pm[cp]], w=[b_PB])
                dve(lambda e, cp=cp: e.tensor_tensor(out=Pm[1 - cp][:].rearrange("p a b -> p (a b)"), in0=PB[:, 0:512],
                                                     in1=Pm[cp][:].rearrange("p a b -> p (a b)"), op=ALU.add), r=[b_PB, b_Pm[cp]], w=[b_Pm[1 - cp]])
                cu = nu
                cp = 1 - cp
            yield
            Tt = Pm[cp]
            bTt = b_Pm[cp]
            dve(lambda e: e.tensor_tensor(out=Xm[:, :, 0:128], in0=gtok[:, :, 2, :], in1=colb(G_BETA), op=ALU.mult), r=[b_gtok, b_gsc], w=[b_Xm])
            dve(lambda e: e.tensor_tensor(out=Xm[:, :, 128:256], in0=gtok[:, :, 1, :], in1=colb(G_SKBG), op=ALU.mult), r=[b_gtok, b_gsc], w=[b_Xm])
            for tt in range(4):
                mm(PC[:, tt * 256:(tt + 1) * 256], lhsT=Tt[:, tt, :], rhs=Xm[:, tt, :], r=[bTt, b_Xm], w=[b_PC])
            act(lambda e: e.copy(out=uw[:].rearrange("p a b -> p (a b)"), in_=PC[:, 0:1024]), r=[b_PC], w=[b_uw])
            dve(lambda e: e.tensor_tensor(out=kgm[0:64, :, 0, :], in0=gtok[0:64, :, 1, :], in1=bcast(gsc[0:64, G_EKG, :], [64, 4, 128], 2), op=ALU.mult),
                r=[b_gtok, b_gsc], w=[b_kgm])
            dve(lambda e: e.tensor_tensor(out=kgm[64:128, :, 1, :], in0=gtok[64:128, :, 1, :], in1=bcast(gsc[64:128, G_EKG, :], [64, 4, 128], 2), op=ALU.mult),
                r=[b_gtok, b_gsc], w=[b_kgm])
            dve(lambda e: e.tensor_tensor(out=Dg[:], in0=identb4, in1=colb(G_EG), op=ALU.mult), r=[b_identb, b_gsc], w=[b_Dg])
            for tt in range(4):
                mm(PA[:, tt * 128:(tt + 1) * 128], lhsT=gtok[:, tt, 0, :], rhs=Dg[:, tt, :], start=True, stop=False, r=[b_gtok, b_Dg], w=[b_PA])
                mm(PA[:, tt * 128:(tt + 1) * 128], lhsT=uw[:, tt, 128:256], rhs=aqkT[:, tt, :], start=False, stop=True, r=[b_uw, b_aqkT], w=[b_PA])
            act(lambda e: e.copy(out=QpA[:, :, 0:64], in_=PA[:, 0:512].rearrange("p (a b) -> p a b", a=4)[:, :, 0:64]), r=[b_PA], w=[b_Qp])
            act(lambda e: e.copy(out=QpB[:, :, 64:128], in_=PA[:, 0:512].rearrange("p (a b) -> p a b", a=4)[:, :, 64:128]), r=[b_PA], w=[b_Qp])
            yield
            for tt in range(4):
                for c in range(2):
                    r0 = 64 * c
                    ch = tt * 2 + c
                    mm(PB[:, ch * 128:(ch + 1) * 128], lhsT=uw[:, tt, 128:256], rhs=kgm[:, tt, c, :], r=[b_uw, b_kgm], w=[b_PB])
            dve(lambda e: e.tensor_tensor(out=MTf[:], in0=bcast(ident_f[:], [128, 8, 128], 1), in1=bcast(glc8[:], [128, 8, 128], 2), op=ALU.mult),
                r=[b_identf, b_glc8], w=[b_MTf])
            dve(lambda e: e.tensor_tensor(out=MT8[:].rearrange("p a b -> p (a b)"), in0=PB[:, 0:1024], in1=MTf[:].rearrange("p a b -> p (a b)"),
                                          op=ALU.add), r=[b_PB, b_MTf], w=[b_MT8])
            for ch in range(8):
                tt, c = ch // 2, ch % 2
                r0 = 64 * c
                i = grp * 4 + tt
                PSc = PC[:, (ch % 2) * 512:(ch % 2) * 512 + 128]
                mm(PSc, lhsT=kgm[:, tt, c, :], rhs=uw[:, tt, 0:128], start=True, stop=False, r=[b_kgm, b_uw], w=[b_PC])
                mm(PSc, lhsT=MT8[:, ch, :], rhs=Sb9[:, ch, :], start=False, stop=True, r=[b_MT8, b_Sb9[ch]], w=[b_PC])
                if ch < 7:
                    act(lambda e, ch=ch, PSc=PSc: e.copy(out=Sb9[:, ch + 1, :], in_=PSc), r=[b_PC], w=[b_Sb9[ch + 1]])
                else:
                    act(lambda e, PSc=PSc: e.copy(out=Sb9[:, 8, :], in_=PSc), r=[b_PC], w=[b_Sb9[8]])
                    if i == NT - 1:
                        act(lambda e, PSc=PSc: e.copy(out=Sf[:], in_=PSc), r=[b_PC], w=[b_Sf])
            yield
            for tt in range(4):
                mm(PA[:, tt * 128:(tt + 1) * 128], lhsT=QpA[:, tt, :], rhs=Sb9[:, 2 * tt, :], start=True, stop=False, r=[b_Qp, b_Sb9[2 * tt]], w=[b_PA])
                mm(PA[:, tt * 128:(tt + 1) * 128], lhsT=QpB[:, tt, :], rhs=Sb9[:, 2 * tt + 1, :], start=False, stop=False,
                   r=[b_Qp, b_Sb9[2 * tt + 1]], w=[b_PA])
                mm(PA[:, tt * 128:(tt + 1) * 128], lhsT=aqkT[:, tt, :], rhs=uw[:, tt, 0:128], start=False, stop=True, r=[b_aqkT, b_uw], w=[b_PA])
            act(lambda e: e.copy(out=Sb9[:, 0, :], in_=Sb9[:, 8, :]), r=[b_Sb9[8]], w=[b_Sb9[0]])
            act(lambda e: e.copy(out=og4[:].rearrange("p a b -> p (a b)"), in_=PA[:, 0:512]), r=[b_PA], w=[b_og4])
            dve(lambda e: e.tensor_tensor(out=og4q[:], in0=og4[:], in1=og4[:], op=ALU.mult), r=[b_og4], w=[b_og4q])
            dve(lambda e: e.tensor_reduce(out=og4s[:], in_=og4q[:], axis=AX.X, op=ALU.add), r=[b_og4q], w=[b_og4s])
            act(lambda e: e.activation(out=og4s[:], in_=og4s[:], func=AF.Ln, scale=1.0 / 128, bias=EPS), w=[b_og4s])
            act(lambda e: e.activation(out=og4s[:], in_=og4s[:], func=AF.Exp, scale=-0.5), w=[b_og4s])
            dve(lambda e: e.tensor_tensor(out=og4[:], in0=og4[:], in1=bcast(og4s[:], [128, 4, 128], 2), op=ALU.mult), r=[b_og4s], w=[b_og4])
            dve(lambda e: e.tensor_tensor(out=og4[:], in0=og4[:], in1=bcast(gnb[:], [128, 4, 128], 1), op=ALU.mult), r=[b_gnb], w=[b_og4])
            dve(lambda e: e.tensor_tensor(out=omg[:], in0=og4[:], in1=zsil[gp][:], op=ALU.mult), r=[b_og4] + b_zsil[gp], w=[b_omg])
            for tt in range(4):
                tr(PT[:, tt * 128:(tt + 1) * 128], omg[:, tt, :], ident_b[:], r=[b_omg, b_identb], w=[b_PT])
            act(lambda e: e.copy(out=omT[gp][:, 1, :], in_=PT[:, 0:512]), r=[b_PT], w=[b_omT[gp]])
            for hh in range(2):
                if phaseB:
                    kch = (grp * 512) // CH
                    oc = (grp * 512) % CH
                    fw.dma("sp", xin[kch][hh * 128:(hh + 1) * 128, oc:oc + 512], omT[gp][:, hh, :], reads=[b_omT[gp]], writes=[b_xin[kch]])
                else:
                    fw.dma("sp", omT_o[hh * 128:(hh + 1) * 128, grp * 512:(grp + 1) * 512], omT[gp][:, hh, :], reads=[b_omT[gp]])


        def gen_G(grp):
            gp2 = grp % 2
            fw.dma("sp", csT[gp2][:, 0].rearrange("p a b -> p (a b)"), cos_d[:, grp * 128:(grp + 1) * 128], writes=[b_csT[gp2]])
            fw.dma("sp", csT[gp2][:, 1].rearrange("p a b -> p (a b)"), sin_d[:, grp * 128:(grp + 1) * 128], writes=[b_csT[gp2]])
            for tt in range(4):
                i = grp * 4 + tt
                s = i % NXB
                fw.dma("sp", xt[s][:], x_d[i * 128:(i + 1) * 128, :], writes=[b_xt[s]])
                act(lambda e, s=s: e.activation(out=xs[s][:], in_=xt[s][:], func=AF.Square, accum_out=ssq[s][:]),
                    r=[b_xt[s]], w=[b_xs[s], b_ssq[s]])
                act(lambda e, s=s: e.activation(out=ssq[s][:], in_=ssq[s][:], func=AF.Ln, scale=1.0 / 1024, bias=EPS), w=[b_ssq[s]])
                act(lambda e, s=s: e.activation(out=ssq[s][:], in_=ssq[s][:], func=AF.Exp, scale=-0.5), w=[b_ssq[s]])
                dve(lambda e, s=s: e.tensor_scalar(out=xs[s][:], in0=xt[s][:], scalar1=ssq[s][:, 0:1], scalar2=None,
                                                   op0=ALU.mult), r=[b_xt[s], b_ssq[s]], w=[b_xs[s]])
                for kt in range(8):
                    tr(PT[:, kt * 128:(kt + 1) * 128], xs[s][:, kt * 128:(kt + 1) * 128], ident_b[:],
                       r=[b_xs[s], b_identb], w=[b_PT])
                act(lambda e, tt=tt: e.copy(out=xnT[:, :, tt * 128:(tt + 1) * 128],
                                            in_=PT[:].rearrange("p (k t) -> p k t", k=8)),
                    r=[b_PT], w=[b_xnT[tt]])

            yield
            pool(lambda e: e.tensor_copy(out=raw[gp2][:, :, 0:3], in_=raw[1 - gp2][:, :, 512:515]), r=[b_raw[1 - gp2]], w=[b_raw[gp2]])
            for c3 in range(3):
                for kt in range(8):
                    mm(PB[:, 0:512], lhsT=wgdn[:, kt, c3 * 128:(c3 + 1) * 128], rhs=xnT[:, kt, :],
                       start=(kt == 0), stop=(kt == 7), r=[b_wgdn] + b_xnT, w=[b_PB])
                act(lambda e, c3=c3: e.copy(out=raw[gp2][:, c3, 3:515], in_=PB[:, 0:512]), r=[b_PB], w=[b_raw[gp2]])

            yield

        def gen_F(i):
            grp = i // 4
            tt = i % 4
            gp2 = grp % 2
            p2 = i % 2
            for kt in range(8):
                mm(PA[:, 0:512], lhsT=xnT[:, kt, tt * 128:(tt + 1) * 128], rhs=wtok[:, kt, 0:512],
                   start=(kt == 0), stop=(kt == 7), r=[b_xnT[tt], b_wtok], w=[b_PA])
            yield
            for kt in range(8):
                mm(PA[:, 512:782], lhsT=xnT[:, kt, tt * 128:(tt + 1) * 128], rhs=wtok[:, kt, 512:782],
                   start=(kt == 0), stop=(kt == 7), r=[b_xnT[tt], b_wtok], w=[b_PA])
            yield
            act(lambda e: e.copy(out=pj[p2][:], in_=PA[:, 0:782]), r=[b_PA], w=[b_pj[p2]])
            yield
            x1 = pj[p2][:, 0:448].rearrange("p (h d) -> p h d", h=7)[:, :, 0:32]
            x2 = pj[p2][:, 0:448].rearrange("p (h d) -> p h d", h=7)[:, :, 32:64]
            cosb = bcast(csT[gp2][:, 0, tt, :], [128, 7, 32], 1)
            sinb = bcast(csT[gp2][:, 1, tt, :], [128, 7, 32], 1)
            dve(lambda e: e.tensor_tensor(out=rt[:, 0], in0=x1, in1=cosb, op=ALU.mult), r=[b_pj[p2], b_csT[gp2]], w=[b_rt])
            yield
            dve(lambda e: e.tensor_tensor(out=rt[:, 1], in0=x2, in1=sinb, op=ALU.mult), r=[b_pj[p2], b_csT[gp2]], w=[b_rt])
            yield
            dve(lambda e: e.tensor_tensor(out=rt[:, 2], in0=x2, in1=cosb, op=ALU.mult), r=[b_pj[p2], b_csT[gp2]], w=[b_rt])
            yield
            dve(lambda e: e.tensor_tensor(out=rt[:, 3], in0=x1, in1=sinb, op=ALU.mult), r=[b_pj[p2], b_csT[gp2]], w=[b_rt])
            yield
            dve(lambda e: e.tensor_tensor(out=rq[p2][:, :, 0:32], in0=rt[:, 0], in1=rt[:, 1], op=ALU.subtract),
                 r=[b_rt], w=[b_rq[p2]])
            yield
            dve(lambda e: e.tensor_tensor(out=rq[p2][:, :, 32:64], in0=rt[:, 2], in1=rt[:, 3], op=ALU.add),
                 r=[b_rt], w=[b_rq[p2]])
            yield
            pool(lambda e: e.tensor_copy(out=ko[p2][:, 0:6:2, :], in_=rq[p2][:, 4:7, :]), r=[b_rq[p2]], w=[b_ko[p2]])
            yield
            pool(lambda e: e.tensor_copy(out=ko[p2][:, 1:6:2, :],
                                         in_=pj[p2][:, 448:640].rearrange("p (h d) -> p h d", h=3)),
                 r=[b_pj[p2]], w=[b_ko[p2]])
            yield
            fw.dma("sp", kv_o[i * 128:(i + 1) * 128, :], ko[p2][:, 0:4, :].rearrange("p a b -> p (a b)"), reads=[b_ko[p2]])
            yield
            if i >= NT - 4:
                wi = i - (NT - 4)
                fw.dma("sp", win_o[wi * 128:(wi + 1) * 128, :], ko[p2][:, 4:6, :].rearrange("p a b -> p (a b)"),
                       reads=[b_ko[p2]])
            yield
            act(lambda e: e.copy(out=qkb[:], in_=rq[p2][:]), r=[b_rq[p2]], w=[b_qkb])
            yield
            act(lambda e: e.copy(out=Vsel[:, i, 0:64], in_=pj[p2][:, 512:576]), r=[b_pj[p2]], w=[b_vsel[i]])
            yield
            act(lambda e: e.copy(out=Vwin[:, i % 8, 0:64], in_=pj[p2][:, 576:640]), r=[b_pj[p2]], w=[b_vwin[i % 8]])
            yield
            dve(lambda e: e.tensor_copy(out=kvc2[:, 0, :].rearrange("p (a d) -> p a d", a=2),
                                         in_=bcast(rq[p2][:, 4, :], [128, 2, 64], 1)), r=[b_rq[p2]], w=[b_kvc2])
            yield
            dve(lambda e: e.tensor_copy(out=kvc2[:, 1, :].rearrange("p (a d) -> p a d", a=2),
                                         in_=bcast(pj[p2][:, 448:512], [128, 2, 64], 1)), r=[b_pj[p2]], w=[b_kvc2])
            yield
            act(lambda e: e.activation(out=gz[:], in_=pj[p2][:, 640:780], func=AF.Exp, scale=-1.0), r=[b_pj[p2]], w=[b_gz])
            yield
            dve(lambda e: e.tensor_scalar(out=gz[:], in0=gz[:], scalar1=1.0, scalar2=None, op0=ALU.add), w=[b_gz])
            yield
            dve(lambda e: e.reciprocal(out=gz[:], in_=gz[:]), w=[b_gz])
            yield
            dve(lambda e: e.tensor_copy(out=gates[i % 2][:], in_=gz[:, 0:12]), r=[b_gz], w=[b_gates[i % 2]])
            yield
            dve(lambda e, tt=tt: e.tensor_tensor(out=zsil[gp2][:, tt, :], in0=gz[:, 12:140], in1=pj[p2][:, 652:780], op=ALU.mult),
                r=[b_gz, b_pj[p2]], w=[b_zsil[gp2][tt]])
            yield
            pool(lambda e, tt=tt: e.tensor_copy(out=abg[gp2][:, tt, :], in_=pj[p2][:, 780:782]), r=[b_pj[p2]], w=[b_abg[gp2]])
            yield
            for h in range(4):
                tr(PT[0:64, h * 128:(h + 1) * 128], qkb[:, h, :], ident_b[:], r=[b_qkb, b_identb], w=[b_PT])
            yield
            tr(PT[0:64, 512:640], qkb[:, 5, :], ident_b[:], r=[b_qkb, b_identb], w=[b_PT])
            yield
            tr(PT[0:64, 640:768], qkb[:, 6, :], ident_b[:], r=[b_qkb, b_identb], w=[b_PT])
            yield
            tr(PT[:, 768:896], kvc2[:, 0, :], ident_b[:], r=[b_kvc2, b_identb], w=[b_PT])
            yield
            tr(PT[:, 896:1024], kvc2[:, 1, :], ident_b[:], r=[b_kvc2, b_identb], w=[b_PT])
            yield
            act(lambda e: e.copy(out=QT[i % 2][:], in_=PT[0:64, 0:512]), r=[b_PT], w=[b_QT[i % 2]])
            yield
            act(lambda e: e.copy(out=Qaug[i % 2][0:64, 0, :], in_=PT[0:64, 0:256]), r=[b_PT], w=[b_Qaug[i % 2]])
            yield
            act(lambda e: e.copy(out=Qaug[i % 2][0:64, 1, :], in_=PT[0:64, 0:256]), r=[b_PT], w=[b_Qaug[i % 2]])
            yield
            act(lambda e: e.copy(out=KselT[0:64, i * 128:(i + 1) * 128], in_=PT[0:64, 512:640]),
                r=[b_PT], w=[b_ksel[i]])
            yield
            act(lambda e: e.copy(out=KwinT[0:64, (i % 8) * 128:(i % 8 + 1) * 128], in_=PT[0:64, 640:768]),
                r=[b_PT], w=[b_kwin[i % 8]])
            yield
            pool(lambda e: e.tensor_copy(out=Rk[:, :, 0:32], in_=Rk[:, :, 128:160]), w=[b_Rk])
            yield
            act(lambda e: e.copy(out=Rk[0:64, :, 32:160], in_=PT[0:64, 768:1024].rearrange("p (a t) -> p a t", a=2)),
                r=[b_PT], w=[b_Rk])
            yield
            act(lambda e: e.copy(out=Rk[64:128, :, 31:159], in_=PT[64:128, 768:1024].rearrange("p (a t) -> p a t", a=2)),
                r=[b_PT], w=[b_Rk])
            yield
            m0 = 1 if i == 0 else 0
            nb = 8 - m0
            n0 = 8 * i - 1 + m0
            for lp in range(16):
                c0 = 16 + 2 * lp + 16 * m0
                mm(PD[0:64, 0:nb], lhsT=cmpw[:, 0, lp, :], rhs=Rk[:, 0, c0:c0 + 16 * (nb - 1) + 1:16],
                   start=(lp == 0), stop=(lp == 15), r=[b_cmpw, b_Rk], w=[b_PD])
            yield
            act(lambda e: e.activation(out=ckT[:, n0:n0 + nb], in_=PD[0:64, 0:nb], func=AF.Identity, bias=ckb[:, 0:1]),
                r=[b_PD, b_ckb], w=[b_ckT])
            yield
            for lp in range(16):
                c0 = 16 + 2 * lp + 16 * m0
                mm(PD[0:nb, 64:128], lhsT=Rk[:, 1, c0:c0 + 16 * (nb - 1) + 1:16], rhs=cmpw[:, 1, lp, :],
                   start=(lp == 0), stop=(lp == 15), r=[b_cmpw, b_Rk], w=[b_PD])
            yield
            dve(lambda e: e.tensor_tensor(out=cvnew[0:nb, :], in0=PD[0:nb, 64:128], in1=cvb[0:nb, :], op=ALU.add),
                r=[b_PD, b_cvb], w=[b_cvnew])
            yield
            segs = []
            n = n0
            while n < n0 + nb:
                jt = n // 128
                cnt = min(n0 + nb - n, (jt + 1) * 128 - n)
                segs.append((n, cnt))
                n += cnt
            for (ns, cnt) in segs:
                fw.dma("sp", cvx[ns % 128:ns % 128 + cnt, ns // 128, 0:64], cvnew[ns - n0:ns - n0 + cnt, :],
                       reads=[b_cvnew], writes=[b_cvx])
            yield

            yield

        def gen_B(i):
            grp = i // 4
            tt = i % 4
            gp2 = grp % 2
            p2 = i % 2
            njt = (8 * i + 6) // 128 + 1
            for jt in range(njt):
                pb = jt
                mm(PA[:, 0:512], lhsT=ckT[:, jt * 128:(jt + 1) * 128], rhs=QT[i % 2][:], r=[b_ckT, b_QT[i % 2]], w=[b_PA])
                act(lambda e, pb=pb: e.activation(out=PTc[pb][:], in_=PA[:, 0:512], func=AF.Exp, scale=0.125),
                    r=[b_PA], w=[b_PTc[pb]])
                mk = None
                if jt == njt - 1:
                    mk = i % 16
                elif jt == njt - 2 and i % 16 == 0:
                    mk = 16
                if mk is not None:
                    dve(lambda e, mk=mk, pb=pb: e.tensor_tensor(out=PTc[pb][:].rearrange("p (h q) -> p h q", h=4),
                                                                in0=PTc[pb][:].rearrange("p (h q) -> p h q", h=4),
                                                                in1=bcast(cmpmask[:, mk, :], [128, 4, 128], 1), op=ALU.mult),
                        r=[b_cmpmask], w=[b_PTc[pb]])
            yield
            for h in range(4):
                for jt in range(njt):
                    mm(PC[:, (h // 2) * 512 + (h % 2) * 193:(h // 2) * 512 + (h % 2) * 193 + 193],
                       lhsT=PTc[jt][:, h * 128:(h + 1) * 128], rhs=cvx[:, jt, :],
                       start=(jt == 0), stop=(jt == njt - 1), r=[b_PTc[jt], b_cvx], w=[b_PC])
            yield
            act(lambda e: e.copy(out=acc_c[:, 0:2, :], in_=PC[:, 0:386].rearrange("p (h c) -> p h c", h=2)),
                r=[b_PC], w=[b_accc])
            yield
            act(lambda e: e.copy(out=acc_c[:, 2:4, :], in_=PC[:, 512:898].rearrange("p (h c) -> p h c", h=2)),
                r=[b_PC], w=[b_accc])
            yield
            dve(lambda e: e.tensor_scalar(out=rcp[:, 0:4], in0=acc_c[:, :, 192], scalar1=1e-30, scalar2=None, op0=ALU.max),
                r=[b_accc], w=[b_rcp])
            yield
            dve(lambda e: e.reciprocal(out=rcp[:, 0:4], in_=rcp[:, 0:4]), w=[b_rcp])
            yield
            dve(lambda e: e.tensor_scalar(out=imp[:], in0=acc_c[:, 0, 64:192], scalar1=rcp[:, 0:1], scalar2=None, op0=ALU.mult),
                r=[b_accc, b_rcp], w=[b_imp])
            yield
            for h in range(1, 4):
                dve(lambda e, h=h: e.scalar_tensor_tensor(out=imp[:], in0=acc_c[:, h, 64:192], scalar=rcp[:, h:h + 1],
                                                          in1=imp[:], op0=ALU.mult, op1=ALU.add),
                    r=[b_accc, b_rcp], w=[b_imp])
            yield
            dve(lambda e: e.tensor_tensor(out=score[:], in0=imp[:], in1=prel[:, 128 - 2 * i:256 - 2 * i], op=ALU.add),
                r=[b_imp, b_prel], w=[b_score])
            yield
            dve(lambda e: e.tensor_scalar(out=score[:, 0:1], in0=score[:, 0:1], scalar1=1e4, scalar2=None, op0=ALU.add),
                w=[b_score])
            yield
            dve(lambda e: e.max(out=mx8[:, 0:8], in_=score[:]), r=[b_score], w=[b_mx8])
            yield
            dve(lambda e: e.match_replace(out=sc2[:], in_to_replace=mx8[:, 0:8], in_values=score[:], imm_value=-3e38),
                r=[b_score, b_mx8], w=[b_sc2])
            yield
            dve(lambda e: e.max(out=mx8[:, 8:16], in_=sc2[:]), r=[b_sc2], w=[b_mx8])
            yield
            dve(lambda e: e.tensor_reduce(out=thr[:], in_=mx8[:, 8:16], axis=AX.X, op=ALU.min), r=[b_mx8], w=[b_thr])
            yield
            dve(lambda e: e.tensor_scalar(out=sc2[:], in0=score[:], scalar1=thr[:, 0:1], scalar2=None, op0=ALU.is_ge),
                r=[b_score, b_thr], w=[b_sc2])
            yield
            dve(lambda e: e.scalar_tensor_tensor(out=sc2[:], in0=score[:], scalar=-1e29, in1=sc2[:],
                                                 op0=ALU.is_gt, op1=ALU.mult), r=[b_score], w=[b_sc2])
            yield
            dve(lambda e: e.tensor_scalar(out=mbt[:, 0, :], in0=sc2[:], scalar1=-NEGB, scalar2=NEGB,
                                          op0=ALU.mult, op1=ALU.add), r=[b_sc2], w=[b_mbt])
            yield
            dve(lambda e: e.tensor_copy(out=mbt[:, 1, 0:64], in_=mbt[:, 0, 64:128]), w=[b_mbt])
            yield
            dve(lambda e: e.tensor_copy(out=mbt[:, 1, 64:128], in_=mbt[:, 0, 0:64]), w=[b_mbt])
            yield
            mm(PD[:, 0:128], lhsT=mbt[:, 1, :], rhs=ident_f[:], r=[b_mbt, b_identf], w=[b_PD])
            yield
            mm(PD[:, 128:256], lhsT=mbt[:, 0, :], rhs=ident_f[:], r=[b_mbt, b_identf], w=[b_PD])
            yield
            for hh_ in range(2):
                dve(lambda e, hh_=hh_: e.tensor_copy(out=Qaug[i % 2][64:128, 0, hh_ * 128:(hh_ + 1) * 128], in_=PD[64:128, 0:128]),
                    r=[b_PD], w=[b_Qaug[i % 2]])
                dve(lambda e, hh_=hh_: e.tensor_copy(out=Qaug[i % 2][64:128, 1, hh_ * 128:(hh_ + 1) * 128], in_=PD[64:128, 128:256]),
                    r=[b_PD], w=[b_Qaug[i % 2]])
            yield

            yield
            sgroups = []
            t = 0
            while t <= i:
                gt_ = min(4, i + 1 - t)
                sgroups.append((t, gt_))
                t += gt_

            def emit_S(gi_):
                t_, gt_ = sgroups[gi_]
                PSs_ = PA if gi_ % 2 == 0 else PB
                bPSs_ = b_PA if gi_ % 2 == 0 else b_PB
                for u in range(gt_):
                    tk = t_ + u
                    ab = 0 if tk < 32 else 1
                    mm(PSs_[:, u * 256:(u + 1) * 256], lhsT=KselT[:, tk * 128:(tk + 1) * 128], rhs=Qaug[i % 2][:, ab, :],
                       r=[b_ksel[tk], b_eind, b_Qaug[i % 2]], w=[bPSs_])
            emit_S(0)
            yield
            for gi in range(len(sgroups)):
                t, gt_ = sgroups[gi]
                pb = gi % 2
                PSs = PA if pb == 0 else PB
                bPSs = b_PA if pb == 0 else b_PB
                if gi + 1 < len(sgroups):
                    emit_S(gi + 1)
                yield
                act(lambda e, PSs=PSs, gt_=gt_, pb=pb: e.activation(out=PTs[pb][:, 0:gt_ * 256], in_=PSs[:, 0:gt_ * 256],
                                                                  func=AF.Exp, scale=0.125), r=[bPSs], w=[b_PTs[pb]])
                if t + gt_ - 1 == i:
                    u = gt_ - 1
                    dve(lambda e, u=u, pb=pb: e.tensor_tensor(
                        out=PTs[pb][:, u * 256:(u + 1) * 256].rearrange("p (h q) -> p h q", h=2),
                        in0=PTs[pb][:, u * 256:(u + 1) * 256].rearrange("p (h q) -> p h q", h=2),
                        in1=bcast(tri[:, 0, :], [128, 2, 128], 1), op=ALU.mult), r=[b_tri], w=[b_PTs[pb]])
                for u in range(gt_):
                    tk = t + u
                    for h in range(2):
                        mm(PC[:, h * 512:h * 512 + 65], lhsT=PTs[pb][:, u * 256 + h * 128:u * 256 + (h + 1) * 128],
                           rhs=Vsel[:, tk, :], start=(tk == 0), stop=(tk == i), r=[b_PTs[pb], b_vsel[tk]], w=[b_PC])
            yield
            gi = len(sgroups)
            yield
            act(lambda e: e.copy(out=acc_sw[:, 0, :], in_=PC[:, 0:65]), r=[b_PC], w=[b_accsw])
            yield
            act(lambda e: e.copy(out=acc_sw[:, 1, :], in_=PC[:, 512:577]), r=[b_PC], w=[b_accsw])
            yield

            yield
            t0w = max(0, i - 4)
            wt = list(range(t0w, i + 1))
            pb = gi % 2
            PSs = PA if pb == 0 else PB
            bPSs = b_PA if pb == 0 else b_PB
            for gsub in range(0, len(wt), 4):
                sub = wt[gsub:gsub + 4]
                for u, tk in enumerate(sub):
                    mm(PSs[:, u * 256:(u + 1) * 256], lhsT=KwinT[:, (tk % 8) * 128:(tk % 8 + 1) * 128], rhs=QT[i % 2][:, 0:256],
                       r=[b_kwin[tk % 8], b_QT[i % 2]], w=[bPSs])
                act(lambda e, PSs=PSs, n_=len(sub), pb=pb: e.activation(out=PTs[pb][:, 0:n_ * 256], in_=PSs[:, 0:n_ * 256],
                                                                       func=AF.Exp, scale=0.125), r=[bPSs], w=[b_PTs[pb]])
                for u, tk in enumerate(sub):
                    mkk = None
                    if tk == i:
                        mkk = 0
                    elif tk == i - 4:
                        mkk = 1
                    if mkk is not None:
                        dve(lambda e, u=u, pb=pb, mkk=mkk: e.tensor_tensor(
                            out=PTs[pb][:, u * 256:(u + 1) * 256].rearrange("p (h q) -> p h q", h=2),
                            in0=PTs[pb][:, u * 256:(u + 1) * 256].rearrange("p (h q) -> p h q", h=2),
                            in1=bcast(tri[:, mkk, :], [128, 2, 128], 1), op=ALU.mult), r=[b_tri], w=[b_PTs[pb]])
                for u, tk in enumerate(sub):
                    for h in range(2):
                        mm(PC[:, 386 + h * 512:451 + h * 512], lhsT=PTs[pb][:, u * 256 + h * 128:u * 256 + (h + 1) * 128],
                           rhs=Vwin[:, tk % 8, :], start=(tk == wt[0]), stop=(tk == i), r=[b_PTs[pb], b_vwin[tk % 8]], w=[b_PC])
                pb = 1 - pb
                PSs = PA if pb == 0 else PB
                bPSs = b_PA if pb == 0 else b_PB
            yield
            act(lambda e: e.copy(out=acc_sw[:, 2, :], in_=PC[:, 386:451]), r=[b_PC], w=[b_accsw])
            yield
            act(lambda e: e.copy(out=acc_sw[:, 3, :], in_=PC[:, 898:963]), r=[b_PC], w=[b_accsw])
            yield

            yield
            dve(lambda e: e.tensor_scalar(out=rcp[:, 4:8], in0=acc_sw[:, :, 64], scalar1=1e-30, scalar2=None, op0=ALU.max),
                r=[b_accsw], w=[b_rcp])
            yield
            dve(lambda e: e.reciprocal(out=rcp[:, 4:8], in_=rcp[:, 4:8]), w=[b_rcp])
            yield
            g3 = gates[i % 2][:, 0:6].rearrange("p (h j) -> p h j", h=2)
            cf = coef[:].rearrange("p (h j) -> p h j", h=2)
            dve(lambda e: e.tensor_tensor(out=cf[:, :, 0], in0=g3[:, :, 0], in1=rcp[:, 0:2], op=ALU.mult),
                r=[b_gates[i % 2], b_rcp], w=[b_coef])
            yield
            dve(lambda e: e.tensor_tensor(out=cf[:, :, 1], in0=g3[:, :, 1], in1=rcp[:, 4:6], op=ALU.mult),
                r=[b_gates[i % 2], b_rcp], w=[b_coef])
            yield
            dve(lambda e: e.tensor_tensor(out=cf[:, :, 2], in0=g3[:, :, 2], in1=rcp[:, 6:8], op=ALU.mult),
                r=[b_gates[i % 2], b_rcp], w=[b_coef])
            yield
            for h in range(2):
                dve(lambda e, h=h: e.tensor_scalar(out=onsa[:, h * 64:(h + 1) * 64], in0=acc_c[:, h, 0:64],
                                                   scalar1=coef[:, 3 * h:3 * h + 1], scalar2=None, op0=ALU.mult),
                    r=[b_accc, b_coef], w=[b_onsa])
                dve(lambda e, h=h: e.scalar_tensor_tensor(out=onsa[:, h * 64:(h + 1) * 64], in0=acc_sw[:, h, 0:64],
                                                          scalar=coef[:, 3 * h + 1:3 * h + 2], in1=onsa[:, h * 64:(h + 1) * 64],
                                                          op0=ALU.mult, op1=ALU.add), r=[b_accsw, b_coef], w=[b_onsa])
                dve(lambda e, h=h: e.scalar_tensor_tensor(out=om[:, h * 64:(h + 1) * 64], in0=acc_sw[:, 2 + h, 0:64],
                                                          scalar=coef[:, 3 * h + 2:3 * h + 3], in1=onsa[:, h * 64:(h + 1) * 64],
                                                          op0=ALU.mult, op1=ALU.add), r=[b_accsw, b_coef, b_onsa], w=[b_om])
            yield
            b_om_tiles = None

            if dbg and i == 1:
                fw.dma("sp", dbg_o[:, 0:772], acc_c[:].rearrange("p a b -> p (a b)"), reads=[b_accc])
                fw.dma("sp", dbg_o[:, 772:1032], acc_sw[:].rearrange("p a b -> p (a b)"), reads=[b_accsw])
                fw.dma("sp", dbg_o[:, 1032:1160], score[:], reads=[b_score])
                fw.dma("sp", dbg_o[:, 1160:1172], gates[i % 2][:], reads=[b_gates[i % 2]])
                fw.dma("sp", dbg_o[:, 1172:1300], mbt[:, 0, :], reads=[b_mbt])
            yield
            tr(PT[:, 0:128], om[:, 0:128], ident_b[:], r=[b_om, b_identb], w=[b_PT])
            yield
            act(lambda e, tt=tt: e.copy(out=omT[gp2][:, 0, tt * 128:(tt + 1) * 128], in_=PT[:, 0:128]), r=[b_PT], w=[b_omT[gp2]])
            yield

        if phaseB:
            wgu_s = nc.dram_tensor("wgu_s", [22, 128, 8 * 256], BF16).ap()
            wdn_s = nc.dram_tensor("wdn_s", [22, 128, 1024], BF16).ap()
            b_wgus = [Buf() for _ in range(22)]
            b_wdns = Buf()
            for f in range(22):
                dstv = wgu_s[f].rearrange("p (k c) -> p k c", k=8)
                fw.dma("pool", dstv[:, :, 0:128], wgu_d[:, f * 128:(f + 1) * 128].rearrange("(k p) c -> p k c", p=128), writes=[b_wgus[f]])
                fw.dma("pool", dstv[:, :, 128:256], wgu_d[:, 2816 + f * 128:2816 + (f + 1) * 128].rearrange("(k p) c -> p k c", p=128),
                       writes=[b_wgus[f]])
            fw.dma("pool", wdn_s[:, :, :], wdn_d[:, :].rearrange("(f p) c -> f p c", p=128), writes=[b_wdns])
        gdn_prev = None
        try:
          chk("setup")
          def drain_gen(g_):
              if g_ is not None:
                  for _ in g_:
                      pass

          def interleave(gens):
              alive = [g_ for g_ in gens if g_ is not None]
              while alive:
                  for g_ in list(alive):
                      try:
                          next(g_)
                      except StopIteration:
                          alive.remove(g_)

          drain_gen(gen_G(0))
          drain_gen(gen_F(0))
          for i in range(NT):
              grp = i // 4
              gF = None
              if i + 1 < NT:
                  def chain_next(i=i):
                      if (i + 1) % 4 == 0:
                          yield from gen_G((i + 1) // 4)
                      yield from gen_F(i + 1)
                  gF = chain_next()
              interleave([gen_B(i), gF])
              if gdn_prev is not None:
                  for _ in range(4):
                      next(gdn_prev, None)
              if i % 4 == 3:
                  drain_gen(gdn_prev)
                  gdn_prev = gdn_gen(grp)

        except _Stop:
            pass
        if gdn_prev is not None:
            for _ in gdn_prev:
                pass
        fw.dma("sp", S_o[:, :], Sf[:], reads=[b_Sf])
        fw.dma("sp", conv_o[:, :, :], raw[(NG - 1) % 2][:, :, 512:515], reads=[b_raw[(NG - 1) % 2]])

        if phaseB:
            xout = [nc.dram_tensor("xout%d" % k, [1024, CH], BF16).ap() for k in range(NCH)]
            b_xout = Buf()
            RG = [[0, 1, 2, 3], [4, 5, 6, 7]]
            ccs = st.enter_context(nc.semaphore("ccs"))
            for k in range(NCH if not SKIP_CC else 0):
                fw._wait("pool", b_xin[k].w)
                nc.gpsimd.collective_compute("AllGather", ALU.bypass, replica_groups=RG, ins=[xin[k][:, :].opt()],
                                             outs=[xout[k][:, :].opt()]).then_inc(ccs)
                nc.gpsimd.wait_ge(ccs, k + 1)
            pool(lambda e: e.memset(cvnew[0:1, 0:1], 0.0), w=[b_xout, b_cvnew])
            fw.barrier()
            stA.close()
            stS = st.enter_context(ExitStack())
            cur[0] = stS
            nb_ = [0]

            def T(shape, dt=F32):
                nb_[0] += 1
                return sb("sm%d" % nb_[0], shape, dt), Buf()

            xs_t, b_xs = T([4, 1024])
            gmix2, b_gmix2 = T([128, 8])
            ropes, b_ropes = T([4, 64])
            ptab, b_ptab = T([128, 256], I32)
            iota_c, b_iota = T([128, 1])
            idxs, b_idxs = T([128, 256], I32)
            cmpw64, b_cmpw64 = T([128, 2, 32, 64], BF16)
            pe64, b_pe64 = T([128, 2, 32], BF16)
            c2s2, b_c2s2 = T([128, 4, 128])
            on511, b_on511 = T([128, 4])
            oh4, b_oh4 = T([4, 80])
            bonus, b_bonus = T([1, 128])
            alogb, b_alogb = T([4, 8])
            gnrow, b_gnrow = T([1, 128])
            pjs, b_pjs = T([4, 3360])
            Xs, b_Xs = T([4, 2056])
            QKT, b_QKT = T([128, 14, 4], BF16)
            OGT, b_OGT = T([128, 4, 4], BF16)
            one11, b_one11 = T([1, 1])
            cb_s, b_cbs = T([64, 2])
            stSg = st.enter_context(ExitStack())
            cur[0] = stSg
            cwb, b_cwb = T([4, 4, 1536])
            stc, b_stc = T([4, 3, 1536])
            fw.dma("sp", xs_t[:], xs_d[:, :], writes=[b_xs])
            fw.dma("sp", gmix2[:], gmix_d[:, :], writes=[b_gmix2])
            fw.dma("sp", ropes[:], ropes_d[:, :], writes=[b_ropes])
            fw.dma("sp", ptab[:], ptab_d[:, :], writes=[b_ptab])
            fw.dma("sp", iota_c[:], iota_d[:, :], writes=[b_iota])
            fw.dma("pool", cmpw64[:].rearrange("p a b c -> p (a b c)"), cmpw64_d[:, :], writes=[b_cmpw64])
            fw.dma("pool", pe64[:].rearrange("p a b -> p (a b)"), pe64_d[:, :], writes=[b_pe64])
            fw.dma("sp", c2s2[:].rearrange("p a b -> p (a b)"), c2s_d[:, :], writes=[b_c2s2])
            fw.dma("sp", on511[:], ones511_d[:, :], writes=[b_on511])
            fw.dma("sp", oh4[:], oh4_d[:, :], writes=[b_oh4])
            fw.dma("sp", bonus[:], bonus_d[:, :], writes=[b_bonus])
            fw.dma("sp", alogb[:], alogb_d[:, :], writes=[b_alogb])
            fw.dma("sp", gnrow[:], gnrow_d[:, :], writes=[b_gnrow])
            fw.dma("sp", cwb[:].rearrange("p a b -> p (a b)"), convwb_d[:, :, :].rearrange("p a b -> p (a b)"), writes=[b_cwb])
            fw.dma("sp", stc[:].rearrange("p a b -> p (a b)"), gconv_d[:, :, :].rearrange("p a b -> p (a b)"), writes=[b_stc])
            dve(lambda e: e.tensor_scalar(out=idxs[:], in0=ptab[:], scalar1=128.0, scalar2=iota_c[:, 0:1], op0=ALU.mult, op1=ALU.add),
                r=[b_ptab, b_iota], w=[b_idxs])

            s_sq, b_ssq_ = T([4, 1024], BF16)
            s_ss, b_sss = T([4, 1])
            s_xn, b_sxn = T([4, 1024], BF16)
            xsT, b_xsT = T([128, 8, 4], BF16)
            wch0, b_wch0 = T([128, 8, 480], BF16)
            wch1, b_wch1 = T([128, 8, 480], BF16)
            wch = [wch0, wch1]; b_wch = [b_wch0, b_wch1]
            act(lambda e: e.activation(out=s_sq[:], in_=xs_t[:], func=AF.Square, accum_out=s_ss[:]), r=[b_xs], w=[b_ssq_, b_sss])
            act(lambda e: e.activation(out=s_ss[:], in_=s_ss[:], func=AF.Sqrt, scale=1.0 / 1024, bias=EPS), w=[b_sss])
            dve(lambda e: e.reciprocal(out=s_ss[:], in_=s_ss[:]), w=[b_sss])
            dve(lambda e: e.tensor_scalar(out=s_xn[:], in0=xs_t[:], scalar1=s_ss[:, 0:1], scalar2=None, op0=ALU.mult), r=[b_xs, b_sss], w=[b_sxn])
            for kt in range(8):
                tr(PT[:, kt * 4:(kt + 1) * 4], s_xn[0:4, kt * 128:(kt + 1) * 128], ident_b[0:4, 0:4], r=[b_sxn, b_identb], w=[b_PT])
            for kt in range(8):
                act(lambda e, kt=kt: e.activation(out=xsT[:, kt, :], in_=PT[:, kt * 4:(kt + 1) * 4], func=AF.Copy, scale=gmix2[:, kt:kt + 1]),
                    r=[b_PT, b_gmix2], w=[b_xsT])
            for ch in range(7):
                wb = ch % 2
                fw.dma("pool", wch[wb][:], win_full_d[:, ch * 480:(ch + 1) * 480].rearrange("(k p) c -> p k c", p=128), writes=[b_wch[wb]])
                for kt in range(8):
                    mm(PA[0:4, 0:480], lhsT=xsT[:, kt, :], rhs=wch[wb][:, kt, :], start=(kt == 0), stop=(kt == 7), r=[b_xsT, b_wch[wb]], w=[b_PA])
                act(lambda e, ch=ch: e.copy(out=pjs[:, ch * 480:(ch + 1) * 480], in_=PA[0:4, 0:480]), r=[b_PA], w=[b_pjs])

            chk2('s1')
            qkr, b_qkr = T([4, 14, 64])
            rqk, b_rqk = T([4, 14, 64])
            rts, b_rts = T([4, 4, 14, 32])
            kvv = pjs[:, 512:1280].rearrange("p (b k g d) -> p b k g d", b=3, k=2, g=2)
            pool(lambda e: e.tensor_copy(out=qkr[:, 0:8, :], in_=pjs[:, 0:512].rearrange("p (h d) -> p h d", h=8)), r=[b_pjs], w=[b_qkr])
            for br in range(3):
                pool(lambda e, br=br: e.tensor_copy(out=qkr[:, 8 + 2 * br:10 + 2 * br, :], in_=kvv[:, br, 0, :, :]), r=[b_pjs], w=[b_qkr])
            cs_b = bcast(ropes[:, 0:32], [4, 14, 32], 1)
            sn_b = bcast(ropes[:, 32:64], [4, 14, 32], 1)
            pool(lambda e: e.tensor_tensor(out=rts[:, 0], in0=qkr[:, :, 0:32], in1=cs_b, op=ALU.mult), r=[b_qkr, b_ropes], w=[b_rts])
            pool(lambda e: e.tensor_tensor(out=rts[:, 1], in0=qkr[:, :, 32:64], in1=sn_b, op=ALU.mult), r=[b_qkr, b_ropes], w=[b_rts])
            pool(lambda e: e.tensor_tensor(out=rts[:, 2], in0=qkr[:, :, 32:64], in1=cs_b, op=ALU.mult), r=[b_qkr, b_ropes], w=[b_rts])
            pool(lambda e: e.tensor_tensor(out=rts[:, 3], in0=qkr[:, :, 0:32], in1=sn_b, op=ALU.mult), r=[b_qkr, b_ropes], w=[b_rts])
            pool(lambda e: e.tensor_tensor(out=rqk[:, :, 0:32], in0=rts[:, 0], in1=rts[:, 1], op=ALU.subtract), r=[b_rts], w=[b_rqk])
            pool(lambda e: e.tensor_tensor(out=rqk[:, :, 32:64], in0=rts[:, 2], in1=rts[:, 3], op=ALU.add), r=[b_rts], w=[b_rqk])
            kvs_t, b_kvs = T([4, 4, 2, 64])
            wnew, b_wnew = T([4, 2, 2, 64])
            pool(lambda e: e.tensor_copy(out=kvs_t[:, 0], in_=rqk[:, 8:10, :]), r=[b_rqk], w=[b_kvs])
            pool(lambda e: e.tensor_copy(out=kvs_t[:, 1], in_=kvv[:, 0, 1, :, :]), r=[b_pjs], w=[b_kvs])
            pool(lambda e: e.tensor_copy(out=kvs_t[:, 2], in_=rqk[:, 10:12, :]), r=[b_rqk], w=[b_kvs])
            pool(lambda e: e.tensor_copy(out=kvs_t[:, 3], in_=kvv[:, 1, 1, :, :]), r=[b_pjs], w=[b_kvs])
            pool(lambda e: e.tensor_copy(out=wnew[:, 0], in_=rqk[:, 12:14, :]), r=[b_rqk], w=[b_wnew])
            pool(lambda e: e.tensor_copy(out=wnew[:, 1], in_=kvv[:, 2, 1, :, :]), r=[b_pjs], w=[b_wnew])
            fw.dma("sp", kvs_o[:, :], kvs_t[:].rearrange("p a b c -> p (a b c)"), reads=[b_kvs])
            fw.dma("sp", wins_o[:, 511, :], wnew[:].rearrange("p a b c -> p (a b c)"), reads=[b_wnew])
            for s_ in range(4):
                fw.dma("sp", wins_o[s_, 0:511, :], wincache_d[s_, 1:512, :])
            fw.dma("sp", convs_o[:, 0:2, :], gconv_d[:, 1:3, :])
            fw.dma("sp", convs_o[:, 2, :], pjs[:, 1304:2840], reads=[b_pjs])
            chk2('s2')
            qkb_s, b_qkbs = T([4, 14, 64], BF16)
            act(lambda e: e.copy(out=qkb_s[:], in_=rqk[:]), r=[b_rqk], w=[b_qkbs])
            for hd in range(14):
                tr(PT[0:64, hd * 4:(hd + 1) * 4], qkb_s[0:4, hd, :], ident_b[0:4, 0:4], r=[b_qkbs, b_identb], w=[b_PT])
            act(lambda e: e.copy(out=QKT[0:64].rearrange("p a b -> p (a b)"), in_=PT[0:64, 0:56]), r=[b_PT], w=[b_QKT])
            fw.dma("sp", QKT[64:128].rearrange("p a b -> p (a b)"), QKT[0:64].rearrange("p a b -> p (a b)"), reads=[b_QKT], writes=[b_QKT])

            chk2('s3')
            for kv in range(2):
                for l in range(32):
                    mm(PD[0:64, kv:kv + 1], lhsT=cmpw64[0:64, kv, l, :], rhs=pe64[0:64, kv, l:l + 1], start=(l == 0), stop=(l == 31),
                       r=[b_cmpw64, b_pe64], w=[b_PD])
            act(lambda e: e.copy(out=cb_s[:], in_=PD[0:64, 0:2]), r=[b_PD], w=[b_cbs])

            chk2('s3b')
            tmpc, b_tmpc = T([4, 1536])
            caccs, b_caccs = T([4, 1536])
            sqs, b_sqs = T([4, 1024])
            rn8, b_rn8 = T([4, 8])
            gsm, b_gsm = T([4, 8])
            dve(lambda e: e.tensor_tensor(out=caccs[:], in0=stc[:, 0, :], in1=cwb[:, 0, :], op=ALU.mult), r=[b_stc, b_cwb], w=[b_caccs])
            for jj in range(1, 4):
                src = stc[:, jj, :] if jj < 3 else pjs[:, 1304:2840]
                dve(lambda e, jj=jj, src=src: e.tensor_tensor(out=tmpc[:], in0=src, in1=cwb[:, jj, :], op=ALU.mult),
                    r=[b_stc, b_cwb, b_pjs], w=[b_tmpc])
                dve(lambda e: e.tensor_tensor(out=caccs[:], in0=caccs[:], in1=tmpc[:], op=ALU.add), r=[b_tmpc], w=[b_caccs])
            act(lambda e: e.activation(out=Xs[:, 0:1536], in_=caccs[:], func=AF.Silu), r=[b_caccs], w=[b_Xs])
            dve(lambda e: e.tensor_tensor(out=sqs[:], in0=Xs[:, 0:1024], in1=Xs[:, 0:1024], op=ALU.mult), r=[b_Xs], w=[b_sqs])
            dve(lambda e: e.tensor_reduce(out=rn8[:], in_=sqs[:].rearrange("p (h d) -> p h d", h=8), axis=AX.X, op=ALU.add), r=[b_sqs], w=[b_rn8])
            act(lambda e: e.activation(out=rn8[:], in_=rn8[:], func=AF.Sqrt, bias=EPS), w=[b_rn8])
            dve(lambda e: e.reciprocal(out=rn8[:], in_=rn8[:]), w=[b_rn8])
            dve(lambda e: e.tensor_scalar(out=rn8[:, 0:4], in0=rn8[:, 0:4], scalar1=128.0 ** -0.5, scalar2=None, op0=ALU.mult), w=[b_rn8])
            dve(lambda e: e.tensor_tensor(out=Xs[:, 0:1024].rearrange("p (h d) -> p h d", h=8), in0=Xs[:, 0:1024].rearrange("p (h d) -> p h d", h=8),
                                          in1=bcast(rn8[:], [4, 8, 128], 2), op=ALU.mult), r=[b_rn8], w=[b_Xs])
            act(lambda e: e.activation(out=Xs[:, 1536:2048], in_=pjs[:, 2840:3352], func=AF.Silu), r=[b_pjs], w=[b_Xs])
            act(lambda e: e.activation(out=Xs[:, 2048:2052], in_=pjs[:, 3356:3360], func=AF.Sigmoid), r=[b_pjs], w=[b_Xs])
            dve(lambda e: e.tensor_tensor(out=gsm[:, 0:4], in0=pjs[:, 3352:3356], in1=alogb[:, 4:8], op=ALU.add), r=[b_pjs, b_alogb], w=[b_gsm])
            act(lambda e: e.activation(out=gsm[:, 0:4], in_=gsm[:, 0:4], func=AF.Exp), w=[b_gsm])
            act(lambda e: e.activation(out=gsm[:, 0:4], in_=gsm[:, 0:4], func=AF.Ln, bias=1.0), w=[b_gsm])
            act(lambda e: e.activation(out=gsm[:, 4:8], in_=alogb[:, 0:4], func=AF.Exp), r=[b_alogb], w=[b_gsm])
            dve(lambda e: e.tensor_tensor(out=gsm[:, 0:4], in0=gsm[:, 0:4], in1=gsm[:, 4:8], op=ALU.mult), w=[b_gsm])
            act(lambda e: e.activation(out=Xs[:, 2052:2056], in_=gsm[:, 0:4], func=AF.Exp, scale=-1.0), r=[b_gsm], w=[b_Xs])

            chk2('s4')
            Rrow, b_Rrow = T([1, 2056])
            cols_s, b_cols = T([128, 8])
            S_t = [T([128, 128]) for _ in range(2)]
            r1, b_r1 = T([1, 257])
            vn, b_vn = T([1, 128])
            og, b_og = T([1, 128])
            ogs, b_ogs = T([1, 4])
            ogq, b_ogq = T([1, 128])
            egb, b_egb = T([128, 1])
            Snew = [T([128, 128]) for _ in range(2)]
            pool(lambda e: e.memset(one11[:], 1.0), w=[b_one11])
            for s_ in range(4):
                for chn, (c0, c1) in enumerate([(0, 512), (512, 1024), (1024, 1536), (1536, 2048), (2048, 2056)]):
                    mm(PD[0:1, 0:c1 - c0], lhsT=ident_f[0:4, s_:s_ + 1], rhs=Xs[0:4, c0:c1], r=[b_identf, b_Xs], w=[b_PD])
                    act(lambda e, c0=c0, c1=c1: e.copy(out=Rrow[0:1, c0:c1], in_=PD[0:1, 0:c1 - c0]), r=[b_PD], w=[b_Rrow])
                for hq in range(8):
                    mm(PD[:, hq:hq + 1], lhsT=Rrow[0:1, hq * 128:(hq + 1) * 128], rhs=one11[:], r=[b_Rrow, b_one11], w=[b_PD])
                act(lambda e: e.copy(out=cols_s[:], in_=PD[:, 0:8]), r=[b_PD], w=[b_cols])
                for h in range(4):
                    sbi = (s_ * 4 + h) % 2
                    St, bSt = S_t[sbi]
                    Sn, bSn = Snew[sbi]
                    fw.dma("sp", St[:], gS_d[s_ * 4 + h, :, :], writes=[bSt])
                    mm(PD[0:1, 0:128], lhsT=cols_s[:, 4 + h:5 + h], rhs=St[:], r=[b_cols, bSt], w=[b_PD])
                    mm(PD[0:1, 128:256], lhsT=cols_s[:, h:h + 1], rhs=St[:], r=[b_cols, bSt], w=[b_PD])
                    mm(PD[0:1, 256:257], lhsT=cols_s[:, h:h + 1], rhs=cols_s[:, 4 + h:5 + h], r=[b_cols], w=[b_PD])
                    act(lambda e: e.copy(out=r1[:], in_=PD[0:1, 0:257]), r=[b_PD], w=[b_r1])
                    egs = Rrow[0:1, 2052 + h:2053 + h]
                    bts = Rrow[0:1, 2048 + h:2049 + h]
                    dve(lambda e, egs=egs: e.tensor_scalar(out=vn[:], in0=r1[0:1, 0:128], scalar1=egs, scalar2=None, op0=ALU.mult),
                        r=[b_r1, b_Rrow], w=[b_vn])
                    dve(lambda e, h=h: e.tensor_tensor(out=vn[:], in0=Rrow[0:1, 1024 + h * 128:1024 + (h + 1) * 128], in1=vn[:], op=ALU.subtract),
                        r=[b_Rrow], w=[b_vn])
                    dve(lambda e, bts=bts: e.tensor_scalar(out=vn[:], in0=vn[:], scalar1=bts, scalar2=None, op0=ALU.mult), r=[b_Rrow], w=[b_vn])
                    dve(lambda e, egs=egs: e.tensor_scalar(out=og[:], in0=r1[0:1, 128:256], scalar1=egs, scalar2=None, op0=ALU.mult),
                        r=[b_r1, b_Rrow], w=[b_og])
                    dve(lambda e: e.scalar_tensor_tensor(out=og[:], in0=vn[:], scalar=r1[0:1, 256:257], in1=og[:], op0=ALU.mult, op1=ALU.add),
                        r=[b_vn, b_r1], w=[b_og])
                    act(lambda e: e.activation(out=ogq[:], in_=og[:], func=AF.Square, accum_out=ogs[0:1, 0:1]), r=[b_og], w=[b_ogq, b_ogs])
                    act(lambda e: e.activation(out=ogs[0:1, 0:1], in_=ogs[0:1, 0:1], func=AF.Sqrt, scale=1.0 / 128, bias=EPS), w=[b_ogs])
                    dve(lambda e: e.reciprocal(out=ogs[0:1, 0:1], in_=ogs[0:1, 0:1]), w=[b_ogs])
                    dve(lambda e: e.scalar_tensor_tensor(out=og[:], in0=og[:], scalar=ogs[0:1, 0:1], in1=gnrow[:], op0=ALU.mult, op1=ALU.mult),
                        r=[b_ogs, b_gnrow], w=[b_og])
                    dve(lambda e, h=h: e.tensor_tensor(out=og[:], in0=og[:], in1=Rrow[0:1, 1536 + h * 128:1536 + (h + 1) * 128], op=ALU.mult),
                        r=[b_Rrow], w=[b_og])
                    mm(PD[:, 300:301], lhsT=og[:], rhs=one11[:], r=[b_og, b_one11], w=[b_PD])
                    act(lambda e, h=h, s_=s_: e.copy(out=OGT[:, h, s_:s_ + 1], in_=PD[:, 300:301]), r=[b_PD], w=[b_OGT])
                    mm(PB[:, 0:128], lhsT=Rrow[0:1, 512 + h * 128:512 + (h + 1) * 128], rhs=vn[:], r=[b_Rrow, b_vn], w=[b_PB])
                    mm(PD[:, 310:311], lhsT=ones_f2[0:1, :], rhs=egs, r=[b_onesf2, b_Rrow], w=[b_PD])
                    act(lambda e: e.copy(out=egb[:], in_=PD[:, 310:311]), r=[b_PD], w=[b_egb])
                    dve(lambda e, St=St, Sn=Sn: e.scalar_tensor_tensor(out=Sn[:], in0=St[:], scalar=egb[:, 0:1], in1=PB[:, 0:128],
                                                                    op0=ALU.mult, op1=ALU.add), r=[bSt, b_egb, b_PB], w=[bSn])
                    fw.dma("sp", Ss_o[s_ * 4 + h, :, :], Sn[:], reads=[bSn])

            chk2('s5')
            fw.barrier()
            stSg.close()
            stSn = st.enter_context(ExitStack())
            cur[0] = stSn
            ckTs, b_ckTs = T([64, 512], BF16)
            cvTs, b_cvTs = T([64, 512], BF16)
            cvxs, b_cvxs = T([128, 4, 193], BF16)
            Pc, b_Pc = T([128, 16], BF16)
            accs, b_accs = T([4, 193])
            rcs, b_rcs = T([4, 4])
            impn, b_impn = T([4, 128])
            scs, b_scs = T([1, 136])
            sc2s, b_sc2s = T([1, 136])
            mx8s, b_mx8s = T([1, 16])
            thrs, b_thrs = T([1, 1])
            mbrow, b_mbrow = T([1, 128])
            MBp1, b_MBp1 = T([2, 64])
            MBp, b_MBp = T([2, 64, 4], BF16)
            efix, b_efix = T([2, 128], BF16)
            oh2, b_oh2 = T([1, 4])
            fw.dma("pool", efix[:], efix_d[:, :], writes=[b_efix])
            fw.dma("sp", oh2[:], oh2_d[:, :], writes=[b_oh2])
            Psel, b_Psel = T([128, 256], BF16)
            pnew, b_pnew = T([4, 2])
            vrow, b_vrow = T([4, 128])
            Abr, b_Abr = T([4, 3, 8, 64])
            asel, b_asel = T([4, 2, 65])
            wc, b_wc = T([128, 4, 256])
            wcb, b_wcb = T([128, 4, 128], BF16)
            Vws2 = [T([128, 4, 2, 65], BF16) for _ in range(2)]
            KwTs2 = [T([128, 4, 128], BF16) for _ in range(2)]
            Pw, b_Pw = T([128, 16], BF16)
            stSk = st.enter_context(ExitStack())
            cur[0] = stSk
            KTs2 = [T([128, 3, 8192], BF16) for _ in range(2)]
            Vs2 = [T([128, 64, 2, 65], BF16) for _ in range(2)]
            pg = [T([128, 512]) for _ in range(3)]
            pgb = [T([128, 384], BF16) for _ in range(2)]
            for q_ in range(2):
                pool(lambda e, q_=q_: e.memset(Vs2[q_][0][:, :, :, 64:65], 1.0), w=[Vs2[q_][1]])
                pool(lambda e, q_=q_: e.memset(Vws2[q_][0][:, :, :, 64:65], 1.0), w=[Vws2[q_][1]])
            pool(lambda e: e.memset(ckTs[:], 0.0), w=[b_ckTs])
            pool(lambda e: e.memset(cvTs[:], 0.0), w=[b_cvTs])
            pool(lambda e: e.memset(cvxs[:], 0.0), w=[b_cvxs])
            pool(lambda e: e.tensor_copy(out=cvxs[:, :, 64:192], in_=c2s2[:]), r=[b_c2s2], w=[b_cvxs])
            pool(lambda e: e.tensor_copy(out=cvxs[:, :, 192], in_=on511[:]), r=[b_on511], w=[b_cvxs])
            pool(lambda e: e.memset(scs[:], 1e4), w=[b_scs])
            def gen_pages(s_):
                KTs, b_KTs = KTs2[s_ % 2]
                Vs, b_Vs = Vs2[s_ % 2]
                Vws, b_Vws = Vws2[s_ % 2]
                KwTs, b_KwTs = KwTs2[s_ % 2]
                for p_ in range(64):
                    pgt, bpg = pg[p_ % 3]
                    pgbt, bpgb = pgb[p_ % 2]
                    col = s_ * 64 + p_
                    fw.dma("pool", None, None, reads=[b_idxs], writes=[bpg],
                           fn=lambda e, pgt=pgt, col=col: e.indirect_dma_start(
                               out=pgt[:], out_offset=None, in_=cache_d[:, :],
                               in_offset=bass.IndirectOffsetOnAxis(ap=idxs[:, col:col + 1], axis=0)))
                    dve(lambda e, pgt=pgt, pgbt=pgbt: e.tensor_copy(out=pgbt[:], in_=pgt[:, 0:384]), r=[bpg], w=[bpgb])
                    act(lambda e, pgt=pgt, p_=p_: e.copy(out=Vs[:, p_, :, 0:64], in_=pgt[:, 384:512].rearrange("p (g d) -> p g d", g=2)),
                        r=[bpg], w=[b_Vs])
                    for kg in range(3):
                        tr(PT[:, kg * 128:(kg + 1) * 128], pgbt[:, kg * 128:(kg + 1) * 128], ident_b[:], r=[bpgb, b_identb], w=[b_PT])
                    act(lambda e, p_=p_: e.copy(out=KTs[:, :, p_ * 128:(p_ + 1) * 128], in_=PT[:, 0:384].rearrange("p (a t) -> p a t", a=3)),
                        r=[b_PT], w=[b_KTs])
                    yield
                fw.dma("sp", wc[:], wincache_d[s_, :, :].rearrange("(t p) c -> p t c", p=128), writes=[b_wc])
                pool(lambda e: e.tensor_copy(out=wcb[:], in_=wc[:, :, 0:128]), r=[b_wc], w=[b_wcb])
                pool(lambda e: e.tensor_copy(out=Vws[:, :, :, 0:64], in_=wc[:, :, 128:256].rearrange("p t (g d) -> p t g d", g=2)),
                     r=[b_wc], w=[b_Vws])
                for t_ in range(4):
                    tr(PT[:, t_ * 128:(t_ + 1) * 128], wcb[:, t_, :], ident_b[:], r=[b_wcb, b_identb], w=[b_PT])
                act(lambda e: e.copy(out=KwTs[:].rearrange("p a b -> p (a b)"), in_=PT[:, 0:512]), r=[b_PT], w=[b_KwTs])
                yield

            def gen_attn(s_):
                KTs, b_KTs = KTs2[s_ % 2]
                Vs, b_Vs = Vs2[s_ % 2]
                Vws, b_Vws = Vws2[s_ % 2]
                KwTs, b_KwTs = KwTs2[s_ % 2]
                for g_ in range(2):
                    sg = s_ * 2 + g_
                    Qg = QKT[0:64, 4 * g_:4 * g_ + 4, s_]
                    g0_, g1_ = g_ * 64, (g_ + 1) * 64
                    Qgg = QKT[g0_:g1_, 4 * g_:4 * g_ + 4, s_]
                    for kv in range(2):
                        for l in range(32):
                            mm(PA[0:64, 0:511], lhsT=cmpw64[g0_:g1_, kv, l, :], rhs=KTs[g0_:g1_, kv, l:l + 16 * 510 + 1:16],
                               start=(l == 0), stop=(l == 31), r=[b_cmpw64, b_KTs], w=[b_PA])
                        dst = ckTs if kv == 0 else cvTs
                        bd = b_ckTs if kv == 0 else b_cvTs
                        act(lambda e, dst=dst, kv=kv: e.activation(out=dst[:, 0:511], in_=PA[0:64, 0:511], func=AF.Identity, bias=cb_s[:, kv:kv + 1]),
                            r=[b_PA, b_cbs], w=[bd])
                    yield
                    for jt in range(4):
                        tr(PT[:, jt * 64:(jt + 1) * 64], cvTs[:, jt * 128:(jt + 1) * 128], ident_b[0:64, 0:64], r=[b_cvTs, b_identb], w=[b_PT])
                    act(lambda e: e.copy(out=cvxs[:, :, 0:64], in_=PT[:, 0:256].rearrange("p (a d) -> p a d", a=4)), r=[b_PT], w=[b_cvxs])
                    yield
                    for jt in range(4):
                        mm(PD[:, jt * 4:(jt + 1) * 4], lhsT=ckTs[:, jt * 128:(jt + 1) * 128], rhs=Qg, r=[b_ckTs, b_QKT], w=[b_PD])
                    act(lambda e: e.activation(out=Pc[:], in_=PD[:, 0:16], func=AF.Exp, scale=0.125), r=[b_PD], w=[b_Pc])
                    for jt in range(4):
                        mm(PB[0:4, 0:193], lhsT=Pc[:, jt * 4:(jt + 1) * 4], rhs=cvxs[:, jt, :], start=(jt == 0), stop=(jt == 3),
                           r=[b_Pc, b_cvxs], w=[b_PB])
                    act(lambda e: e.copy(out=accs[:], in_=PB[0:4, 0:193]), r=[b_PB], w=[b_accs])
                    dve(lambda e: e.tensor_scalar(out=rcs[:, 0:1], in0=accs[:, 192:193], scalar1=1e-30, scalar2=None, op0=ALU.max), r=[b_accs], w=[b_rcs])
                    dve(lambda e: e.reciprocal(out=rcs[:, 0:1], in_=rcs[:, 0:1]), w=[b_rcs])
                    dve(lambda e, sg=sg: e.tensor_scalar(out=Abr[:, 0, sg, :], in0=accs[:, 0:64], scalar1=rcs[:, 0:1], scalar2=None, op0=ALU.mult),
                        r=[b_accs, b_rcs], w=[b_Abr])
                    dve(lambda e: e.tensor_scalar(out=impn[:], in0=accs[:, 64:192], scalar1=rcs[:, 0:1], scalar2=None, op0=ALU.mult),
                        r=[b_accs, b_rcs], w=[b_impn])
                    mm(PD[0:1, 64:192], lhsT=ones_f2[0:4, 0:1], rhs=impn[:], r=[b_onesf2, b_impn], w=[b_PD])
                    yield
                    dve(lambda e: e.tensor_tensor(out=scs[0:1, 0:128], in0=PD[0:1, 64:192], in1=bonus[:], op=ALU.add), r=[b_PD, b_bonus], w=[b_scs])
                    dve(lambda e: e.max(out=mx8s[:, 0:8], in_=scs[0:1, 0:129]), r=[b_scs], w=[b_mx8s])
                    dve(lambda e: e.match_replace(out=sc2s[0:1, 0:129], in_to_replace=mx8s[:, 0:8], in_values=scs[0:1, 0:129], imm_value=-3e38),
                        r=[b_scs, b_mx8s], w=[b_sc2s])
                    dve(lambda e: e.max(out=mx8s[:, 8:16], in_=sc2s[0:1, 0:129]), r=[b_sc2s], w=[b_mx8s])
                    dve(lambda e: e.tensor_reduce(out=thrs[:], in_=mx8s[:, 8:16], axis=AX.X, op=ALU.min), r=[b_mx8s], w=[b_thrs])
                    dve(lambda e: e.tensor_scalar(out=mbrow[:], in0=scs[0:1, 0:128], scalar1=thrs[0:1, 0:1], scalar2=None, op0=ALU.is_ge),
                        r=[b_scs, b_thrs], w=[b_mbrow])
                    dve(lambda e: e.tensor_scalar(out=mbrow[:], in0=mbrow[:], scalar1=-NEGB, scalar2=NEGB, op0=ALU.mult, op1=ALU.add), w=[b_mbrow])
                    mm(PD[0:2, 200:264], lhsT=oh2[0:1, 0:2], rhs=mbrow[0:1, 0:128:2], start=True, stop=False, r=[b_oh2, b_mbrow], w=[b_PD])
                    mm(PD[0:2, 200:264], lhsT=oh2[0:1, 2:4], rhs=mbrow[0:1, 1:128:2], start=False, stop=True, r=[b_oh2, b_mbrow], w=[b_PD])
                    act(lambda e: e.copy(out=MBp1[:], in_=PD[0:2, 200:264]), r=[b_PD], w=[b_MBp1])
                    dve(lambda e: e.tensor_copy(out=MBp[:], in_=bcast(MBp1[:], [2, 64, 4], 2)), r=[b_MBp1], w=[b_MBp])
                    yield
                    for t_ in range(64):
                        a_ = 0 if t_ < 32 else 1
                        mm(PA[:, 512 + t_ * 4:512 + (t_ + 1) * 4], lhsT=KTs[g0_:g1_, 2, t_ * 128:(t_ + 1) * 128], rhs=Qgg, start=True, stop=False,
                           r=[b_KTs, b_QKT], w=[b_PA])
                        mm(PA[:, 512 + t_ * 4:512 + (t_ + 1) * 4], lhsT=efix[0:2, :], rhs=MBp[0:2, t_, :], start=False, stop=True,
                           r=[b_efix, b_MBp], w=[b_PA])
                    act(lambda e: e.activation(out=Psel[:], in_=PA[:, 512:768], func=AF.Exp, scale=0.125), r=[b_PA], w=[b_Psel])
                    for t_ in range(64):
                        mm(PB[0:4, 256:321], lhsT=Psel[:, t_ * 4:(t_ + 1) * 4], rhs=Vs[:, t_, g_, :], start=(t_ == 0), stop=(t_ == 63),
                           r=[b_Psel, b_Vs], w=[b_PB])
                    yield
                    mm(PD[0:4, 210:211], lhsT=Qg, rhs=QKT[0:64, 10 + g_, s_:s_ + 1], r=[b_QKT], w=[b_PD])
                    mm(PD[0:4, 211:212], lhsT=Qg, rhs=QKT[0:64, 12 + g_, s_:s_ + 1], r=[b_QKT], w=[b_PD])
                    act(lambda e: e.activation(out=pnew[:], in_=PD[0:4, 210:212], func=AF.Exp, scale=0.125), r=[b_PD], w=[b_pnew])
                    mm(PD[0:4, 220:284], lhsT=oh4[0:4, s_ * 4:(s_ + 1) * 4], rhs=kvv[:, 1, 1, g_, :], r=[b_oh4, b_pjs], w=[b_PD])
                    mm(PD[0:4, 284:348], lhsT=oh4[0:4, s_ * 4:(s_ + 1) * 4], rhs=kvv[:, 2, 1, g_, :], r=[b_oh4, b_pjs], w=[b_PD])
                    act(lambda e: e.copy(out=vrow[:], in_=PD[0:4, 220:348]), r=[b_PD], w=[b_vrow])
                    dve(lambda e: e.scalar_tensor_tensor(out=asel[:, 0, 0:64], in0=vrow[:, 0:64], scalar=pnew[:, 0:1], in1=PB[0:4, 256:320],
                                                         op0=ALU.mult, op1=ALU.add), r=[b_vrow, b_pnew, b_PB], w=[b_asel])
                    dve(lambda e: e.tensor_tensor(out=asel[:, 0, 64:65], in0=PB[0:4, 320:321], in1=pnew[:, 0:1], op=ALU.add),
                        r=[b_PB, b_pnew], w=[b_asel])
                    yield
                    for t_ in range(4):
                        mm(PD[:, 352 + t_ * 4:356 + t_ * 4], lhsT=KwTs[g0_:g1_, t_, :], rhs=Qgg, r=[b_KwTs, b_QKT], w=[b_PD])
                    act(lambda e: e.activation(out=Pw[:], in_=PD[:, 352:368], func=AF.Exp, scale=0.125), r=[b_PD], w=[b_Pw])
                    for t_ in range(4):
                        mm(PB[0:4, 384:449], lhsT=Pw[:, t_ * 4:(t_ + 1) * 4], rhs=Vws[:, t_, g_, :], start=(t_ == 0), stop=(t_ == 3),
                           r=[b_Pw, b_Vws], w=[b_PB])
                    dve(lambda e: e.scalar_tensor_tensor(out=asel[:, 1, 0:64], in0=vrow[:, 64:128], scalar=pnew[:, 1:2], in1=PB[0:4, 384:448],
                                                         op0=ALU.mult, op1=ALU.add), r=[b_vrow, b_pnew, b_PB], w=[b_asel])
                    dve(lambda e: e.tensor_tensor(out=asel[:, 1, 64:65], in0=PB[0:4, 448:449], in1=pnew[:, 1:2], op=ALU.add),
                        r=[b_PB, b_pnew], w=[b_asel])
                    dve(lambda e: e.reciprocal(out=rcs[:, 1:3], in_=asel[:, :, 64]), r=[b_asel], w=[b_rcs])
                    for br in range(2):
                        dve(lambda e, br=br, sg=sg: e.tensor_scalar(out=Abr[:, 1 + br, sg, :], in0=asel[:, br, 0:64], scalar1=rcs[:, 1 + br:2 + br],
                                                                    scalar2=None, op0=ALU.mult), r=[b_asel, b_rcs], w=[b_Abr])

                yield

            def drain_gen2(g_):
                for _ in g_:
                    pass

            def interleave2(gens):
                alive = [g_ for g_ in gens if g_ is not None]
                while alive:
                    for g_ in list(alive):
                        try:
                            next(g_)
                        except StopIteration:
                            alive.remove(g_)
            drain_gen2(gen_pages(0))
            for s_ in range(4):
                interleave2([gen_attn(s_), gen_pages(s_ + 1) if s_ < 3 else None])
            fw.barrier()
            stSk.close()
            cur[0] = stSn
            woutn, b_woutn = T([64, 8, 1024], BF16)
            woutg, b_woutg = T([128, 4, 1024], BF16)
            fw.dma("pool", woutn[:].rearrange("p a b -> p (a b)"), woutn_d[:, :], writes=[b_woutn])
            fw.dma("pool", woutg[:].rearrange("p a b -> p (a b)"), woutg_d[:, :], writes=[b_woutg])
            chk2('s11')
            gts, b_gts = T([4, 8, 3])
            osum, b_osum = T([4, 8, 64])
            otmp, b_otmp = T([4, 8, 64])
            onb, b_onb = T([4, 8, 64], BF16)
            OT, b_OT = T([64, 8, 4], BF16)
            act(lambda e: e.activation(out=gts[:].rearrange("p a b -> p (a b)"), in_=pjs[:, 1280:1304], func=AF.Sigmoid), r=[b_pjs], w=[b_gts])
            for br in range(3):
                for g_ in range(2):
                    for r_ in range(4):
                        h_ = 4 * g_ + r_
                        for s_ in range(4):
                            mm(PC[0:4, h_ * 64:(h_ + 1) * 64], lhsT=oh4[0:4, 16 + (r_ * 4 + s_) * 4:16 + (r_ * 4 + s_ + 1) * 4],
                               rhs=Abr[:, br, s_ * 2 + g_, :], start=(s_ == 0), stop=(s_ == 3), r=[b_oh4, b_Abr], w=[b_PC])
                gb = bcast(gts[:, :, br], [4, 8, 64], 2)
                if br == 0:
                    dve(lambda e, gb=gb: e.tensor_tensor(out=osum[:], in0=PC[0:4, 0:512].rearrange("p (h d) -> p h d", h=8), in1=gb, op=ALU.mult),
                        r=[b_PC, b_gts], w=[b_osum])
                else:
                    dve(lambda e, gb=gb: e.tensor_tensor(out=otmp[:], in0=PC[0:4, 0:512].rearrange("p (h d) -> p h d", h=8), in1=gb, op=ALU.mult),
                        r=[b_PC, b_gts], w=[b_otmp])
                    dve(lambda e: e.tensor_tensor(out=osum[:], in0=osum[:], in1=otmp[:], op=ALU.add), r=[b_otmp], w=[b_osum])
            act(lambda e: e.copy(out=onb[:], in_=osum[:]), r=[b_osum], w=[b_onb])
            for h_ in range(8):
                tr(PT[0:64, h_ * 4:(h_ + 1) * 4], onb[0:4, h_, :], ident_b[0:4, 0:4], r=[b_onb, b_identb], w=[b_PT])
            act(lambda e: e.copy(out=OT[:].rearrange("p a b -> p (a b)"), in_=PT[0:64, 0:32]), r=[b_PT], w=[b_OT])
            chk2('s12')
            for half in range(2):
                for h_ in range(8):
                    mm(PA[0:4, half * 512:(half + 1) * 512], lhsT=OT[:, h_, :], rhs=woutn[:, h_, half * 512:(half + 1) * 512],
                       start=(h_ == 0), stop=False, r=[b_OT, b_woutn], w=[b_PA])
                for h_ in range(4):
                    mm(PA[0:4, half * 512:(half + 1) * 512], lhsT=OGT[:, h_, :], rhs=woutg[:, h_, half * 512:(half + 1) * 512],
                       start=False, stop=(h_ == 3), r=[b_OGT, b_woutg], w=[b_PA])
            dve(lambda e: e.tensor_tensor(out=hs_res[:], in0=PA[0:4, :], in1=xs_t[:], op=ALU.add), r=[b_PA, b_xs], w=[b_hsres])
            fw.barrier()
            stSn.close()
            stS.close()
            stB = st.enter_context(ExitStack())
            cur[0] = stB
            wout = sb("wout", [128, 8, 1024], BF16); b_wout = Buf()
            wdn = sb("wdn", [128, 22, 1024], BF16); b_wdn = Buf()
            nfb = sb("nfb", [128, 1024]); b_nfb = Buf()
            gffn = sb("gffn", [128, 8]); b_gffn = Buf()
            sel4 = sb("sel4_sb", [128, 4]); b_sel4 = Buf()
            cand = [sb("cand%d" % i, [128, 8, 512], BF16) for i in range(2)]; b_cand = [Buf() for _ in range(2)]
            mixT = sb("mixT", [128, 8, 512], BF16); b_mixT = Buf()
            xt2 = [sb("xt2_%d" % i, [128, 1024]) for i in range(2)]; b_xt2 = [Buf() for _ in range(2)]
            hres = sb("hres", [128, 4, 1024]); b_hres = [Buf() for _ in range(4)]
            hsq = sb("hsq", [128, 1024], BF16); b_hsq = Buf()
            hss = sb("hss", [128, 1]); b_hss = Buf()
            hs = sb("hs", [128, 1024], BF16); b_hs = Buf()
            hnT = sb("hnT", [128, 8, 512], BF16); b_hnT = Buf()
            wg = [sb("wg%d" % i, [128, 8, 256], BF16) for i in range(3)]; b_wg = [Buf() for _ in range(3)]
            sg = [sb("sg%d" % i, [128, 512]) for i in range(2)]; b_sg = [Buf() for _ in range(2)]
            actT = sb("actT", [128, 22, 512], BF16); b_actT = Buf()
            yb = [sb("yb%d" % i, [128, 1024]) for i in range(2)]; b_yb = [Buf() for _ in range(2)]
            ysq = sb("ysq", [128, 1024], BF16); b_ysq = Buf()
            yss = sb("yss", [128, 1]); b_yss = Buf()
            hsn_ss = sb("hsn_ss", [4, 1]); b_hsnss = Buf()
            hsn_sq = sb("hsn_sq", [4, 1024], BF16); b_hsnsq = Buf()
            hsn = sb("hsn", [4, 1024], BF16); b_hsn = Buf()
            hnTs = sb("hnTs", [128, 8, 4], BF16); b_hnTs = Buf()
            sgs = sb("sgs", [128, 4]); b_sgs = Buf()
            actTs = sb("actTs", [128, 22, 4], BF16); b_actTs = Buf()
            ysb = sb("ysb", [4, 1024]); b_ysb = Buf()
            fw.dma("sp", nfb[:], nfin_d[:, :], writes=[b_nfb])
            fw.dma("sp", gffn[:], gffn_d[:, :], writes=[b_gffn])
            fw.dma("sp", sel4[:], sel4_d[:, :], writes=[b_sel4])
            fw.dma("pool", wout[:], wout_d[:, :].rearrange("(k p) c -> p k c", p=128), writes=[b_wout])
            fw.dma("sp", wdn[:], wdn_s[:, :, :].rearrange("f p c -> p f c"), reads=[b_wdns], writes=[b_wdn])
            act(lambda e: e.activation(out=hsn_sq[:], in_=hs_res[:], func=AF.Square, accum_out=hsn_ss[:]), r=[b_hsres], w=[b_hsnsq, b_hsnss])
            act(lambda e: e.activation(out=hsn_ss[:], in_=hsn_ss[:], func=AF.Sqrt, scale=1.0 / 1024, bias=EPS), w=[b_hsnss])
            dve(lambda e: e.reciprocal(out=hsn_ss[:], in_=hsn_ss[:]), w=[b_hsnss])
            dve(lambda e: e.tensor_scalar(out=hsn[:], in0=hs_res[:], scalar1=hsn_ss[:, 0:1], scalar2=None, op0=ALU.mult), r=[b_hsres, b_hsnss], w=[b_hsn])
            for kt in range(8):
                tr(PT[:, kt * 4:(kt + 1) * 4], hsn[0:4, kt * 128:(kt + 1) * 128], ident_b[0:4, 0:4], r=[b_hsn, b_identb], w=[b_PT])
            for kt in range(8):
                act(lambda e, kt=kt: e.activation(out=hnTs[:, kt, :], in_=PT[:, kt * 4:(kt + 1) * 4], func=AF.Copy, scale=gffn[:, kt:kt + 1]),
                    r=[b_PT, b_gffn], w=[b_hnTs])
            wgi = 0
            for bi in range(NB):
                for j4 in range(4):
                    cb = j4 % 2
                    c0 = j4 * TB + bi * 512
                    fw.dma("sp", cand[cb][:], xout[c0 // CH][:, c0 % CH:c0 % CH + 512].rearrange("(k p) t -> p k t", p=128),
                           reads=[b_xout], writes=[b_cand[cb]])
                    if j4 == 0:
                        dve(lambda e, cb=cb: e.tensor_scalar(out=mixT[:], in0=cand[cb][:], scalar1=sel4[:, 0:1], scalar2=None, op0=ALU.mult),
                            r=[b_cand[cb], b_sel4], w=[b_mixT])
                    else:
                        dve(lambda e, cb=cb, j4=j4: e.scalar_tensor_tensor(out=mixT[:], in0=cand[cb][:], scalar=sel4[:, j4:j4 + 1], in1=mixT[:],
                                                                           op0=ALU.mult, op1=ALU.add), r=[b_cand[cb], b_sel4], w=[b_mixT])
                for tt in range(4):
                    r0 = bi * 512 + tt * 128
                    xs_ = tt % 2
                    fw.dma("sp", xt2[xs_][:], xown_d[r0:r0 + 128, :], writes=[b_xt2[xs_]])
                    for half in range(2):
                        for kt in range(8):
                            mm(PA[:, half * 512:(half + 1) * 512], lhsT=mixT[:, kt, tt * 128:(tt + 1) * 128],
                               rhs=wout[:, kt, half * 512:(half + 1) * 512], start=(kt == 0), stop=(kt == 7),
                               r=[b_mixT, b_wout], w=[b_PA])
                    dve(lambda e, tt=tt, xs_=xs_: e.tensor_tensor(out=hres[:, tt, :], in0=PA[:, :], in1=xt2[xs_][:], op=ALU.add),
                        r=[b_PA, b_xt2[xs_]], w=[b_hres[tt]])
                    act(lambda e, tt=tt: e.activation(out=hsq[:], in_=hres[:, tt, :], func=AF.Square, accum_out=hss[:]),
                        r=[b_hres[tt]], w=[b_hsq, b_hss])
                    act(lambda e: e.activation(out=hss[:], in_=hss[:], func=AF.Sqrt, scale=1.0 / 1024, bias=EPS), w=[b_hss])
                    dve(lambda e: e.reciprocal(out=hss[:], in_=hss[:]), w=[b_hss])
                    dve(lambda e, tt=tt: e.tensor_scalar(out=hs[:], in0=hres[:, tt, :], scalar1=hss[:, 0:1], scalar2=None, op0=ALU.mult),
                        r=[b_hres[tt], b_hss], w=[b_hs])
                    for kt in range(8):
                        tr(PT[:, kt * 128:(kt + 1) * 128], hs[:, kt * 128:(kt + 1) * 128], ident_b[:], r=[b_hs, b_identb], w=[b_PT])
                    for kt in range(8):
                        act(lambda e, kt=kt, tt=tt: e.activation(out=hnT[:, kt, tt * 128:(tt + 1) * 128], in_=PT[:, kt * 128:(kt + 1) * 128],
                                                                 func=AF.Copy, scale=gffn[:, kt:kt + 1]), r=[b_PT, b_gffn], w=[b_hnT])
                for f in range(22):
                    wb = wgi % 3
                    wgi += 1
                    fw.dma("sp", wg[wb][:].rearrange("p k c -> p (k c)"), wgu_s[f], reads=[b_wgus[f]], writes=[b_wg[wb]])
                    for kt in range(8):
                        mm(PA[:, 0:512], lhsT=wg[wb][:, kt, 0:128], rhs=hnT[:, kt, :], start=(kt == 0), stop=(kt == 7),
                           r=[b_wg[wb], b_hnT], w=[b_PA])
                    for kt in range(8):
                        mm(PB[:, 0:512], lhsT=wg[wb][:, kt, 128:256], rhs=hnT[:, kt, :], start=(kt == 0), stop=(kt == 7),
                           r=[b_wg[wb], b_hnT], w=[b_PB])
                    sb_ = f % 2
                    act(lambda e, sb_=sb_: e.activation(out=sg[sb_][:], in_=PA[:, 0:512], func=AF.Silu), r=[b_PA], w=[b_sg[sb_]])
                    dve(lambda e, sb_=sb_, f=f: e.tensor_tensor(out=actT[:, f, :], in0=PB[:, 0:512], in1=sg[sb_][:], op=ALU.mult),
                        r=[b_PB, b_sg[sb_]], w=[b_actT])
                    if bi == 0:
                        for kt in range(8):
                            mm(PD[:, 0:4], lhsT=wg[wb][:, kt, 0:128], rhs=hnTs[:, kt, :], start=(kt == 0), stop=(kt == 7),
                               r=[b_wg[wb], b_hnTs], w=[b_PD])
                        for kt in range(8):
                            mm(PD[:, 4:8], lhsT=wg[wb][:, kt, 128:256], rhs=hnTs[:, kt, :], start=(kt == 0), stop=(kt == 7),
                               r=[b_wg[wb], b_hnTs], w=[b_PD])
                        act(lambda e: e.activation(out=sgs[:], in_=PD[:, 0:4], func=AF.Silu), r=[b_PD], w=[b_sgs])
                        dve(lambda e, f=f: e.tensor_tensor(out=actTs[:, f, :], in0=PD[:, 4:8], in1=sgs[:], op=ALU.mult),
                            r=[b_PD, b_sgs], w=[b_actTs])
                for tt in range(4):
                    r0 = bi * 512 + tt * 128
                    for half in range(2):
                        for f in range(22):
                            mm(PC[:, half * 512:(half + 1) * 512], lhsT=actT[:, f, tt * 128:(tt + 1) * 128],
                               rhs=wdn[:, f, half * 512:(half + 1) * 512], start=(f == 0), stop=(f == 21),
                               r=[b_actT, b_wdn], w=[b_PC])
                    ys_ = tt % 2
                    dve(lambda e, tt=tt, ys_=ys_: e.tensor_tensor(out=yb[ys_][:], in0=PC[:, :], in1=hres[:, tt, :], op=ALU.add),
                        r=[b_PC, b_hres[tt]], w=[b_yb[ys_]])
                    act(lambda e, ys_=ys_: e.activation(out=ysq[:], in_=yb[ys_][:], func=AF.Square, accum_out=yss[:]),
                        r=[b_yb[ys_]], w=[b_ysq, b_yss])
                    act(lambda e: e.activation(out=yss[:], in_=yss[:], func=AF.Sqrt, scale=1.0 / 1024, bias=EPS), w=[b_yss])
                    dve(lambda e: e.reciprocal(out=yss[:], in_=yss[:]), w=[b_yss])
                    dve(lambda e, ys_=ys_: e.scalar_tensor_tensor(out=yb[ys_][:], in0=yb[ys_][:], scalar=yss[:, 0:1], in1=nfb[:],
                                                                  op0=ALU.mult, op1=ALU.mult), r=[b_yss, b_nfb], w=[b_yb[ys_]])
                    fw.dma("sp", y_o[r0:r0 + 128, :], yb[ys_][:], reads=[b_yb[ys_]])
            for half in range(2):
                for f in range(22):
                    mm(PC[0:4, half * 512:(half + 1) * 512], lhsT=actTs[:, f, :], rhs=wdn[:, f, half * 512:(half + 1) * 512],
                       start=(f == 0), stop=(f == 21), r=[b_actTs, b_wdn], w=[b_PC])
            dve(lambda e: e.tensor_tensor(out=ysb[:], in0=PC[0:4, :], in1=hs_res[:], op=ALU.add), r=[b_PC, b_hsres], w=[b_ysb])
            act(lambda e: e.activation(out=hsn_sq[:], in_=ysb[:], func=AF.Square, accum_out=hsn_ss[:]), r=[b_ysb], w=[b_hsnsq, b_hsnss])
            act(lambda e: e.activation(out=hsn_ss[:], in_=hsn_ss[:], func=AF.Sqrt, scale=1.0 / 1024, bias=EPS), w=[b_hsnss])
            dve(lambda e: e.reciprocal(out=hsn_ss[:], in_=hsn_ss[:]), w=[b_hsnss])
            dve(lambda e: e.scalar_tensor_tensor(out=ysb[:], in0=ysb[:], scalar=hsn_ss[:, 0:1], in1=nfb[0:4, :], op0=ALU.mult, op1=ALU.mult),
                r=[b_hsnss, b_nfb], w=[b_ysb])
            fw.dma("sp", ys_o[:, :], ysb[:], reads=[b_ysb])
        fw.drain()
    return nc


def _consts(NT):
    TT = NT * 128
    c = {}
    c["c_ident"] = np.eye(128, dtype=np.float32)
    half = 32
    inv = np.power(np.float32(10000.0), -np.arange(half, dtype=np.float32) * np.float32(2.0) / np.float32(64)).astype(np.float32)
    pos = (np.arange(NT)[None, :] * 128 + np.arange(128)[:, None]).astype(np.float32)
    ang = (pos[:, :, None] * inv[None, None, :]).astype(np.float32)
    c["c_cos"] = np.cos(ang).astype(np.float32).reshape(128, NT * 32)
    c["c_sin"] = np.sin(ang).astype(np.float32).reshape(128, NT * 32)
    k = np.arange(128)[:, None]
    q = np.arange(128)[None, :]
    tri = np.stack([(k <= q), (k >= q)], axis=1).astype(np.float32)
    c["c_tri"] = tri.reshape(128, 256)
    cm = np.zeros((128, 17, 128), np.float32)
    for m in range(17):
        cm[:, m, :] = (16 * k - q <= 128 * m - 31)
    c["c_cmpmask"] = cm.reshape(128, 17 * 128)
    qq = np.arange(128)[:, None]
    r = np.arange(256)[None, :] - 128
    hi = (qq >= 64).astype(np.int64)
    prel = np.zeros((128, 256), np.float32)
    prel[(r == hi) | (r == hi - 1)] = 1e4
    prel[r > hi] = -1e30
    c["c_prel"] = prel
    kk = np.arange(TT)[None, :]
    e = np.arange(64)[:, None]
    c["c_eind"] = (e == (kk // 64) % 64).astype(np.float32)
    n = np.arange(512)[:, None]
    s_ = np.arange(128)[None, :]
    c2s = ((n * 16 < s_ * 64 + 64) & (n * 16 + 32 > s_ * 64) & (n < 511)).astype(np.float32)
    c["c_c2s"] = c2s.reshape(4, 128, 128).transpose(1, 0, 2).reshape(128, 512)
    j = np.arange(128)[:, None]
    i = np.arange(128)[None, :]
    same = (j // 64) == (i // 64)
    gm = np.zeros((128, 5, 128), np.float32)
    gm[:, 0, :] = np.where(same & (i >= j), 0.0, NEGB)
    gm[:, 1, :] = np.where(same & (i > j), 0.0, NEGB)
    gm[:, 2, :] = (same & (j <= i))
    gm[:, 3, :] = (j < 64) * np.ones((1, 128))
    gm[:, 4, :] = (j >= 64) * np.ones((1, 128))
    c["c_gmask"] = gm.reshape(128, 5 * 128)
    angs = (np.float32(8192.0) * inv).astype(np.float32)
    c["c_rope_s"] = np.tile(np.concatenate([np.cos(angs), np.sin(angs)]).astype(np.float32)[None, :], (4, 1))
    nn_ = np.arange(512).reshape(4, 128).T
    c["c_ones511"] = (nn_ < 511).astype(np.float32)
    oh = np.zeros((4, 80), np.float32)
    for s_ in range(4):
        oh[s_, s_ * 4:(s_ + 1) * 4] = 1.0
    for r_ in range(4):
        for s_ in range(4):
            oh[r_, 16 + (r_ * 4 + s_) * 4 + s_] = 1.0
    c["c_oh4"] = oh
    bon = np.zeros((1, 128), np.float32)
    bon[0, 0] = 1e4
    bon[0, 127] = 1e4
    c["c_bonus_s"] = bon
    ef = np.zeros((2, 128), np.float32)
    ef[0, 0:64] = 1.0
    ef[1, 64:128] = 1.0
    c["c_efix"] = ef
    c["c_oh2"] = np.array([[1.0, 0.0, 0.0, 1.0]], np.float32)
    return c


def _core_weights(inp, g, hp):
    jh = 2 * g + hp
    w_in = inp["w_in"][0]
    own = [4 * g + 2 * hp, 4 * g + 2 * hp + 1]
    oth = [4 * g + 2 * (1 - hp), 4 * g + 2 * (1 - hp) + 1]
    heads = own + oth
    cols = []
    for h in heads:
        cols += list(range(h * 64, (h + 1) * 64))

    def kvcol(branch, kv):
        base = 512 + ((branch * 2 + kv) * 2 + g) * 64
        return list(range(base, base + 64))
    for branch in range(3):
        cols += kvcol(branch, 0)
    for branch in range(3):
        cols += kvcol(branch, 1)
    for h in heads:
        cols += [1280 + h * 3 + t for t in range(3)]
    cols += list(range(2840 + jh * 128, 2840 + (jh + 1) * 128))
    cols += [3352 + jh, 3356 + jh]
    w_tok = np.ascontiguousarray(w_in[:, cols])
    gcols = []
    for part in range(3):
        gcols += list(range(1304 + part * 512 + jh * 128, 1304 + part * 512 + (jh + 1) * 128))
    w_gdn = np.ascontiguousarray(w_in[:, gcols])
    d = {"w_tok": w_tok, "w_gdn": w_gdn}
    d["g_mix"] = np.ascontiguousarray(inp["norm_mix"][0].reshape(8, 128).T)
    cwf = inp["gdn_conv_w"][0]
    gch = [jh * 128 + part * 512 + np.arange(128) for part in range(3)]
    cw = np.stack([cwf[:, ch].T for ch in gch], axis=1)
    d["conv_w"] = np.ascontiguousarray(cw.reshape(128, 12))
    d["head_sc"] = np.ascontiguousarray(np.stack([np.full(128, inp["gdn_a_log"][0, jh]),
                                                  np.full(128, inp["gdn_dt_bias"][0, jh])], axis=1).astype(np.float32))
    d["gdn_norm_b"] = np.ascontiguousarray(np.tile(inp["gdn_norm"][0][None, :], (128, 1)))
    cwt = inp["nsa_cmp_w"][0]
    d["cmp_w"] = np.ascontiguousarray(cwt.reshape(2, 16, 2, 64, 64).transpose(2, 3, 0, 1, 4).reshape(128, 2 * 16 * 64))
    pe = inp["nsa_cmp_pe"][0]
    d["cmp_pe"] = np.ascontiguousarray(pe.reshape(2, 16, 2, 64).transpose(2, 3, 0, 1).reshape(128, 32))
    return d


def _core_inputs(inp, c, NT, consts):
    b, j = c // 4, c % 4
    g, hp = j // 2, j % 2
    TT = NT * 128
    TB = TT // 4
    d = dict(consts)
    d.update(_core_weights(inp, g, hp))
    d["x"] = np.ascontiguousarray(inp["x_prompt"][b, :TT])
    d["x_own"] = np.ascontiguousarray(inp["x_prompt"][b, j * TB:(j + 1) * TB])
    sel = np.zeros((128, 4), np.float32)
    sel[:, j] = 1.0
    d["sel4"] = sel
    perm = []
    for jj in range(4):
        perm += list(range(128 * jj, 128 * jj + 128)) + list(range(512 + 128 * jj, 512 + 128 * jj + 128))
    d["w_out_p"] = np.ascontiguousarray(inp["w_out"][0][perm, :])
    d["g_ffn"] = np.ascontiguousarray(inp["norm_ffn"][0].reshape(8, 128).T)
    d["w_gu"] = np.ascontiguousarray(inp["w_gate_up"][0])
    d["w_dn"] = np.ascontiguousarray(inp["w_down"][0])
    d["nfin_b"] = np.ascontiguousarray(np.tile(inp["norm_final"][None, :], (128, 1)))
    s0 = 4 * c
    d["xs"] = np.ascontiguousarray(inp["x_sample"][s0:s0 + 4, 0, :])
    d["w_in_full"] = np.ascontiguousarray(inp["w_in"][0])
    d["cache_kv"] = inp["cache_nsa_kv"][0].reshape(2560 * 128, 512)
    d["ptab_b"] = np.ascontiguousarray(np.tile(inp["page_table"][s0:s0 + 4].reshape(1, 256), (128, 1)).astype(np.int32))
    d["c_iota"] = np.arange(128, dtype=np.float32).reshape(128, 1)
    d["win_cache"] = np.ascontiguousarray(inp["cache_nsa_win"][0, s0:s0 + 4].reshape(4, 512, 256))
    d["gdn_S"] = np.ascontiguousarray(inp["state_gdn_S"][0, s0:s0 + 4].reshape(16, 128, 128))
    d["gdn_conv"] = np.ascontiguousarray(inp["state_gdn_conv"][0, s0:s0 + 4])
    d["conv_w_b"] = np.ascontiguousarray(np.tile(inp["gdn_conv_w"][0][None], (4, 1, 1)))
    d["alog_b"] = np.ascontiguousarray(np.tile(np.concatenate([inp["gdn_a_log"][0], inp["gdn_dt_bias"][0]])[None, :], (4, 1)))
    d["gnorm_row"] = np.ascontiguousarray(inp["gdn_norm"][0][None, :])
    cwt = inp["nsa_cmp_w"][0]
    w64 = cwt.transpose(2, 0, 1, 3).reshape(64, 2 * 32 * 64)
    d["cmp_w64"] = np.ascontiguousarray(np.concatenate([w64, w64], axis=0))
    pe = inp["nsa_cmp_pe"][0]
    p64 = pe.transpose(2, 0, 1).reshape(64, 64)
    d["cmp_pe64"] = np.ascontiguousarray(np.concatenate([p64, p64], axis=0))
    wo = inp["w_out"][0]
    d["w_out_n"] = np.ascontiguousarray(wo[:512].reshape(8, 64, 1024).transpose(1, 0, 2).reshape(64, 8192))
    d["w_out_g"] = np.ascontiguousarray(wo[512:].reshape(4, 128, 1024).transpose(1, 0, 2).reshape(128, 4096))
    return d


def _run(inp, NT):
    nc = build_nc(NT, phaseB=True)
    consts = _consts(NT)
    maps = [_core_inputs(inp, c, NT, consts) for c in range(8)]
    res = run_bass_kernel_spmd(nc, maps, core_ids=list(range(8)))
    return res.results


def kernel(**inputs):
    inp = {k: np.asarray(v) for k, v in inputs.items()}
    NT = 64
    TT = NT * 128
    TB = TT // 4
    R = _run(inp, NT)
    y_prompt = np.zeros((2, TT, 1024), np.float32)
    kv_prompt = np.zeros((1, 2, TT, 4, 2, 64), np.float32)
    win_prompt = np.zeros((1, 2, 512, 2, 2, 64), np.float32)
    S_prompt = np.zeros((1, 2, 4, 128, 128), np.float32)
    conv_prompt = np.zeros((1, 2, 3, 1536), np.float32)
    for c in range(8):
        b, j = c // 4, c % 4
        g, hp = j // 2, j % 2
        r = R[c]
        y_prompt[b, j * TB:(j + 1) * TB] = r["y_out"]
        if hp == 0:
            kv_prompt[0, b, :, :, g, :] = r["kv_out"].reshape(TT, 4, 64)
            win_prompt[0, b, :, :, g, :] = r["win_out"].reshape(512, 2, 64)
        S_prompt[0, b, j] = r["S_out"]
        cv = r["conv_out"]
        for part in range(3):
            conv_prompt[0, b, :, part * 512 + j * 128:part * 512 + (j + 1) * 128] = cv[:, part, :].T
    y_sample = np.zeros((32, 1, 1024), np.float32)
    kv_sample = np.zeros((1, 32, 1, 4, 2, 64), np.float32)
    win_sample = np.zeros((1, 32, 512, 2, 2, 64), np.float32)
    S_sample = np.zeros((1, 32, 4, 128, 128), np.float32)
    conv_sample = np.zeros((1, 32, 3, 1536), np.float32)
    for c in range(8):
        r = R[c]
        s0 = 4 * c
        y_sample[s0:s0 + 4, 0] = r["ys_out"]
        kv_sample[0, s0:s0 + 4, 0] = r["kvs_out"].reshape(4, 4, 2, 64)
        win_sample[0, s0:s0 + 4] = r["wins_out"].reshape(4, 512, 2, 2, 64)
        S_sample[0, s0:s0 + 4] = r["Ss_out"].reshape(4, 4, 128, 128)
        conv_sample[0, s0:s0 + 4] = r["convs_out"]
    return (y_prompt, y_sample, kv_prompt, win_prompt, S_prompt, conv_prompt, kv_sample, win_sample, S_sample, conv_sample)
```

```python
import numpy as np
from contextlib import ExitStack
import concourse.bass as bass
import concourse.mybir as mybir
from concourse.bass_utils import run_bass_kernel_spmd

F32 = mybir.dt.float32
BF16 = mybir.dt.bfloat16
I32 = mybir.dt.int32
AF = mybir.ActivationFunctionType
ALU = mybir.AluOpType
AX = mybir.AxisListType

ENGS = ("pe", "act", "dve", "pool", "sp")

D_MODEL = 1024
SEQ = 8192
HEAD_DIM = 64
D_FF = 2816
EPS = 1e-6
NEGB = -30000.0


class Buf:
    __slots__ = ("name", "w", "rs")

    def __init__(self, name=""):
        self.name = name
        self.w = None
        self.rs = []


class FW:
    def __init__(self, nc, stack, ndma_sems=16):
        self.nc = nc
        self.eng = {"pe": nc.tensor, "act": nc.scalar, "dve": nc.vector, "pool": nc.gpsimd, "sp": nc.sync}
        self.sem = {e: stack.enter_context(nc.semaphore("s_" + e)) for e in ENGS}
        self.cnt = {e: 0 for e in ENGS}
        self.waited = {e: {} for e in ENGS}
        self.dsems = {}
        self.dstate = {}
        for q in ("sp", "pool"):
            self.dsems[q] = [stack.enter_context(nc.semaphore("d_%s%d" % (q, i))) for i in range(ndma_sems)]
            self.dstate[q] = {"i": 0, "val": [0] * ndma_sems}
        self.n_inst = 0
        self.dead = False

    def _wait(self, e, ev):
        if ev is None:
            return
        if ev[0] == "c":
            _, src, n = ev
            if src == "pe" and e == "pe":
                return
            key = ("c", src)
            if self.waited[e].get(key, 0) >= n:
                return
            self.eng[e].wait_ge(self.sem[src], n)
            self.waited[e][key] = n
        else:
            _, q, idx, val = ev
            key = ("d", q, idx)
            if self.waited[e].get(key, 0) >= val:
                return
            self.eng[e].wait_ge(self.dsems[q][idx], val)
            self.waited[e][key] = val

    def _deps(self, e, reads, writes):
        for b in reads:
            self._wait(e, b.w)
        for b in writes:
            self._wait(e, b.w)
            for r in b.rs:
                self._wait(e, r)

    def _commit(self, ev, reads, writes):
        for b in reads:
            b.rs.append(ev)
            if len(b.rs) > 96:
                b.rs = b.rs[-96:]
        for b in writes:
            b.w = ev
            b.rs = []

    def op(self, e, fn, reads=(), writes=()):
        if self.dead:
            return None
        self._deps(e, reads, writes)
        ins = fn(self.eng[e])
        self.cnt[e] += 1
        ins.then_inc(self.sem[e], 1)
        ev = ("c", e, self.cnt[e])
        self._commit(ev, reads, writes)
        self.n_inst += 1
        return ev

    def dma(self, q, out, in_, reads=(), writes=(), fn=None):
        if self.dead:
            return None
        st = self.dstate[q]
        idx = st["i"] % len(self.dsems[q])
        st["i"] += 1
        if st["val"][idx] > 0:
            self._wait(q, ("d", q, idx, st["val"][idx]))
        self._deps(q, reads, writes)
        if fn is None:
            ins = self.eng[q].dma_start(out=out, in_=in_)
        else:
            ins = fn(self.eng[q])
        st["val"][idx] += 16
        ins.then_inc(self.dsems[q][idx], 16)
        ev = ("d", q, idx, st["val"][idx])
        self._commit(ev, reads, writes)
        self.n_inst += 1
        return ev

    def barrier(self):
        for e in ENGS:
            for src in ENGS:
                if src != e and self.cnt[src] > 0:
                    self._wait(e, ("c", src, self.cnt[src]))
            for q in ("sp", "pool"):
                stq = self.dstate[q]
                for idx, v in enumerate(stq["val"]):
                    if v:
                        self._wait(e, ("d", q, idx, v))

    def drain(self):
        for q in ("sp", "pool"):
            st = self.dstate[q]
            for idx, v in enumerate(st["val"]):
                if v:
                    self._wait("sp", ("d", q, idx, v))


class _Stop(Exception):
    pass


STOP = None
SKIP_CC = False


def build_nc(NT=64, dbg=False, phaseB=False):
    TT = NT * 128
    NG = NT // 4
    nc = bass.Bass("TRN2", target_bir_lowering=False)

    def din(name, shape, dt=F32):
        return nc.dram_tensor(name, list(shape), dt, kind="ExternalInput").ap()

    def dout(name, shape, dt=F32):
        return nc.dram_tensor(name, list(shape), dt, kind="ExternalOutput").ap()

    x_d = din("x", [TT, 1024])
    wtok_d = din("w_tok", [1024, 782])
    wgdn_d = din("w_gdn", [1024, 384])
    gmix_d = din("g_mix", [128, 8])
    cw_d = din("conv_w", [128, 12])
    hsc_d = din("head_sc", [128, 2])
    gnorm_d = din("gdn_norm_b", [128, 128])
    cmpw_d = din("cmp_w", [128, 2 * 16 * 64])
    cmppe_d = din("cmp_pe", [128, 32])
    ident_d = din("c_ident", [128, 128])
    cos_d = din("c_cos", [128, NT * 32])
    sin_d = din("c_sin", [128, NT * 32])
    tri_d = din("c_tri", [128, 256])
    cmpmask_d = din("c_cmpmask", [128, 17 * 128])
    prel_d = din("c_prel", [128, 256])
    eind_d = din("c_eind", [64, TT])
    c2s_d = din("c_c2s", [128, 4 * 128])
    gmask_d = din("c_gmask", [128, 5 * 128])

    TB = TT // 4
    NB = TB // 512
    if phaseB:
        xown_d = din("x_own", [TB, 1024])
        sel4_d = din("sel4", [128, 4])
        wout_d = din("w_out_p", [1024, 1024])
        gffn_d = din("g_ffn", [128, 8])
        wgu_d = din("w_gu", [1024, 5632])
        wdn_d = din("w_dn", [2816, 1024])
        nfin_d = din("nfin_b", [128, 1024])
        y_o = dout("y_out", [TB, 1024])
    if phaseB:
        xs_d = din("xs", [4, 1024])
        win_full_d = din("w_in_full", [1024, 3360])
        cache_d = din("cache_kv", [2560 * 128, 512])
        ptab_d = din("ptab_b", [128, 256], I32)
        iota_d = din("c_iota", [128, 1])
        wincache_d = din("win_cache", [4, 512, 256])
        gS_d = din("gdn_S", [16, 128, 128])
        gconv_d = din("gdn_conv", [4, 3, 1536])
        convwb_d = din("conv_w_b", [4, 4, 1536])
        alogb_d = din("alog_b", [4, 8])
        gnrow_d = din("gnorm_row", [1, 128])
        cmpw64_d = din("cmp_w64", [128, 2 * 32 * 64])
        woutn_d = din("w_out_n", [64, 8 * 1024])
        woutg_d = din("w_out_g", [128, 4 * 1024])
        ropes_d = din("c_rope_s", [4, 64])
        ones511_d = din("c_ones511", [128, 4])
        pe64_d = din("cmp_pe64", [128, 64])
        efix_d = din("c_efix", [2, 128])
        oh2_d = din("c_oh2", [1, 4])
        oh4_d = din("c_oh4", [4, 80])
        bonus_d = din("c_bonus_s", [1, 128])
        ys_o = dout("ys_out", [4, 1024])
        kvs_o = dout("kvs_out", [4, 512])
        wins_o = dout("wins_out", [4, 512, 256])
        Ss_o = dout("Ss_out", [16, 128, 128])
        convs_o = dout("convs_out", [4, 3, 1536])
    kv_o = dout("kv_out", [TT, 256])
    win_o = dout("win_out", [512, 128])
    S_o = dout("S_out", [128, 128])
    conv_o = dout("conv_out", [128, 3, 3])
    CH = min(2048, TT)
    NCH = TT // CH
    omT_o = dout("omT_out", [256, TT], BF16) if not phaseB else None
    xin = [nc.dram_tensor("xin%d" % k, [256, CH], BF16).ap() for k in range(NCH)] if phaseB else None
    b_xin = [Buf() for _ in range(NCH)]
    dbg_o = dout("dbg_out", [128, 1300]) if dbg else None

    st = ExitStack()
    with st:
        fw = FW(nc, st)

        cur = [st]

        def sb(name, shape, dt=F32):
            return cur[0].enter_context(nc.sbuf_tensor(name, list(shape), dt))

        def ps(name, shape, dt=F32):
            return st.enter_context(nc.psum_tensor(name, list(shape), dt))

        def pe(fn, r=(), w=()):
            return fw.op("pe", fn, r, w)

        def act(fn, r=(), w=()):
            return fw.op("act", fn, r, w)

        def dve(fn, r=(), w=()):
            return fw.op("dve", fn, r, w)

        def pool(fn, r=(), w=()):
            return fw.op("pool", fn, r, w)

        def mm(out, lhsT, rhs, start=True, stop=True, r=(), w=()):
            return pe(lambda e: e.matmul(out, lhsT=lhsT, rhs=rhs, start=start, stop=stop), r, w)

        def tr(out, in_, ident, r=(), w=()):
            return pe(lambda e: e.transpose(out, in_, ident), r, w)

        def bcast(ap, shape, axis):
            return ap.unsqueeze(axis).to_broadcast(list(shape))

        ident_f = sb("ident_f", [128, 128]); b_identf = Buf()
        ident_b = sb("ident_b", [128, 128], BF16); b_identb = Buf()
        ones_f2 = sb("ones_f2", [128, 128]); b_onesf2 = Buf()
        hs_res = sb("hs_res", [4, 1024]); b_hsres = Buf()
        stA = st.enter_context(ExitStack())
        cur[0] = stA
        pool(lambda e: e.memset(ones_f2[:], 1.0), w=[b_onesf2])
        ones_b = sb("ones_b", [128, 128], BF16); b_onesb = Buf()
        ones_f = sb("ones_f", [128, 128]); b_onesf = Buf()
        csT = [sb("csT%d" % i_, [128, 2, 4, 32]) for i_ in range(2)]; b_csT = [Buf() for _ in range(2)]
        tri = sb("tri", [128, 2, 128], BF16); b_tri = Buf()
        cmpmask = sb("cmpmask", [128, 17, 128], BF16); b_cmpmask = Buf()
        prel = sb("prel", [128, 256]); b_prel = Buf()
        gmask = sb("gmask", [128, 5, 128]); b_gmask = Buf()
        wtok = sb("wtok", [128, 8, 782], BF16); b_wtok = Buf()
        wgdn = sb("wgdn", [128, 8, 384], BF16); b_wgdn = Buf()
        gmix = sb("gmix", [128, 8]); b_gmix = Buf()
        cw = sb("cw", [128, 12]); b_cw = Buf()
        hsc = sb("hsc", [128, 2]); b_hsc = Buf()
        negA = sb("negA", [128, 1]); b_negA = Buf()
        gnb = sb("gnb", [128, 128]); b_gnb = Buf()
        cmpw = sb("cmpw", [128, 2, 16, 64], BF16); b_cmpw = Buf()
        cmppe = sb("cmppe", [128, 2, 16], BF16); b_cmppe = Buf()

        fw.dma("sp", ident_f[:], ident_d[:, :], writes=[b_identf])
        fw.dma("pool", ident_b[:], ident_d[:, :], writes=[b_identb])
        fw.dma("pool", tri[:].rearrange("p a b -> p (a b)"), tri_d[:, :], writes=[b_tri])
        fw.dma("pool", cmpmask[:].rearrange("p a b -> p (a b)"), cmpmask_d[:, :], writes=[b_cmpmask])
        fw.dma("sp", prel[:], prel_d[:, :], writes=[b_prel])
        fw.dma("sp", gmask[:].rearrange("p a b -> p (a b)"), gmask_d[:, :], writes=[b_gmask])
        fw.dma("sp", gmix[:], gmix_d[:, :], writes=[b_gmix])
        fw.dma("sp", cw[:], cw_d[:, :], writes=[b_cw])
        fw.dma("sp", hsc[:], hsc_d[:, :], writes=[b_hsc])
        fw.dma("sp", gnb[:], gnorm_d[:, :], writes=[b_gnb])
        fw.dma("pool", cmpw[:].rearrange("p a b c -> p (a b c)"), cmpw_d[:, :], writes=[b_cmpw])
        fw.dma("pool", cmppe[:].rearrange("p a b -> p (a b)"), cmppe_d[:, :], writes=[b_cmppe])
        for kt in range(8):
            fw.dma("pool", wtok[:, kt, :], wtok_d[kt * 128:(kt + 1) * 128, :], writes=[b_wtok])
            fw.dma("pool", wgdn[:, kt, :], wgdn_d[kt * 128:(kt + 1) * 128, :], writes=[b_wgdn])
        pool(lambda e: e.memset(ones_b[:], 1.0), w=[b_onesb])
        pool(lambda e: e.memset(ones_f[:], 1.0), w=[b_onesf])
        for kt in range(8):
            dve(lambda e, kt=kt: e.tensor_scalar(out=wtok[:, kt, :], in0=wtok[:, kt, :], scalar1=gmix[:, kt:kt + 1],
                                                 scalar2=None, op0=ALU.mult), r=[b_gmix], w=[b_wtok])
            dve(lambda e, kt=kt: e.tensor_scalar(out=wgdn[:, kt, :], in0=wgdn[:, kt, :], scalar1=gmix[:, kt:kt + 1],
                                                 scalar2=None, op0=ALU.mult), r=[b_gmix], w=[b_wgdn])
        act(lambda e: e.activation(out=negA[:], in_=hsc[:, 0:1], func=AF.Exp), r=[b_hsc], w=[b_negA])
        dve(lambda e: e.tensor_scalar(out=negA[:], in0=negA[:], scalar1=-1.0, scalar2=None, op0=ALU.mult), w=[b_negA])

        KselT = sb("KselT", [128, TT], BF16); b_ksel = [Buf() for _ in range(NT)]; b_eind = Buf()
        Vsel = sb("Vsel", [128, NT, 65], BF16); b_vsel = [Buf() for _ in range(NT)]
        KwinT = sb("KwinT", [64, 8 * 128], BF16); b_kwin = [Buf() for _ in range(8)]
        Vwin = sb("Vwin", [128, 8, 65], BF16); b_vwin = [Buf() for _ in range(8)]
        Rk = sb("Rk", [128, 2, 160], BF16); b_Rk = Buf()
        ckT = sb("ckT", [64, 512], BF16); b_ckT = Buf()
        cvx = sb("cvx", [128, 4, 193], BF16); b_cvx = Buf()
        c2s_f = sb("c2s_f", [128, 4, 128]); b_c2sf = Buf()
        ckb = sb("ckb", [64, 1]); b_ckb = Buf()
        cvb = sb("cvb", [8, 64]); b_cvb = Buf()
        cvrow = sb("cvrow", [1, 64], BF16); b_cvrow = Buf()

        fw.dma("pool", KselT[64:128, :], eind_d[:, :], writes=[b_eind])
        pool(lambda e: e.memset(Vsel[:, :, 64:65], 1.0), w=b_vsel)
        pool(lambda e: e.memset(Vwin[:, :, 64:65], 1.0), w=b_vwin)
        pool(lambda e: e.memset(Rk[:], 0.0), w=[b_Rk])
        pool(lambda e: e.memset(ckT[:], 0.0), w=[b_ckT])
        pool(lambda e: e.memset(cvx[:], 0.0), w=[b_cvx])
        fw.dma("sp", c2s_f[:].rearrange("p a b -> p (a b)"), c2s_d[:, :], writes=[b_c2sf])
        pool(lambda e: e.tensor_copy(out=cvx[:, :, 64:192], in_=c2s_f[:]), r=[b_c2sf], w=[b_cvx])
        pool(lambda e: e.memset(cvx[:, :, 192:193], 1.0), w=[b_cvx])

        PA = ps("PA", [128, 1024]); b_PA = Buf()
        PB = ps("PB", [128, 1024]); b_PB = Buf()
        PC = ps("PC", [128, 1024]); b_PC = Buf()
        PD = ps("PD", [128, 512]); b_PD = Buf()
        PT = ps("PT", [128, 1024], BF16); b_PT = Buf()

        for lp in range(16):
            mm(PD[0:64, 0:1], lhsT=cmpw[:, 0, lp, :], rhs=cmppe[:, 0, lp:lp + 1], start=(lp == 0), stop=(lp == 15),
               r=[b_cmpw, b_cmppe], w=[b_PD])
        act(lambda e: e.copy(out=ckb[:], in_=PD[0:64, 0:1]), r=[b_PD], w=[b_ckb])
        for lp in range(16):
            mm(PD[0:1, 64:128], lhsT=cmppe[:, 1, lp:lp + 1], rhs=cmpw[:, 1, lp, :], start=(lp == 0), stop=(lp == 15),
               r=[b_cmpw, b_cmppe], w=[b_PD])
        act(lambda e: e.copy(out=cvrow[:], in_=PD[0:1, 64:128]), r=[b_PD], w=[b_cvrow])
        mm(PD[0:8, 128:192], lhsT=ones_b[0:1, 0:8], rhs=cvrow[:], r=[b_onesb, b_cvrow], w=[b_PD])
        act(lambda e: e.copy(out=cvb[:], in_=PD[0:8, 128:192]), r=[b_PD], w=[b_cvb])

        NXB = 2
        xt = [sb("xt%d" % i, [128, 1024]) for i in range(NXB)]; b_xt = [Buf() for _ in range(NXB)]
        ssq = [sb("ssq%d" % i, [128, 1]) for i in range(NXB)]; b_ssq = [Buf() for _ in range(NXB)]
        xs = [sb("xs%d" % i, [128, 1024], BF16) for i in range(NXB)]; b_xs = [Buf() for _ in range(NXB)]
        xnT = sb("xnT", [128, 8, 512], BF16); b_xnT = [Buf() for _ in range(4)]
        pj = [sb("pj%d" % i, [128, 782]) for i in range(2)]; b_pj = [Buf() for _ in range(2)]
        rq = [sb("rq%d" % i, [128, 7, 64]) for i in range(2)]; b_rq = [Buf() for _ in range(2)]
        rt = sb("rt", [128, 4, 7, 32]); b_rt = Buf()
        ko = [sb("ko%d" % i, [128, 6, 64]) for i in range(2)]; b_ko = [Buf() for _ in range(2)]
        qkb = sb("qkb", [128, 7, 64], BF16); b_qkb = Buf()
        kvc2 = sb("kvc2", [128, 2, 128], BF16); b_kvc2 = Buf()
        QT = [sb("QT%d" % i_, [64, 512], BF16) for i_ in range(2)]; b_QT = [Buf() for _ in range(2)]
        Qaug = [sb("Qaug%d" % i_, [128, 2, 256], BF16) for i_ in range(2)]; b_Qaug = [Buf() for _ in range(2)]
        gates = [sb("gates%d" % i_, [128, 12]) for i_ in range(2)]; b_gates = [Buf() for _ in range(2)]
        gz = sb("gz", [128, 140]); b_gz = Buf()
        zsil = [sb("zsil%d" % i_, [128, 4, 128]) for i_ in range(2)]; b_zsil = [[Buf() for _ in range(4)] for _ in range(2)]
        abg = [sb("abg%d" % i_, [128, 4, 2]) for i_ in range(2)]; b_abg = [Buf() for _ in range(2)]
        cvnew = sb("cvnew", [8, 64], BF16); b_cvnew = Buf()
        PTc = [sb("PTc%d" % i, [128, 512], BF16) for i in range(4)]; b_PTc = [Buf() for _ in range(4)]
        PTs = [sb("PTs%d" % i, [128, 1024], BF16) for i in range(2)]; b_PTs = [Buf() for _ in range(2)]
        acc_c = sb("acc_c", [128, 4, 193]); b_accc = Buf()
        acc_sw = sb("acc_sw", [128, 4, 65]); b_accsw = Buf()
        rcp = sb("rcp", [128, 12]); b_rcp = Buf()
        imp = sb("imp", [128, 128]); b_imp = Buf()
        score = sb("score", [128, 128]); b_score = Buf()
        mx8 = sb("mx8", [128, 16]); b_mx8 = Buf()
        thr = sb("thr", [128, 1]); b_thr = Buf()
        sc2 = sb("sc2", [128, 128]); b_sc2 = Buf()
        mbt = sb("mbt", [128, 2, 128]); b_mbt = Buf()
        coef = sb("coef", [128, 6]); b_coef = Buf()
        onsa = sb("onsa", [128, 128]); b_onsa = Buf()
        om = sb("om", [128, 256], BF16); b_om = Buf()
        omT = [sb("omT%d" % i_, [128, 2, 512], BF16) for i_ in range(2)]; b_omT = [Buf() for _ in range(2)]
        raw = [sb("raw%d" % i_, [128, 3, 515]) for i_ in range(2)]; b_raw = [Buf() for _ in range(2)]
        cacc = sb("cacc", [128, 3, 512]); b_cacc = Buf()
        csil = cacc; b_csil = b_cacc
        sqb = sb("sqb", [128, 2, 512], BF16); b_sqb = Buf()
        rnorm = sb("rnorm", [128, 2, 512]); b_rnorm = Buf()
        gT = sb("gT", [128, 3, 512], BF16); b_gT = Buf()
        gtok = sb("gtok", [128, 4, 3, 128], BF16); b_gtok = Buf()
        gsc = sb("gsc", [128, 16, 4]); b_gsc = Buf()
        glc = sb("glc", [128, 2, 4]); b_glc = Buf()
        dg1 = sb("dg1", [128, 4, 128]); b_dg1 = Buf()
        dg2 = sb("dg2", [128, 4, 128]); b_dg2 = Buf()
        dgn = sb("dgn", [128, 4, 128]); b_dgn = Buf()
        gmask4 = sb("gmask4", [128, 2, 4, 128]); b_gmask4 = Buf()
        decT = sb("decT", [128, 4, 128], BF16); b_decT = Buf()
        decbT = sb("decbT", [128, 4, 128], BF16); b_decbT = Buf()
        Um = [sb("Um%d" % i, [128, 4, 128], BF16) for i in range(2)]; b_Um = [Buf() for _ in range(2)]
        Lm = [sb("Lm%d" % i, [128, 4, 128], BF16) for i in range(2)]; b_Lm = [Buf() for _ in range(2)]
        Pm = [sb("Pm%d" % i, [128, 4, 128], BF16) for i in range(2)]; b_Pm = [Buf() for _ in range(2)]
        Xm = sb("Xm", [128, 4, 256], BF16); b_Xm = Buf()
        uw = sb("uw", [128, 4, 256], BF16); b_uw = Buf()
        kgm = sb("kgm", [128, 4, 2, 128], BF16); b_kgm = Buf()
        aqkT = sb("aqkT", [128, 4, 128], BF16); b_aqkT = Buf()
        Dg = sb("Dg", [128, 4, 128], BF16); b_Dg = Buf()
        QpA = sb("QpA", [128, 4, 128], BF16); QpB = sb("QpB", [128, 4, 128], BF16); b_Qp = Buf()
        glc8 = sb("glc8", [128, 8]); b_glc8 = Buf()
        MTf = sb("MTf", [128, 8, 128], BF16); b_MTf = Buf()
        MT8 = sb("MT8", [128, 8, 128], BF16); b_MT8 = Buf()
        Sb9 = sb("Sb9", [128, 9, 128], BF16); b_Sb9 = [Buf() for _ in range(9)]
        Sf = sb("Sf", [128, 128]); b_Sf = Buf()
        og4 = sb("og4", [128, 4, 128]); b_og4 = Buf()
        og4q = dgn; b_og4q = b_dgn
        og4s = sb("og4s", [128, 4]); b_og4s = Buf()
        omg = sb("omg", [128, 4, 128], BF16); b_omg = Buf()

        pool(lambda e: e.memset(kgm[:], 0.0), w=[b_kgm])
        for mk_ in range(2):
            pool(lambda e, mk_=mk_: e.tensor_copy(out=gmask4[:, mk_], in_=bcast(gmask[:, mk_, :], [128, 4, 128], 1)), r=[b_gmask], w=[b_gmask4])
        pool(lambda e: e.memset(raw[0][:], 0.0), w=[b_raw[0]])
        pool(lambda e: e.memset(raw[1][:], 0.0), w=[b_raw[1]])
        pool(lambda e: e.memset(QpA[:], 0.0), w=[b_Qp])
        pool(lambda e: e.memset(QpB[:], 0.0), w=[b_Qp])
        pool(lambda e: e.memset(Sb9[:, 0, :], 0.0), w=[b_Sb9[0]])

        G_G, G_BETA, G_GCUM, G_GL, G_EG, G_EKG, G_LNB, G_NEGG, G_SKBG, G_GB = range(10)


        hits = {}

        def chk2(name):
            if STOP == name:
                fw.dead = True

        def chk(name):
            hits[name] = hits.get(name, 0) + 1
            if STOP == name or STOP == "%s@%d" % (name, hits[name]):
                raise _Stop()

        def gdn_gen(grp):
            gp = grp % 2
            for c3 in range(3):
                dve(lambda e, c3=c3: e.tensor_scalar(out=cacc[:, c3, :], in0=raw[gp][:, c3, 0:512], scalar1=cw[:, c3 * 4:c3 * 4 + 1],
                                                      scalar2=None, op0=ALU.mult), r=[b_raw[gp], b_cw], w=[b_cacc])
                for jj in range(1, 4):
                    dve(lambda e, c3=c3, jj=jj: e.scalar_tensor_tensor(out=cacc[:, c3, :], in0=raw[gp][:, c3, jj:jj + 512],
                                                                        scalar=cw[:, c3 * 4 + jj:c3 * 4 + jj + 1], in1=cacc[:, c3, :],
                                                                        op0=ALU.mult, op1=ALU.add), r=[b_raw[gp], b_cw], w=[b_cacc])
            act(lambda e: e.activation(out=csil[:].rearrange("p a b -> p (a b)"), in_=cacc[:].rearrange("p a b -> p (a b)"),
                                       func=AF.Silu), r=[b_cacc], w=[b_csil])
            yield
            act(lambda e: e.activation(out=sqb[:].rearrange("p a b -> p (a b)"), in_=csil[:, 0:2, :].rearrange("p a b -> p (a b)"),
                                       func=AF.Square), r=[b_csil], w=[b_sqb])
            for c3 in range(2):
                mm(PB[:, c3 * 512:(c3 + 1) * 512], lhsT=ones_b[:], rhs=sqb[:, c3, :], r=[b_onesb, b_sqb], w=[b_PB])
            act(lambda e: e.activation(out=rnorm[:].rearrange("p a b -> p (a b)"), in_=PB[:, 0:1024], func=AF.Ln, bias=EPS),
                r=[b_PB], w=[b_rnorm])
            act(lambda e: e.activation(out=rnorm[:].rearrange("p a b -> p (a b)"), in_=rnorm[:].rearrange("p a b -> p (a b)"),
                                       func=AF.Exp, scale=-0.5), w=[b_rnorm])
            dve(lambda e: e.scalar_tensor_tensor(out=gT[:, 0, :], in0=csil[:, 0, :], scalar=128.0 ** -0.5, in1=rnorm[:, 0, :],
                                                 op0=ALU.mult, op1=ALU.mult), r=[b_csil, b_rnorm], w=[b_gT])
            dve(lambda e: e.tensor_tensor(out=gT[:, 1, :], in0=csil[:, 1, :], in1=rnorm[:, 1, :], op=ALU.mult),
                r=[b_csil, b_rnorm], w=[b_gT])
            act(lambda e: e.copy(out=gT[:, 2, :], in_=csil[:, 2, :]), r=[b_csil], w=[b_gT])
            for tt in range(4):
                for c3 in range(3):
                    tr(PT[:, c3 * 128:(c3 + 1) * 128], gT[:, c3, tt * 128:(tt + 1) * 128], ident_b[:], r=[b_gT, b_identb], w=[b_PT])
                act(lambda e, tt=tt: e.copy(out=gtok[:, tt, :, :], in_=PT[:, 0:384].rearrange("p (c d) -> p c d", c=3)),
                    r=[b_PT], w=[b_gtok])
            yield
            a_ap = abg[gp][:, :, 0]
            b_ap = abg[gp][:, :, 1]
            act(lambda e: e.activation(out=gsc[:, G_G, :], in_=a_ap, func=AF.Exp, bias=hsc[:, 1:2]), r=[b_abg[gp], b_hsc], w=[b_gsc])
            act(lambda e: e.activation(out=gsc[:, G_G, :], in_=gsc[:, G_G, :], func=AF.Ln, bias=1.0), w=[b_gsc])
            dve(lambda e: e.tensor_scalar(out=gsc[:, G_G, :], in0=gsc[:, G_G, :], scalar1=negA[:, 0:1], scalar2=None, op0=ALU.mult),
                r=[b_negA], w=[b_gsc])
            act(lambda e: e.activation(out=gsc[:, G_BETA, :], in_=b_ap, func=AF.Exp, scale=-1.0), r=[b_abg[gp]], w=[b_gsc])
            dve(lambda e: e.tensor_scalar(out=gsc[:, G_BETA, :], in0=gsc[:, G_BETA, :], scalar1=1.0, scalar2=None, op0=ALU.add), w=[b_gsc])
            dve(lambda e: e.reciprocal(out=gsc[:, G_BETA, :], in_=gsc[:, G_BETA, :]), w=[b_gsc])
            act(lambda e: e.activation(out=gsc[:, G_LNB, :], in_=gsc[:, G_BETA, :], func=AF.Ln), w=[b_gsc])
            mm(PD[:, 0:4], lhsT=gmask[:, 2, :], rhs=gsc[:, G_G, :], r=[b_gmask, b_gsc], w=[b_PD])
            mm(PD[:, 4:8], lhsT=gmask[:, 3, :], rhs=gsc[:, G_G, :], r=[b_gmask, b_gsc], w=[b_PD])
            mm(PD[:, 8:12], lhsT=gmask[:, 4, :], rhs=gsc[:, G_G, :], r=[b_gmask, b_gsc], w=[b_PD])
            act(lambda e: e.copy(out=gsc[:, G_GCUM, :], in_=PD[:, 0:4]), r=[b_PD], w=[b_gsc])
            act(lambda e: e.copy(out=glc[:].rearrange("p a b -> p (a b)"), in_=PD[:, 4:12]), r=[b_PD], w=[b_glc])
            dve(lambda e: e.tensor_copy(out=gsc[0:64, G_GL, :], in_=glc[0:64, 0, :]), r=[b_glc], w=[b_gsc])
            dve(lambda e: e.tensor_copy(out=gsc[64:128, G_GL, :], in_=glc[64:128, 1, :]), r=[b_glc], w=[b_gsc])
            act(lambda e: e.activation(out=gsc[:, G_EG, :], in_=gsc[:, G_GCUM, :], func=AF.Exp), w=[b_gsc])
            dve(lambda e: e.tensor_tensor(out=gsc[:, G_EKG, :], in0=gsc[:, G_GL, :], in1=gsc[:, G_GCUM, :], op=ALU.subtract), w=[b_gsc])
            act(lambda e: e.activation(out=gsc[:, G_EKG, :], in_=gsc[:, G_EKG, :], func=AF.Exp), w=[b_gsc])
            act(lambda e: e.activation(out=glc[:].rearrange("p a b -> p (a b)"), in_=glc[:].rearrange("p a b -> p (a b)"), func=AF.Exp),
                w=[b_glc])
            dve(lambda e: e.tensor_scalar(out=gsc[:, G_NEGG, :], in0=gsc[:, G_GCUM, :], scalar1=-1.0, scalar2=None, op0=ALU.mult), w=[b_gsc])
            dve(lambda e: e.tensor_tensor(out=gsc[:, G_SKBG, :], in0=gsc[:, G_BETA, :], in1=gsc[:, G_EG, :], op=ALU.mult), w=[b_gsc])
            dve(lambda e: e.tensor_scalar(out=gsc[:, G_SKBG, :], in0=gsc[:, G_SKBG, :], scalar1=-1.0, scalar2=None, op0=ALU.mult), w=[b_gsc])
            dve(lambda e: e.tensor_tensor(out=gsc[:, G_GB, :], in0=gsc[:, G_GCUM, :], in1=gsc[:, G_LNB, :], op=ALU.add), w=[b_gsc])

            yield
            dve(lambda e: e.tensor_copy(out=glc8[:, 0:8:2], in_=glc[:, 0, :]), r=[b_glc], w=[b_glc8])
            dve(lambda e: e.tensor_copy(out=glc8[:, 1:8:2], in_=glc[:, 1, :]), r=[b_glc], w=[b_glc8])
            identf4 = bcast(ident_f[:], [128, 4, 128], 1)
            identb4 = bcast(ident_b[:], [128, 4, 128], 1)

            def colb(col):
                return bcast(gsc[:, col, :], [128, 4, 128], 2)
            dve(lambda e: e.tensor_tensor(out=dg1[:], in0=identf4, in1=colb(G_GCUM), op=ALU.mult), r=[b_identf, b_gsc], w=[b_dg1])
            dve(lambda e: e.tensor_tensor(out=dg2[:], in0=identf4, in1=colb(G_GB), op=ALU.mult), r=[b_identf, b_gsc], w=[b_dg2])
            dve(lambda e: e.tensor_tensor(out=dgn[:], in0=identf4, in1=colb(G_NEGG), op=ALU.mult), r=[b_identf, b_gsc], w=[b_dgn])
            for (PSx, bPSx, dgx, mk_) in ((PC[:, 0:512], b_PC, dg1, 0), (PA[:, 0:512], b_PA, dg2, 1)):
                for tt in range(4):
                    reg = PSx[:, tt * 128:(tt + 1) * 128]
                    mm(reg, lhsT=ones_f[:], rhs=dgx[:, tt, :], start=True, stop=False, r=[b_onesf, b_dg1, b_dg2], w=[bPSx])
                    mm(reg, lhsT=ident_f[:], rhs=gmask[:, mk_, :], start=False, stop=False, r=[b_identf, b_gmask], w=[bPSx])
                    mm(reg, lhsT=dgn[:, tt, :], rhs=ones_f[:], start=False, stop=True, r=[b_dgn, b_onesf], w=[bPSx])
            act(lambda e: e.activation(out=decT[:].rearrange("p a b -> p (a b)"), in_=PC[:, 0:512], func=AF.Exp), r=[b_PC], w=[b_decT])
            act(lambda e: e.activation(out=decbT[:].rearrange("p a b -> p (a b)"), in_=PA[:, 0:512], func=AF.Exp), r=[b_PA], w=[b_decbT])
            yield
            for tt in range(4):
                kT_t = gT[:, 1, tt * 128:(tt + 1) * 128]
                qT_t = gT[:, 0, tt * 128:(tt + 1) * 128]
                mm(PC[:, 512 + tt * 128:512 + (tt + 1) * 128], lhsT=kT_t, rhs=kT_t, r=[b_gT], w=[b_PC])
                mm(PB[:, tt * 128:(tt + 1) * 128], lhsT=kT_t, rhs=qT_t, r=[b_gT], w=[b_PB])
            dve(lambda e: e.tensor_tensor(out=Um[0][:].rearrange("p a b -> p (a b)"), in0=PC[:, 512:1024], in1=decbT[:].rearrange("p a b -> p (a b)"),
                                          op=ALU.mult), r=[b_PC, b_decbT], w=[b_Um[0]])
            dve(lambda e: e.tensor_tensor(out=aqkT[:].rearrange("p a b -> p (a b)"), in0=PB[:, 0:512], in1=decT[:].rearrange("p a b -> p (a b)"),
                                          op=ALU.mult), r=[b_PB, b_decT], w=[b_aqkT])
            for tt in range(4):
                tr(PT[:, tt * 128:(tt + 1) * 128], Um[0][:, tt, :], ident_b[:], r=[b_Um[0], b_identb], w=[b_PT])
            act(lambda e: e.copy(out=Lm[0][:].rearrange("p a b -> p (a b)"), in_=PT[:, 0:512]), r=[b_PT], w=[b_Lm[0]])
            dve(lambda e: e.tensor_tensor(out=Pm[0][:], in0=identb4, in1=Um[0][:], op=ALU.subtract), r=[b_identb, b_Um[0]], w=[b_Pm[0]])
            yield
            cu, cp = 0, 0
            for lvl in range(5):
                nu = 1 - cu
                for tt in range(4):
                    mm(PC[:, tt * 128:(tt + 1) * 128], lhsT=Um[cu][:, tt, :], rhs=Lm[cu][:, tt, :], r=[b_Um[cu], b_Lm[cu]], w=[b_PC])
                if lvl < 4:
                    for tt in range(4):
                        mm(PA[:, tt * 128:(tt + 1) * 128], lhsT=Lm[cu][:, tt, :], rhs=Um[cu][:, tt, :], r=[b_Um[cu], b_Lm[cu]], w=[b_PA])
                act(lambda e, nu=nu: e.copy(out=Lm[nu][:].rearrange("p a b -> p (a b)"), in_=PC[:, 0:512]), r=[b_PC], w=[b_Lm[nu]])
                if lvl < 4:
                    act(lambda e, nu=nu: e.copy(out=Um[nu][:].rearrange("p a b -> p (a b)"), in_=PA[:, 0:512]), r=[b_PA], w=[b_Um[nu]])
                for tt in range(4):
                    mm(PB[:, tt * 128:(tt + 1) * 128], lhsT=Lm[nu][:, tt, :], rhs=Pm[cp][:, tt, :], r=[b_Lm[nu], b_Pm[cp]], w=[b_PB])
                dve(lambda e, cp=cp: e.tensor_tensor(out=Pm[1 - cp][:].rearrange("p a b -> p (a b)"), in0=PB[:, 0:512],
                                                     in1=Pm[cp][:].rearrange("p a b -> p (a b)"), op=ALU.add), r=[b_PB, b_Pm[cp]], w=[b_Pm[1 - cp]])
                cu = nu
                cp = 1 - cp
            yield
            Tt = Pm[cp]
            bTt = b_Pm[cp]
            dve(lambda e: e.tensor_tensor(out=Xm[:, :, 0:128], in0=gtok[:, :, 2, :], in1=colb(G_BETA), op=ALU.mult), r=[b_gtok, b_gsc], w=[b_Xm])
            dve(lambda e: e.tensor_tensor(out=Xm[:, :, 128:256], in0=gtok[:, :, 1, :], in1=colb(G_SKBG), op=ALU.mult), r=[b_gtok, b_gsc], w=[b_Xm])
            for tt in range(4):
                mm(PC[:, tt * 256:(tt + 1) * 256], lhsT=Tt[:, tt, :], rhs=Xm[:, tt, :], r=[bTt, b_Xm], w=[b_PC])
            act(lambda e: e.copy(out=uw[:].rearrange("p a b -> p (a b)"), in_=PC[:, 0:1024]), r=[b_PC], w=[b_uw])
            dve(lambda e: e.tensor_tensor(out=kgm[0:64, :, 0, :], in0=gtok[0:64, :, 1, :], in1=bcast(gsc[0:64, G_EKG, :], [64, 4, 128], 2), op=ALU.mult),
                r=[b_gtok, b_gsc], w=[b_kgm])
            dve(lambda e: e.tensor_tensor(out=kgm[64:128, :, 1, :], in0=gtok[64:128, :, 1, :], in1=bcast(gsc[64:128, G_EKG, :], [64, 4, 128], 2), op=ALU.mult),
                r=[b_gtok, b_gsc], w=[b_kgm])
            dve(lambda e: e.tensor_tensor(out=Dg[:], in0=identb4, in1=colb(G_EG), op=ALU.mult), r=[b_identb, b_gsc], w=[b_Dg])
            for tt in range(4):
                mm(PA[:, tt * 128:(tt + 1) * 128], lhsT=gtok[:, tt, 0, :], rhs=Dg[:, tt, :], start=True, stop=False, r=[b_gtok, b_Dg], w=[b_PA])
                mm(PA[:, tt * 128:(tt + 1) * 128], lhsT=uw[:, tt, 128:256], rhs=aqkT[:, tt, :], start=False, stop=True, r=[b_uw, b_aqkT], w=[b_PA])
            act(lambda e: e.copy(out=QpA[:, :, 0:64], in_=PA[:, 0:512].rearrange("p (a b) -> p a b", a=4)[:, :, 0:64]), r=[b_PA], w=[b_Qp])
            act(lambda e: e.copy(out=QpB[:, :, 64:128], in_=PA[:, 0:512].rearrange("p (a b) -> p a b", a=4)[:, :, 64:128]), r=[b_PA], w=[b_Qp])
            yield
            for tt in range(4):
                for c in range(2):
                    r0 = 64 * c
                    ch = tt * 2 + c
                    mm(PB[:, ch * 128:(ch + 1) * 128], lhsT=uw[:, tt, 128:256], rhs=kgm[:, tt, c, :], r=[b_uw, b_kgm], w=[b_PB])
            dve(lambda e: e.tensor_tensor(out=MTf[:], in0=bcast(ident_f[:], [128, 8, 128], 1), in1=bcast(glc8[:], [128, 8, 128], 2), op=ALU.mult),
                r=[b_identf, b_glc8], w=[b_MTf])
            dve(lambda e: e.tensor_tensor(out=MT8[:].rearrange("p a b -> p (a b)"), in0=PB[:, 0:1024], in1=MTf[:].rearrange("p a b -> p (a b)"),
                                          op=ALU.add), r=[b_PB, b_MTf], w=[b_MT8])
            for ch in range(8):
                tt, c = ch // 2, ch % 2
                r0 = 64 * c
                i = grp * 4 + tt
                PSc = PC[:, (ch % 2) * 512:(ch % 2) * 512 + 128]
                mm(PSc, lhsT=kgm[:, tt, c, :], rhs=uw[:, tt, 0:128], start=True, stop=False, r=[b_kgm, b_uw], w=[b_PC])
                mm(PSc, lhsT=MT8[:, ch, :], rhs=Sb9[:, ch, :], start=False, stop=True, r=[b_MT8, b_Sb9[ch]], w=[b_PC])
                if ch < 7:
                    act(lambda e, ch=ch, PSc=PSc: e.copy(out=Sb9[:, ch + 1, :], in_=PSc), r=[b_PC], w=[b_Sb9[ch + 1]])
                else:
                    act(lambda e, PSc=PSc: e.copy(out=Sb9[:, 8, :], in_=PSc), r=[b_PC], w=[b_Sb9[8]])
                    if i == NT - 1:
                        act(lambda e, PSc=PSc: e.copy(out=Sf[:], in_=PSc), r=[b_PC], w=[b_Sf])
            yield
            for tt in range(4):
                mm(PA[:, tt * 128:(tt + 1) * 128], lhsT=QpA[:, tt, :], rhs=Sb9[:, 2 * tt, :], start=True, stop=False, r=[b_Qp, b_Sb9[2 * tt]], w=[b_PA])
                mm(PA[:, tt * 128:(tt + 1) * 128], lhsT=QpB[:, tt, :], rhs=Sb9[:, 2 * tt + 1, :], start=False, stop=False,
                   r=[b_Qp, b_Sb9[2 * tt + 1]], w=[b_PA])
                mm(PA[:, tt * 128:(tt + 1) * 128], lhsT=aqkT[:, tt, :], rhs=uw[:, tt, 0:128], start=False, stop=True, r=[b_aqkT, b_uw], w=[b_PA])
            act(lambda e: e.copy(out=Sb9[:, 0, :], in_=Sb9[:, 8, :]), r=[b_Sb9[8]], w=[b_Sb9[0]])
            act(lambda e: e.copy(out=og4[:].rearrange("p a b -> p (a b)"), in_=PA[:, 0:512]), r=[b_PA], w=[b_og4])
            dve(lambda e: e.tensor_tensor(out=og4q[:], in0=og4[:], in1=og4[:], op=ALU.mult), r=[b_og4], w=[b_og4q])
            dve(lambda e: e.tensor_reduce(out=og4s[:], in_=og4q[:], axis=AX.X, op=ALU.add), r=[b_og4q], w=[b_og4s])
            act(lambda e: e.activation(out=og4s[:], in_=og4s[:], func=AF.Ln, scale=1.0 / 128, bias=EPS), w=[b_og4s])
            act(lambda e: e.activation(out=og4s[:], in_=og4s[:], func=AF.Exp, scale=-0.5), w=[b_og4s])
            dve(lambda e: e.tensor_tensor(out=og4[:], in0=og4[:], in1=bcast(og4s[:], [128, 4, 128], 2), op=ALU.mult), r=[b_og4s], w=[b_og4])
            dve(lambda e: e.tensor_tensor(out=og4[:], in0=og4[:], in1=bcast(gnb[:], [128, 4, 128], 1), op=ALU.mult), r=[b_gnb], w=[b_og4])
            dve(lambda e: e.tensor_tensor(out=omg[:], in0=og4[:], in1=zsil[gp][:], op=ALU.mult), r=[b_og4] + b_zsil[gp], w=[b_omg])
            for tt in range(4):
                tr(PT[:, tt * 128:(tt + 1) * 128], omg[:, tt, :], ident_b[:], r=[b_omg, b_identb], w=[b_PT])
            act(lambda e: e.copy(out=omT[gp][:, 1, :], in_=PT[:, 0:512]), r=[b_PT], w=[b_omT[gp]])
            for hh in range(2):
                if phaseB:
                    kch = (grp * 512) // CH
                    oc = (grp * 512) % CH
                    fw.dma("sp", xin[kch][hh * 128:(hh + 1) * 128, oc:oc + 512], omT[gp][:, hh, :], reads=[b_omT[gp]], writes=[b_xin[kch]])
                else:
                    fw.dma("sp", omT_o[hh * 128:(hh + 1) * 128, grp * 512:(grp + 1) * 512], omT[gp][:, hh, :], reads=[b_omT[gp]])


        def gen_G(grp):
            gp2 = grp % 2
            fw.dma("sp", csT[gp2][:, 0].rearrange("p a b -> p (a b)"), cos_d[:, grp * 128:(grp + 1) * 128], writes=[b_csT[gp2]])
            fw.dma("sp", csT[gp2][:, 1].rearrange("p a b -> p (a b)"), sin_d[:, grp * 128:(grp + 1) * 128], writes=[b_csT[gp2]])
            for tt in range(4):
                i = grp * 4 + tt
                s = i % NXB
                fw.dma("sp", xt[s][:], x_d[i * 128:(i + 1) * 128, :], writes=[b_xt[s]])
                act(lambda e, s=s: e.activation(out=xs[s][:], in_=xt[s][:], func=AF.Square, accum_out=ssq[s][:]),
                    r=[b_xt[s]], w=[b_xs[s], b_ssq[s]])
                act(lambda e, s=s: e.activation(out=ssq[s][:], in_=ssq[s][:], func=AF.Ln, scale=1.0 / 1024, bias=EPS), w=[b_ssq[s]])
                act(lambda e, s=s: e.activation(out=ssq[s][:], in_=ssq[s][:], func=AF.Exp, scale=-0.5), w=[b_ssq[s]])
                dve(lambda e, s=s: e.tensor_scalar(out=xs[s][:], in0=xt[s][:], scalar1=ssq[s][:, 0:1], scalar2=None,
                                                   op0=ALU.mult), r=[b_xt[s], b_ssq[s]], w=[b_xs[s]])
                for kt in range(8):
                    tr(PT[:, kt * 128:(kt + 1) * 128], xs[s][:, kt * 128:(kt + 1) * 128], ident_b[:],
                       r=[b_xs[s], b_identb], w=[b_PT])
                act(lambda e, tt=tt: e.copy(out=xnT[:, :, tt * 128:(tt + 1) * 128],
                                            in_=PT[:].rearrange("p (k t) -> p k t", k=8)),
                    r=[b_PT], w=[b_xnT[tt]])

            yield
            pool(lambda e: e.tensor_copy(out=raw[gp2][:, :, 0:3], in_=raw[1 - gp2][:, :, 512:515]), r=[b_raw[1 - gp2]], w=[b_raw[gp2]])
            for c3 in range(3):
                for kt in range(8):
                    mm(PB[:, 0:512], lhsT=wgdn[:, kt, c3 * 128:(c3 + 1) * 128], rhs=xnT[:, kt, :],
                       start=(kt == 0), stop=(kt == 7), r=[b_wgdn] + b_xnT, w=[b_PB])
                act(lambda e, c3=c3: e.copy(out=raw[gp2][:, c3, 3:515], in_=PB[:, 0:512]), r=[b_PB], w=[b_raw[gp2]])

            yield

        def gen_F(i):
            grp = i // 4
            tt = i % 4
            gp2 = grp % 2
            p2 = i % 2
            for kt in range(8):
                mm(PA[:, 0:512], lhsT=xnT[:, kt, tt * 128:(tt + 1) * 128], rhs=wtok[:, kt, 0:512],
                   start=(kt == 0), stop=(kt == 7), r=[b_xnT[tt], b_wtok], w=[b_PA])
            yield
            for kt in range(8):
                mm(PA[:, 512:782], lhsT=xnT[:, kt, tt * 128:(tt + 1) * 128], rhs=wtok[:, kt, 512:782],
                   start=(kt == 0), stop=(kt == 7), r=[b_xnT[tt], b_wtok], w=[b_PA])
            yield
            act(lambda e: e.copy(out=pj[p2][:], in_=PA[:, 0:782]), r=[b_PA], w=[b_pj[p2]])
            yield
            x1 = pj[p2][:, 0:448].rearrange("p (h d) -> p h d", h=7)[:, :, 0:32]
            x2 = pj[p2][:, 0:448].rearrange("p (h d) -> p h d", h=7)[:, :, 32:64]
            cosb = bcast(csT[gp2][:, 0, tt, :], [128, 7, 32], 1)
            sinb = bcast(csT[gp2][:, 1, tt, :], [128, 7, 32], 1)
            dve(lambda e: e.tensor_tensor(out=rt[:, 0], in0=x1, in1=cosb, op=ALU.mult), r=[b_pj[p2], b_csT[gp2]], w=[b_rt])
            yield
            dve(lambda e: e.tensor_tensor(out=rt[:, 1], in0=x2, in1=sinb, op=ALU.mult), r=[b_pj[p2], b_csT[gp2]], w=[b_rt])
            yield
            dve(lambda e: e.tensor_tensor(out=rt[:, 2], in0=x2, in1=cosb, op=ALU.mult), r=[b_pj[p2], b_csT[gp2]], w=[b_rt])
            yield
            dve(lambda e: e.tensor_tensor(out=rt[:, 3], in0=x1, in1=sinb, op=ALU.mult), r=[b_pj[p2], b_csT[gp2]], w=[b_rt])
            yield
            dve(lambda e: e.tensor_tensor(out=rq[p2][:, :, 0:32], in0=rt[:, 0], in1=rt[:, 1], op=ALU.subtract),
                 r=[b_rt], w=[b_rq[p2]])
            yield
            dve(lambda e: e.tensor_tensor(out=rq[p2][:, :, 32:64], in0=rt[:, 2], in1=rt[:, 3], op=ALU.add),
                 r=[b_rt], w=[b_rq[p2]])
            yield
            pool(lambda e: e.tensor_copy(out=ko[p2][:, 0:6:2, :], in_=rq[p2][:, 4:7, :]), r=[b_rq[p2]], w=[b_ko[p2]])
            yield
            pool(lambda e: e.tensor_copy(out=ko[p2][:, 1:6:2, :],
                                         in_=pj[p2][:, 448:640].rearrange("p (h d) -> p h d", h=3)),
                 r=[b_pj[p2]], w=[b_ko[p2]])
            yield
            fw.dma("sp", kv_o[i * 128:(i + 1) * 128, :], ko[p2][:, 0:4, :].rearrange("p a b -> p (a b)"), reads=[b_ko[p2]])
            yield
            if i >= NT - 4:
                wi = i - (NT - 4)
                fw.dma("sp", win_o[wi * 128:(wi + 1) * 128, :], ko[p2][:, 4:6, :].rearrange("p a b -> p (a b)"),
                       reads=[b_ko[p2]])
            yield
            act(lambda e: e.copy(out=qkb[:], in_=rq[p2][:]), r=[b_rq[p2]], w=[b_qkb])
            yield
            act(lambda e: e.copy(out=Vsel[:, i, 0:64], in_=pj[p2][:, 512:576]), r=[b_pj[p2]], w=[b_vsel[i]])
            yield
            act(lambda e: e.copy(out=Vwin[:, i % 8, 0:64], in_=pj[p2][:, 576:640]), r=[b_pj[p2]], w=[b_vwin[i % 8]])
            yield
            dve(lambda e: e.tensor_copy(out=kvc2[:, 0, :].rearrange("p (a d) -> p a d", a=2),
                                         in_=bcast(rq[p2][:, 4, :], [128, 2, 64], 1)), r=[b_rq[p2]], w=[b_kvc2])
            yield
            dve(lambda e: e.tensor_copy(out=kvc2[:, 1, :].rearrange("p (a d) -> p a d", a=2),
                                         in_=bcast(pj[p2][:, 448:512], [128, 2, 64], 1)), r=[b_pj[p2]], w=[b_kvc2])
            yield
            act(lambda e: e.activation(out=gz[:], in_=pj[p2][:, 640:780], func=AF.Exp, scale=-1.0), r=[b_pj[p2]], w=[b_gz])
            yield
            dve(lambda e: e.tensor_scalar(out=gz[:], in0=gz[:], scalar1=1.0, scalar2=None, op0=ALU.add), w=[b_gz])
            yield
            dve(lambda e: e.reciprocal(out=gz[:], in_=gz[:]), w=[b_gz])
            yield
            dve(lambda e: e.tensor_copy(out=gates[i % 2][:], in_=gz[:, 0:12]), r=[b_gz], w=[b_gates[i % 2]])
            yield
            dve(lambda e, tt=tt: e.tensor_tensor(out=zsil[gp2][:, tt, :], in0=gz[:, 12:140], in1=pj[p2][:, 652:780], op=ALU.mult),
                r=[b_gz, b_pj[p2]], w=[b_zsil[gp2][tt]])
            yield
            pool(lambda e, tt=tt: e.tensor_copy(out=abg[gp2][:, tt, :], in_=pj[p2][:, 780:782]), r=[b_pj[p2]], w=[b_abg[gp2]])
            yield
            for h in range(4):
                tr(PT[0:64, h * 128:(h + 1) * 128], qkb[:, h, :], ident_b[:], r=[b_qkb, b_identb], w=[b_PT])
            yield
            tr(PT[0:64, 512:640], qkb[:, 5, :], ident_b[:], r=[b_qkb, b_identb], w=[b_PT])
            yield
            tr(PT[0:64, 640:768], qkb[:, 6, :], ident_b[:], r=[b_qkb, b_identb], w=[b_PT])
            yield
            tr(PT[:, 768:896], kvc2[:, 0, :], ident_b[:], r=[b_kvc2, b_identb], w=[b_PT])
            yield
            tr(PT[:, 896:1024], kvc2[:, 1, :], ident_b[:], r=[b_kvc2, b_identb], w=[b_PT])
            yield
            act(lambda e: e.copy(out=QT[i % 2][:], in_=PT[0:64, 0:512]), r=[b_PT], w=[b_QT[i % 2]])
            yield
            act(lambda e: e.copy(out=Qaug[i % 2][0:64, 0, :], in_=PT[0:64, 0:256]), r=[b_PT], w=[b_Qaug[i % 2]])
            yield
            act(lambda e: e.copy(out=Qaug[i % 2][0:64, 1, :], in_=PT[0:64, 0:256]), r=[b_PT], w=[b_Qaug[i % 2]])
            yield
            act(lambda e: e.copy(out=KselT[0:64, i * 128:(i + 1) * 128], in_=PT[0:64, 512:640]),
                r=[b_PT], w=[b_ksel[i]])
            yield
            act(lambda e: e.copy(out=KwinT[0:64, (i % 8) * 128:(i % 8 + 1) * 128], in_=PT[0:64, 640:768]),
                r=[b_PT], w=[b_kwin[i % 8]])
            yield
            pool(lambda e: e.tensor_copy(out=Rk[:, :, 0:32], in_=Rk[:, :, 128:160]), w=[b_Rk])
            yield
            act(lambda e: e.copy(out=Rk[0:64, :, 32:160], in_=PT[0:64, 768:1024].rearrange("p (a t) -> p a t", a=2)),
                r=[b_PT], w=[b_Rk])
            yield
            act(lambda e: e.copy(out=Rk[64:128, :, 31:159], in_=PT[64:128, 768:1024].rearrange("p (a t) -> p a t", a=2)),
                r=[b_PT], w=[b_Rk])
            yield
            m0 = 1 if i == 0 else 0
            nb = 8 - m0
            n0 = 8 * i - 1 + m0
            for lp in range(16):
                c0 = 16 + 2 * lp + 16 * m0
                mm(PD[0:64, 0:nb], lhsT=cmpw[:, 0, lp, :], rhs=Rk[:, 0, c0:c0 + 16 * (nb - 1) + 1:16],
                   start=(lp == 0), stop=(lp == 15), r=[b_cmpw, b_Rk], w=[b_PD])
            yield
            act(lambda e: e.activation(out=ckT[:, n0:n0 + nb], in_=PD[0:64, 0:nb], func=AF.Identity, bias=ckb[:, 0:1]),
                r=[b_PD, b_ckb], w=[b_ckT])
            yield
            for lp in range(16):
                c0 = 16 + 2 * lp + 16 * m0
                mm(PD[0:nb, 64:128], lhsT=Rk[:, 1, c0:c0 + 16 * (nb - 1) + 1:16], rhs=cmpw[:, 1, lp, :],
                   start=(lp == 0), stop=(lp == 15), r=[b_cmpw, b_Rk], w=[b_PD])
            yield
            dve(lambda e: e.tensor_tensor(out=cvnew[0:nb, :], in0=PD[0:nb, 64:128], in1=cvb[0:nb, :], op=ALU.add),
                r=[b_PD, b_cvb], w=[b_cvnew])
            yield
            segs = []
            n = n0
            while n < n0 + nb:
                jt = n // 128
                cnt = min(n0 + nb - n, (jt + 1) * 128 - n)
                segs.append((n, cnt))
                n += cnt
            for (ns, cnt) in segs:
                fw.dma("sp", cvx[ns % 128:ns % 128 + cnt, ns // 128, 0:64], cvnew[ns - n0:ns - n0 + cnt, :],
                       reads=[b_cvnew], writes=[b_cvx])
            yield

            yield

        def gen_B(i):
            grp = i // 4
            tt = i % 4
            gp2 = grp % 2
            p2 = i % 2
            njt = (8 * i + 6) // 128 + 1
            for jt in range(njt):
                pb = jt
                mm(PA[:, 0:512], lhsT=ckT[:, jt * 128:(jt + 1) * 128], rhs=QT[i % 2][:], r=[b_ckT, b_QT[i % 2]], w=[b_PA])
                act(lambda e, pb=pb: e.activation(out=PTc[pb][:], in_=PA[:, 0:512], func=AF.Exp, scale=0.125),
                    r=[b_PA], w=[b_PTc[pb]])
                mk = None
                if jt == njt - 1:
                    mk = i % 16
                elif jt == njt - 2 and i % 16 == 0:
                    mk = 16
                if mk is not None:
                    dve(lambda e, mk=mk, pb=pb: e.tensor_tensor(out=PTc[pb][:].rearrange("p (h q) -> p h q", h=4),
                                                                in0=PTc[pb][:].rearrange("p (h q) -> p h q", h=4),
                                                                in1=bcast(cmpmask[:, mk, :], [128, 4, 128], 1), op=ALU.mult),
                        r=[b_cmpmask], w=[b_PTc[pb]])
            yield
            for h in range(4):
                for jt in range(njt):
                    mm(PC[:, (h // 2) * 512 + (h % 2) * 193:(h // 2) * 512 + (h % 2) * 193 + 193],
                       lhsT=PTc[jt][:, h * 128:(h + 1) * 128], rhs=cvx[:, jt, :],
                       start=(jt == 0), stop=(jt == njt - 1), r=[b_PTc[jt], b_cvx], w=[b_PC])
            yield
            act(lambda e: e.copy(out=acc_c[:, 0:2, :], in_=PC[:, 0:386].rearrange("p (h c) -> p h c", h=2)),
                r=[b_PC], w=[b_accc])
            yield
            act(lambda e: e.copy(out=acc_c[:, 2:4, :], in_=PC[:, 512:898].rearrange("p (h c) -> p h c", h=2)),
                r=[b_PC], w=[b_accc])
            yield
            dve(lambda e: e.tensor_scalar(out=rcp[:, 0:4], in0=acc_c[:, :, 192], scalar1=1e-30, scalar2=None, op0=ALU.max),
                r=[b_accc], w=[b_rcp])
            yield
            dve(lambda e: e.reciprocal(out=rcp[:, 0:4], in_=rcp[:, 0:4]), w=[b_rcp])
            yield
            dve(lambda e: e.tensor_scalar(out=imp[:], in0=acc_c[:, 0, 64:192], scalar1=rcp[:, 0:1], scalar2=None, op0=ALU.mult),
                r=[b_accc, b_rcp], w=[b_imp])
            yield
            for h in range(1, 4):
                dve(lambda e, h=h: e.scalar_tensor_tensor(out=imp[:], in0=acc_c[:, h, 64:192], scalar=rcp[:, h:h + 1],
                                                          in1=imp[:], op0=ALU.mult, op1=ALU.add),
                    r=[b_accc, b_rcp], w=[b_imp])
            yield
            dve(lambda e: e.tensor_tensor(out=score[:], in0=imp[:], in1=prel[:, 128 - 2 * i:256 - 2 * i], op=ALU.add),
                r=[b_imp, b_prel], w=[b_score])
            yield
            dve(lambda e: e.tensor_scalar(out=score[:, 0:1], in0=score[:, 0:1], scalar1=1e4, scalar2=None, op0=ALU.add),
                w=[b_score])
            yield
            dve(lambda e: e.max(out=mx8[:, 0:8], in_=score[:]), r=[b_score], w=[b_mx8])
            yield
            dve(lambda e: e.match_replace(out=sc2[:], in_to_replace=mx8[:, 0:8], in_values=score[:], imm_value=-3e38),
                r=[b_score, b_mx8], w=[b_sc2])
            yield
            dve(lambda e: e.max(out=mx8[:, 8:16], in_=sc2[:]), r=[b_sc2], w=[b_mx8])
            yield
            dve(lambda e: e.tensor_reduce(out=thr[:], in_=mx8[:, 8:16], axis=AX.X, op=ALU.min), r=[b_mx8], w=[b_thr])
            yield
            dve(lambda e: e.tensor_scalar(out=sc2[:], in0=score[:], scalar1=thr[:, 0:1], scalar2=None, op0=ALU.is_ge),
                r=[b_score, b_thr], w=[b_sc2])
            yield
            dve(lambda e: e.scalar_tensor_tensor(out=sc2[:], in0=score[:], scalar=-1e29, in1=sc2[:],
                                                 op0=ALU.is_gt, op1=ALU.mult), r=[b_score], w=[b_sc2])
            yield
            dve(lambda e: e.tensor_scalar(out=mbt[:, 0, :], in0=sc2[:], scalar1=-NEGB, scalar2=NEGB,
                                          op0=ALU.mult, op1=ALU.add), r=[b_sc2], w=[b_mbt])
            yield
            dve(lambda e: e.tensor_copy(out=mbt[:, 1, 0:64], in_=mbt[:, 0, 64:128]), w=[b_mbt])
            yield
            dve(lambda e: e.tensor_copy(out=mbt[:, 1, 64:128], in_=mbt[:, 0, 0:64]), w=[b_mbt])
            yield
            mm(PD[:, 0:128], lhsT=mbt[:, 1, :], rhs=ident_f[:], r=[b_mbt, b_identf], w=[b_PD])
            yield
            mm(PD[:, 128:256], lhsT=mbt[:, 0, :], rhs=ident_f[:], r=[b_mbt, b_identf], w=[b_PD])
            yield
            for hh_ in range(2):
                dve(lambda e, hh_=hh_: e.tensor_copy(out=Qaug[i % 2][64:128, 0, hh_ * 128:(hh_ + 1) * 128], in_=PD[64:128, 0:128]),
                    r=[b_PD], w=[b_Qaug[i % 2]])
                dve(lambda e, hh_=hh_: e.tensor_copy(out=Qaug[i % 2][64:128, 1, hh_ * 128:(hh_ + 1) * 128], in_=PD[64:128, 128:256]),
                    r=[b_PD], w=[b_Qaug[i % 2]])
            yield

            yield
            sgroups = []
            t = 0
            while t <= i:
                gt_ = min(4, i + 1 - t)
                sgroups.append((t, gt_))
                t += gt_

            def emit_S(gi_):
                t_, gt_ = sgroups[gi_]
                PSs_ = PA if gi_ % 2 == 0 else PB
                bPSs_ = b_PA if gi_ % 2 == 0 else b_PB
                for u in range(gt_):
                    tk = t_ + u
                    ab = 0 if tk < 32 else 1
                    mm(PSs_[:, u * 256:(u + 1) * 256], lhsT=KselT[:, tk * 128:(tk + 1) * 128], rhs=Qaug[i % 2][:, ab, :],
                       r=[b_ksel[tk], b_eind, b_Qaug[i % 2]], w=[bPSs_])
            emit_S(0)
            yield
            for gi in range(len(sgroups)):
                t, gt_ = sgroups[gi]
                pb = gi % 2
                PSs = PA if pb == 0 else PB
                bPSs = b_PA if pb == 0 else b_PB
                if gi + 1 < len(sgroups):
                    emit_S(gi + 1)
                yield
                act(lambda e, PSs=PSs, gt_=gt_, pb=pb: e.activation(out=PTs[pb][:, 0:gt_ * 256], in_=PSs[:, 0:gt_ * 256],
                                                                  func=AF.Exp, scale=0.125), r=[bPSs], w=[b_PTs[pb]])
                if t + gt_ - 1 == i:
                    u = gt_ - 1
                    dve(lambda e, u=u, pb=pb: e.tensor_tensor(
                        out=PTs[pb][:, u * 256:(u + 1) * 256].rearrange("p (h q) -> p h q", h=2),
                        in0=PTs[pb][:, u * 256:(u + 1) * 256].rearrange("p (h q) -> p h q", h=2),
                        in1=bcast(tri[:, 0, :], [128, 2, 128], 1), op=ALU.mult), r=[b_tri], w=[b_PTs[pb]])
                for u in range(gt_):
                    tk = t + u
                    for h in range(2):
                        mm(PC[:, h * 512:h * 512 + 65], lhsT=PTs[pb][:, u * 256 + h * 128:u * 256 + (h + 1) * 128],
                           rhs=Vsel[:, tk, :], start=(tk == 0), stop=(tk == i), r=[b_PTs[pb], b_vsel[tk]], w=[b_PC])
            yield
            gi = len(sgroups)
            yield
            act(lambda e: e.copy(out=acc_sw[:, 0, :], in_=PC[:, 0:65]), r=[b_PC], w=[b_accsw])
            yield
            act(lambda e: e.copy(out=acc_sw[:, 1, :], in_=PC[:, 512:577]), r=[b_PC], w=[b_accsw])
            yield

            yield
            t0w = max(0, i - 4)
            wt = list(range(t0w, i + 1))
            pb = gi % 2
            PSs = PA if pb == 0 else PB
            bPSs = b_PA if pb == 0 else b_PB
            for gsub in range(0, len(wt), 4):
                sub = wt[gsub:gsub + 4]
                for u, tk in enumerate(sub):
                    mm(PSs[:, u * 256:(u + 1) * 256], lhsT=KwinT[:, (tk % 8) * 128:(tk % 8 + 1) * 128], rhs=QT[i % 2][:, 0:256],
                       r=[b_kwin[tk % 8], b_QT[i % 2]], w=[bPSs])
                act(lambda e, PSs=PSs, n_=len(sub), pb=pb: e.activation(out=PTs[pb][:, 0:n_ * 256], in_=PSs[:, 0:n_ * 256],
                                                                       func=AF.Exp, scale=0.125), r=[bPSs], w=[b_PTs[pb]])
                for u, tk in enumerate(sub):
                    mkk = None
                    if tk == i:
                        mkk = 0
                    elif tk == i - 4:
                        mkk = 1
                    if mkk is not None:
                        dve(lambda e, u=u, pb=pb, mkk=mkk: e.tensor_tensor(
                            out=PTs[pb][:, u * 256:(u + 1) * 256].rearrange("p (h q) -> p h q", h=2),
                            in0=PTs[pb][:, u * 256:(u + 1) * 256].rearrange("p (h q) -> p h q", h=2),
                            in1=bcast(tri[:, mkk, :], [128, 2, 128], 1), op=ALU.mult), r=[b_tri], w=[b_PTs[pb]])
                for u, tk in enumerate(sub):
                    for h in range(2):
                        mm(PC[:, 386 + h * 512:451 + h * 512], lhsT=PTs[pb][:, u * 256 + h * 128:u * 256 + (h + 1) * 128],
                           rhs=Vwin[:, tk % 8, :], start=(tk == wt[0]), stop=(tk == i), r=[b_PTs[pb], b_vwin[tk % 8]], w=[b_PC])
                pb = 1 - pb
                PSs = PA if pb == 0 else PB
                bPSs = b_PA if pb == 0 else b_PB
            yield
            act(lambda e: e.copy(out=acc_sw[:, 2, :], in_=PC[:, 386:451]), r=[b_PC], w=[b_accsw])
            yield
            act(lambda e: e.copy(out=acc_sw[:, 3, :], in_=PC[:, 898:963]), r=[b_PC], w=[b_accsw])
            yield

            yield
            dve(lambda e: e.tensor_scalar(out=rcp[:, 4:8], in0=acc_sw[:, :, 64], scalar1=1e-30, scalar2=None, op0=ALU.max),
                r=[b_accsw], w=[b_rcp])
            yield
            dve(lambda e: e.reciprocal(out=rcp[:, 4:8], in_=rcp[:, 4:8]), w=[b_rcp])
            yield
            g3 = gates[i % 2][:, 0:6].rearrange("p (h j) -> p h j", h=2)
            cf = coef[:].rearrange("p (h j) -> p h j", h=2)
            dve(lambda e: e.tensor_tensor(out=cf[:, :, 0], in0=g3[:, :, 0], in1=rcp[:, 0:2], op=ALU.mult),
                r=[b_gates[i % 2], b_rcp], w=[b_coef])
            yield
            dve(lambda e: e.tensor_tensor(out=cf[:, :, 1], in0=g3[:, :, 1], in1=rcp[:, 4:6], op=ALU.mult),
                r=[b_gates[i % 2], b_rcp], w=[b_coef])
            yield
            dve(lambda e: e.tensor_tensor(out=cf[:, :, 2], in0=g3[:, :, 2], in1=rcp[:, 6:8], op=ALU.mult),
                r=[b_gates[i % 2], b_rcp], w=[b_coef])
            yield
            for h in range(2):
                dve(lambda e, h=h: e.tensor_scalar(out=onsa[:, h * 64:(h + 1) * 64], in0=acc_c[:, h, 0:64],
                                                   scalar1=coef[:, 3 * h:3 * h + 1], scalar2=None, op0=ALU.mult),
                    r=[b_accc, b_coef], w=[b_onsa])
                dve(lambda e, h=h: e.scalar_tensor_tensor(out=onsa[:, h * 64:(h + 1) * 64], in0=acc_sw[:, h, 0:64],
                                                          scalar=coef[:, 3 * h + 1:3 * h + 2], in1=onsa[:, h * 64:(h + 1) * 64],
                                                          op0=ALU.mult, op1=ALU.add), r=[b_accsw, b_coef], w=[b_onsa])
                dve(lambda e, h=h: e.scalar_tensor_tensor(out=om[:, h * 64:(h + 1) * 64], in0=acc_sw[:, 2 + h, 0:64],
                                                          scalar=coef[:, 3 * h + 2:3 * h + 3], in1=onsa[:, h * 64:(h + 1) * 64],
                                                          op0=ALU.mult, op1=ALU.add), r=[b_accsw, b_coef, b_onsa], w=[b_om])
            yield
            b_om_tiles = None

            if dbg and i == 1:
                fw.dma("sp", dbg_o[:, 0:772], acc_c[:].rearrange("p a b -> p (a b)"), reads=[b_accc])
                fw.dma("sp", dbg_o[:, 772:1032], acc_sw[:].rearrange("p a b -> p (a b)"), reads=[b_accsw])
                fw.dma("sp", dbg_o[:, 1032:1160], score[:], reads=[b_score])
                fw.dma("sp", dbg_o[:, 1160:1172], gates[i % 2][:], reads=[b_gates[i % 2]])
                fw.dma("sp", dbg_o[:, 1172:1300], mbt[:, 0, :], reads=[b_mbt])
            yield
            tr(PT[:, 0:128], om[:, 0:128], ident_b[:], r=[b_om, b_identb], w=[b_PT])
            yield
            act(lambda e, tt=tt: e.copy(out=omT[gp2][:, 0, tt * 128:(tt + 1) * 128], in_=PT[:, 0:128]), r=[b_PT], w=[b_omT[gp2]])
            yield

        if phaseB:
            wgu_s = nc.dram_tensor("wgu_s", [22, 128, 8 * 256], BF16).ap()
            wdn_s = nc.dram_tensor("wdn_s", [22, 128, 1024], BF16).ap()
            b_wgus = [Buf() for _ in range(22)]
            b_wdns = Buf()
            precast = []
            for f in range(22):
                dstv = wgu_s[f].rearrange("p (k c) -> p k c", k=8)
                precast.append(lambda f=f, dstv=dstv: fw.dma("pool", dstv[:, :, 0:128],
                                                              wgu_d[:, f * 128:(f + 1) * 128].rearrange("(k p) c -> p k c", p=128),
                                                              writes=[b_wgus[f]]))
                precast.append(lambda f=f, dstv=dstv: fw.dma("pool", dstv[:, :, 128:256],
                                                              wgu_d[:, 2816 + f * 128:2816 + (f + 1) * 128].rearrange("(k p) c -> p k c", p=128),
                                                              writes=[b_wgus[f]]))
            precast.append(lambda: fw.dma("pool", wdn_s[:, :, :], wdn_d[:, :].rearrange("(f p) c -> f p c", p=128), writes=[b_wdns]))
        gdn_prev = None
        try:
          chk("setup")
          def drain_gen(g_):
              if g_ is not None:
                  for _ in g_:
                      pass

          def interleave(gens):
              alive = [g_ for g_ in gens if g_ is not None]
              while alive:
                  for g_ in list(alive):
                      try:
                          next(g_)
                      except StopIteration:
                          alive.remove(g_)

          drain_gen(gen_G(0))
          drain_gen(gen_F(0))
          for i in range(NT):
              grp = i // 4
              if phaseB and i >= 2 and precast:
                  precast.pop(0)()
              gF = None
              if i + 1 < NT:
                  def chain_next(i=i):
                      if (i + 1) % 4 == 0:
                          yield from gen_G((i + 1) // 4)
                      yield from gen_F(i + 1)
                  gF = chain_next()
              interleave([gen_B(i), gF])
              if gdn_prev is not None:
                  for _ in range(4):
                      next(gdn_prev, None)
              if i % 4 == 3:
                  drain_gen(gdn_prev)
                  gdn_prev = gdn_gen(grp)

        except _Stop:
            pass
        if gdn_prev is not None:
            for _ in gdn_prev:
                pass
        if phaseB:
            while precast:
                precast.pop(0)()
        fw.dma("sp", S_o[:, :], Sf[:], reads=[b_Sf])
        fw.dma("sp", conv_o[:, :, :], raw[(NG - 1) % 2][:, :, 512:515], reads=[b_raw[(NG - 1) % 2]])

        if phaseB:
            xout = [nc.dram_tensor("xout%d" % k, [1024, CH], BF16).ap() for k in range(NCH)]
            b_xout = Buf()
            RG = [[0, 1, 2, 3], [4, 5, 6, 7]]
            ccs = st.enter_context(nc.semaphore("ccs"))
            for k in range(NCH if not SKIP_CC else 0):
                fw._wait("pool", b_xin[k].w)
                nc.gpsimd.collective_compute("AllGather", ALU.bypass, replica_groups=RG, ins=[xin[k][:, :].opt()],
                                             outs=[xout[k][:, :].opt()]).then_inc(ccs)
                nc.gpsimd.wait_ge(ccs, k + 1)
            pool(lambda e: e.memset(cvnew[0:1, 0:1], 0.0), w=[b_xout, b_cvnew])
            fw.barrier()
            stA.close()
            stS = st.enter_context(ExitStack())
            cur[0] = stS
            nb_ = [0]

            def T(shape, dt=F32):
                nb_[0] += 1
                return sb("sm%d" % nb_[0], shape, dt), Buf()

            xs_t, b_xs = T([4, 1024])
            gmix2, b_gmix2 = T([128, 8])
            ropes, b_ropes = T([4, 64])
            ptab, b_ptab = T([128, 256], I32)
            iota_c, b_iota = T([128, 1])
            idxs, b_idxs = T([128, 256], I32)
            cmpw64, b_cmpw64 = T([128, 2, 32, 64], BF16)
            pe64, b_pe64 = T([128, 2, 32], BF16)
            c2s2, b_c2s2 = T([128, 4, 128])
            on511, b_on511 = T([128, 4])
            oh4, b_oh4 = T([4, 80])
            bonus, b_bonus = T([1, 128])
            alogb, b_alogb = T([4, 8])
            gnrow, b_gnrow = T([1, 128])
            pjs, b_pjs = T([4, 3360])
            Xs, b_Xs = T([4, 2056])
            QKT, b_QKT = T([128, 14, 4], BF16)
            OGT, b_OGT = T([128, 4, 4], BF16)
            one11, b_one11 = T([1, 1])
            cb_s, b_cbs = T([64, 2])
            stSg = st.enter_context(ExitStack())
            cur[0] = stSg
            cwb, b_cwb = T([4, 4, 1536])
            stc, b_stc = T([4, 3, 1536])
            fw.dma("sp", xs_t[:], xs_d[:, :], writes=[b_xs])
            fw.dma("sp", gmix2[:], gmix_d[:, :], writes=[b_gmix2])
            fw.dma("sp", ropes[:], ropes_d[:, :], writes=[b_ropes])
            fw.dma("sp", ptab[:], ptab_d[:, :], writes=[b_ptab])
            fw.dma("sp", iota_c[:], iota_d[:, :], writes=[b_iota])
            fw.dma("pool", cmpw64[:].rearrange("p a b c -> p (a b c)"), cmpw64_d[:, :], writes=[b_cmpw64])
            fw.dma("pool", pe64[:].rearrange("p a b -> p (a b)"), pe64_d[:, :], writes=[b_pe64])
            fw.dma("sp", c2s2[:].rearrange("p a b -> p (a b)"), c2s_d[:, :], writes=[b_c2s2])
            fw.dma("sp", on511[:], ones511_d[:, :], writes=[b_on511])
            fw.dma("sp", oh4[:], oh4_d[:, :], writes=[b_oh4])
            fw.dma("sp", bonus[:], bonus_d[:, :], writes=[b_bonus])
            fw.dma("sp", alogb[:], alogb_d[:, :], writes=[b_alogb])
            fw.dma("sp", gnrow[:], gnrow_d[:, :], writes=[b_gnrow])
            fw.dma("sp", cwb[:].rearrange("p a b -> p (a b)"), convwb_d[:, :, :].rearrange("p a b -> p (a b)"), writes=[b_cwb])
            fw.dma("sp", stc[:].rearrange("p a b -> p (a b)"), gconv_d[:, :, :].rearrange("p a b -> p (a b)"), writes=[b_stc])
            dve(lambda e: e.tensor_scalar(out=idxs[:], in0=ptab[:], scalar1=128.0, scalar2=iota_c[:, 0:1], op0=ALU.mult, op1=ALU.add),
                r=[b_ptab, b_iota], w=[b_idxs])

            s_sq, b_ssq_ = T([4, 1024], BF16)
            s_ss, b_sss = T([4, 1])
            s_xn, b_sxn = T([4, 1024], BF16)
            xsT, b_xsT = T([128, 8, 4], BF16)
            wch0, b_wch0 = T([128, 8, 480], BF16)
            wch1, b_wch1 = T([128, 8, 480], BF16)
            wch = [wch0, wch1]; b_wch = [b_wch0, b_wch1]
            act(lambda e: e.activation(out=s_sq[:], in_=xs_t[:], func=AF.Square, accum_out=s_ss[:]), r=[b_xs], w=[b_ssq_, b_sss])
            act(lambda e: e.activation(out=s_ss[:], in_=s_ss[:], func=AF.Sqrt, scale=1.0 / 1024, bias=EPS), w=[b_sss])
            dve(lambda e: e.reciprocal(out=s_ss[:], in_=s_ss[:]), w=[b_sss])
            dve(lambda e: e.tensor_scalar(out=s_xn[:], in0=xs_t[:], scalar1=s_ss[:, 0:1], scalar2=None, op0=ALU.mult), r=[b_xs, b_sss], w=[b_sxn])
            for kt in range(8):
                tr(PT[:, kt * 4:(kt + 1) * 4], s_xn[0:4, kt * 128:(kt + 1) * 128], ident_b[0:4, 0:4], r=[b_sxn, b_identb], w=[b_PT])
            for kt in range(8):
                act(lambda e, kt=kt: e.activation(out=xsT[:, kt, :], in_=PT[:, kt * 4:(kt + 1) * 4], func=AF.Copy, scale=gmix2[:, kt:kt + 1]),
                    r=[b_PT, b_gmix2], w=[b_xsT])
            for ch in range(7):
                wb = ch % 2
                fw.dma("pool", wch[wb][:], win_full_d[:, ch * 480:(ch + 1) * 480].rearrange("(k p) c -> p k c", p=128), writes=[b_wch[wb]])
                for kt in range(8):
                    mm(PA[0:4, 0:480], lhsT=xsT[:, kt, :], rhs=wch[wb][:, kt, :], start=(kt == 0), stop=(kt == 7), r=[b_xsT, b_wch[wb]], w=[b_PA])
                act(lambda e, ch=ch: e.copy(out=pjs[:, ch * 480:(ch + 1) * 480], in_=PA[0:4, 0:480]), r=[b_PA], w=[b_pjs])

            chk2('s1')
            qkr, b_qkr = T([4, 14, 64])
            rqk, b_rqk = T([4, 14, 64])
            rts, b_rts = T([4, 4, 14, 32])
            kvv = pjs[:, 512:1280].rearrange("p (b k g d) -> p b k g d", b=3, k=2, g=2)
            pool(lambda e: e.tensor_copy(out=qkr[:, 0:8, :], in_=pjs[:, 0:512].rearrange("p (h d) -> p h d", h=8)), r=[b_pjs], w=[b_qkr])
            for br in range(3):
                pool(lambda e, br=br: e.tensor_copy(out=qkr[:, 8 + 2 * br:10 + 2 * br, :], in_=kvv[:, br, 0, :, :]), r=[b_pjs], w=[b_qkr])
            cs_b = bcast(ropes[:, 0:32], [4, 14, 32], 1)
            sn_b = bcast(ropes[:, 32:64], [4, 14, 32], 1)
            pool(lambda e: e.tensor_tensor(out=rts[:, 0], in0=qkr[:, :, 0:32], in1=cs_b, op=ALU.mult), r=[b_qkr, b_ropes], w=[b_rts])
            pool(lambda e: e.tensor_tensor(out=rts[:, 1], in0=qkr[:, :, 32:64], in1=sn_b, op=ALU.mult), r=[b_qkr, b_ropes], w=[b_rts])
            pool(lambda e: e.tensor_tensor(out=rts[:, 2], in0=qkr[:, :, 32:64], in1=cs_b, op=ALU.mult), r=[b_qkr, b_ropes], w=[b_rts])
            pool(lambda e: e.tensor_tensor(out=rts[:, 3], in0=qkr[:, :, 0:32], in1=sn_b, op=ALU.mult), r=[b_qkr, b_ropes], w=[b_rts])
            pool(lambda e: e.tensor_tensor(out=rqk[:, :, 0:32], in0=rts[:, 0], in1=rts[:, 1], op=ALU.subtract), r=[b_rts], w=[b_rqk])
            pool(lambda e: e.tensor_tensor(out=rqk[:, :, 32:64], in0=rts[:, 2], in1=rts[:, 3], op=ALU.add), r=[b_rts], w=[b_rqk])
            kvs_t, b_kvs = T([4, 4, 2, 64])
            wnew, b_wnew = T([4, 2, 2, 64])
            pool(lambda e: e.tensor_copy(out=kvs_t[:, 0], in_=rqk[:, 8:10, :]), r=[b_rqk], w=[b_kvs])
            pool(lambda e: e.tensor_copy(out=kvs_t[:, 1], in_=kvv[:, 0, 1, :, :]), r=[b_pjs], w=[b_kvs])
            pool(lambda e: e.tensor_copy(out=kvs_t[:, 2], in_=rqk[:, 10:12, :]), r=[b_rqk], w=[b_kvs])
            pool(lambda e: e.tensor_copy(out=kvs_t[:, 3], in_=kvv[:, 1, 1, :, :]), r=[b_pjs], w=[b_kvs])
            pool(lambda e: e.tensor_copy(out=wnew[:, 0], in_=rqk[:, 12:14, :]), r=[b_rqk], w=[b_wnew])
            pool(lambda e: e.tensor_copy(out=wnew[:, 1], in_=kvv[:, 2, 1, :, :]), r=[b_pjs], w=[b_wnew])
            fw.dma("sp", kvs_o[:, :], kvs_t[:].rearrange("p a b c -> p (a b c)"), reads=[b_kvs])
            fw.dma("sp", wins_o[:, 511, :], wnew[:].rearrange("p a b c -> p (a b c)"), reads=[b_wnew])
            for s_ in range(4):
                fw.dma("sp", wins_o[s_, 0:511, :], wincache_d[s_, 1:512, :])
            fw.dma("sp", convs_o[:, 0:2, :], gconv_d[:, 1:3, :])
            fw.dma("sp", convs_o[:, 2, :], pjs[:, 1304:2840], reads=[b_pjs])
            chk2('s2')
            qkb_s, b_qkbs = T([4, 14, 64], BF16)
            act(lambda e: e.copy(out=qkb_s[:], in_=rqk[:]), r=[b_rqk], w=[b_qkbs])
            for hd in range(14):
                tr(PT[0:64, hd * 4:(hd + 1) * 4], qkb_s[0:4, hd, :], ident_b[0:4, 0:4], r=[b_qkbs, b_identb], w=[b_PT])
            act(lambda e: e.copy(out=QKT[0:64].rearrange("p a b -> p (a b)"), in_=PT[0:64, 0:56]), r=[b_PT], w=[b_QKT])
            fw.dma("sp", QKT[64:128].rearrange("p a b -> p (a b)"), QKT[0:64].rearrange("p a b -> p (a b)"), reads=[b_QKT], writes=[b_QKT])

            chk2('s3')
            for kv in range(2):
                for l in range(32):
                    mm(PD[0:64, kv:kv + 1], lhsT=cmpw64[0:64, kv, l, :], rhs=pe64[0:64, kv, l:l + 1], start=(l == 0), stop=(l == 31),
                       r=[b_cmpw64, b_pe64], w=[b_PD])
            act(lambda e: e.copy(out=cb_s[:], in_=PD[0:64, 0:2]), r=[b_PD], w=[b_cbs])

            chk2('s3b')
            tmpc, b_tmpc = T([4, 1536])
            caccs, b_caccs = T([4, 1536])
            sqs, b_sqs = T([4, 1024])
            rn8, b_rn8 = T([4, 8])
            gsm, b_gsm = T([4, 8])
            dve(lambda e: e.tensor_tensor(out=caccs[:], in0=stc[:, 0, :], in1=cwb[:, 0, :], op=ALU.mult), r=[b_stc, b_cwb], w=[b_caccs])
            for jj in range(1, 4):
                src = stc[:, jj, :] if jj < 3 else pjs[:, 1304:2840]
                dve(lambda e, jj=jj, src=src: e.tensor_tensor(out=tmpc[:], in0=src, in1=cwb[:, jj, :], op=ALU.mult),
                    r=[b_stc, b_cwb, b_pjs], w=[b_tmpc])
                dve(lambda e: e.tensor_tensor(out=caccs[:], in0=caccs[:], in1=tmpc[:], op=ALU.add), r=[b_tmpc], w=[b_caccs])
            act(lambda e: e.activation(out=Xs[:, 0:1536], in_=caccs[:], func=AF.Silu), r=[b_caccs], w=[b_Xs])
            dve(lambda e: e.tensor_tensor(out=sqs[:], in0=Xs[:, 0:1024], in1=Xs[:, 0:1024], op=ALU.mult), r=[b_Xs], w=[b_sqs])
            dve(lambda e: e.tensor_reduce(out=rn8[:], in_=sqs[:].rearrange("p (h d) -> p h d", h=8), axis=AX.X, op=ALU.add), r=[b_sqs], w=[b_rn8])
            act(lambda e: e.activation(out=rn8[:], in_=rn8[:], func=AF.Sqrt, bias=EPS), w=[b_rn8])
            dve(lambda e: e.reciprocal(out=rn8[:], in_=rn8[:]), w=[b_rn8])
            dve(lambda e: e.tensor_scalar(out=rn8[:, 0:4], in0=rn8[:, 0:4], scalar1=128.0 ** -0.5, scalar2=None, op0=ALU.mult), w=[b_rn8])
            dve(lambda e: e.tensor_tensor(out=Xs[:, 0:1024].rearrange("p (h d) -> p h d", h=8), in0=Xs[:, 0:1024].rearrange("p (h d) -> p h d", h=8),
                                          in1=bcast(rn8[:], [4, 8, 128], 2), op=ALU.mult), r=[b_rn8], w=[b_Xs])
            act(lambda e: e.activation(out=Xs[:, 1536:2048], in_=pjs[:, 2840:3352], func=AF.Silu), r=[b_pjs], w=[b_Xs])
            act(lambda e: e.activation(out=Xs[:, 2048:2052], in_=pjs[:, 3356:3360], func=AF.Sigmoid), r=[b_pjs], w=[b_Xs])
            dve(lambda e: e.tensor_tensor(out=gsm[:, 0:4], in0=pjs[:, 3352:3356], in1=alogb[:, 4:8], op=ALU.add), r=[b_pjs, b_alogb], w=[b_gsm])
            act(lambda e: e.activation(out=gsm[:, 0:4], in_=gsm[:, 0:4], func=AF.Exp), w=[b_gsm])
            act(lambda e: e.activation(out=gsm[:, 0:4], in_=gsm[:, 0:4], func=AF.Ln, bias=1.0), w=[b_gsm])
            act(lambda e: e.activation(out=gsm[:, 4:8], in_=alogb[:, 0:4], func=AF.Exp), r=[b_alogb], w=[b_gsm])
            dve(lambda e: e.tensor_tensor(out=gsm[:, 0:4], in0=gsm[:, 0:4], in1=gsm[:, 4:8], op=ALU.mult), w=[b_gsm])
            act(lambda e: e.activation(out=Xs[:, 2052:2056], in_=gsm[:, 0:4], func=AF.Exp, scale=-1.0), r=[b_gsm], w=[b_Xs])

            chk2('s4')
            Rrow, b_Rrow = T([1, 2056])
            cols_s, b_cols = T([128, 8])
            S_t = [T([128, 128]) for _ in range(2)]
            r1, b_r1 = T([1, 257])
            vn, b_vn = T([1, 128])
            og, b_og = T([1, 128])
            ogs, b_ogs = T([1, 4])
            ogq, b_ogq = T([1, 128])
            egb, b_egb = T([128, 1])
            Snew = [T([128, 128]) for _ in range(2)]
            pool(lambda e: e.memset(one11[:], 1.0), w=[b_one11])
            for s_ in range(4):
                for chn, (c0, c1) in enumerate([(0, 512), (512, 1024), (1024, 1536), (1536, 2048), (2048, 2056)]):
                    mm(PD[0:1, 0:c1 - c0], lhsT=ident_f[0:4, s_:s_ + 1], rhs=Xs[0:4, c0:c1], r=[b_identf, b_Xs], w=[b_PD])
                    act(lambda e, c0=c0, c1=c1: e.copy(out=Rrow[0:1, c0:c1], in_=PD[0:1, 0:c1 - c0]), r=[b_PD], w=[b_Rrow])
                for hq in range(8):
                    mm(PD[:, hq:hq + 1], lhsT=Rrow[0:1, hq * 128:(hq + 1) * 128], rhs=one11[:], r=[b_Rrow, b_one11], w=[b_PD])
                act(lambda e: e.copy(out=cols_s[:], in_=PD[:, 0:8]), r=[b_PD], w=[b_cols])
                for h in range(4):
                    sbi = (s_ * 4 + h) % 2
                    St, bSt = S_t[sbi]
                    Sn, bSn = Snew[sbi]
                    fw.dma("sp", St[:], gS_d[s_ * 4 + h, :, :], writes=[bSt])
                    mm(PD[0:1, 0:128], lhsT=cols_s[:, 4 + h:5 + h], rhs=St[:], r=[b_cols, bSt], w=[b_PD])
                    mm(PD[0:1, 128:256], lhsT=cols_s[:, h:h + 1], rhs=St[:], r=[b_cols, bSt], w=[b_PD])
                    mm(PD[0:1, 256:257], lhsT=cols_s[:, h:h + 1], rhs=cols_s[:, 4 + h:5 + h], r=[b_cols], w=[b_PD])
                    act(lambda e: e.copy(out=r1[:], in_=PD[0:1, 0:257]), r=[b_PD], w=[b_r1])
                    egs = Rrow[0:1, 2052 + h:2053 + h]
                    bts = Rrow[0:1, 2048 + h:2049 + h]
                    dve(lambda e, egs=egs: e.tensor_scalar(out=vn[:], in0=r1[0:1, 0:128], scalar1=egs, scalar2=None, op0=ALU.mult),
                        r=[b_r1, b_Rrow], w=[b_vn])
                    dve(lambda e, h=h: e.tensor_tensor(out=vn[:], in0=Rrow[0:1, 1024 + h * 128:1024 + (h + 1) * 128], in1=vn[:], op=ALU.subtract),
                        r=[b_Rrow], w=[b_vn])
                    dve(lambda e, bts=bts: e.tensor_scalar(out=vn[:], in0=vn[:], scalar1=bts, scalar2=None, op0=ALU.mult), r=[b_Rrow], w=[b_vn])
                    dve(lambda e, egs=egs: e.tensor_scalar(out=og[:], in0=r1[0:1, 128:256], scalar1=egs, scalar2=None, op0=ALU.mult),
                        r=[b_r1, b_Rrow], w=[b_og])
                    dve(lambda e: e.scalar_tensor_tensor(out=og[:], in0=vn[:], scalar=r1[0:1, 256:257], in1=og[:], op0=ALU.mult, op1=ALU.add),
                        r=[b_vn, b_r1], w=[b_og])
                    act(lambda e: e.activation(out=ogq[:], in_=og[:], func=AF.Square, accum_out=ogs[0:1, 0:1]), r=[b_og], w=[b_ogq, b_ogs])
                    act(lambda e: e.activation(out=ogs[0:1, 0:1], in_=ogs[0:1, 0:1], func=AF.Sqrt, scale=1.0 / 128, bias=EPS), w=[b_ogs])
                    dve(lambda e: e.reciprocal(out=ogs[0:1, 0:1], in_=ogs[0:1, 0:1]), w=[b_ogs])
                    dve(lambda e: e.scalar_tensor_tensor(out=og[:], in0=og[:], scalar=ogs[0:1, 0:1], in1=gnrow[:], op0=ALU.mult, op1=ALU.mult),
                        r=[b_ogs, b_gnrow], w=[b_og])
                    dve(lambda e, h=h: e.tensor_tensor(out=og[:], in0=og[:], in1=Rrow[0:1, 1536 + h * 128:1536 + (h + 1) * 128], op=ALU.mult),
                        r=[b_Rrow], w=[b_og])
                    mm(PD[:, 300:301], lhsT=og[:], rhs=one11[:], r=[b_og, b_one11], w=[b_PD])
                    act(lambda e, h=h, s_=s_: e.copy(out=OGT[:, h, s_:s_ + 1], in_=PD[:, 300:301]), r=[b_PD], w=[b_OGT])
                    mm(PB[:, 0:128], lhsT=Rrow[0:1, 512 + h * 128:512 + (h + 1) * 128], rhs=vn[:], r=[b_Rrow, b_vn], w=[b_PB])
                    mm(PD[:, 310:311], lhsT=ones_f2[0:1, :], rhs=egs, r=[b_onesf2, b_Rrow], w=[b_PD])
                    act(lambda e: e.copy(out=egb[:], in_=PD[:, 310:311]), r=[b_PD], w=[b_egb])
                    dve(lambda e, St=St, Sn=Sn: e.scalar_tensor_tensor(out=Sn[:], in0=St[:], scalar=egb[:, 0:1], in1=PB[:, 0:128],
                                                                    op0=ALU.mult, op1=ALU.add), r=[bSt, b_egb, b_PB], w=[bSn])
                    fw.dma("sp", Ss_o[s_ * 4 + h, :, :], Sn[:], reads=[bSn])

            chk2('s5')
            fw.barrier()
            stSg.close()
            stSn = st.enter_context(ExitStack())
            cur[0] = stSn
            ckTs, b_ckTs = T([64, 512], BF16)
            cvTs, b_cvTs = T([64, 512], BF16)
            cvxs, b_cvxs = T([128, 4, 193], BF16)
            Pc, b_Pc = T([128, 16], BF16)
            accs, b_accs = T([4, 193])
            rcs, b_rcs = T([4, 4])
            impn, b_impn = T([4, 128])
            scs, b_scs = T([1, 136])
            sc2s, b_sc2s = T([1, 136])
            mx8s, b_mx8s = T([1, 16])
            thrs, b_thrs = T([1, 1])
            mbrow, b_mbrow = T([1, 128])
            MBp1, b_MBp1 = T([2, 64])
            MBp, b_MBp = T([2, 64, 4], BF16)
            efix, b_efix = T([2, 128], BF16)
            oh2, b_oh2 = T([1, 4])
            fw.dma("pool", efix[:], efix_d[:, :], writes=[b_efix])
            fw.dma("sp", oh2[:], oh2_d[:, :], writes=[b_oh2])
            Psel, b_Psel = T([128, 256], BF16)
            pnew, b_pnew = T([4, 2])
            vrow, b_vrow = T([4, 128])
            Abr, b_Abr = T([4, 3, 8, 64])
            asel, b_asel = T([4, 2, 65])
            wc, b_wc = T([128, 4, 256])
            wcb, b_wcb = T([128, 4, 128], BF16)
            Vws2 = [T([128, 4, 2, 65], BF16) for _ in range(2)]
            KwTs2 = [T([128, 4, 128], BF16) for _ in range(2)]
            Pw, b_Pw = T([128, 16], BF16)
            stSk = st.enter_context(ExitStack())
            cur[0] = stSk
            KTs2 = [T([128, 3, 8192], BF16) for _ in range(2)]
            Vs2 = [T([128, 64, 2, 65], BF16) for _ in range(2)]
            pg = [T([128, 512]) for _ in range(4)]
            pgb = [T([128, 384], BF16) for _ in range(3)]
            for q_ in range(2):
                pool(lambda e, q_=q_: e.memset(Vs2[q_][0][:, :, :, 64:65], 1.0), w=[Vs2[q_][1]])
                pool(lambda e, q_=q_: e.memset(Vws2[q_][0][:, :, :, 64:65], 1.0), w=[Vws2[q_][1]])
            pool(lambda e: e.memset(ckTs[:], 0.0), w=[b_ckTs])
            pool(lambda e: e.memset(cvTs[:], 0.0), w=[b_cvTs])
            pool(lambda e: e.memset(cvxs[:], 0.0), w=[b_cvxs])
            pool(lambda e: e.tensor_copy(out=cvxs[:, :, 64:192], in_=c2s2[:]), r=[b_c2s2], w=[b_cvxs])
            pool(lambda e: e.tensor_copy(out=cvxs[:, :, 192], in_=on511[:]), r=[b_on511], w=[b_cvxs])
            pool(lambda e: e.memset(scs[:], 1e4), w=[b_scs])
            def gen_pages(s_):
                KTs, b_KTs = KTs2[s_ % 2]
                Vs, b_Vs = Vs2[s_ % 2]
                Vws, b_Vws = Vws2[s_ % 2]
                KwTs, b_KwTs = KwTs2[s_ % 2]
                for p_ in range(64):
                    pgt, bpg = pg[p_ % 4]
                    pgbt, bpgb = pgb[p_ % 3]
                    col = s_ * 64 + p_
                    fw.dma("pool", None, None, reads=[b_idxs], writes=[bpg],
                           fn=lambda e, pgt=pgt, col=col: e.indirect_dma_start(
                               out=pgt[:], out_offset=None, in_=cache_d[:, :],
                               in_offset=bass.IndirectOffsetOnAxis(ap=idxs[:, col:col + 1], axis=0)))
                    dve(lambda e, pgt=pgt, pgbt=pgbt: e.tensor_copy(out=pgbt[:], in_=pgt[:, 0:384]), r=[bpg], w=[bpgb])
                    act(lambda e, pgt=pgt, p_=p_: e.copy(out=Vs[:, p_, :, 0:64], in_=pgt[:, 384:512].rearrange("p (g d) -> p g d", g=2)),
                        r=[bpg], w=[b_Vs])
                    for kg in range(3):
                        tr(PT[:, kg * 128:(kg + 1) * 128], pgbt[:, kg * 128:(kg + 1) * 128], ident_b[:], r=[bpgb, b_identb], w=[b_PT])
                    act(lambda e, p_=p_: e.copy(out=KTs[:, :, p_ * 128:(p_ + 1) * 128], in_=PT[:, 0:384].rearrange("p (a t) -> p a t", a=3)),
                        r=[b_PT], w=[b_KTs])
                    yield
                fw.dma("sp", wc[:], wincache_d[s_, :, :].rearrange("(t p) c -> p t c", p=128), writes=[b_wc])
                pool(lambda e: e.tensor_copy(out=wcb[:], in_=wc[:, :, 0:128]), r=[b_wc], w=[b_wcb])
                pool(lambda e: e.tensor_copy(out=Vws[:, :, :, 0:64], in_=wc[:, :, 128:256].rearrange("p t (g d) -> p t g d", g=2)),
                     r=[b_wc], w=[b_Vws])
                for t_ in range(4):
                    tr(PT[:, t_ * 128:(t_ + 1) * 128], wcb[:, t_, :], ident_b[:], r=[b_wcb, b_identb], w=[b_PT])
                act(lambda e: e.copy(out=KwTs[:].rearrange("p a b -> p (a b)"), in_=PT[:, 0:512]), r=[b_PT], w=[b_KwTs])
                yield

            def gen_attn(s_):
                KTs, b_KTs = KTs2[s_ % 2]
                Vs, b_Vs = Vs2[s_ % 2]
                Vws, b_Vws = Vws2[s_ % 2]
                KwTs, b_KwTs = KwTs2[s_ % 2]
                for g_ in range(2):
                    sg = s_ * 2 + g_
                    Qg = QKT[0:64, 4 * g_:4 * g_ + 4, s_]
                    g0_, g1_ = g_ * 64, (g_ + 1) * 64
                    Qgg = QKT[g0_:g1_, 4 * g_:4 * g_ + 4, s_]
                    for kv in range(2):
                        for l in range(32):
                            mm(PA[0:64, 0:511], lhsT=cmpw64[g0_:g1_, kv, l, :], rhs=KTs[g0_:g1_, kv, l:l + 16 * 510 + 1:16],
                               start=(l == 0), stop=(l == 31), r=[b_cmpw64, b_KTs], w=[b_PA])
                        dst = ckTs if kv == 0 else cvTs
                        bd = b_ckTs if kv == 0 else b_cvTs
                        act(lambda e, dst=dst, kv=kv: e.activation(out=dst[:, 0:511], in_=PA[0:64, 0:511], func=AF.Identity, bias=cb_s[:, kv:kv + 1]),
                            r=[b_PA, b_cbs], w=[bd])
                    yield
                    for jt in range(4):
                        tr(PT[:, jt * 64:(jt + 1) * 64], cvTs[:, jt * 128:(jt + 1) * 128], ident_b[0:64, 0:64], r=[b_cvTs, b_identb], w=[b_PT])
                    act(lambda e: e.copy(out=cvxs[:, :, 0:64], in_=PT[:, 0:256].rearrange("p (a d) -> p a d", a=4)), r=[b_PT], w=[b_cvxs])
                    yield
                    for jt in range(4):
                        mm(PD[:, jt * 4:(jt + 1) * 4], lhsT=ckTs[:, jt * 128:(jt + 1) * 128], rhs=Qg, r=[b_ckTs, b_QKT], w=[b_PD])
                    act(lambda e: e.activation(out=Pc[:], in_=PD[:, 0:16], func=AF.Exp, scale=0.125), r=[b_PD], w=[b_Pc])
                    for jt in range(4):
                        mm(PB[0:4, 0:193], lhsT=Pc[:, jt * 4:(jt + 1) * 4], rhs=cvxs[:, jt, :], start=(jt == 0), stop=(jt == 3),
                           r=[b_Pc, b_cvxs], w=[b_PB])
                    act(lambda e: e.copy(out=accs[:], in_=PB[0:4, 0:193]), r=[b_PB], w=[b_accs])
                    dve(lambda e: e.tensor_scalar(out=rcs[:, 0:1], in0=accs[:, 192:193], scalar1=1e-30, scalar2=None, op0=ALU.max), r=[b_accs], w=[b_rcs])
                    dve(lambda e: e.reciprocal(out=rcs[:, 0:1], in_=rcs[:, 0:1]), w=[b_rcs])
                    dve(lambda e, sg=sg: e.tensor_scalar(out=Abr[:, 0, sg, :], in0=accs[:, 0:64], scalar1=rcs[:, 0:1], scalar2=None, op0=ALU.mult),
                        r=[b_accs, b_rcs], w=[b_Abr])
                    dve(lambda e: e.tensor_scalar(out=impn[:], in0=accs[:, 64:192], scalar1=rcs[:, 0:1], scalar2=None, op0=ALU.mult),
                        r=[b_accs, b_rcs], w=[b_impn])
                    mm(PD[0:1, 64:192], lhsT=ones_f2[0:4, 0:1], rhs=impn[:], r=[b_onesf2, b_impn], w=[b_PD])
                    yield
                    dve(lambda e: e.tensor_tensor(out=scs[0:1, 0:128], in0=PD[0:1, 64:192], in1=bonus[:], op=ALU.add), r=[b_PD, b_bonus], w=[b_scs])
                    dve(lambda e: e.max(out=mx8s[:, 0:8], in_=scs[0:1, 0:129]), r=[b_scs], w=[b_mx8s])
                    dve(lambda e: e.match_replace(out=sc2s[0:1, 0:129], in_to_replace=mx8s[:, 0:8], in_values=scs[0:1, 0:129], imm_value=-3e38),
                        r=[b_scs, b_mx8s], w=[b_sc2s])
                    dve(lambda e: e.max(out=mx8s[:, 8:16], in_=sc2s[0:1, 0:129]), r=[b_sc2s], w=[b_mx8s])
                    dve(lambda e: e.tensor_reduce(out=thrs[:], in_=mx8s[:, 8:16], axis=AX.X, op=ALU.min), r=[b_mx8s], w=[b_thrs])
                    dve(lambda e: e.tensor_scalar(out=mbrow[:], in0=scs[0:1, 0:128], scalar1=thrs[0:1, 0:1], scalar2=None, op0=ALU.is_ge),
                        r=[b_scs, b_thrs], w=[b_mbrow])
                    dve(lambda e: e.tensor_scalar(out=mbrow[:], in0=mbrow[:], scalar1=-NEGB, scalar2=NEGB, op0=ALU.mult, op1=ALU.add), w=[b_mbrow])
                    mm(PD[0:2, 200:264], lhsT=oh2[0:1, 0:2], rhs=mbrow[0:1, 0:128:2], start=True, stop=False, r=[b_oh2, b_mbrow], w=[b_PD])
                    mm(PD[0:2, 200:264], lhsT=oh2[0:1, 2:4], rhs=mbrow[0:1, 1:128:2], start=False, stop=True, r=[b_oh2, b_mbrow], w=[b_PD])
                    act(lambda e: e.copy(out=MBp1[:], in_=PD[0:2, 200:264]), r=[b_PD], w=[b_MBp1])
                    dve(lambda e: e.tensor_copy(out=MBp[:], in_=bcast(MBp1[:], [2, 64, 4], 2)), r=[b_MBp1], w=[b_MBp])
                    yield
                    for t_ in range(64):
                        a_ = 0 if t_ < 32 else 1
                        mm(PA[:, 512 + t_ * 4:512 + (t_ + 1) * 4], lhsT=KTs[g0_:g1_, 2, t_ * 128:(t_ + 1) * 128], rhs=Qgg, start=True, stop=False,
                           r=[b_KTs, b_QKT], w=[b_PA])
                        mm(PA[:, 512 + t_ * 4:512 + (t_ + 1) * 4], lhsT=efix[0:2, :], rhs=MBp[0:2, t_, :], start=False, stop=True,
                           r=[b_efix, b_MBp], w=[b_PA])
                    act(lambda e: e.activation(out=Psel[:], in_=PA[:, 512:768], func=AF.Exp, scale=0.125), r=[b_PA], w=[b_Psel])
                    for t_ in range(64):
                        mm(PB[0:4, 256:321], lhsT=Psel[:, t_ * 4:(t_ + 1) * 4], rhs=Vs[:, t_, g_, :], start=(t_ == 0), stop=(t_ == 63),
                           r=[b_Psel, b_Vs], w=[b_PB])
                    yield
                    mm(PD[0:4, 210:211], lhsT=Qg, rhs=QKT[0:64, 10 + g_, s_:s_ + 1], r=[b_QKT], w=[b_PD])
                    mm(PD[0:4, 211:212], lhsT=Qg, rhs=QKT[0:64, 12 + g_, s_:s_ + 1], r=[b_QKT], w=[b_PD])
                    act(lambda e: e.activation(out=pnew[:], in_=PD[0:4, 210:212], func=AF.Exp, scale=0.125), r=[b_PD], w=[b_pnew])
                    mm(PD[0:4, 220:284], lhsT=oh4[0:4, s_ * 4:(s_ + 1) * 4], rhs=kvv[:, 1, 1, g_, :], r=[b_oh4, b_pjs], w=[b_PD])
                    mm(PD[0:4, 284:348], lhsT=oh4[0:4, s_ * 4:(s_ + 1) * 4], rhs=kvv[:, 2, 1, g_, :], r=[b_oh4, b_pjs], w=[b_PD])
                    act(lambda e: e.copy(out=vrow[:], in_=PD[0:4, 220:348]), r=[b_PD], w=[b_vrow])
                    dve(lambda e: e.scalar_tensor_tensor(out=asel[:, 0, 0:64], in0=vrow[:, 0:64], scalar=pnew[:, 0:1], in1=PB[0:4, 256:320],
                                                         op0=ALU.mult, op1=ALU.add), r=[b_vrow, b_pnew, b_PB], w=[b_asel])
                    dve(lambda e: e.tensor_tensor(out=asel[:, 0, 64:65], in0=PB[0:4, 320:321], in1=pnew[:, 0:1], op=ALU.add),
                        r=[b_PB, b_pnew], w=[b_asel])
                    yield
                    for t_ in range(4):
                        mm(PD[:, 352 + t_ * 4:356 + t_ * 4], lhsT=KwTs[g0_:g1_, t_, :], rhs=Qgg, r=[b_KwTs, b_QKT], w=[b_PD])
                    act(lambda e: e.activation(out=Pw[:], in_=PD[:, 352:368], func=AF.Exp, scale=0.125), r=[b_PD], w=[b_Pw])
                    for t_ in range(4):
                        mm(PB[0:4, 384:449], lhsT=Pw[:, t_ * 4:(t_ + 1) * 4], rhs=Vws[:, t_, g_, :], start=(t_ == 0), stop=(t_ == 3),
                           r=[b_Pw, b_Vws], w=[b_PB])
                    dve(lambda e: e.scalar_tensor_tensor(out=asel[:, 1, 0:64], in0=vrow[:, 64:128], scalar=pnew[:, 1:2], in1=PB[0:4, 384:448],
                                                         op0=ALU.mult, op1=ALU.add), r=[b_vrow, b_pnew, b_PB], w=[b_asel])
                    dve(lambda e: e.tensor_tensor(out=asel[:, 1, 64:65], in0=PB[0:4, 448:449], in1=pnew[:, 1:2], op=ALU.add),
                        r=[b_PB, b_pnew], w=[b_asel])
                    dve(lambda e: e.reciprocal(out=rcs[:, 1:3], in_=asel[:, :, 64]), r=[b_asel], w=[b_rcs])
                    for br in range(2):
                        dve(lambda e, br=br, sg=sg: e.tensor_scalar(out=Abr[:, 1 + br, sg, :], in0=asel[:, br, 0:64], scalar1=rcs[:, 1 + br:2 + br],
                                                                    scalar2=None, op0=ALU.mult), r=[b_asel, b_rcs], w=[b_Abr])

                yield

            def drain_gen2(g_):
                for _ in g_:
                    pass

            def interleave2(gens):
                alive = [g_ for g_ in gens if g_ is not None]
                while alive:
                    for g_ in list(alive):
                        try:
                            next(g_)
                        except StopIteration:
                            alive.remove(g_)
            drain_gen2(gen_pages(0))
            for s_ in range(4):
                interleave2([gen_attn(s_), gen_pages(s_ + 1) if s_ < 3 else None])
            fw.barrier()
            stSk.close()
            cur[0] = stSn
            woutn, b_woutn = T([64, 8, 1024], BF16)
            woutg, b_woutg = T([128, 4, 1024], BF16)
            fw.dma("pool", woutn[:].rearrange("p a b -> p (a b)"), woutn_d[:, :], writes=[b_woutn])
            fw.dma("pool", woutg[:].rearrange("p a b -> p (a b)"), woutg_d[:, :], writes=[b_woutg])
            chk2('s11')
            gts, b_gts = T([4, 8, 3])
            osum, b_osum = T([4, 8, 64])
            otmp, b_otmp = T([4, 8, 64])
            onb, b_onb = T([4, 8, 64], BF16)
            OT, b_OT = T([64, 8, 4], BF16)
            act(lambda e: e.activation(out=gts[:].rearrange("p a b -> p (a b)"), in_=pjs[:, 1280:1304], func=AF.Sigmoid), r=[b_pjs], w=[b_gts])
            for br in range(3):
                for g_ in range(2):
                    for r_ in range(4):
                        h_ = 4 * g_ + r_
                        for s_ in range(4):
                            mm(PC[0:4, h_ * 64:(h_ + 1) * 64], lhsT=oh4[0:4, 16 + (r_ * 4 + s_) * 4:16 + (r_ * 4 + s_ + 1) * 4],
                               rhs=Abr[:, br, s_ * 2 + g_, :], start=(s_ == 0), stop=(s_ == 3), r=[b_oh4, b_Abr], w=[b_PC])
                gb = bcast(gts[:, :, br], [4, 8, 64], 2)
                if br == 0:
                    dve(lambda e, gb=gb: e.tensor_tensor(out=osum[:], in0=PC[0:4, 0:512].rearrange("p (h d) -> p h d", h=8), in1=gb, op=ALU.mult),
                        r=[b_PC, b_gts], w=[b_osum])
                else:
                    dve(lambda e, gb=gb: e.tensor_tensor(out=otmp[:], in0=PC[0:4, 0:512].rearrange("p (h d) -> p h d", h=8), in1=gb, op=ALU.mult),
                        r=[b_PC, b_gts], w=[b_otmp])
                    dve(lambda e: e.tensor_tensor(out=osum[:], in0=osum[:], in1=otmp[:], op=ALU.add), r=[b_otmp], w=[b_osum])
            act(lambda e: e.copy(out=onb[:], in_=osum[:]), r=[b_osum], w=[b_onb])
            for h_ in range(8):
                tr(PT[0:64, h_ * 4:(h_ + 1) * 4], onb[0:4, h_, :], ident_b[0:4, 0:4], r=[b_onb, b_identb], w=[b_PT])
            act(lambda e: e.copy(out=OT[:].rearrange("p a b -> p (a b)"), in_=PT[0:64, 0:32]), r=[b_PT], w=[b_OT])
            chk2('s12')
            for half in range(2):
                for h_ in range(8):
                    mm(PA[0:4, half * 512:(half + 1) * 512], lhsT=OT[:, h_, :], rhs=woutn[:, h_, half * 512:(half + 1) * 512],
                       start=(h_ == 0), stop=False, r=[b_OT, b_woutn], w=[b_PA])
                for h_ in range(4):
                    mm(PA[0:4, half * 512:(half + 1) * 512], lhsT=OGT[:, h_, :], rhs=woutg[:, h_, half * 512:(half + 1) * 512],
                       start=False, stop=(h_ == 3), r=[b_OGT, b_woutg], w=[b_PA])
            dve(lambda e: e.tensor_tensor(out=hs_res[:], in0=PA[0:4, :], in1=xs_t[:], op=ALU.add), r=[b_PA, b_xs], w=[b_hsres])
            fw.barrier()
            stSn.close()
            stS.close()
            stB = st.enter_context(ExitStack())
            cur[0] = stB
            wout = sb("wout", [128, 8, 1024], BF16); b_wout = Buf()
            wdn = sb("wdn", [128, 22, 1024], BF16); b_wdn = Buf()
            nfb = sb("nfb", [128, 1024]); b_nfb = Buf()
            gffn = sb("gffn", [128, 8]); b_gffn = Buf()
            sel4 = sb("sel4_sb", [128, 4]); b_sel4 = Buf()
            cand = [sb("cand%d" % i, [128, 8, 512], BF16) for i in range(2)]; b_cand = [Buf() for _ in range(2)]
            mixT = sb("mixT", [128, 8, 512], BF16); b_mixT = Buf()
            xt2 = [sb("xt2_%d" % i, [128, 1024]) for i in range(2)]; b_xt2 = [Buf() for _ in range(2)]
            hres = sb("hres", [128, 4, 1024]); b_hres = [Buf() for _ in range(4)]
            hsq = sb("hsq", [128, 1024], BF16); b_hsq = Buf()
            hss = sb("hss", [128, 1]); b_hss = Buf()
            hs = sb("hs", [128, 1024], BF16); b_hs = Buf()
            hnT = sb("hnT", [128, 8, 512], BF16); b_hnT = Buf()
            wg = [sb("wg%d" % i, [128, 8, 256], BF16) for i in range(3)]; b_wg = [Buf() for _ in range(3)]
            sg = [sb("sg%d" % i, [128, 512]) for i in range(2)]; b_sg = [Buf() for _ in range(2)]
            actT = sb("actT", [128, 22, 512], BF16); b_actT = Buf()
            yb = [sb("yb%d" % i, [128, 1024]) for i in range(2)]; b_yb = [Buf() for _ in range(2)]
            ysq = sb("ysq", [128, 1024], BF16); b_ysq = Buf()
            yss = sb("yss", [128, 1]); b_yss = Buf()
            hsn_ss = sb("hsn_ss", [4, 1]); b_hsnss = Buf()
            hsn_sq = sb("hsn_sq", [4, 1024], BF16); b_hsnsq = Buf()
            hsn = sb("hsn", [4, 1024], BF16); b_hsn = Buf()
            hnTs = sb("hnTs", [128, 8, 4], BF16); b_hnTs = Buf()
            sgs = sb("sgs", [128, 4]); b_sgs = Buf()
            actTs = sb("actTs", [128, 22, 4], BF16); b_actTs = Buf()
            ysb = sb("ysb", [4, 1024]); b_ysb = Buf()
            fw.dma("sp", nfb[:], nfin_d[:, :], writes=[b_nfb])
            fw.dma("sp", gffn[:], gffn_d[:, :], writes=[b_gffn])
            fw.dma("sp", sel4[:], sel4_d[:, :], writes=[b_sel4])
            fw.dma("pool", wout[:], wout_d[:, :].rearrange("(k p) c -> p k c", p=128), writes=[b_wout])
            fw.dma("sp", wdn[:], wdn_s[:, :, :].rearrange("f p c -> p f c"), reads=[b_wdns], writes=[b_wdn])
            act(lambda e: e.activation(out=hsn_sq[:], in_=hs_res[:], func=AF.Square, accum_out=hsn_ss[:]), r=[b_hsres], w=[b_hsnsq, b_hsnss])
            act(lambda e: e.activation(out=hsn_ss[:], in_=hsn_ss[:], func=AF.Sqrt, scale=1.0 / 1024, bias=EPS), w=[b_hsnss])
            dve(lambda e: e.reciprocal(out=hsn_ss[:], in_=hsn_ss[:]), w=[b_hsnss])
            dve(lambda e: e.tensor_scalar(out=hsn[:], in0=hs_res[:], scalar1=hsn_ss[:, 0:1], scalar2=None, op0=ALU.mult), r=[b_hsres, b_hsnss], w=[b_hsn])
            for kt in range(8):
                tr(PT[:, kt * 4:(kt + 1) * 4], hsn[0:4, kt * 128:(kt + 1) * 128], ident_b[0:4, 0:4], r=[b_hsn, b_identb], w=[b_PT])
            for kt in range(8):
                act(lambda e, kt=kt: e.activation(out=hnTs[:, kt, :], in_=PT[:, kt * 4:(kt + 1) * 4], func=AF.Copy, scale=gffn[:, kt:kt + 1]),
                    r=[b_PT, b_gffn], w=[b_hnTs])
            wgi = 0
            for bi in range(NB):
                for j4 in range(4):
                    cb = j4 % 2
                    c0 = j4 * TB + bi * 512
                    fw.dma("sp", cand[cb][:], xout[c0 // CH][:, c0 % CH:c0 % CH + 512].rearrange("(k p) t -> p k t", p=128),
                           reads=[b_xout], writes=[b_cand[cb]])
                    if j4 == 0:
                        dve(lambda e, cb=cb: e.tensor_scalar(out=mixT[:], in0=cand[cb][:], scalar1=sel4[:, 0:1], scalar2=None, op0=ALU.mult),
                            r=[b_cand[cb], b_sel4], w=[b_mixT])
                    else:
                        dve(lambda e, cb=cb, j4=j4: e.scalar_tensor_tensor(out=mixT[:], in0=cand[cb][:], scalar=sel4[:, j4:j4 + 1], in1=mixT[:],
                                                                           op0=ALU.mult, op1=ALU.add), r=[b_cand[cb], b_sel4], w=[b_mixT])
                for tt in range(4):
                    r0 = bi * 512 + tt * 128
                    xs_ = tt % 2
                    fw.dma("sp", xt2[xs_][:], xown_d[r0:r0 + 128, :], writes=[b_xt2[xs_]])
                    for half in range(2):
                        for kt in range(8):
                            mm(PA[:, half * 512:(half + 1) * 512], lhsT=mixT[:, kt, tt * 128:(tt + 1) * 128],
                               rhs=wout[:, kt, half * 512:(half + 1) * 512], start=(kt == 0), stop=(kt == 7),
                               r=[b_mixT, b_wout], w=[b_PA])
                    dve(lambda e, tt=tt, xs_=xs_: e.tensor_tensor(out=hres[:, tt, :], in0=PA[:, :], in1=xt2[xs_][:], op=ALU.add),
                        r=[b_PA, b_xt2[xs_]], w=[b_hres[tt]])
                    act(lambda e, tt=tt: e.activation(out=hsq[:], in_=hres[:, tt, :], func=AF.Square, accum_out=hss[:]),
                        r=[b_hres[tt]], w=[b_hsq, b_hss])
                    act(lambda e: e.activation(out=hss[:], in_=hss[:], func=AF.Sqrt, scale=1.0 / 1024, bias=EPS), w=[b_hss])
                    dve(lambda e: e.reciprocal(out=hss[:], in_=hss[:]), w=[b_hss])
                    dve(lambda e, tt=tt: e.tensor_scalar(out=hs[:], in0=hres[:, tt, :], scalar1=hss[:, 0:1], scalar2=None, op0=ALU.mult),
                        r=[b_hres[tt], b_hss], w=[b_hs])
                    for kt in range(8):
                        tr(PT[:, kt * 128:(kt + 1) * 128], hs[:, kt * 128:(kt + 1) * 128], ident_b[:], r=[b_hs, b_identb], w=[b_PT])
                    for kt in range(8):
                        act(lambda e, kt=kt, tt=tt: e.activation(out=hnT[:, kt, tt * 128:(tt + 1) * 128], in_=PT[:, kt * 128:(kt + 1) * 128],
                                                                 func=AF.Copy, scale=gffn[:, kt:kt + 1]), r=[b_PT, b_gffn], w=[b_hnT])
                for f in range(22):
                    wb = wgi % 3
                    wgi += 1
                    fw.dma("sp", wg[wb][:].rearrange("p k c -> p (k c)"), wgu_s[f], reads=[b_wgus[f]], writes=[b_wg[wb]])
                    for kt in range(8):
                        mm(PA[:, 0:512], lhsT=wg[wb][:, kt, 0:128], rhs=hnT[:, kt, :], start=(kt == 0), stop=(kt == 7),
                           r=[b_wg[wb], b_hnT], w=[b_PA])
                    for kt in range(8):
                        mm(PB[:, 0:512], lhsT=wg[wb][:, kt, 128:256], rhs=hnT[:, kt, :], start=(kt == 0), stop=(kt == 7),
                           r=[b_wg[wb], b_hnT], w=[b_PB])
                    sb_ = f % 2
                    act(lambda e, sb_=sb_: e.activation(out=sg[sb_][:], in_=PA[:, 0:512], func=AF.Silu), r=[b_PA], w=[b_sg[sb_]])
                    dve(lambda e, sb_=sb_, f=f: e.tensor_tensor(out=actT[:, f, :], in0=PB[:, 0:512], in1=sg[sb_][:], op=ALU.mult),
                        r=[b_PB, b_sg[sb_]], w=[b_actT])
                    if bi == 0:
                        for kt in range(8):
                            mm(PD[:, 0:4], lhsT=wg[wb][:, kt, 0:128], rhs=hnTs[:, kt, :], start=(kt == 0), stop=(kt == 7),
                               r=[b_wg[wb], b_hnTs], w=[b_PD])
                        for kt in range(8):
                            mm(PD[:, 4:8], lhsT=wg[wb][:, kt, 128:256], rhs=hnTs[:, kt, :], start=(kt == 0), stop=(kt == 7),
                               r=[b_wg[wb], b_hnTs], w=[b_PD])
                        act(lambda e: e.activation(out=sgs[:], in_=PD[:, 0:4], func=AF.Silu), r=[b_PD], w=[b_sgs])
                        dve(lambda e, f=f: e.tensor_tensor(out=actTs[:, f, :], in0=PD[:, 4:8], in1=sgs[:], op=ALU.mult),
                            r=[b_PD, b_sgs], w=[b_actTs])
                for tt in range(4):
                    r0 = bi * 512 + tt * 128
                    for half in range(2):
                        for f in range(22):
                            mm(PC[:, half * 512:(half + 1) * 512], lhsT=actT[:, f, tt * 128:(tt + 1) * 128],
                               rhs=wdn[:, f, half * 512:(half + 1) * 512], start=(f == 0), stop=(f == 21),
                               r=[b_actT, b_wdn], w=[b_PC])
                    ys_ = tt % 2
                    dve(lambda e, tt=tt, ys_=ys_: e.tensor_tensor(out=yb[ys_][:], in0=PC[:, :], in1=hres[:, tt, :], op=ALU.add),
                        r=[b_PC, b_hres[tt]], w=[b_yb[ys_]])
                    act(lambda e, ys_=ys_: e.activation(out=ysq[:], in_=yb[ys_][:], func=AF.Square, accum_out=yss[:]),
                        r=[b_yb[ys_]], w=[b_ysq, b_yss])
                    act(lambda e: e.activation(out=yss[:], in_=yss[:], func=AF.Sqrt, scale=1.0 / 1024, bias=EPS), w=[b_yss])
                    dve(lambda e: e.reciprocal(out=yss[:], in_=yss[:]), w=[b_yss])
                    dve(lambda e, ys_=ys_: e.scalar_tensor_tensor(out=yb[ys_][:], in0=yb[ys_][:], scalar=yss[:, 0:1], in1=nfb[:],
                                                                  op0=ALU.mult, op1=ALU.mult), r=[b_yss, b_nfb], w=[b_yb[ys_]])
                    fw.dma("sp", y_o[r0:r0 + 128, :], yb[ys_][:], reads=[b_yb[ys_]])
            for half in range(2):
                for f in range(22):
                    mm(PC[0:4, half * 512:(half + 1) * 512], lhsT=actTs[:, f, :], rhs=wdn[:, f, half * 512:(half + 1) * 512],
                       start=(f == 0), stop=(f == 21), r=[b_actTs, b_wdn], w=[b_PC])
            dve(lambda e: e.tensor_tensor(out=ysb[:], in0=PC[0:4, :], in1=hs_res[:], op=ALU.add), r=[b_PC, b_hsres], w=[b_ysb])
            act(lambda e: e.activation(out=hsn_sq[:], in_=ysb[:], func=AF.Square, accum_out=hsn_ss[:]), r=[b_ysb], w=[b_hsnsq, b_hsnss])
            act(lambda e: e.activation(out=hsn_ss[:], in_=hsn_ss[:], func=AF.Sqrt, scale=1.0 / 1024, bias=EPS), w=[b_hsnss])
            dve(lambda e: e.reciprocal(out=hsn_ss[:], in_=hsn_ss[:]), w=[b_hsnss])
            dve(lambda e: e.scalar_tensor_tensor(out=ysb[:], in0=ysb[:], scalar=hsn_ss[:, 0:1], in1=nfb[0:4, :], op0=ALU.mult, op1=ALU.mult),
                r=[b_hsnss, b_nfb], w=[b_ysb])
            fw.dma("sp", ys_o[:, :], ysb[:], reads=[b_ysb])
        fw.drain()
    return nc


def _consts(NT):
    TT = NT * 128
    c = {}
    c["c_ident"] = np.eye(128, dtype=np.float32)
    half = 32
    inv = np.power(np.float32(10000.0), -np.arange(half, dtype=np.float32) * np.float32(2.0) / np.float32(64)).astype(np.float32)
    pos = (np.arange(NT)[None, :] * 128 + np.arange(128)[:, None]).astype(np.float32)
    ang = (pos[:, :, None] * inv[None, None, :]).astype(np.float32)
    c["c_cos"] = np.cos(ang).astype(np.float32).reshape(128, NT * 32)
    c["c_sin"] = np.sin(ang).astype(np.float32).reshape(128, NT * 32)
    k = np.arange(128)[:, None]
    q = np.arange(128)[None, :]
    tri = np.stack([(k <= q), (k >= q)], axis=1).astype(np.float32)
    c["c_tri"] = tri.reshape(128, 256)
    cm = np.zeros((128, 17, 128), np.float32)
    for m in range(17):
        cm[:, m, :] = (16 * k - q <= 128 * m - 31)
    c["c_cmpmask"] = cm.reshape(128, 17 * 128)
    qq = np.arange(128)[:, None]
    r = np.arange(256)[None, :] - 128
    hi = (qq >= 64).astype(np.int64)
    prel = np.zeros((128, 256), np.float32)
    prel[(r == hi) | (r == hi - 1)] = 1e4
    prel[r > hi] = -1e30
    c["c_prel"] = prel
    kk = np.arange(TT)[None, :]
    e = np.arange(64)[:, None]
    c["c_eind"] = (e == (kk // 64) % 64).astype(np.float32)
    n = np.arange(512)[:, None]
    s_ = np.arange(128)[None, :]
    c2s = ((n * 16 < s_ * 64 + 64) & (n * 16 + 32 > s_ * 64) & (n < 511)).astype(np.float32)
    c["c_c2s"] = c2s.reshape(4, 128, 128).transpose(1, 0, 2).reshape(128, 512)
    j = np.arange(128)[:, None]
    i = np.arange(128)[None, :]
    same = (j // 64) == (i // 64)
    gm = np.zeros((128, 5, 128), np.float32)
    gm[:, 0, :] = np.where(same & (i >= j), 0.0, NEGB)
    gm[:, 1, :] = np.where(same & (i > j), 0.0, NEGB)
    gm[:, 2, :] = (same & (j <= i))
    gm[:, 3, :] = (j < 64) * np.ones((1, 128))
    gm[:, 4, :] = (j >= 64) * np.ones((1, 128))
    c["c_gmask"] = gm.reshape(128, 5 * 128)
    angs = (np.float32(8192.0) * inv).astype(np.float32)
    c["c_rope_s"] = np.tile(np.concatenate([np.cos(angs), np.sin(angs)]).astype(np.float32)[None, :], (4, 1))
    nn_ = np.arange(512).reshape(4, 128).T
    c["c_ones511"] = (nn_ < 511).astype(np.float32)
    oh = np.zeros((4, 80), np.float32)
    for s_ in range(4):
        oh[s_, s_ * 4:(s_ + 1) * 4] = 1.0
    for r_ in range(4):
        for s_ in range(4):
            oh[r_, 16 + (r_ * 4 + s_) * 4 + s_] = 1.0
    c["c_oh4"] = oh
    bon = np.zeros((1, 128), np.float32)
    bon[0, 0] = 1e4
    bon[0, 127] = 1e4
    c["c_bonus_s"] = bon
    ef = np.zeros((2, 128), np.float32)
    ef[0, 0:64] = 1.0
    ef[1, 64:128] = 1.0
    c["c_efix"] = ef
    c["c_oh2"] = np.array([[1.0, 0.0, 0.0, 1.0]], np.float32)
    return c


def _core_weights(inp, g, hp):
    jh = 2 * g + hp
    w_in = inp["w_in"][0]
    own = [4 * g + 2 * hp, 4 * g + 2 * hp + 1]
    oth = [4 * g + 2 * (1 - hp), 4 * g + 2 * (1 - hp) + 1]
    heads = own + oth
    cols = []
    for h in heads:
        cols += list(range(h * 64, (h + 1) * 64))

    def kvcol(branch, kv):
        base = 512 + ((branch * 2 + kv) * 2 + g) * 64
        return list(range(base, base + 64))
    for branch in range(3):
        cols += kvcol(branch, 0)
    for branch in range(3):
        cols += kvcol(branch, 1)
    for h in heads:
        cols += [1280 + h * 3 + t for t in range(3)]
    cols += list(range(2840 + jh * 128, 2840 + (jh + 1) * 128))
    cols += [3352 + jh, 3356 + jh]
    w_tok = np.ascontiguousarray(w_in[:, cols])
    gcols = []
    for part in range(3):
        gcols += list(range(1304 + part * 512 + jh * 128, 1304 + part * 512 + (jh + 1) * 128))
    w_gdn = np.ascontiguousarray(w_in[:, gcols])
    d = {"w_tok": w_tok, "w_gdn": w_gdn}
    d["g_mix"] = np.ascontiguousarray(inp["norm_mix"][0].reshape(8, 128).T)
    cwf = inp["gdn_conv_w"][0]
    gch = [jh * 128 + part * 512 + np.arange(128) for part in range(3)]
    cw = np.stack([cwf[:, ch].T for ch in gch], axis=1)
    d["conv_w"] = np.ascontiguousarray(cw.reshape(128, 12))
    d["head_sc"] = np.ascontiguousarray(np.stack([np.full(128, inp["gdn_a_log"][0, jh]),
                                                  np.full(128, inp["gdn_dt_bias"][0, jh])], axis=1).astype(np.float32))
    d["gdn_norm_b"] = np.ascontiguousarray(np.tile(inp["gdn_norm"][0][None, :], (128, 1)))
    cwt = inp["nsa_cmp_w"][0]
    d["cmp_w"] = np.ascontiguousarray(cwt.reshape(2, 16, 2, 64, 64).transpose(2, 3, 0, 1, 4).reshape(128, 2 * 16 * 64))
    pe = inp["nsa_cmp_pe"][0]
    d["cmp_pe"] = np.ascontiguousarray(pe.reshape(2, 16, 2, 64).transpose(2, 3, 0, 1).reshape(128, 32))
    return d


def _core_inputs(inp, c, NT, consts):
    b, j = c // 4, c % 4
    g, hp = j // 2, j % 2
    TT = NT * 128
    TB = TT // 4
    d = dict(consts)
    d.update(_core_weights(inp, g, hp))
    d["x"] = np.ascontiguousarray(inp["x_prompt"][b, :TT])
    d["x_own"] = np.ascontiguousarray(inp["x_prompt"][b, j * TB:(j + 1) * TB])
    sel = np.zeros((128, 4), np.float32)
    sel[:, j] = 1.0
    d["sel4"] = sel
    perm = []
    for jj in range(4):
        perm += list(range(128 * jj, 128 * jj + 128)) + list(range(512 + 128 * jj, 512 + 128 * jj + 128))
    d["w_out_p"] = np.ascontiguousarray(inp["w_out"][0][perm, :])
    d["g_ffn"] = np.ascontiguousarray(inp["norm_ffn"][0].reshape(8, 128).T)
    d["w_gu"] = np.ascontiguousarray(inp["w_gate_up"][0])
    d["w_dn"] = np.ascontiguousarray(inp["w_down"][0])
    d["nfin_b"] = np.ascontiguousarray(np.tile(inp["norm_final"][None, :], (128, 1)))
    s0 = 4 * c
    d["xs"] = np.ascontiguousarray(inp["x_sample"][s0:s0 + 4, 0, :])
    d["w_in_full"] = np.ascontiguousarray(inp["w_in"][0])
    d["cache_kv"] = inp["cache_nsa_kv"][0].reshape(2560 * 128, 512)
    d["ptab_b"] = np.ascontiguousarray(np.tile(inp["page_table"][s0:s0 + 4].reshape(1, 256), (128, 1)).astype(np.int32))
    d["c_iota"] = np.arange(128, dtype=np.float32).reshape(128, 1)
    d["win_cache"] = np.ascontiguousarray(inp["cache_nsa_win"][0, s0:s0 + 4].reshape(4, 512, 256))
    d["gdn_S"] = np.ascontiguousarray(inp["state_gdn_S"][0, s0:s0 + 4].reshape(16, 128, 128))
    d["gdn_conv"] = np.ascontiguousarray(inp["state_gdn_conv"][0, s0:s0 + 4])
    d["conv_w_b"] = np.ascontiguousarray(np.tile(inp["gdn_conv_w"][0][None], (4, 1, 1)))
    d["alog_b"] = np.ascontiguousarray(np.tile(np.concatenate([inp["gdn_a_log"][0], inp["gdn_dt_bias"][0]])[None, :], (4, 1)))
    d["gnorm_row"] = np.ascontiguousarray(inp["gdn_norm"][0][None, :])
    cwt = inp["nsa_cmp_w"][0]
    w64 = cwt.transpose(2, 0, 1, 3).reshape(64, 2 * 32 * 64)
    d["cmp_w64"] = np.ascontiguousarray(np.concatenate([w64, w64], axis=0))
    pe = inp["nsa_cmp_pe"][0]
    p64 = pe.transpose(2, 0, 1).reshape(64, 64)
    d["cmp_pe64"] = np.ascontiguousarray(np.concatenate([p64, p64], axis=0))
    wo = inp["w_out"][0]
    d["w_out_n"] = np.ascontiguousarray(wo[:512].reshape(8, 64, 1024).transpose(1, 0, 2).reshape(64, 8192))
    d["w_out_g"] = np.ascontiguousarray(wo[512:].reshape(4, 128, 1024).transpose(1, 0, 2).reshape(128, 4096))
    return d


def _run(inp, NT):
    nc = build_nc(NT, phaseB=True)
    consts = _consts(NT)
    maps = [_core_inputs(inp, c, NT, consts) for c in range(8)]
    res = run_bass_kernel_spmd(nc, maps, core_ids=list(range(8)))
    return res.results


def kernel(**inputs):
    inp = {k: np.asarray(v) for k, v in inputs.items()}
    NT = 64
    TT = NT * 128
    TB = TT // 4
    R = _run(inp, NT)
    y_prompt = np.zeros((2, TT, 1024), np.float32)
    kv_prompt = np.zeros((1, 2, TT, 4, 2, 64), np.float32)
    win_prompt = np.zeros((1, 2, 512, 2, 2, 64), np.float32)
    S_prompt = np.zeros((1, 2, 4, 128, 128), np.float32)
    conv_prompt = np.zeros((1, 2, 3, 1536), np.float32)
    for c in range(8):
        b, j = c // 4, c % 4
        g, hp = j // 2, j % 2
        r = R[c]
        y_prompt[b, j * TB:(j + 1) * TB] = r["y_out"]
        if hp == 0:
            kv_prompt[0, b, :, :, g, :] = r["kv_out"].reshape(TT, 4, 64)
            win_prompt[0, b, :, :, g, :] = r["win_out"].reshape(512, 2, 64)
        S_prompt[0, b, j] = r["S_out"]
        cv = r["conv_out"]
        for part in range(3):
            conv_prompt[0, b, :, part * 512 + j * 128:part * 512 + (j + 1) * 128] = cv[:, part, :].T
    y_sample = np.zeros((32, 1, 1024), np.float32)
    kv_sample = np.zeros((1, 32, 1, 4, 2, 64), np.float32)
    win_sample = np.zeros((1, 32, 512, 2, 2, 64), np.float32)
    S_sample = np.zeros((1, 32, 4, 128, 128), np.float32)
    conv_sample = np.zeros((1, 32, 3, 1536), np.float32)
    for c in range(8):
        r = R[c]
        s0 = 4 * c
        y_sample[s0:s0 + 4, 0] = r["ys_out"]
        kv_sample[0, s0:s0 + 4, 0] = r["kvs_out"].reshape(4, 4, 2, 64)
        win_sample[0, s0:s0 + 4] = r["wins_out"].reshape(4, 512, 2, 2, 64)
        S_sample[0, s0:s0 + 4] = r["Ss_out"].reshape(4, 4, 128, 128)
        conv_sample[0, s0:s0 + 4] = r["convs_out"]
    return (y_prompt, y_sample, kv_prompt, win_prompt, S_prompt, conv_prompt, kv_sample, win_sample, S_sample, conv_sample)
```

```python
import numpy as np
from contextlib import ExitStack
import concourse.bass as bass
import concourse.mybir as mybir
from concourse.bass_utils import run_bass_kernel_spmd

F32 = mybir.dt.float32
BF16 = mybir.dt.bfloat16
I32 = mybir.dt.int32
AF = mybir.ActivationFunctionType
ALU = mybir.AluOpType
AX = mybir.AxisListType

ENGS = ("pe", "act", "dve", "pool", "sp")

D_MODEL = 1024
SEQ = 8192
HEAD_DIM = 64
D_FF = 2816
EPS = 1e-6
NEGB = -30000.0


class Buf:
    __slots__ = ("name", "w", "rs")

    def __init__(self, name=""):
        self.name = name
        self.w = None
        self.rs = []


class FW:
    def __init__(self, nc, stack, ndma_sems=16):
        self.nc = nc
        self.eng = {"pe": nc.tensor, "act": nc.scalar, "dve": nc.vector, "pool": nc.gpsimd, "sp": nc.sync}
        self.sem = {e: stack.enter_context(nc.semaphore("s_" + e)) for e in ENGS}
        self.cnt = {e: 0 for e in ENGS}
        self.waited = {e: {} for e in ENGS}
        self.dsems = {}
        self.dstate = {}
        for q in ("sp", "pool"):
            self.dsems[q] = [stack.enter_context(nc.semaphore("d_%s%d" % (q, i))) for i in range(ndma_sems)]
            self.dstate[q] = {"i": 0, "val": [0] * ndma_sems}
        self.n_inst = 0
        self.dead = False

    def _wait(self, e, ev):
        if ev is None:
            return
        if ev[0] == "c":
            _, src, n = ev
            if src == "pe" and e == "pe":
                return
            key = ("c", src)
            if self.waited[e].get(key, 0) >= n:
                return
            self.eng[e].wait_ge(self.sem[src], n)
            self.waited[e][key] = n
        else:
            _, q, idx, val = ev
            key = ("d", q, idx)
            if self.waited[e].get(key, 0) >= val:
                return
            self.eng[e].wait_ge(self.dsems[q][idx], val)
            self.waited[e][key] = val

    def _deps(self, e, reads, writes):
        for b in reads:
            self._wait(e, b.w)
        for b in writes:
            self._wait(e, b.w)
            for r in b.rs:
                self._wait(e, r)

    def _commit(self, ev, reads, writes):
        for b in reads:
            b.rs.append(ev)
            if len(b.rs) > 96:
                b.rs = b.rs[-96:]
        for b in writes:
            b.w = ev
            b.rs = []

    def op(self, e, fn, reads=(), writes=()):
        if self.dead:
            return None
        self._deps(e, reads, writes)
        ins = fn(self.eng[e])
        self.cnt[e] += 1
        ins.then_inc(self.sem[e], 1)
        ev = ("c", e, self.cnt[e])
        self._commit(ev, reads, writes)
        self.n_inst += 1
        return ev

    def dma(self, q, out, in_, reads=(), writes=(), fn=None):
        if self.dead:
            return None
        st = self.dstate[q]
        idx = st["i"] % len(self.dsems[q])
        st["i"] += 1
        if st["val"][idx] > 0:
            self._wait(q, ("d", q, idx, st["val"][idx]))
        self._deps(q, reads, writes)
        if fn is None:
            ins = self.eng[q].dma_start(out=out, in_=in_)
        else:
            ins = fn(self.eng[q])
        st["val"][idx] += 16
        ins.then_inc(self.dsems[q][idx], 16)
        ev = ("d", q, idx, st["val"][idx])
        self._commit(ev, reads, writes)
        self.n_inst += 1
        return ev

    def barrier(self):
        for e in ENGS:
            for src in ENGS:
                if src != e and self.cnt[src] > 0:
                    self._wait(e, ("c", src, self.cnt[src]))
            for q in ("sp", "pool"):
                stq = self.dstate[q]
                for idx, v in enumerate(stq["val"]):
                    if v:
                        self._wait(e, ("d", q, idx, v))

    def drain(self):
        for q in ("sp", "pool"):
            st = self.dstate[q]
            for idx, v in enumerate(st["val"]):
                if v:
                    self._wait("sp", ("d", q, idx, v))


class _Stop(Exception):
    pass


STOP = None
SKIP_CC = False


def build_nc(NT=64, dbg=False, phaseB=False):
    TT = NT * 128
    NG = NT // 4
    nc = bass.Bass("TRN2", target_bir_lowering=False)

    def din(name, shape, dt=F32):
        return nc.dram_tensor(name, list(shape), dt, kind="ExternalInput").ap()

    def dout(name, shape, dt=F32):
        return nc.dram_tensor(name, list(shape), dt, kind="ExternalOutput").ap()

    x_d = din("x", [TT, 1024])
    wtok_d = din("w_tok", [1024, 782])
    wgdn_d = din("w_gdn", [1024, 384])
    gmix_d = din("g_mix", [128, 8])
    cw_d = din("conv_w", [128, 12])
    hsc_d = din("head_sc", [128, 2])
    gnorm_d = din("gdn_norm_b", [128, 128])
    cmpw_d = din("cmp_w", [128, 2 * 16 * 64])
    cmppe_d = din("cmp_pe", [128, 32])
    ident_d = din("c_ident", [128, 128])
    cos_d = din("c_cos", [128, NT * 32])
    sin_d = din("c_sin", [128, NT * 32])
    tri_d = din("c_tri", [128, 256])
    cmpmask_d = din("c_cmpmask", [128, 17 * 128])
    prel_d = din("c_prel", [128, 256])
    eind_d = din("c_eind", [64, TT])
    c2s_d = din("c_c2s", [128, 4 * 128])
    gmask_d = din("c_gmask", [128, 5 * 128])

    TB = TT // 4
    NB = TB // 512
    if phaseB:
        xown_d = din("x_own", [TB, 1024])
        sel4_d = din("sel4", [128, 4])
        wout_d = din("w_out_p", [1024, 1024])
        gffn_d = din("g_ffn", [128, 8])
        wgu_d = din("w_gu", [1024, 5632])
        wdn_d = din("w_dn", [2816, 1024])
        nfin_d = din("nfin_b", [128, 1024])
        y_o = dout("y_out", [TB, 1024])
    if phaseB:
        xs_d = din("xs", [4, 1024])
        win_full_d = din("w_in_full", [1024, 3360])
        cache_d = din("cache_kv", [2560 * 128, 512])
        ptab_d = din("ptab_b", [128, 256], I32)
        iota_d = din("c_iota", [128, 1])
        wincache_d = din("win_cache", [4, 512, 256])
        gS_d = din("gdn_S", [16, 128, 128])
        gconv_d = din("gdn_conv", [4, 3, 1536])
        convwb_d = din("conv_w_b", [4, 4, 1536])
        alogb_d = din("alog_b", [4, 8])
        gnrow_d = din("gnorm_row", [1, 128])
        cmpw64_d = din("cmp_w64", [128, 2 * 32 * 64])
        woutn_d = din("w_out_n", [64, 8 * 1024])
        woutg_d = din("w_out_g", [128, 4 * 1024])
        ropes_d = din("c_rope_s", [4, 64])
        ones511_d = din("c_ones511", [128, 4])
        pe64_d = din("cmp_pe64", [128, 64])
        efix_d = din("c_efix", [2, 128])
        oh2_d = din("c_oh2", [1, 4])
        oh4_d = din("c_oh4", [4, 80])
        bonus_d = din("c_bonus_s", [1, 128])
        ys_o = dout("ys_out", [4, 1024])
        kvs_o = dout("kvs_out", [4, 512])
        wins_o = dout("wins_out", [4, 512, 256])
        Ss_o = dout("Ss_out", [16, 128, 128])
        convs_o = dout("convs_out", [4, 3, 1536])
    kv_o = dout("kv_out", [TT, 256])
    win_o = dout("win_out", [512, 128])
    S_o = dout("S_out", [128, 128])
    conv_o = dout("conv_out", [128, 3, 3])
    CH = min(2048, TT)
    NCH = TT // CH
    omT_o = dout("omT_out", [256, TT], BF16) if not phaseB else None
    xin = [nc.dram_tensor("xin%d" % k, [256, CH], BF16).ap() for k in range(NCH)] if phaseB else None
    b_xin = [Buf() for _ in range(NCH)]
    dbg_o = dout("dbg_out", [128, 1300]) if dbg else None

    st = ExitStack()
    with st:
        fw = FW(nc, st)

        cur = [st]

        def sb(name, shape, dt=F32):
            return cur[0].enter_context(nc.sbuf_tensor(name, list(shape), dt))

        def ps(name, shape, dt=F32):
            return st.enter_context(nc.psum_tensor(name, list(shape), dt))

        def pe(fn, r=(), w=()):
            return fw.op("pe", fn, r, w)

        def act(fn, r=(), w=()):
            return fw.op("act", fn, r, w)

        def dve(fn, r=(), w=()):
            return fw.op("dve", fn, r, w)

        def pool(fn, r=(), w=()):
            return fw.op("pool", fn, r, w)

        def mm(out, lhsT, rhs, start=True, stop=True, r=(), w=()):
            return pe(lambda e: e.matmul(out, lhsT=lhsT, rhs=rhs, start=start, stop=stop), r, w)

        def tr(out, in_, ident, r=(), w=()):
            return pe(lambda e: e.transpose(out, in_, ident), r, w)

        def bcast(ap, shape, axis):
            return ap.unsqueeze(axis).to_broadcast(list(shape))

        ident_f = sb("ident_f", [128, 128]); b_identf = Buf()
        ident_b = sb("ident_b", [128, 128], BF16); b_identb = Buf()
        ones_f2 = sb("ones_f2", [128, 128]); b_onesf2 = Buf()
        hs_res = sb("hs_res", [4, 1024]); b_hsres = Buf()
        stA = st.enter_context(ExitStack())
        cur[0] = stA
        pool(lambda e: e.memset(ones_f2[:], 1.0), w=[b_onesf2])
        ones_b = sb("ones_b", [128, 128], BF16); b_onesb = Buf()
        ones_f = sb("ones_f", [128, 128]); b_onesf = Buf()
        csT = [sb("csT%d" % i_, [128, 2, 4, 32]) for i_ in range(2)]; b_csT = [Buf() for _ in range(2)]
        tri = sb("tri", [128, 2, 128], BF16); b_tri = Buf()
        cmpmask = sb("cmpmask", [128, 17, 128], BF16); b_cmpmask = Buf()
        prel = sb("prel", [128, 256]); b_prel = Buf()
        gmask = sb("gmask", [128, 5, 128]); b_gmask = Buf()
        wtok = sb("wtok", [128, 8, 782], BF16); b_wtok = Buf()
        wgdn = sb("wgdn", [128, 8, 384], BF16); b_wgdn = Buf()
        gmix = sb("gmix", [128, 8]); b_gmix = Buf()
        cw = sb("cw", [128, 12]); b_cw = Buf()
        hsc = sb("hsc", [128, 2]); b_hsc = Buf()
        negA = sb("negA", [128, 1]); b_negA = Buf()
        gnb = sb("gnb", [128, 128]); b_gnb = Buf()
        cmpw = sb("cmpw", [128, 2, 16, 64], BF16); b_cmpw = Buf()
        cmppe = sb("cmppe", [128, 2, 16], BF16); b_cmppe = Buf()

        fw.dma("sp", ident_f[:], ident_d[:, :], writes=[b_identf])
        fw.dma("pool", ident_b[:], ident_d[:, :], writes=[b_identb])
        fw.dma("pool", tri[:].rearrange("p a b -> p (a b)"), tri_d[:, :], writes=[b_tri])
        fw.dma("pool", cmpmask[:].rearrange("p a b -> p (a b)"), cmpmask_d[:, :], writes=[b_cmpmask])
        fw.dma("sp", prel[:], prel_d[:, :], writes=[b_prel])
        fw.dma("sp", gmask[:].rearrange("p a b -> p (a b)"), gmask_d[:, :], writes=[b_gmask])
        fw.dma("sp", gmix[:], gmix_d[:, :], writes=[b_gmix])
        fw.dma("sp", cw[:], cw_d[:, :], writes=[b_cw])
        fw.dma("sp", hsc[:], hsc_d[:, :], writes=[b_hsc])
        fw.dma("sp", gnb[:], gnorm_d[:, :], writes=[b_gnb])
        fw.dma("pool", cmpw[:].rearrange("p a b c -> p (a b c)"), cmpw_d[:, :], writes=[b_cmpw])
        fw.dma("pool", cmppe[:].rearrange("p a b -> p (a b)"), cmppe_d[:, :], writes=[b_cmppe])
        for kt in range(8):
            fw.dma("pool", wtok[:, kt, :], wtok_d[kt * 128:(kt + 1) * 128, :], writes=[b_wtok])
            fw.dma("pool", wgdn[:, kt, :], wgdn_d[kt * 128:(kt + 1) * 128, :], writes=[b_wgdn])
        pool(lambda e: e.memset(ones_b[:], 1.0), w=[b_onesb])
        pool(lambda e: e.memset(ones_f[:], 1.0), w=[b_onesf])
        for kt in range(8):
            dve(lambda e, kt=kt: e.tensor_scalar(out=wtok[:, kt, :], in0=wtok[:, kt, :], scalar1=gmix[:, kt:kt + 1],
                                                 scalar2=None, op0=ALU.mult), r=[b_gmix], w=[b_wtok])
            dve(lambda e, kt=kt: e.tensor_scalar(out=wgdn[:, kt, :], in0=wgdn[:, kt, :], scalar1=gmix[:, kt:kt + 1],
                                                 scalar2=None, op0=ALU.mult), r=[b_gmix], w=[b_wgdn])
        act(lambda e: e.activation(out=negA[:], in_=hsc[:, 0:1], func=AF.Exp), r=[b_hsc], w=[b_negA])
        dve(lambda e: e.tensor_scalar(out=negA[:], in0=negA[:], scalar1=-1.0, scalar2=None, op0=ALU.mult), w=[b_negA])

        KselT = sb("KselT", [128, TT], BF16); b_ksel = [Buf() for _ in range(NT)]; b_eind = Buf()
        Vsel = sb("Vsel", [128, NT, 65], BF16); b_vsel = [Buf() for _ in range(NT)]
        KwinT = sb("KwinT", [64, 8 * 128], BF16); b_kwin = [Buf() for _ in range(8)]
        Vwin = sb("Vwin", [128, 8, 65], BF16); b_vwin = [Buf() for _ in range(8)]
        Rk = sb("Rk", [128, 2, 160], BF16); b_Rk = Buf()
        ckT = sb("ckT", [64, 512], BF16); b_ckT = Buf()
        cvx = sb("cvx", [128, 4, 193], BF16); b_cvx = Buf()
        c2s_f = sb("c2s_f", [128, 4, 128]); b_c2sf = Buf()
        ckb = sb("ckb", [64, 1]); b_ckb = Buf()
        cvb = sb("cvb", [8, 64]); b_cvb = Buf()
        cvrow = sb("cvrow", [1, 64], BF16); b_cvrow = Buf()

        fw.dma("pool", KselT[64:128, :], eind_d[:, :], writes=[b_eind])
        pool(lambda e: e.memset(Vsel[:, :, 64:65], 1.0), w=b_vsel)
        pool(lambda e: e.memset(Vwin[:, :, 64:65], 1.0), w=b_vwin)
        pool(lambda e: e.memset(Rk[:], 0.0), w=[b_Rk])
        pool(lambda e: e.memset(ckT[:], 0.0), w=[b_ckT])
        pool(lambda e: e.memset(cvx[:], 0.0), w=[b_cvx])
        fw.dma("sp", c2s_f[:].rearrange("p a b -> p (a b)"), c2s_d[:, :], writes=[b_c2sf])
        pool(lambda e: e.tensor_copy(out=cvx[:, :, 64:192], in_=c2s_f[:]), r=[b_c2sf], w=[b_cvx])
        pool(lambda e: e.memset(cvx[:, :, 192:193], 1.0), w=[b_cvx])

        PA = ps("PA", [128, 1024]); b_PA = Buf()
        PB = ps("PB", [128, 1024]); b_PB = Buf()
        PC = ps("PC", [128, 1024]); b_PC = Buf()
        PD = ps("PD", [128, 512]); b_PD = Buf()
        PT = ps("PT", [128, 1024], BF16); b_PT = Buf()

        for lp in range(16):
            mm(PD[0:64, 0:1], lhsT=cmpw[:, 0, lp, :], rhs=cmppe[:, 0, lp:lp + 1], start=(lp == 0), stop=(lp == 15),
               r=[b_cmpw, b_cmppe], w=[b_PD])
        act(lambda e: e.copy(out=ckb[:], in_=PD[0:64, 0:1]), r=[b_PD], w=[b_ckb])
        for lp in range(16):
            mm(PD[0:1, 64:128], lhsT=cmppe[:, 1, lp:lp + 1], rhs=cmpw[:, 1, lp, :], start=(lp == 0), stop=(lp == 15),
               r=[b_cmpw, b_cmppe], w=[b_PD])
        act(lambda e: e.copy(out=cvrow[:], in_=PD[0:1, 64:128]), r=[b_PD], w=[b_cvrow])
        mm(PD[0:8, 128:192], lhsT=ones_b[0:1, 0:8], rhs=cvrow[:], r=[b_onesb, b_cvrow], w=[b_PD])
        act(lambda e: e.copy(out=cvb[:], in_=PD[0:8, 128:192]), r=[b_PD], w=[b_cvb])

        NXB = 2
        xt = [sb("xt%d" % i, [128, 1024]) for i in range(NXB)]; b_xt = [Buf() for _ in range(NXB)]
        ssq = [sb("ssq%d" % i, [128, 1]) for i in range(NXB)]; b_ssq = [Buf() for _ in range(NXB)]
        xs = [sb("xs%d" % i, [128, 1024], BF16) for i in range(NXB)]; b_xs = [Buf() for _ in range(NXB)]
        xnT = sb("xnT", [128, 8, 512], BF16); b_xnT = [Buf() for _ in range(4)]
        pj = [sb("pj%d" % i, [128, 782]) for i in range(2)]; b_pj = [Buf() for _ in range(2)]
        rq = [sb("rq%d" % i, [128, 7, 64]) for i in range(2)]; b_rq = [Buf() for _ in range(2)]
        rt = sb("rt", [128, 4, 7, 32]); b_rt = Buf()
        ko = [sb("ko%d" % i, [128, 6, 64]) for i in range(2)]; b_ko = [Buf() for _ in range(2)]
        qkb = sb("qkb", [128, 7, 64], BF16); b_qkb = Buf()
        kvc2 = sb("kvc2", [128, 2, 128], BF16); b_kvc2 = Buf()
        QT = [sb("QT%d" % i_, [64, 512], BF16) for i_ in range(2)]; b_QT = [Buf() for _ in range(2)]
        Qaug = [sb("Qaug%d" % i_, [128, 2, 256], BF16) for i_ in range(2)]; b_Qaug = [Buf() for _ in range(2)]
        gates = [sb("gates%d" % i_, [128, 12]) for i_ in range(2)]; b_gates = [Buf() for _ in range(2)]
        gz = sb("gz", [128, 140]); b_gz = Buf()
        zsil = [sb("zsil%d" % i_, [128, 4, 128]) for i_ in range(2)]; b_zsil = [[Buf() for _ in range(4)] for _ in range(2)]
        abg = [sb("abg%d" % i_, [128, 4, 2]) for i_ in range(2)]; b_abg = [Buf() for _ in range(2)]
        cvnew = sb("cvnew", [8, 64], BF16); b_cvnew = Buf()
        PTc = [sb("PTc%d" % i, [128, 512], BF16) for i in range(4)]; b_PTc = [Buf() for _ in range(4)]
        PTs = [sb("PTs%d" % i, [128, 1024], BF16) for i in range(2)]; b_PTs = [Buf() for _ in range(2)]
        acc_c = sb("acc_c", [128, 4, 193]); b_accc = Buf()
        acc_sw = sb("acc_sw", [128, 4, 65]); b_accsw = Buf()
        rcp = sb("rcp", [128, 12]); b_rcp = Buf()
        imp = sb("imp", [128, 128]); b_imp = Buf()
        score = sb("score", [128, 128]); b_score = Buf()
        mx8 = sb("mx8", [128, 16]); b_mx8 = Buf()
        thr = sb("thr", [128, 1]); b_thr = Buf()
        sc2 = sb("sc2", [128, 128]); b_sc2 = Buf()
        mbt = sb("mbt", [128, 2, 128]); b_mbt = Buf()
        coef = sb("coef", [128, 6]); b_coef = Buf()
        onsa = sb("onsa", [128, 128]); b_onsa = Buf()
        om = sb("om", [128, 256], BF16); b_om = Buf()
        omT = [sb("omT%d" % i_, [128, 2, 512], BF16) for i_ in range(2)]; b_omT = [Buf() for _ in range(2)]
        raw = [sb("raw%d" % i_, [128, 3, 515]) for i_ in range(2)]; b_raw = [Buf() for _ in range(2)]
        cacc = sb("cacc", [128, 3, 512]); b_cacc = Buf()
        csil = cacc; b_csil = b_cacc
        sqb = sb("sqb", [128, 2, 512], BF16); b_sqb = Buf()
        rnorm = sb("rnorm", [128, 2, 512]); b_rnorm = Buf()
        gT = sb("gT", [128, 3, 512], BF16); b_gT = Buf()
        gtok = sb("gtok", [128, 4, 3, 128], BF16); b_gtok = Buf()
        gsc = sb("gsc", [128, 16, 4]); b_gsc = Buf()
        glc = sb("glc", [128, 2, 4]); b_glc = Buf()
        dg1 = sb("dg1", [128, 4, 128]); b_dg1 = Buf()
        dg2 = sb("dg2", [128, 4, 128]); b_dg2 = Buf()
        dgn = sb("dgn", [128, 4, 128]); b_dgn = Buf()
        gmask4 = sb("gmask4", [128, 2, 4, 128]); b_gmask4 = Buf()
        decT = sb("decT", [128, 4, 128], BF16); b_decT = Buf()
        decbT = sb("decbT", [128, 4, 128], BF16); b_decbT = Buf()
        Um = [sb("Um%d" % i, [128, 4, 128], BF16) for i in range(2)]; b_Um = [Buf() for _ in range(2)]
        Lm = [sb("Lm%d" % i, [128, 4, 128], BF16) for i in range(2)]; b_Lm = [Buf() for _ in range(2)]
        Pm = [sb("Pm%d" % i, [128, 4, 128], BF16) for i in range(2)]; b_Pm = [Buf() for _ in range(2)]
        Xm = sb("Xm", [128, 4, 256], BF16); b_Xm = Buf()
        uw = sb("uw", [128, 4, 256], BF16); b_uw = Buf()
        kgm = sb("kgm", [128, 4, 2, 128], BF16); b_kgm = Buf()
        aqkT = sb("aqkT", [128, 4, 128], BF16); b_aqkT = Buf()
        Dg = sb("Dg", [128, 4, 128], BF16); b_Dg = Buf()
        QpA = sb("QpA", [128, 4, 128], BF16); QpB = sb("QpB", [128, 4, 128], BF16); b_Qp = Buf()
        glc8 = sb("glc8", [128, 8]); b_glc8 = Buf()
        MTf = sb("MTf", [128, 8, 128], BF16); b_MTf = Buf()
        MT8 = sb("MT8", [128, 8, 128], BF16); b_MT8 = Buf()
        Sb9 = sb("Sb9", [128, 9, 128], BF16); b_Sb9 = [Buf() for _ in range(9)]
        Sf = sb("Sf", [128, 128]); b_Sf = Buf()
        og4 = sb("og4", [128, 4, 128]); b_og4 = Buf()
        og4q = dgn; b_og4q = b_dgn
        og4s = sb("og4s", [128, 4]); b_og4s = Buf()
        omg = sb("omg", [128, 4, 128], BF16); b_omg = Buf()

        pool(lambda e: e.memset(kgm[:], 0.0), w=[b_kgm])
        for mk_ in range(2):
            pool(lambda e, mk_=mk_: e.tensor_copy(out=gmask4[:, mk_], in_=bcast(gmask[:, mk_, :], [128, 4, 128], 1)), r=[b_gmask], w=[b_gmask4])
        pool(lambda e: e.memset(raw[0][:], 0.0), w=[b_raw[0]])
        pool(lambda e: e.memset(raw[1][:], 0.0), w=[b_raw[1]])
        pool(lambda e: e.memset(QpA[:], 0.0), w=[b_Qp])
        pool(lambda e: e.memset(QpB[:], 0.0), w=[b_Qp])
        pool(lambda e: e.memset(Sb9[:, 0, :], 0.0), w=[b_Sb9[0]])

        G_G, G_BETA, G_GCUM, G_GL, G_EG, G_EKG, G_LNB, G_NEGG, G_SKBG, G_GB = range(10)


        hits = {}

        def chk2(name):
            if STOP == name:
                fw.dead = True

        def chk(name):
            hits[name] = hits.get(name, 0) + 1
            if STOP == name or STOP == "%s@%d" % (name, hits[name]):
                raise _Stop()

        def gdn_gen(grp):
            gp = grp % 2
            for c3 in range(3):
                dve(lambda e, c3=c3: e.tensor_scalar(out=cacc[:, c3, :], in0=raw[gp][:, c3, 0:512], scalar1=cw[:, c3 * 4:c3 * 4 + 1],
                                                      scalar2=None, op0=ALU.mult), r=[b_raw[gp], b_cw], w=[b_cacc])
                for jj in range(1, 4):
                    dve(lambda e, c3=c3, jj=jj: e.scalar_tensor_tensor(out=cacc[:, c3, :], in0=raw[gp][:, c3, jj:jj + 512],
                                                                        scalar=cw[:, c3 * 4 + jj:c3 * 4 + jj + 1], in1=cacc[:, c3, :],
                                                                        op0=ALU.mult, op1=ALU.add), r=[b_raw[gp], b_cw], w=[b_cacc])
            act(lambda e: e.activation(out=csil[:].rearrange("p a b -> p (a b)"), in_=cacc[:].rearrange("p a b -> p (a b)"),
                                       func=AF.Silu), r=[b_cacc], w=[b_csil])
            yield
            act(lambda e: e.activation(out=sqb[:].rearrange("p a b -> p (a b)"), in_=csil[:, 0:2, :].rearrange("p a b -> p (a b)"),
                                       func=AF.Square), r=[b_csil], w=[b_sqb])
            for c3 in range(2):
                mm(PB[:, c3 * 512:(c3 + 1) * 512], lhsT=ones_b[:], rhs=sqb[:, c3, :], r=[b_onesb, b_sqb], w=[b_PB])
            act(lambda e: e.activation(out=rnorm[:].rearrange("p a b -> p (a b)"), in_=PB[:, 0:1024], func=AF.Ln, bias=EPS),
                r=[b_PB], w=[b_rnorm])
            act(lambda e: e.activation(out=rnorm[:].rearrange("p a b -> p (a b)"), in_=rnorm[:].rearrange("p a b -> p (a b)"),
                                       func=AF.Exp, scale=-0.5), w=[b_rnorm])
            dve(lambda e: e.scalar_tensor_tensor(out=gT[:, 0, :], in0=csil[:, 0, :], scalar=128.0 ** -0.5, in1=rnorm[:, 0, :],
                                                 op0=ALU.mult, op1=ALU.mult), r=[b_csil, b_rnorm], w=[b_gT])
            dve(lambda e: e.tensor_tensor(out=gT[:, 1, :], in0=csil[:, 1, :], in1=rnorm[:, 1, :], op=ALU.mult),
                r=[b_csil, b_rnorm], w=[b_gT])
            act(lambda e: e.copy(out=gT[:, 2, :], in_=csil[:, 2, :]), r=[b_csil], w=[b_gT])
            for tt in range(4):
                for c3 in range(3):
                    tr(PT[:, c3 * 128:(c3 + 1) * 128], gT[:, c3, tt * 128:(tt + 1) * 128], ident_b[:], r=[b_gT, b_identb], w=[b_PT])
                act(lambda e, tt=tt: e.copy(out=gtok[:, tt, :, :], in_=PT[:, 0:384].rearrange("p (c d) -> p c d", c=3)),
                    r=[b_PT], w=[b_gtok])
            yield
            a_ap = abg[gp][:, :, 0]
            b_ap = abg[gp][:, :, 1]
            act(lambda e: e.activation(out=gsc[:, G_G, :], in_=a_ap, func=AF.Exp, bias=hsc[:, 1:2]), r=[b_abg[gp], b_hsc], w=[b_gsc])
            act(lambda e: e.activation(out=gsc[:, G_G, :], in_=gsc[:, G_G, :], func=AF.Ln, bias=1.0), w=[b_gsc])
            dve(lambda e: e.tensor_scalar(out=gsc[:, G_G, :], in0=gsc[:, G_G, :], scalar1=negA[:, 0:1], scalar2=None, op0=ALU.mult),
                r=[b_negA], w=[b_gsc])
            act(lambda e: e.activation(out=gsc[:, G_BETA, :], in_=b_ap, func=AF.Exp, scale=-1.0), r=[b_abg[gp]], w=[b_gsc])
            dve(lambda e: e.tensor_scalar(out=gsc[:, G_BETA, :], in0=gsc[:, G_BETA, :], scalar1=1.0, scalar2=None, op0=ALU.add), w=[b_gsc])
            dve(lambda e: e.reciprocal(out=gsc[:, G_BETA, :], in_=gsc[:, G_BETA, :]), w=[b_gsc])
            act(lambda e: e.activation(out=gsc[:, G_LNB, :], in_=gsc[:, G_BETA, :], func=AF.Ln), w=[b_gsc])
            mm(PD[:, 0:4], lhsT=gmask[:, 2, :], rhs=gsc[:, G_G, :], r=[b_gmask, b_gsc], w=[b_PD])
            mm(PD[:, 4:8], lhsT=gmask[:, 3, :], rhs=gsc[:, G_G, :], r=[b_gmask, b_gsc], w=[b_PD])
            mm(PD[:, 8:12], lhsT=gmask[:, 4, :], rhs=gsc[:, G_G, :], r=[b_gmask, b_gsc], w=[b_PD])
            act(lambda e: e.copy(out=gsc[:, G_GCUM, :], in_=PD[:, 0:4]), r=[b_PD], w=[b_gsc])
            act(lambda e: e.copy(out=glc[:].rearrange("p a b -> p (a b)"), in_=PD[:, 4:12]), r=[b_PD], w=[b_glc])
            dve(lambda e: e.tensor_copy(out=gsc[0:64, G_GL, :], in_=glc[0:64, 0, :]), r=[b_glc], w=[b_gsc])
            dve(lambda e: e.tensor_copy(out=gsc[64:128, G_GL, :], in_=glc[64:128, 1, :]), r=[b_glc], w=[b_gsc])
            act(lambda e: e.activation(out=gsc[:, G_EG, :], in_=gsc[:, G_GCUM, :], func=AF.Exp), w=[b_gsc])
            dve(lambda e: e.tensor_tensor(out=gsc[:, G_EKG, :], in0=gsc[:, G_GL, :], in1=gsc[:, G_GCUM, :], op=ALU.subtract), w=[b_gsc])
            act(lambda e: e.activation(out=gsc[:, G_EKG, :], in_=gsc[:, G_EKG, :], func=AF.Exp), w=[b_gsc])
            act(lambda e: e.activation(out=glc[:].rearrange("p a b -> p (a b)"), in_=glc[:].rearrange("p a b -> p (a b)"), func=AF.Exp),
                w=[b_glc])
            dve(lambda e: e.tensor_scalar(out=gsc[:, G_NEGG, :], in0=gsc[:, G_GCUM, :], scalar1=-1.0, scalar2=None, op0=ALU.mult), w=[b_gsc])
            dve(lambda e: e.tensor_tensor(out=gsc[:, G_SKBG, :], in0=gsc[:, G_BETA, :], in1=gsc[:, G_EG, :], op=ALU.mult), w=[b_gsc])
            dve(lambda e: e.tensor_scalar(out=gsc[:, G_SKBG, :], in0=gsc[:, G_SKBG, :], scalar1=-1.0, scalar2=None, op0=ALU.mult), w=[b_gsc])
            dve(lambda e: e.tensor_tensor(out=gsc[:, G_GB, :], in0=gsc[:, G_GCUM, :], in1=gsc[:, G_LNB, :], op=ALU.add), w=[b_gsc])

            yield
            dve(lambda e: e.tensor_copy(out=glc8[:, 0:8:2], in_=glc[:, 0, :]), r=[b_glc], w=[b_glc8])
            dve(lambda e: e.tensor_copy(out=glc8[:, 1:8:2], in_=glc[:, 1, :]), r=[b_glc], w=[b_glc8])
            identf4 = bcast(ident_f[:], [128, 4, 128], 1)
            identb4 = bcast(ident_b[:], [128, 4, 128], 1)

            def colb(col):
                return bcast(gsc[:, col, :], [128, 4, 128], 2)
            dve(lambda e: e.tensor_tensor(out=dg1[:], in0=identf4, in1=colb(G_GCUM), op=ALU.mult), r=[b_identf, b_gsc], w=[b_dg1])
            dve(lambda e: e.tensor_tensor(out=dg2[:], in0=identf4, in1=colb(G_GB), op=ALU.mult), r=[b_identf, b_gsc], w=[b_dg2])
            dve(lambda e: e.tensor_tensor(out=dgn[:], in0=identf4, in1=colb(G_NEGG), op=ALU.mult), r=[b_identf, b_gsc], w=[b_dgn])
            for (PSx, bPSx, dgx, mk_) in ((PC[:, 0:512], b_PC, dg1, 0), (PA[:, 0:512], b_PA, dg2, 1)):
                for tt in range(4):
                    reg = PSx[:, tt * 128:(tt + 1) * 128]
                    mm(reg, lhsT=ones_f[:], rhs=dgx[:, tt, :], start=True, stop=False, r=[b_onesf, b_dg1, b_dg2], w=[bPSx])
                    mm(reg, lhsT=ident_f[:], rhs=gmask[:, mk_, :], start=False, stop=False, r=[b_identf, b_gmask], w=[bPSx])
                    mm(reg, lhsT=dgn[:, tt, :], rhs=ones_f[:], start=False, stop=True, r=[b_dgn, b_onesf], w=[bPSx])
            act(lambda e: e.activation(out=decT[:].rearrange("p a b -> p (a b)"), in_=PC[:, 0:512], func=AF.Exp), r=[b_PC], w=[b_decT])
            act(lambda e: e.activation(out=decbT[:].rearrange("p a b -> p (a b)"), in_=PA[:, 0:512], func=AF.Exp), r=[b_PA], w=[b_decbT])
            yield
            for tt in range(4):
                kT_t = gT[:, 1, tt * 128:(tt + 1) * 128]
                qT_t = gT[:, 0, tt * 128:(tt + 1) * 128]
                mm(PC[:, 512 + tt * 128:512 + (tt + 1) * 128], lhsT=kT_t, rhs=kT_t, r=[b_gT], w=[b_PC])
                mm(PB[:, tt * 128:(tt + 1) * 128], lhsT=kT_t, rhs=qT_t, r=[b_gT], w=[b_PB])
            dve(lambda e: e.tensor_tensor(out=Um[0][:].rearrange("p a b -> p (a b)"), in0=PC[:, 512:1024], in1=decbT[:].rearrange("p a b -> p (a b)"),
                                          op=ALU.mult), r=[b_PC, b_decbT], w=[b_Um[0]])
            dve(lambda e: e.tensor_tensor(out=aqkT[:].rearrange("p a b -> p (a b)"), in0=PB[:, 0:512], in1=decT[:].rearrange("p a b -> p (a b)"),
                                          op=ALU.mult), r=[b_PB, b_decT], w=[b_aqkT])
            for tt in range(4):
                tr(PT[:, tt * 128:(tt + 1) * 128], Um[0][:, tt, :], ident_b[:], r=[b_Um[0], b_identb], w=[b_PT])
            act(lambda e: e.copy(out=Lm[0][:].rearrange("p a b -> p (a b)"), in_=PT[:, 0:512]), r=[b_PT], w=[b_Lm[0]])
            dve(lambda e: e.tensor_tensor(out=Pm[0][:], in0=identb4, in1=Um[0][:], op=ALU.subtract), r=[b_identb, b_Um[0]], w=[b_Pm[0]])
            yield
            cu, cp = 0, 0
            for lvl in range(5):
                nu = 1 - cu
                for tt in range(4):
                    mm(PC[:, tt * 128:(tt + 1) * 128], lhsT=Um[cu][:, tt, :], rhs=Lm[cu][:, tt, :], r=[b_Um[cu], b_Lm[cu]], w=[b_PC])
                if lvl < 4:
                    for tt in range(4):
                        mm(PA[:, tt * 128:(tt + 1) * 128], lhsT=Lm[cu][:, tt, :], rhs=Um[cu][:, tt, :], r=[b_Um[cu], b_Lm[cu]], w=[b_PA])
                act(lambda e, nu=nu: e.copy(out=Lm[nu][:].rearrange("p a b -> p (a b)"), in_=PC[:, 0:512]), r=[b_PC], w=[b_Lm[nu]])
                if lvl < 4:
                    act(lambda e, nu=nu: e.copy(out=Um[nu][:].rearrange("p a b -> p (a b)"), in_=PA[:, 0:512]), r=[b_PA], w=[b_Um[nu]])
                for tt in range(4):
                    mm(PB[:, tt * 128:(tt + 1) * 128], lhsT=Lm[nu][:, tt, :], rhs=Pm[cp][:, tt, :], r=[b_Lm[nu], b_Pm[cp]], w=[b_PB])
                dve(lambda e, cp=cp: e.tensor_tensor(out=Pm[1 - cp][:].rearrange("p a b -> p (a b)"), in0=PB[:, 0:512],
                                                     in1=Pm[cp][:].rearrange("p a b -> p (a b)"), op=ALU.add), r=[b_PB, b_Pm[cp]], w=[b_Pm[1 - cp]])
                cu = nu
                cp = 1 - cp
            yield
            Tt = Pm[cp]
            bTt = b_Pm[cp]
            dve(lambda e: e.tensor_tensor(out=Xm[:, :, 0:128], in0=gtok[:, :, 2, :], in1=colb(G_BETA), op=ALU.mult), r=[b_gtok, b_gsc], w=[b_Xm])
            dve(lambda e: e.tensor_tensor(out=Xm[:, :, 128:256], in0=gtok[:, :, 1, :], in1=colb(G_SKBG), op=ALU.mult), r=[b_gtok, b_gsc], w=[b_Xm])
            for tt in range(4):
                mm(PC[:, tt * 256:(tt + 1) * 256], lhsT=Tt[:, tt, :], rhs=Xm[:, tt, :], r=[bTt, b_Xm], w=[b_PC])
            act(lambda e: e.copy(out=uw[:].rearrange("p a b -> p (a b)"), in_=PC[:, 0:1024]), r=[b_PC], w=[b_uw])
            dve(lambda e: e.tensor_tensor(out=kgm[0:64, :, 0, :], in0=gtok[0:64, :, 1, :], in1=bcast(gsc[0:64, G_EKG, :], [64, 4, 128], 2), op=ALU.mult),
                r=[b_gtok, b_gsc], w=[b_kgm])
            dve(lambda e: e.tensor_tensor(out=kgm[64:128, :, 1, :], in0=gtok[64:128, :, 1, :], in1=bcast(gsc[64:128, G_EKG, :], [64, 4, 128], 2), op=ALU.mult),
                r=[b_gtok, b_gsc], w=[b_kgm])
            dve(lambda e: e.tensor_tensor(out=Dg[:], in0=identb4, in1=colb(G_EG), op=ALU.mult), r=[b_identb, b_gsc], w=[b_Dg])
            for tt in range(4):
                mm(PA[:, tt * 128:(tt + 1) * 128], lhsT=gtok[:, tt, 0, :], rhs=Dg[:, tt, :], start=True, stop=False, r=[b_gtok, b_Dg], w=[b_PA])
                mm(PA[:, tt * 128:(tt + 1) * 128], lhsT=uw[:, tt, 128:256], rhs=aqkT[:, tt, :], start=False, stop=True, r=[b_uw, b_aqkT], w=[b_PA])
            act(lambda e: e.copy(out=QpA[:, :, 0:64], in_=PA[:, 0:512].rearrange("p (a b) -> p a b", a=4)[:, :, 0:64]), r=[b_PA], w=[b_Qp])
            act(lambda e: e.copy(out=QpB[:, :, 64:128], in_=PA[:, 0:512].rearrange("p (a b) -> p a b", a=4)[:, :, 64:128]), r=[b_PA], w=[b_Qp])
            yield
            for tt in range(4):
                for c in range(2):
                    r0 = 64 * c
                    ch = tt * 2 + c
                    mm(PB[:, ch * 128:(ch + 1) * 128], lhsT=uw[:, tt, 128:256], rhs=kgm[:, tt, c, :], r=[b_uw, b_kgm], w=[b_PB])
            dve(lambda e: e.tensor_tensor(out=MTf[:], in0=bcast(ident_f[:], [128, 8, 128], 1), in1=bcast(glc8[:], [128, 8, 128], 2), op=ALU.mult),
                r=[b_identf, b_glc8], w=[b_MTf])
            dve(lambda e: e.tensor_tensor(out=MT8[:].rearrange("p a b -> p (a b)"), in0=PB[:, 0:1024], in1=MTf[:].rearrange("p a b -> p (a b)"),
                                          op=ALU.add), r=[b_PB, b_MTf], w=[b_MT8])
            for ch in range(8):
                tt, c = ch // 2, ch % 2
                r0 = 64 * c
                i = grp * 4 + tt
                PSc = PC[:, (ch % 2) * 512:(ch % 2) * 512 + 128]
                mm(PSc, lhsT=kgm[:, tt, c, :], rhs=uw[:, tt, 0:128], start=True, stop=False, r=[b_kgm, b_uw], w=[b_PC])
                mm(PSc, lhsT=MT8[:, ch, :], rhs=Sb9[:, ch, :], start=False, stop=True, r=[b_MT8, b_Sb9[ch]], w=[b_PC])
                if ch < 7:
                    act(lambda e, ch=ch, PSc=PSc: e.copy(out=Sb9[:, ch + 1, :], in_=PSc), r=[b_PC], w=[b_Sb9[ch + 1]])
                else:
                    act(lambda e, PSc=PSc: e.copy(out=Sb9[:, 8, :], in_=PSc), r=[b_PC], w=[b_Sb9[8]])
                    if i == NT - 1:
                        act(lambda e, PSc=PSc: e.copy(out=Sf[:], in_=PSc), r=[b_PC], w=[b_Sf])
            yield
            for tt in range(4):
                mm(PA[:, tt * 128:(tt + 1) * 128], lhsT=QpA[:, tt, :], rhs=Sb9[:, 2 * tt, :], start=True, stop=False, r=[b_Qp, b_Sb9[2 * tt]], w=[b_PA])
                mm(PA[:, tt * 128:(tt + 1) * 128], lhsT=QpB[:, tt, :], rhs=Sb9[:, 2 * tt + 1, :], start=False, stop=False,
                   r=[b_Qp, b_Sb9[2 * tt + 1]], w=[b_PA])
                mm(PA[:, tt * 128:(tt + 1) * 128], lhsT=aqkT[:, tt, :], rhs=uw[:, tt, 0:128], start=False, stop=True, r=[b_aqkT, b_uw], w=[b_PA])
            act(lambda e: e.copy(out=Sb9[:, 0, :], in_=Sb9[:, 8, :]), r=[b_Sb9[8]], w=[b_Sb9[0]])
            act(lambda e: e.copy(out=og4[:].rearrange("p a b -> p (a b)"), in_=PA[:, 0:512]), r=[b_PA], w=[b_og4])
            dve(lambda e: e.tensor_tensor(out=og4q[:], in0=og4[:], in1=og4[:], op=ALU.mult), r=[b_og4], w=[b_og4q])
            dve(lambda e: e.tensor_reduce(out=og4s[:], in_=og4q[:], axis=AX.X, op=ALU.add), r=[b_og4q], w=[b_og4s])
            act(lambda e: e.activation(out=og4s[:], in_=og4s[:], func=AF.Ln, scale=1.0 / 128, bias=EPS), w=[b_og4s])
            act(lambda e: e.activation(out=og4s[:], in_=og4s[:], func=AF.Exp, scale=-0.5), w=[b_og4s])
            dve(lambda e: e.tensor_tensor(out=og4[:], in0=og4[:], in1=bcast(og4s[:], [128, 4, 128], 2), op=ALU.mult), r=[b_og4s], w=[b_og4])
            dve(lambda e: e.tensor_tensor(out=og4[:], in0=og4[:], in1=bcast(gnb[:], [128, 4, 128], 1), op=ALU.mult), r=[b_gnb], w=[b_og4])
            dve(lambda e: e.tensor_tensor(out=omg[:], in0=og4[:], in1=zsil[gp][:], op=ALU.mult), r=[b_og4] + b_zsil[gp], w=[b_omg])
            for tt in range(4):
                tr(PT[:, tt * 128:(tt + 1) * 128], omg[:, tt, :], ident_b[:], r=[b_omg, b_identb], w=[b_PT])
            act(lambda e: e.copy(out=omT[gp][:, 1, :], in_=PT[:, 0:512]), r=[b_PT], w=[b_omT[gp]])
            for hh in range(2):
                if phaseB:
                    kch = (grp * 512) // CH
                    oc = (grp * 512) % CH
                    fw.dma("sp", xin[kch][hh * 128:(hh + 1) * 128, oc:oc + 512], omT[gp][:, hh, :], reads=[b_omT[gp]], writes=[b_xin[kch]])
                else:
                    fw.dma("sp", omT_o[hh * 128:(hh + 1) * 128, grp * 512:(grp + 1) * 512], omT[gp][:, hh, :], reads=[b_omT[gp]])


        def gen_G(grp):
            gp2 = grp % 2
            fw.dma("sp", csT[gp2][:, 0].rearrange("p a b -> p (a b)"), cos_d[:, grp * 128:(grp + 1) * 128], writes=[b_csT[gp2]])
            fw.dma("sp", csT[gp2][:, 1].rearrange("p a b -> p (a b)"), sin_d[:, grp * 128:(grp + 1) * 128], writes=[b_csT[gp2]])
            for tt in range(4):
                i = grp * 4 + tt
                s = i % NXB
                fw.dma("sp", xt[s][:], x_d[i * 128:(i + 1) * 128, :], writes=[b_xt[s]])
                act(lambda e, s=s: e.activation(out=xs[s][:], in_=xt[s][:], func=AF.Square, accum_out=ssq[s][:]),
                    r=[b_xt[s]], w=[b_xs[s], b_ssq[s]])
                act(lambda e, s=s: e.activation(out=ssq[s][:], in_=ssq[s][:], func=AF.Ln, scale=1.0 / 1024, bias=EPS), w=[b_ssq[s]])
                act(lambda e, s=s: e.activation(out=ssq[s][:], in_=ssq[s][:], func=AF.Exp, scale=-0.5), w=[b_ssq[s]])
                dve(lambda e, s=s: e.tensor_scalar(out=xs[s][:], in0=xt[s][:], scalar1=ssq[s][:, 0:1], scalar2=None,
                                                   op0=ALU.mult), r=[b_xt[s], b_ssq[s]], w=[b_xs[s]])
                for kt in range(8):
                    tr(PT[:, kt * 128:(kt + 1) * 128], xs[s][:, kt * 128:(kt + 1) * 128], ident_b[:],
                       r=[b_xs[s], b_identb], w=[b_PT])
                act(lambda e, tt=tt: e.copy(out=xnT[:, :, tt * 128:(tt + 1) * 128],
                                            in_=PT[:].rearrange("p (k t) -> p k t", k=8)),
                    r=[b_PT], w=[b_xnT[tt]])

            yield
            pool(lambda e: e.tensor_copy(out=raw[gp2][:, :, 0:3], in_=raw[1 - gp2][:, :, 512:515]), r=[b_raw[1 - gp2]], w=[b_raw[gp2]])
            for c3 in range(3):
                for kt in range(8):
                    mm(PB[:, 0:512], lhsT=wgdn[:, kt, c3 * 128:(c3 + 1) * 128], rhs=xnT[:, kt, :],
                       start=(kt == 0), stop=(kt == 7), r=[b_wgdn] + b_xnT, w=[b_PB])
                act(lambda e, c3=c3: e.copy(out=raw[gp2][:, c3, 3:515], in_=PB[:, 0:512]), r=[b_PB], w=[b_raw[gp2]])

            yield

        def gen_F(i):
            grp = i // 4
            tt = i % 4
            gp2 = grp % 2
            p2 = i % 2
            for kt in range(8):
                mm(PA[:, 0:512], lhsT=xnT[:, kt, tt * 128:(tt + 1) * 128], rhs=wtok[:, kt, 0:512],
                   start=(kt == 0), stop=(kt == 7), r=[b_xnT[tt], b_wtok], w=[b_PA])
            yield
            for kt in range(8):
                mm(PA[:, 512:782], lhsT=xnT[:, kt, tt * 128:(tt + 1) * 128], rhs=wtok[:, kt, 512:782],
                   start=(kt == 0), stop=(kt == 7), r=[b_xnT[tt], b_wtok], w=[b_PA])
            yield
            act(lambda e: e.copy(out=pj[p2][:], in_=PA[:, 0:782]), r=[b_PA], w=[b_pj[p2]])
            yield
            x1 = pj[p2][:, 0:448].rearrange("p (h d) -> p h d", h=7)[:, :, 0:32]
            x2 = pj[p2][:, 0:448].rearrange("p (h d) -> p h d", h=7)[:, :, 32:64]
            cosb = bcast(csT[gp2][:, 0, tt, :], [128, 7, 32], 1)
            sinb = bcast(csT[gp2][:, 1, tt, :], [128, 7, 32], 1)
            dve(lambda e: e.tensor_tensor(out=rt[:, 0], in0=x1, in1=cosb, op=ALU.mult), r=[b_pj[p2], b_csT[gp2]], w=[b_rt])
            yield
            dve(lambda e: e.tensor_tensor(out=rt[:, 1], in0=x2, in1=sinb, op=ALU.mult), r=[b_pj[p2], b_csT[gp2]], w=[b_rt])
            yield
            dve(lambda e: e.tensor_tensor(out=rt[:, 2], in0=x2, in1=cosb, op=ALU.mult), r=[b_pj[p2], b_csT[gp2]], w=[b_rt])
            yield
            dve(lambda e: e.tensor_tensor(out=rt[:, 3], in0=x1, in1=sinb, op=ALU.mult), r=[b_pj[p2], b_csT[gp2]], w=[b_rt])
            yield
            dve(lambda e: e.tensor_tensor(out=rq[p2][:, :, 0:32], in0=rt[:, 0], in1=rt[:, 1], op=ALU.subtract),
                 r=[b_rt], w=[b_rq[p2]])
            yield
            dve(lambda e: e.tensor_tensor(out=rq[p2][:, :, 32:64], in0=rt[:, 2], in1=rt[:, 3], op=ALU.add),
                 r=[b_rt], w=[b_rq[p2]])
            yield
            pool(lambda e: e.tensor_copy(out=ko[p2][:, 0:6:2, :], in_=rq[p2][:, 4:7, :]), r=[b_rq[p2]], w=[b_ko[p2]])
            yield
            pool(lambda e: e.tensor_copy(out=ko[p2][:, 1:6:2, :],
                                         in_=pj[p2][:, 448:640].rearrange("p (h d) -> p h d", h=3)),
                 r=[b_pj[p2]], w=[b_ko[p2]])
            yield
            fw.dma("pool", kv_o[i * 128:(i + 1) * 128, :], ko[p2][:, 0:4, :].rearrange("p a b -> p (a b)"), reads=[b_ko[p2]])
            yield
            if i >= NT - 4:
                wi = i - (NT - 4)
                fw.dma("pool", win_o[wi * 128:(wi + 1) * 128, :], ko[p2][:, 4:6, :].rearrange("p a b -> p (a b)"),
                       reads=[b_ko[p2]])
            yield
            act(lambda e: e.copy(out=qkb[:], in_=rq[p2][:]), r=[b_rq[p2]], w=[b_qkb])
            yield
            act(lambda e: e.copy(out=Vsel[:, i, 0:64], in_=pj[p2][:, 512:576]), r=[b_pj[p2]], w=[b_vsel[i]])
            yield
            act(lambda e: e.copy(out=Vwin[:, i % 8, 0:64], in_=pj[p2][:, 576:640]), r=[b_pj[p2]], w=[b_vwin[i % 8]])
            yield
            dve(lambda e: e.tensor_copy(out=kvc2[:, 0, :].rearrange("p (a d) -> p a d", a=2),
                                         in_=bcast(rq[p2][:, 4, :], [128, 2, 64], 1)), r=[b_rq[p2]], w=[b_kvc2])
            yield
            dve(lambda e: e.tensor_copy(out=kvc2[:, 1, :].rearrange("p (a d) -> p a d", a=2),
                                         in_=bcast(pj[p2][:, 448:512], [128, 2, 64], 1)), r=[b_pj[p2]], w=[b_kvc2])
            yield
            act(lambda e: e.activation(out=gz[:], in_=pj[p2][:, 640:780], func=AF.Exp, scale=-1.0), r=[b_pj[p2]], w=[b_gz])
            yield
            dve(lambda e: e.tensor_scalar(out=gz[:], in0=gz[:], scalar1=1.0, scalar2=None, op0=ALU.add), w=[b_gz])
            yield
            dve(lambda e: e.reciprocal(out=gz[:], in_=gz[:]), w=[b_gz])
            yield
            dve(lambda e: e.tensor_copy(out=gates[i % 2][:], in_=gz[:, 0:12]), r=[b_gz], w=[b_gates[i % 2]])
            yield
            dve(lambda e, tt=tt: e.tensor_tensor(out=zsil[gp2][:, tt, :], in0=gz[:, 12:140], in1=pj[p2][:, 652:780], op=ALU.mult),
                r=[b_gz, b_pj[p2]], w=[b_zsil[gp2][tt]])
            yield
            pool(lambda e, tt=tt: e.tensor_copy(out=abg[gp2][:, tt, :], in_=pj[p2][:, 780:782]), r=[b_pj[p2]], w=[b_abg[gp2]])
            yield
            for h in range(4):
                tr(PT[0:64, h * 128:(h + 1) * 128], qkb[:, h, :], ident_b[:], r=[b_qkb, b_identb], w=[b_PT])
            yield
            tr(PT[0:64, 512:640], qkb[:, 5, :], ident_b[:], r=[b_qkb, b_identb], w=[b_PT])
            yield
            tr(PT[0:64, 640:768], qkb[:, 6, :], ident_b[:], r=[b_qkb, b_identb], w=[b_PT])
            yield
            tr(PT[:, 768:896], kvc2[:, 0, :], ident_b[:], r=[b_kvc2, b_identb], w=[b_PT])
            yield
            tr(PT[:, 896:1024], kvc2[:, 1, :], ident_b[:], r=[b_kvc2, b_identb], w=[b_PT])
            yield
            act(lambda e: e.copy(out=QT[i % 2][:], in_=PT[0:64, 0:512]), r=[b_PT], w=[b_QT[i % 2]])
            yield
            act(lambda e: e.copy(out=Qaug[i % 2][0:64, 0, :], in_=PT[0:64, 0:256]), r=[b_PT], w=[b_Qaug[i % 2]])
            yield
            act(lambda e: e.copy(out=Qaug[i % 2][0:64, 1, :], in_=PT[0:64, 0:256]), r=[b_PT], w=[b_Qaug[i % 2]])
            yield
            act(lambda e: e.copy(out=KselT[0:64, i * 128:(i + 1) * 128], in_=PT[0:64, 512:640]),
                r=[b_PT], w=[b_ksel[i]])
            yield
            act(lambda e: e.copy(out=KwinT[0:64, (i % 8) * 128:(i % 8 + 1) * 128], in_=PT[0:64, 640:768]),
                r=[b_PT], w=[b_kwin[i % 8]])
            yield
            pool(lambda e: e.tensor_copy(out=Rk[:, :, 0:32], in_=Rk[:, :, 128:160]), w=[b_Rk])
            yield
            act(lambda e: e.copy(out=Rk[0:64, :, 32:160], in_=PT[0:64, 768:1024].rearrange("p (a t) -> p a t", a=2)),
                r=[b_PT], w=[b_Rk])
            yield
            act(lambda e: e.copy(out=Rk[64:128, :, 31:159], in_=PT[64:128, 768:1024].rearrange("p (a t) -> p a t", a=2)),
                r=[b_PT], w=[b_Rk])
            yield
            m0 = 1 if i == 0 else 0
            nb = 8 - m0
            n0 = 8 * i - 1 + m0
            for lp in range(16):
                c0 = 16 + 2 * lp + 16 * m0
                mm(PD[0:64, 0:nb], lhsT=cmpw[:, 0, lp, :], rhs=Rk[:, 0, c0:c0 + 16 * (nb - 1) + 1:16],
                   start=(lp == 0), stop=(lp == 15), r=[b_cmpw, b_Rk], w=[b_PD])
            yield
            act(lambda e: e.activation(out=ckT[:, n0:n0 + nb], in_=PD[0:64, 0:nb], func=AF.Identity, bias=ckb[:, 0:1]),
                r=[b_PD, b_ckb], w=[b_ckT])
            yield
            for lp in range(16):
                c0 = 16 + 2 * lp + 16 * m0
                mm(PD[0:nb, 64:128], lhsT=Rk[:, 1, c0:c0 + 16 * (nb - 1) + 1:16], rhs=cmpw[:, 1, lp, :],
                   start=(lp == 0), stop=(lp == 15), r=[b_cmpw, b_Rk], w=[b_PD])
            yield
            dve(lambda e: e.tensor_tensor(out=cvnew[0:nb, :], in0=PD[0:nb, 64:128], in1=cvb[0:nb, :], op=ALU.add),
                r=[b_PD, b_cvb], w=[b_cvnew])
            yield
            segs = []
            n = n0
            while n < n0 + nb:
                jt = n // 128
                cnt = min(n0 + nb - n, (jt + 1) * 128 - n)
                segs.append((n, cnt))
                n += cnt
            for (ns, cnt) in segs:
                fw.dma("sp", cvx[ns % 128:ns % 128 + cnt, ns // 128, 0:64], cvnew[ns - n0:ns - n0 + cnt, :],
                       reads=[b_cvnew], writes=[b_cvx])
            yield

            yield

        def gen_B(i):
            grp = i // 4
            tt = i % 4
            gp2 = grp % 2
            p2 = i % 2
            njt = (8 * i + 6) // 128 + 1
            for jt in range(njt):
                pb = jt
                mm(PA[:, 0:512], lhsT=ckT[:, jt * 128:(jt + 1) * 128], rhs=QT[i % 2][:], r=[b_ckT, b_QT[i % 2]], w=[b_PA])
                act(lambda e, pb=pb: e.activation(out=PTc[pb][:], in_=PA[:, 0:512], func=AF.Exp, scale=0.125),
                    r=[b_PA], w=[b_PTc[pb]])
                mk = None
                if jt == njt - 1:
                    mk = i % 16
                elif jt == njt - 2 and i % 16 == 0:
                    mk = 16
                if mk is not None:
                    dve(lambda e, mk=mk, pb=pb: e.tensor_tensor(out=PTc[pb][:].rearrange("p (h q) -> p h q", h=4),
                                                                in0=PTc[pb][:].rearrange("p (h q) -> p h q", h=4),
                                                                in1=bcast(cmpmask[:, mk, :], [128, 4, 128], 1), op=ALU.mult),
                        r=[b_cmpmask], w=[b_PTc[pb]])
            yield
            for h in range(4):
                for jt in range(njt):
                    mm(PC[:, (h // 2) * 512 + (h % 2) * 193:(h // 2) * 512 + (h % 2) * 193 + 193],
                       lhsT=PTc[jt][:, h * 128:(h + 1) * 128], rhs=cvx[:, jt, :],
                       start=(jt == 0), stop=(jt == njt - 1), r=[b_PTc[jt], b_cvx], w=[b_PC])
            yield
            act(lambda e: e.copy(out=acc_c[:, 0:2, :], in_=PC[:, 0:386].rearrange("p (h c) -> p h c", h=2)),
                r=[b_PC], w=[b_accc])
            yield
            act(lambda e: e.copy(out=acc_c[:, 2:4, :], in_=PC[:, 512:898].rearrange("p (h c) -> p h c", h=2)),
                r=[b_PC], w=[b_accc])
            yield
            dve(lambda e: e.tensor_scalar(out=rcp[:, 0:4], in0=acc_c[:, :, 192], scalar1=1e-30, scalar2=None, op0=ALU.max),
                r=[b_accc], w=[b_rcp])
            yield
            dve(lambda e: e.reciprocal(out=rcp[:, 0:4], in_=rcp[:, 0:4]), w=[b_rcp])
            yield
            dve(lambda e: e.tensor_scalar(out=imp[:], in0=acc_c[:, 0, 64:192], scalar1=rcp[:, 0:1], scalar2=None, op0=ALU.mult),
                r=[b_accc, b_rcp], w=[b_imp])
            yield
            for h in range(1, 4):
                dve(lambda e, h=h: e.scalar_tensor_tensor(out=imp[:], in0=acc_c[:, h, 64:192], scalar=rcp[:, h:h + 1],
                                                          in1=imp[:], op0=ALU.mult, op1=ALU.add),
                    r=[b_accc, b_rcp], w=[b_imp])
            yield
            dve(lambda e: e.tensor_tensor(out=score[:], in0=imp[:], in1=prel[:, 128 - 2 * i:256 - 2 * i], op=ALU.add),
                r=[b_imp, b_prel], w=[b_score])
            yield
            dve(lambda e: e.tensor_scalar(out=score[:, 0:1], in0=score[:, 0:1], scalar1=1e4, scalar2=None, op0=ALU.add),
                w=[b_score])
            yield
            dve(lambda e: e.max(out=mx8[:, 0:8], in_=score[:]), r=[b_score], w=[b_mx8])
            yield
            dve(lambda e: e.match_replace(out=sc2[:], in_to_replace=mx8[:, 0:8], in_values=score[:], imm_value=-3e38),
                r=[b_score, b_mx8], w=[b_sc2])
            yield
            dve(lambda e: e.max(out=mx8[:, 8:16], in_=sc2[:]), r=[b_sc2], w=[b_mx8])
            yield
            dve(lambda e: e.tensor_reduce(out=thr[:], in_=mx8[:, 8:16], axis=AX.X, op=ALU.min), r=[b_mx8], w=[b_thr])
            yield
            dve(lambda e: e.tensor_scalar(out=sc2[:], in0=score[:], scalar1=thr[:, 0:1], scalar2=None, op0=ALU.is_ge),
                r=[b_score, b_thr], w=[b_sc2])
            yield
            dve(lambda e: e.scalar_tensor_tensor(out=sc2[:], in0=score[:], scalar=-1e29, in1=sc2[:],
                                                 op0=ALU.is_gt, op1=ALU.mult), r=[b_score], w=[b_sc2])
            yield
            dve(lambda e: e.tensor_scalar(out=mbt[:, 0, :], in0=sc2[:], scalar1=-NEGB, scalar2=NEGB,
                                          op0=ALU.mult, op1=ALU.add), r=[b_sc2], w=[b_mbt])
            yield
            dve(lambda e: e.tensor_copy(out=mbt[:, 1, 0:64], in_=mbt[:, 0, 64:128]), w=[b_mbt])
            yield
            dve(lambda e: e.tensor_copy(out=mbt[:, 1, 64:128], in_=mbt[:, 0, 0:64]), w=[b_mbt])
            yield
            mm(PD[:, 0:128], lhsT=mbt[:, 1, :], rhs=ident_f[:], r=[b_mbt, b_identf], w=[b_PD])
            yield
            mm(PD[:, 128:256], lhsT=mbt[:, 0, :], rhs=ident_f[:], r=[b_mbt, b_identf], w=[b_PD])
            yield
            for hh_ in range(2):
                dve(lambda e, hh_=hh_: e.tensor_copy(out=Qaug[i % 2][64:128, 0, hh_ * 128:(hh_ + 1) * 128], in_=PD[64:128, 0:128]),
                    r=[b_PD], w=[b_Qaug[i % 2]])
                dve(lambda e, hh_=hh_: e.tensor_copy(out=Qaug[i % 2][64:128, 1, hh_ * 128:(hh_ + 1) * 128], in_=PD[64:128, 128:256]),
                    r=[b_PD], w=[b_Qaug[i % 2]])
            yield

            yield
            sgroups = []
            t = 0
            while t <= i:
                gt_ = min(4, i + 1 - t)
                sgroups.append((t, gt_))
                t += gt_

            def emit_S(gi_):
                t_, gt_ = sgroups[gi_]
                PSs_ = PA if gi_ % 2 == 0 else PB
                bPSs_ = b_PA if gi_ % 2 == 0 else b_PB
                for u in range(gt_):
                    tk = t_ + u
                    ab = 0 if tk < 32 else 1
                    mm(PSs_[:, u * 256:(u + 1) * 256], lhsT=KselT[:, tk * 128:(tk + 1) * 128], rhs=Qaug[i % 2][:, ab, :],
                       r=[b_ksel[tk], b_eind, b_Qaug[i % 2]], w=[bPSs_])
            emit_S(0)
            yield
            for gi in range(len(sgroups)):
                t, gt_ = sgroups[gi]
                pb = gi % 2
                PSs = PA if pb == 0 else PB
                bPSs = b_PA if pb == 0 else b_PB
                if gi + 1 < len(sgroups):
                    emit_S(gi + 1)
                yield
                act(lambda e, PSs=PSs, gt_=gt_, pb=pb: e.activation(out=PTs[pb][:, 0:gt_ * 256], in_=PSs[:, 0:gt_ * 256],
                                                                  func=AF.Exp, scale=0.125), r=[bPSs], w=[b_PTs[pb]])
                if t + gt_ - 1 == i:
                    u = gt_ - 1
                    dve(lambda e, u=u, pb=pb: e.tensor_tensor(
                        out=PTs[pb][:, u * 256:(u + 1) * 256].rearrange("p (h q) -> p h q", h=2),
                        in0=PTs[pb][:, u * 256:(u + 1) * 256].rearrange("p (h q) -> p h q", h=2),
                        in1=bcast(tri[:, 0, :], [128, 2, 128], 1), op=ALU.mult), r=[b_tri], w=[b_PTs[pb]])
                for u in range(gt_):
                    tk = t + u
                    for h in range(2):
                        mm(PC[:, h * 512:h * 512 + 65], lhsT=PTs[pb][:, u * 256 + h * 128:u * 256 + (h + 1) * 128],
                           rhs=Vsel[:, tk, :], start=(tk == 0), stop=(tk == i), r=[b_PTs[pb], b_vsel[tk]], w=[b_PC])
            yield
            gi = len(sgroups)
            yield
            act(lambda e: e.copy(out=acc_sw[:, 0, :], in_=PC[:, 0:65]), r=[b_PC], w=[b_accsw])
            yield
            act(lambda e: e.copy(out=acc_sw[:, 1, :], in_=PC[:, 512:577]), r=[b_PC], w=[b_accsw])
            yield

            yield
            t0w = max(0, i - 4)
            wt = list(range(t0w, i + 1))
            pb = gi % 2
            PSs = PA if pb == 0 else PB
            bPSs = b_PA if pb == 0 else b_PB
            for gsub in range(0, len(wt), 4):
                sub = wt[gsub:gsub + 4]
                for u, tk in enumerate(sub):
                    mm(PSs[:, u * 256:(u + 1) * 256], lhsT=KwinT[:, (tk % 8) * 128:(tk % 8 + 1) * 128], rhs=QT[i % 2][:, 0:256],
                       r=[b_kwin[tk % 8], b_QT[i % 2]], w=[bPSs])
                act(lambda e, PSs=PSs, n_=len(sub), pb=pb: e.activation(out=PTs[pb][:, 0:n_ * 256], in_=PSs[:, 0:n_ * 256],
                                                                       func=AF.Exp, scale=0.125), r=[bPSs], w=[b_PTs[pb]])
                for u, tk in enumerate(sub):
                    mkk = None
                    if tk == i:
                        mkk = 0
                    elif tk == i - 4:
                        mkk = 1
                    if mkk is not None:
                        dve(lambda e, u=u, pb=pb, mkk=mkk: e.tensor_tensor(
                            out=PTs[pb][:, u * 256:(u + 1) * 256].rearrange("p (h q) -> p h q", h=2),
                            in0=PTs[pb][:, u * 256:(u + 1) * 256].rearrange("p (h q) -> p h q", h=2),
                            in1=bcast(tri[:, mkk, :], [128, 2, 128], 1), op=ALU.mult), r=[b_tri], w=[b_PTs[pb]])
                for u, tk in enumerate(sub):
                    for h in range(2):
                        mm(PC[:, 386 + h * 512:451 + h * 512], lhsT=PTs[pb][:, u * 256 + h * 128:u * 256 + (h + 1) * 128],
                           rhs=Vwin[:, tk % 8, :], start=(tk == wt[0]), stop=(tk == i), r=[b_PTs[pb], b_vwin[tk % 8]], w=[b_PC])
                pb = 1 - pb
                PSs = PA if pb == 0 else PB
                bPSs = b_PA if pb == 0 else b_PB
            yield
            act(lambda e: e.copy(out=acc_sw[:, 2, :], in_=PC[:, 386:451]), r=[b_PC], w=[b_accsw])
            yield
            act(lambda e: e.copy(out=acc_sw[:, 3, :], in_=PC[:, 898:963]), r=[b_PC], w=[b_accsw])
            yield

            yield
            dve(lambda e: e.tensor_scalar(out=rcp[:, 4:8], in0=acc_sw[:, :, 64], scalar1=1e-30, scalar2=None, op0=ALU.max),
                r=[b_accsw], w=[b_rcp])
            yield
            dve(lambda e: e.reciprocal(out=rcp[:, 4:8], in_=rcp[:, 4:8]), w=[b_rcp])
            yield
            g3 = gates[i % 2][:, 0:6].rearrange("p (h j) -> p h j", h=2)
            cf = coef[:].rearrange("p (h j) -> p h j", h=2)
            dve(lambda e: e.tensor_tensor(out=cf[:, :, 0], in0=g3[:, :, 0], in1=rcp[:, 0:2], op=ALU.mult),
                r=[b_gates[i % 2], b_rcp], w=[b_coef])
            yield
            dve(lambda e: e.tensor_tensor(out=cf[:, :, 1], in0=g3[:, :, 1], in1=rcp[:, 4:6], op=ALU.mult),
                r=[b_gates[i % 2], b_rcp], w=[b_coef])
            yield
            dve(lambda e: e.tensor_tensor(out=cf[:, :, 2], in0=g3[:, :, 2], in1=rcp[:, 6:8], op=ALU.mult),
                r=[b_gates[i % 2], b_rcp], w=[b_coef])
            yield
            for h in range(2):
                dve(lambda e, h=h: e.tensor_scalar(out=onsa[:, h * 64:(h + 1) * 64], in0=acc_c[:, h, 0:64],
                                                   scalar1=coef[:, 3 * h:3 * h + 1], scalar2=None, op0=ALU.mult),
                    r=[b_accc, b_coef], w=[b_onsa])
                dve(lambda e, h=h: e.scalar_tensor_tensor(out=onsa[:, h * 64:(h + 1) * 64], in0=acc_sw[:, h, 0:64],
                                                          scalar=coef[:, 3 * h + 1:3 * h + 2], in1=onsa[:, h * 64:(h + 1) * 64],
                                                          op0=ALU.mult, op1=ALU.add), r=[b_accsw, b_coef], w=[b_onsa])
                dve(lambda e, h=h: e.scalar_tensor_tensor(out=om[:, h * 64:(h + 1) * 64], in0=acc_sw[:, 2 + h, 0:64],
                                                          scalar=coef[:, 3 * h + 2:3 * h + 3], in1=onsa[:, h * 64:(h + 1) * 64],
                                                          op0=ALU.mult, op1=ALU.add), r=[b_accsw, b_coef, b_onsa], w=[b_om])
            yield
            b_om_tiles = None

            if dbg and i == 1:
                fw.dma("sp", dbg_o[:, 0:772], acc_c[:].rearrange("p a b -> p (a b)"), reads=[b_accc])
                fw.dma("sp", dbg_o[:, 772:1032], acc_sw[:].rearrange("p a b -> p (a b)"), reads=[b_accsw])
                fw.dma("sp", dbg_o[:, 1032:1160], score[:], reads=[b_score])
                fw.dma("sp", dbg_o[:, 1160:1172], gates[i % 2][:], reads=[b_gates[i % 2]])
                fw.dma("sp", dbg_o[:, 1172:1300], mbt[:, 0, :], reads=[b_mbt])
            yield
            tr(PT[:, 0:128], om[:, 0:128], ident_b[:], r=[b_om, b_identb], w=[b_PT])
            yield
            act(lambda e, tt=tt: e.copy(out=omT[gp2][:, 0, tt * 128:(tt + 1) * 128], in_=PT[:, 0:128]), r=[b_PT], w=[b_omT[gp2]])
            yield

        if phaseB:
            wgu_s = nc.dram_tensor("wgu_s", [22, 128, 8 * 256], BF16).ap()
            wdn_s = nc.dram_tensor("wdn_s", [22, 128, 1024], BF16).ap()
            b_wgus = [Buf() for _ in range(22)]
            b_wdns = Buf()
            precast = []
            for f in range(22):
                dstv = wgu_s[f].rearrange("p (k c) -> p k c", k=8)
                precast.append(lambda f=f, dstv=dstv: fw.dma("pool", dstv[:, :, 0:128],
                                                              wgu_d[:, f * 128:(f + 1) * 128].rearrange("(k p) c -> p k c", p=128),
                                                              writes=[b_wgus[f]]))
                precast.append(lambda f=f, dstv=dstv: fw.dma("pool", dstv[:, :, 128:256],
                                                              wgu_d[:, 2816 + f * 128:2816 + (f + 1) * 128].rearrange("(k p) c -> p k c", p=128),
                                                              writes=[b_wgus[f]]))
            precast.append(lambda: fw.dma("pool", wdn_s[:, :, :], wdn_d[:, :].rearrange("(f p) c -> f p c", p=128), writes=[b_wdns]))
        gdn_prev = None
        try:
          chk("setup")
          def drain_gen(g_):
              if g_ is not None:
                  for _ in g_:
                      pass

          def interleave(gens):
              alive = [g_ for g_ in gens if g_ is not None]
              while alive:
                  for g_ in list(alive):
                      try:
                          next(g_)
                      except StopIteration:
                          alive.remove(g_)

          drain_gen(gen_G(0))
          drain_gen(gen_F(0))
          for i in range(NT):
              grp = i // 4
              if phaseB and i >= 2 and precast:
                  precast.pop(0)()
              gF = None
              if i + 1 < NT:
                  def chain_next(i=i):
                      if (i + 1) % 4 == 0:
                          yield from gen_G((i + 1) // 4)
                      yield from gen_F(i + 1)
                  gF = chain_next()
              interleave([gen_B(i), gF])
              if gdn_prev is not None:
                  for _ in range(4):
                      next(gdn_prev, None)
              if i % 4 == 3:
                  drain_gen(gdn_prev)
                  gdn_prev = gdn_gen(grp)

        except _Stop:
            pass
        if gdn_prev is not None:
            for _ in gdn_prev:
                pass
        if phaseB:
            while precast:
                precast.pop(0)()
        fw.dma("sp", S_o[:, :], Sf[:], reads=[b_Sf])
        fw.dma("sp", conv_o[:, :, :], raw[(NG - 1) % 2][:, :, 512:515], reads=[b_raw[(NG - 1) % 2]])

        if phaseB:
            xout = [nc.dram_tensor("xout%d" % k, [1024, CH], BF16).ap() for k in range(NCH)]
            b_xout = Buf()
            RG = [[0, 1, 2, 3], [4, 5, 6, 7]]
            ccs = st.enter_context(nc.semaphore("ccs"))
            for k in range(NCH if not SKIP_CC else 0):
                fw._wait("pool", b_xin[k].w)
                nc.gpsimd.collective_compute("AllGather", ALU.bypass, replica_groups=RG, ins=[xin[k][:, :].opt()],
                                             outs=[xout[k][:, :].opt()]).then_inc(ccs)
                nc.gpsimd.wait_ge(ccs, k + 1)
            pool(lambda e: e.memset(cvnew[0:1, 0:1], 0.0), w=[b_xout, b_cvnew])
            fw.barrier()
            stA.close()
            stS = st.enter_context(ExitStack())
            cur[0] = stS
            nb_ = [0]

            def T(shape, dt=F32):
                nb_[0] += 1
                return sb("sm%d" % nb_[0], shape, dt), Buf()

            xs_t, b_xs = T([4, 1024])
            gmix2, b_gmix2 = T([128, 8])
            ropes, b_ropes = T([4, 64])
            ptab, b_ptab = T([128, 256], I32)
            iota_c, b_iota = T([128, 1])
            idxs, b_idxs = T([128, 256], I32)
            cmpw64, b_cmpw64 = T([128, 2, 32, 64], BF16)
            pe64, b_pe64 = T([128, 2, 32], BF16)
            c2s2, b_c2s2 = T([128, 4, 128])
            on511, b_on511 = T([128, 4])
            oh4, b_oh4 = T([4, 80])
            bonus, b_bonus = T([1, 128])
            alogb, b_alogb = T([4, 8])
            gnrow, b_gnrow = T([1, 128])
            pjs, b_pjs = T([4, 3360])
            Xs, b_Xs = T([4, 2056])
            QKT, b_QKT = T([128, 14, 4], BF16)
            OGT, b_OGT = T([128, 4, 4], BF16)
            one11, b_one11 = T([1, 1])
            cb_s, b_cbs = T([64, 2])
            stSg = st.enter_context(ExitStack())
            cur[0] = stSg
            cwb, b_cwb = T([4, 4, 1536])
            stc, b_stc = T([4, 3, 1536])
            fw.dma("sp", xs_t[:], xs_d[:, :], writes=[b_xs])
            fw.dma("sp", gmix2[:], gmix_d[:, :], writes=[b_gmix2])
            fw.dma("sp", ropes[:], ropes_d[:, :], writes=[b_ropes])
            fw.dma("sp", ptab[:], ptab_d[:, :], writes=[b_ptab])
            fw.dma("sp", iota_c[:], iota_d[:, :], writes=[b_iota])
            fw.dma("pool", cmpw64[:].rearrange("p a b c -> p (a b c)"), cmpw64_d[:, :], writes=[b_cmpw64])
            fw.dma("pool", pe64[:].rearrange("p a b -> p (a b)"), pe64_d[:, :], writes=[b_pe64])
            fw.dma("sp", c2s2[:].rearrange("p a b -> p (a b)"), c2s_d[:, :], writes=[b_c2s2])
            fw.dma("sp", on511[:], ones511_d[:, :], writes=[b_on511])
            fw.dma("sp", oh4[:], oh4_d[:, :], writes=[b_oh4])
            fw.dma("sp", bonus[:], bonus_d[:, :], writes=[b_bonus])
            fw.dma("sp", alogb[:], alogb_d[:, :], writes=[b_alogb])
            fw.dma("sp", gnrow[:], gnrow_d[:, :], writes=[b_gnrow])
            fw.dma("sp", cwb[:].rearrange("p a b -> p (a b)"), convwb_d[:, :, :].rearrange("p a b -> p (a b)"), writes=[b_cwb])
            fw.dma("sp", stc[:].rearrange("p a b -> p (a b)"), gconv_d[:, :, :].rearrange("p a b -> p (a b)"), writes=[b_stc])
            dve(lambda e: e.tensor_scalar(out=idxs[:], in0=ptab[:], scalar1=128.0, scalar2=iota_c[:, 0:1], op0=ALU.mult, op1=ALU.add),
                r=[b_ptab, b_iota], w=[b_idxs])

            s_sq, b_ssq_ = T([4, 1024], BF16)
            s_ss, b_sss = T([4, 1])
            s_xn, b_sxn = T([4, 1024], BF16)
            xsT, b_xsT = T([128, 8, 4], BF16)
            wch0, b_wch0 = T([128, 8, 480], BF16)
            wch1, b_wch1 = T([128, 8, 480], BF16)
            wch = [wch0, wch1]; b_wch = [b_wch0, b_wch1]
            act(lambda e: e.activation(out=s_sq[:], in_=xs_t[:], func=AF.Square, accum_out=s_ss[:]), r=[b_xs], w=[b_ssq_, b_sss])
            act(lambda e: e.activation(out=s_ss[:], in_=s_ss[:], func=AF.Sqrt, scale=1.0 / 1024, bias=EPS), w=[b_sss])
            dve(lambda e: e.reciprocal(out=s_ss[:], in_=s_ss[:]), w=[b_sss])
            dve(lambda e: e.tensor_scalar(out=s_xn[:], in0=xs_t[:], scalar1=s_ss[:, 0:1], scalar2=None, op0=ALU.mult), r=[b_xs, b_sss], w=[b_sxn])
            for kt in range(8):
                tr(PT[:, kt * 4:(kt + 1) * 4], s_xn[0:4, kt * 128:(kt + 1) * 128], ident_b[0:4, 0:4], r=[b_sxn, b_identb], w=[b_PT])
            for kt in range(8):
                act(lambda e, kt=kt: e.activation(out=xsT[:, kt, :], in_=PT[:, kt * 4:(kt + 1) * 4], func=AF.Copy, scale=gmix2[:, kt:kt + 1]),
                    r=[b_PT, b_gmix2], w=[b_xsT])
            for ch in range(7):
                wb = ch % 2
                fw.dma("pool", wch[wb][:], win_full_d[:, ch * 480:(ch + 1) * 480].rearrange("(k p) c -> p k c", p=128), writes=[b_wch[wb]])
                for kt in range(8):
                    mm(PA[0:4, 0:480], lhsT=xsT[:, kt, :], rhs=wch[wb][:, kt, :], start=(kt == 0), stop=(kt == 7), r=[b_xsT, b_wch[wb]], w=[b_PA])
                act(lambda e, ch=ch: e.copy(out=pjs[:, ch * 480:(ch + 1) * 480], in_=PA[0:4, 0:480]), r=[b_PA], w=[b_pjs])

            chk2('s1')
            qkr, b_qkr = T([4, 14, 64])
            rqk, b_rqk = T([4, 14, 64])
            rts, b_rts = T([4, 4, 14, 32])
            kvv = pjs[:, 512:1280].rearrange("p (b k g d) -> p b k g d", b=3, k=2, g=2)
            pool(lambda e: e.tensor_copy(out=qkr[:, 0:8, :], in_=pjs[:, 0:512].rearrange("p (h d) -> p h d", h=8)), r=[b_pjs], w=[b_qkr])
            for br in range(3):
                pool(lambda e, br=br: e.tensor_copy(out=qkr[:, 8 + 2 * br:10 + 2 * br, :], in_=kvv[:, br, 0, :, :]), r=[b_pjs], w=[b_qkr])
            cs_b = bcast(ropes[:, 0:32], [4, 14, 32], 1)
            sn_b = bcast(ropes[:, 32:64], [4, 14, 32], 1)
            pool(lambda e: e.tensor_tensor(out=rts[:, 0], in0=qkr[:, :, 0:32], in1=cs_b, op=ALU.mult), r=[b_qkr, b_ropes], w=[b_rts])
            pool(lambda e: e.tensor_tensor(out=rts[:, 1], in0=qkr[:, :, 32:64], in1=sn_b, op=ALU.mult), r=[b_qkr, b_ropes], w=[b_rts])
            pool(lambda e: e.tensor_tensor(out=rts[:, 2], in0=qkr[:, :, 32:64], in1=cs_b, op=ALU.mult), r=[b_qkr, b_ropes], w=[b_rts])
            pool(lambda e: e.tensor_tensor(out=rts[:, 3], in0=qkr[:, :, 0:32], in1=sn_b, op=ALU.mult), r=[b_qkr, b_ropes], w=[b_rts])
            pool(lambda e: e.tensor_tensor(out=rqk[:, :, 0:32], in0=rts[:, 0], in1=rts[:, 1], op=ALU.subtract), r=[b_rts], w=[b_rqk])
            pool(lambda e: e.tensor_tensor(out=rqk[:, :, 32:64], in0=rts[:, 2], in1=rts[:, 3], op=ALU.add), r=[b_rts], w=[b_rqk])
            kvs_t, b_kvs = T([4, 4, 2, 64])
            wnew, b_wnew = T([4, 2, 2, 64])
            pool(lambda e: e.tensor_copy(out=kvs_t[:, 0], in_=rqk[:, 8:10, :]), r=[b_rqk], w=[b_kvs])
            pool(lambda e: e.tensor_copy(out=kvs_t[:, 1], in_=kvv[:, 0, 1, :, :]), r=[b_pjs], w=[b_kvs])
            pool(lambda e: e.tensor_copy(out=kvs_t[:, 2], in_=rqk[:, 10:12, :]), r=[b_rqk], w=[b_kvs])
            pool(lambda e: e.tensor_copy(out=kvs_t[:, 3], in_=kvv[:, 1, 1, :, :]), r=[b_pjs], w=[b_kvs])
            pool(lambda e: e.tensor_copy(out=wnew[:, 0], in_=rqk[:, 12:14, :]), r=[b_rqk], w=[b_wnew])
            pool(lambda e: e.tensor_copy(out=wnew[:, 1], in_=kvv[:, 2, 1, :, :]), r=[b_pjs], w=[b_wnew])
            fw.dma("sp", kvs_o[:, :], kvs_t[:].rearrange("p a b c -> p (a b c)"), reads=[b_kvs])
            fw.dma("sp", wins_o[:, 511, :], wnew[:].rearrange("p a b c -> p (a b c)"), reads=[b_wnew])
            for s_ in range(4):
                fw.dma("sp", wins_o[s_, 0:511, :], wincache_d[s_, 1:512, :])
            fw.dma("sp", convs_o[:, 0:2, :], gconv_d[:, 1:3, :])
            fw.dma("sp", convs_o[:, 2, :], pjs[:, 1304:2840], reads=[b_pjs])
            chk2('s2')
            qkb_s, b_qkbs = T([4, 14, 64], BF16)
            act(lambda e: e.copy(out=qkb_s[:], in_=rqk[:]), r=[b_rqk], w=[b_qkbs])
            for hd in range(14):
                tr(PT[0:64, hd * 4:(hd + 1) * 4], qkb_s[0:4, hd, :], ident_b[0:4, 0:4], r=[b_qkbs, b_identb], w=[b_PT])
            act(lambda e: e.copy(out=QKT[0:64].rearrange("p a b -> p (a b)"), in_=PT[0:64, 0:56]), r=[b_PT], w=[b_QKT])
            fw.dma("sp", QKT[64:128].rearrange("p a b -> p (a b)"), QKT[0:64].rearrange("p a b -> p (a b)"), reads=[b_QKT], writes=[b_QKT])

            chk2('s3')
            for kv in range(2):
                for l in range(32):
                    mm(PD[0:64, kv:kv + 1], lhsT=cmpw64[0:64, kv, l, :], rhs=pe64[0:64, kv, l:l + 1], start=(l == 0), stop=(l == 31),
                       r=[b_cmpw64, b_pe64], w=[b_PD])
            act(lambda e: e.copy(out=cb_s[:], in_=PD[0:64, 0:2]), r=[b_PD], w=[b_cbs])

            chk2('s3b')
            tmpc, b_tmpc = T([4, 1536])
            caccs, b_caccs = T([4, 1536])
            sqs, b_sqs = T([4, 1024])
            rn8, b_rn8 = T([4, 8])
            gsm, b_gsm = T([4, 8])
            dve(lambda e: e.tensor_tensor(out=caccs[:], in0=stc[:, 0, :], in1=cwb[:, 0, :], op=ALU.mult), r=[b_stc, b_cwb], w=[b_caccs])
            for jj in range(1, 4):
                src = stc[:, jj, :] if jj < 3 else pjs[:, 1304:2840]
                dve(lambda e, jj=jj, src=src: e.tensor_tensor(out=tmpc[:], in0=src, in1=cwb[:, jj, :], op=ALU.mult),
                    r=[b_stc, b_cwb, b_pjs], w=[b_tmpc])
                dve(lambda e: e.tensor_tensor(out=caccs[:], in0=caccs[:], in1=tmpc[:], op=ALU.add), r=[b_tmpc], w=[b_caccs])
            act(lambda e: e.activation(out=Xs[:, 0:1536], in_=caccs[:], func=AF.Silu), r=[b_caccs], w=[b_Xs])
            dve(lambda e: e.tensor_tensor(out=sqs[:], in0=Xs[:, 0:1024], in1=Xs[:, 0:1024], op=ALU.mult), r=[b_Xs], w=[b_sqs])
            dve(lambda e: e.tensor_reduce(out=rn8[:], in_=sqs[:].rearrange("p (h d) -> p h d", h=8), axis=AX.X, op=ALU.add), r=[b_sqs], w=[b_rn8])
            act(lambda e: e.activation(out=rn8[:], in_=rn8[:], func=AF.Sqrt, bias=EPS), w=[b_rn8])
            dve(lambda e: e.reciprocal(out=rn8[:], in_=rn8[:]), w=[b_rn8])
            dve(lambda e: e.tensor_scalar(out=rn8[:, 0:4], in0=rn8[:, 0:4], scalar1=128.0 ** -0.5, scalar2=None, op0=ALU.mult), w=[b_rn8])
            dve(lambda e: e.tensor_tensor(out=Xs[:, 0:1024].rearrange("p (h d) -> p h d", h=8), in0=Xs[:, 0:1024].rearrange("p (h d) -> p h d", h=8),
                                          in1=bcast(rn8[:], [4, 8, 128], 2), op=ALU.mult), r=[b_rn8], w=[b_Xs])
            act(lambda e: e.activation(out=Xs[:, 1536:2048], in_=pjs[:, 2840:3352], func=AF.Silu), r=[b_pjs], w=[b_Xs])
            act(lambda e: e.activation(out=Xs[:, 2048:2052], in_=pjs[:, 3356:3360], func=AF.Sigmoid), r=[b_pjs], w=[b_Xs])
            dve(lambda e: e.tensor_tensor(out=gsm[:, 0:4], in0=pjs[:, 3352:3356], in1=alogb[:, 4:8], op=ALU.add), r=[b_pjs, b_alogb], w=[b_gsm])
            act(lambda e: e.activation(out=gsm[:, 0:4], in_=gsm[:, 0:4], func=AF.Exp), w=[b_gsm])
            act(lambda e: e.activation(out=gsm[:, 0:4], in_=gsm[:, 0:4], func=AF.Ln, bias=1.0), w=[b_gsm])
            act(lambda e: e.activation(out=gsm[:, 4:8], in_=alogb[:, 0:4], func=AF.Exp), r=[b_alogb], w=[b_gsm])
            dve(lambda e: e.tensor_tensor(out=gsm[:, 0:4], in0=gsm[:, 0:4], in1=gsm[:, 4:8], op=ALU.mult), w=[b_gsm])
            act(lambda e: e.activation(out=Xs[:, 2052:2056], in_=gsm[:, 0:4], func=AF.Exp, scale=-1.0), r=[b_gsm], w=[b_Xs])

            chk2('s4')
            Rrow, b_Rrow = T([1, 2056])
            cols_s, b_cols = T([128, 8])
            S_t = [T([128, 128]) for _ in range(2)]
            r1, b_r1 = T([1, 257])
            vn, b_vn = T([1, 128])
            og, b_og = T([1, 128])
            ogs, b_ogs = T([1, 4])
            ogq, b_ogq = T([1, 128])
            egb, b_egb = T([128, 1])
            Snew = [T([128, 128]) for _ in range(2)]
            pool(lambda e: e.memset(one11[:], 1.0), w=[b_one11])
            for s_ in range(4):
                for chn, (c0, c1) in enumerate([(0, 512), (512, 1024), (1024, 1536), (1536, 2048), (2048, 2056)]):
                    mm(PD[0:1, 0:c1 - c0], lhsT=ident_f[0:4, s_:s_ + 1], rhs=Xs[0:4, c0:c1], r=[b_identf, b_Xs], w=[b_PD])
                    act(lambda e, c0=c0, c1=c1: e.copy(out=Rrow[0:1, c0:c1], in_=PD[0:1, 0:c1 - c0]), r=[b_PD], w=[b_Rrow])
                for hq in range(8):
                    mm(PD[:, hq:hq + 1], lhsT=Rrow[0:1, hq * 128:(hq + 1) * 128], rhs=one11[:], r=[b_Rrow, b_one11], w=[b_PD])
                act(lambda e: e.copy(out=cols_s[:], in_=PD[:, 0:8]), r=[b_PD], w=[b_cols])
                for h in range(4):
                    sbi = (s_ * 4 + h) % 2
                    St, bSt = S_t[sbi]
                    Sn, bSn = Snew[sbi]
                    fw.dma("sp", St[:], gS_d[s_ * 4 + h, :, :], writes=[bSt])
                    mm(PD[0:1, 0:128], lhsT=cols_s[:, 4 + h:5 + h], rhs=St[:], r=[b_cols, bSt], w=[b_PD])
                    mm(PD[0:1, 128:256], lhsT=cols_s[:, h:h + 1], rhs=St[:], r=[b_cols, bSt], w=[b_PD])
                    mm(PD[0:1, 256:257], lhsT=cols_s[:, h:h + 1], rhs=cols_s[:, 4 + h:5 + h], r=[b_cols], w=[b_PD])
                    act(lambda e: e.copy(out=r1[:], in_=PD[0:1, 0:257]), r=[b_PD], w=[b_r1])
                    egs = Rrow[0:1, 2052 + h:2053 + h]
                    bts = Rrow[0:1, 2048 + h:2049 + h]
                    dve(lambda e, egs=egs: e.tensor_scalar(out=vn[:], in0=r1[0:1, 0:128], scalar1=egs, scalar2=None, op0=ALU.mult),
                        r=[b_r1, b_Rrow], w=[b_vn])
                    dve(lambda e, h=h: e.tensor_tensor(out=vn[:], in0=Rrow[0:1, 1024 + h * 128:1024 + (h + 1) * 128], in1=vn[:], op=ALU.subtract),
                        r=[b_Rrow], w=[b_vn])
                    dve(lambda e, bts=bts: e.tensor_scalar(out=vn[:], in0=vn[:], scalar1=bts, scalar2=None, op0=ALU.mult), r=[b_Rrow], w=[b_vn])
                    dve(lambda e, egs=egs: e.tensor_scalar(out=og[:], in0=r1[0:1, 128:256], scalar1=egs, scalar2=None, op0=ALU.mult),
                        r=[b_r1, b_Rrow], w=[b_og])
                    dve(lambda e: e.scalar_tensor_tensor(out=og[:], in0=vn[:], scalar=r1[0:1, 256:257], in1=og[:], op0=ALU.mult, op1=ALU.add),
                        r=[b_vn, b_r1], w=[b_og])
                    act(lambda e: e.activation(out=ogq[:], in_=og[:], func=AF.Square, accum_out=ogs[0:1, 0:1]), r=[b_og], w=[b_ogq, b_ogs])
                    act(lambda e: e.activation(out=ogs[0:1, 0:1], in_=ogs[0:1, 0:1], func=AF.Sqrt, scale=1.0 / 128, bias=EPS), w=[b_ogs])
                    dve(lambda e: e.reciprocal(out=ogs[0:1, 0:1], in_=ogs[0:1, 0:1]), w=[b_ogs])
                    dve(lambda e: e.scalar_tensor_tensor(out=og[:], in0=og[:], scalar=ogs[0:1, 0:1], in1=gnrow[:], op0=ALU.mult, op1=ALU.mult),
                        r=[b_ogs, b_gnrow], w=[b_og])
                    dve(lambda e, h=h: e.tensor_tensor(out=og[:], in0=og[:], in1=Rrow[0:1, 1536 + h * 128:1536 + (h + 1) * 128], op=ALU.mult),
                        r=[b_Rrow], w=[b_og])
                    mm(PD[:, 300:301], lhsT=og[:], rhs=one11[:], r=[b_og, b_one11], w=[b_PD])
                    act(lambda e, h=h, s_=s_: e.copy(out=OGT[:, h, s_:s_ + 1], in_=PD[:, 300:301]), r=[b_PD], w=[b_OGT])
                    mm(PB[:, 0:128], lhsT=Rrow[0:1, 512 + h * 128:512 + (h + 1) * 128], rhs=vn[:], r=[b_Rrow, b_vn], w=[b_PB])
                    mm(PD[:, 310:311], lhsT=ones_f2[0:1, :], rhs=egs, r=[b_onesf2, b_Rrow], w=[b_PD])
                    act(lambda e: e.copy(out=egb[:], in_=PD[:, 310:311]), r=[b_PD], w=[b_egb])
                    dve(lambda e, St=St, Sn=Sn: e.scalar_tensor_tensor(out=Sn[:], in0=St[:], scalar=egb[:, 0:1], in1=PB[:, 0:128],
                                                                    op0=ALU.mult, op1=ALU.add), r=[bSt, b_egb, b_PB], w=[bSn])
                    fw.dma("sp", Ss_o[s_ * 4 + h, :, :], Sn[:], reads=[bSn])

            chk2('s5')
            fw.barrier()
            stSg.close()
            stSn = st.enter_context(ExitStack())
            cur[0] = stSn
            ckTs, b_ckTs = T([64, 512], BF16)
            cvTs, b_cvTs = T([64, 512], BF16)
            cvxs, b_cvxs = T([128, 4, 193], BF16)
            Pc, b_Pc = T([128, 16], BF16)
            accs, b_accs = T([4, 193])
            rcs, b_rcs = T([4, 4])
            impn, b_impn = T([4, 128])
            scs, b_scs = T([1, 136])
            sc2s, b_sc2s = T([1, 136])
            mx8s, b_mx8s = T([1, 16])
            thrs, b_thrs = T([1, 1])
            mbrow, b_mbrow = T([1, 128])
            MBp1, b_MBp1 = T([2, 64])
            MBp, b_MBp = T([2, 64, 4], BF16)
            efix, b_efix = T([2, 128], BF16)
            oh2, b_oh2 = T([1, 4])
            fw.dma("pool", efix[:], efix_d[:, :], writes=[b_efix])
            fw.dma("sp", oh2[:], oh2_d[:, :], writes=[b_oh2])
            Psel, b_Psel = T([128, 256], BF16)
            pnew, b_pnew = T([4, 2])
            vrow, b_vrow = T([4, 128])
            Abr, b_Abr = T([4, 3, 8, 64])
            asel, b_asel = T([4, 2, 65])
            wc, b_wc = T([128, 4, 256])
            wcb, b_wcb = T([128, 4, 128], BF16)
            Vws2 = [T([128, 4, 2, 65], BF16) for _ in range(2)]
            KwTs2 = [T([128, 4, 128], BF16) for _ in range(2)]
            Pw, b_Pw = T([128, 16], BF16)
            stSk = st.enter_context(ExitStack())
            cur[0] = stSk
            KTs2 = [T([128, 3, 8192], BF16) for _ in range(2)]
            Vs2 = [T([128, 64, 2, 65], BF16) for _ in range(2)]
            pg = [T([128, 512]) for _ in range(4)]
            pgb = [T([128, 384], BF16) for _ in range(3)]
            for q_ in range(2):
                pool(lambda e, q_=q_: e.memset(Vs2[q_][0][:, :, :, 64:65], 1.0), w=[Vs2[q_][1]])
                pool(lambda e, q_=q_: e.memset(Vws2[q_][0][:, :, :, 64:65], 1.0), w=[Vws2[q_][1]])
            pool(lambda e: e.memset(ckTs[:], 0.0), w=[b_ckTs])
            pool(lambda e: e.memset(cvTs[:], 0.0), w=[b_cvTs])
            pool(lambda e: e.memset(cvxs[:], 0.0), w=[b_cvxs])
            pool(lambda e: e.tensor_copy(out=cvxs[:, :, 64:192], in_=c2s2[:]), r=[b_c2s2], w=[b_cvxs])
            pool(lambda e: e.tensor_copy(out=cvxs[:, :, 192], in_=on511[:]), r=[b_on511], w=[b_cvxs])
            pool(lambda e: e.memset(scs[:], 1e4), w=[b_scs])
            def gen_pages(s_):
                KTs, b_KTs = KTs2[s_ % 2]
                Vs, b_Vs = Vs2[s_ % 2]
                Vws, b_Vws = Vws2[s_ % 2]
                KwTs, b_KwTs = KwTs2[s_ % 2]
                for p_ in range(64):
                    pgt, bpg = pg[p_ % 4]
                    pgbt, bpgb = pgb[p_ % 3]
                    col = s_ * 64 + p_
                    fw.dma("pool", None, None, reads=[b_idxs], writes=[bpg],
                           fn=lambda e, pgt=pgt, col=col: e.indirect_dma_start(
                               out=pgt[:], out_offset=None, in_=cache_d[:, :],
                               in_offset=bass.IndirectOffsetOnAxis(ap=idxs[:, col:col + 1], axis=0)))
                    dve(lambda e, pgt=pgt, pgbt=pgbt: e.tensor_copy(out=pgbt[:], in_=pgt[:, 0:384]), r=[bpg], w=[bpgb])
                    act(lambda e, pgt=pgt, p_=p_: e.copy(out=Vs[:, p_, :, 0:64], in_=pgt[:, 384:512].rearrange("p (g d) -> p g d", g=2)),
                        r=[bpg], w=[b_Vs])
                    for kg in range(3):
                        tr(PT[:, kg * 128:(kg + 1) * 128], pgbt[:, kg * 128:(kg + 1) * 128], ident_b[:], r=[bpgb, b_identb], w=[b_PT])
                    act(lambda e, p_=p_: e.copy(out=KTs[:, :, p_ * 128:(p_ + 1) * 128], in_=PT[:, 0:384].rearrange("p (a t) -> p a t", a=3)),
                        r=[b_PT], w=[b_KTs])
                    yield
                fw.dma("sp", wc[:], wincache_d[s_, :, :].rearrange("(t p) c -> p t c", p=128), writes=[b_wc])
                pool(lambda e: e.tensor_copy(out=wcb[:], in_=wc[:, :, 0:128]), r=[b_wc], w=[b_wcb])
                pool(lambda e: e.tensor_copy(out=Vws[:, :, :, 0:64], in_=wc[:, :, 128:256].rearrange("p t (g d) -> p t g d", g=2)),
                     r=[b_wc], w=[b_Vws])
                for t_ in range(4):
                    tr(PT[:, t_ * 128:(t_ + 1) * 128], wcb[:, t_, :], ident_b[:], r=[b_wcb, b_identb], w=[b_PT])
                act(lambda e: e.copy(out=KwTs[:].rearrange("p a b -> p (a b)"), in_=PT[:, 0:512]), r=[b_PT], w=[b_KwTs])
                yield

            def gen_attn(s_):
                KTs, b_KTs = KTs2[s_ % 2]
                Vs, b_Vs = Vs2[s_ % 2]
                Vws, b_Vws = Vws2[s_ % 2]
                KwTs, b_KwTs = KwTs2[s_ % 2]
                for g_ in range(2):
                    sg = s_ * 2 + g_
                    Qg = QKT[0:64, 4 * g_:4 * g_ + 4, s_]
                    g0_, g1_ = g_ * 64, (g_ + 1) * 64
                    Qgg = QKT[g0_:g1_, 4 * g_:4 * g_ + 4, s_]
                    for kv in range(2):
                        for l in range(32):
                            mm(PA[0:64, 0:511], lhsT=cmpw64[g0_:g1_, kv, l, :], rhs=KTs[g0_:g1_, kv, l:l + 16 * 510 + 1:16],
                               start=(l == 0), stop=(l == 31), r=[b_cmpw64, b_KTs], w=[b_PA])
                        dst = ckTs if kv == 0 else cvTs
                        bd = b_ckTs if kv == 0 else b_cvTs
                        act(lambda e, dst=dst, kv=kv: e.activation(out=dst[:, 0:511], in_=PA[0:64, 0:511], func=AF.Identity, bias=cb_s[:, kv:kv + 1]),
                            r=[b_PA, b_cbs], w=[bd])
                    yield
                    for jt in range(4):
                        tr(PT[:, jt * 64:(jt + 1) * 64], cvTs[:, jt * 128:(jt + 1) * 128], ident_b[0:64, 0:64], r=[b_cvTs, b_identb], w=[b_PT])
                    act(lambda e: e.copy(out=cvxs[:, :, 0:64], in_=PT[:, 0:256].rearrange("p (a d) -> p a d", a=4)), r=[b_PT], w=[b_cvxs])
                    yield
                    for jt in range(4):
                        mm(PD[:, jt * 4:(jt + 1) * 4], lhsT=ckTs[:, jt * 128:(jt + 1) * 128], rhs=Qg, r=[b_ckTs, b_QKT], w=[b_PD])
                    act(lambda e: e.activation(out=Pc[:], in_=PD[:, 0:16], func=AF.Exp, scale=0.125), r=[b_PD], w=[b_Pc])
                    for jt in range(4):
                        mm(PB[0:4, 0:193], lhsT=Pc[:, jt * 4:(jt + 1) * 4], rhs=cvxs[:, jt, :], start=(jt == 0), stop=(jt == 3),
                           r=[b_Pc, b_cvxs], w=[b_PB])
                    act(lambda e: e.copy(out=accs[:], in_=PB[0:4, 0:193]), r=[b_PB], w=[b_accs])
                    dve(lambda e: e.tensor_scalar(out=rcs[:, 0:1], in0=accs[:, 192:193], scalar1=1e-30, scalar2=None, op0=ALU.max), r=[b_accs], w=[b_rcs])
                    dve(lambda e: e.reciprocal(out=rcs[:, 0:1], in_=rcs[:, 0:1]), w=[b_rcs])
                    dve(lambda e, sg=sg: e.tensor_scalar(out=Abr[:, 0, sg, :], in0=accs[:, 0:64], scalar1=rcs[:, 0:1], scalar2=None, op0=ALU.mult),
                        r=[b_accs, b_rcs], w=[b_Abr])
                    dve(lambda e: e.tensor_scalar(out=impn[:], in0=accs[:, 64:192], scalar1=rcs[:, 0:1], scalar2=None, op0=ALU.mult),
                        r=[b_accs, b_rcs], w=[b_impn])
                    mm(PD[0:1, 64:192], lhsT=ones_f2[0:4, 0:1], rhs=impn[:], r=[b_onesf2, b_impn], w=[b_PD])
                    yield
                    dve(lambda e: e.tensor_tensor(out=scs[0:1, 0:128], in0=PD[0:1, 64:192], in1=bonus[:], op=ALU.add), r=[b_PD, b_bonus], w=[b_scs])
                    dve(lambda e: e.max(out=mx8s[:, 0:8], in_=scs[0:1, 0:129]), r=[b_scs], w=[b_mx8s])
                    dve(lambda e: e.match_replace(out=sc2s[0:1, 0:129], in_to_replace=mx8s[:, 0:8], in_values=scs[0:1, 0:129], imm_value=-3e38),
                        r=[b_scs, b_mx8s], w=[b_sc2s])
                    dve(lambda e: e.max(out=mx8s[:, 8:16], in_=sc2s[0:1, 0:129]), r=[b_sc2s], w=[b_mx8s])
                    dve(lambda e: e.tensor_reduce(out=thrs[:], in_=mx8s[:, 8:16], axis=AX.X, op=ALU.min), r=[b_mx8s], w=[b_thrs])
                    dve(lambda e: e.tensor_scalar(out=mbrow[:], in0=scs[0:1, 0:128], scalar1=thrs[0:1, 0:1], scalar2=None, op0=ALU.is_ge),
                        r=[b_scs, b_thrs], w=[b_mbrow])
                    dve(lambda e: e.tensor_scalar(out=mbrow[:], in0=mbrow[:], scalar1=-NEGB, scalar2=NEGB, op0=ALU.mult, op1=ALU.add), w=[b_mbrow])
                    mm(PD[0:2, 200:264], lhsT=oh2[0:1, 0:2], rhs=mbrow[0:1, 0:128:2], start=True, stop=False, r=[b_oh2, b_mbrow], w=[b_PD])
                    mm(PD[0:2, 200:264], lhsT=oh2[0:1, 2:4], rhs=mbrow[0:1, 1:128:2], start=False, stop=True, r=[b_oh2, b_mbrow], w=[b_PD])
                    act(lambda e: e.copy(out=MBp1[:], in_=PD[0:2, 200:264]), r=[b_PD], w=[b_MBp1])
                    dve(lambda e: e.tensor_copy(out=MBp[:], in_=bcast(MBp1[:], [2, 64, 4], 2)), r=[b_MBp1], w=[b_MBp])
                    yield
                    for t_ in range(64):
                        a_ = 0 if t_ < 32 else 1
                        mm(PA[:, 512 + t_ * 4:512 + (t_ + 1) * 4], lhsT=KTs[g0_:g1_, 2, t_ * 128:(t_ + 1) * 128], rhs=Qgg, start=True, stop=False,
                           r=[b_KTs, b_QKT], w=[b_PA])
                        mm(PA[:, 512 + t_ * 4:512 + (t_ + 1) * 4], lhsT=efix[0:2, :], rhs=MBp[0:2, t_, :], start=False, stop=True,
                           r=[b_efix, b_MBp], w=[b_PA])
                    act(lambda e: e.activation(out=Psel[:], in_=PA[:, 512:768], func=AF.Exp, scale=0.125), r=[b_PA], w=[b_Psel])
                    for t_ in range(64):
                        mm(PB[0:4, 256:321], lhsT=Psel[:, t_ * 4:(t_ + 1) * 4], rhs=Vs[:, t_, g_, :], start=(t_ == 0), stop=(t_ == 63),
                           r=[b_Psel, b_Vs], w=[b_PB])
                    yield
                    mm(PD[0:4, 210:211], lhsT=Qg, rhs=QKT[0:64, 10 + g_, s_:s_ + 1], r=[b_QKT], w=[b_PD])
                    mm(PD[0:4, 211:212], lhsT=Qg, rhs=QKT[0:64, 12 + g_, s_:s_ + 1], r=[b_QKT], w=[b_PD])
                    act(lambda e: e.activation(out=pnew[:], in_=PD[0:4, 210:212], func=AF.Exp, scale=0.125), r=[b_PD], w=[b_pnew])
                    mm(PD[0:4, 220:284], lhsT=oh4[0:4, s_ * 4:(s_ + 1) * 4], rhs=kvv[:, 1, 1, g_, :], r=[b_oh4, b_pjs], w=[b_PD])
                    mm(PD[0:4, 284:348], lhsT=oh4[0:4, s_ * 4:(s_ + 1) * 4], rhs=kvv[:, 2, 1, g_, :], r=[b_oh4, b_pjs], w=[b_PD])
                    act(lambda e: e.copy(out=vrow[:], in_=PD[0:4, 220:348]), r=[b_PD], w=[b_vrow])
                    dve(lambda e: e.scalar_tensor_tensor(out=asel[:, 0, 0:64], in0=vrow[:, 0:64], scalar=pnew[:, 0:1], in1=PB[0:4, 256:320],
                                                         op0=ALU.mult, op1=ALU.add), r=[b_vrow, b_pnew, b_PB], w=[b_asel])
                    dve(lambda e: e.tensor_tensor(out=asel[:, 0, 64:65], in0=PB[0:4, 320:321], in1=pnew[:, 0:1], op=ALU.add),
                        r=[b_PB, b_pnew], w=[b_asel])
                    yield
                    for t_ in range(4):
                        mm(PD[:, 352 + t_ * 4:356 + t_ * 4], lhsT=KwTs[g0_:g1_, t_, :], rhs=Qgg, r=[b_KwTs, b_QKT], w=[b_PD])
                    act(lambda e: e.activation(out=Pw[:], in_=PD[:, 352:368], func=AF.Exp, scale=0.125), r=[b_PD], w=[b_Pw])
                    for t_ in range(4):
                        mm(PB[0:4, 384:449], lhsT=Pw[:, t_ * 4:(t_ + 1) * 4], rhs=Vws[:, t_, g_, :], start=(t_ == 0), stop=(t_ == 3),
                           r=[b_Pw, b_Vws], w=[b_PB])
                    dve(lambda e: e.scalar_tensor_tensor(out=asel[:, 1, 0:64], in0=vrow[:, 64:128], scalar=pnew[:, 1:2], in1=PB[0:4, 384:448],
                                                         op0=ALU.mult, op1=ALU.add), r=[b_vrow, b_pnew, b_PB], w=[b_asel])
                    dve(lambda e: e.tensor_tensor(out=asel[:, 1, 64:65], in0=PB[0:4, 448:449], in1=pnew[:, 1:2], op=ALU.add),
                        r=[b_PB, b_pnew], w=[b_asel])
                    dve(lambda e: e.reciprocal(out=rcs[:, 1:3], in_=asel[:, :, 64]), r=[b_asel], w=[b_rcs])
                    for br in range(2):
                        dve(lambda e, br=br, sg=sg: e.tensor_scalar(out=Abr[:, 1 + br, sg, :], in0=asel[:, br, 0:64], scalar1=rcs[:, 1 + br:2 + br],
                                                                    scalar2=None, op0=ALU.mult), r=[b_asel, b_rcs], w=[b_Abr])

                yield

            def drain_gen2(g_):
                for _ in g_:
                    pass

            def interleave2(gens):
                alive = [g_ for g_ in gens if g_ is not None]
                while alive:
                    for g_ in list(alive):
                        try:
                            next(g_)
                        except StopIteration:
                            alive.remove(g_)
            drain_gen2(gen_pages(0))
            for s_ in range(4):
                interleave2([gen_attn(s_), gen_pages(s_ + 1) if s_ < 3 else None])
            fw.barrier()
            stSk.close()
            cur[0] = stSn
            woutn, b_woutn = T([64, 8, 1024], BF16)
            woutg, b_woutg = T([128, 4, 1024], BF16)
            fw.dma("pool", woutn[:].rearrange("p a b -> p (a b)"), woutn_d[:, :], writes=[b_woutn])
            fw.dma("pool", woutg[:].rearrange("p a b -> p (a b)"), woutg_d[:, :], writes=[b_woutg])
            chk2('s11')
            gts, b_gts = T([4, 8, 3])
            osum, b_osum = T([4, 8, 64])
            otmp, b_otmp = T([4, 8, 64])
            onb, b_onb = T([4, 8, 64], BF16)
            OT, b_OT = T([64, 8, 4], BF16)
            act(lambda e: e.activation(out=gts[:].rearrange("p a b -> p (a b)"), in_=pjs[:, 1280:1304], func=AF.Sigmoid), r=[b_pjs], w=[b_gts])
            for br in range(3):
                for g_ in range(2):
                    for r_ in range(4):
                        h_ = 4 * g_ + r_
                        for s_ in range(4):
                            mm(PC[0:4, h_ * 64:(h_ + 1) * 64], lhsT=oh4[0:4, 16 + (r_ * 4 + s_) * 4:16 + (r_ * 4 + s_ + 1) * 4],
                               rhs=Abr[:, br, s_ * 2 + g_, :], start=(s_ == 0), stop=(s_ == 3), r=[b_oh4, b_Abr], w=[b_PC])
                gb = bcast(gts[:, :, br], [4, 8, 64], 2)
                if br == 0:
                    dve(lambda e, gb=gb: e.tensor_tensor(out=osum[:], in0=PC[0:4, 0:512].rearrange("p (h d) -> p h d", h=8), in1=gb, op=ALU.mult),
                        r=[b_PC, b_gts], w=[b_osum])
                else:
                    dve(lambda e, gb=gb: e.tensor_tensor(out=otmp[:], in0=PC[0:4, 0:512].rearrange("p (h d) -> p h d", h=8), in1=gb, op=ALU.mult),
                        r=[b_PC, b_gts], w=[b_otmp])
                    dve(lambda e: e.tensor_tensor(out=osum[:], in0=osum[:], in1=otmp[:], op=ALU.add), r=[b_otmp], w=[b_osum])
            act(lambda e: e.copy(out=onb[:], in_=osum[:]), r=[b_osum], w=[b_onb])
            for h_ in range(8):
                tr(PT[0:64, h_ * 4:(h_ + 1) * 4], onb[0:4, h_, :], ident_b[0:4, 0:4], r=[b_onb, b_identb], w=[b_PT])
            act(lambda e: e.copy(out=OT[:].rearrange("p a b -> p (a b)"), in_=PT[0:64, 0:32]), r=[b_PT], w=[b_OT])
            chk2('s12')
            for half in range(2):
                for h_ in range(8):
                    mm(PA[0:4, half * 512:(half + 1) * 512], lhsT=OT[:, h_, :], rhs=woutn[:, h_, half * 512:(half + 1) * 512],
                       start=(h_ == 0), stop=False, r=[b_OT, b_woutn], w=[b_PA])
                for h_ in range(4):
                    mm(PA[0:4, half * 512:(half + 1) * 512], lhsT=OGT[:, h_, :], rhs=woutg[:, h_, half * 512:(half + 1) * 512],
                       start=False, stop=(h_ == 3), r=[b_OGT, b_woutg], w=[b_PA])
            dve(lambda e: e.tensor_tensor(out=hs_res[:], in0=PA[0:4, :], in1=xs_t[:], op=ALU.add), r=[b_PA, b_xs], w=[b_hsres])
            fw.barrier()
            stSn.close()
            stS.close()
            stB = st.enter_context(ExitStack())
            cur[0] = stB
            wout = sb("wout", [128, 8, 1024], BF16); b_wout = Buf()
            wdn = sb("wdn", [128, 22, 1024], BF16); b_wdn = Buf()
            nfb = sb("nfb", [128, 1024]); b_nfb = Buf()
            gffn = sb("gffn", [128, 8]); b_gffn = Buf()
            sel4 = sb("sel4_sb", [128, 4]); b_sel4 = Buf()
            cand = [sb("cand%d" % i, [128, 8, 512], BF16) for i in range(2)]; b_cand = [Buf() for _ in range(2)]
            mixT = sb("mixT", [128, 8, 512], BF16); b_mixT = Buf()
            xt2 = [sb("xt2_%d" % i, [128, 1024]) for i in range(2)]; b_xt2 = [Buf() for _ in range(2)]
            hres = sb("hres", [128, 4, 1024]); b_hres = [Buf() for _ in range(4)]
            hsq = sb("hsq", [128, 1024], BF16); b_hsq = Buf()
            hss = sb("hss", [128, 1]); b_hss = Buf()
            hs = sb("hs", [128, 1024], BF16); b_hs = Buf()
            hnT = sb("hnT", [128, 8, 512], BF16); b_hnT = Buf()
            wg = [sb("wg%d" % i, [128, 8, 256], BF16) for i in range(3)]; b_wg = [Buf() for _ in range(3)]
            sg = [sb("sg%d" % i, [128, 512]) for i in range(2)]; b_sg = [Buf() for _ in range(2)]
            actT = sb("actT", [128, 22, 512], BF16); b_actT = Buf()
            yb = [sb("yb%d" % i, [128, 1024]) for i in range(2)]; b_yb = [Buf() for _ in range(2)]
            ysq = sb("ysq", [128, 1024], BF16); b_ysq = Buf()
            yss = sb("yss", [128, 1]); b_yss = Buf()
            hsn_ss = sb("hsn_ss", [4, 1]); b_hsnss = Buf()
            hsn_sq = sb("hsn_sq", [4, 1024], BF16); b_hsnsq = Buf()
            hsn = sb("hsn", [4, 1024], BF16); b_hsn = Buf()
            hnTs = sb("hnTs", [128, 8, 4], BF16); b_hnTs = Buf()
            sgs = sb("sgs", [128, 4]); b_sgs = Buf()
            actTs = sb("actTs", [128, 22, 4], BF16); b_actTs = Buf()
            ysb = sb("ysb", [4, 1024]); b_ysb = Buf()
            fw.dma("sp", nfb[:], nfin_d[:, :], writes=[b_nfb])
            fw.dma("sp", gffn[:], gffn_d[:, :], writes=[b_gffn])
            fw.dma("sp", sel4[:], sel4_d[:, :], writes=[b_sel4])
            fw.dma("pool", wout[:], wout_d[:, :].rearrange("(k p) c -> p k c", p=128), writes=[b_wout])
            fw.dma("sp", wdn[:], wdn_s[:, :, :].rearrange("f p c -> p f c"), reads=[b_wdns], writes=[b_wdn])
            act(lambda e: e.activation(out=hsn_sq[:], in_=hs_res[:], func=AF.Square, accum_out=hsn_ss[:]), r=[b_hsres], w=[b_hsnsq, b_hsnss])
            act(lambda e: e.activation(out=hsn_ss[:], in_=hsn_ss[:], func=AF.Sqrt, scale=1.0 / 1024, bias=EPS), w=[b_hsnss])
            dve(lambda e: e.reciprocal(out=hsn_ss[:], in_=hsn_ss[:]), w=[b_hsnss])
            dve(lambda e: e.tensor_scalar(out=hsn[:], in0=hs_res[:], scalar1=hsn_ss[:, 0:1], scalar2=None, op0=ALU.mult), r=[b_hsres, b_hsnss], w=[b_hsn])
            for kt in range(8):
                tr(PT[:, kt * 4:(kt + 1) * 4], hsn[0:4, kt * 128:(kt + 1) * 128], ident_b[0:4, 0:4], r=[b_hsn, b_identb], w=[b_PT])
            for kt in range(8):
                act(lambda e, kt=kt: e.activation(out=hnTs[:, kt, :], in_=PT[:, kt * 4:(kt + 1) * 4], func=AF.Copy, scale=gffn[:, kt:kt + 1]),
                    r=[b_PT, b_gffn], w=[b_hnTs])
            wgi = 0
            for bi in range(NB):
                for j4 in range(4):
                    cb = j4 % 2
                    c0 = j4 * TB + bi * 512
                    fw.dma("sp", cand[cb][:], xout[c0 // CH][:, c0 % CH:c0 % CH + 512].rearrange("(k p) t -> p k t", p=128),
                           reads=[b_xout], writes=[b_cand[cb]])
                    if j4 == 0:
                        dve(lambda e, cb=cb: e.tensor_scalar(out=mixT[:], in0=cand[cb][:], scalar1=sel4[:, 0:1], scalar2=None, op0=ALU.mult),
                            r=[b_cand[cb], b_sel4], w=[b_mixT])
                    else:
                        dve(lambda e, cb=cb, j4=j4: e.scalar_tensor_tensor(out=mixT[:], in0=cand[cb][:], scalar=sel4[:, j4:j4 + 1], in1=mixT[:],
                                                                           op0=ALU.mult, op1=ALU.add), r=[b_cand[cb], b_sel4], w=[b_mixT])
                for tt in range(4):
                    r0 = bi * 512 + tt * 128
                    xs_ = tt % 2
                    fw.dma("sp", xt2[xs_][:], xown_d[r0:r0 + 128, :], writes=[b_xt2[xs_]])
                    for half in range(2):
                        for kt in range(8):
                            mm(PA[:, half * 512:(half + 1) * 512], lhsT=mixT[:, kt, tt * 128:(tt + 1) * 128],
                               rhs=wout[:, kt, half * 512:(half + 1) * 512], start=(kt == 0), stop=(kt == 7),
                               r=[b_mixT, b_wout], w=[b_PA])
                    dve(lambda e, tt=tt, xs_=xs_: e.tensor_tensor(out=hres[:, tt, :], in0=PA[:, :], in1=xt2[xs_][:], op=ALU.add),
                        r=[b_PA, b_xt2[xs_]], w=[b_hres[tt]])
                    act(lambda e, tt=tt: e.activation(out=hsq[:], in_=hres[:, tt, :], func=AF.Square, accum_out=hss[:]),
                        r=[b_hres[tt]], w=[b_hsq, b_hss])
                    act(lambda e: e.activation(out=hss[:], in_=hss[:], func=AF.Sqrt, scale=1.0 / 1024, bias=EPS), w=[b_hss])
                    dve(lambda e: e.reciprocal(out=hss[:], in_=hss[:]), w=[b_hss])
                    dve(lambda e, tt=tt: e.tensor_scalar(out=hs[:], in0=hres[:, tt, :], scalar1=hss[:, 0:1], scalar2=None, op0=ALU.mult),
                        r=[b_hres[tt], b_hss], w=[b_hs])
                    for kt in range(8):
                        tr(PT[:, kt * 128:(kt + 1) * 128], hs[:, kt * 128:(kt + 1) * 128], ident_b[:], r=[b_hs, b_identb], w=[b_PT])
                    for kt in range(8):
                        act(lambda e, kt=kt, tt=tt: e.activation(out=hnT[:, kt, tt * 128:(tt + 1) * 128], in_=PT[:, kt * 128:(kt + 1) * 128],
                                                                 func=AF.Copy, scale=gffn[:, kt:kt + 1]), r=[b_PT, b_gffn], w=[b_hnT])
                for f in range(22):
                    wb = wgi % 3
                    wgi += 1
                    fw.dma("sp", wg[wb][:].rearrange("p k c -> p (k c)"), wgu_s[f], reads=[b_wgus[f]], writes=[b_wg[wb]])
                    for kt in range(8):
                        mm(PA[:, 0:512], lhsT=wg[wb][:, kt, 0:128], rhs=hnT[:, kt, :], start=(kt == 0), stop=(kt == 7),
                           r=[b_wg[wb], b_hnT], w=[b_PA])
                    for kt in range(8):
                        mm(PB[:, 0:512], lhsT=wg[wb][:, kt, 128:256], rhs=hnT[:, kt, :], start=(kt == 0), stop=(kt == 7),
                           r=[b_wg[wb], b_hnT], w=[b_PB])
                    sb_ = f % 2
                    act(lambda e, sb_=sb_: e.activation(out=sg[sb_][:], in_=PA[:, 0:512], func=AF.Silu), r=[b_PA], w=[b_sg[sb_]])
                    dve(lambda e, sb_=sb_, f=f: e.tensor_tensor(out=actT[:, f, :], in0=PB[:, 0:512], in1=sg[sb_][:], op=ALU.mult),
                        r=[b_PB, b_sg[sb_]], w=[b_actT])
                    if bi == 0:
                        for kt in range(8):
                            mm(PD[:, 0:4], lhsT=wg[wb][:, kt, 0:128], rhs=hnTs[:, kt, :], start=(kt == 0), stop=(kt == 7),
                               r=[b_wg[wb], b_hnTs], w=[b_PD])
                        for kt in range(8):
                            mm(PD[:, 4:8], lhsT=wg[wb][:, kt, 128:256], rhs=hnTs[:, kt, :], start=(kt == 0), stop=(kt == 7),
                               r=[b_wg[wb], b_hnTs], w=[b_PD])
                        act(lambda e: e.activation(out=sgs[:], in_=PD[:, 0:4], func=AF.Silu), r=[b_PD], w=[b_sgs])
                        dve(lambda e, f=f: e.tensor_tensor(out=actTs[:, f, :], in0=PD[:, 4:8], in1=sgs[:], op=ALU.mult),
                            r=[b_PD, b_sgs], w=[b_actTs])
                for tt in range(4):
                    r0 = bi * 512 + tt * 128
                    for half in range(2):
                        for f in range(22):
                            mm(PC[:, half * 512:(half + 1) * 512], lhsT=actT[:, f, tt * 128:(tt + 1) * 128],
                               rhs=wdn[:, f, half * 512:(half + 1) * 512], start=(f == 0), stop=(f == 21),
                               r=[b_actT, b_wdn], w=[b_PC])
                    ys_ = tt % 2
                    dve(lambda e, tt=tt, ys_=ys_: e.tensor_tensor(out=yb[ys_][:], in0=PC[:, :], in1=hres[:, tt, :], op=ALU.add),
                        r=[b_PC, b_hres[tt]], w=[b_yb[ys_]])
                    act(lambda e, ys_=ys_: e.activation(out=ysq[:], in_=yb[ys_][:], func=AF.Square, accum_out=yss[:]),
                        r=[b_yb[ys_]], w=[b_ysq, b_yss])
                    act(lambda e: e.activation(out=yss[:], in_=yss[:], func=AF.Sqrt, scale=1.0 / 1024, bias=EPS), w=[b_yss])
                    dve(lambda e: e.reciprocal(out=yss[:], in_=yss[:]), w=[b_yss])
                    dve(lambda e, ys_=ys_: e.scalar_tensor_tensor(out=yb[ys_][:], in0=yb[ys_][:], scalar=yss[:, 0:1], in1=nfb[:],
                                                                  op0=ALU.mult, op1=ALU.mult), r=[b_yss, b_nfb], w=[b_yb[ys_]])
                    fw.dma("sp", y_o[r0:r0 + 128, :], yb[ys_][:], reads=[b_yb[ys_]])
            for half in range(2):
                for f in range(22):
                    mm(PC[0:4, half * 512:(half + 1) * 512], lhsT=actTs[:, f, :], rhs=wdn[:, f, half * 512:(half + 1) * 512],
                       start=(f == 0), stop=(f == 21), r=[b_actTs, b_wdn], w=[b_PC])
            dve(lambda e: e.tensor_tensor(out=ysb[:], in0=PC[0:4, :], in1=hs_res[:], op=ALU.add), r=[b_PC, b_hsres], w=[b_ysb])
            act(lambda e: e.activation(out=hsn_sq[:], in_=ysb[:], func=AF.Square, accum_out=hsn_ss[:]), r=[b_ysb], w=[b_hsnsq, b_hsnss])
            act(lambda e: e.activation(out=hsn_ss[:], in_=hsn_ss[:], func=AF.Sqrt, scale=1.0 / 1024, bias=EPS), w=[b_hsnss])
            dve(lambda e: e.reciprocal(out=hsn_ss[:], in_=hsn_ss[:]), w=[b_hsnss])
            dve(lambda e: e.scalar_tensor_tensor(out=ysb[:], in0=ysb[:], scalar=hsn_ss[:, 0:1], in1=nfb[0:4, :], op0=ALU.mult, op1=ALU.mult),
                r=[b_hsnss, b_nfb], w=[b_ysb])
            fw.dma("sp", ys_o[:, :], ysb[:], reads=[b_ysb])
        fw.drain()
    return nc


def _consts(NT):
    TT = NT * 128
    c = {}
    c["c_ident"] = np.eye(128, dtype=np.float32)
    half = 32
    inv = np.power(np.float32(10000.0), -np.arange(half, dtype=np.float32) * np.float32(2.0) / np.float32(64)).astype(np.float32)
    pos = (np.arange(NT)[None, :] * 128 + np.arange(128)[:, None]).astype(np.float32)
    ang = (pos[:, :, None] * inv[None, None, :]).astype(np.float32)
    c["c_cos"] = np.cos(ang).astype(np.float32).reshape(128, NT * 32)
    c["c_sin"] = np.sin(ang).astype(np.float32).reshape(128, NT * 32)
    k = np.arange(128)[:, None]
    q = np.arange(128)[None, :]
    tri = np.stack([(k <= q), (k >= q)], axis=1).astype(np.float32)
    c["c_tri"] = tri.reshape(128, 256)
    cm = np.zeros((128, 17, 128), np.float32)
    for m in range(17):
        cm[:, m, :] = (16 * k - q <= 128 * m - 31)
    c["c_cmpmask"] = cm.reshape(128, 17 * 128)
    qq = np.arange(128)[:, None]
    r = np.arange(256)[None, :] - 128
    hi = (qq >= 64).astype(np.int64)
    prel = np.zeros((128, 256), np.float32)
    prel[(r == hi) | (r == hi - 1)] = 1e4
    prel[r > hi] = -1e30
    c["c_prel"] = prel
    kk = np.arange(TT)[None, :]
    e = np.arange(64)[:, None]
    c["c_eind"] = (e == (kk // 64) % 64).astype(np.float32)
    n = np.arange(512)[:, None]
    s_ = np.arange(128)[None, :]
    c2s = ((n * 16 < s_ * 64 + 64) & (n * 16 + 32 > s_ * 64) & (n < 511)).astype(np.float32)
    c["c_c2s"] = c2s.reshape(4, 128, 128).transpose(1, 0, 2).reshape(128, 512)
    j = np.arange(128)[:, None]
    i = np.arange(128)[None, :]
    same = (j // 64) == (i // 64)
    gm = np.zeros((128, 5, 128), np.float32)
    gm[:, 0, :] = np.where(same & (i >= j), 0.0, NEGB)
    gm[:, 1, :] = np.where(same & (i > j), 0.0, NEGB)
    gm[:, 2, :] = (same & (j <= i))
    gm[:, 3, :] = (j < 64) * np.ones((1, 128))
    gm[:, 4, :] = (j >= 64) * np.ones((1, 128))
    c["c_gmask"] = gm.reshape(128, 5 * 128)
    angs = (np.float32(8192.0) * inv).astype(np.float32)
    c["c_rope_s"] = np.tile(np.concatenate([np.cos(angs), np.sin(angs)]).astype(np.float32)[None, :], (4, 1))
    nn_ = np.arange(512).reshape(4, 128).T
    c["c_ones511"] = (nn_ < 511).astype(np.float32)
    oh = np.zeros((4, 80), np.float32)
    for s_ in range(4):
        oh[s_, s_ * 4:(s_ + 1) * 4] = 1.0
    for r_ in range(4):
        for s_ in range(4):
            oh[r_, 16 + (r_ * 4 + s_) * 4 + s_] = 1.0
    c["c_oh4"] = oh
    bon = np.zeros((1, 128), np.float32)
    bon[0, 0] = 1e4
    bon[0, 127] = 1e4
    c["c_bonus_s"] = bon
    ef = np.zeros((2, 128), np.float32)
    ef[0, 0:64] = 1.0
    ef[1, 64:128] = 1.0
    c["c_efix"] = ef
    c["c_oh2"] = np.array([[1.0, 0.0, 0.0, 1.0]], np.float32)
    return c


def _core_weights(inp, g, hp):
    jh = 2 * g + hp
    w_in = inp["w_in"][0]
    own = [4 * g + 2 * hp, 4 * g + 2 * hp + 1]
    oth = [4 * g + 2 * (1 - hp), 4 * g + 2 * (1 - hp) + 1]
    heads = own + oth
    cols = []
    for h in heads:
        cols += list(range(h * 64, (h + 1) * 64))

    def kvcol(branch, kv):
        base = 512 + ((branch * 2 + kv) * 2 + g) * 64
        return list(range(base, base + 64))
    for branch in range(3):
        cols += kvcol(branch, 0)
    for branch in range(3):
        cols += kvcol(branch, 1)
    for h in heads:
        cols += [1280 + h * 3 + t for t in range(3)]
    cols += list(range(2840 + jh * 128, 2840 + (jh + 1) * 128))
    cols += [3352 + jh, 3356 + jh]
    w_tok = np.ascontiguousarray(w_in[:, cols])
    gcols = []
    for part in range(3):
        gcols += list(range(1304 + part * 512 + jh * 128, 1304 + part * 512 + (jh + 1) * 128))
    w_gdn = np.ascontiguousarray(w_in[:, gcols])
    d = {"w_tok": w_tok, "w_gdn": w_gdn}
    d["g_mix"] = np.ascontiguousarray(inp["norm_mix"][0].reshape(8, 128).T)
    cwf = inp["gdn_conv_w"][0]
    gch = [jh * 128 + part * 512 + np.arange(128) for part in range(3)]
    cw = np.stack([cwf[:, ch].T for ch in gch], axis=1)
    d["conv_w"] = np.ascontiguousarray(cw.reshape(128, 12))
    d["head_sc"] = np.ascontiguousarray(np.stack([np.full(128, inp["gdn_a_log"][0, jh]),
                                                  np.full(128, inp["gdn_dt_bias"][0, jh])], axis=1).astype(np.float32))
    d["gdn_norm_b"] = np.ascontiguousarray(np.tile(inp["gdn_norm"][0][None, :], (128, 1)))
    cwt = inp["nsa_cmp_w"][0]
    d["cmp_w"] = np.ascontiguousarray(cwt.reshape(2, 16, 2, 64, 64).transpose(2, 3, 0, 1, 4).reshape(128, 2 * 16 * 64))
    pe = inp["nsa_cmp_pe"][0]
    d["cmp_pe"] = np.ascontiguousarray(pe.reshape(2, 16, 2, 64).transpose(2, 3, 0, 1).reshape(128, 32))
    return d


def _core_inputs(inp, c, NT, consts):
    b, j = c // 4, c % 4
    g, hp = j // 2, j % 2
    TT = NT * 128
    TB = TT // 4
    d = dict(consts)
    d.update(_core_weights(inp, g, hp))
    d["x"] = np.ascontiguousarray(inp["x_prompt"][b, :TT])
    d["x_own"] = np.ascontiguousarray(inp["x_prompt"][b, j * TB:(j + 1) * TB])
    sel = np.zeros((128, 4), np.float32)
    sel[:, j] = 1.0
    d["sel4"] = sel
    perm = []
    for jj in range(4):
        perm += list(range(128 * jj, 128 * jj + 128)) + list(range(512 + 128 * jj, 512 + 128 * jj + 128))
    d["w_out_p"] = np.ascontiguousarray(inp["w_out"][0][perm, :])
    d["g_ffn"] = np.ascontiguousarray(inp["norm_ffn"][0].reshape(8, 128).T)
    d["w_gu"] = np.ascontiguousarray(inp["w_gate_up"][0])
    d["w_dn"] = np.ascontiguousarray(inp["w_down"][0])
    d["nfin_b"] = np.ascontiguousarray(np.tile(inp["norm_final"][None, :], (128, 1)))
    s0 = 4 * c
    d["xs"] = np.ascontiguousarray(inp["x_sample"][s0:s0 + 4, 0, :])
    d["w_in_full"] = np.ascontiguousarray(inp["w_in"][0])
    d["cache_kv"] = inp["cache_nsa_kv"][0].reshape(2560 * 128, 512)
    d["ptab_b"] = np.ascontiguousarray(np.tile(inp["page_table"][s0:s0 + 4].reshape(1, 256), (128, 1)).astype(np.int32))
    d["c_iota"] = np.arange(128, dtype=np.float32).reshape(128, 1)
    d["win_cache"] = np.ascontiguousarray(inp["cache_nsa_win"][0, s0:s0 + 4].reshape(4, 512, 256))
    d["gdn_S"] = np.ascontiguousarray(inp["state_gdn_S"][0, s0:s0 + 4].reshape(16, 128, 128))
    d["gdn_conv"] = np.ascontiguousarray(inp["state_gdn_conv"][0, s0:s0 + 4])
    d["conv_w_b"] = np.ascontiguousarray(np.tile(inp["gdn_conv_w"][0][None], (4, 1, 1)))
    d["alog_b"] = np.ascontiguousarray(np.tile(np.concatenate([inp["gdn_a_log"][0], inp["gdn_dt_bias"][0]])[None, :], (4, 1)))
    d["gnorm_row"] = np.ascontiguousarray(inp["gdn_norm"][0][None, :])
    cwt = inp["nsa_cmp_w"][0]
    w64 = cwt.transpose(2, 0, 1, 3).reshape(64, 2 * 32 * 64)
    d["cmp_w64"] = np.ascontiguousarray(np.concatenate([w64, w64], axis=0))
    pe = inp["nsa_cmp_pe"][0]
    p64 = pe.transpose(2, 0, 1).reshape(64, 64)
    d["cmp_pe64"] = np.ascontiguousarray(np.concatenate([p64, p64], axis=0))
    wo = inp["w_out"][0]
    d["w_out_n"] = np.ascontiguousarray(wo[:512].reshape(8, 64, 1024).transpose(1, 0, 2).reshape(64, 8192))
    d["w_out_g"] = np.ascontiguousarray(wo[512:].reshape(4, 128, 1024).transpose(1, 0, 2).reshape(128, 4096))
    return d


def _run(inp, NT):
    nc = build_nc(NT, phaseB=True)
    consts = _consts(NT)
    maps = [_core_inputs(inp, c, NT, consts) for c in range(8)]
    res = run_bass_kernel_spmd(nc, maps, core_ids=list(range(8)))
    return res.results


def kernel(**inputs):
    inp = {k: np.asarray(v) for k, v in inputs.items()}
    NT = 64
    TT = NT * 128
    TB = TT // 4
    R = _run(inp, NT)
    y_prompt = np.zeros((2, TT, 1024), np.float32)
    kv_prompt = np.zeros((1, 2, TT, 4, 2, 64), np.float32)
    win_prompt = np.zeros((1, 2, 512, 2, 2, 64), np.float32)
    S_prompt = np.zeros((1, 2, 4, 128, 128), np.float32)
    conv_prompt = np.zeros((1, 2, 3, 1536), np.float32)
    for c in range(8):
        b, j = c // 4, c % 4
        g, hp = j // 2, j % 2
        r = R[c]
        y_prompt[b, j * TB:(j + 1) * TB] = r["y_out"]
        if hp == 0:
            kv_prompt[0, b, :, :, g, :] = r["kv_out"].reshape(TT, 4, 64)
            win_prompt[0, b, :, :, g, :] = r["win_out"].reshape(512, 2, 64)
        S_prompt[0, b, j] = r["S_out"]
        cv = r["conv_out"]
        for part in range(3):
            conv_prompt[0, b, :, part * 512 + j * 128:part * 512 + (j + 1) * 128] = cv[:, part, :].T
    y_sample = np.zeros((32, 1, 1024), np.float32)
    kv_sample = np.zeros((1, 32, 1, 4, 2, 64), np.float32)
    win_sample = np.zeros((1, 32, 512, 2, 2, 64), np.float32)
    S_sample = np.zeros((1, 32, 4, 128, 128), np.float32)
    conv_sample = np.zeros((1, 32, 3, 1536), np.float32)
    for c in range(8):
        r = R[c]
        s0 = 4 * c
        y_sample[s0:s0 + 4, 0] = r["ys_out"]
        kv_sample[0, s0:s0 + 4, 0] = r["kvs_out"].reshape(4, 4, 2, 64)
        win_sample[0, s0:s0 + 4] = r["wins_out"].reshape(4, 512, 2, 2, 64)
        S_sample[0, s0:s0 + 4] = r["Ss_out"].reshape(4, 4, 128, 128)
        conv_sample[0, s0:s0 + 4] = r["convs_out"]
    return (y_prompt, y_sample, kv_prompt, win_prompt, S_prompt, conv_prompt, kv_sample, win_sample, S_sample, conv_sample)
```
